# Optimizing a Trainium2 kernel written in Bass

```python
import math
import jax, jax.numpy as jnp
from jax import lax
import numpy as np

D_MODEL = 1024
BATCH = 8
SEQ = 2048
DEPTH = 1
DEC_BATCH = 128
DEC_SEQ = 4
PAST_LEN = 8192
PAGE_SIZE = 128

N_META = 16
D_MIX = D_MODEL
HEAD_DIM = 64
ATTN_WIDTH = D_MIX // 2
N_HEADS = ATTN_WIDTH // HEAD_DIM
N_KV_HEADS = 2
GQA_GROUP = N_HEADS // N_KV_HEADS
KV_WIDTH = N_KV_HEADS * HEAD_DIM
WINDOW = 128
BLOCK = 128
SSM_WIDTH = D_MIX - ATTN_WIDTH
SSM_GROUP = 16
N_SSM_GROUPS = SSM_WIDTH // SSM_GROUP
SSM_STATE = 64
IN_COLS = ATTN_WIDTH + 2 * KV_WIDTH + SSM_WIDTH
D_FF = 2816
CONV_W = 3
EPS = 1e-5
DT_MIN = 1e-3
DT_MAX = 1e-1

kernel_name = "hymba_swa_sink_s5_convffn_step"


def rms_norm(x, g):
    xf = x.astype(jnp.float32)
    y = xf * lax.rsqrt(jnp.mean(xf * xf, axis=-1, keepdims=True) + EPS)
    return (y * g.astype(jnp.float32)).astype(x.dtype)


def sink_softmax(s, sink):
    m = jnp.maximum(jnp.max(s, axis=-1), sink)
    p = jnp.exp(s - m[..., None])
    denom = jnp.sum(p, axis=-1) + jnp.exp(sink - m)
    return p / denom[..., None]


def swa_banded(q, k, v, sinks):
    n, length = q.shape[:2]
    front = (-N_META) % BLOCK
    back = (-(front + length)) % BLOCK
    lp = front + length + back
    nb = lp // BLOCK
    pad = ((0, 0), (front, back), (0, 0), (0, 0))
    qb = jnp.pad(q, pad).reshape(n, nb, BLOCK, N_KV_HEADS, GQA_GROUP, HEAD_DIM)
    kb = jnp.pad(k, pad).reshape(n, nb, BLOCK, N_KV_HEADS, HEAD_DIM)
    vb = jnp.pad(v, pad).reshape(n, nb, BLOCK, N_KV_HEADS, HEAD_DIM)
    zero_blk = jnp.zeros_like(kb[:, :1])
    kk = jnp.concatenate([jnp.concatenate([zero_blk, kb[:, :-1]], axis=1), kb], axis=2)
    vv = jnp.concatenate([jnp.concatenate([zero_blk, vb[:, :-1]], axis=1), vb], axis=2)
    qpos = (jnp.arange(lp) - front).reshape(nb, BLOCK)
    kpos = jnp.concatenate([qpos - BLOCK, qpos], axis=1)
    mask = ((kpos[:, None, :] <= qpos[:, :, None]) & (kpos[:, None, :] > qpos[:, :, None] - WINDOW)
            & (kpos[:, None, :] >= 0) & (kpos[:, None, :] < length))
    scale = HEAD_DIM ** -0.5
    s = jnp.einsum('bnqkgd,bnskd->bnkgqs', qb.astype(jnp.float32), kk.astype(jnp.float32)) * scale
    s = jnp.where(mask[None, :, None, None], s, -jnp.inf)
    sink = sinks.astype(jnp.float32).reshape(N_KV_HEADS, GQA_GROUP)[None, None, :, :, None]
    p = sink_softmax(s, sink)
    o = jnp.einsum('bnkgqs,bnskd->bnqkgd', p, vv.astype(jnp.float32))
    o = o.reshape(n, lp, ATTN_WIDTH)[:, front:front + length]
    return o.astype(q.dtype)


def swa_cached(q, k, v, k_buf, v_buf, sinks):
    n, t = q.shape[:2]
    wb = k_buf.shape[1]
    k_all = jnp.concatenate([k_buf.astype(k.dtype), k], axis=1)
    v_all = jnp.concatenate([v_buf.astype(v.dtype), v], axis=1)
    qpos = PAST_LEN + jnp.arange(t)
    kpos = PAST_LEN - wb + jnp.arange(wb + t)
    mask = (kpos[None, :] <= qpos[:, None]) & (kpos[None, :] > qpos[:, None] - WINDOW)
    qg = q.reshape(n, t, N_KV_HEADS, GQA_GROUP, HEAD_DIM).astype(jnp.float32)
    s = jnp.einsum('btkgd,bskd->bkgts', qg, k_all.astype(jnp.float32)) * (HEAD_DIM ** -0.5)
    s = jnp.where(mask[None, None, None], s, -jnp.inf)
    sink = sinks.astype(jnp.float32).reshape(N_KV_HEADS, GQA_GROUP)[None, :, :, None]
    p = sink_softmax(s, sink)
    o = jnp.einsum('bkgts,bskd->btkgd', p, v_all.astype(jnp.float32)).reshape(n, t, ATTN_WIDTH)
    return o.astype(q.dtype), k_all, v_all


def s5_scan(u, h0, lam_re, lam_im, log_dt, b_re, b_im, c_re, c_im, d_skip):
    f32 = jnp.float32
    n, t, _ = u.shape
    lam = lax.complex(lam_re.astype(f32), lam_im.astype(f32))
    dt = jnp.exp(log_dt.astype(f32))[:, None]
    lam_bar = jnp.exp(lam * dt)
    bmat = lax.complex(b_re.astype(f32), b_im.astype(f32))
    b_bar = ((lam_bar - 1.0) / lam)[..., None] * bmat
    uf = u.astype(f32)
    ug = uf.reshape(n, t, N_SSM_GROUPS, SSM_GROUP).astype(jnp.complex64)
    bu = jnp.einsum('ntgc,gpc->ntgp', ug, b_bar)
    bu = bu.at[:, 0].add(lam_bar[None] * h0)
    a = jnp.broadcast_to(lam_bar, bu.shape)

    def combine(e1, e2):
        a1, b1 = e1
        a2, b2 = e2
        return a2 * a1, a2 * b1 + b2

    _, h = lax.associative_scan(combine, (a, bu), axis=1)
    cmat = lax.complex(c_re.astype(f32), c_im.astype(f32))
    y = jnp.real(jnp.einsum('ntgp,gcp->ntgc', h, cmat)).reshape(n, t, SSM_WIDTH)
    y = y + d_skip.astype(f32) * uf
    return y, h[:, -1]


def conv_ffn(xn, buf, w_gate, w_up, conv_w, conv_b, w_down):
    t = xn.shape[1]
    g = xn @ w_gate
    up = xn @ w_up
    gp = jnp.concatenate([buf.astype(g.dtype), g], axis=1)
    c = conv_b
    for j in range(CONV_W):
        c = c + conv_w[j] * gp[:, j:j + t]
    h = jax.nn.silu(c) * up
    return h @ w_down, gp[:, -(CONV_W - 1):]


def decoder_layer(x, k_buf, v_buf, h0, conv_buf,
                  g_mix, w_in, sinks, lam_re, lam_im, log_dt, b_re, b_im, c_re, c_im, d_skip,
                  w_glu, b_glu, g_attn_out, g_ssm_out, w_o, g_ffn, w_gate, w_up, conv_w, conv_b, w_down):
    n, t, _ = x.shape
    wb = min(WINDOW, PAST_LEN)
    xn = rms_norm(x, g_mix)
    proj = xn @ w_in
    q = proj[..., :ATTN_WIDTH].reshape(n, t, N_HEADS, HEAD_DIM)
    k = proj[..., ATTN_WIDTH:ATTN_WIDTH + KV_WIDTH].reshape(n, t, N_KV_HEADS, HEAD_DIM)
    v = proj[..., ATTN_WIDTH + KV_WIDTH:ATTN_WIDTH + 2 * KV_WIDTH].reshape(n, t, N_KV_HEADS, HEAD_DIM)
    u = proj[..., ATTN_WIDTH + 2 * KV_WIDTH:]
    if k_buf is None:
        attn = swa_banded(q, k, v, sinks)
        k_all, v_all = k, v
    else:
        attn, k_all, v_all = swa_cached(q, k, v, k_buf, v_buf, sinks)
    new_k, new_v = k_all[:, -wb:], v_all[:, -wb:]
    y_ssm, h_last = s5_scan(u, h0, lam_re, lam_im, log_dt, b_re, b_im, c_re, c_im, d_skip)
    g = jax.nn.gelu(y_ssm)
    ssm = (g * jax.nn.sigmoid(g @ w_glu.astype(jnp.float32) + b_glu.astype(jnp.float32))).astype(x.dtype)
    mixed = jnp.concatenate([rms_norm(attn, g_attn_out), rms_norm(ssm, g_ssm_out)], axis=-1) @ w_o
    x = x + mixed
    ffn, new_conv = conv_ffn(rms_norm(x, g_ffn), conv_buf, w_gate, w_up, conv_w, conv_b, w_down)
    x = x + ffn
    return x, new_k, new_v, h_last, new_conv


def setup_inputs(seed: int = 0) -> dict:
    key = jax.random.key(seed)
    ks = iter(jax.random.split(key, 40))
    f32 = jnp.float32
    wb = min(WINDOW, PAST_LEN)
    nrm = lambda shape, s: jax.random.normal(next(ks), shape, f32) * s
    gain = lambda shape: 1.0 + 0.01 * jax.random.normal(next(ks), shape, f32)
    n_idx = jnp.arange(SSM_STATE, dtype=f32)
    inp = {
        "x_prompt": nrm((BATCH, SEQ, D_MODEL), 1.0),
        "x_sample": nrm((DEC_BATCH, DEC_SEQ, D_MODEL), 1.0),
        "cache_k_win": nrm((DEPTH, DEC_BATCH, wb, N_KV_HEADS, HEAD_DIM), 1.0),
        "cache_v_win": nrm((DEPTH, DEC_BATCH, wb, N_KV_HEADS, HEAD_DIM), 1.0),
        "state_ssm_re": nrm((DEPTH, DEC_BATCH, N_SSM_GROUPS, SSM_STATE), 0.1),
        "state_ssm_im": nrm((DEPTH, DEC_BATCH, N_SSM_GROUPS, SSM_STATE), 0.1),
        "state_conv": nrm((DEPTH, DEC_BATCH, CONV_W - 1, D_FF), 1.0),
        "meta_tokens": nrm((N_META, D_MODEL), 1.0),
        "g_mix": gain((DEPTH, D_MODEL)),
        "w_in": nrm((DEPTH, D_MODEL, IN_COLS), D_MODEL ** -0.5),
        "sinks": nrm((DEPTH, N_HEADS), 0.5),
        "lam_re": -0.5 + nrm((DEPTH, N_SSM_GROUPS, SSM_STATE), 0.01),
        "lam_im": math.pi * n_idx + nrm((DEPTH, N_SSM_GROUPS, SSM_STATE), 0.01),
        "log_dt": jax.random.uniform(next(ks), (DEPTH, N_SSM_GROUPS), f32,
                                     math.log(DT_MIN), math.log(DT_MAX)),
        "b_re": nrm((DEPTH, N_SSM_GROUPS, SSM_STATE, SSM_GROUP), (2 * SSM_GROUP) ** -0.5),
        "b_im": nrm((DEPTH, N_SSM_GROUPS, SSM_STATE, SSM_GROUP), (2 * SSM_GROUP) ** -0.5),
        "c_re": nrm((DEPTH, N_SSM_GROUPS, SSM_GROUP, SSM_STATE), (2 * SSM_STATE) ** -0.5),
        "c_im": nrm((DEPTH, N_SSM_GROUPS, SSM_GROUP, SSM_STATE), (2 * SSM_STATE) ** -0.5),
        "d_skip": nrm((DEPTH, SSM_WIDTH), 1.0),
        "w_glu": nrm((DEPTH, SSM_WIDTH, SSM_WIDTH), SSM_WIDTH ** -0.5),
        "b_glu": nrm((DEPTH, SSM_WIDTH), 0.01),
        "g_attn_out": gain((DEPTH, ATTN_WIDTH)),
        "g_ssm_out": gain((DEPTH, SSM_WIDTH)),
        "w_o": nrm((DEPTH, D_MIX, D_MODEL), D_MIX ** -0.5),
        "g_ffn": gain((DEPTH, D_MODEL)),
        "w_gate": nrm((DEPTH, D_MODEL, D_FF), D_MODEL ** -0.5),
        "w_up": nrm((DEPTH, D_MODEL, D_FF), D_MODEL ** -0.5),
        "conv_w": nrm((DEPTH, CONV_W, D_FF), CONV_W ** -0.5),
        "conv_b": nrm((DEPTH, D_FF), 0.01),
        "w_down": nrm((DEPTH, D_FF, D_MODEL), D_FF ** -0.5),
        "g_final": gain((D_MODEL,)),
    }
    return inp


def reference(x_prompt, x_sample, cache_k_win, cache_v_win, state_ssm_re, state_ssm_im, state_conv,
              meta_tokens, g_mix, w_in, sinks, lam_re, lam_im, log_dt, b_re, b_im, c_re, c_im, d_skip,
              w_glu, b_glu, g_attn_out, g_ssm_out, w_o, g_ffn, w_gate, w_up, conv_w, conv_b, w_down,
              g_final):
    f32 = jnp.float32
    bsz = x_prompt.shape[0]
    meta = jnp.broadcast_to(meta_tokens.astype(x_prompt.dtype)[None], (bsz, N_META, D_MODEL))
    xp = jnp.concatenate([meta, x_prompt], axis=1)
    xs = x_sample
    pk, pv, pre, pim, pc = [], [], [], [], []
    sk, sv, sre, sim, sc = [], [], [], [], []
    for l in range(DEPTH):
        weights = (g_mix[l], w_in[l], sinks[l], lam_re[l], lam_im[l], log_dt[l], b_re[l], b_im[l],
                   c_re[l], c_im[l], d_skip[l], w_glu[l], b_glu[l], g_attn_out[l], g_ssm_out[l], w_o[l],
                   g_ffn[l], w_gate[l], w_up[l], conv_w[l], conv_b[l], w_down[l])
        h0_p = jnp.zeros((bsz, N_SSM_GROUPS, SSM_STATE), jnp.complex64)
        conv0_p = jnp.zeros((bsz, CONV_W - 1, D_FF), xp.dtype)
        xp, k_p, v_p, h_p, c_p = decoder_layer(xp, None, None, h0_p, conv0_p, *weights)
        h0_s = lax.complex(state_ssm_re[l].astype(f32), state_ssm_im[l].astype(f32))
        xs, k_s, v_s, h_s, c_s = decoder_layer(xs, cache_k_win[l], cache_v_win[l], h0_s, state_conv[l], *weights)
        pk.append(k_p); pv.append(v_p); pre.append(jnp.real(h_p)); pim.append(jnp.imag(h_p)); pc.append(c_p)
        sk.append(k_s); sv.append(v_s); sre.append(jnp.real(h_s)); sim.append(jnp.imag(h_s)); sc.append(c_s)
    y_prompt = rms_norm(xp, g_final)[:, N_META:]
    y_sample = rms_norm(xs, g_final)
    return (y_prompt, y_sample,
            jnp.stack(pk), jnp.stack(pv), jnp.stack(pre), jnp.stack(pim), jnp.stack(pc),
            jnp.stack(sk), jnp.stack(sv), jnp.stack(sre), jnp.stack(sim), jnp.stack(sc))
```

```python
import numpy as np
from contextlib import ExitStack
import concourse.bass as bass
import concourse.mybir as mybir
from concourse.bass_utils import run_bass_kernel_spmd

F32 = mybir.dt.float32
BF16 = mybir.dt.bfloat16
ALU = mybir.AluOpType
AF = mybir.ActivationFunctionType
AX = mybir.AxisListType

ENGS = ("pe", "act", "dve", "pool", "sp")
NDMASEM = 12
NCORES = 8
D = 1024
NPT = 17
NPAD = 112
NS = 64
NSEQ = 16
FF = 2816
NFC = 22
EPS = 1e-5
PI = float(np.pi)
MASKV = -30000.0


class _Recorder:
    def __init__(self):
        self.call = None

    def __getattr__(self, name):
        def f(*args, **kwargs):
            self.call = (name, args, kwargs)
            return self
        return f


class Prog:
    def __init__(self):
        self.streams = {e: [] for e in ENGS}
        self.count = {e: 0 for e in ENGS}
        self.waited = {e: {} for e in ENGS}
        self.regs = {}
        self.nrec = 0
        self.limit = None
        self.dma_rr = {"sp": 0, "pool": 0}
        self.dma_cnt = {}

    @staticmethod
    def _region(ap):
        dims = [(int(st), int(sz)) for st, sz in ap.ap]
        off = int(ap.offset)
        esz = mybir.dt.size(ap.dtype)
        space = str(ap.space)
        if space == "DRAM":
            ext = sum((sz - 1) * abs(st) for st, sz in dims)
            return (0, 1, off, off + ext)
        pst, npart = dims[0]
        pst = max(pst, 1)
        p0, f0 = off // pst, off % pst
        ext = sum((sz - 1) * abs(st) for st, sz in dims[1:])
        f0, ext = f0 * esz, ext * esz + esz - 1
        if space == "PSUM":
            return (0, 128, 0, 1 << 30)
        return (p0, p0 + npart, f0, f0 + ext)

    @staticmethod
    def _is_ap(v):
        return hasattr(v, "tensor") and hasattr(v, "ap") and hasattr(v, "offset")

    def _record(self, fn):
        rec = _Recorder()
        fn(rec)
        name, args, kwargs = rec.call
        acc = []
        for i, a in enumerate(args):
            if self._is_ap(a):
                acc.append((a, i == 0))
        for k, v in kwargs.items():
            if self._is_ap(v):
                acc.append((v, k in ("out", "accum_out")))
        out = []
        for ap, w in acc:
            if str(ap.space) == "PSUM":
                w = True
            out.append((ap.tensor.name, self._region(ap), w))
        return out

    def _deps(self, eng, acc):
        deps = {}
        for name, R, w in acc:
            for (R2, w2), evs in self.regs.get(name, {}).items():
                if not (w or w2):
                    continue
                if R[0] < R2[1] and R2[0] < R[1] and R[2] <= R2[3] and R2[2] <= R[3]:
                    for s_, v in evs.items():
                        if s_ == eng and eng == "pe":
                            continue
                        if deps.get(s_, 0) < v:
                            deps[s_] = v
        out = []
        wd = self.waited[eng]
        for s_, v in deps.items():
            if wd.get(s_, 0) < v:
                wd[s_] = v
                out.append((s_, v))
        return out

    def _commit(self, ev, acc):
        s_, v = ev
        for name, R, w in acc:
            d = self.regs.setdefault(name, {})
            if w:
                for key in [k for k in d if k[0][0] >= R[0] and k[0][1] <= R[1] and k[0][2] >= R[2] and k[0][3] <= R[3]]:
                    del d[key]
            e = d.setdefault((R, w), {})
            if e.get(s_, 0) < v:
                e[s_] = v

    def op(self, eng, fn, reads=(), writes=()):
        self.nrec += 1
        if self.limit is not None and self.nrec > self.limit:
            return
        acc = self._record(fn)
        deps = self._deps(eng, acc)
        self.count[eng] += 1
        ev = (eng, self.count[eng])
        self.streams[eng].append((deps, fn, (eng, 1)))
        self._commit(ev, acc)

    def dma(self, eng, fn, reads=(), writes=()):
        self.nrec += 1
        if self.limit is not None and self.nrec > self.limit:
            return
        acc = self._record(fn)
        k = self.dma_rr[eng]
        self.dma_rr[eng] = (k + 1) % NDMASEM
        sname = "dma%s%d" % (eng, k)
        deps = self._deps(eng, acc)
        prev = self.dma_cnt.get(sname, 0) * 16
        if prev and self.waited[eng].get(sname, 0) < prev:
            self.waited[eng][sname] = prev
            deps.append((sname, prev))
        self.dma_cnt[sname] = self.dma_cnt.get(sname, 0) + 1
        ev = (sname, self.dma_cnt[sname] * 16)
        self.streams[eng].append((deps, fn, (sname, 16)))
        self._commit(ev, acc)

    def barrier(self):
        evs = [(e, self.count[e]) for e in ENGS if self.count[e]]
        evs += [(k, c * 16) for k, c in self.dma_cnt.items()]
        for e in ENGS:
            deps = []
            for s, v in evs:
                if s == e:
                    continue
                if self.waited[e].get(s, 0) < v:
                    self.waited[e][s] = v
                    deps.append((s, v))
            if deps:
                self.streams[e].append((deps, None, None))

    def run(self, eng, h, sems):
        for deps, fn, inc in self.streams[eng]:
            for s, v in deps:
                h.wait_ge(sems[s], v)
            if fn is not None:
                fn(h).then_inc(sems[inc[0]], inc[1])


def build_nc(tiles=None, limit=None):
    nc = bass.Bass("TRN2", target_bir_lowering=False)

    def din(name, shape, dt=F32):
        return nc.dram_tensor(name, list(shape), dt, kind="ExternalInput").ap()

    def dout(name, shape, dt=F32):
        return nc.dram_tensor(name, list(shape), dt, kind="ExternalOutput").ap()

    xin = din("xin", [NPT * 128 + 128, D])
    w_in = din("w_in", [D, 1280])
    w_o = din("w_o", [D, D])
    w_glu = din("w_glu", [512, 512])
    w_gate = din("w_gate", [D, FF])
    w_up = din("w_up", [D, FF])
    w_down = din("w_down", [FF, D])
    ident_d = din("ident", [128, 128])
    masks_d = din("masks", [128, 3, 256])
    smask_d = din("smask", [128, 17 * 128])
    pcol_d = din("pcol", [128, 128])
    sinks_d = din("sinks", [8])
    gfin_d = din("gfin", [D])
    lam_d = din("lam", [128, 3, 16])
    lamb_d = din("lamb", [3, 16 * 128])
    bblk_d = din("bblk", [128, 2, 16, 128])
    cblk_d = din("cblk", [128, 2, 16, 128])
    h0_d = din("h0", [128, 2, 16, NSEQ])
    cvst_d = din("cvst", [128, NFC, NSEQ, 2])
    kcT_d = din("kcT", [128, NSEQ * 128])
    vc_d = din("vc", [128, NSEQ, 128])
    kcache_d = din("kcache", [NSEQ, 128, 128])
    vcache_d = din("vcache", [NSEQ, 128, 128])

    y_d = dout("y", [NPT * 128 + 128, D])
    kvp_d = dout("kvp", [128, 256])
    kvs_k = dout("kvs_k", [NSEQ, 128, 128])
    kvs_v = dout("kvs_v", [NSEQ, 128, 128])
    hfin_d = dout("hfin", [128, 2, 16, NSEQ + 1])
    pconv_d = dout("pconv", [128, NFC, 2])
    sconv_d = dout("sconv", [128, NFC, NSEQ, 2])

    P = Prog()
    P.limit = limit
    es = ExitStack()
    with es:
        def sb(name, shape, dt=F32):
            return es.enter_context(nc.sbuf_tensor(name, list(shape), dt))

        def ps(name, shape, dt=F32):
            return es.enter_context(nc.psum_tensor(name, list(shape), dt))

        sems = {e: es.enter_context(nc.semaphore("s_" + e)) for e in ENGS}
        for k in range(NDMASEM):
            for e_ in ("sp", "pool"):
                sems["dma%s%d" % (e_, k)] = es.enter_context(nc.semaphore("s_dma%s%d" % (e_, k)))

        idf = sb("idf", [128, 128]); idb = sb("idb", [128, 128], BF16)
        onesb = sb("onesb", [128, 128], BF16)
        masks = sb("masks_s", [128, 3, 256])
        smask = sb("smask_s", [128, 17 * 128], BF16)
        pcol = sb("pcol_s", [128, 128])
        sink8 = sb("sink8", [128, 8])
        gfb = sb("gfb", [128, D])
        epsb = sb("epsb", [128, 1])
        lam = sb("lam_s", [128, 3, 16])
        WBre = sb("WBre", [128, 16, 128], BF16); WBim = sb("WBim", [128, 16, 128], BF16)
        WCre = sb("WCre", [128, 16, 128], BF16); WCimn = sb("WCimn", [128, 16, 128], BF16)
        cs = sb("cs", [128, 16, 129]); sn = sb("sn", [128, 16, 129])
        mask64 = sb("mask64", [128, NS]); dtmp = sb("dtmp", [128, NS])
        are = sb("are", [128, 16]); aim = sb("aim", [128, 16])
        ah = sb("ah", [128, 2, 16, NSEQ])
        car = sb("car", [128, 2, 16])
        hfin = sb("hfin_s", [128, 2, 16, NSEQ + 1])
        gcar = sb("gcar", [128, NFC, 2])
        sconv = sb("sconv_s", [128, NFC, NSEQ, 2])
        kTx = sb("kTx", [128, 2, 256], BF16)
        vx = sb("vx", [128, 2, 128], BF16)
        GMIX, GFFN, GATT, GSSM, BGLU, DSK, CVW, CVB = 0, 8, 16, 20, 24, 28, 32, 98

        def col(c):
            return pcol[:, c:c + 1]

        junk = sb("junk", [128, D], BF16)
        ssq = sb("ssq", [128, 4]); rstd = sb("rstd", [128, 4])
        xn = sb("xn", [128, D], BF16)
        xnT = sb("xnT", [128, 8, 128], BF16)
        qT = sb("qT", [128, 4, 128], BF16); uT = sb("uT", [128, 4, 128], BF16)
        kvtok = sb("kvtok", [128, 256])
        Sx = sb("Sx", [128, 1, 2, 257]); mx = sb("mx", [128, 2]); nbias = sb("nbias", [128, 2])
        rs = sb("rs", [128, 2]); rinv = sb("rinv", [128, 2])
        Pb = sb("Pb", [128, 2, 257], BF16); PTs = sb("PTs", [128, 4, 128], BF16)
        attn = sb("attn", [128, 512]); anb = sb("anb", [128, 512], BF16)
        mixT = sb("mixT", [128, 8, 128], BF16)
        Psb = sb("Psb", [128, 17 * 128 + 1], BF16)
        PTss = sb("PTss", [128, 17, 128], BF16)
        Bsb = sb("Bsb", [128, 2, 128])
        tt = sb("tt", [128, 8, 128]); rr = sb("rr", [128, 2, 128]); vv = sb("vv", [128, 2, 128])
        hh = sb("hh", [128, 2, 128]); hb = sb("hb", [128, 4, 2, 128], BF16)
        yv = sb("yv", [128, 128]); gl32 = sb("gl32", [128, 4, 128]); glb = sb("glb", [128, 4, 128], BF16)
        sg = sb("sg", [128, 128]); sqb = sb("sqb", [128, 4, 128], BF16); rsb = sb("rsb", [128, 128])
        xn2T = sb("xn2T", [128, 8, 512], BF16)
        gx = sb("gx", [128, 2, 514]); gxs = sb("gxs", [128, NSEQ, 6])
        cv = sb("cv", [128, 1, 512]); sl = sb("sl", [128, 1, 512])
        wi = [sb("wi0", [128, 8, 1280], BF16)] * 2
        wo = [sb("wo0", [128, 8, D], BF16)] * 2
        wgl = [sb("wgl0", [128, 4, 512], BF16)] * 2
        wg = [sb("wg%d" % i, [128, 8, 128], BF16) for i in range(2)]
        wu = [sb("wu%d" % i, [128, 8, 128], BF16) for i in range(2)]
        wd = [sb("wd%d" % i, [128, 512], BF16) for i in range(3)]
        big = sb("big", [128, 9728])
        xb = big[:, 0:4096].rearrange("p (t d) -> p t d", t=4)
        hTall = big[:, 4096:9728].bitcast(BF16).rearrange("p (c n) -> p c n", c=NFC)
        hTsm = big[:, 4096:5504].bitcast(BF16).rearrange("p (c n) -> p c n", c=NFC)
        kTs = big[:, 5504:7680].bitcast(BF16).rearrange("p (a k) -> p a k", a=2)
        vs = big[:, 7680:8768].bitcast(BF16).rearrange("p (b k) -> p b k", b=17)
        scr = big
        h0 = big[:, 4352:4864].rearrange("p (a q s) -> p a q s", a=2, q=16)
        cvst = big[:, 8768:9472].rearrange("p (c s j) -> p c s j", c=NFC, s=NSEQ)
        lb = scr[:, 0:768].rearrange("p (a b) -> p a b", a=3); ft = scr[:, 768:2304].rearrange("p (a b) -> p a b", a=6)
        bblk = scr[:, 2304:2816].rearrange("p (a q m) -> p a q m", a=2, q=2); cblk = scr[:, 2816:3328].rearrange("p (a q m) -> p a q m", a=2, q=2)
        Ssx = big[:, 1024:1024 + 17 * 128 + 1]

        mmA = ps("mmA", [128, 512]); mmB = ps("mmB", [128, 512]); mmC = ps("mmC", [128, 512])
        pT = ps("pT", [128, 8, 128], BF16)
        Sps = ps("Sps", [128, 512]); Ops = ps("Ops", [128, 512])
        acc0 = ps("acc0", [128, 512]); acc1 = ps("acc1", [128, 512])

        op = P.op

        P.dma("sp", lambda h: h.dma_start(out=idf[:], in_=ident_d), writes=["idf"])
        P.dma("sp", lambda h: h.dma_start(out=masks[:], in_=masks_d), writes=["masks"])
        P.dma("pool", lambda h: h.dma_start(out=smask[:], in_=smask_d), writes=["smask"])
        P.dma("sp", lambda h: h.dma_start(out=pcol[:], in_=pcol_d), writes=["pcol"])
        P.dma("sp", lambda h: h.dma_start(out=sink8[:], in_=sinks_d.partition_broadcast(128)), writes=["sink8"])
        P.dma("sp", lambda h: h.dma_start(out=gfb[:], in_=gfin_d.partition_broadcast(128)), writes=["gfb"])
        P.dma("sp", lambda h: h.dma_start(out=lam[:], in_=lam_d), writes=["lam"])
        P.dma("sp", lambda h: h.dma_start(out=h0, in_=h0_d))
        op("pool", lambda h: h.memset(hb[:], 0.0))
        op("pool", lambda h: h.memset(cv[:], 0.0))
        op("pool", lambda h: h.memset(gx[:], 0.0))
        P.dma("pool", lambda h: h.dma_start(out=wi[0][:], in_=w_in.rearrange("(c p) n -> p c n", p=128)))
        P.dma("pool", lambda h: h.dma_start(out=wgl[0][:], in_=w_glu.rearrange("(c p) n -> p c n", p=128)))
        P.dma("pool", lambda h: h.dma_start(out=wo[0][:], in_=w_o.rearrange("(c p) n -> p c n", p=128)))
        P.dma("sp", lambda h: h.dma_start(out=kvs_k[:, 0:124, :], in_=kcache_d[:, 4:128, :]), writes=["kvs_k_a"])
        P.dma("sp", lambda h: h.dma_start(out=kvs_v[:, 0:124, :], in_=vcache_d[:, 4:128, :]), writes=["kvs_v_a"])

        op("dve", lambda h: h.tensor_copy(out=idb[:], in_=idf[:]), ["idf"], ["idb"])
        op("pool", lambda h: h.memset(onesb[:], 1.0), [], ["onesb"])
        op("pool", lambda h: h.memset(epsb[:], EPS), [], ["epsb"])
        op("pool", lambda h: h.memset(kTx[:], 0.0), [], ["kTx"])
        op("pool", lambda h: h.memset(vx[:], 0.0), [], ["vx"])
        op("pool", lambda h: h.memset(car[:], 0.0), [], ["car"])
        op("pool", lambda h: h.memset(gcar[:], 0.0), [], ["gcar"])
        pass
        op("pool", lambda h: h.memset(hfin[:], 0.0))
        op("pool", lambda h: h.memset(sconv[:], 0.0))
        dtl = sb("dtl", [128, 16]); thl = sb("thl", [128, 16]); rho = sb("rho", [128, 16])
        c1 = sb("c1", [128, 16]); s1 = sb("s1", [128, 16]); tmpa = sb("tmpa", [128, 16])
        kiL = sb("kiL", [128, 16], mybir.dt.int32); kfL = sb("kfL", [128, 16])
        kiR = sb("kiR", [128, 256], mybir.dt.int32); kfR = sb("kfR", [128, 256])
        op("act", lambda h: h.activation(out=dtl[:], in_=lam[:, 2, :], func=AF.Exp), ["lam"], ["dtl"])
        op("dve", lambda h: h.tensor_tensor(out=thl[:], in0=lam[:, 1, :], in1=dtl[:], op=ALU.mult), ["lam", "dtl"], ["thl"])
        op("dve", lambda h: h.tensor_tensor(out=tmpa[:], in0=lam[:, 0, :], in1=dtl[:], op=ALU.mult), ["lam", "dtl"], ["tmpa"])
        op("act", lambda h: h.activation(out=rho[:], in_=tmpa[:], func=AF.Exp), ["tmpa"], ["rho"])

        def sincos(eng_tag, th_ap, s_ap, c_ap, tmp_ap, shape_key, ki_ap, kf_ap):
            K = shape_key

            def reduce_(shift, dst_key):
                op("dve", lambda h: h.tensor_scalar(out=tmp_ap, in0=th_ap, scalar1=shift, scalar2=1.0 / (2 * PI), op0=ALU.add, op1=ALU.mult),
                   [K + "th", K + "s", K + "c"], [K + "tmp"])
                op("dve", lambda h: h.tensor_copy(out=ki_ap, in_=tmp_ap), [K + "tmp"], [K + "ki"])
                op("dve", lambda h: h.tensor_copy(out=kf_ap, in_=ki_ap), [K + "ki"], [K + "kf"])
                op("dve", lambda h: h.tensor_scalar(out=tmp_ap, in0=th_ap, scalar1=shift, scalar2=None, op0=ALU.add), [K + "th", K + "ki"], [K + "tmp"])
                op("dve", lambda h: h.scalar_tensor_tensor(out=tmp_ap, in0=kf_ap, scalar=-2 * PI, in1=tmp_ap, op0=ALU.mult, op1=ALU.add),
                   [K + "kf", K + "tmp"], [K + "tmp"])
                op("dve", lambda h: h.tensor_scalar(out=kf_ap, in0=tmp_ap, scalar1=PI, scalar2=None, op0=ALU.is_gt), [K + "tmp"], [K + "kf"])
                op("dve", lambda h: h.scalar_tensor_tensor(out=tmp_ap, in0=kf_ap, scalar=-2 * PI, in1=tmp_ap, op0=ALU.mult, op1=ALU.add),
                   [K + "kf", K + "tmp"], [K + "tmp"])
                op("dve", lambda h: h.tensor_scalar(out=kf_ap, in0=tmp_ap, scalar1=-PI, scalar2=None, op0=ALU.is_lt), [K + "tmp"], [K + "kf"])
                op("dve", lambda h: h.scalar_tensor_tensor(out=tmp_ap, in0=kf_ap, scalar=2 * PI, in1=tmp_ap, op0=ALU.mult, op1=ALU.add),
                   [K + "kf", K + "tmp"], [K + "tmp"])
                op("dve", lambda h: h.tensor_scalar(out=tmp_ap, in0=tmp_ap, scalar1=-PI, scalar2=PI, op0=ALU.max, op1=ALU.min), [K + "tmp"], [K + "tmp"])

            reduce_(0.0, "s")
            op("act", lambda h: h.activation(out=s_ap, in_=tmp_ap, func=AF.Sin), [K + "tmp"], [K + "s"])
            reduce_(0.5 * PI, "c")
            op("act", lambda h: h.activation(out=c_ap, in_=tmp_ap, func=AF.Sin), [K + "tmp"], [K + "c"])

        sincos("L", thl[:], s1[:], c1[:], tmpa[:], "L", kiL[:], kfL[:])
        op("dve", lambda h: h.tensor_tensor(out=are[:], in0=rho[:], in1=c1[:], op=ALU.mult), ["rho", "Lc"], ["are"])
        op("dve", lambda h: h.tensor_tensor(out=aim[:], in0=rho[:], in1=s1[:], op=ALU.mult), ["rho", "Ls"], ["aim"])
        op("pool", lambda h: h.memset(cs[:, :, 0:1], 1.0), [], ["cs"])
        op("pool", lambda h: h.memset(sn[:, :, 0:1], 0.0), [], ["sn"])
        op("dve", lambda h: h.tensor_copy(out=cs[:, :, 1], in_=c1[:]), ["Lc", "cs"], ["cs"])
        op("dve", lambda h: h.tensor_copy(out=sn[:, :, 1], in_=s1[:]), ["Ls", "sn"], ["sn"])
        tA = tt[:].rearrange("p a (b c) -> p (a b) c", c=64); tB = big[:, 3328:4352].rearrange("p (a c) -> p a c", c=64)
        m = 1
        while m < 128:
            cm = cs[:, :, m:m + 1].broadcast_to([128, 16, m]); sm = sn[:, :, m:m + 1].broadcast_to([128, 16, m])
            a_c = cs[:, :, 1:m + 1]; a_s = sn[:, :, 1:m + 1]
            o_c = cs[:, :, m + 1:2 * m + 1]; o_s = sn[:, :, m + 1:2 * m + 1]
            ta = tA[:, :, 0:m]; tb = tB[:, :, 0:m]
            op("dve", lambda h, a_c=a_c, cm=cm, ta=ta: h.tensor_tensor(out=ta, in0=a_c, in1=cm, op=ALU.mult), ["cs", "sn"], ["tA"])
            op("dve", lambda h, a_s=a_s, sm=sm, tb=tb: h.tensor_tensor(out=tb, in0=a_s, in1=sm, op=ALU.mult), ["cs", "sn"], ["tB"])
            op("dve", lambda h, o_c=o_c, ta=ta, tb=tb: h.tensor_tensor(out=o_c, in0=ta, in1=tb, op=ALU.subtract), ["tA", "tB", "sn"], ["cs"])
            op("dve", lambda h, a_c=a_c, sm=sm, ta=ta: h.tensor_tensor(out=ta, in0=a_c, in1=sm, op=ALU.mult), ["cs", "sn"], ["tA"])
            op("dve", lambda h, a_s=a_s, cm=cm, tb=tb: h.tensor_tensor(out=tb, in0=a_s, in1=cm, op=ALU.mult), ["cs", "sn"], ["tB"])
            op("dve", lambda h, o_s=o_s, ta=ta, tb=tb: h.tensor_tensor(out=o_s, in0=ta, in1=tb, op=ALU.add), ["tA", "tB", "cs"], ["sn"])
            m *= 2
        op("pool", lambda h: h.memset(mask64[:], 1.0))
        op("pool", lambda h: h.memset(mask64[:].rearrange("p (s t) -> p s t", t=4)[:, :, 0:1], 0.0))
        a_re_b = are[:].unsqueeze(2).broadcast_to([128, 16, NSEQ]); a_im_b = aim[:].unsqueeze(2).broadcast_to([128, 16, NSEQ])
        tC = sb("tC", [128, 16, NSEQ])
        op("dve", lambda h: h.tensor_tensor(out=ah[:, 0], in0=h0[:, 0], in1=a_re_b, op=ALU.mult), ["h0", "are"], ["ah0"])
        op("dve", lambda h: h.tensor_tensor(out=tC[:], in0=h0[:, 1], in1=a_im_b, op=ALU.mult), ["h0", "aim"], ["tC"])
        op("dve", lambda h: h.tensor_tensor(out=ah[:, 0], in0=ah[:, 0], in1=tC[:], op=ALU.subtract), ["ah0", "tC"], ["ah0"])
        op("dve", lambda h: h.tensor_tensor(out=ah[:, 1], in0=h0[:, 0], in1=a_im_b, op=ALU.mult), ["h0", "aim"], ["ah1"])
        op("dve", lambda h: h.tensor_tensor(out=tC[:], in0=h0[:, 1], in1=a_re_b, op=ALU.mult), ["h0", "are", "ah0"], ["tC"])
        op("dve", lambda h: h.tensor_tensor(out=ah[:, 1], in0=ah[:, 1], in1=tC[:], op=ALU.add), ["ah1", "tC"], ["ah1"])

        for pc in range(8):
            for i in range(3):
                P.dma("sp", lambda h, i=i, pc=pc: h.dma_start(out=lb[:, i, :], in_=lamb_d[i, pc * 256:(pc + 1) * 256].partition_broadcast(128)), writes=["lb"])
            P.dma("sp", lambda h, pc=pc: h.dma_start(out=bblk[:], in_=bblk_d[:, :, pc * 2:(pc + 1) * 2, :]), writes=["bblk"])
            P.dma("sp", lambda h, pc=pc: h.dma_start(out=cblk[:], in_=cblk_d[:, :, pc * 2:(pc + 1) * 2, :]), writes=["cblk"])
            LR, LI, LD = lb[:, 0, :], lb[:, 1, :], lb[:, 2, :]
            f0, f1, f2, f3, f4, f5 = (ft[:, i, :] for i in range(6))
            op("act", lambda h: h.activation(out=f0, in_=LD, func=AF.Exp), ["lb"], ["f0"])
            op("dve", lambda h: h.tensor_tensor(out=f1, in0=LI, in1=f0, op=ALU.mult), ["lb", "f0"], ["Rth"])
            op("dve", lambda h: h.tensor_tensor(out=f2, in0=LR, in1=f0, op=ALU.mult), ["lb", "f0"], ["f2"])
            op("act", lambda h: h.activation(out=f2, in_=f2, func=AF.Exp), ["f2"], ["f2"])
            sincos("R", f1, f3, f4, f5, "R", kiR[:], kfR[:])
            op("dve", lambda h: h.tensor_tensor(out=f4, in0=f4, in1=f2, op=ALU.mult), ["Rc", "f2"], ["Rc"])
            op("dve", lambda h: h.tensor_scalar(out=f4, in0=f4, scalar1=-1.0, scalar2=None, op0=ALU.add), ["Rc"], ["Rc"])
            op("dve", lambda h: h.tensor_tensor(out=f3, in0=f3, in1=f2, op=ALU.mult), ["Rs", "f2"], ["Rs"])
            op("dve", lambda h: h.tensor_tensor(out=f0, in0=LR, in1=LR, op=ALU.mult), ["lb", "Rth", "f2"], ["f0"])
            op("dve", lambda h: h.tensor_tensor(out=f5, in0=LI, in1=LI, op=ALU.mult), ["lb", "Rc"], ["Rtmp"])
            op("dve", lambda h: h.tensor_tensor(out=f0, in0=f0, in1=f5, op=ALU.add), ["f0", "Rtmp"], ["f0"])
            op("dve", lambda h: h.reciprocal(out=f0, in_=f0), ["f0"], ["f0"])
            op("dve", lambda h: h.tensor_tensor(out=f1, in0=f4, in1=LR, op=ALU.mult), ["Rc", "lb", "Rth", "Rtmp"], ["f1"])
            op("dve", lambda h: h.tensor_tensor(out=f5, in0=f3, in1=LI, op=ALU.mult), ["Rs", "lb"], ["Rtmp"])
            op("dve", lambda h: h.tensor_tensor(out=f1, in0=f1, in1=f5, op=ALU.add), ["f1", "Rtmp"], ["f1"])
            op("dve", lambda h: h.tensor_tensor(out=f2, in0=f3, in1=LR, op=ALU.mult), ["Rs", "lb"], ["f2"])
            op("dve", lambda h: h.tensor_tensor(out=f5, in0=f4, in1=LI, op=ALU.mult), ["Rc", "lb", "f1"], ["Rtmp"])
            op("dve", lambda h: h.tensor_tensor(out=f2, in0=f2, in1=f5, op=ALU.subtract), ["f2", "Rtmp"], ["f2"])
            op("dve", lambda h: h.tensor_tensor(out=f1, in0=f1, in1=f0, op=ALU.mult), ["f1", "f0"], ["f1"])
            op("dve", lambda h: h.tensor_tensor(out=f2, in0=f2, in1=f0, op=ALU.mult), ["f2", "f0"], ["f2"])
            Bre_, Bim_ = bblk[:, 0].rearrange("p q m -> p (q m)"), bblk[:, 1].rearrange("p q m -> p (q m)")
            op("dve", lambda h: h.tensor_tensor(out=f3, in0=Bre_, in1=f1, op=ALU.mult), ["bblk", "f1", "Rs"], ["f3"])
            op("dve", lambda h: h.tensor_tensor(out=f4, in0=Bim_, in1=f2, op=ALU.mult), ["bblk", "f2", "Rc"], ["f4"])
            op("dve", lambda h, pc=pc: h.tensor_tensor(out=WBre[:, pc * 2:(pc + 1) * 2, :].rearrange("p q m -> p (q m)"), in0=f3, in1=f4, op=ALU.subtract), ["f3", "f4"], ["WBre"])
            op("dve", lambda h: h.tensor_tensor(out=f3, in0=Bre_, in1=f2, op=ALU.mult), ["bblk", "f2", "WBre"], ["f3"])
            op("dve", lambda h: h.tensor_tensor(out=f4, in0=Bim_, in1=f1, op=ALU.mult), ["bblk", "f1", "WBre"], ["f4"])
            op("dve", lambda h, pc=pc: h.tensor_tensor(out=WBim[:, pc * 2:(pc + 1) * 2, :].rearrange("p q m -> p (q m)"), in0=f3, in1=f4, op=ALU.add), ["f3", "f4"], ["WBim"])
            op("act", lambda h, pc=pc: h.activation(out=WCre[:, pc * 2:(pc + 1) * 2, :], in_=cblk[:, 0], func=AF.Copy), ["cblk"], ["WCre"])
            op("act", lambda h, pc=pc: h.activation(out=WCimn[:, pc * 2:(pc + 1) * 2, :], in_=cblk[:, 1], func=AF.Copy, scale=-1.0), ["cblk"], ["WCimn"])

        P.barrier()
        def rms(src_ap, key_src, n, slot, scale, pn):
            op("act", lambda h: h.activation(out=junk[0:pn, 0:n], in_=src_ap, func=AF.Square, accum_out=ssq[0:pn, slot:slot + 1]),
               [key_src], ["junk", "ssq%d" % slot])
            op("act", lambda h: h.activation(out=rstd[0:pn, slot:slot + 1], in_=ssq[0:pn, slot:slot + 1], func=AF.Sqrt, scale=scale, bias=epsb[0:pn, 0:1]),
               ["ssq%d" % slot, "epsb"], ["rstd%d" % slot])
            op("dve", lambda h: h.reciprocal(out=rstd[0:pn, slot:slot + 1], in_=rstd[0:pn, slot:slot + 1]), ["rstd%d" % slot], ["rstd%d" % slot])

        def mixer(ti, tl):
            sample = (ti == NPT)
            xt = xb[:, tl, :]
            n = 128
            ns = NS if sample else 128
            r0 = ti * 128
            par = ti % 2
            if sample:
                op("pool", lambda h: h.memset(kTs[:], 0.0))
                P.dma("pool", lambda h: h.dma_start(out=kTs[0:64, 0, 0:NSEQ * 128], in_=kcT_d[0:64, :]))
                P.dma("pool", lambda h: h.dma_start(out=kTs[64:128, 1, 0:NSEQ * 128], in_=kcT_d[64:128, :]))
                P.dma("pool", lambda h: h.dma_start(out=vs[:, 0:NSEQ, :], in_=vc_d))
                P.dma("sp", lambda h: h.dma_start(out=cvst, in_=cvst_d))
            P.dma("sp", lambda h: h.dma_start(out=xt[:], in_=xin[r0:r0 + 128, :]))
            rms(xt[:], "xt", D, 0, 1.0 / D, 128)
            op("dve", lambda h: h.tensor_scalar(out=xn[:], in0=xt[:], scalar1=rstd[:, 0:1], scalar2=None, op0=ALU.mult))
            for c in range(8):
                op("pe", lambda h, c=c: h.transpose(out=pT[:, c, :], in_=xn[:, c * 128:(c + 1) * 128], identity=idb[:]))
            for c in range(8):
                op("act", lambda h, c=c: h.activation(out=xnT[:, c, :], in_=pT[:, c, :], func=AF.Copy, scale=col(GMIX + c)))
            W = wi[par]
            for i in range(4):
                bank = mmA if i % 2 == 0 else mmB
                for c in range(8):
                    op("pe", lambda h, i=i, c=c, bank=bank: h.matmul(bank[:, 0:128], lhsT=W[:, c, i * 128:(i + 1) * 128], rhs=xnT[:, c, :],
                                                                   start=(c == 0), stop=(c == 7)))
                op("act", lambda h, i=i, bank=bank: h.activation(out=qT[:, i, :], in_=bank[:, 0:128], func=AF.Copy))
            for c in range(8):
                op("pe", lambda h, c=c: h.matmul(mmC[:, 0:128], lhsT=W[:, c, 512:640], rhs=xnT[:, c, :], start=(c == 0), stop=(c == 7)))
            if sample:
                op("dve", lambda h: h.tensor_copy(out=kTs[0:64, 0, NSEQ * 128:17 * 128], in_=mmC[0:64, 0:128]))
                op("dve", lambda h: h.tensor_copy(out=kTs[64:128, 1, NSEQ * 128:17 * 128], in_=mmC[64:128, 0:128]))
            else:
                op("dve", lambda h: h.tensor_copy(out=kTx[0:64, 0, 128:256], in_=mmC[0:64, 0:128]))
                op("dve", lambda h: h.tensor_copy(out=kTx[64:128, 1, 128:256], in_=mmC[64:128, 0:128]))
            for i in range(4):
                bank = mmA if i % 2 == 0 else mmB
                for c in range(8):
                    op("pe", lambda h, i=i, c=c, bank=bank: h.matmul(bank[:, 0:128], lhsT=W[:, c, 768 + i * 128:768 + (i + 1) * 128], rhs=xnT[:, c, :],
                                                                   start=(c == 0), stop=(c == 7)))
                op("act", lambda h, i=i, bank=bank: h.activation(out=uT[:, i, :], in_=bank[:, 0:128], func=AF.Copy))
            for c in range(8):
                op("pe", lambda h, c=c: h.matmul(mmC[:, 0:256], lhsT=xnT[:, c, :], rhs=W[:, c, 512:768], start=(c == 0), stop=(c == 7)))
            if sample:
                op("dve", lambda h: h.tensor_copy(out=vs[:, 16, :], in_=mmC[:, 128:256]))
            else:
                op("dve", lambda h: h.tensor_copy(out=vx[:, 1, :], in_=mmC[:, 128:256]))
            if sample or ti == NPT - 1:
                op("dve", lambda h: h.tensor_copy(out=kvtok[:], in_=mmC[:, 0:256]))
                if sample:
                    P.dma("sp", lambda h: h.dma_start(out=kvs_k[:, 124:128, :], in_=kvtok[0:NS, 0:128]))
                    P.dma("sp", lambda h: h.dma_start(out=kvs_v[:, 124:128, :], in_=kvtok[0:NS, 128:256]))
                else:
                    P.dma("sp", lambda h: h.dma_start(out=kvp_d, in_=kvtok[:]))

            if not sample:
                mi = 0 if ti == 0 else (1 if ti == 1 else 2)
                for i in range(4):
                    for hh_ in range(2):
                        op("pe", lambda h, i=i, hh_=hh_: h.matmul(Sps[:, hh_ * 256:(hh_ + 1) * 256], lhsT=qT[:, i, :], rhs=kTx[:, hh_, :], start=True, stop=True))
                    for hh_ in range(2):
                        op("dve", lambda h, hh_=hh_, hd=i + 4 * hh_: h.tensor_scalar(out=Sx[:, 0, hh_, 256:257], in0=sink8[:, hd:hd + 1], scalar1=8.0, scalar2=None, op0=ALU.mult))
                    op("dve", lambda h: h.tensor_tensor(out=Sx[:, 0, :, 0:256], in0=Sps[:].rearrange("p (a k) -> p a k", a=2),
                                                        in1=masks[:, mi:mi + 1, :].broadcast_to([128, 2, 256]), op=ALU.add))
                    op("dve", lambda h: h.tensor_reduce(out=mx[:], in_=Sx[:, 0], axis=AX.X, op=ALU.max))
                    op("dve", lambda h: h.tensor_scalar(out=nbias[:], in0=mx[:], scalar1=-0.125, scalar2=None, op0=ALU.mult))
                    for hh_ in range(2):
                        op("act", lambda h, hh_=hh_: h.activation(out=Pb[:, hh_, :], in_=Sx[:, 0, hh_, :], func=AF.Exp, scale=0.125,
                                                                 bias=nbias[:, hh_:hh_ + 1], accum_out=rs[:, hh_:hh_ + 1]))
                    op("dve", lambda h: h.reciprocal(out=rinv[:], in_=rs[:]))
                    for hh_ in range(2):
                        for blk in range(2):
                            op("pe", lambda h, hh_=hh_, blk=blk: h.transpose(out=pT[:, hh_ * 2 + blk, :], in_=Pb[:, hh_, blk * 128:(blk + 1) * 128], identity=idb[:]))
                    op("act", lambda h: h.activation(out=PTs[:], in_=pT[:, 0:4, :], func=AF.Copy))
                    for hh_ in range(2):
                        for blk in range(2):
                            op("pe", lambda h, hh_=hh_, blk=blk: h.matmul(Ops[:, hh_ * 64:(hh_ + 1) * 64], lhsT=PTs[:, hh_ * 2 + blk, :],
                                                                         rhs=vx[:, blk, hh_ * 64:(hh_ + 1) * 64], start=(blk == 0), stop=(blk == 1)))
                    for hh_ in range(2):
                        hd = i + 4 * hh_
                        op("dve", lambda h, hh_=hh_, hd=hd: h.tensor_scalar(out=attn[:, hd * 64:(hd + 1) * 64], in0=Ops[:, hh_ * 64:(hh_ + 1) * 64],
                                                                           scalar1=rinv[:, hh_:hh_ + 1], scalar2=None, op0=ALU.mult))
                op("pool", lambda h: h.tensor_copy(out=kTx[:, :, 0:128], in_=kTx[:, :, 128:256]))
                op("pool", lambda h: h.tensor_copy(out=vx[:, 0, :], in_=vx[:, 1, :]))
            else:
                W17 = 17 * 128
                for hd in range(8):
                    i, hh_ = hd % 4, hd // 4
                    for cb in range(5):
                        c0 = cb * 512
                        cw = min(512, W17 - c0)
                        bank = mmA if cb % 2 == 0 else mmB
                        op("pe", lambda h, i=i, hh_=hh_, c0=c0, cw=cw, bank=bank: h.matmul(bank[:, 0:cw], lhsT=qT[:, i, :], rhs=kTs[:, hh_, c0:c0 + cw], start=True, stop=True))
                        op("dve", lambda h, c0=c0, cw=cw, bank=bank: h.tensor_tensor(out=Ssx[:, c0:c0 + cw], in0=bank[:, 0:cw], in1=smask[:, c0:c0 + cw], op=ALU.add))
                    op("dve", lambda h, hd=hd: h.tensor_scalar(out=Ssx[:, W17:W17 + 1], in0=sink8[:, hd:hd + 1], scalar1=8.0, scalar2=None, op0=ALU.mult))
                    op("dve", lambda h: h.tensor_reduce(out=mx[:, 0:1], in_=Ssx[:], axis=AX.X, op=ALU.max))
                    op("dve", lambda h: h.tensor_scalar(out=nbias[:, 0:1], in0=mx[:, 0:1], scalar1=-0.125, scalar2=None, op0=ALU.mult))
                    op("act", lambda h: h.activation(out=Psb[:], in_=Ssx[:], func=AF.Exp, scale=0.125, bias=nbias[:, 0:1], accum_out=rs[:, 0:1]))
                    op("dve", lambda h: h.reciprocal(out=rinv[:, 0:1], in_=rs[:, 0:1]))
                    for g8 in range(3):
                        nb_ = 8 if g8 < 2 else 1
                        for b in range(nb_):
                            blk = g8 * 8 + b
                            op("pe", lambda h, b=b, blk=blk: h.transpose(out=pT[:, b, :], in_=Psb[:, blk * 128:(blk + 1) * 128], identity=idb[:]))
                        op("act", lambda h, g8=g8, nb_=nb_: h.activation(out=PTss[:, g8 * 8:g8 * 8 + nb_, :], in_=pT[:, 0:nb_, :], func=AF.Copy))
                    for blk in range(17):
                        op("pe", lambda h, blk=blk, hh_=hh_: h.matmul(Ops[:, 0:64], lhsT=PTss[:, blk, :], rhs=vs[:, blk, hh_ * 64:(hh_ + 1) * 64],
                                                                     start=(blk == 0), stop=(blk == 16)))
                    op("dve", lambda h, hd=hd: h.tensor_scalar(out=attn[:, hd * 64:(hd + 1) * 64], in0=Ops[:, 0:64], scalar1=rinv[:, 0:1], scalar2=None, op0=ALU.mult))
            rms(attn[0:n, :], "attn", 512, 1, 1.0 / 512, n)
            op("dve", lambda h: h.tensor_scalar(out=anb[0:n, :], in0=attn[0:n, :], scalar1=rstd[0:n, 1:2], scalar2=None, op0=ALU.mult), ["attn", "rstd1"], ["anb"])
            for c in range(4):
                op("pe", lambda h, c=c: h.transpose(out=pT[:, c, 0:n], in_=anb[0:n, c * 128:(c + 1) * 128], identity=idb[0:n, 0:n]), ["anb", "idb"], ["pT"])
            for c in range(4):
                op("act", lambda h, c=c: h.activation(out=mixT[:, c, 0:n], in_=pT[:, c, 0:n], func=AF.Copy, scale=col(GATT + c)), ["pT", "pcol"], ["mixT"])

            for c4 in range(4):
                for qq in range(4):
                    q = c4 * 4 + qq
                    op("pe", lambda h, q=q, c4=c4: h.matmul(mmA[:, 0:n], lhsT=WBre[:, q, :], rhs=uT[:, c4, 0:n], start=True, stop=True), ["WBre", "uT"], ["mmA"])
                    op("pe", lambda h, q=q, c4=c4: h.matmul(mmA[:, 128:128 + n], lhsT=WBim[:, q, :], rhs=uT[:, c4, 0:n], start=True, stop=True), ["WBim", "uT"], ["mmA"])
                    op("act", lambda h: h.activation(out=Bsb[:, :, 0:n], in_=mmA[:, 0:256].rearrange("p (a k) -> p a k", a=2)[:, :, 0:n], func=AF.Copy), ["mmA"], ["Bsb"])
                    if sample:
                        for ri in range(2):
                            v4 = Bsb[:, ri, 0:NS].rearrange("p (s t) -> p s t", t=4)[:, :, 0]
                            op("dve", lambda h, v4=v4, ri=ri, q=q: h.tensor_tensor(out=v4, in0=v4, in1=ah[:, ri, q, :], op=ALU.add), ["Bsb", "ah%d" % ri], ["Bsb"])
                        csq = cs[:, q:q + 1, 1:5].broadcast_to([128, NSEQ, 4]); snq = sn[:, q:q + 1, 1:5].broadcast_to([128, NSEQ, 4])

                        def v3(ap):
                            return ap.rearrange("p (s t) -> p s t", t=4)
                    else:
                        csq = cs[:, q, 1:129]; snq = sn[:, q, 1:129]

                        def v3(ap):
                            return ap
                    Bre, Bim = v3(Bsb[:, 0, 0:ns]), v3(Bsb[:, 1, 0:ns])
                    T = [v3(tt[:, k, 0:ns]) for k in range(8)]
                    op("dve", lambda h, csq=csq, Bre=Bre, T=T: h.tensor_tensor(out=T[0], in0=Bre, in1=csq, op=ALU.mult), ["Bsb", "cs"], ["t0"])
                    op("pool", lambda h, snq=snq, Bim=Bim, T=T: h.tensor_tensor(out=T[1], in0=Bim, in1=snq, op=ALU.mult), ["Bsb", "sn"], ["t1"])
                    op("dve", lambda h, csq=csq, Bim=Bim, T=T: h.tensor_tensor(out=T[2], in0=Bim, in1=csq, op=ALU.mult), ["Bsb", "cs"], ["t2"])
                    op("pool", lambda h, snq=snq, Bre=Bre, T=T: h.tensor_tensor(out=T[3], in0=Bre, in1=snq, op=ALU.mult), ["Bsb", "sn"], ["t3"])
                    op("dve", lambda h, T=T: h.tensor_tensor(out=v3(rr[:, 0, 0:ns]), in0=T[0], in1=T[1], op=ALU.add), ["t0", "t1"], ["rr0"])
                    op("pool", lambda h, T=T: h.tensor_tensor(out=v3(rr[:, 1, 0:ns]), in0=T[2], in1=T[3], op=ALU.subtract), ["t2", "t3"], ["rr1"])
                    for ri in range(2):
                        if sample:
                            if ri == 0:
                                op("dve", lambda h, q=q: h.tensor_scalar(out=dtmp[:], in0=mask64[:], scalar1=rho[:, q:q + 1], scalar2=None, op0=ALU.mult))
                            op("dve", lambda h, ri=ri, q=q: h.tensor_tensor_scan(out=vv[:, ri, 0:ns], data0=dtmp[:], data1=rr[:, ri, 0:ns], initial=0.0,
                                                                              op0=ALU.mult, op1=ALU.add), ["rr%d" % ri, "decs"], ["vv%d" % ri])
                        else:
                            op("dve", lambda h, ri=ri, q=q: h.tensor_tensor_scan(out=vv[:, ri, 0:ns], data0=rho[:, q:q + 1].broadcast_to([128, 128]), data1=rr[:, ri, 0:ns], initial=car[:, ri, q:q + 1],
                                                                              op0=ALU.mult, op1=ALU.add), ["rr%d" % ri, "dec", "car"], ["vv%d" % ri])
                    Vre, Vim = v3(vv[:, 0, 0:ns]), v3(vv[:, 1, 0:ns])
                    op("dve", lambda h, csq=csq, Vre=Vre, T=T: h.tensor_tensor(out=T[4], in0=Vre, in1=csq, op=ALU.mult), ["vv0", "cs"], ["t4"])
                    op("pool", lambda h, snq=snq, Vim=Vim, T=T: h.tensor_tensor(out=T[5], in0=Vim, in1=snq, op=ALU.mult), ["vv1", "sn"], ["t5"])
                    op("pool", lambda h, snq=snq, Vre=Vre, T=T: h.tensor_tensor(out=T[6], in0=Vre, in1=snq, op=ALU.mult), ["vv0", "sn"], ["t6"])
                    op("dve", lambda h, csq=csq, Vim=Vim, T=T: h.tensor_tensor(out=T[7], in0=Vim, in1=csq, op=ALU.mult), ["vv1", "cs"], ["t7"])
                    op("pool", lambda h, T=T: h.tensor_tensor(out=v3(hh[:, 0, 0:ns]), in0=T[4], in1=T[5], op=ALU.subtract), ["t4", "t5"], ["hh0"])
                    op("dve", lambda h, T=T: h.tensor_tensor(out=v3(hh[:, 1, 0:ns]), in0=T[6], in1=T[7], op=ALU.add), ["t6", "t7"], ["hh1"])
                    for ri in range(2):
                        op("act", lambda h, ri=ri, qq=qq: h.activation(out=hb[:, qq, ri, 0:ns], in_=hh[:, ri, 0:ns], func=AF.Copy), ["hh%d" % ri], ["hb"])
                        if sample:
                            op("act", lambda h, ri=ri, q=q: h.activation(out=hfin[:, ri, q, 0:NSEQ], in_=hh[:, ri, 0:NS].rearrange("p (s t) -> p s t", t=4)[:, :, 3], func=AF.Copy),
                               ["hh%d" % ri], ["hfin"])
                        else:
                            op("act", lambda h, ri=ri, q=q: h.activation(out=car[:, ri, q:q + 1], in_=hh[:, ri, 127:128], func=AF.Copy), ["hh%d" % ri], ["car"])
                for qq in range(4):
                    q = c4 * 4 + qq
                    op("pe", lambda h, q=q, qq=qq: h.matmul(mmB[:, 0:n], lhsT=WCre[:, q, :], rhs=hb[:, qq, 0, 0:n], start=(qq == 0), stop=False), ["WCre", "hb"], ["mmB"])
                    op("pe", lambda h, q=q, qq=qq: h.matmul(mmB[:, 0:n], lhsT=WCimn[:, q, :], rhs=hb[:, qq, 1, 0:n], start=False, stop=(qq == 3)), ["WCimn", "hb"], ["mmB"])
                op("dve", lambda h, c4=c4: h.scalar_tensor_tensor(out=yv[:, 0:n], in0=uT[:, c4, 0:n], scalar=col(DSK + c4), in1=mmB[:, 0:n], op0=ALU.mult, op1=ALU.add),
                   ["uT", "mmB", "pcol"], ["yv"])
                op("act", lambda h, c4=c4: h.activation(out=gl32[:, c4, 0:n], in_=yv[:, 0:n], func=AF.Gelu), ["yv"], ["gl32"])
                op("pool", lambda h, c4=c4: h.tensor_copy(out=glb[:, c4, 0:n], in_=gl32[:, c4, 0:n]), ["gl32"], ["glb"])
            if ti == NPT - 1:
                op("act", lambda h: h.activation(out=hfin[:, :, :, NSEQ], in_=car[:], func=AF.Copy), ["car"], ["hfin"])
            for oc in range(4):
                for c4 in range(4):
                    op("pe", lambda h, oc=oc, c4=c4: h.matmul(mmC[:, 0:n], lhsT=wgl[par][:, c4, oc * 128:(oc + 1) * 128], rhs=glb[:, c4, 0:n], start=(c4 == 0), stop=(c4 == 3)),
                       ["glb", "wgl"], ["mmC"])
                op("act", lambda h, oc=oc: h.activation(out=sg[:, 0:n], in_=mmC[:, 0:n], func=AF.Sigmoid, bias=col(BGLU + oc)), ["mmC", "pcol"], ["sg"])
                op("dve", lambda h, oc=oc: h.tensor_tensor(out=gl32[:, oc, 0:n], in0=gl32[:, oc, 0:n], in1=sg[:, 0:n], op=ALU.mult), ["gl32", "sg", "glb"], ["gl32"])
                op("act", lambda h, oc=oc: h.activation(out=sqb[:, oc, 0:n], in_=gl32[:, oc, 0:n], func=AF.Square), ["gl32"], ["sqb"])
            for oc in range(4):
                op("pe", lambda h, oc=oc: h.matmul(mmC[:, 0:n], lhsT=onesb[:], rhs=sqb[:, oc, 0:n], start=(oc == 0), stop=(oc == 3)), ["sqb", "onesb"], ["mmC"])
            op("act", lambda h: h.activation(out=rsb[:, 0:n], in_=mmC[:, 0:n], func=AF.Sqrt, scale=1.0 / 512, bias=epsb[:, 0:1]), ["mmC", "epsb"], ["rsb"])
            op("dve", lambda h: h.reciprocal(out=rsb[:, 0:n], in_=rsb[:, 0:n]), ["rsb"], ["rsb"])
            for oc in range(4):
                op("dve", lambda h, oc=oc: h.scalar_tensor_tensor(out=mixT[:, 4 + oc, 0:n], in0=gl32[:, oc, 0:n], scalar=col(GSSM + oc), in1=rsb[:, 0:n], op0=ALU.mult, op1=ALU.mult),
                   ["gl32", "rsb", "pcol"], ["mixT"])

            for hf in range(2):
                acc, ak = (acc0, "acc0") if hf == 0 else (acc1, "acc1")
                for c in range(8):
                    op("pe", lambda h, hf=hf, c=c, acc=acc: h.matmul(acc[0:n, :], lhsT=mixT[:, c, 0:n], rhs=wo[par][:, c, hf * 512:(hf + 1) * 512], start=(c == 0), stop=(c == 7)),
                       ["mixT", "wo"], [ak])
                op("dve", lambda h, hf=hf, acc=acc: h.tensor_tensor(out=xt[0:n, hf * 512:(hf + 1) * 512], in0=xt[0:n, hf * 512:(hf + 1) * 512], in1=acc[0:n, :], op=ALU.add),
                   ["xt", ak], ["xt"])
            rms(xt[0:n, :], "xt", D, 2, 1.0 / D, n)
            op("dve", lambda h: h.tensor_scalar(out=xn[0:n, :], in0=xt[0:n, :], scalar1=rstd[0:n, 2:3], scalar2=None, op0=ALU.mult), ["xt", "rstd2"], ["xn"])
            for c in range(8):
                op("pe", lambda h, c=c: h.transpose(out=pT[:, c, 0:n], in_=xn[0:n, c * 128:(c + 1) * 128], identity=idb[0:n, 0:n]), ["xn", "idb"], ["pT"])
            for c in range(8):
                op("act", lambda h, c=c: h.activation(out=xn2T[:, c, tl * 128:(tl + 1) * 128], in_=pT[:, c, :], func=AF.Copy, scale=col(GFFN + c)))

        def ffn(tiles_):
            sample = (tiles_[0] == NPT)
            ntl = len(tiles_)
            nb = ntl * 128
            hTv = hTsm if sample else hTall
            for ch in range(NFC):
                b3 = ch % 2
                P.dma("pool", lambda h, ch=ch, b3=b3: h.dma_start(out=wg[b3][:], in_=w_gate[:, ch * 128:(ch + 1) * 128].rearrange("(c p) n -> p c n", p=128)))
                P.dma("pool", lambda h, ch=ch, b3=b3: h.dma_start(out=wu[b3][:], in_=w_up[:, ch * 128:(ch + 1) * 128].rearrange("(c p) n -> p c n", p=128)))
                for c in range(8):
                    op("pe", lambda h, c=c, b3=b3: h.matmul(mmA[:, 0:nb], lhsT=wg[b3][:, c, :], rhs=xn2T[:, c, 0:nb], start=(c == 0), stop=(c == 7)))
                for c in range(8):
                    op("pe", lambda h, c=c, b3=b3: h.matmul(mmB[:, 0:nb], lhsT=wu[b3][:, c, :], rhs=xn2T[:, c, 0:nb], start=(c == 0), stop=(c == 7)))
                w0, w1, w2, bb = col(CVW + ch * 3), col(CVW + ch * 3 + 1), col(CVW + ch * 3 + 2), col(CVB + ch)
                cvb, slb, gxb = cv[:, 0, :], sl[:, 0, :], gx[:, b3, :]
                if sample:
                    op("pool", lambda h, ch=ch: h.tensor_copy(out=gxs[:, :, 0:2], in_=cvst[:, ch, :, :]))
                    op("act", lambda h: h.activation(out=gxs[:, :, 2:6], in_=mmA[:, 0:NS].rearrange("p (s t) -> p s t", t=4), func=AF.Copy))
                    g0, g1, g2 = gxs[:, :, 0:4], gxs[:, :, 1:5], gxs[:, :, 2:6]
                    cvv = cvb[:, 0:NS].rearrange("p (s t) -> p s t", t=4)
                    op("pool", lambda h, ch=ch: h.tensor_copy(out=sconv[:, ch, :, :], in_=gxs[:, :, 4:6]))
                else:
                    op("pool", lambda h, ch=ch, gxb=gxb: h.tensor_copy(out=gxb[:, 0:2], in_=gcar[:, ch, :]))
                    op("act", lambda h, gxb=gxb: h.activation(out=gxb[:, 2:2 + nb], in_=mmA[:, 0:nb], func=AF.Copy))
                    g0, g1, g2 = gxb[:, 0:nb], gxb[:, 1:1 + nb], gxb[:, 2:2 + nb]
                    cvv = cvb[:, 0:nb]
                    op("pool", lambda h, ch=ch, gxb=gxb: h.tensor_copy(out=gcar[:, ch, :], in_=gxb[:, nb:nb + 2]))
                op("dve", lambda h, g2=g2, cvv=cvv, w2=w2, bb=bb: h.tensor_scalar(out=cvv, in0=g2, scalar1=w2, scalar2=bb, op0=ALU.mult, op1=ALU.add))
                op("dve", lambda h, g1=g1, cvv=cvv, w1=w1: h.scalar_tensor_tensor(out=cvv, in0=g1, scalar=w1, in1=cvv, op0=ALU.mult, op1=ALU.add))
                op("dve", lambda h, g0=g0, cvv=cvv, w0=w0: h.scalar_tensor_tensor(out=cvv, in0=g0, scalar=w0, in1=cvv, op0=ALU.mult, op1=ALU.add))
                op("act", lambda h, cvb=cvb, slb=slb: h.activation(out=slb[:, 0:nb], in_=cvb[:, 0:nb], func=AF.Silu))
                op("dve", lambda h, ch=ch, slb=slb: h.tensor_tensor(out=hTv[:, ch, 0:nb], in0=slb[:, 0:nb], in1=mmB[:, 0:nb], op=ALU.mult))
            accs = [acc0, acc1, Sps, Ops]
            k3 = 0
            for hf in range(2):
                for ch in range(NFC):
                    wdb = wd[k3 % 3]
                    k3 += 1
                    P.dma("pool", lambda h, ch=ch, hf=hf, wdb=wdb: h.dma_start(out=wdb[:], in_=w_down[ch * 128:(ch + 1) * 128, hf * 512:(hf + 1) * 512]))
                    for tl in range(ntl):
                        op("pe", lambda h, tl=tl, ch=ch, wdb=wdb: h.matmul(accs[tl][:, :], lhsT=hTv[:, ch, tl * 128:(tl + 1) * 128], rhs=wdb[:],
                                                                         start=(ch == 0), stop=(ch == NFC - 1)))
                for tl in range(ntl):
                    op("dve", lambda h, tl=tl, hf=hf: h.tensor_tensor(out=xb[:, tl, hf * 512:(hf + 1) * 512], in0=xb[:, tl, hf * 512:(hf + 1) * 512], in1=accs[tl][:, :], op=ALU.add))
            for tl, ti in enumerate(tiles_):
                xt = xb[:, tl, :]
                rms(xt, "xt", D, 3, 1.0 / D, 128)
                op("dve", lambda h, xt=xt: h.scalar_tensor_tensor(out=xt, in0=xt, scalar=rstd[:, 3:4], in1=gfb[:], op0=ALU.mult, op1=ALU.mult))
                P.dma("sp", lambda h, xt=xt, ti=ti: h.dma_start(out=y_d[ti * 128:(ti + 1) * 128, :], in_=xt))

        if tiles is None:
            blocks = [[0, 1, 2, 3], [4, 5, 6, 7], [8, 9, 10, 11], [12, 13, 14, 15], [16], [NPT]]
        else:
            blocks = tiles
        for blk_ in blocks:
            for tl, ti in enumerate(blk_):
                mixer(ti, tl)
            ffn(blk_)

        P.dma("sp", lambda h: h.dma_start(out=hfin_d, in_=hfin[:]), reads=["hfin"], writes=["hfin_d"])
        P.dma("sp", lambda h: h.dma_start(out=pconv_d, in_=gcar[:]), reads=["gcar"], writes=["pconv_d"])
        P.dma("sp", lambda h: h.dma_start(out=sconv_d, in_=sconv[:]), reads=["sconv"], writes=["sconv_d"])
        P.limit = None
        P.barrier()

        with nc.Block() as block:
            @block.tensor
            def _(h):
                P.run("pe", h, sems)

            @block.scalar
            def _(h):
                P.run("act", h, sems)

            @block.vector
            def _(h):
                P.run("dve", h, sems)

            @block.gpsimd
            def _(h):
                P.run("pool", h, sems)

            @block.sync
            def _(h):
                P.run("sp", h, sems)
    return nc


def _consts():
    ident = np.eye(128, dtype=np.float32)
    i = np.arange(128)[:, None]
    c = np.arange(256)[None, :]
    full = np.where(((c < 128) & (c > i)) | ((c >= 128) & (c - 128 <= i)), 0.0, MASKV)
    m1 = np.where(((c < 128) & (c > i) & (c >= NPAD)) | ((c >= 128) & (c - 128 <= i)), 0.0, MASKV)
    m0 = np.where((c >= 128) & (c - 128 <= i) & (c - 128 >= NPAD), 0.0, MASKV)
    masks = np.stack([m0, m1, full], axis=1).astype(np.float32)
    sm = np.full((128, 17 * 128), MASKV, np.float32)
    for s in range(NSEQ):
        for t in range(4):
            r = s * 4 + t
            sm[r, s * 128 + t + 1:(s + 1) * 128] = 0.0
            sm[r, 2048 + s * 4:2048 + s * 4 + t + 1] = 0.0
    return ident, masks, sm


def prep_inputs(x_prompt, x_sample, cache_k_win, cache_v_win, state_ssm_re, state_ssm_im, state_conv,
           meta_tokens, g_mix, w_in, sinks, lam_re, lam_im, log_dt, b_re, b_im, c_re, c_im, d_skip,
           w_glu, b_glu, g_attn_out, g_ssm_out, w_o, g_ffn, w_gate, w_up, conv_w, conv_b, w_down,
           g_final):
    f32 = np.float32
    A = lambda a: np.ascontiguousarray(np.asarray(a, dtype=f32))
    x_prompt, x_sample = A(x_prompt), A(x_sample)
    ident, masks, smask = _consts()
    w_in0 = A(w_in)[0]
    perm = []
    for i in range(4):
        perm += list(range(i * 64, (i + 1) * 64)) + list(range((4 + i) * 64, (5 + i) * 64))
    w_in_p = np.ascontiguousarray(np.concatenate([w_in0[:, perm], w_in0[:, 512:]], axis=1))
    pcol = np.zeros((128, 128), f32)
    pcol[:, 0:8] = A(g_mix)[0].reshape(8, 128).T
    pcol[:, 8:16] = A(g_ffn)[0].reshape(8, 128).T
    pcol[:, 16:20] = A(g_attn_out)[0].reshape(4, 128).T
    pcol[:, 20:24] = A(g_ssm_out)[0].reshape(4, 128).T
    pcol[:, 24:28] = A(b_glu)[0].reshape(4, 128).T
    pcol[:, 28:32] = A(d_skip)[0].reshape(4, 128).T
    cw = A(conv_w)[0].reshape(3, NFC, 128)
    pcol[:, 32:98] = cw.transpose(2, 1, 0).reshape(128, 66)
    pcol[:, 98:120] = A(conv_b)[0].reshape(NFC, 128).T
    lr, li, ld = A(lam_re)[0], A(lam_im)[0], A(log_dt)[0]
    ldx = np.repeat(ld[:, None], 64, axis=1)

    def pl(a):
        return a.reshape(16, 2, 64).transpose(1, 2, 0).reshape(128, 16)

    lam = np.ascontiguousarray(np.stack([pl(lr), pl(li), pl(ldx)], axis=1))
    lamb = np.ascontiguousarray(np.stack([lr.reshape(-1), li.reshape(-1), ldx.reshape(-1)], axis=0))
    bre, bim, cre, cim = A(b_re)[0], A(b_im)[0], A(c_re)[0], A(c_im)[0]
    bblk = np.zeros((128, 2, 16, 128), f32)
    cblk = np.zeros((128, 2, 16, 128), f32)
    for q in range(16):
        for j2 in range(2):
            g = 2 * q + j2
            g8 = g % 8
            rows = slice(g8 * 16, g8 * 16 + 16)
            cols = slice(j2 * 64, j2 * 64 + 64)
            bblk[rows, 0, q, cols] = bre[g].T
            bblk[rows, 1, q, cols] = bim[g].T
            cblk[cols, 0, q, rows] = cre[g].T
            cblk[cols, 1, q, rows] = cim[g].T
    sre, sim_ = A(state_ssm_re)[0], A(state_ssm_im)[0]
    ck, cvv = A(cache_k_win)[0].reshape(128, 128, 128), A(cache_v_win)[0].reshape(128, 128, 128)
    sc = A(state_conv)[0]
    meta = A(meta_tokens)
    in_maps = []
    for c in range(NCORES):
        xin = np.zeros((NPT * 128 + 128, D), f32)
        xin[NPAD:128] = meta
        xin[128:NPT * 128] = x_prompt[c]
        xin[NPT * 128:NPT * 128 + NS] = x_sample[c * NSEQ:(c + 1) * NSEQ].reshape(NS, D)
        sl_ = slice(c * NSEQ, (c + 1) * NSEQ)

        def hl(a):
            return a.reshape(NSEQ, 16, 2, 64).transpose(2, 3, 1, 0).reshape(128, 16, NSEQ)

        h0 = np.ascontiguousarray(np.stack([hl(sre[sl_]), hl(sim_[sl_])], axis=1))
        cvst = np.ascontiguousarray(sc[sl_].reshape(NSEQ, 2, NFC, 128).transpose(3, 2, 0, 1))
        kc, vc = ck[sl_], cvv[sl_]
        kcT = np.ascontiguousarray(kc.transpose(2, 0, 1).reshape(128, NSEQ * 128))
        vcl = np.ascontiguousarray(vc.transpose(1, 0, 2))
        in_maps.append(dict(
            xin=xin, w_in=w_in_p, w_o=A(w_o)[0], w_glu=A(w_glu)[0], w_gate=A(w_gate)[0], w_up=A(w_up)[0],
            w_down=A(w_down)[0], ident=ident, masks=masks, smask=smask, pcol=pcol, sinks=A(sinks)[0],
            gfin=A(g_final), lam=lam, lamb=lamb, bblk=bblk, cblk=cblk, h0=h0, cvst=cvst, kcT=kcT, vc=vcl,
            kcache=np.ascontiguousarray(kc), vcache=np.ascontiguousarray(vc)))
    return in_maps


def assemble(R):
    f32 = np.float32
    y_prompt = np.stack([R[c]["y"][128:NPT * 128] for c in range(NCORES)])
    y_sample = np.concatenate([R[c]["y"][NPT * 128:NPT * 128 + NS].reshape(NSEQ, 4, D) for c in range(NCORES)])
    p_k = np.stack([R[c]["kvp"][:, 0:128].reshape(128, 2, 64) for c in range(NCORES)])[None]
    p_v = np.stack([R[c]["kvp"][:, 128:256].reshape(128, 2, 64) for c in range(NCORES)])[None]
    s_k = np.concatenate([R[c]["kvs_k"].reshape(NSEQ, 128, 2, 64) for c in range(NCORES)])[None]
    s_v = np.concatenate([R[c]["kvs_v"].reshape(NSEQ, 128, 2, 64) for c in range(NCORES)])[None]

    def unh(a):
        nn = a.shape[-1]
        return a.reshape(2, 64, 16, nn).transpose(3, 2, 0, 1).reshape(nn, 32, 64)

    p_re = np.stack([unh(R[c]["hfin"][:, 0, :, NSEQ:])[0] for c in range(NCORES)])[None]
    p_im = np.stack([unh(R[c]["hfin"][:, 1, :, NSEQ:])[0] for c in range(NCORES)])[None]
    s_re = np.concatenate([unh(R[c]["hfin"][:, 0, :, :NSEQ]) for c in range(NCORES)])[None]
    s_im = np.concatenate([unh(R[c]["hfin"][:, 1, :, :NSEQ]) for c in range(NCORES)])[None]
    p_conv = np.stack([R[c]["pconv"].transpose(2, 1, 0).reshape(2, FF) for c in range(NCORES)])[None]
    s_conv = np.concatenate([R[c]["sconv"].transpose(2, 3, 1, 0).reshape(NSEQ, 2, FF) for c in range(NCORES)])[None]
    out = (y_prompt, y_sample, p_k, p_v, p_re, p_im, p_conv, s_k, s_v, s_re, s_im, s_conv)
    return tuple(np.ascontiguousarray(o, dtype=f32) for o in out)


def kernel(**inputs):
    in_maps = prep_inputs(**inputs)
    nc = build_nc()
    res = run_bass_kernel_spmd(nc, in_maps, core_ids=list(range(NCORES)))
    return assemble(res.results)
```

```python
import numpy as np
from contextlib import ExitStack
import concourse.bass as bass
import concourse.mybir as mybir
from concourse.bass_utils import run_bass_kernel_spmd

F32 = mybir.dt.float32
BF16 = mybir.dt.bfloat16
ALU = mybir.AluOpType
AF = mybir.ActivationFunctionType
AX = mybir.AxisListType

ENGS = ("pe", "act", "dve", "pool", "sp")
NDMASEM = 12
NCORES = 8
D = 1024
NPT = 17
NPAD = 112
NS = 64
NSEQ = 16
FF = 2816
NFC = 22
EPS = 1e-5
PI = float(np.pi)
MASKV = -30000.0


class _Recorder:
    def __init__(self):
        self.call = None

    def __getattr__(self, name):
        def f(*args, **kwargs):
            self.call = (name, args, kwargs)
            return self
        return f


class Prog:
    def __init__(self):
        self.streams = {e: [] for e in ENGS}
        self.count = {e: 0 for e in ENGS}
        self.waited = {e: {} for e in ENGS}
        self.regs = {}
        self.nrec = 0
        self.limit = None
        self.dma_rr = {"sp": 0, "pool": 0}
        self.dma_cnt = {}

    @staticmethod
    def _region(ap):
        dims = [(int(st), int(sz)) for st, sz in ap.ap]
        off = int(ap.offset)
        esz = mybir.dt.size(ap.dtype)
        space = str(ap.space)
        if space == "DRAM":
            ext = sum((sz - 1) * abs(st) for st, sz in dims)
            return (0, 1, off, off + ext)
        pst, npart = dims[0]
        pst = max(pst, 1)
        p0, f0 = off // pst, off % pst
        ext = sum((sz - 1) * abs(st) for st, sz in dims[1:])
        f0, ext = f0 * esz, ext * esz + esz - 1
        if space == "PSUM":
            return (0, 128, 0, 1 << 30)
        return (p0, p0 + npart, f0, f0 + ext)

    @staticmethod
    def _is_ap(v):
        return hasattr(v, "tensor") and hasattr(v, "ap") and hasattr(v, "offset")

    def _record(self, fn):
        rec = _Recorder()
        fn(rec)
        name, args, kwargs = rec.call
        acc = []
        for i, a in enumerate(args):
            if self._is_ap(a):
                acc.append((a, i == 0))
        for k, v in kwargs.items():
            if self._is_ap(v):
                acc.append((v, k in ("out", "accum_out")))
        out = []
        for ap, w in acc:
            if str(ap.space) == "PSUM":
                w = True
            out.append((ap.tensor.name, self._region(ap), w))
        self._last_call = rec.call
        return out

    def _deps(self, eng, acc):
        deps = {}
        for name, R, w in acc:
            for (R2, w2), evs in self.regs.get(name, {}).items():
                if not (w or w2):
                    continue
                if R[0] < R2[1] and R2[0] < R[1] and R[2] <= R2[3] and R2[2] <= R[3]:
                    for s_, v in evs.items():
                        if s_ == eng and eng == "pe":
                            continue
                        if deps.get(s_, 0) < v:
                            deps[s_] = v
        out = []
        wd = self.waited[eng]
        for s_, v in deps.items():
            if wd.get(s_, 0) < v:
                wd[s_] = v
                out.append((s_, v))
        return out

    def _commit(self, ev, acc):
        s_, v = ev
        for name, R, w in acc:
            d = self.regs.setdefault(name, {})
            if w:
                for key in [k for k in d if k[0][0] >= R[0] and k[0][1] <= R[1] and k[0][2] >= R[2] and k[0][3] <= R[3]]:
                    del d[key]
            e = d.setdefault((R, w), {})
            if e.get(s_, 0) < v:
                e[s_] = v

    def op(self, eng, fn, reads=(), writes=()):
        self.nrec += 1
        if self.limit is not None and self.nrec > self.limit:
            return
        acc = self._record(fn)
        deps = self._deps(eng, acc)
        self.count[eng] += 1
        ev = (eng, self.count[eng])
        self.streams[eng].append((deps, self._last_call, (eng, 1)))
        self._commit(ev, acc)

    def dma(self, eng, fn, reads=(), writes=()):
        self.nrec += 1
        if self.limit is not None and self.nrec > self.limit:
            return
        acc = self._record(fn)
        k = self.dma_rr[eng]
        self.dma_rr[eng] = (k + 1) % NDMASEM
        sname = "dma%s%d" % (eng, k)
        deps = self._deps(eng, acc)
        prev = self.dma_cnt.get(sname, 0) * 16
        if prev and self.waited[eng].get(sname, 0) < prev:
            self.waited[eng][sname] = prev
            deps.append((sname, prev))
        self.dma_cnt[sname] = self.dma_cnt.get(sname, 0) + 1
        ev = (sname, self.dma_cnt[sname] * 16)
        self.streams[eng].append((deps, self._last_call, (sname, 16)))
        self._commit(ev, acc)

    def barrier(self):
        evs = [(e, self.count[e]) for e in ENGS if self.count[e]]
        evs += [(k, c * 16) for k, c in self.dma_cnt.items()]
        for e in ENGS:
            deps = []
            for s, v in evs:
                if s == e:
                    continue
                if self.waited[e].get(s, 0) < v:
                    self.waited[e][s] = v
                    deps.append((s, v))
            if deps:
                self.streams[e].append((deps, None, None))

    def run(self, eng, h, sems):
        for deps, fn, inc in self.streams[eng]:
            for s, v in deps:
                h.wait_ge(sems[s], v)
            if fn is not None:
                name, args, kwargs = fn
                getattr(h, name)(*args, **kwargs).then_inc(sems[inc[0]], inc[1])


def build_nc(tiles=None, limit=None):
    nc = bass.Bass("TRN2", target_bir_lowering=False)

    def din(name, shape, dt=F32):
        return nc.dram_tensor(name, list(shape), dt, kind="ExternalInput").ap()

    def dout(name, shape, dt=F32):
        return nc.dram_tensor(name, list(shape), dt, kind="ExternalOutput").ap()

    xin = din("xin", [NPT * 128 + 128, D])
    w_in = din("w_in", [D, 1280])
    w_o = din("w_o", [D, D])
    w_glu = din("w_glu", [512, 512])
    w_gate = din("w_gate", [NFC, 128, 8, 128])
    w_up = din("w_up", [NFC, 128, 8, 128])
    w_down = din("w_down", [2, NFC // 2, 128, 2, 512])
    ident_d = din("ident", [128, 128])
    masks_d = din("masks", [128, 3, 256])
    smask_d = din("smask", [128, 17 * 128])
    pcol_d = din("pcol", [128, 128])
    sinks_d = din("sinks", [8])
    gfin_d = din("gfin", [D])
    lam_d = din("lam", [128, 3, 16])
    lamb_d = din("lamb", [3, 16 * 128])
    bblk_d = din("bblk", [128, 2, 16, 128])
    cblk_d = din("cblk", [128, 2, 16, 128])
    h0_d = din("h0", [128, 2, 16, NSEQ])
    cvst_d = din("cvst", [128, NFC, NSEQ, 2])
    kcT_d = din("kcT", [128, NSEQ * 128])
    vc_d = din("vc", [128, NSEQ, 128])
    kcache_d = din("kcache", [NSEQ, 128, 128])
    vcache_d = din("vcache", [NSEQ, 128, 128])

    wgb_d = nc.dram_tensor("wgb", [NFC, 128, 8, 128], BF16, kind="Internal").ap()
    wub_d = nc.dram_tensor("wub", [NFC, 128, 8, 128], BF16, kind="Internal").ap()
    wdb_d = nc.dram_tensor("wdb", [2, NFC // 2, 128, 2, 512], BF16, kind="Internal").ap()
    y_d = dout("y", [NPT * 128 + 128, D])
    kvp_d = dout("kvp", [128, 256])
    kvs_k = dout("kvs_k", [NSEQ, 128, 128])
    kvs_v = dout("kvs_v", [NSEQ, 128, 128])
    hfin_d = dout("hfin", [128, 2, 16, NSEQ + 1])
    pconv_d = dout("pconv", [128, NFC, 2])
    sconv_d = dout("sconv", [128, NFC, NSEQ, 2])

    P = Prog()
    P.limit = limit
    es = ExitStack()
    with es:
        def sb(name, shape, dt=F32):
            return es.enter_context(nc.sbuf_tensor(name, list(shape), dt))

        def ps(name, shape, dt=F32):
            return es.enter_context(nc.psum_tensor(name, list(shape), dt))

        sems = {e: es.enter_context(nc.semaphore("s_" + e)) for e in ENGS}
        for k in range(NDMASEM):
            for e_ in ("sp", "pool"):
                sems["dma%s%d" % (e_, k)] = es.enter_context(nc.semaphore("s_dma%s%d" % (e_, k)))

        idb = sb("idb", [128, 128], BF16)
        onesb = sb("onesb", [128, 128], BF16)
        masks = sb("masks_s", [128, 3, 256])
        smask = sb("smask_s", [128, 17 * 128], BF16)
        pcol = sb("pcol_s", [128, 128])
        sink8 = sb("sink8", [128, 8])
        gfb = sb("gfb", [128, D])
        epsb = sb("epsb", [128, 1])
        lam = sb("lam_s", [128, 3, 16])
        WBre = sb("WBre", [128, 16, 128], BF16); WBim = sb("WBim", [128, 16, 128], BF16)
        WCre = sb("WCre", [128, 16, 128], BF16); WCimn = sb("WCimn", [128, 16, 128], BF16)
        cs = sb("cs", [128, 16, 129]); sn = sb("sn", [128, 16, 129])
        mask64 = sb("mask64", [128, NS]); dtmp = sb("dtmp", [128, NS])
        are = sb("are", [128, 16]); aim = sb("aim", [128, 16])
        ah = sb("ah", [128, 2, 16, NSEQ])
        car = sb("car", [128, 2, 16])
        hfin = sb("hfin_s", [128, 2, 16, NSEQ + 1])
        gcar = sb("gcar", [128, NFC, 2])
        sconv = sb("sconv_s", [128, NFC, NSEQ, 2])
        kTx = sb("kTx", [128, 2, 256], BF16)
        vx = sb("vx", [128, 2, 128], BF16)
        GMIX, GFFN, GATT, GSSM, BGLU, DSK, CVW, CVB = 0, 8, 16, 20, 24, 28, 32, 98

        def col(c):
            return pcol[:, c:c + 1]

        ssq = sb("ssq", [128, 4]); rstd = sb("rstd", [128, 4])
        xn = sb("xn", [128, D], BF16)
        junk = xn
        xnT = sb("xnT", [128, 8, 128], BF16)
        qT = sb("qT", [128, 4, 128], BF16); uT = sb("uT", [128, 4, 128], BF16)
        Sx = sb("Sx", [128, 1, 2, 257]); mx = sb("mx", [128, 2]); nbias = sb("nbias", [128, 2])
        rs = sb("rs", [128, 2]); rinv = sb("rinv", [128, 2])
        Pb = sb("Pb", [128, 2, 257], BF16); PTs = sb("PTs", [128, 4, 128], BF16)
        anb = sb("anb", [128, 512], BF16)
        mixT = sb("mixT", [128, 8, 128], BF16)
        Psb = sb("Psb", [128, 17 * 128 + 1], BF16)
        PTss = sb("PTss", [128, 17, 128], BF16)
        Bs2 = [sb("Bs%d" % i, [128, 512]) for i in range(2)]
        pp2 = [sb("pp%d" % i, [128, 4, 2, 128]) for i in range(2)]
        rr2 = [sb("rr%d" % i, [128, 2, 2, 128]) for i in range(2)]
        vv2 = [sb("vv%d" % i, [128, 2, 2, 128]) for i in range(2)]
        hb = sb("hb", [128, 4, 2, 128], BF16)
        yv = sb("yv", [128, 128]); gl32 = sb("gl32", [128, 4, 128]); glb = sb("glb", [128, 4, 128], BF16)
        sg = sb("sg", [128, 128]); sqb = sb("sqb", [128, 4, 128], BF16); rsb = sb("rsb", [128, 128])
        xn2T = sb("xn2T", [128, 8, 512], BF16)
        gx = sb("gx", [128, 2, 514]); gxs = sb("gxs", [128, NSEQ, 6])
        cv = sb("cv", [128, 1, 512]); sl = sb("sl", [128, 1, 512])
        kvtok = sl[:, 0, 0:256]
        attn = cv[:, 0, :]
        wi = [sb("wi0", [128, 8, 1280], BF16)] * 2
        wo = [sb("wo0", [128, 8, D], BF16)] * 2
        wgl = [sb("wgl0", [128, 4, 512], BF16)] * 2
        wg = [sb("wg%d" % i, [128, 8, 128], BF16) for i in range(2)] + [xnT]
        wu = [sb("wu%d" % i, [128, 8, 128], BF16) for i in range(2)] + [mixT]
        wdA = sb("wdA", [128, 2, 512], BF16)
        wd = [wdA[:], Sx[:].rearrange("p a b c -> p (a b c)").bitcast(BF16)[:, 0:1024].rearrange("p (j n) -> p j n", j=2),
              xn[:].rearrange("p (j n) -> p j n", j=2)]
        big = sb("big", [128, 9728])
        xb = big[:, 0:4096].rearrange("p (t d) -> p t d", t=4)
        hTall = big[:, 4096:9728].bitcast(BF16).rearrange("p (c n) -> p c n", c=NFC)
        hTsm = big[:, 4096:5504].bitcast(BF16).rearrange("p (c n) -> p c n", c=NFC)
        kTs = big[:, 5504:7680].bitcast(BF16).rearrange("p (a k) -> p a k", a=2)
        vs = big[:, 7680:8768].bitcast(BF16).rearrange("p (b k) -> p b k", b=17)
        scr = big
        h0 = big[:, 4352:4864].rearrange("p (a q s) -> p a q s", a=2, q=16)
        cvst = big[:, 8768:9472].rearrange("p (c s j) -> p c s j", c=NFC, s=NSEQ)
        lb = scr[:, 0:768].rearrange("p (a b) -> p a b", a=3); ft = scr[:, 768:2304].rearrange("p (a b) -> p a b", a=6)
        bblk = scr[:, 2304:2816].rearrange("p (a q m) -> p a q m", a=2, q=2); cblk = scr[:, 2816:3328].rearrange("p (a q m) -> p a q m", a=2, q=2)
        Ssx = big[:, 1024:1024 + 17 * 128 + 1]
        idf = big[:, 5632:5760]

        mmA = ps("mmA", [128, 512]); mmB = ps("mmB", [128, 512]); mmC = ps("mmC", [128, 512])
        pT = ps("pT", [128, 8, 128], BF16)
        pT32 = pT[:].rearrange("p c k -> p (c k)").bitcast(F32)
        Sps = ps("Sps", [128, 512]); Ops = ps("Ops", [128, 512])
        acc0 = ps("acc0", [128, 512]); acc1 = ps("acc1", [128, 512])

        op = P.op

        P.dma("sp", lambda h: h.dma_start(out=idf[:], in_=ident_d), writes=["idf"])
        P.dma("sp", lambda h: h.dma_start(out=masks[:], in_=masks_d), writes=["masks"])
        P.dma("pool", lambda h: h.dma_start(out=smask[:], in_=smask_d), writes=["smask"])
        P.dma("sp", lambda h: h.dma_start(out=pcol[:], in_=pcol_d), writes=["pcol"])
        P.dma("sp", lambda h: h.dma_start(out=sink8[:], in_=sinks_d.partition_broadcast(128)), writes=["sink8"])
        P.dma("sp", lambda h: h.dma_start(out=gfb[:], in_=gfin_d.partition_broadcast(128)), writes=["gfb"])
        P.dma("sp", lambda h: h.dma_start(out=lam[:], in_=lam_d), writes=["lam"])
        P.dma("sp", lambda h: h.dma_start(out=h0, in_=h0_d))
        op("pool", lambda h: h.memset(hb[:], 0.0))
        op("pool", lambda h: h.memset(cv[:], 0.0))
        op("pool", lambda h: h.memset(gx[:], 0.0))
        P.dma("pool", lambda h: h.dma_start(out=wi[0][:], in_=w_in.rearrange("(c p) n -> p c n", p=128)))
        P.dma("pool", lambda h: h.dma_start(out=wgl[0][:], in_=w_glu.rearrange("(c p) n -> p c n", p=128)))
        P.dma("pool", lambda h: h.dma_start(out=wo[0][:], in_=w_o.rearrange("(c p) n -> p c n", p=128)))
        for a_ in range(0, NFC, 11):
            P.dma("pool", lambda h: h.dma_start(out=wgb_d[a_:a_ + 11], in_=w_gate[a_:a_ + 11]))
            P.dma("pool", lambda h: h.dma_start(out=wub_d[a_:a_ + 11], in_=w_up[a_:a_ + 11]))
        for hf_ in range(2):
            P.dma("pool", lambda h: h.dma_start(out=wdb_d[hf_], in_=w_down[hf_]))
        P.dma("sp", lambda h: h.dma_start(out=kvs_k[:, 0:124, :], in_=kcache_d[:, 4:128, :]), writes=["kvs_k_a"])
        P.dma("sp", lambda h: h.dma_start(out=kvs_v[:, 0:124, :], in_=vcache_d[:, 4:128, :]), writes=["kvs_v_a"])

        op("dve", lambda h: h.tensor_copy(out=idb[:], in_=idf[:]), ["idf"], ["idb"])
        op("pool", lambda h: h.memset(onesb[:], 1.0), [], ["onesb"])
        op("pool", lambda h: h.memset(epsb[:], EPS), [], ["epsb"])
        op("pool", lambda h: h.memset(kTx[:], 0.0), [], ["kTx"])
        op("pool", lambda h: h.memset(vx[:], 0.0), [], ["vx"])
        op("pool", lambda h: h.memset(car[:], 0.0), [], ["car"])
        op("pool", lambda h: h.memset(gcar[:], 0.0), [], ["gcar"])
        pass
        op("pool", lambda h: h.memset(hfin[:], 0.0))
        op("pool", lambda h: h.memset(sconv[:], 0.0))
        rho = sb("rho", [128, 16])
        dtl, thl, c1, s1, tmpa, kfL = (big[:, 5760 + 16 * i:5776 + 16 * i] for i in range(6))
        kiL = big[:, 5856:5872].bitcast(mybir.dt.int32)
        kiR = big[:, 4864:5120].bitcast(mybir.dt.int32); kfR = big[:, 5120:5376]
        op("act", lambda h: h.activation(out=dtl[:], in_=lam[:, 2, :], func=AF.Exp), ["lam"], ["dtl"])
        op("dve", lambda h: h.tensor_tensor(out=thl[:], in0=lam[:, 1, :], in1=dtl[:], op=ALU.mult), ["lam", "dtl"], ["thl"])
        op("dve", lambda h: h.tensor_tensor(out=tmpa[:], in0=lam[:, 0, :], in1=dtl[:], op=ALU.mult), ["lam", "dtl"], ["tmpa"])
        op("act", lambda h: h.activation(out=rho[:], in_=tmpa[:], func=AF.Exp), ["tmpa"], ["rho"])

        def sincos(eng_tag, th_ap, s_ap, c_ap, tmp_ap, shape_key, ki_ap, kf_ap):
            K = shape_key

            def reduce_(shift, dst_key):
                op("dve", lambda h: h.tensor_scalar(out=tmp_ap, in0=th_ap, scalar1=shift, scalar2=1.0 / (2 * PI), op0=ALU.add, op1=ALU.mult),
                   [K + "th", K + "s", K + "c"], [K + "tmp"])
                op("dve", lambda h: h.tensor_copy(out=ki_ap, in_=tmp_ap), [K + "tmp"], [K + "ki"])
                op("dve", lambda h: h.tensor_copy(out=kf_ap, in_=ki_ap), [K + "ki"], [K + "kf"])
                op("dve", lambda h: h.tensor_scalar(out=tmp_ap, in0=th_ap, scalar1=shift, scalar2=None, op0=ALU.add), [K + "th", K + "ki"], [K + "tmp"])
                op("dve", lambda h: h.scalar_tensor_tensor(out=tmp_ap, in0=kf_ap, scalar=-2 * PI, in1=tmp_ap, op0=ALU.mult, op1=ALU.add),
                   [K + "kf", K + "tmp"], [K + "tmp"])
                op("dve", lambda h: h.tensor_scalar(out=kf_ap, in0=tmp_ap, scalar1=PI, scalar2=None, op0=ALU.is_gt), [K + "tmp"], [K + "kf"])
                op("dve", lambda h: h.scalar_tensor_tensor(out=tmp_ap, in0=kf_ap, scalar=-2 * PI, in1=tmp_ap, op0=ALU.mult, op1=ALU.add),
                   [K + "kf", K + "tmp"], [K + "tmp"])
                op("dve", lambda h: h.tensor_scalar(out=kf_ap, in0=tmp_ap, scalar1=-PI, scalar2=None, op0=ALU.is_lt), [K + "tmp"], [K + "kf"])
                op("dve", lambda h: h.scalar_tensor_tensor(out=tmp_ap, in0=kf_ap, scalar=2 * PI, in1=tmp_ap, op0=ALU.mult, op1=ALU.add),
                   [K + "kf", K + "tmp"], [K + "tmp"])
                op("dve", lambda h: h.tensor_scalar(out=tmp_ap, in0=tmp_ap, scalar1=-PI, scalar2=PI, op0=ALU.max, op1=ALU.min), [K + "tmp"], [K + "tmp"])

            reduce_(0.0, "s")
            op("act", lambda h: h.activation(out=s_ap, in_=tmp_ap, func=AF.Sin), [K + "tmp"], [K + "s"])
            reduce_(0.5 * PI, "c")
            op("act", lambda h: h.activation(out=c_ap, in_=tmp_ap, func=AF.Sin), [K + "tmp"], [K + "c"])

        sincos("L", thl[:], s1[:], c1[:], tmpa[:], "L", kiL[:], kfL[:])
        op("dve", lambda h: h.tensor_tensor(out=are[:], in0=rho[:], in1=c1[:], op=ALU.mult), ["rho", "Lc"], ["are"])
        op("dve", lambda h: h.tensor_tensor(out=aim[:], in0=rho[:], in1=s1[:], op=ALU.mult), ["rho", "Ls"], ["aim"])
        op("pool", lambda h: h.memset(cs[:, :, 0:1], 1.0), [], ["cs"])
        op("pool", lambda h: h.memset(sn[:, :, 0:1], 0.0), [], ["sn"])
        op("dve", lambda h: h.tensor_copy(out=cs[:, :, 1], in_=c1[:]), ["Lc", "cs"], ["cs"])
        op("dve", lambda h: h.tensor_copy(out=sn[:, :, 1], in_=s1[:]), ["Ls", "sn"], ["sn"])
        tA = pp2[0][:].rearrange("p a j (b c) -> p (a j b) c", c=64); tB = big[:, 3328:4352].rearrange("p (a c) -> p a c", c=64)
        m = 1
        while m < 128:
            cm = cs[:, :, m:m + 1].broadcast_to([128, 16, m]); sm = sn[:, :, m:m + 1].broadcast_to([128, 16, m])
            a_c = cs[:, :, 1:m + 1]; a_s = sn[:, :, 1:m + 1]
            o_c = cs[:, :, m + 1:2 * m + 1]; o_s = sn[:, :, m + 1:2 * m + 1]
            ta = tA[:, :, 0:m]; tb = tB[:, :, 0:m]
            op("dve", lambda h, a_c=a_c, cm=cm, ta=ta: h.tensor_tensor(out=ta, in0=a_c, in1=cm, op=ALU.mult), ["cs", "sn"], ["tA"])
            op("dve", lambda h, a_s=a_s, sm=sm, tb=tb: h.tensor_tensor(out=tb, in0=a_s, in1=sm, op=ALU.mult), ["cs", "sn"], ["tB"])
            op("dve", lambda h, o_c=o_c, ta=ta, tb=tb: h.tensor_tensor(out=o_c, in0=ta, in1=tb, op=ALU.subtract), ["tA", "tB", "sn"], ["cs"])
            op("dve", lambda h, a_c=a_c, sm=sm, ta=ta: h.tensor_tensor(out=ta, in0=a_c, in1=sm, op=ALU.mult), ["cs", "sn"], ["tA"])
            op("dve", lambda h, a_s=a_s, cm=cm, tb=tb: h.tensor_tensor(out=tb, in0=a_s, in1=cm, op=ALU.mult), ["cs", "sn"], ["tB"])
            op("dve", lambda h, o_s=o_s, ta=ta, tb=tb: h.tensor_tensor(out=o_s, in0=ta, in1=tb, op=ALU.add), ["tA", "tB", "cs"], ["sn"])
            m *= 2
        op("pool", lambda h: h.memset(mask64[:], 1.0))
        op("pool", lambda h: h.memset(mask64[:].rearrange("p (s t) -> p s t", t=4)[:, :, 0:1], 0.0))
        a_re_b = are[:].unsqueeze(2).broadcast_to([128, 16, NSEQ]); a_im_b = aim[:].unsqueeze(2).broadcast_to([128, 16, NSEQ])
        tC = big[:, 5376:5632].rearrange("p (q s) -> p q s", q=16)
        op("dve", lambda h: h.tensor_tensor(out=ah[:, 0], in0=h0[:, 0], in1=a_re_b, op=ALU.mult), ["h0", "are"], ["ah0"])
        op("dve", lambda h: h.tensor_tensor(out=tC[:], in0=h0[:, 1], in1=a_im_b, op=ALU.mult), ["h0", "aim"], ["tC"])
        op("dve", lambda h: h.tensor_tensor(out=ah[:, 0], in0=ah[:, 0], in1=tC[:], op=ALU.subtract), ["ah0", "tC"], ["ah0"])
        op("dve", lambda h: h.tensor_tensor(out=ah[:, 1], in0=h0[:, 0], in1=a_im_b, op=ALU.mult), ["h0", "aim"], ["ah1"])
        op("dve", lambda h: h.tensor_tensor(out=tC[:], in0=h0[:, 1], in1=a_re_b, op=ALU.mult), ["h0", "are", "ah0"], ["tC"])
        op("dve", lambda h: h.tensor_tensor(out=ah[:, 1], in0=ah[:, 1], in1=tC[:], op=ALU.add), ["ah1", "tC"], ["ah1"])

        for pc in range(8):
            for i in range(3):
                P.dma("sp", lambda h, i=i, pc=pc: h.dma_start(out=lb[:, i, :], in_=lamb_d[i, pc * 256:(pc + 1) * 256].partition_broadcast(128)), writes=["lb"])
            P.dma("sp", lambda h, pc=pc: h.dma_start(out=bblk[:], in_=bblk_d[:, :, pc * 2:(pc + 1) * 2, :]), writes=["bblk"])
            P.dma("sp", lambda h, pc=pc: h.dma_start(out=cblk[:], in_=cblk_d[:, :, pc * 2:(pc + 1) * 2, :]), writes=["cblk"])
            LR, LI, LD = lb[:, 0, :], lb[:, 1, :], lb[:, 2, :]
            f0, f1, f2, f3, f4, f5 = (ft[:, i, :] for i in range(6))
            op("act", lambda h: h.activation(out=f0, in_=LD, func=AF.Exp), ["lb"], ["f0"])
            op("dve", lambda h: h.tensor_tensor(out=f1, in0=LI, in1=f0, op=ALU.mult), ["lb", "f0"], ["Rth"])
            op("dve", lambda h: h.tensor_tensor(out=f2, in0=LR, in1=f0, op=ALU.mult), ["lb", "f0"], ["f2"])
            op("act", lambda h: h.activation(out=f2, in_=f2, func=AF.Exp), ["f2"], ["f2"])
            sincos("R", f1, f3, f4, f5, "R", kiR[:], kfR[:])
            op("dve", lambda h: h.tensor_tensor(out=f4, in0=f4, in1=f2, op=ALU.mult), ["Rc", "f2"], ["Rc"])
            op("dve", lambda h: h.tensor_scalar(out=f4, in0=f4, scalar1=-1.0, scalar2=None, op0=ALU.add), ["Rc"], ["Rc"])
            op("dve", lambda h: h.tensor_tensor(out=f3, in0=f3, in1=f2, op=ALU.mult), ["Rs", "f2"], ["Rs"])
            op("dve", lambda h: h.tensor_tensor(out=f0, in0=LR, in1=LR, op=ALU.mult), ["lb", "Rth", "f2"], ["f0"])
            op("dve", lambda h: h.tensor_tensor(out=f5, in0=LI, in1=LI, op=ALU.mult), ["lb", "Rc"], ["Rtmp"])
            op("dve", lambda h: h.tensor_tensor(out=f0, in0=f0, in1=f5, op=ALU.add), ["f0", "Rtmp"], ["f0"])
            op("dve", lambda h: h.reciprocal(out=f0, in_=f0), ["f0"], ["f0"])
            op("dve", lambda h: h.tensor_tensor(out=f1, in0=f4, in1=LR, op=ALU.mult), ["Rc", "lb", "Rth", "Rtmp"], ["f1"])
            op("dve", lambda h: h.tensor_tensor(out=f5, in0=f3, in1=LI, op=ALU.mult), ["Rs", "lb"], ["Rtmp"])
            op("dve", lambda h: h.tensor_tensor(out=f1, in0=f1, in1=f5, op=ALU.add), ["f1", "Rtmp"], ["f1"])
            op("dve", lambda h: h.tensor_tensor(out=f2, in0=f3, in1=LR, op=ALU.mult), ["Rs", "lb"], ["f2"])
            op("dve", lambda h: h.tensor_tensor(out=f5, in0=f4, in1=LI, op=ALU.mult), ["Rc", "lb", "f1"], ["Rtmp"])
            op("dve", lambda h: h.tensor_tensor(out=f2, in0=f2, in1=f5, op=ALU.subtract), ["f2", "Rtmp"], ["f2"])
            op("dve", lambda h: h.tensor_tensor(out=f1, in0=f1, in1=f0, op=ALU.mult), ["f1", "f0"], ["f1"])
            op("dve", lambda h: h.tensor_tensor(out=f2, in0=f2, in1=f0, op=ALU.mult), ["f2", "f0"], ["f2"])
            Bre_, Bim_ = bblk[:, 0].rearrange("p q m -> p (q m)"), bblk[:, 1].rearrange("p q m -> p (q m)")
            op("dve", lambda h: h.tensor_tensor(out=f3, in0=Bre_, in1=f1, op=ALU.mult), ["bblk", "f1", "Rs"], ["f3"])
            op("dve", lambda h: h.tensor_tensor(out=f4, in0=Bim_, in1=f2, op=ALU.mult), ["bblk", "f2", "Rc"], ["f4"])
            op("dve", lambda h, pc=pc: h.tensor_tensor(out=WBre[:, pc * 2:(pc + 1) * 2, :].rearrange("p q m -> p (q m)"), in0=f3, in1=f4, op=ALU.subtract), ["f3", "f4"], ["WBre"])
            op("dve", lambda h: h.tensor_tensor(out=f3, in0=Bre_, in1=f2, op=ALU.mult), ["bblk", "f2", "WBre"], ["f3"])
            op("dve", lambda h: h.tensor_tensor(out=f4, in0=Bim_, in1=f1, op=ALU.mult), ["bblk", "f1", "WBre"], ["f4"])
            op("dve", lambda h, pc=pc: h.tensor_tensor(out=WBim[:, pc * 2:(pc + 1) * 2, :].rearrange("p q m -> p (q m)"), in0=f3, in1=f4, op=ALU.add), ["f3", "f4"], ["WBim"])
            op("act", lambda h, pc=pc: h.activation(out=WCre[:, pc * 2:(pc + 1) * 2, :], in_=cblk[:, 0], func=AF.Copy), ["cblk"], ["WCre"])
            op("act", lambda h, pc=pc: h.activation(out=WCimn[:, pc * 2:(pc + 1) * 2, :], in_=cblk[:, 1], func=AF.Copy, scale=-1.0), ["cblk"], ["WCimn"])

        P.barrier()
        def rms(src_ap, key_src, n, slot, scale, pn):
            op("act", lambda h: h.activation(out=junk[0:pn, 0:n], in_=src_ap, func=AF.Square, accum_out=ssq[0:pn, slot:slot + 1]),
               [key_src], ["junk", "ssq%d" % slot])
            op("act", lambda h: h.activation(out=rstd[0:pn, slot:slot + 1], in_=ssq[0:pn, slot:slot + 1], func=AF.Sqrt, scale=scale, bias=epsb[0:pn, 0:1]),
               ["ssq%d" % slot, "epsb"], ["rstd%d" % slot])
            op("dve", lambda h: h.reciprocal(out=rstd[0:pn, slot:slot + 1], in_=rstd[0:pn, slot:slot + 1]), ["rstd%d" % slot], ["rstd%d" % slot])

        def mixer(ti, tl):
            sample = (ti == NPT)
            xt = xb[:, tl, :]
            n = 128
            ns = NS if sample else 128
            r0 = ti * 128
            par = ti % 2
            if sample:
                op("pool", lambda h: h.memset(kTs[:], 0.0))
                P.dma("pool", lambda h: h.dma_start(out=kTs[0:64, 0, 0:NSEQ * 128], in_=kcT_d[0:64, :]))
                P.dma("pool", lambda h: h.dma_start(out=kTs[64:128, 1, 0:NSEQ * 128], in_=kcT_d[64:128, :]))
                P.dma("pool", lambda h: h.dma_start(out=vs[:, 0:NSEQ, :], in_=vc_d))
                P.dma("sp", lambda h: h.dma_start(out=cvst, in_=cvst_d))
            P.dma("sp", lambda h: h.dma_start(out=xt[:], in_=xin[r0:r0 + 128, :]))
            rms(xt[:], "xt", D, 0, 1.0 / D, 128)
            op("dve", lambda h: h.tensor_scalar(out=xn[:], in0=xt[:], scalar1=rstd[:, 0:1], scalar2=None, op0=ALU.mult))
            for c in range(8):
                op("pe", lambda h, c=c: h.transpose(out=pT[:, c, :], in_=xn[:, c * 128:(c + 1) * 128], identity=idb[:]))
            op("dve", lambda h: h.tensor_tensor(out=xnT[:], in0=pT[:], in1=pcol[:, GMIX:GMIX + 8].unsqueeze(2).broadcast_to([128, 8, 128]), op=ALU.mult))
            W = wi[par]
            for i in range(4):
                bank = mmA if i % 2 == 0 else mmB
                for c in range(8):
                    op("pe", lambda h, i=i, c=c, bank=bank: h.matmul(bank[:, 0:128], lhsT=W[:, c, i * 128:(i + 1) * 128], rhs=xnT[:, c, :],
                                                                   start=(c == 0), stop=(c == 7)))
                op("act", lambda h, i=i, bank=bank: h.activation(out=qT[:, i, :], in_=bank[:, 0:128], func=AF.Copy))
            for c in range(8):
                op("pe", lambda h, c=c: h.matmul(mmC[:, 0:128], lhsT=W[:, c, 512:640], rhs=xnT[:, c, :], start=(c == 0), stop=(c == 7)))
            if sample:
                op("dve", lambda h: h.tensor_copy(out=kTs[0:64, 0, NSEQ * 128:17 * 128], in_=mmC[0:64, 0:128]))
                op("dve", lambda h: h.tensor_copy(out=kTs[64:128, 1, NSEQ * 128:17 * 128], in_=mmC[64:128, 0:128]))
            else:
                op("dve", lambda h: h.tensor_copy(out=kTx[0:64, 0, 128:256], in_=mmC[0:64, 0:128]))
                op("dve", lambda h: h.tensor_copy(out=kTx[64:128, 1, 128:256], in_=mmC[64:128, 0:128]))
            for i in range(4):
                bank = mmA if i % 2 == 0 else mmB
                for c in range(8):
                    op("pe", lambda h, i=i, c=c, bank=bank: h.matmul(bank[:, 0:128], lhsT=W[:, c, 768 + i * 128:768 + (i + 1) * 128], rhs=xnT[:, c, :],
                                                                   start=(c == 0), stop=(c == 7)))
                op("act", lambda h, i=i, bank=bank: h.activation(out=uT[:, i, :], in_=bank[:, 0:128], func=AF.Copy))
            for c in range(8):
                op("pe", lambda h, c=c: h.matmul(mmC[:, 0:256], lhsT=xnT[:, c, :], rhs=W[:, c, 512:768], start=(c == 0), stop=(c == 7)))
            if sample:
                op("dve", lambda h: h.tensor_copy(out=vs[:, 16, :], in_=mmC[:, 128:256]))
            else:
                op("dve", lambda h: h.tensor_copy(out=vx[:, 1, :], in_=mmC[:, 128:256]))
            if sample or ti == NPT - 1:
                op("dve", lambda h: h.tensor_copy(out=kvtok[:], in_=mmC[:, 0:256]))
                if sample:
                    P.dma("sp", lambda h: h.dma_start(out=kvs_k[:, 124:128, :], in_=kvtok[0:NS, 0:128]))
                    P.dma("sp", lambda h: h.dma_start(out=kvs_v[:, 124:128, :], in_=kvtok[0:NS, 128:256]))
                else:
                    P.dma("sp", lambda h: h.dma_start(out=kvp_d, in_=kvtok[:]))

            if not sample:
                mi = 0 if ti == 0 else (1 if ti == 1 else 2)
                for i in range(4):
                    for hh_ in range(2):
                        op("pe", lambda h, i=i, hh_=hh_: h.matmul(Sps[:, hh_ * 256:(hh_ + 1) * 256], lhsT=qT[:, i, :], rhs=kTx[:, hh_, :], start=True, stop=True))
                    for hh_ in range(2):
                        op("dve", lambda h, hh_=hh_, hd=i + 4 * hh_: h.tensor_scalar(out=Sx[:, 0, hh_, 256:257], in0=sink8[:, hd:hd + 1], scalar1=8.0, scalar2=None, op0=ALU.mult))
                    op("dve", lambda h: h.tensor_tensor(out=Sx[:, 0, :, 0:256], in0=Sps[:].rearrange("p (a k) -> p a k", a=2),
                                                        in1=masks[:, mi:mi + 1, :].broadcast_to([128, 2, 256]), op=ALU.add))
                    op("dve", lambda h: h.tensor_reduce(out=mx[:], in_=Sx[:, 0], axis=AX.X, op=ALU.max))
                    op("dve", lambda h: h.tensor_scalar(out=nbias[:], in0=mx[:], scalar1=-0.125, scalar2=None, op0=ALU.mult))
                    for hh_ in range(2):
                        op("act", lambda h, hh_=hh_: h.activation(out=Pb[:, hh_, :], in_=Sx[:, 0, hh_, :], func=AF.Exp, scale=0.125,
                                                                 bias=nbias[:, hh_:hh_ + 1], accum_out=rs[:, hh_:hh_ + 1]))
                    op("dve", lambda h: h.reciprocal(out=rinv[:], in_=rs[:]))
                    for hh_ in range(2):
                        for blk in range(2):
                            op("pe", lambda h, hh_=hh_, blk=blk: h.transpose(out=pT[:, hh_ * 2 + blk, :], in_=Pb[:, hh_, blk * 128:(blk + 1) * 128], identity=idb[:]))
                    op("act", lambda h: h.activation(out=PTs[:], in_=pT[:, 0:4, :], func=AF.Copy))
                    for hh_ in range(2):
                        for blk in range(2):
                            op("pe", lambda h, hh_=hh_, blk=blk: h.matmul(Ops[:, hh_ * 64:(hh_ + 1) * 64], lhsT=PTs[:, hh_ * 2 + blk, :],
                                                                         rhs=vx[:, blk, hh_ * 64:(hh_ + 1) * 64], start=(blk == 0), stop=(blk == 1)))
                    for hh_ in range(2):
                        hd = i + 4 * hh_
                        op("dve", lambda h, hh_=hh_, hd=hd: h.tensor_scalar(out=attn[:, hd * 64:(hd + 1) * 64], in0=Ops[:, hh_ * 64:(hh_ + 1) * 64],
                                                                           scalar1=rinv[:, hh_:hh_ + 1], scalar2=None, op0=ALU.mult))
                op("pool", lambda h: h.tensor_copy(out=kTx[:, :, 0:128], in_=kTx[:, :, 128:256]))
                op("pool", lambda h: h.tensor_copy(out=vx[:, 0, :], in_=vx[:, 1, :]))
            else:
                W17 = 17 * 128
                for hd in range(8):
                    i, hh_ = hd % 4, hd // 4
                    for cb in range(5):
                        c0 = cb * 512
                        cw = min(512, W17 - c0)
                        bank = mmA if cb % 2 == 0 else mmB
                        op("pe", lambda h, i=i, hh_=hh_, c0=c0, cw=cw, bank=bank: h.matmul(bank[:, 0:cw], lhsT=qT[:, i, :], rhs=kTs[:, hh_, c0:c0 + cw], start=True, stop=True))
                        op("dve", lambda h, c0=c0, cw=cw, bank=bank: h.tensor_tensor(out=Ssx[:, c0:c0 + cw], in0=bank[:, 0:cw], in1=smask[:, c0:c0 + cw], op=ALU.add))
                    op("dve", lambda h, hd=hd: h.tensor_scalar(out=Ssx[:, W17:W17 + 1], in0=sink8[:, hd:hd + 1], scalar1=8.0, scalar2=None, op0=ALU.mult))
                    op("dve", lambda h: h.tensor_reduce(out=mx[:, 0:1], in_=Ssx[:], axis=AX.X, op=ALU.max))
                    op("dve", lambda h: h.tensor_scalar(out=nbias[:, 0:1], in0=mx[:, 0:1], scalar1=-0.125, scalar2=None, op0=ALU.mult))
                    op("act", lambda h: h.activation(out=Psb[:], in_=Ssx[:], func=AF.Exp, scale=0.125, bias=nbias[:, 0:1], accum_out=rs[:, 0:1]))
                    op("dve", lambda h: h.reciprocal(out=rinv[:, 0:1], in_=rs[:, 0:1]))
                    for g8 in range(3):
                        nb_ = 8 if g8 < 2 else 1
                        for b in range(nb_):
                            blk = g8 * 8 + b
                            op("pe", lambda h, b=b, blk=blk: h.transpose(out=pT[:, b, :], in_=Psb[:, blk * 128:(blk + 1) * 128], identity=idb[:]))
                        op("act", lambda h, g8=g8, nb_=nb_: h.activation(out=PTss[:, g8 * 8:g8 * 8 + nb_, :], in_=pT[:, 0:nb_, :], func=AF.Copy))
                    for blk in range(17):
                        op("pe", lambda h, blk=blk, hh_=hh_: h.matmul(Ops[:, 0:64], lhsT=PTss[:, blk, :], rhs=vs[:, blk, hh_ * 64:(hh_ + 1) * 64],
                                                                     start=(blk == 0), stop=(blk == 16)))
                    op("dve", lambda h, hd=hd: h.tensor_scalar(out=attn[:, hd * 64:(hd + 1) * 64], in0=Ops[:, 0:64], scalar1=rinv[:, 0:1], scalar2=None, op0=ALU.mult))
            rms(attn[0:n, :], "attn", 512, 1, 1.0 / 512, n)
            op("dve", lambda h: h.tensor_scalar(out=anb[0:n, :], in0=attn[0:n, :], scalar1=rstd[0:n, 1:2], scalar2=None, op0=ALU.mult), ["attn", "rstd1"], ["anb"])
            for c in range(4):
                op("pe", lambda h, c=c: h.transpose(out=pT[:, c, 0:n], in_=anb[0:n, c * 128:(c + 1) * 128], identity=idb[0:n, 0:n]), ["anb", "idb"], ["pT"])
            op("dve", lambda h: h.tensor_tensor(out=mixT[:, 0:4, :], in0=pT[:, 0:4, :], in1=pcol[:, GATT:GATT + 4].unsqueeze(2).broadcast_to([128, 4, 128]), op=ALU.mult))

            def ssm_views(k):
                c4, hp = k // 2, k % 2
                q0 = c4 * 4 + 2 * hp
                bi = k % 2
                Bv = Bs2[bi][:].rearrange("p (r j k) -> p r j k", r=2, j=2)
                if sample:
                    csq = cs[:, q0:q0 + 2, 1:5].unsqueeze(2).broadcast_to([128, 2, NSEQ, 4])
                    snq = sn[:, q0:q0 + 2, 1:5].unsqueeze(2).broadcast_to([128, 2, NSEQ, 4])

                    def v3(ap):
                        return ap[:, :, 0:NS].rearrange("p j (s t) -> p j s t", t=4)
                else:
                    csq = cs[:, q0:q0 + 2, 1:129]; snq = sn[:, q0:q0 + 2, 1:129]

                    def v3(ap):
                        return ap
                T = [v3(pp2[bi][:, kk]) for kk in range(4)]
                RR = [v3(rr2[bi][:, kk]) for kk in range(2)]
                VV = [v3(vv2[bi][:, kk]) for kk in range(2)]
                return c4, hp, q0, bi, Bv, csq, snq, v3, T, RR, VV

            def ssm_front(k):
                c4, hp, q0, bi, Bv, csq, snq, v3, T, RR, VV = ssm_views(k)
                bank = mmA if bi == 0 else mmC
                for ri, WBx in enumerate((WBre, WBim)):
                    for j in range(2):
                        op("pe", lambda h: h.matmul(bank[:, (ri * 2 + j) * 128:(ri * 2 + j + 1) * 128], lhsT=WBx[:, q0 + j, :], rhs=uT[:, c4, :], start=True, stop=True))
                op("act", lambda h: h.activation(out=Bs2[bi][:], in_=bank[:, :], func=AF.Copy))
                if sample:
                    for ri in range(2):
                        v4 = Bv[:, ri, :, 0:NS].rearrange("p j (s t) -> p j s t", t=4)[:, :, :, 0]
                        op("dve", lambda h: h.tensor_tensor(out=v4, in0=v4, in1=ah[:, ri, q0:q0 + 2, :], op=ALU.add))
                Bre, Bim = v3(Bv[:, 0]), v3(Bv[:, 1])
                op("dve", lambda h: h.tensor_tensor(out=T[0], in0=Bre, in1=csq, op=ALU.mult))
                op("pool", lambda h: h.tensor_tensor(out=T[1], in0=Bim, in1=snq, op=ALU.mult))
                op("dve", lambda h: h.tensor_tensor(out=T[2], in0=Bim, in1=csq, op=ALU.mult))
                op("pool", lambda h: h.tensor_tensor(out=T[3], in0=Bre, in1=snq, op=ALU.mult))
                op("dve", lambda h: h.tensor_tensor(out=RR[0], in0=T[0], in1=T[1], op=ALU.add))
                op("pool", lambda h: h.tensor_tensor(out=RR[1], in0=T[2], in1=T[3], op=ALU.subtract))

            def ssm_back(k):
                c4, hp, q0, bi, Bv, csq, snq, v3, T, RR, VV = ssm_views(k)
                rrb, vvb = rr2[bi], vv2[bi]
                for j in range(2):
                    q = q0 + j
                    if sample:
                        op("dve", lambda h: h.tensor_scalar(out=dtmp[:], in0=mask64[:], scalar1=rho[:, q:q + 1], scalar2=None, op0=ALU.mult))
                    for ri in range(2):
                        if sample:
                            op("dve", lambda h: h.tensor_tensor_scan(out=vvb[:, ri, j, 0:NS], data0=dtmp[:], data1=rrb[:, ri, j, 0:NS], initial=0.0, op0=ALU.mult, op1=ALU.add))
                        else:
                            op("dve", lambda h: h.tensor_tensor_scan(out=vvb[:, ri, j, :], data0=rho[:, q:q + 1].broadcast_to([128, 128]), data1=rrb[:, ri, j, :],
                                                                   initial=car[:, ri, q:q + 1], op0=ALU.mult, op1=ALU.add))
                op("dve", lambda h: h.tensor_tensor(out=T[0], in0=VV[0], in1=csq, op=ALU.mult))
                op("pool", lambda h: h.tensor_tensor(out=T[1], in0=VV[1], in1=snq, op=ALU.mult))
                op("pool", lambda h: h.tensor_tensor(out=T[2], in0=VV[0], in1=snq, op=ALU.mult))
                op("dve", lambda h: h.tensor_tensor(out=T[3], in0=VV[1], in1=csq, op=ALU.mult))
                hb_re, hb_im = v3(hb[:, 2 * hp:2 * hp + 2, 0, :]), v3(hb[:, 2 * hp:2 * hp + 2, 1, :])
                op("pool", lambda h: h.tensor_tensor(out=hb_re, in0=T[0], in1=T[1], op=ALU.subtract))
                op("dve", lambda h: h.tensor_tensor(out=hb_im, in0=T[2], in1=T[3], op=ALU.add))
                if sample:
                    op("pool", lambda h: h.tensor_tensor(out=hfin[:, 0, q0:q0 + 2, 0:NSEQ], in0=T[0][:, :, :, 3], in1=T[1][:, :, :, 3], op=ALU.subtract))
                    op("dve", lambda h: h.tensor_tensor(out=hfin[:, 1, q0:q0 + 2, 0:NSEQ], in0=T[2][:, :, :, 3], in1=T[3][:, :, :, 3], op=ALU.add))
                else:
                    op("pool", lambda h: h.tensor_tensor(out=car[:, 0, q0:q0 + 2], in0=T[0][:, :, 127], in1=T[1][:, :, 127], op=ALU.subtract))
                    op("dve", lambda h: h.tensor_tensor(out=car[:, 1, q0:q0 + 2], in0=T[2][:, :, 127], in1=T[3][:, :, 127], op=ALU.add))

            def ssm_y(c4):
                for qq in range(4):
                    q = c4 * 4 + qq
                    op("pe", lambda h: h.matmul(mmB[:, 0:n], lhsT=WCre[:, q, :], rhs=hb[:, qq, 0, 0:n], start=(qq == 0), stop=False))
                    op("pe", lambda h: h.matmul(mmB[:, 0:n], lhsT=WCimn[:, q, :], rhs=hb[:, qq, 1, 0:n], start=False, stop=(qq == 3)))
                op("dve", lambda h: h.scalar_tensor_tensor(out=yv[:, 0:n], in0=uT[:, c4, 0:n], scalar=col(DSK + c4), in1=mmB[:, 0:n], op0=ALU.mult, op1=ALU.add))
                op("act", lambda h: h.activation(out=gl32[:, c4, 0:n], in_=yv[:, 0:n], func=AF.Gelu))
                op("pool", lambda h: h.tensor_copy(out=glb[:, c4, 0:n], in_=gl32[:, c4, 0:n]))

            ssm_front(0)
            for k in range(8):
                if k + 1 < 8:
                    ssm_front(k + 1)
                ssm_back(k)
                if k % 2 == 1:
                    ssm_y(k // 2)
            if ti == NPT - 1:
                op("act", lambda h: h.activation(out=hfin[:, :, :, NSEQ], in_=car[:], func=AF.Copy), ["car"], ["hfin"])
            for oc in range(4):
                for c4 in range(4):
                    op("pe", lambda h, oc=oc, c4=c4: h.matmul(mmC[:, 0:n], lhsT=wgl[par][:, c4, oc * 128:(oc + 1) * 128], rhs=glb[:, c4, 0:n], start=(c4 == 0), stop=(c4 == 3)),
                       ["glb", "wgl"], ["mmC"])
                op("act", lambda h, oc=oc: h.activation(out=sg[:, 0:n], in_=mmC[:, 0:n], func=AF.Sigmoid, bias=col(BGLU + oc)), ["mmC", "pcol"], ["sg"])
                op("dve", lambda h, oc=oc: h.tensor_tensor(out=gl32[:, oc, 0:n], in0=gl32[:, oc, 0:n], in1=sg[:, 0:n], op=ALU.mult), ["gl32", "sg", "glb"], ["gl32"])
                op("act", lambda h, oc=oc: h.activation(out=sqb[:, oc, 0:n], in_=gl32[:, oc, 0:n], func=AF.Square), ["gl32"], ["sqb"])
            for oc in range(4):
                op("pe", lambda h, oc=oc: h.matmul(mmC[:, 0:n], lhsT=onesb[:], rhs=sqb[:, oc, 0:n], start=(oc == 0), stop=(oc == 3)), ["sqb", "onesb"], ["mmC"])
            op("act", lambda h: h.activation(out=rsb[:, 0:n], in_=mmC[:, 0:n], func=AF.Sqrt, scale=1.0 / 512, bias=epsb[:, 0:1]), ["mmC", "epsb"], ["rsb"])
            op("dve", lambda h: h.reciprocal(out=rsb[:, 0:n], in_=rsb[:, 0:n]), ["rsb"], ["rsb"])
            for oc in range(4):
                op("dve", lambda h, oc=oc: h.scalar_tensor_tensor(out=mixT[:, 4 + oc, 0:n], in0=gl32[:, oc, 0:n], scalar=col(GSSM + oc), in1=rsb[:, 0:n], op0=ALU.mult, op1=ALU.mult),
                   ["gl32", "rsb", "pcol"], ["mixT"])

            for hf in range(2):
                acc, ak = (acc0, "acc0") if hf == 0 else (acc1, "acc1")
                for c in range(8):
                    op("pe", lambda h, hf=hf, c=c, acc=acc: h.matmul(acc[0:n, :], lhsT=mixT[:, c, 0:n], rhs=wo[par][:, c, hf * 512:(hf + 1) * 512], start=(c == 0), stop=(c == 7)),
                       ["mixT", "wo"], [ak])
                op("dve", lambda h, hf=hf, acc=acc: h.tensor_tensor(out=xt[0:n, hf * 512:(hf + 1) * 512], in0=xt[0:n, hf * 512:(hf + 1) * 512], in1=acc[0:n, :], op=ALU.add),
                   ["xt", ak], ["xt"])
            rms(xt[0:n, :], "xt", D, 2, 1.0 / D, n)
            op("dve", lambda h: h.tensor_scalar(out=xn[0:n, :], in0=xt[0:n, :], scalar1=rstd[0:n, 2:3], scalar2=None, op0=ALU.mult), ["xt", "rstd2"], ["xn"])
            for c in range(8):
                op("pe", lambda h, c=c: h.transpose(out=pT[:, c, 0:n], in_=xn[0:n, c * 128:(c + 1) * 128], identity=idb[0:n, 0:n]), ["xn", "idb"], ["pT"])
            op("dve", lambda h: h.tensor_tensor(out=xn2T[:, :, tl * 128:(tl + 1) * 128], in0=pT[:], in1=pcol[:, GFFN:GFFN + 8].unsqueeze(2).broadcast_to([128, 8, 128]), op=ALU.mult))

        def ffn(tiles_):
            sample = (tiles_[0] == NPT)
            ntl = len(tiles_)
            nb = ntl * 128
            hTv = hTsm if sample else hTall
            for ch in range(NFC):
                b3 = ch % 2
                w3 = ch % 3
                P.dma("sp", lambda h: h.dma_start(out=wg[w3][:], in_=wgb_d[ch]))
                P.dma("sp", lambda h: h.dma_start(out=wu[w3][:], in_=wub_d[ch]))
                gps, ups = (mmA, mmB) if b3 == 0 else (mmC, pT32)
                for c in range(8):
                    op("pe", lambda h, c=c, b3=b3, gps=gps: h.matmul(gps[:, 0:nb], lhsT=wg[w3][:, c, :], rhs=xn2T[:, c, 0:nb], start=(c == 0), stop=(c == 7)))
                for c in range(8):
                    op("pe", lambda h, c=c, b3=b3, ups=ups: h.matmul(ups[:, 0:nb], lhsT=wu[w3][:, c, :], rhs=xn2T[:, c, 0:nb], start=(c == 0), stop=(c == 7)))
                w0, w1, w2, bb = col(CVW + ch * 3), col(CVW + ch * 3 + 1), col(CVW + ch * 3 + 2), col(CVB + ch)
                cvb, slb, gxb = cv[:, 0, :], sl[:, 0, :], gx[:, b3, :]
                if sample:
                    op("pool", lambda h, ch=ch: h.tensor_copy(out=gxs[:, :, 0:2], in_=cvst[:, ch, :, :]))
                    op("act", lambda h: h.activation(out=gxs[:, :, 2:6], in_=gps[:, 0:NS].rearrange("p (s t) -> p s t", t=4), func=AF.Copy))
                    g0, g1, g2 = gxs[:, :, 0:4], gxs[:, :, 1:5], gxs[:, :, 2:6]
                    cvv = cvb[:, 0:NS].rearrange("p (s t) -> p s t", t=4)
                    op("pool", lambda h, ch=ch: h.tensor_copy(out=sconv[:, ch, :, :], in_=gxs[:, :, 4:6]))
                else:
                    op("pool", lambda h, ch=ch, gxb=gxb: h.tensor_copy(out=gxb[:, 0:2], in_=gcar[:, ch, :]))
                    op("act", lambda h, gxb=gxb: h.activation(out=gxb[:, 2:2 + nb], in_=gps[:, 0:nb], func=AF.Copy))
                    g0, g1, g2 = gxb[:, 0:nb], gxb[:, 1:1 + nb], gxb[:, 2:2 + nb]
                    cvv = cvb[:, 0:nb]
                    op("pool", lambda h, ch=ch, gxb=gxb: h.tensor_copy(out=gcar[:, ch, :], in_=gxb[:, nb:nb + 2]))
                op("dve", lambda h, g2=g2, cvv=cvv, w2=w2, bb=bb: h.tensor_scalar(out=cvv, in0=g2, scalar1=w2, scalar2=bb, op0=ALU.mult, op1=ALU.add))
                op("dve", lambda h, g1=g1, cvv=cvv, w1=w1: h.scalar_tensor_tensor(out=cvv, in0=g1, scalar=w1, in1=cvv, op0=ALU.mult, op1=ALU.add))
                op("dve", lambda h, g0=g0, cvv=cvv, w0=w0: h.scalar_tensor_tensor(out=cvv, in0=g0, scalar=w0, in1=cvv, op0=ALU.mult, op1=ALU.add))
                op("act", lambda h, cvb=cvb, slb=slb: h.activation(out=slb[:, 0:nb], in_=cvb[:, 0:nb], func=AF.Silu))
                op("dve", lambda h, ch=ch, slb=slb: h.tensor_tensor(out=hTv[:, ch, 0:nb], in0=slb[:, 0:nb], in1=ups[:, 0:nb], op=ALU.mult))
            accs = [acc0, acc1, Sps, Ops]
            for hf in range(2):
                for g in range(NFC // 2):
                    wdb = wd[g % 3]
                    P.dma("sp", lambda h: h.dma_start(out=wdb, in_=wdb_d[hf, g]))
                    for j in range(2):
                        ch = 2 * g + j
                        for tl in range(ntl):
                            op("pe", lambda h: h.matmul(accs[tl][:, :], lhsT=hTv[:, ch, tl * 128:(tl + 1) * 128], rhs=wdb[:, j, :],
                                                        start=(ch == 0), stop=(ch == NFC - 1)))
                for tl in range(ntl):
                    op("dve", lambda h, tl=tl, hf=hf: h.tensor_tensor(out=xb[:, tl, hf * 512:(hf + 1) * 512], in0=xb[:, tl, hf * 512:(hf + 1) * 512], in1=accs[tl][:, :], op=ALU.add))
            for tl, ti in enumerate(tiles_):
                xt = xb[:, tl, :]
                rms(xt, "xt", D, 3, 1.0 / D, 128)
                op("dve", lambda h, xt=xt: h.scalar_tensor_tensor(out=xt, in0=xt, scalar=rstd[:, 3:4], in1=gfb[:], op0=ALU.mult, op1=ALU.mult))
                P.dma("sp", lambda h, xt=xt, ti=ti: h.dma_start(out=y_d[ti * 128:(ti + 1) * 128, :], in_=xt))

        if tiles is None:
            blocks = [[0, 1, 2, 3], [4, 5, 6, 7], [8, 9, 10, 11], [12, 13, 14, 15], [16], [NPT]]
        else:
            blocks = tiles
        for blk_ in blocks:
            for tl, ti in enumerate(blk_):
                mixer(ti, tl)
            ffn(blk_)

        P.dma("sp", lambda h: h.dma_start(out=hfin_d, in_=hfin[:]), reads=["hfin"], writes=["hfin_d"])
        P.dma("sp", lambda h: h.dma_start(out=pconv_d, in_=gcar[:]), reads=["gcar"], writes=["pconv_d"])
        P.dma("sp", lambda h: h.dma_start(out=sconv_d, in_=sconv[:]), reads=["sconv"], writes=["sconv_d"])
        P.limit = None
        P.barrier()

        with nc.Block() as block:
            @block.tensor
            def _(h):
                P.run("pe", h, sems)

            @block.scalar
            def _(h):
                P.run("act", h, sems)

            @block.vector
            def _(h):
                P.run("dve", h, sems)

            @block.gpsimd
            def _(h):
                P.run("pool", h, sems)

            @block.sync
            def _(h):
                P.run("sp", h, sems)
    return nc


def _consts():
    ident = np.eye(128, dtype=np.float32)
    i = np.arange(128)[:, None]
    c = np.arange(256)[None, :]
    full = np.where(((c < 128) & (c > i)) | ((c >= 128) & (c - 128 <= i)), 0.0, MASKV)
    m1 = np.where(((c < 128) & (c > i) & (c >= NPAD)) | ((c >= 128) & (c - 128 <= i)), 0.0, MASKV)
    m0 = np.where((c >= 128) & (c - 128 <= i) & (c - 128 >= NPAD), 0.0, MASKV)
    masks = np.stack([m0, m1, full], axis=1).astype(np.float32)
    sm = np.full((128, 17 * 128), MASKV, np.float32)
    for s in range(NSEQ):
        for t in range(4):
            r = s * 4 + t
            sm[r, s * 128 + t + 1:(s + 1) * 128] = 0.0
            sm[r, 2048 + s * 4:2048 + s * 4 + t + 1] = 0.0
    return ident, masks, sm


def prep_inputs(x_prompt, x_sample, cache_k_win, cache_v_win, state_ssm_re, state_ssm_im, state_conv,
           meta_tokens, g_mix, w_in, sinks, lam_re, lam_im, log_dt, b_re, b_im, c_re, c_im, d_skip,
           w_glu, b_glu, g_attn_out, g_ssm_out, w_o, g_ffn, w_gate, w_up, conv_w, conv_b, w_down,
           g_final):
    f32 = np.float32
    A = lambda a: np.ascontiguousarray(np.asarray(a, dtype=f32))
    x_prompt, x_sample = A(x_prompt), A(x_sample)
    ident, masks, smask = _consts()
    w_in0 = A(w_in)[0]
    perm = []
    for i in range(4):
        perm += list(range(i * 64, (i + 1) * 64)) + list(range((4 + i) * 64, (5 + i) * 64))
    w_in_p = np.ascontiguousarray(np.concatenate([w_in0[:, perm], w_in0[:, 512:]], axis=1))
    pcol = np.zeros((128, 128), f32)
    pcol[:, 0:8] = A(g_mix)[0].reshape(8, 128).T
    pcol[:, 8:16] = A(g_ffn)[0].reshape(8, 128).T
    pcol[:, 16:20] = A(g_attn_out)[0].reshape(4, 128).T
    pcol[:, 20:24] = A(g_ssm_out)[0].reshape(4, 128).T
    pcol[:, 24:28] = A(b_glu)[0].reshape(4, 128).T
    pcol[:, 28:32] = A(d_skip)[0].reshape(4, 128).T
    cw = A(conv_w)[0].reshape(3, NFC, 128)
    pcol[:, 32:98] = cw.transpose(2, 1, 0).reshape(128, 66)
    pcol[:, 98:120] = A(conv_b)[0].reshape(NFC, 128).T
    lr, li, ld = A(lam_re)[0], A(lam_im)[0], A(log_dt)[0]
    ldx = np.repeat(ld[:, None], 64, axis=1)

    def pl(a):
        return a.reshape(16, 2, 64).transpose(1, 2, 0).reshape(128, 16)

    lam = np.ascontiguousarray(np.stack([pl(lr), pl(li), pl(ldx)], axis=1))
    lamb = np.ascontiguousarray(np.stack([lr.reshape(-1), li.reshape(-1), ldx.reshape(-1)], axis=0))
    bre, bim, cre, cim = A(b_re)[0], A(b_im)[0], A(c_re)[0], A(c_im)[0]
    bblk = np.zeros((128, 2, 16, 128), f32)
    cblk = np.zeros((128, 2, 16, 128), f32)
    for q in range(16):
        for j2 in range(2):
            g = 2 * q + j2
            g8 = g % 8
            rows = slice(g8 * 16, g8 * 16 + 16)
            cols = slice(j2 * 64, j2 * 64 + 64)
            bblk[rows, 0, q, cols] = bre[g].T
            bblk[rows, 1, q, cols] = bim[g].T
            cblk[cols, 0, q, rows] = cre[g].T
            cblk[cols, 1, q, rows] = cim[g].T
    sre, sim_ = A(state_ssm_re)[0], A(state_ssm_im)[0]
    ck, cvv = A(cache_k_win)[0].reshape(128, 128, 128), A(cache_v_win)[0].reshape(128, 128, 128)
    sc = A(state_conv)[0]
    meta = A(meta_tokens)
    wg_l = np.ascontiguousarray(A(w_gate)[0].reshape(8, 128, NFC, 128).transpose(2, 1, 0, 3))
    wu_l = np.ascontiguousarray(A(w_up)[0].reshape(8, 128, NFC, 128).transpose(2, 1, 0, 3))
    wd_l = np.ascontiguousarray(A(w_down)[0].reshape(NFC // 2, 2, 128, 2, 512).transpose(3, 0, 2, 1, 4))
    in_maps = []
    for c in range(NCORES):
        xin = np.zeros((NPT * 128 + 128, D), f32)
        xin[NPAD:128] = meta
        xin[128:NPT * 128] = x_prompt[c]
        xin[NPT * 128:NPT * 128 + NS] = x_sample[c * NSEQ:(c + 1) * NSEQ].reshape(NS, D)
        sl_ = slice(c * NSEQ, (c + 1) * NSEQ)

        def hl(a):
            return a.reshape(NSEQ, 16, 2, 64).transpose(2, 3, 1, 0).reshape(128, 16, NSEQ)

        h0 = np.ascontiguousarray(np.stack([hl(sre[sl_]), hl(sim_[sl_])], axis=1))
        cvst = np.ascontiguousarray(sc[sl_].reshape(NSEQ, 2, NFC, 128).transpose(3, 2, 0, 1))
        kc, vc = ck[sl_], cvv[sl_]
        kcT = np.ascontiguousarray(kc.transpose(2, 0, 1).reshape(128, NSEQ * 128))
        vcl = np.ascontiguousarray(vc.transpose(1, 0, 2))
        in_maps.append(dict(
            xin=xin, w_in=w_in_p, w_o=A(w_o)[0], w_glu=A(w_glu)[0], w_gate=wg_l, w_up=wu_l,
            w_down=wd_l, ident=ident, masks=masks, smask=smask, pcol=pcol, sinks=A(sinks)[0],
            gfin=A(g_final), lam=lam, lamb=lamb, bblk=bblk, cblk=cblk, h0=h0, cvst=cvst, kcT=kcT, vc=vcl,
            kcache=np.ascontiguousarray(kc), vcache=np.ascontiguousarray(vc)))
    return in_maps


def assemble(R):
    f32 = np.float32
    y_prompt = np.stack([R[c]["y"][128:NPT * 128] for c in range(NCORES)])
    y_sample = np.concatenate([R[c]["y"][NPT * 128:NPT * 128 + NS].reshape(NSEQ, 4, D) for c in range(NCORES)])
    p_k = np.stack([R[c]["kvp"][:, 0:128].reshape(128, 2, 64) for c in range(NCORES)])[None]
    p_v = np.stack([R[c]["kvp"][:, 128:256].reshape(128, 2, 64) for c in range(NCORES)])[None]
    s_k = np.concatenate([R[c]["kvs_k"].reshape(NSEQ, 128, 2, 64) for c in range(NCORES)])[None]
    s_v = np.concatenate([R[c]["kvs_v"].reshape(NSEQ, 128, 2, 64) for c in range(NCORES)])[None]

    def unh(a):
        nn = a.shape[-1]
        return a.reshape(2, 64, 16, nn).transpose(3, 2, 0, 1).reshape(nn, 32, 64)

    p_re = np.stack([unh(R[c]["hfin"][:, 0, :, NSEQ:])[0] for c in range(NCORES)])[None]
    p_im = np.stack([unh(R[c]["hfin"][:, 1, :, NSEQ:])[0] for c in range(NCORES)])[None]
    s_re = np.concatenate([unh(R[c]["hfin"][:, 0, :, :NSEQ]) for c in range(NCORES)])[None]
    s_im = np.concatenate([unh(R[c]["hfin"][:, 1, :, :NSEQ]) for c in range(NCORES)])[None]
    p_conv = np.stack([R[c]["pconv"].transpose(2, 1, 0).reshape(2, FF) for c in range(NCORES)])[None]
    s_conv = np.concatenate([R[c]["sconv"].transpose(2, 3, 1, 0).reshape(NSEQ, 2, FF) for c in range(NCORES)])[None]
    out = (y_prompt, y_sample, p_k, p_v, p_re, p_im, p_conv, s_k, s_v, s_re, s_im, s_conv)
    return tuple(np.ascontiguousarray(o, dtype=f32) for o in out)


def kernel(**inputs):
    in_maps = prep_inputs(**inputs)
    nc = build_nc()
    res = run_bass_kernel_spmd(nc, in_maps, core_ids=list(range(NCORES)))
    return assemble(res.results)
```

```python
import numpy as np
from contextlib import ExitStack
import concourse.bass as bass
import concourse.mybir as mybir
from concourse.bass_utils import run_bass_kernel_spmd

F32 = mybir.dt.float32
BF16 = mybir.dt.bfloat16
ALU = mybir.AluOpType
AF = mybir.ActivationFunctionType
AX = mybir.AxisListType

ENGS = ("pe", "act", "dve", "pool", "sp")
NDMASEM = 12
NCORES = 8
D = 1024
NPT = 17
NPAD = 112
NS = 64
NSEQ = 16
FF = 2816
NFC = 22
EPS = 1e-5
PI = float(np.pi)
MASKV = -30000.0


class _Recorder:
    def __init__(self):
        self.call = None

    def __getattr__(self, name):
        def f(*args, **kwargs):
            self.call = (name, args, kwargs)
            return self
        return f


class Prog:
    def __init__(self):
        self.streams = {e: [] for e in ENGS}
        self.count = {e: 0 for e in ENGS}
        self.waited = {e: {} for e in ENGS}
        self.regs = {}
        self.nrec = 0
        self.limit = None
        self.dma_rr = {"sp": 0, "pool": 0}
        self.dma_cnt = {}

    @staticmethod
    def _region(ap):
        dims = [(int(st), int(sz)) for st, sz in ap.ap]
        off = int(ap.offset)
        esz = mybir.dt.size(ap.dtype)
        space = str(ap.space)
        if space == "DRAM":
            ext = sum((sz - 1) * abs(st) for st, sz in dims)
            return (0, 1, off, off + ext)
        pst, npart = dims[0]
        pst = max(pst, 1)
        p0, f0 = off // pst, off % pst
        ext = sum((sz - 1) * abs(st) for st, sz in dims[1:])
        f0, ext = f0 * esz, ext * esz + esz - 1
        if space == "PSUM":
            return (0, 128, 0, 1 << 30)
        return (p0, p0 + npart, f0, f0 + ext)

    @staticmethod
    def _is_ap(v):
        return hasattr(v, "tensor") and hasattr(v, "ap") and hasattr(v, "offset")

    def _record(self, fn):
        rec = _Recorder()
        fn(rec)
        name, args, kwargs = rec.call
        acc = []
        for i, a in enumerate(args):
            if self._is_ap(a):
                acc.append((a, i == 0))
        for k, v in kwargs.items():
            if self._is_ap(v):
                acc.append((v, k in ("out", "accum_out")))
        out = []
        for ap, w in acc:
            if str(ap.space) == "PSUM":
                w = True
            out.append((ap.tensor.name, self._region(ap), w))
        self._last_call = rec.call
        return out

    def _deps(self, eng, acc):
        deps = {}
        for name, R, w in acc:
            for (R2, w2), evs in self.regs.get(name, {}).items():
                if not (w or w2):
                    continue
                if R[0] < R2[1] and R2[0] < R[1] and R[2] <= R2[3] and R2[2] <= R[3]:
                    for s_, v in evs.items():
                        if s_ == eng and eng == "pe":
                            continue
                        if deps.get(s_, 0) < v:
                            deps[s_] = v
        out = []
        wd = self.waited[eng]
        for s_, v in deps.items():
            if wd.get(s_, 0) < v:
                wd[s_] = v
                out.append((s_, v))
        return out

    def _commit(self, ev, acc):
        s_, v = ev
        for name, R, w in acc:
            d = self.regs.setdefault(name, {})
            if w:
                for key in [k for k in d if k[0][0] >= R[0] and k[0][1] <= R[1] and k[0][2] >= R[2] and k[0][3] <= R[3]]:
                    del d[key]
            e = d.setdefault((R, w), {})
            if e.get(s_, 0) < v:
                e[s_] = v

    def op(self, eng, fn, reads=(), writes=()):
        self.nrec += 1
        if self.limit is not None and self.nrec > self.limit:
            return
        acc = self._record(fn)
        deps = self._deps(eng, acc)
        self.count[eng] += 1
        ev = (eng, self.count[eng])
        self.streams[eng].append((deps, self._last_call, (eng, 1)))
        self._commit(ev, acc)

    def dma(self, eng, fn, reads=(), writes=()):
        self.nrec += 1
        if self.limit is not None and self.nrec > self.limit:
            return
        acc = self._record(fn)
        k = self.dma_rr[eng]
        self.dma_rr[eng] = (k + 1) % NDMASEM
        sname = "dma%s%d" % (eng, k)
        deps = self._deps(eng, acc)
        prev = self.dma_cnt.get(sname, 0) * 16
        if prev and self.waited[eng].get(sname, 0) < prev:
            self.waited[eng][sname] = prev
            deps.append((sname, prev))
        self.dma_cnt[sname] = self.dma_cnt.get(sname, 0) + 1
        ev = (sname, self.dma_cnt[sname] * 16)
        self.streams[eng].append((deps, self._last_call, (sname, 16)))
        self._commit(ev, acc)

    def barrier(self):
        evs = [(e, self.count[e]) for e in ENGS if self.count[e]]
        evs += [(k, c * 16) for k, c in self.dma_cnt.items()]
        for e in ENGS:
            deps = []
            for s, v in evs:
                if s == e:
                    continue
                if self.waited[e].get(s, 0) < v:
                    self.waited[e][s] = v
                    deps.append((s, v))
            if deps:
                self.streams[e].append((deps, None, None))

    def run(self, eng, h, sems):
        for deps, fn, inc in self.streams[eng]:
            for s, v in deps:
                h.wait_ge(sems[s], v)
            if fn is not None:
                name, args, kwargs = fn
                getattr(h, name)(*args, **kwargs).then_inc(sems[inc[0]], inc[1])


def build_nc(tiles=None, limit=None):
    nc = bass.Bass("TRN2", target_bir_lowering=False)

    def din(name, shape, dt=F32):
        return nc.dram_tensor(name, list(shape), dt, kind="ExternalInput").ap()

    def dout(name, shape, dt=F32):
        return nc.dram_tensor(name, list(shape), dt, kind="ExternalOutput").ap()

    xin = din("xin", [NPT * 128 + 128, D])
    w_in = din("w_in", [D, 1280])
    w_o = din("w_o", [D, D])
    w_glu = din("w_glu", [512, 512])
    w_gate = din("w_gate", [NFC, 128, 8, 128])
    w_up = din("w_up", [NFC, 128, 8, 128])
    w_down = din("w_down", [2, NFC // 2, 128, 2, 512])
    ident_d = din("ident", [128, 128])
    masks_d = din("masks", [128, 3, 256])
    smask_d = din("smask", [128, 17 * 128])
    pcol_d = din("pcol", [128, 128])
    sinks_d = din("sinks", [8])
    gfin_d = din("gfin", [D])
    lam_d = din("lam", [128, 3, 16])
    lamb_d = din("lamb", [3, 16 * 128])
    bblk_d = din("bblk", [128, 2, 16, 128])
    cblk_d = din("cblk", [128, 2, 16, 128])
    h0_d = din("h0", [128, 2, 16, NSEQ])
    cvst_d = din("cvst", [128, NFC, NSEQ, 2])
    kcT_d = din("kcT", [128, NSEQ * 128])
    vc_d = din("vc", [128, NSEQ, 128])
    kcache_d = din("kcache", [NSEQ, 128, 128])
    vcache_d = din("vcache", [NSEQ, 128, 128])

    wgb_d = nc.dram_tensor("wgb", [NFC, 128, 8, 128], BF16, kind="Internal").ap()
    wub_d = nc.dram_tensor("wub", [NFC, 128, 8, 128], BF16, kind="Internal").ap()
    wdb_d = nc.dram_tensor("wdb", [2, NFC // 2, 128, 2, 512], BF16, kind="Internal").ap()
    y_d = dout("y", [NPT * 128 + 128, D])
    kvp_d = dout("kvp", [128, 256])
    kvs_k = dout("kvs_k", [NSEQ, 128, 128])
    kvs_v = dout("kvs_v", [NSEQ, 128, 128])
    hfin_d = dout("hfin", [128, 2, 16, NSEQ + 1])
    pconv_d = dout("pconv", [128, NFC, 2])
    sconv_d = dout("sconv", [128, NFC, NSEQ, 2])

    P = Prog()
    P.limit = limit
    es = ExitStack()
    with es:
        def sb(name, shape, dt=F32):
            return es.enter_context(nc.sbuf_tensor(name, list(shape), dt))

        def ps(name, shape, dt=F32):
            return es.enter_context(nc.psum_tensor(name, list(shape), dt))

        sems = {e: es.enter_context(nc.semaphore("s_" + e)) for e in ENGS}
        for k in range(NDMASEM):
            for e_ in ("sp", "pool"):
                sems["dma%s%d" % (e_, k)] = es.enter_context(nc.semaphore("s_dma%s%d" % (e_, k)))

        idb = sb("idb", [128, 128], BF16)
        onesb = sb("onesb", [128, 128], BF16)
        masks = sb("masks_s", [128, 3, 256])
        smask = sb("smask_s", [128, 17 * 128], BF16)
        pcol = sb("pcol_s", [128, 128])
        sink8 = sb("sink8", [128, 8])
        gfb = sb("gfb", [128, D])
        epsb = sb("epsb", [128, 1])
        lam = sb("lam_s", [128, 3, 16])
        WBre = sb("WBre", [128, 16, 128], BF16); WBim = sb("WBim", [128, 16, 128], BF16)
        WCre = sb("WCre", [128, 16, 128], BF16); WCimn = sb("WCimn", [128, 16, 128], BF16)
        cs = sb("cs", [128, 16, 129]); sn = sb("sn", [128, 16, 129])
        mask64 = sb("mask64", [128, NS]); dtmp = sb("dtmp", [128, NS])
        are = sb("are", [128, 16]); aim = sb("aim", [128, 16])
        ah = sb("ah", [128, 2, 16, NSEQ])
        car = sb("car", [128, 2, 16])
        hfin = sb("hfin_s", [128, 2, 16, NSEQ + 1])
        gcar = sb("gcar", [128, NFC, 2])
        sconv = sb("sconv_s", [128, NFC, NSEQ, 2])
        kTx = sb("kTx", [128, 2, 256], BF16)
        vx = sb("vx", [128, 2, 128], BF16)
        GMIX, GFFN, GATT, GSSM, BGLU, DSK, CVW, CVB = 0, 8, 16, 20, 24, 28, 32, 98

        def col(c):
            return pcol[:, c:c + 1]

        ssq = sb("ssq", [128, 4]); rstd = sb("rstd", [128, 4])
        xn = sb("xn", [128, D], BF16)
        junk = xn
        xnT = sb("xnT", [128, 8, 128], BF16)
        qT = sb("qT", [128, 4, 128], BF16); uT = sb("uT", [128, 4, 128], BF16)
        Sx = sb("Sx", [128, 1, 2, 257]); mx = sb("mx", [128, 2]); nbias = sb("nbias", [128, 2])
        rs = sb("rs", [128, 2]); rinv = sb("rinv", [128, 2])
        Pb = sb("Pb", [128, 2, 257], BF16); PTs = sb("PTs", [128, 4, 128], BF16)
        anb = sb("anb", [128, 512], BF16)
        mixT = sb("mixT", [128, 8, 128], BF16)
        Psb = sb("Psb", [128, 17 * 128 + 1], BF16)
        PTss = sb("PTss", [128, 17, 128], BF16)
        Bs2 = [sb("Bs%d" % i, [128, 512]) for i in range(2)]
        pp2 = [sb("pp%d" % i, [128, 4, 2, 128]) for i in range(2)]
        rr2 = [sb("rr%d" % i, [128, 2, 2, 128]) for i in range(2)]
        vv2 = [sb("vv%d" % i, [128, 2, 2, 128]) for i in range(2)]
        hb = sb("hb", [128, 4, 2, 128], BF16)
        yv = sb("yv", [128, 128]); gl32 = sb("gl32", [128, 4, 128]); glb = sb("glb", [128, 4, 128], BF16)
        sg = sb("sg", [128, 128]); sqb = sb("sqb", [128, 4, 128], BF16); rsb = sb("rsb", [128, 128])
        xn2T = sb("xn2T", [128, 8, 512], BF16)
        gx = sb("gx", [128, 2, 514]); gxs = sb("gxs", [128, NSEQ, 6])
        cv = sb("cv", [128, 1, 512]); sl = sb("sl", [128, 1, 512])
        kvtok = sl[:, 0, 0:256]
        attn = cv[:, 0, :]
        wi = [sb("wi0", [128, 8, 1280], BF16)] * 2
        wo = [sb("wo0", [128, 8, D], BF16)] * 2
        wgl = [sb("wgl0", [128, 4, 512], BF16)] * 2
        wg = [sb("wg%d" % i, [128, 8, 128], BF16) for i in range(2)] + [xnT]
        wu = [sb("wu%d" % i, [128, 8, 128], BF16) for i in range(2)] + [mixT]
        wdA = sb("wdA", [128, 2, 512], BF16)
        wd = [wdA[:], Sx[:].rearrange("p a b c -> p (a b c)").bitcast(BF16)[:, 0:1024].rearrange("p (j n) -> p j n", j=2),
              xn[:].rearrange("p (j n) -> p j n", j=2)]
        big = sb("big", [128, 9728])
        xb = big[:, 0:4096].rearrange("p (t d) -> p t d", t=4)
        hTall = big[:, 4096:9728].bitcast(BF16).rearrange("p (c n) -> p c n", c=NFC)
        hTsm = big[:, 4096:5504].bitcast(BF16).rearrange("p (c n) -> p c n", c=NFC)
        kTs = big[:, 5504:7680].bitcast(BF16).rearrange("p (a k) -> p a k", a=2)
        vs = big[:, 7680:8768].bitcast(BF16).rearrange("p (b k) -> p b k", b=17)
        scr = big
        h0 = big[:, 4352:4864].rearrange("p (a q s) -> p a q s", a=2, q=16)
        cvst = big[:, 8768:9472].rearrange("p (c s j) -> p c s j", c=NFC, s=NSEQ)
        lb = scr[:, 0:768].rearrange("p (a b) -> p a b", a=3); ft = scr[:, 768:2304].rearrange("p (a b) -> p a b", a=6)
        bblk = scr[:, 2304:2816].rearrange("p (a q m) -> p a q m", a=2, q=2); cblk = scr[:, 2816:3328].rearrange("p (a q m) -> p a q m", a=2, q=2)
        Ssx = big[:, 1024:1024 + 17 * 128 + 1]
        idf = big[:, 5632:5760]

        mmA = ps("mmA", [128, 512]); mmB = ps("mmB", [128, 512]); mmC = ps("mmC", [128, 512])
        pT = ps("pT", [128, 8, 128], BF16)
        pT32 = pT[:].rearrange("p c k -> p (c k)").bitcast(F32)
        Sps = ps("Sps", [128, 512]); Ops = ps("Ops", [128, 512])
        acc0 = ps("acc0", [128, 512]); acc1 = ps("acc1", [128, 512])

        op = P.op

        P.dma("sp", lambda h: h.dma_start(out=idf[:], in_=ident_d), writes=["idf"])
        P.dma("sp", lambda h: h.dma_start(out=masks[:], in_=masks_d), writes=["masks"])
        P.dma("pool", lambda h: h.dma_start(out=smask[:], in_=smask_d), writes=["smask"])
        P.dma("sp", lambda h: h.dma_start(out=pcol[:], in_=pcol_d), writes=["pcol"])
        P.dma("sp", lambda h: h.dma_start(out=sink8[:], in_=sinks_d.partition_broadcast(128)), writes=["sink8"])
        P.dma("sp", lambda h: h.dma_start(out=gfb[:], in_=gfin_d.partition_broadcast(128)), writes=["gfb"])
        P.dma("sp", lambda h: h.dma_start(out=lam[:], in_=lam_d), writes=["lam"])
        P.dma("sp", lambda h: h.dma_start(out=h0, in_=h0_d))
        op("pool", lambda h: h.memset(hb[:], 0.0))
        op("pool", lambda h: h.memset(cv[:], 0.0))
        op("pool", lambda h: h.memset(gx[:], 0.0))
        P.dma("pool", lambda h: h.dma_start(out=wi[0][:], in_=w_in.rearrange("(c p) n -> p c n", p=128)))
        P.dma("pool", lambda h: h.dma_start(out=wgl[0][:], in_=w_glu.rearrange("(c p) n -> p c n", p=128)))
        P.dma("pool", lambda h: h.dma_start(out=wo[0][:], in_=w_o.rearrange("(c p) n -> p c n", p=128)))
        for a_ in range(0, NFC, 11):
            P.dma("pool", lambda h: h.dma_start(out=wgb_d[a_:a_ + 11], in_=w_gate[a_:a_ + 11]))
            P.dma("pool", lambda h: h.dma_start(out=wub_d[a_:a_ + 11], in_=w_up[a_:a_ + 11]))
        for hf_ in range(2):
            P.dma("pool", lambda h: h.dma_start(out=wdb_d[hf_], in_=w_down[hf_]))
        P.dma("sp", lambda h: h.dma_start(out=kvs_k[:, 0:124, :], in_=kcache_d[:, 4:128, :]), writes=["kvs_k_a"])
        P.dma("sp", lambda h: h.dma_start(out=kvs_v[:, 0:124, :], in_=vcache_d[:, 4:128, :]), writes=["kvs_v_a"])

        op("dve", lambda h: h.tensor_copy(out=idb[:], in_=idf[:]), ["idf"], ["idb"])
        op("pool", lambda h: h.memset(onesb[:], 1.0), [], ["onesb"])
        op("pool", lambda h: h.memset(epsb[:], EPS), [], ["epsb"])
        op("pool", lambda h: h.memset(kTx[:], 0.0), [], ["kTx"])
        op("pool", lambda h: h.memset(vx[:], 0.0), [], ["vx"])
        op("pool", lambda h: h.memset(car[:], 0.0), [], ["car"])
        op("pool", lambda h: h.memset(gcar[:], 0.0), [], ["gcar"])
        pass
        op("pool", lambda h: h.memset(hfin[:], 0.0))
        op("pool", lambda h: h.memset(sconv[:], 0.0))
        rho = sb("rho", [128, 16])
        dtl, thl, c1, s1, tmpa, kfL = (big[:, 5760 + 16 * i:5776 + 16 * i] for i in range(6))
        kiL = big[:, 5856:5872].bitcast(mybir.dt.int32)
        kiR = big[:, 4864:5120].bitcast(mybir.dt.int32); kfR = big[:, 5120:5376]
        op("act", lambda h: h.activation(out=dtl[:], in_=lam[:, 2, :], func=AF.Exp), ["lam"], ["dtl"])
        op("dve", lambda h: h.tensor_tensor(out=thl[:], in0=lam[:, 1, :], in1=dtl[:], op=ALU.mult), ["lam", "dtl"], ["thl"])
        op("dve", lambda h: h.tensor_tensor(out=tmpa[:], in0=lam[:, 0, :], in1=dtl[:], op=ALU.mult), ["lam", "dtl"], ["tmpa"])
        op("act", lambda h: h.activation(out=rho[:], in_=tmpa[:], func=AF.Exp), ["tmpa"], ["rho"])

        def sincos(eng_tag, th_ap, s_ap, c_ap, tmp_ap, shape_key, ki_ap, kf_ap):
            K = shape_key

            def reduce_(shift, dst_key):
                op("dve", lambda h: h.tensor_scalar(out=tmp_ap, in0=th_ap, scalar1=shift, scalar2=1.0 / (2 * PI), op0=ALU.add, op1=ALU.mult),
                   [K + "th", K + "s", K + "c"], [K + "tmp"])
                op("dve", lambda h: h.tensor_copy(out=ki_ap, in_=tmp_ap), [K + "tmp"], [K + "ki"])
                op("dve", lambda h: h.tensor_copy(out=kf_ap, in_=ki_ap), [K + "ki"], [K + "kf"])
                op("dve", lambda h: h.tensor_scalar(out=tmp_ap, in0=th_ap, scalar1=shift, scalar2=None, op0=ALU.add), [K + "th", K + "ki"], [K + "tmp"])
                op("dve", lambda h: h.scalar_tensor_tensor(out=tmp_ap, in0=kf_ap, scalar=-2 * PI, in1=tmp_ap, op0=ALU.mult, op1=ALU.add),
                   [K + "kf", K + "tmp"], [K + "tmp"])
                op("dve", lambda h: h.tensor_scalar(out=kf_ap, in0=tmp_ap, scalar1=PI, scalar2=None, op0=ALU.is_gt), [K + "tmp"], [K + "kf"])
                op("dve", lambda h: h.scalar_tensor_tensor(out=tmp_ap, in0=kf_ap, scalar=-2 * PI, in1=tmp_ap, op0=ALU.mult, op1=ALU.add),
                   [K + "kf", K + "tmp"], [K + "tmp"])
                op("dve", lambda h: h.tensor_scalar(out=kf_ap, in0=tmp_ap, scalar1=-PI, scalar2=None, op0=ALU.is_lt), [K + "tmp"], [K + "kf"])
                op("dve", lambda h: h.scalar_tensor_tensor(out=tmp_ap, in0=kf_ap, scalar=2 * PI, in1=tmp_ap, op0=ALU.mult, op1=ALU.add),
                   [K + "kf", K + "tmp"], [K + "tmp"])
                op("dve", lambda h: h.tensor_scalar(out=tmp_ap, in0=tmp_ap, scalar1=-PI, scalar2=PI, op0=ALU.max, op1=ALU.min), [K + "tmp"], [K + "tmp"])

            reduce_(0.0, "s")
            op("act", lambda h: h.activation(out=s_ap, in_=tmp_ap, func=AF.Sin), [K + "tmp"], [K + "s"])
            reduce_(0.5 * PI, "c")
            op("act", lambda h: h.activation(out=c_ap, in_=tmp_ap, func=AF.Sin), [K + "tmp"], [K + "c"])

        sincos("L", thl[:], s1[:], c1[:], tmpa[:], "L", kiL[:], kfL[:])
        op("dve", lambda h: h.tensor_tensor(out=are[:], in0=rho[:], in1=c1[:], op=ALU.mult), ["rho", "Lc"], ["are"])
        op("dve", lambda h: h.tensor_tensor(out=aim[:], in0=rho[:], in1=s1[:], op=ALU.mult), ["rho", "Ls"], ["aim"])
        op("pool", lambda h: h.memset(cs[:, :, 0:1], 1.0), [], ["cs"])
        op("pool", lambda h: h.memset(sn[:, :, 0:1], 0.0), [], ["sn"])
        op("dve", lambda h: h.tensor_copy(out=cs[:, :, 1], in_=c1[:]), ["Lc", "cs"], ["cs"])
        op("dve", lambda h: h.tensor_copy(out=sn[:, :, 1], in_=s1[:]), ["Ls", "sn"], ["sn"])
        tA = pp2[0][:].rearrange("p a j (b c) -> p (a j b) c", c=64); tB = big[:, 3328:4352].rearrange("p (a c) -> p a c", c=64)
        m = 1
        while m < 128:
            cm = cs[:, :, m:m + 1].broadcast_to([128, 16, m]); sm = sn[:, :, m:m + 1].broadcast_to([128, 16, m])
            a_c = cs[:, :, 1:m + 1]; a_s = sn[:, :, 1:m + 1]
            o_c = cs[:, :, m + 1:2 * m + 1]; o_s = sn[:, :, m + 1:2 * m + 1]
            ta = tA[:, :, 0:m]; tb = tB[:, :, 0:m]
            op("dve", lambda h, a_c=a_c, cm=cm, ta=ta: h.tensor_tensor(out=ta, in0=a_c, in1=cm, op=ALU.mult), ["cs", "sn"], ["tA"])
            op("dve", lambda h, a_s=a_s, sm=sm, tb=tb: h.tensor_tensor(out=tb, in0=a_s, in1=sm, op=ALU.mult), ["cs", "sn"], ["tB"])
            op("dve", lambda h, o_c=o_c, ta=ta, tb=tb: h.tensor_tensor(out=o_c, in0=ta, in1=tb, op=ALU.subtract), ["tA", "tB", "sn"], ["cs"])
            op("dve", lambda h, a_c=a_c, sm=sm, ta=ta: h.tensor_tensor(out=ta, in0=a_c, in1=sm, op=ALU.mult), ["cs", "sn"], ["tA"])
            op("dve", lambda h, a_s=a_s, cm=cm, tb=tb: h.tensor_tensor(out=tb, in0=a_s, in1=cm, op=ALU.mult), ["cs", "sn"], ["tB"])
            op("dve", lambda h, o_s=o_s, ta=ta, tb=tb: h.tensor_tensor(out=o_s, in0=ta, in1=tb, op=ALU.add), ["tA", "tB", "cs"], ["sn"])
            m *= 2
        op("pool", lambda h: h.memset(mask64[:], 1.0))
        op("pool", lambda h: h.memset(mask64[:].rearrange("p (s t) -> p s t", t=4)[:, :, 0:1], 0.0))
        a_re_b = are[:].unsqueeze(2).broadcast_to([128, 16, NSEQ]); a_im_b = aim[:].unsqueeze(2).broadcast_to([128, 16, NSEQ])
        tC = big[:, 5376:5632].rearrange("p (q s) -> p q s", q=16)
        op("dve", lambda h: h.tensor_tensor(out=ah[:, 0], in0=h0[:, 0], in1=a_re_b, op=ALU.mult), ["h0", "are"], ["ah0"])
        op("dve", lambda h: h.tensor_tensor(out=tC[:], in0=h0[:, 1], in1=a_im_b, op=ALU.mult), ["h0", "aim"], ["tC"])
        op("dve", lambda h: h.tensor_tensor(out=ah[:, 0], in0=ah[:, 0], in1=tC[:], op=ALU.subtract), ["ah0", "tC"], ["ah0"])
        op("dve", lambda h: h.tensor_tensor(out=ah[:, 1], in0=h0[:, 0], in1=a_im_b, op=ALU.mult), ["h0", "aim"], ["ah1"])
        op("dve", lambda h: h.tensor_tensor(out=tC[:], in0=h0[:, 1], in1=a_re_b, op=ALU.mult), ["h0", "are", "ah0"], ["tC"])
        op("dve", lambda h: h.tensor_tensor(out=ah[:, 1], in0=ah[:, 1], in1=tC[:], op=ALU.add), ["ah1", "tC"], ["ah1"])

        for pc in range(8):
            for i in range(3):
                P.dma("sp", lambda h, i=i, pc=pc: h.dma_start(out=lb[:, i, :], in_=lamb_d[i, pc * 256:(pc + 1) * 256].partition_broadcast(128)), writes=["lb"])
            P.dma("sp", lambda h, pc=pc: h.dma_start(out=bblk[:], in_=bblk_d[:, :, pc * 2:(pc + 1) * 2, :]), writes=["bblk"])
            P.dma("sp", lambda h, pc=pc: h.dma_start(out=cblk[:], in_=cblk_d[:, :, pc * 2:(pc + 1) * 2, :]), writes=["cblk"])
            LR, LI, LD = lb[:, 0, :], lb[:, 1, :], lb[:, 2, :]
            f0, f1, f2, f3, f4, f5 = (ft[:, i, :] for i in range(6))
            op("act", lambda h: h.activation(out=f0, in_=LD, func=AF.Exp), ["lb"], ["f0"])
            op("dve", lambda h: h.tensor_tensor(out=f1, in0=LI, in1=f0, op=ALU.mult), ["lb", "f0"], ["Rth"])
            op("dve", lambda h: h.tensor_tensor(out=f2, in0=LR, in1=f0, op=ALU.mult), ["lb", "f0"], ["f2"])
            op("act", lambda h: h.activation(out=f2, in_=f2, func=AF.Exp), ["f2"], ["f2"])
            sincos("R", f1, f3, f4, f5, "R", kiR[:], kfR[:])
            op("dve", lambda h: h.tensor_tensor(out=f4, in0=f4, in1=f2, op=ALU.mult), ["Rc", "f2"], ["Rc"])
            op("dve", lambda h: h.tensor_scalar(out=f4, in0=f4, scalar1=-1.0, scalar2=None, op0=ALU.add), ["Rc"], ["Rc"])
            op("dve", lambda h: h.tensor_tensor(out=f3, in0=f3, in1=f2, op=ALU.mult), ["Rs", "f2"], ["Rs"])
            op("dve", lambda h: h.tensor_tensor(out=f0, in0=LR, in1=LR, op=ALU.mult), ["lb", "Rth", "f2"], ["f0"])
            op("dve", lambda h: h.tensor_tensor(out=f5, in0=LI, in1=LI, op=ALU.mult), ["lb", "Rc"], ["Rtmp"])
            op("dve", lambda h: h.tensor_tensor(out=f0, in0=f0, in1=f5, op=ALU.add), ["f0", "Rtmp"], ["f0"])
            op("dve", lambda h: h.reciprocal(out=f0, in_=f0), ["f0"], ["f0"])
            op("dve", lambda h: h.tensor_tensor(out=f1, in0=f4, in1=LR, op=ALU.mult), ["Rc", "lb", "Rth", "Rtmp"], ["f1"])
            op("dve", lambda h: h.tensor_tensor(out=f5, in0=f3, in1=LI, op=ALU.mult), ["Rs", "lb"], ["Rtmp"])
            op("dve", lambda h: h.tensor_tensor(out=f1, in0=f1, in1=f5, op=ALU.add), ["f1", "Rtmp"], ["f1"])
            op("dve", lambda h: h.tensor_tensor(out=f2, in0=f3, in1=LR, op=ALU.mult), ["Rs", "lb"], ["f2"])
            op("dve", lambda h: h.tensor_tensor(out=f5, in0=f4, in1=LI, op=ALU.mult), ["Rc", "lb", "f1"], ["Rtmp"])
            op("dve", lambda h: h.tensor_tensor(out=f2, in0=f2, in1=f5, op=ALU.subtract), ["f2", "Rtmp"], ["f2"])
            op("dve", lambda h: h.tensor_tensor(out=f1, in0=f1, in1=f0, op=ALU.mult), ["f1", "f0"], ["f1"])
            op("dve", lambda h: h.tensor_tensor(out=f2, in0=f2, in1=f0, op=ALU.mult), ["f2", "f0"], ["f2"])
            Bre_, Bim_ = bblk[:, 0].rearrange("p q m -> p (q m)"), bblk[:, 1].rearrange("p q m -> p (q m)")
            op("dve", lambda h: h.tensor_tensor(out=f3, in0=Bre_, in1=f1, op=ALU.mult), ["bblk", "f1", "Rs"], ["f3"])
            op("dve", lambda h: h.tensor_tensor(out=f4, in0=Bim_, in1=f2, op=ALU.mult), ["bblk", "f2", "Rc"], ["f4"])
            op("dve", lambda h, pc=pc: h.tensor_tensor(out=WBre[:, pc * 2:(pc + 1) * 2, :].rearrange("p q m -> p (q m)"), in0=f3, in1=f4, op=ALU.subtract), ["f3", "f4"], ["WBre"])
            op("dve", lambda h: h.tensor_tensor(out=f3, in0=Bre_, in1=f2, op=ALU.mult), ["bblk", "f2", "WBre"], ["f3"])
            op("dve", lambda h: h.tensor_tensor(out=f4, in0=Bim_, in1=f1, op=ALU.mult), ["bblk", "f1", "WBre"], ["f4"])
            op("dve", lambda h, pc=pc: h.tensor_tensor(out=WBim[:, pc * 2:(pc + 1) * 2, :].rearrange("p q m -> p (q m)"), in0=f3, in1=f4, op=ALU.add), ["f3", "f4"], ["WBim"])
            op("act", lambda h, pc=pc: h.activation(out=WCre[:, pc * 2:(pc + 1) * 2, :], in_=cblk[:, 0], func=AF.Copy), ["cblk"], ["WCre"])
            op("act", lambda h, pc=pc: h.activation(out=WCimn[:, pc * 2:(pc + 1) * 2, :], in_=cblk[:, 1], func=AF.Copy, scale=-1.0), ["cblk"], ["WCimn"])

        def rms(src_ap, key_src, n, slot, scale, pn):
            op("act", lambda h: h.activation(out=junk[0:pn, 0:n], in_=src_ap, func=AF.Square, accum_out=ssq[0:pn, slot:slot + 1]),
               [key_src], ["junk", "ssq%d" % slot])
            op("act", lambda h: h.activation(out=rstd[0:pn, slot:slot + 1], in_=ssq[0:pn, slot:slot + 1], func=AF.Ln, scale=scale, bias=epsb[0:pn, 0:1]))
            op("act", lambda h: h.activation(out=rstd[0:pn, slot:slot + 1], in_=rstd[0:pn, slot:slot + 1], func=AF.Exp, scale=-0.5))

        def mixer(ti, tl):
            sample = (ti == NPT)
            xt = xb[:, tl, :]
            n = 128
            ns = NS if sample else 128
            r0 = ti * 128
            par = ti % 2
            if sample:
                op("pool", lambda h: h.memset(kTs[:], 0.0))
                P.dma("pool", lambda h: h.dma_start(out=kTs[0:64, 0, 0:NSEQ * 128], in_=kcT_d[0:64, :]))
                P.dma("pool", lambda h: h.dma_start(out=kTs[64:128, 1, 0:NSEQ * 128], in_=kcT_d[64:128, :]))
                P.dma("pool", lambda h: h.dma_start(out=vs[:, 0:NSEQ, :], in_=vc_d))
                P.dma("sp", lambda h: h.dma_start(out=cvst, in_=cvst_d))
            rms(xt[:], "xt", D, 0, 1.0 / D, 128)
            op("dve", lambda h: h.tensor_scalar(out=xn[:], in0=xt[:], scalar1=rstd[:, 0:1], scalar2=None, op0=ALU.mult))
            for c in range(8):
                op("pe", lambda h, c=c: h.transpose(out=pT[:, c, :], in_=xn[:, c * 128:(c + 1) * 128], identity=idb[:]))
            op("dve", lambda h: h.tensor_tensor(out=xnT[:], in0=pT[:], in1=pcol[:, GMIX:GMIX + 8].unsqueeze(2).broadcast_to([128, 8, 128]), op=ALU.mult))
            W = wi[par]
            for i in range(4):
                bank = mmA if i % 2 == 0 else mmB
                for c in range(8):
                    op("pe", lambda h, i=i, c=c, bank=bank: h.matmul(bank[:, 0:128], lhsT=W[:, c, i * 128:(i + 1) * 128], rhs=xnT[:, c, :],
                                                                   start=(c == 0), stop=(c == 7)))
                op("act", lambda h, i=i, bank=bank: h.activation(out=qT[:, i, :], in_=bank[:, 0:128], func=AF.Copy))
            for c in range(8):
                op("pe", lambda h, c=c: h.matmul(mmC[:, 0:128], lhsT=W[:, c, 512:640], rhs=xnT[:, c, :], start=(c == 0), stop=(c == 7)))
            if sample:
                op("dve", lambda h: h.tensor_copy(out=kTs[0:64, 0, NSEQ * 128:17 * 128], in_=mmC[0:64, 0:128]))
                op("dve", lambda h: h.tensor_copy(out=kTs[64:128, 1, NSEQ * 128:17 * 128], in_=mmC[64:128, 0:128]))
            else:
                op("dve", lambda h: h.tensor_copy(out=kTx[0:64, 0, 128:256], in_=mmC[0:64, 0:128]))
                op("dve", lambda h: h.tensor_copy(out=kTx[64:128, 1, 128:256], in_=mmC[64:128, 0:128]))
            for i in range(4):
                bank = mmA if i % 2 == 0 else mmB
                for c in range(8):
                    op("pe", lambda h, i=i, c=c, bank=bank: h.matmul(bank[:, 0:128], lhsT=W[:, c, 768 + i * 128:768 + (i + 1) * 128], rhs=xnT[:, c, :],
                                                                   start=(c == 0), stop=(c == 7)))
                op("act", lambda h, i=i, bank=bank: h.activation(out=uT[:, i, :], in_=bank[:, 0:128], func=AF.Copy))
            for c in range(8):
                op("pe", lambda h, c=c: h.matmul(mmC[:, 0:256], lhsT=xnT[:, c, :], rhs=W[:, c, 512:768], start=(c == 0), stop=(c == 7)))
            if sample:
                op("dve", lambda h: h.tensor_copy(out=vs[:, 16, :], in_=mmC[:, 128:256]))
            else:
                op("dve", lambda h: h.tensor_copy(out=vx[:, 1, :], in_=mmC[:, 128:256]))
            if sample or ti == NPT - 1:
                op("dve", lambda h: h.tensor_copy(out=kvtok[:], in_=mmC[:, 0:256]))
                if sample:
                    P.dma("sp", lambda h: h.dma_start(out=kvs_k[:, 124:128, :], in_=kvtok[0:NS, 0:128]))
                    P.dma("sp", lambda h: h.dma_start(out=kvs_v[:, 124:128, :], in_=kvtok[0:NS, 128:256]))
                else:
                    P.dma("sp", lambda h: h.dma_start(out=kvp_d, in_=kvtok[:]))

            if not sample:
                mi = 0 if ti == 0 else (1 if ti == 1 else 2)
                for i in range(4):
                    for hh_ in range(2):
                        op("pe", lambda h, i=i, hh_=hh_: h.matmul(Sps[:, hh_ * 256:(hh_ + 1) * 256], lhsT=qT[:, i, :], rhs=kTx[:, hh_, :], start=True, stop=True))
                    for hh_ in range(2):
                        op("dve", lambda h, hh_=hh_, hd=i + 4 * hh_: h.tensor_scalar(out=Sx[:, 0, hh_, 256:257], in0=sink8[:, hd:hd + 1], scalar1=8.0, scalar2=None, op0=ALU.mult))
                    op("dve", lambda h: h.tensor_tensor(out=Sx[:, 0, :, 0:256], in0=Sps[:].rearrange("p (a k) -> p a k", a=2),
                                                        in1=masks[:, mi:mi + 1, :].broadcast_to([128, 2, 256]), op=ALU.add))
                    op("dve", lambda h: h.tensor_reduce(out=mx[:], in_=Sx[:, 0], axis=AX.X, op=ALU.max))
                    op("dve", lambda h: h.tensor_scalar(out=nbias[:], in0=mx[:], scalar1=-0.125, scalar2=None, op0=ALU.mult))
                    for hh_ in range(2):
                        op("act", lambda h, hh_=hh_: h.activation(out=Pb[:, hh_, :], in_=Sx[:, 0, hh_, :], func=AF.Exp, scale=0.125,
                                                                 bias=nbias[:, hh_:hh_ + 1], accum_out=rs[:, hh_:hh_ + 1]))
                    op("dve", lambda h: h.reciprocal(out=rinv[:], in_=rs[:]))
                    for hh_ in range(2):
                        for blk in range(2):
                            op("pe", lambda h, hh_=hh_, blk=blk: h.transpose(out=pT[:, hh_ * 2 + blk, :], in_=Pb[:, hh_, blk * 128:(blk + 1) * 128], identity=idb[:]))
                    op("act", lambda h: h.activation(out=PTs[:], in_=pT[:, 0:4, :], func=AF.Copy))
                    for hh_ in range(2):
                        for blk in range(2):
                            op("pe", lambda h, hh_=hh_, blk=blk: h.matmul(Ops[:, hh_ * 64:(hh_ + 1) * 64], lhsT=PTs[:, hh_ * 2 + blk, :],
                                                                         rhs=vx[:, blk, hh_ * 64:(hh_ + 1) * 64], start=(blk == 0), stop=(blk == 1)))
                    for hh_ in range(2):
                        hd = i + 4 * hh_
                        op("dve", lambda h, hh_=hh_, hd=hd: h.tensor_scalar(out=attn[:, hd * 64:(hd + 1) * 64], in0=Ops[:, hh_ * 64:(hh_ + 1) * 64],
                                                                           scalar1=rinv[:, hh_:hh_ + 1], scalar2=None, op0=ALU.mult))
                op("pool", lambda h: h.tensor_copy(out=kTx[:, :, 0:128], in_=kTx[:, :, 128:256]))
                op("pool", lambda h: h.tensor_copy(out=vx[:, 0, :], in_=vx[:, 1, :]))
            else:
                W17 = 17 * 128
                for hd in range(8):
                    i, hh_ = hd % 4, hd // 4
                    for cb in range(5):
                        c0 = cb * 512
                        cw = min(512, W17 - c0)
                        bank = mmA if cb % 2 == 0 else mmB
                        op("pe", lambda h, i=i, hh_=hh_, c0=c0, cw=cw, bank=bank: h.matmul(bank[:, 0:cw], lhsT=qT[:, i, :], rhs=kTs[:, hh_, c0:c0 + cw], start=True, stop=True))
                        op("dve", lambda h, c0=c0, cw=cw, bank=bank: h.tensor_tensor(out=Ssx[:, c0:c0 + cw], in0=bank[:, 0:cw], in1=smask[:, c0:c0 + cw], op=ALU.add))
                    op("dve", lambda h, hd=hd: h.tensor_scalar(out=Ssx[:, W17:W17 + 1], in0=sink8[:, hd:hd + 1], scalar1=8.0, scalar2=None, op0=ALU.mult))
                    op("dve", lambda h: h.tensor_reduce(out=mx[:, 0:1], in_=Ssx[:], axis=AX.X, op=ALU.max))
                    op("dve", lambda h: h.tensor_scalar(out=nbias[:, 0:1], in0=mx[:, 0:1], scalar1=-0.125, scalar2=None, op0=ALU.mult))
                    op("act", lambda h: h.activation(out=Psb[:], in_=Ssx[:], func=AF.Exp, scale=0.125, bias=nbias[:, 0:1], accum_out=rs[:, 0:1]))
                    op("dve", lambda h: h.reciprocal(out=rinv[:, 0:1], in_=rs[:, 0:1]))
                    for g8 in range(3):
                        nb_ = 8 if g8 < 2 else 1
                        for b in range(nb_):
                            blk = g8 * 8 + b
                            op("pe", lambda h, b=b, blk=blk: h.transpose(out=pT[:, b, :], in_=Psb[:, blk * 128:(blk + 1) * 128], identity=idb[:]))
                        op("act", lambda h, g8=g8, nb_=nb_: h.activation(out=PTss[:, g8 * 8:g8 * 8 + nb_, :], in_=pT[:, 0:nb_, :], func=AF.Copy))
                    for blk in range(17):
                        op("pe", lambda h, blk=blk, hh_=hh_: h.matmul(Ops[:, 0:64], lhsT=PTss[:, blk, :], rhs=vs[:, blk, hh_ * 64:(hh_ + 1) * 64],
                                                                     start=(blk == 0), stop=(blk == 16)))
                    op("dve", lambda h, hd=hd: h.tensor_scalar(out=attn[:, hd * 64:(hd + 1) * 64], in0=Ops[:, 0:64], scalar1=rinv[:, 0:1], scalar2=None, op0=ALU.mult))
            rms(attn[0:n, :], "attn", 512, 1, 1.0 / 512, n)
            op("dve", lambda h: h.tensor_scalar(out=anb[0:n, :], in0=attn[0:n, :], scalar1=rstd[0:n, 1:2], scalar2=None, op0=ALU.mult), ["attn", "rstd1"], ["anb"])
            for c in range(4):
                op("pe", lambda h, c=c: h.transpose(out=pT[:, c, 0:n], in_=anb[0:n, c * 128:(c + 1) * 128], identity=idb[0:n, 0:n]), ["anb", "idb"], ["pT"])
            op("dve", lambda h: h.tensor_tensor(out=mixT[:, 0:4, :], in0=pT[:, 0:4, :], in1=pcol[:, GATT:GATT + 4].unsqueeze(2).broadcast_to([128, 4, 128]), op=ALU.mult))

            def ssm_views(k):
                c4, hp = k // 2, k % 2
                q0 = c4 * 4 + 2 * hp
                bi = k % 2
                Bv = Bs2[bi][:].rearrange("p (r j k) -> p r j k", r=2, j=2)
                if sample:
                    csq = cs[:, q0:q0 + 2, 1:5].unsqueeze(2).broadcast_to([128, 2, NSEQ, 4])
                    snq = sn[:, q0:q0 + 2, 1:5].unsqueeze(2).broadcast_to([128, 2, NSEQ, 4])

                    def v3(ap):
                        return ap[:, :, 0:NS].rearrange("p j (s t) -> p j s t", t=4)
                else:
                    csq = cs[:, q0:q0 + 2, 1:129]; snq = sn[:, q0:q0 + 2, 1:129]

                    def v3(ap):
                        return ap
                T = [v3(pp2[bi][:, kk]) for kk in range(4)]
                RR = [v3(rr2[bi][:, kk]) for kk in range(2)]
                VV = [v3(vv2[bi][:, kk]) for kk in range(2)]
                return c4, hp, q0, bi, Bv, csq, snq, v3, T, RR, VV

            def ssm_front(k):
                c4, hp, q0, bi, Bv, csq, snq, v3, T, RR, VV = ssm_views(k)
                bank = mmA if bi == 0 else mmC
                for ri, WBx in enumerate((WBre, WBim)):
                    for j in range(2):
                        op("pe", lambda h: h.matmul(bank[:, (ri * 2 + j) * 128:(ri * 2 + j + 1) * 128], lhsT=WBx[:, q0 + j, :], rhs=uT[:, c4, :], start=True, stop=True))
                op("act", lambda h: h.activation(out=Bs2[bi][:], in_=bank[:, :], func=AF.Copy))
                if sample:
                    for ri in range(2):
                        v4 = Bv[:, ri, :, 0:NS].rearrange("p j (s t) -> p j s t", t=4)[:, :, :, 0]
                        op("dve", lambda h: h.tensor_tensor(out=v4, in0=v4, in1=ah[:, ri, q0:q0 + 2, :], op=ALU.add))
                Bre, Bim = v3(Bv[:, 0]), v3(Bv[:, 1])
                op("dve", lambda h: h.tensor_tensor(out=T[0], in0=Bre, in1=csq, op=ALU.mult))
                op("pool", lambda h: h.tensor_tensor(out=T[1], in0=Bim, in1=snq, op=ALU.mult))
                op("dve", lambda h: h.tensor_tensor(out=T[2], in0=Bim, in1=csq, op=ALU.mult))
                op("pool", lambda h: h.tensor_tensor(out=T[3], in0=Bre, in1=snq, op=ALU.mult))
                op("dve", lambda h: h.tensor_tensor(out=RR[0], in0=T[0], in1=T[1], op=ALU.add))
                op("pool", lambda h: h.tensor_tensor(out=RR[1], in0=T[2], in1=T[3], op=ALU.subtract))

            def ssm_back(k):
                c4, hp, q0, bi, Bv, csq, snq, v3, T, RR, VV = ssm_views(k)
                rrb, vvb = rr2[bi], vv2[bi]
                for j in range(2):
                    q = q0 + j
                    if sample:
                        op("dve", lambda h: h.tensor_scalar(out=dtmp[:], in0=mask64[:], scalar1=rho[:, q:q + 1], scalar2=None, op0=ALU.mult))
                    for ri in range(2):
                        if sample:
                            op("dve", lambda h: h.tensor_tensor_scan(out=vvb[:, ri, j, 0:NS], data0=dtmp[:], data1=rrb[:, ri, j, 0:NS], initial=0.0, op0=ALU.mult, op1=ALU.add))
                        else:
                            op("dve", lambda h: h.tensor_tensor_scan(out=vvb[:, ri, j, :], data0=rho[:, q:q + 1].broadcast_to([128, 128]), data1=rrb[:, ri, j, :],
                                                                   initial=car[:, ri, q:q + 1], op0=ALU.mult, op1=ALU.add))
                op("dve", lambda h: h.tensor_tensor(out=T[0], in0=VV[0], in1=csq, op=ALU.mult))
                op("pool", lambda h: h.tensor_tensor(out=T[1], in0=VV[1], in1=snq, op=ALU.mult))
                op("pool", lambda h: h.tensor_tensor(out=T[2], in0=VV[0], in1=snq, op=ALU.mult))
                op("dve", lambda h: h.tensor_tensor(out=T[3], in0=VV[1], in1=csq, op=ALU.mult))
                hb_re, hb_im = v3(hb[:, 2 * hp:2 * hp + 2, 0, :]), v3(hb[:, 2 * hp:2 * hp + 2, 1, :])
                op("pool", lambda h: h.tensor_tensor(out=hb_re, in0=T[0], in1=T[1], op=ALU.subtract))
                op("dve", lambda h: h.tensor_tensor(out=hb_im, in0=T[2], in1=T[3], op=ALU.add))
                if sample:
                    op("pool", lambda h: h.tensor_tensor(out=hfin[:, 0, q0:q0 + 2, 0:NSEQ], in0=T[0][:, :, :, 3], in1=T[1][:, :, :, 3], op=ALU.subtract))
                    op("dve", lambda h: h.tensor_tensor(out=hfin[:, 1, q0:q0 + 2, 0:NSEQ], in0=T[2][:, :, :, 3], in1=T[3][:, :, :, 3], op=ALU.add))
                else:
                    op("pool", lambda h: h.tensor_tensor(out=car[:, 0, q0:q0 + 2], in0=T[0][:, :, 127], in1=T[1][:, :, 127], op=ALU.subtract))
                    op("dve", lambda h: h.tensor_tensor(out=car[:, 1, q0:q0 + 2], in0=T[2][:, :, 127], in1=T[3][:, :, 127], op=ALU.add))

            def ssm_y(c4):
                for qq in range(4):
                    q = c4 * 4 + qq
                    op("pe", lambda h: h.matmul(mmB[:, 0:n], lhsT=WCre[:, q, :], rhs=hb[:, qq, 0, 0:n], start=(qq == 0), stop=False))
                    op("pe", lambda h: h.matmul(mmB[:, 0:n], lhsT=WCimn[:, q, :], rhs=hb[:, qq, 1, 0:n], start=False, stop=(qq == 3)))
                op("dve", lambda h: h.scalar_tensor_tensor(out=yv[:, 0:n], in0=uT[:, c4, 0:n], scalar=col(DSK + c4), in1=mmB[:, 0:n], op0=ALU.mult, op1=ALU.add))
                op("act", lambda h: h.activation(out=gl32[:, c4, 0:n], in_=yv[:, 0:n], func=AF.Gelu))
                op("pool", lambda h: h.tensor_copy(out=glb[:, c4, 0:n], in_=gl32[:, c4, 0:n]))

            ssm_front(0)
            for k in range(8):
                if k + 1 < 8:
                    ssm_front(k + 1)
                ssm_back(k)
                if k % 2 == 1:
                    ssm_y(k // 2)
            if ti == NPT - 1:
                op("act", lambda h: h.activation(out=hfin[:, :, :, NSEQ], in_=car[:], func=AF.Copy), ["car"], ["hfin"])
            for oc in range(4):
                for c4 in range(4):
                    op("pe", lambda h, oc=oc, c4=c4: h.matmul(mmC[:, 0:n], lhsT=wgl[par][:, c4, oc * 128:(oc + 1) * 128], rhs=glb[:, c4, 0:n], start=(c4 == 0), stop=(c4 == 3)),
                       ["glb", "wgl"], ["mmC"])
                op("act", lambda h, oc=oc: h.activation(out=sg[:, 0:n], in_=mmC[:, 0:n], func=AF.Sigmoid, bias=col(BGLU + oc)), ["mmC", "pcol"], ["sg"])
                op("dve", lambda h, oc=oc: h.tensor_tensor(out=gl32[:, oc, 0:n], in0=gl32[:, oc, 0:n], in1=sg[:, 0:n], op=ALU.mult), ["gl32", "sg", "glb"], ["gl32"])
                op("act", lambda h, oc=oc: h.activation(out=sqb[:, oc, 0:n], in_=gl32[:, oc, 0:n], func=AF.Square), ["gl32"], ["sqb"])
            for oc in range(4):
                op("pe", lambda h, oc=oc: h.matmul(mmC[:, 0:n], lhsT=onesb[:], rhs=sqb[:, oc, 0:n], start=(oc == 0), stop=(oc == 3)), ["sqb", "onesb"], ["mmC"])
            op("act", lambda h: h.activation(out=rsb[:, 0:n], in_=mmC[:, 0:n], func=AF.Ln, scale=1.0 / 512, bias=epsb[:, 0:1]))
            op("act", lambda h: h.activation(out=rsb[:, 0:n], in_=rsb[:, 0:n], func=AF.Exp, scale=-0.5))
            for oc in range(4):
                op("dve", lambda h, oc=oc: h.scalar_tensor_tensor(out=mixT[:, 4 + oc, 0:n], in0=gl32[:, oc, 0:n], scalar=col(GSSM + oc), in1=rsb[:, 0:n], op0=ALU.mult, op1=ALU.mult),
                   ["gl32", "rsb", "pcol"], ["mixT"])

            for hf in range(2):
                acc, ak = (acc0, "acc0") if hf == 0 else (acc1, "acc1")
                for c in range(8):
                    op("pe", lambda h, hf=hf, c=c, acc=acc: h.matmul(acc[0:n, :], lhsT=mixT[:, c, 0:n], rhs=wo[par][:, c, hf * 512:(hf + 1) * 512], start=(c == 0), stop=(c == 7)),
                       ["mixT", "wo"], [ak])
                op("dve", lambda h, hf=hf, acc=acc: h.tensor_tensor(out=xt[0:n, hf * 512:(hf + 1) * 512], in0=xt[0:n, hf * 512:(hf + 1) * 512], in1=acc[0:n, :], op=ALU.add),
                   ["xt", ak], ["xt"])
            rms(xt[0:n, :], "xt", D, 2, 1.0 / D, n)
            op("dve", lambda h: h.tensor_scalar(out=xn[0:n, :], in0=xt[0:n, :], scalar1=rstd[0:n, 2:3], scalar2=None, op0=ALU.mult), ["xt", "rstd2"], ["xn"])
            for c in range(8):
                op("pe", lambda h, c=c: h.transpose(out=pT[:, c, 0:n], in_=xn[0:n, c * 128:(c + 1) * 128], identity=idb[0:n, 0:n]), ["xn", "idb"], ["pT"])
            op("dve", lambda h: h.tensor_tensor(out=xn2T[:, :, tl * 128:(tl + 1) * 128], in0=pT[:], in1=pcol[:, GFFN:GFFN + 8].unsqueeze(2).broadcast_to([128, 8, 128]), op=ALU.mult))

        def ffn(tiles_):
            sample = (tiles_[0] == NPT)
            ntl = len(tiles_)
            nb = ntl * 128
            hTv = hTsm if sample else hTall
            for ch in range(NFC):
                b3 = ch % 2
                w3 = ch % 3
                P.dma("sp", lambda h: h.dma_start(out=wg[w3][:], in_=wgb_d[ch]))
                P.dma("sp", lambda h: h.dma_start(out=wu[w3][:], in_=wub_d[ch]))
                gps, ups = (mmA, mmB) if b3 == 0 else (mmC, pT32)
                for c in range(8):
                    op("pe", lambda h, c=c, b3=b3, gps=gps: h.matmul(gps[:, 0:nb], lhsT=wg[w3][:, c, :], rhs=xn2T[:, c, 0:nb], start=(c == 0), stop=(c == 7)))
                for c in range(8):
                    op("pe", lambda h, c=c, b3=b3, ups=ups: h.matmul(ups[:, 0:nb], lhsT=wu[w3][:, c, :], rhs=xn2T[:, c, 0:nb], start=(c == 0), stop=(c == 7)))
                w0, w1, w2, bb = col(CVW + ch * 3), col(CVW + ch * 3 + 1), col(CVW + ch * 3 + 2), col(CVB + ch)
                cvb, slb, gxb = cv[:, 0, :], sl[:, 0, :], gx[:, b3, :]
                if sample:
                    op("pool", lambda h, ch=ch: h.tensor_copy(out=gxs[:, :, 0:2], in_=cvst[:, ch, :, :]))
                    op("act", lambda h: h.activation(out=gxs[:, :, 2:6], in_=gps[:, 0:NS].rearrange("p (s t) -> p s t", t=4), func=AF.Copy))
                    g0, g1, g2 = gxs[:, :, 0:4], gxs[:, :, 1:5], gxs[:, :, 2:6]
                    cvv = cvb[:, 0:NS].rearrange("p (s t) -> p s t", t=4)
                    op("pool", lambda h, ch=ch: h.tensor_copy(out=sconv[:, ch, :, :], in_=gxs[:, :, 4:6]))
                else:
                    op("pool", lambda h, ch=ch, gxb=gxb: h.tensor_copy(out=gxb[:, 0:2], in_=gcar[:, ch, :]))
                    op("act", lambda h, gxb=gxb: h.activation(out=gxb[:, 2:2 + nb], in_=gps[:, 0:nb], func=AF.Copy))
                    g0, g1, g2 = gxb[:, 0:nb], gxb[:, 1:1 + nb], gxb[:, 2:2 + nb]
                    cvv = cvb[:, 0:nb]
                    op("pool", lambda h, ch=ch, gxb=gxb: h.tensor_copy(out=gcar[:, ch, :], in_=gxb[:, nb:nb + 2]))
                op("dve", lambda h, g2=g2, cvv=cvv, w2=w2, bb=bb: h.tensor_scalar(out=cvv, in0=g2, scalar1=w2, scalar2=bb, op0=ALU.mult, op1=ALU.add))
                op("dve", lambda h, g1=g1, cvv=cvv, w1=w1: h.scalar_tensor_tensor(out=cvv, in0=g1, scalar=w1, in1=cvv, op0=ALU.mult, op1=ALU.add))
                op("dve", lambda h, g0=g0, cvv=cvv, w0=w0: h.scalar_tensor_tensor(out=cvv, in0=g0, scalar=w0, in1=cvv, op0=ALU.mult, op1=ALU.add))
                op("act", lambda h, cvb=cvb, slb=slb: h.activation(out=slb[:, 0:nb], in_=cvb[:, 0:nb], func=AF.Silu))
                op("dve", lambda h, ch=ch, slb=slb: h.tensor_tensor(out=hTv[:, ch, 0:nb], in0=slb[:, 0:nb], in1=ups[:, 0:nb], op=ALU.mult))
            accs = [acc0, acc1, Sps, Ops]
            for hf in range(2):
                for g in range(NFC // 2):
                    wdb = wd[g % 3]
                    P.dma("sp", lambda h: h.dma_start(out=wdb, in_=wdb_d[hf, g]))
                    for j in range(2):
                        ch = 2 * g + j
                        for tl in range(ntl):
                            op("pe", lambda h: h.matmul(accs[tl][:, :], lhsT=hTv[:, ch, tl * 128:(tl + 1) * 128], rhs=wdb[:, j, :],
                                                        start=(ch == 0), stop=(ch == NFC - 1)))
                for tl in range(ntl):
                    op("dve", lambda h, tl=tl, hf=hf: h.tensor_tensor(out=xb[:, tl, hf * 512:(hf + 1) * 512], in0=xb[:, tl, hf * 512:(hf + 1) * 512], in1=accs[tl][:, :], op=ALU.add))
            for tl, ti in enumerate(tiles_):
                xt = xb[:, tl, :]
                rms(xt, "xt", D, 3, 1.0 / D, 128)
                op("dve", lambda h, xt=xt: h.scalar_tensor_tensor(out=xt, in0=xt, scalar=rstd[:, 3:4], in1=gfb[:], op0=ALU.mult, op1=ALU.mult))
                P.dma("sp", lambda h, xt=xt, ti=ti: h.dma_start(out=y_d[ti * 128:(ti + 1) * 128, :], in_=xt))

        if tiles is None:
            blocks = [[0, 1, 2, 3], [4, 5, 6, 7], [8, 9, 10, 11], [12, 13, 14, 15], [16], [NPT]]
        else:
            blocks = tiles
        for blk_ in blocks:
            for tl, ti in enumerate(blk_):
                P.dma("sp", lambda h: h.dma_start(out=xb[:, tl, :], in_=xin[ti * 128:(ti + 1) * 128, :]))
            for tl, ti in enumerate(blk_):
                mixer(ti, tl)
            ffn(blk_)

        P.dma("sp", lambda h: h.dma_start(out=hfin_d, in_=hfin[:]), reads=["hfin"], writes=["hfin_d"])
        P.dma("sp", lambda h: h.dma_start(out=pconv_d, in_=gcar[:]), reads=["gcar"], writes=["pconv_d"])
        P.dma("sp", lambda h: h.dma_start(out=sconv_d, in_=sconv[:]), reads=["sconv"], writes=["sconv_d"])
        P.limit = None
        P.barrier()

        with nc.Block() as block:
            @block.tensor
            def _(h):
                P.run("pe", h, sems)

            @block.scalar
            def _(h):
                P.run("act", h, sems)

            @block.vector
            def _(h):
                P.run("dve", h, sems)

            @block.gpsimd
            def _(h):
                P.run("pool", h, sems)

            @block.sync
            def _(h):
                P.run("sp", h, sems)
    return nc


def _consts():
    ident = np.eye(128, dtype=np.float32)
    i = np.arange(128)[:, None]
    c = np.arange(256)[None, :]
    full = np.where(((c < 128) & (c > i)) | ((c >= 128) & (c - 128 <= i)), 0.0, MASKV)
    m1 = np.where(((c < 128) & (c > i) & (c >= NPAD)) | ((c >= 128) & (c - 128 <= i)), 0.0, MASKV)
    m0 = np.where((c >= 128) & (c - 128 <= i) & (c - 128 >= NPAD), 0.0, MASKV)
    masks = np.stack([m0, m1, full], axis=1).astype(np.float32)
    sm = np.full((128, 17 * 128), MASKV, np.float32)
    for s in range(NSEQ):
        for t in range(4):
            r = s * 4 + t
            sm[r, s * 128 + t + 1:(s + 1) * 128] = 0.0
            sm[r, 2048 + s * 4:2048 + s * 4 + t + 1] = 0.0
    return ident, masks, sm


def prep_inputs(x_prompt, x_sample, cache_k_win, cache_v_win, state_ssm_re, state_ssm_im, state_conv,
           meta_tokens, g_mix, w_in, sinks, lam_re, lam_im, log_dt, b_re, b_im, c_re, c_im, d_skip,
           w_glu, b_glu, g_attn_out, g_ssm_out, w_o, g_ffn, w_gate, w_up, conv_w, conv_b, w_down,
           g_final):
    f32 = np.float32
    A = lambda a: np.ascontiguousarray(np.asarray(a, dtype=f32))
    x_prompt, x_sample = A(x_prompt), A(x_sample)
    ident, masks, smask = _consts()
    w_in0 = A(w_in)[0]
    perm = []
    for i in range(4):
        perm += list(range(i * 64, (i + 1) * 64)) + list(range((4 + i) * 64, (5 + i) * 64))
    w_in_p = np.ascontiguousarray(np.concatenate([w_in0[:, perm], w_in0[:, 512:]], axis=1))
    pcol = np.zeros((128, 128), f32)
    pcol[:, 0:8] = A(g_mix)[0].reshape(8, 128).T
    pcol[:, 8:16] = A(g_ffn)[0].reshape(8, 128).T
    pcol[:, 16:20] = A(g_attn_out)[0].reshape(4, 128).T
    pcol[:, 20:24] = A(g_ssm_out)[0].reshape(4, 128).T
    pcol[:, 24:28] = A(b_glu)[0].reshape(4, 128).T
    pcol[:, 28:32] = A(d_skip)[0].reshape(4, 128).T
    cw = A(conv_w)[0].reshape(3, NFC, 128)
    pcol[:, 32:98] = cw.transpose(2, 1, 0).reshape(128, 66)
    pcol[:, 98:120] = A(conv_b)[0].reshape(NFC, 128).T
    lr, li, ld = A(lam_re)[0], A(lam_im)[0], A(log_dt)[0]
    ldx = np.repeat(ld[:, None], 64, axis=1)

    def pl(a):
        return a.reshape(16, 2, 64).transpose(1, 2, 0).reshape(128, 16)

    lam = np.ascontiguousarray(np.stack([pl(lr), pl(li), pl(ldx)], axis=1))
    lamb = np.ascontiguousarray(np.stack([lr.reshape(-1), li.reshape(-1), ldx.reshape(-1)], axis=0))
    bre, bim, cre, cim = A(b_re)[0], A(b_im)[0], A(c_re)[0], A(c_im)[0]
    bblk = np.zeros((128, 2, 16, 128), f32)
    cblk = np.zeros((128, 2, 16, 128), f32)
    for q in range(16):
        for j2 in range(2):
            g = 2 * q + j2
            g8 = g % 8
            rows = slice(g8 * 16, g8 * 16 + 16)
            cols = slice(j2 * 64, j2 * 64 + 64)
            bblk[rows, 0, q, cols] = bre[g].T
            bblk[rows, 1, q, cols] = bim[g].T
            cblk[cols, 0, q, rows] = cre[g].T
            cblk[cols, 1, q, rows] = cim[g].T
    sre, sim_ = A(state_ssm_re)[0], A(state_ssm_im)[0]
    ck, cvv = A(cache_k_win)[0].reshape(128, 128, 128), A(cache_v_win)[0].reshape(128, 128, 128)
    sc = A(state_conv)[0]
    meta = A(meta_tokens)
    wg_l = np.ascontiguousarray(A(w_gate)[0].reshape(8, 128, NFC, 128).transpose(2, 1, 0, 3))
    wu_l = np.ascontiguousarray(A(w_up)[0].reshape(8, 128, NFC, 128).transpose(2, 1, 0, 3))
    wd_l = np.ascontiguousarray(A(w_down)[0].reshape(NFC // 2, 2, 128, 2, 512).transpose(3, 0, 2, 1, 4))
    in_maps = []
    for c in range(NCORES):
        xin = np.zeros((NPT * 128 + 128, D), f32)
        xin[NPAD:128] = meta
        xin[128:NPT * 128] = x_prompt[c]
        xin[NPT * 128:NPT * 128 + NS] = x_sample[c * NSEQ:(c + 1) * NSEQ].reshape(NS, D)
        sl_ = slice(c * NSEQ, (c + 1) * NSEQ)

        def hl(a):
            return a.reshape(NSEQ, 16, 2, 64).transpose(2, 3, 1, 0).reshape(128, 16, NSEQ)

        h0 = np.ascontiguousarray(np.stack([hl(sre[sl_]), hl(sim_[sl_])], axis=1))
        cvst = np.ascontiguousarray(sc[sl_].reshape(NSEQ, 2, NFC, 128).transpose(3, 2, 0, 1))
        kc, vc = ck[sl_], cvv[sl_]
        kcT = np.ascontiguousarray(kc.transpose(2, 0, 1).reshape(128, NSEQ * 128))
        vcl = np.ascontiguousarray(vc.transpose(1, 0, 2))
        in_maps.append(dict(
            xin=xin, w_in=w_in_p, w_o=A(w_o)[0], w_glu=A(w_glu)[0], w_gate=wg_l, w_up=wu_l,
            w_down=wd_l, ident=ident, masks=masks, smask=smask, pcol=pcol, sinks=A(sinks)[0],
            gfin=A(g_final), lam=lam, lamb=lamb, bblk=bblk, cblk=cblk, h0=h0, cvst=cvst, kcT=kcT, vc=vcl,
            kcache=np.ascontiguousarray(kc), vcache=np.ascontiguousarray(vc)))
    return in_maps


def assemble(R):
    f32 = np.float32
    y_prompt = np.stack([R[c]["y"][128:NPT * 128] for c in range(NCORES)])
    y_sample = np.concatenate([R[c]["y"][NPT * 128:NPT * 128 + NS].reshape(NSEQ, 4, D) for c in range(NCORES)])
    p_k = np.stack([R[c]["kvp"][:, 0:128].reshape(128, 2, 64) for c in range(NCORES)])[None]
    p_v = np.stack([R[c]["kvp"][:, 128:256].reshape(128, 2, 64) for c in range(NCORES)])[None]
    s_k = np.concatenate([R[c]["kvs_k"].reshape(NSEQ, 128, 2, 64) for c in range(NCORES)])[None]
    s_v = np.concatenate([R[c]["kvs_v"].reshape(NSEQ, 128, 2, 64) for c in range(NCORES)])[None]

    def unh(a):
        nn = a.shape[-1]
        return a.reshape(2, 64, 16, nn).transpose(3, 2, 0, 1).reshape(nn, 32, 64)

    p_re = np.stack([unh(R[c]["hfin"][:, 0, :, NSEQ:])[0] for c in range(NCORES)])[None]
    p_im = np.stack([unh(R[c]["hfin"][:, 1, :, NSEQ:])[0] for c in range(NCORES)])[None]
    s_re = np.concatenate([unh(R[c]["hfin"][:, 0, :, :NSEQ]) for c in range(NCORES)])[None]
    s_im = np.concatenate([unh(R[c]["hfin"][:, 1, :, :NSEQ]) for c in range(NCORES)])[None]
    p_conv = np.stack([R[c]["pconv"].transpose(2, 1, 0).reshape(2, FF) for c in range(NCORES)])[None]
    s_conv = np.concatenate([R[c]["sconv"].transpose(2, 3, 1, 0).reshape(NSEQ, 2, FF) for c in range(NCORES)])[None]
    out = (y_prompt, y_sample, p_k, p_v, p_re, p_im, p_conv, s_k, s_v, s_re, s_im, s_conv)
    return tuple(np.ascontiguousarray(o, dtype=f32) for o in out)


def kernel(**inputs):
    in_maps = prep_inputs(**inputs)
    nc = build_nc()
    res = run_bass_kernel_spmd(nc, in_maps, core_ids=list(range(NCORES)))
    return assemble(res.results)
```

```python
import numpy as np
from contextlib import ExitStack
import concourse.bass as bass
import concourse.mybir as mybir
from concourse.bass_utils import run_bass_kernel_spmd

F32 = mybir.dt.float32
BF16 = mybir.dt.bfloat16
ALU = mybir.AluOpType
AF = mybir.ActivationFunctionType
AX = mybir.AxisListType

ENGS = ("pe", "act", "dve", "pool", "sp")
NDMASEM = 12
NCORES = 8
D = 1024
NPT = 17
NPAD = 112
NS = 64
NSEQ = 16
FF = 2816
NFC = 22
EPS = 1e-5
PI = float(np.pi)
MASKV = -30000.0


class _Recorder:
    def __init__(self):
        self.call = None

    def __getattr__(self, name):
        def f(*args, **kwargs):
            self.call = (name, args, kwargs)
            return self
        return f


class Prog:
    def __init__(self):
        self.streams = {e: [] for e in ENGS}
        self.count = {e: 0 for e in ENGS}
        self.waited = {e: {} for e in ENGS}
        self.regs = {}
        self.nrec = 0
        self.limit = None
        self.dma_rr = {"sp": 0, "pool": 0}
        self.dma_cnt = {}

    @staticmethod
    def _region(ap):
        dims = [(int(st), int(sz)) for st, sz in ap.ap]
        off = int(ap.offset)
        esz = mybir.dt.size(ap.dtype)
        space = str(ap.space)
        if space == "DRAM":
            ext = sum((sz - 1) * abs(st) for st, sz in dims)
            return (0, 1, off, off + ext)
        pst, npart = dims[0]
        pst = max(pst, 1)
        p0, f0 = off // pst, off % pst
        ext = sum((sz - 1) * abs(st) for st, sz in dims[1:])
        f0, ext = f0 * esz, ext * esz + esz - 1
        if space == "PSUM":
            return (0, 128, 0, 1 << 30)
        return (p0, p0 + npart, f0, f0 + ext)

    @staticmethod
    def _is_ap(v):
        return hasattr(v, "tensor") and hasattr(v, "ap") and hasattr(v, "offset")

    def _record(self, fn):
        rec = _Recorder()
        fn(rec)
        name, args, kwargs = rec.call
        acc = []
        for i, a in enumerate(args):
            if self._is_ap(a):
                acc.append((a, i == 0))
        for k, v in kwargs.items():
            if self._is_ap(v):
                acc.append((v, k in ("out", "accum_out")))
        out = []
        for ap, w in acc:
            if str(ap.space) == "PSUM":
                w = True
            out.append((ap.tensor.name, self._region(ap), w))
        self._last_call = rec.call
        return out

    def _deps(self, eng, acc):
        deps = {}
        for name, R, w in acc:
            for (R2, w2), evs in self.regs.get(name, {}).items():
                if not (w or w2):
                    continue
                if R[0] < R2[1] and R2[0] < R[1] and R[2] <= R2[3] and R2[2] <= R[3]:
                    for s_, v in evs.items():
                        if s_ == eng and eng == "pe":
                            continue
                        if deps.get(s_, 0) < v:
                            deps[s_] = v
        out = []
        wd = self.waited[eng]
        for s_, v in deps.items():
            if wd.get(s_, 0) < v:
                wd[s_] = v
                out.append((s_, v))
        return out

    def _commit(self, ev, acc):
        s_, v = ev
        for name, R, w in acc:
            d = self.regs.setdefault(name, {})
            if w:
                for key in [k for k in d if k[0][0] >= R[0] and k[0][1] <= R[1] and k[0][2] >= R[2] and k[0][3] <= R[3]]:
                    del d[key]
            e = d.setdefault((R, w), {})
            if e.get(s_, 0) < v:
                e[s_] = v

    def op(self, eng, fn, reads=(), writes=()):
        self.nrec += 1
        if self.limit is not None and self.nrec > self.limit:
            return
        acc = self._record(fn)
        deps = self._deps(eng, acc)
        self.count[eng] += 1
        ev = (eng, self.count[eng])
        self.streams[eng].append((deps, self._last_call, (eng, 1)))
        self._commit(ev, acc)

    def dma(self, eng, fn, reads=(), writes=()):
        self.nrec += 1
        if self.limit is not None and self.nrec > self.limit:
            return
        acc = self._record(fn)
        k = self.dma_rr[eng]
        self.dma_rr[eng] = (k + 1) % NDMASEM
        sname = "dma%s%d" % (eng, k)
        deps = self._deps(eng, acc)
        prev = self.dma_cnt.get(sname, 0) * 16
        if prev and self.waited[eng].get(sname, 0) < prev:
            self.waited[eng][sname] = prev
            deps.append((sname, prev))
        self.dma_cnt[sname] = self.dma_cnt.get(sname, 0) + 1
        ev = (sname, self.dma_cnt[sname] * 16)
        self.streams[eng].append((deps, self._last_call, (sname, 16)))
        self._commit(ev, acc)

    def barrier(self):
        evs = [(e, self.count[e]) for e in ENGS if self.count[e]]
        evs += [(k, c * 16) for k, c in self.dma_cnt.items()]
        for e in ENGS:
            deps = []
            for s, v in evs:
                if s == e:
                    continue
                if self.waited[e].get(s, 0) < v:
                    self.waited[e][s] = v
                    deps.append((s, v))
            if deps:
                self.streams[e].append((deps, None, None))

    def run(self, eng, h, sems):
        for deps, fn, inc in self.streams[eng]:
            for s, v in deps:
                h.wait_ge(sems[s], v)
            if fn is not None:
                name, args, kwargs = fn
                getattr(h, name)(*args, **kwargs).then_inc(sems[inc[0]], inc[1])


def build_nc(tiles=None, limit=None):
    nc = bass.Bass("TRN2", target_bir_lowering=False)

    def din(name, shape, dt=F32):
        return nc.dram_tensor(name, list(shape), dt, kind="ExternalInput").ap()

    def dout(name, shape, dt=F32):
        return nc.dram_tensor(name, list(shape), dt, kind="ExternalOutput").ap()

    xin = din("xin", [NPT * 128 + 128, D])
    w_in = din("w_in", [D, 1280])
    w_o = din("w_o", [D, D])
    w_glu = din("w_glu", [512, 512])
    w_gate = din("w_gate", [NFC, 128, 8, 128])
    w_up = din("w_up", [NFC, 128, 8, 128])
    w_down = din("w_down", [2, NFC // 2, 128, 2, 512])
    ident_d = din("ident", [128, 128])
    masks_d = din("masks", [128, 3, 256])
    smask_d = din("smask", [128, 17 * 128])
    pcol_d = din("pcol", [128, 128])
    sinks_d = din("sinks", [8])
    gfin_d = din("gfin", [D])
    lam_d = din("lam", [128, 3, 16])
    lamb_d = din("lamb", [3, 16 * 128])
    bblk_d = din("bblk", [128, 2, 16, 128])
    cblk_d = din("cblk", [128, 2, 16, 128])
    h0_d = din("h0", [128, 2, 16, NSEQ])
    cvst_d = din("cvst", [128, NFC, NSEQ, 2])
    kcT_d = din("kcT", [128, NSEQ * 128])
    vc_d = din("vc", [128, NSEQ, 128])
    kcache_d = din("kcache", [NSEQ, 128, 128])
    vcache_d = din("vcache", [NSEQ, 128, 128])

    wgb_d = nc.dram_tensor("wgb", [NFC, 128, 8, 128], BF16, kind="Internal").ap()
    wub_d = nc.dram_tensor("wub", [NFC, 128, 8, 128], BF16, kind="Internal").ap()
    wdb_d = nc.dram_tensor("wdb", [2, NFC // 2, 128, 2, 512], BF16, kind="Internal").ap()
    y_d = dout("y", [NPT * 128 + 128, D])
    kvp_d = dout("kvp", [128, 256])
    kvs_k = dout("kvs_k", [NSEQ, 128, 128])
    kvs_v = dout("kvs_v", [NSEQ, 128, 128])
    hfin_d = dout("hfin", [128, 2, 16, NSEQ + 1])
    pconv_d = dout("pconv", [128, NFC, 2])
    sconv_d = dout("sconv", [128, NFC, NSEQ, 2])

    P = Prog()
    P.limit = limit
    es = ExitStack()
    with es:
        def sb(name, shape, dt=F32):
            return es.enter_context(nc.sbuf_tensor(name, list(shape), dt))

        def ps(name, shape, dt=F32):
            return es.enter_context(nc.psum_tensor(name, list(shape), dt))

        sems = {e: es.enter_context(nc.semaphore("s_" + e)) for e in ENGS}
        for k in range(NDMASEM):
            for e_ in ("sp", "pool"):
                sems["dma%s%d" % (e_, k)] = es.enter_context(nc.semaphore("s_dma%s%d" % (e_, k)))

        idb = sb("idb", [128, 128], BF16)
        onesb = sb("onesb", [128, 128], BF16)
        masks = sb("masks_s", [128, 3, 256])
        smask = sb("smask_s", [128, 17 * 128], BF16)
        pcol = sb("pcol_s", [128, 128])
        sink8 = sb("sink8", [128, 8])
        gfb = sb("gfb", [128, D])
        epsb = sb("epsb", [128, 1])
        lam = sb("lam_s", [128, 3, 16])
        WBre = sb("WBre", [128, 16, 128], BF16); WBim = sb("WBim", [128, 16, 128], BF16)
        WCre = sb("WCre", [128, 16, 128], BF16); WCimn = sb("WCimn", [128, 16, 128], BF16)
        cs = sb("cs", [128, 16, 129]); sn = sb("sn", [128, 16, 129])
        mask64 = sb("mask64", [128, NS]); dtmp = sb("dtmp", [128, NS])
        are = sb("are", [128, 16]); aim = sb("aim", [128, 16])
        ah = sb("ah", [128, 2, 16, NSEQ])
        car = sb("car", [128, 2, 16])
        hfin = sb("hfin_s", [128, 2, 16, NSEQ + 1])
        gcar = sb("gcar", [128, NFC, 2])
        sconv = sb("sconv_s", [128, NFC, NSEQ, 2])
        kTx = sb("kTx", [128, 2, 256], BF16)
        vx = sb("vx", [128, 2, 128], BF16)
        GMIX, GFFN, GATT, GSSM, BGLU, DSK, CVW, CVB = 0, 8, 16, 20, 24, 28, 32, 98

        def col(c):
            return pcol[:, c:c + 1]

        ssq = sb("ssq", [128, 4]); rstd = sb("rstd", [128, 4])
        xn = sb("xn", [128, D], BF16)
        junk = xn
        xnT = sb("xnT", [128, 8, 128], BF16)
        qT = sb("qT", [128, 4, 128], BF16); uT = sb("uT", [128, 4, 128], BF16)
        Sx = sb("Sx", [128, 1, 2, 257]); mx = sb("mx", [128, 2]); nbias = sb("nbias", [128, 2])
        rs = sb("rs", [128, 2]); rinv = sb("rinv", [128, 2])
        Pb = sb("Pb", [128, 2, 257], BF16); PTs = sb("PTs", [128, 4, 128], BF16)
        anb = sb("anb", [128, 512], BF16)
        mixT = sb("mixT", [128, 8, 128], BF16)
        Psb = sb("Psb", [128, 17 * 128 + 1], BF16)
        PTss = sb("PTss", [128, 17, 128], BF16)
        Bs2 = [sb("Bs%d" % i, [128, 512]) for i in range(2)]
        pp2 = [sb("pp%d" % i, [128, 4, 2, 128]) for i in range(2)]
        rr2 = [sb("rr%d" % i, [128, 2, 2, 128]) for i in range(2)]
        vv2 = [sb("vv%d" % i, [128, 2, 2, 128]) for i in range(2)]
        hb = sb("hb", [128, 4, 2, 128], BF16)
        yv = sb("yv", [128, 128]); gl32 = sb("gl32", [128, 4, 128]); glb = sb("glb", [128, 4, 128], BF16)
        sg = sb("sg", [128, 128]); sqb = sb("sqb", [128, 4, 128], BF16); rsb = sb("rsb", [128, 128])
        xn2T = sb("xn2T", [128, 8, 512], BF16)
        gx = sb("gx", [128, 2, 514]); gxs = sb("gxs", [128, NSEQ, 6])
        cv = sb("cv", [128, 1, 512]); sl = sb("sl", [128, 1, 512])
        kvtok = sl[:, 0, 0:256]
        attn = cv[:, 0, :]
        wi = [sb("wi0", [128, 8, 1280], BF16)] * 2
        wo = [sb("wo0", [128, 8, D], BF16)] * 2
        wgl = [sb("wgl0", [128, 4, 512], BF16)] * 2
        wg = [sb("wg%d" % i, [128, 8, 128], BF16) for i in range(2)] + [xnT]
        wu = [sb("wu%d" % i, [128, 8, 128], BF16) for i in range(2)] + [mixT]
        wdA = sb("wdA", [128, 2, 512], BF16)
        wd = [wdA[:], Sx[:].rearrange("p a b c -> p (a b c)").bitcast(BF16)[:, 0:1024].rearrange("p (j n) -> p j n", j=2),
              xn[:].rearrange("p (j n) -> p j n", j=2)]
        big = sb("big", [128, 9728])
        xb = big[:, 0:4096].rearrange("p (t d) -> p t d", t=4)
        hTall = big[:, 4096:9728].bitcast(BF16).rearrange("p (c n) -> p c n", c=NFC)
        hTsm = big[:, 4096:5504].bitcast(BF16).rearrange("p (c n) -> p c n", c=NFC)
        kTs = big[:, 5504:7680].bitcast(BF16).rearrange("p (a k) -> p a k", a=2)
        vs = big[:, 7680:8768].bitcast(BF16).rearrange("p (b k) -> p b k", b=17)
        scr = big
        h0 = big[:, 7680:8192].rearrange("p (a q s) -> p a q s", a=2, q=16)
        cvst = big[:, 8768:9472].rearrange("p (c s j) -> p c s j", c=NFC, s=NSEQ)
        def prep_views(o):
            return (scr[:, o:o + 768].rearrange("p (a b) -> p a b", a=3), scr[:, o + 768:o + 2304].rearrange("p (a b) -> p a b", a=6),
                    scr[:, o + 2304:o + 2816].rearrange("p (a q m) -> p a q m", a=2, q=2), scr[:, o + 2816:o + 3328].rearrange("p (a q m) -> p a q m", a=2, q=2))
        Ssx = big[:, 1024:1024 + 17 * 128 + 1]
        idf = big[:, 8960:9088]

        mmA = ps("mmA", [128, 512]); mmB = ps("mmB", [128, 512]); mmC = ps("mmC", [128, 512])
        pT = ps("pT", [128, 8, 128], BF16)
        pT32 = pT[:].rearrange("p c k -> p (c k)").bitcast(F32)
        Sps = ps("Sps", [128, 512]); Ops = ps("Ops", [128, 512])
        acc0 = ps("acc0", [128, 512]); acc1 = ps("acc1", [128, 512])

        op = P.op

        P.dma("sp", lambda h: h.dma_start(out=idf[:], in_=ident_d), writes=["idf"])
        P.dma("sp", lambda h: h.dma_start(out=masks[:], in_=masks_d), writes=["masks"])
        P.dma("pool", lambda h: h.dma_start(out=smask[:], in_=smask_d), writes=["smask"])
        P.dma("sp", lambda h: h.dma_start(out=pcol[:], in_=pcol_d), writes=["pcol"])
        P.dma("sp", lambda h: h.dma_start(out=sink8[:], in_=sinks_d.partition_broadcast(128)), writes=["sink8"])
        P.dma("sp", lambda h: h.dma_start(out=gfb[:], in_=gfin_d.partition_broadcast(128)), writes=["gfb"])
        P.dma("sp", lambda h: h.dma_start(out=lam[:], in_=lam_d), writes=["lam"])
        P.dma("sp", lambda h: h.dma_start(out=h0, in_=h0_d))
        op("pool", lambda h: h.memset(hb[:], 0.0))
        op("pool", lambda h: h.memset(cv[:], 0.0))
        op("pool", lambda h: h.memset(gx[:], 0.0))
        P.dma("pool", lambda h: h.dma_start(out=wi[0][:], in_=w_in.rearrange("(c p) n -> p c n", p=128)))
        P.dma("pool", lambda h: h.dma_start(out=wgl[0][:], in_=w_glu.rearrange("(c p) n -> p c n", p=128)))
        P.dma("pool", lambda h: h.dma_start(out=wo[0][:], in_=w_o.rearrange("(c p) n -> p c n", p=128)))
        for a_ in range(0, NFC, 11):
            P.dma("pool", lambda h: h.dma_start(out=wgb_d[a_:a_ + 11], in_=w_gate[a_:a_ + 11]))
            P.dma("pool", lambda h: h.dma_start(out=wub_d[a_:a_ + 11], in_=w_up[a_:a_ + 11]))
        for hf_ in range(2):
            P.dma("pool", lambda h: h.dma_start(out=wdb_d[hf_], in_=w_down[hf_]))
        P.dma("sp", lambda h: h.dma_start(out=kvs_k[:, 0:124, :], in_=kcache_d[:, 4:128, :]), writes=["kvs_k_a"])
        P.dma("sp", lambda h: h.dma_start(out=kvs_v[:, 0:124, :], in_=vcache_d[:, 4:128, :]), writes=["kvs_v_a"])

        op("dve", lambda h: h.tensor_copy(out=idb[:], in_=idf[:]), ["idf"], ["idb"])
        op("pool", lambda h: h.memset(onesb[:], 1.0), [], ["onesb"])
        op("pool", lambda h: h.memset(epsb[:], EPS), [], ["epsb"])
        op("pool", lambda h: h.memset(kTx[:], 0.0), [], ["kTx"])
        op("pool", lambda h: h.memset(vx[:], 0.0), [], ["vx"])
        op("pool", lambda h: h.memset(car[:], 0.0), [], ["car"])
        op("pool", lambda h: h.memset(gcar[:], 0.0), [], ["gcar"])
        pass
        op("pool", lambda h: h.memset(hfin[:], 0.0))
        op("pool", lambda h: h.memset(sconv[:], 0.0))
        rho = sb("rho", [128, 16])
        dtl, thl, c1, s1, tmpa, kfL = (big[:, 9088 + 16 * i:9104 + 16 * i] for i in range(6))
        kiL = big[:, 9184:9200].bitcast(mybir.dt.int32)
        kiR = big[:, 8192:8448].bitcast(mybir.dt.int32); kfR = big[:, 8448:8704]
        op("act", lambda h: h.activation(out=dtl[:], in_=lam[:, 2, :], func=AF.Exp), ["lam"], ["dtl"])
        op("dve", lambda h: h.tensor_tensor(out=thl[:], in0=lam[:, 1, :], in1=dtl[:], op=ALU.mult), ["lam", "dtl"], ["thl"])
        op("dve", lambda h: h.tensor_tensor(out=tmpa[:], in0=lam[:, 0, :], in1=dtl[:], op=ALU.mult), ["lam", "dtl"], ["tmpa"])
        op("act", lambda h: h.activation(out=rho[:], in_=tmpa[:], func=AF.Exp), ["tmpa"], ["rho"])

        def sincos(eng_tag, th_ap, s_ap, c_ap, tmp_ap, shape_key, ki_ap, kf_ap):
            K = shape_key

            def reduce_(shift, dst_key):
                op("dve", lambda h: h.tensor_scalar(out=tmp_ap, in0=th_ap, scalar1=shift, scalar2=1.0 / (2 * PI), op0=ALU.add, op1=ALU.mult),
                   [K + "th", K + "s", K + "c"], [K + "tmp"])
                op("dve", lambda h: h.tensor_copy(out=ki_ap, in_=tmp_ap), [K + "tmp"], [K + "ki"])
                op("dve", lambda h: h.tensor_copy(out=kf_ap, in_=ki_ap), [K + "ki"], [K + "kf"])
                op("dve", lambda h: h.tensor_scalar(out=tmp_ap, in0=th_ap, scalar1=shift, scalar2=None, op0=ALU.add), [K + "th", K + "ki"], [K + "tmp"])
                op("dve", lambda h: h.scalar_tensor_tensor(out=tmp_ap, in0=kf_ap, scalar=-2 * PI, in1=tmp_ap, op0=ALU.mult, op1=ALU.add),
                   [K + "kf", K + "tmp"], [K + "tmp"])
                op("dve", lambda h: h.tensor_scalar(out=kf_ap, in0=tmp_ap, scalar1=PI, scalar2=None, op0=ALU.is_gt), [K + "tmp"], [K + "kf"])
                op("dve", lambda h: h.scalar_tensor_tensor(out=tmp_ap, in0=kf_ap, scalar=-2 * PI, in1=tmp_ap, op0=ALU.mult, op1=ALU.add),
                   [K + "kf", K + "tmp"], [K + "tmp"])
                op("dve", lambda h: h.tensor_scalar(out=kf_ap, in0=tmp_ap, scalar1=-PI, scalar2=None, op0=ALU.is_lt), [K + "tmp"], [K + "kf"])
                op("dve", lambda h: h.scalar_tensor_tensor(out=tmp_ap, in0=kf_ap, scalar=2 * PI, in1=tmp_ap, op0=ALU.mult, op1=ALU.add),
                   [K + "kf", K + "tmp"], [K + "tmp"])
                op("dve", lambda h: h.tensor_scalar(out=tmp_ap, in0=tmp_ap, scalar1=-PI, scalar2=PI, op0=ALU.max, op1=ALU.min), [K + "tmp"], [K + "tmp"])

            reduce_(0.0, "s")
            op("act", lambda h: h.activation(out=s_ap, in_=tmp_ap, func=AF.Sin), [K + "tmp"], [K + "s"])
            reduce_(0.5 * PI, "c")
            op("act", lambda h: h.activation(out=c_ap, in_=tmp_ap, func=AF.Sin), [K + "tmp"], [K + "c"])

        sincos("L", thl[:], s1[:], c1[:], tmpa[:], "L", kiL[:], kfL[:])
        op("dve", lambda h: h.tensor_tensor(out=are[:], in0=rho[:], in1=c1[:], op=ALU.mult), ["rho", "Lc"], ["are"])
        op("dve", lambda h: h.tensor_tensor(out=aim[:], in0=rho[:], in1=s1[:], op=ALU.mult), ["rho", "Ls"], ["aim"])
        op("pool", lambda h: h.memset(cs[:, :, 0:1], 1.0), [], ["cs"])
        op("pool", lambda h: h.memset(sn[:, :, 0:1], 0.0), [], ["sn"])
        op("dve", lambda h: h.tensor_copy(out=cs[:, :, 1], in_=c1[:]), ["Lc", "cs"], ["cs"])
        op("dve", lambda h: h.tensor_copy(out=sn[:, :, 1], in_=s1[:]), ["Ls", "sn"], ["sn"])
        tA = pp2[0][:].rearrange("p a j (b c) -> p (a j b) c", c=64); tB = big[:, 6656:7680].rearrange("p (a c) -> p a c", c=64)
        m = 1
        while m < 128:
            cm = cs[:, :, m:m + 1].broadcast_to([128, 16, m]); sm = sn[:, :, m:m + 1].broadcast_to([128, 16, m])
            a_c = cs[:, :, 1:m + 1]; a_s = sn[:, :, 1:m + 1]
            o_c = cs[:, :, m + 1:2 * m + 1]; o_s = sn[:, :, m + 1:2 * m + 1]
            ta = tA[:, :, 0:m]; tb = tB[:, :, 0:m]
            op("dve", lambda h, a_c=a_c, cm=cm, ta=ta: h.tensor_tensor(out=ta, in0=a_c, in1=cm, op=ALU.mult), ["cs", "sn"], ["tA"])
            op("dve", lambda h, a_s=a_s, sm=sm, tb=tb: h.tensor_tensor(out=tb, in0=a_s, in1=sm, op=ALU.mult), ["cs", "sn"], ["tB"])
            op("dve", lambda h, o_c=o_c, ta=ta, tb=tb: h.tensor_tensor(out=o_c, in0=ta, in1=tb, op=ALU.subtract), ["tA", "tB", "sn"], ["cs"])
            op("dve", lambda h, a_c=a_c, sm=sm, ta=ta: h.tensor_tensor(out=ta, in0=a_c, in1=sm, op=ALU.mult), ["cs", "sn"], ["tA"])
            op("dve", lambda h, a_s=a_s, cm=cm, tb=tb: h.tensor_tensor(out=tb, in0=a_s, in1=cm, op=ALU.mult), ["cs", "sn"], ["tB"])
            op("dve", lambda h, o_s=o_s, ta=ta, tb=tb: h.tensor_tensor(out=o_s, in0=ta, in1=tb, op=ALU.add), ["tA", "tB", "cs"], ["sn"])
            m *= 2
        op("pool", lambda h: h.memset(mask64[:], 1.0))
        op("pool", lambda h: h.memset(mask64[:].rearrange("p (s t) -> p s t", t=4)[:, :, 0:1], 0.0))
        a_re_b = are[:].unsqueeze(2).broadcast_to([128, 16, NSEQ]); a_im_b = aim[:].unsqueeze(2).broadcast_to([128, 16, NSEQ])
        tC = big[:, 8704:8960].rearrange("p (q s) -> p q s", q=16)
        op("dve", lambda h: h.tensor_tensor(out=ah[:, 0], in0=h0[:, 0], in1=a_re_b, op=ALU.mult), ["h0", "are"], ["ah0"])
        op("dve", lambda h: h.tensor_tensor(out=tC[:], in0=h0[:, 1], in1=a_im_b, op=ALU.mult), ["h0", "aim"], ["tC"])
        op("dve", lambda h: h.tensor_tensor(out=ah[:, 0], in0=ah[:, 0], in1=tC[:], op=ALU.subtract), ["ah0", "tC"], ["ah0"])
        op("dve", lambda h: h.tensor_tensor(out=ah[:, 1], in0=h0[:, 0], in1=a_im_b, op=ALU.mult), ["h0", "aim"], ["ah1"])
        op("dve", lambda h: h.tensor_tensor(out=tC[:], in0=h0[:, 1], in1=a_re_b, op=ALU.mult), ["h0", "are", "ah0"], ["tC"])
        op("dve", lambda h: h.tensor_tensor(out=ah[:, 1], in0=ah[:, 1], in1=tC[:], op=ALU.add), ["ah1", "tC"], ["ah1"])

        for pc in range(8):
            lb, ft, bblk, cblk = prep_views((pc % 2) * 3328)
            for i in range(3):
                P.dma("sp", lambda h, i=i, pc=pc: h.dma_start(out=lb[:, i, :], in_=lamb_d[i, pc * 256:(pc + 1) * 256].partition_broadcast(128)), writes=["lb"])
            P.dma("sp", lambda h, pc=pc: h.dma_start(out=bblk[:], in_=bblk_d[:, :, pc * 2:(pc + 1) * 2, :]), writes=["bblk"])
            P.dma("sp", lambda h, pc=pc: h.dma_start(out=cblk[:], in_=cblk_d[:, :, pc * 2:(pc + 1) * 2, :]), writes=["cblk"])
            LR, LI, LD = lb[:, 0, :], lb[:, 1, :], lb[:, 2, :]
            f0, f1, f2, f3, f4, f5 = (ft[:, i, :] for i in range(6))
            op("act", lambda h: h.activation(out=f0, in_=LD, func=AF.Exp), ["lb"], ["f0"])
            op("dve", lambda h: h.tensor_tensor(out=f1, in0=LI, in1=f0, op=ALU.mult), ["lb", "f0"], ["Rth"])
            op("dve", lambda h: h.tensor_tensor(out=f2, in0=LR, in1=f0, op=ALU.mult), ["lb", "f0"], ["f2"])
            op("act", lambda h: h.activation(out=f2, in_=f2, func=AF.Exp), ["f2"], ["f2"])
            sincos("R", f1, f3, f4, f5, "R", kiR[:], kfR[:])
            op("dve", lambda h: h.tensor_tensor(out=f4, in0=f4, in1=f2, op=ALU.mult), ["Rc", "f2"], ["Rc"])
            op("dve", lambda h: h.tensor_scalar(out=f4, in0=f4, scalar1=-1.0, scalar2=None, op0=ALU.add), ["Rc"], ["Rc"])
            op("dve", lambda h: h.tensor_tensor(out=f3, in0=f3, in1=f2, op=ALU.mult), ["Rs", "f2"], ["Rs"])
            op("dve", lambda h: h.tensor_tensor(out=f0, in0=LR, in1=LR, op=ALU.mult), ["lb", "Rth", "f2"], ["f0"])
            op("dve", lambda h: h.tensor_tensor(out=f5, in0=LI, in1=LI, op=ALU.mult), ["lb", "Rc"], ["Rtmp"])
            op("dve", lambda h: h.tensor_tensor(out=f0, in0=f0, in1=f5, op=ALU.add), ["f0", "Rtmp"], ["f0"])
            op("dve", lambda h: h.reciprocal(out=f0, in_=f0), ["f0"], ["f0"])
            op("dve", lambda h: h.tensor_tensor(out=f1, in0=f4, in1=LR, op=ALU.mult), ["Rc", "lb", "Rth", "Rtmp"], ["f1"])
            op("dve", lambda h: h.tensor_tensor(out=f5, in0=f3, in1=LI, op=ALU.mult), ["Rs", "lb"], ["Rtmp"])
            op("dve", lambda h: h.tensor_tensor(out=f1, in0=f1, in1=f5, op=ALU.add), ["f1", "Rtmp"], ["f1"])
            op("dve", lambda h: h.tensor_tensor(out=f2, in0=f3, in1=LR, op=ALU.mult), ["Rs", "lb"], ["f2"])
            op("dve", lambda h: h.tensor_tensor(out=f5, in0=f4, in1=LI, op=ALU.mult), ["Rc", "lb", "f1"], ["Rtmp"])
            op("dve", lambda h: h.tensor_tensor(out=f2, in0=f2, in1=f5, op=ALU.subtract), ["f2", "Rtmp"], ["f2"])
            op("dve", lambda h: h.tensor_tensor(out=f1, in0=f1, in1=f0, op=ALU.mult), ["f1", "f0"], ["f1"])
            op("dve", lambda h: h.tensor_tensor(out=f2, in0=f2, in1=f0, op=ALU.mult), ["f2", "f0"], ["f2"])
            Bre_, Bim_ = bblk[:, 0].rearrange("p q m -> p (q m)"), bblk[:, 1].rearrange("p q m -> p (q m)")
            op("dve", lambda h: h.tensor_tensor(out=f3, in0=Bre_, in1=f1, op=ALU.mult), ["bblk", "f1", "Rs"], ["f3"])
            op("dve", lambda h: h.tensor_tensor(out=f4, in0=Bim_, in1=f2, op=ALU.mult), ["bblk", "f2", "Rc"], ["f4"])
            op("dve", lambda h, pc=pc: h.tensor_tensor(out=WBre[:, pc * 2:(pc + 1) * 2, :].rearrange("p q m -> p (q m)"), in0=f3, in1=f4, op=ALU.subtract), ["f3", "f4"], ["WBre"])
            op("dve", lambda h: h.tensor_tensor(out=f3, in0=Bre_, in1=f2, op=ALU.mult), ["bblk", "f2", "WBre"], ["f3"])
            op("dve", lambda h: h.tensor_tensor(out=f4, in0=Bim_, in1=f1, op=ALU.mult), ["bblk", "f1", "WBre"], ["f4"])
            op("dve", lambda h, pc=pc: h.tensor_tensor(out=WBim[:, pc * 2:(pc + 1) * 2, :].rearrange("p q m -> p (q m)"), in0=f3, in1=f4, op=ALU.add), ["f3", "f4"], ["WBim"])
            op("act", lambda h, pc=pc: h.activation(out=WCre[:, pc * 2:(pc + 1) * 2, :], in_=cblk[:, 0], func=AF.Copy), ["cblk"], ["WCre"])
            op("act", lambda h, pc=pc: h.activation(out=WCimn[:, pc * 2:(pc + 1) * 2, :], in_=cblk[:, 1], func=AF.Copy, scale=-1.0), ["cblk"], ["WCimn"])

        def rms(src_ap, key_src, n, slot, scale, pn):
            op("act", lambda h: h.activation(out=junk[0:pn, 0:n], in_=src_ap, func=AF.Square, accum_out=ssq[0:pn, slot:slot + 1]),
               [key_src], ["junk", "ssq%d" % slot])
            op("act", lambda h: h.activation(out=rstd[0:pn, slot:slot + 1], in_=ssq[0:pn, slot:slot + 1], func=AF.Ln, scale=scale, bias=epsb[0:pn, 0:1]))
            op("act", lambda h: h.activation(out=rstd[0:pn, slot:slot + 1], in_=rstd[0:pn, slot:slot + 1], func=AF.Exp, scale=-0.5))

        def mixer(ti, tl):
            sample = (ti == NPT)
            xt = xb[:, tl, :]
            n = 128
            ns = NS if sample else 128
            r0 = ti * 128
            par = ti % 2
            if sample:
                op("pool", lambda h: h.memset(kTs[:], 0.0))
                P.dma("pool", lambda h: h.dma_start(out=kTs[0:64, 0, 0:NSEQ * 128], in_=kcT_d[0:64, :]))
                P.dma("pool", lambda h: h.dma_start(out=kTs[64:128, 1, 0:NSEQ * 128], in_=kcT_d[64:128, :]))
                P.dma("pool", lambda h: h.dma_start(out=vs[:, 0:NSEQ, :], in_=vc_d))
                P.dma("sp", lambda h: h.dma_start(out=cvst, in_=cvst_d))
            rms(xt[:], "xt", D, 0, 1.0 / D, 128)
            op("dve", lambda h: h.tensor_scalar(out=xn[:], in0=xt[:], scalar1=rstd[:, 0:1], scalar2=None, op0=ALU.mult))
            for c in range(8):
                op("pe", lambda h, c=c: h.transpose(out=pT[:, c, :], in_=xn[:, c * 128:(c + 1) * 128], identity=idb[:]))
            op("dve", lambda h: h.tensor_tensor(out=xnT[:], in0=pT[:], in1=pcol[:, GMIX:GMIX + 8].unsqueeze(2).broadcast_to([128, 8, 128]), op=ALU.mult))
            W = wi[par]
            for i in range(4):
                bank = mmA if i % 2 == 0 else mmB
                for c in range(8):
                    op("pe", lambda h, i=i, c=c, bank=bank: h.matmul(bank[:, 0:128], lhsT=W[:, c, i * 128:(i + 1) * 128], rhs=xnT[:, c, :],
                                                                   start=(c == 0), stop=(c == 7)))
                op("act", lambda h, i=i, bank=bank: h.activation(out=qT[:, i, :], in_=bank[:, 0:128], func=AF.Copy))
            for c in range(8):
                op("pe", lambda h, c=c: h.matmul(mmC[:, 0:128], lhsT=W[:, c, 512:640], rhs=xnT[:, c, :], start=(c == 0), stop=(c == 7)))
            if sample:
                op("dve", lambda h: h.tensor_copy(out=kTs[0:64, 0, NSEQ * 128:17 * 128], in_=mmC[0:64, 0:128]))
                op("dve", lambda h: h.tensor_copy(out=kTs[64:128, 1, NSEQ * 128:17 * 128], in_=mmC[64:128, 0:128]))
            else:
                op("dve", lambda h: h.tensor_copy(out=kTx[0:64, 0, 128:256], in_=mmC[0:64, 0:128]))
                op("dve", lambda h: h.tensor_copy(out=kTx[64:128, 1, 128:256], in_=mmC[64:128, 0:128]))
            for i in range(4):
                bank = mmA if i % 2 == 0 else mmB
                for c in range(8):
                    op("pe", lambda h, i=i, c=c, bank=bank: h.matmul(bank[:, 0:128], lhsT=W[:, c, 768 + i * 128:768 + (i + 1) * 128], rhs=xnT[:, c, :],
                                                                   start=(c == 0), stop=(c == 7)))
                op("act", lambda h, i=i, bank=bank: h.activation(out=uT[:, i, :], in_=bank[:, 0:128], func=AF.Copy))
            for c in range(8):
                op("pe", lambda h, c=c: h.matmul(mmC[:, 0:256], lhsT=xnT[:, c, :], rhs=W[:, c, 512:768], start=(c == 0), stop=(c == 7)))
            if sample:
                op("dve", lambda h: h.tensor_copy(out=vs[:, 16, :], in_=mmC[:, 128:256]))
            else:
                op("dve", lambda h: h.tensor_copy(out=vx[:, 1, :], in_=mmC[:, 128:256]))
            if sample or ti == NPT - 1:
                op("dve", lambda h: h.tensor_copy(out=kvtok[:], in_=mmC[:, 0:256]))
                if sample:
                    P.dma("sp", lambda h: h.dma_start(out=kvs_k[:, 124:128, :], in_=kvtok[0:NS, 0:128]))
                    P.dma("sp", lambda h: h.dma_start(out=kvs_v[:, 124:128, :], in_=kvtok[0:NS, 128:256]))
                else:
                    P.dma("sp", lambda h: h.dma_start(out=kvp_d, in_=kvtok[:]))

            if not sample:
                mi = 0 if ti == 0 else (1 if ti == 1 else 2)
                for i in range(4):
                    for hh_ in range(2):
                        op("pe", lambda h, i=i, hh_=hh_: h.matmul(Sps[:, hh_ * 256:(hh_ + 1) * 256], lhsT=qT[:, i, :], rhs=kTx[:, hh_, :], start=True, stop=True))
                    for hh_ in range(2):
                        op("dve", lambda h, hh_=hh_, hd=i + 4 * hh_: h.tensor_scalar(out=Sx[:, 0, hh_, 256:257], in0=sink8[:, hd:hd + 1], scalar1=8.0, scalar2=None, op0=ALU.mult))
                    op("dve", lambda h: h.tensor_tensor(out=Sx[:, 0, :, 0:256], in0=Sps[:].rearrange("p (a k) -> p a k", a=2),
                                                        in1=masks[:, mi:mi + 1, :].broadcast_to([128, 2, 256]), op=ALU.add))
                    op("dve", lambda h: h.tensor_reduce(out=mx[:], in_=Sx[:, 0], axis=AX.X, op=ALU.max))
                    op("dve", lambda h: h.tensor_scalar(out=nbias[:], in0=mx[:], scalar1=-0.125, scalar2=None, op0=ALU.mult))
                    for hh_ in range(2):
                        op("act", lambda h, hh_=hh_: h.activation(out=Pb[:, hh_, :], in_=Sx[:, 0, hh_, :], func=AF.Exp, scale=0.125,
                                                                 bias=nbias[:, hh_:hh_ + 1], accum_out=rs[:, hh_:hh_ + 1]))
                    op("dve", lambda h: h.reciprocal(out=rinv[:], in_=rs[:]))
                    for hh_ in range(2):
                        for blk in range(2):
                            op("pe", lambda h, hh_=hh_, blk=blk: h.transpose(out=pT[:, hh_ * 2 + blk, :], in_=Pb[:, hh_, blk * 128:(blk + 1) * 128], identity=idb[:]))
                    op("act", lambda h: h.activation(out=PTs[:], in_=pT[:, 0:4, :], func=AF.Copy))
                    for hh_ in range(2):
                        for blk in range(2):
                            op("pe", lambda h, hh_=hh_, blk=blk: h.matmul(Ops[:, hh_ * 64:(hh_ + 1) * 64], lhsT=PTs[:, hh_ * 2 + blk, :],
                                                                         rhs=vx[:, blk, hh_ * 64:(hh_ + 1) * 64], start=(blk == 0), stop=(blk == 1)))
                    for hh_ in range(2):
                        hd = i + 4 * hh_
                        op("dve", lambda h, hh_=hh_, hd=hd: h.tensor_scalar(out=attn[:, hd * 64:(hd + 1) * 64], in0=Ops[:, hh_ * 64:(hh_ + 1) * 64],
                                                                           scalar1=rinv[:, hh_:hh_ + 1], scalar2=None, op0=ALU.mult))
                op("pool", lambda h: h.tensor_copy(out=kTx[:, :, 0:128], in_=kTx[:, :, 128:256]))
                op("pool", lambda h: h.tensor_copy(out=vx[:, 0, :], in_=vx[:, 1, :]))
            else:
                W17 = 17 * 128
                for hd in range(8):
                    i, hh_ = hd % 4, hd // 4
                    for cb in range(5):
                        c0 = cb * 512
                        cw = min(512, W17 - c0)
                        bank = mmA if cb % 2 == 0 else mmB
                        op("pe", lambda h, i=i, hh_=hh_, c0=c0, cw=cw, bank=bank: h.matmul(bank[:, 0:cw], lhsT=qT[:, i, :], rhs=kTs[:, hh_, c0:c0 + cw], start=True, stop=True))
                        op("dve", lambda h, c0=c0, cw=cw, bank=bank: h.tensor_tensor(out=Ssx[:, c0:c0 + cw], in0=bank[:, 0:cw], in1=smask[:, c0:c0 + cw], op=ALU.add))
                    op("dve", lambda h, hd=hd: h.tensor_scalar(out=Ssx[:, W17:W17 + 1], in0=sink8[:, hd:hd + 1], scalar1=8.0, scalar2=None, op0=ALU.mult))
                    op("dve", lambda h: h.tensor_reduce(out=mx[:, 0:1], in_=Ssx[:], axis=AX.X, op=ALU.max))
                    op("dve", lambda h: h.tensor_scalar(out=nbias[:, 0:1], in0=mx[:, 0:1], scalar1=-0.125, scalar2=None, op0=ALU.mult))
                    op("act", lambda h: h.activation(out=Psb[:], in_=Ssx[:], func=AF.Exp, scale=0.125, bias=nbias[:, 0:1], accum_out=rs[:, 0:1]))
                    op("dve", lambda h: h.reciprocal(out=rinv[:, 0:1], in_=rs[:, 0:1]))
                    for g8 in range(3):
                        nb_ = 8 if g8 < 2 else 1
                        for b in range(nb_):
                            blk = g8 * 8 + b
                            op("pe", lambda h, b=b, blk=blk: h.transpose(out=pT[:, b, :], in_=Psb[:, blk * 128:(blk + 1) * 128], identity=idb[:]))
                        op("act", lambda h, g8=g8, nb_=nb_: h.activation(out=PTss[:, g8 * 8:g8 * 8 + nb_, :], in_=pT[:, 0:nb_, :], func=AF.Copy))
                    for blk in range(17):
                        op("pe", lambda h, blk=blk, hh_=hh_: h.matmul(Ops[:, 0:64], lhsT=PTss[:, blk, :], rhs=vs[:, blk, hh_ * 64:(hh_ + 1) * 64],
                                                                     start=(blk == 0), stop=(blk == 16)))
                    op("dve", lambda h, hd=hd: h.tensor_scalar(out=attn[:, hd * 64:(hd + 1) * 64], in0=Ops[:, 0:64], scalar1=rinv[:, 0:1], scalar2=None, op0=ALU.mult))
            rms(attn[0:n, :], "attn", 512, 1, 1.0 / 512, n)
            op("dve", lambda h: h.tensor_scalar(out=anb[0:n, :], in0=attn[0:n, :], scalar1=rstd[0:n, 1:2], scalar2=None, op0=ALU.mult), ["attn", "rstd1"], ["anb"])
            for c in range(4):
                op("pe", lambda h, c=c: h.transpose(out=pT[:, c, 0:n], in_=anb[0:n, c * 128:(c + 1) * 128], identity=idb[0:n, 0:n]), ["anb", "idb"], ["pT"])
            op("dve", lambda h: h.tensor_tensor(out=mixT[:, 0:4, :], in0=pT[:, 0:4, :], in1=pcol[:, GATT:GATT + 4].unsqueeze(2).broadcast_to([128, 4, 128]), op=ALU.mult))

            def ssm_views(k):
                c4, hp = k // 2, k % 2
                q0 = c4 * 4 + 2 * hp
                bi = k % 2
                Bv = Bs2[bi][:].rearrange("p (r j k) -> p r j k", r=2, j=2)
                if sample:
                    csq = cs[:, q0:q0 + 2, 1:5].unsqueeze(2).broadcast_to([128, 2, NSEQ, 4])
                    snq = sn[:, q0:q0 + 2, 1:5].unsqueeze(2).broadcast_to([128, 2, NSEQ, 4])

                    def v3(ap):
                        return ap[:, :, 0:NS].rearrange("p j (s t) -> p j s t", t=4)
                else:
                    csq = cs[:, q0:q0 + 2, 1:129]; snq = sn[:, q0:q0 + 2, 1:129]

                    def v3(ap):
                        return ap
                T = [v3(pp2[bi][:, kk]) for kk in range(4)]
                RR = [v3(rr2[bi][:, kk]) for kk in range(2)]
                VV = [v3(vv2[bi][:, kk]) for kk in range(2)]
                return c4, hp, q0, bi, Bv, csq, snq, v3, T, RR, VV

            def ssm_front(k):
                c4, hp, q0, bi, Bv, csq, snq, v3, T, RR, VV = ssm_views(k)
                bank = mmA if bi == 0 else mmC
                for ri, WBx in enumerate((WBre, WBim)):
                    for j in range(2):
                        op("pe", lambda h: h.matmul(bank[:, (ri * 2 + j) * 128:(ri * 2 + j + 1) * 128], lhsT=WBx[:, q0 + j, :], rhs=uT[:, c4, :], start=True, stop=True))
                op("act", lambda h: h.activation(out=Bs2[bi][:], in_=bank[:, :], func=AF.Copy))
                if sample:
                    for ri in range(2):
                        v4 = Bv[:, ri, :, 0:NS].rearrange("p j (s t) -> p j s t", t=4)[:, :, :, 0]
                        op("dve", lambda h: h.tensor_tensor(out=v4, in0=v4, in1=ah[:, ri, q0:q0 + 2, :], op=ALU.add))
                Bre, Bim = v3(Bv[:, 0]), v3(Bv[:, 1])
                op("dve", lambda h: h.tensor_tensor(out=T[0], in0=Bre, in1=csq, op=ALU.mult))
                op("pool", lambda h: h.tensor_tensor(out=T[1], in0=Bim, in1=snq, op=ALU.mult))
                op("dve", lambda h: h.tensor_tensor(out=T[2], in0=Bim, in1=csq, op=ALU.mult))
                op("pool", lambda h: h.tensor_tensor(out=T[3], in0=Bre, in1=snq, op=ALU.mult))
                op("dve", lambda h: h.tensor_tensor(out=RR[0], in0=T[0], in1=T[1], op=ALU.add))
                op("pool", lambda h: h.tensor_tensor(out=RR[1], in0=T[2], in1=T[3], op=ALU.subtract))

            def ssm_back(k):
                c4, hp, q0, bi, Bv, csq, snq, v3, T, RR, VV = ssm_views(k)
                rrb, vvb = rr2[bi], vv2[bi]
                for j in range(2):
                    q = q0 + j
                    if sample:
                        op("dve", lambda h: h.tensor_scalar(out=dtmp[:], in0=mask64[:], scalar1=rho[:, q:q + 1], scalar2=None, op0=ALU.mult))
                    for ri in range(2):
                        if sample:
                            op("dve", lambda h: h.tensor_tensor_scan(out=vvb[:, ri, j, 0:NS], data0=dtmp[:], data1=rrb[:, ri, j, 0:NS], initial=0.0, op0=ALU.mult, op1=ALU.add))
                        else:
                            op("dve", lambda h: h.tensor_tensor_scan(out=vvb[:, ri, j, :], data0=rho[:, q:q + 1].broadcast_to([128, 128]), data1=rrb[:, ri, j, :],
                                                                   initial=car[:, ri, q:q + 1], op0=ALU.mult, op1=ALU.add))
                op("dve", lambda h: h.tensor_tensor(out=T[0], in0=VV[0], in1=csq, op=ALU.mult))
                op("pool", lambda h: h.tensor_tensor(out=T[1], in0=VV[1], in1=snq, op=ALU.mult))
                op("pool", lambda h: h.tensor_tensor(out=T[2], in0=VV[0], in1=snq, op=ALU.mult))
                op("dve", lambda h: h.tensor_tensor(out=T[3], in0=VV[1], in1=csq, op=ALU.mult))
                hb_re, hb_im = v3(hb[:, 2 * hp:2 * hp + 2, 0, :]), v3(hb[:, 2 * hp:2 * hp + 2, 1, :])
                op("pool", lambda h: h.tensor_tensor(out=hb_re, in0=T[0], in1=T[1], op=ALU.subtract))
                op("dve", lambda h: h.tensor_tensor(out=hb_im, in0=T[2], in1=T[3], op=ALU.add))
                if sample:
                    op("pool", lambda h: h.tensor_tensor(out=hfin[:, 0, q0:q0 + 2, 0:NSEQ], in0=T[0][:, :, :, 3], in1=T[1][:, :, :, 3], op=ALU.subtract))
                    op("dve", lambda h: h.tensor_tensor(out=hfin[:, 1, q0:q0 + 2, 0:NSEQ], in0=T[2][:, :, :, 3], in1=T[3][:, :, :, 3], op=ALU.add))
                else:
                    op("pool", lambda h: h.tensor_tensor(out=car[:, 0, q0:q0 + 2], in0=T[0][:, :, 127], in1=T[1][:, :, 127], op=ALU.subtract))
                    op("dve", lambda h: h.tensor_tensor(out=car[:, 1, q0:q0 + 2], in0=T[2][:, :, 127], in1=T[3][:, :, 127], op=ALU.add))

            def ssm_y(c4):
                for qq in range(4):
                    q = c4 * 4 + qq
                    op("pe", lambda h: h.matmul(mmB[:, 0:n], lhsT=WCre[:, q, :], rhs=hb[:, qq, 0, 0:n], start=(qq == 0), stop=False))
                    op("pe", lambda h: h.matmul(mmB[:, 0:n], lhsT=WCimn[:, q, :], rhs=hb[:, qq, 1, 0:n], start=False, stop=(qq == 3)))
                op("dve", lambda h: h.scalar_tensor_tensor(out=yv[:, 0:n], in0=uT[:, c4, 0:n], scalar=col(DSK + c4), in1=mmB[:, 0:n], op0=ALU.mult, op1=ALU.add))
                op("act", lambda h: h.activation(out=gl32[:, c4, 0:n], in_=yv[:, 0:n], func=AF.Gelu))
                op("pool", lambda h: h.tensor_copy(out=glb[:, c4, 0:n], in_=gl32[:, c4, 0:n]))

            ssm_front(0)
            for k in range(8):
                if k + 1 < 8:
                    ssm_front(k + 1)
                ssm_back(k)
                if k % 2 == 1:
                    ssm_y(k // 2)
            if ti == NPT - 1:
                op("act", lambda h: h.activation(out=hfin[:, :, :, NSEQ], in_=car[:], func=AF.Copy), ["car"], ["hfin"])
            for oc in range(4):
                for c4 in range(4):
                    op("pe", lambda h, oc=oc, c4=c4: h.matmul(mmC[:, 0:n], lhsT=wgl[par][:, c4, oc * 128:(oc + 1) * 128], rhs=glb[:, c4, 0:n], start=(c4 == 0), stop=(c4 == 3)),
                       ["glb", "wgl"], ["mmC"])
                op("act", lambda h, oc=oc: h.activation(out=sg[:, 0:n], in_=mmC[:, 0:n], func=AF.Sigmoid, bias=col(BGLU + oc)), ["mmC", "pcol"], ["sg"])
                op("dve", lambda h, oc=oc: h.tensor_tensor(out=gl32[:, oc, 0:n], in0=gl32[:, oc, 0:n], in1=sg[:, 0:n], op=ALU.mult), ["gl32", "sg", "glb"], ["gl32"])
                op("act", lambda h, oc=oc: h.activation(out=sqb[:, oc, 0:n], in_=gl32[:, oc, 0:n], func=AF.Square), ["gl32"], ["sqb"])
            for oc in range(4):
                op("pe", lambda h, oc=oc: h.matmul(mmC[:, 0:n], lhsT=onesb[:], rhs=sqb[:, oc, 0:n], start=(oc == 0), stop=(oc == 3)), ["sqb", "onesb"], ["mmC"])
            op("act", lambda h: h.activation(out=rsb[:, 0:n], in_=mmC[:, 0:n], func=AF.Ln, scale=1.0 / 512, bias=epsb[:, 0:1]))
            op("act", lambda h: h.activation(out=rsb[:, 0:n], in_=rsb[:, 0:n], func=AF.Exp, scale=-0.5))
            for oc in range(4):
                op("dve", lambda h, oc=oc: h.scalar_tensor_tensor(out=mixT[:, 4 + oc, 0:n], in0=gl32[:, oc, 0:n], scalar=col(GSSM + oc), in1=rsb[:, 0:n], op0=ALU.mult, op1=ALU.mult),
                   ["gl32", "rsb", "pcol"], ["mixT"])

            for hf in range(2):
                acc, ak = (acc0, "acc0") if hf == 0 else (acc1, "acc1")
                for c in range(8):
                    op("pe", lambda h, hf=hf, c=c, acc=acc: h.matmul(acc[0:n, :], lhsT=mixT[:, c, 0:n], rhs=wo[par][:, c, hf * 512:(hf + 1) * 512], start=(c == 0), stop=(c == 7)),
                       ["mixT", "wo"], [ak])
                op("dve", lambda h, hf=hf, acc=acc: h.tensor_tensor(out=xt[0:n, hf * 512:(hf + 1) * 512], in0=xt[0:n, hf * 512:(hf + 1) * 512], in1=acc[0:n, :], op=ALU.add),
                   ["xt", ak], ["xt"])
            rms(xt[0:n, :], "xt", D, 2, 1.0 / D, n)
            op("dve", lambda h: h.tensor_scalar(out=xn[0:n, :], in0=xt[0:n, :], scalar1=rstd[0:n, 2:3], scalar2=None, op0=ALU.mult), ["xt", "rstd2"], ["xn"])
            for c in range(8):
                op("pe", lambda h, c=c: h.transpose(out=pT[:, c, 0:n], in_=xn[0:n, c * 128:(c + 1) * 128], identity=idb[0:n, 0:n]), ["xn", "idb"], ["pT"])
            op("dve", lambda h: h.tensor_tensor(out=xn2T[:, :, tl * 128:(tl + 1) * 128], in0=pT[:], in1=pcol[:, GFFN:GFFN + 8].unsqueeze(2).broadcast_to([128, 8, 128]), op=ALU.mult))

        def ffn(tiles_):
            sample = (tiles_[0] == NPT)
            ntl = len(tiles_)
            nb = ntl * 128
            hTv = hTsm if sample else hTall
            for ch in range(NFC):
                b3 = ch % 2
                w3 = ch % 3
                P.dma("sp", lambda h: h.dma_start(out=wg[w3][:], in_=wgb_d[ch]))
                P.dma("sp", lambda h: h.dma_start(out=wu[w3][:], in_=wub_d[ch]))
                gps, ups = (mmA, mmB) if b3 == 0 else (mmC, pT32)
                for c in range(8):
                    op("pe", lambda h, c=c, b3=b3, gps=gps: h.matmul(gps[:, 0:nb], lhsT=wg[w3][:, c, :], rhs=xn2T[:, c, 0:nb], start=(c == 0), stop=(c == 7)))
                for c in range(8):
                    op("pe", lambda h, c=c, b3=b3, ups=ups: h.matmul(ups[:, 0:nb], lhsT=wu[w3][:, c, :], rhs=xn2T[:, c, 0:nb], start=(c == 0), stop=(c == 7)))
                w0, w1, w2, bb = col(CVW + ch * 3), col(CVW + ch * 3 + 1), col(CVW + ch * 3 + 2), col(CVB + ch)
                cvb, slb, gxb = cv[:, 0, :], sl[:, 0, :], gx[:, b3, :]
                if sample:
                    op("pool", lambda h, ch=ch: h.tensor_copy(out=gxs[:, :, 0:2], in_=cvst[:, ch, :, :]))
                    op("act", lambda h: h.activation(out=gxs[:, :, 2:6], in_=gps[:, 0:NS].rearrange("p (s t) -> p s t", t=4), func=AF.Copy))
                    g0, g1, g2 = gxs[:, :, 0:4], gxs[:, :, 1:5], gxs[:, :, 2:6]
                    cvv = cvb[:, 0:NS].rearrange("p (s t) -> p s t", t=4)
                    op("pool", lambda h, ch=ch: h.tensor_copy(out=sconv[:, ch, :, :], in_=gxs[:, :, 4:6]))
                else:
                    op("pool", lambda h, ch=ch, gxb=gxb: h.tensor_copy(out=gxb[:, 0:2], in_=gcar[:, ch, :]))
                    op("act", lambda h, gxb=gxb: h.activation(out=gxb[:, 2:2 + nb], in_=gps[:, 0:nb], func=AF.Copy))
                    g0, g1, g2 = gxb[:, 0:nb], gxb[:, 1:1 + nb], gxb[:, 2:2 + nb]
                    cvv = cvb[:, 0:nb]
                    op("pool", lambda h, ch=ch, gxb=gxb: h.tensor_copy(out=gcar[:, ch, :], in_=gxb[:, nb:nb + 2]))
                op("dve", lambda h, g2=g2, cvv=cvv, w2=w2, bb=bb: h.tensor_scalar(out=cvv, in0=g2, scalar1=w2, scalar2=bb, op0=ALU.mult, op1=ALU.add))
                op("dve", lambda h, g1=g1, cvv=cvv, w1=w1: h.scalar_tensor_tensor(out=cvv, in0=g1, scalar=w1, in1=cvv, op0=ALU.mult, op1=ALU.add))
                op("dve", lambda h, g0=g0, cvv=cvv, w0=w0: h.scalar_tensor_tensor(out=cvv, in0=g0, scalar=w0, in1=cvv, op0=ALU.mult, op1=ALU.add))
                op("act", lambda h, cvb=cvb, slb=slb: h.activation(out=slb[:, 0:nb], in_=cvb[:, 0:nb], func=AF.Silu))
                op("dve", lambda h, ch=ch, slb=slb: h.tensor_tensor(out=hTv[:, ch, 0:nb], in0=slb[:, 0:nb], in1=ups[:, 0:nb], op=ALU.mult))
            accs = [acc0, acc1, Sps, Ops]
            for hf in range(2):
                for g in range(NFC // 2):
                    wdb = wd[g % 3]
                    P.dma("sp", lambda h: h.dma_start(out=wdb, in_=wdb_d[hf, g]))
                    for j in range(2):
                        ch = 2 * g + j
                        for tl in range(ntl):
                            op("pe", lambda h: h.matmul(accs[tl][:, :], lhsT=hTv[:, ch, tl * 128:(tl + 1) * 128], rhs=wdb[:, j, :],
                                                        start=(ch == 0), stop=(ch == NFC - 1)))
                for tl in range(ntl):
                    op("dve", lambda h, tl=tl, hf=hf: h.tensor_tensor(out=xb[:, tl, hf * 512:(hf + 1) * 512], in0=xb[:, tl, hf * 512:(hf + 1) * 512], in1=accs[tl][:, :], op=ALU.add))
            for tl, ti in enumerate(tiles_):
                xt = xb[:, tl, :]
                rms(xt, "xt", D, 3, 1.0 / D, 128)
                op("dve", lambda h, xt=xt: h.scalar_tensor_tensor(out=xt, in0=xt, scalar=rstd[:, 3:4], in1=gfb[:], op0=ALU.mult, op1=ALU.mult))
                P.dma("sp", lambda h, xt=xt, ti=ti: h.dma_start(out=y_d[ti * 128:(ti + 1) * 128, :], in_=xt))

        if tiles is None:
            blocks = [[0, 1, 2, 3], [4, 5, 6, 7], [8, 9, 10, 11], [12, 13, 14, 15], [16], [NPT]]
        else:
            blocks = tiles
        for blk_ in blocks:
            for tl, ti in enumerate(blk_):
                P.dma("sp", lambda h: h.dma_start(out=xb[:, tl, :], in_=xin[ti * 128:(ti + 1) * 128, :]))
            for tl, ti in enumerate(blk_):
                mixer(ti, tl)
            ffn(blk_)

        P.dma("sp", lambda h: h.dma_start(out=hfin_d, in_=hfin[:]), reads=["hfin"], writes=["hfin_d"])
        P.dma("sp", lambda h: h.dma_start(out=pconv_d, in_=gcar[:]), reads=["gcar"], writes=["pconv_d"])
        P.dma("sp", lambda h: h.dma_start(out=sconv_d, in_=sconv[:]), reads=["sconv"], writes=["sconv_d"])
        P.limit = None
        P.barrier()

        with nc.Block() as block:
            @block.tensor
            def _(h):
                P.run("pe", h, sems)

            @block.scalar
            def _(h):
                P.run("act", h, sems)

            @block.vector
            def _(h):
                P.run("dve", h, sems)

            @block.gpsimd
            def _(h):
                P.run("pool", h, sems)

            @block.sync
            def _(h):
                P.run("sp", h, sems)
    return nc


def _consts():
    ident = np.eye(128, dtype=np.float32)
    i = np.arange(128)[:, None]
    c = np.arange(256)[None, :]
    full = np.where(((c < 128) & (c > i)) | ((c >= 128) & (c - 128 <= i)), 0.0, MASKV)
    m1 = np.where(((c < 128) & (c > i) & (c >= NPAD)) | ((c >= 128) & (c - 128 <= i)), 0.0, MASKV)
    m0 = np.where((c >= 128) & (c - 128 <= i) & (c - 128 >= NPAD), 0.0, MASKV)
    masks = np.stack([m0, m1, full], axis=1).astype(np.float32)
    sm = np.full((128, 17 * 128), MASKV, np.float32)
    for s in range(NSEQ):
        for t in range(4):
            r = s * 4 + t
            sm[r, s * 128 + t + 1:(s + 1) * 128] = 0.0
            sm[r, 2048 + s * 4:2048 + s * 4 + t + 1] = 0.0
    return ident, masks, sm


def prep_inputs(x_prompt, x_sample, cache_k_win, cache_v_win, state_ssm_re, state_ssm_im, state_conv,
           meta_tokens, g_mix, w_in, sinks, lam_re, lam_im, log_dt, b_re, b_im, c_re, c_im, d_skip,
           w_glu, b_glu, g_attn_out, g_ssm_out, w_o, g_ffn, w_gate, w_up, conv_w, conv_b, w_down,
           g_final):
    f32 = np.float32
    A = lambda a: np.ascontiguousarray(np.asarray(a, dtype=f32))
    x_prompt, x_sample = A(x_prompt), A(x_sample)
    ident, masks, smask = _consts()
    w_in0 = A(w_in)[0]
    perm = []
    for i in range(4):
        perm += list(range(i * 64, (i + 1) * 64)) + list(range((4 + i) * 64, (5 + i) * 64))
    w_in_p = np.ascontiguousarray(np.concatenate([w_in0[:, perm], w_in0[:, 512:]], axis=1))
    pcol = np.zeros((128, 128), f32)
    pcol[:, 0:8] = A(g_mix)[0].reshape(8, 128).T
    pcol[:, 8:16] = A(g_ffn)[0].reshape(8, 128).T
    pcol[:, 16:20] = A(g_attn_out)[0].reshape(4, 128).T
    pcol[:, 20:24] = A(g_ssm_out)[0].reshape(4, 128).T
    pcol[:, 24:28] = A(b_glu)[0].reshape(4, 128).T
    pcol[:, 28:32] = A(d_skip)[0].reshape(4, 128).T
    cw = A(conv_w)[0].reshape(3, NFC, 128)
    pcol[:, 32:98] = cw.transpose(2, 1, 0).reshape(128, 66)
    pcol[:, 98:120] = A(conv_b)[0].reshape(NFC, 128).T
    lr, li, ld = A(lam_re)[0], A(lam_im)[0], A(log_dt)[0]
    ldx = np.repeat(ld[:, None], 64, axis=1)

    def pl(a):
        return a.reshape(16, 2, 64).transpose(1, 2, 0).reshape(128, 16)

    lam = np.ascontiguousarray(np.stack([pl(lr), pl(li), pl(ldx)], axis=1))
    lamb = np.ascontiguousarray(np.stack([lr.reshape(-1), li.reshape(-1), ldx.reshape(-1)], axis=0))
    bre, bim, cre, cim = A(b_re)[0], A(b_im)[0], A(c_re)[0], A(c_im)[0]
    bblk = np.zeros((128, 2, 16, 128), f32)
    cblk = np.zeros((128, 2, 16, 128), f32)
    for q in range(16):
        for j2 in range(2):
            g = 2 * q + j2
            g8 = g % 8
            rows = slice(g8 * 16, g8 * 16 + 16)
            cols = slice(j2 * 64, j2 * 64 + 64)
            bblk[rows, 0, q, cols] = bre[g].T
            bblk[rows, 1, q, cols] = bim[g].T
            cblk[cols, 0, q, rows] = cre[g].T
            cblk[cols, 1, q, rows] = cim[g].T
    sre, sim_ = A(state_ssm_re)[0], A(state_ssm_im)[0]
    ck, cvv = A(cache_k_win)[0].reshape(128, 128, 128), A(cache_v_win)[0].reshape(128, 128, 128)
    sc = A(state_conv)[0]
    meta = A(meta_tokens)
    wg_l = np.ascontiguousarray(A(w_gate)[0].reshape(8, 128, NFC, 128).transpose(2, 1, 0, 3))
    wu_l = np.ascontiguousarray(A(w_up)[0].reshape(8, 128, NFC, 128).transpose(2, 1, 0, 3))
    wd_l = np.ascontiguousarray(A(w_down)[0].reshape(NFC // 2, 2, 128, 2, 512).transpose(3, 0, 2, 1, 4))
    in_maps = []
    for c in range(NCORES):
        xin = np.zeros((NPT * 128 + 128, D), f32)
        xin[NPAD:128] = meta
        xin[128:NPT * 128] = x_prompt[c]
        xin[NPT * 128:NPT * 128 + NS] = x_sample[c * NSEQ:(c + 1) * NSEQ].reshape(NS, D)
        sl_ = slice(c * NSEQ, (c + 1) * NSEQ)

        def hl(a):
            return a.reshape(NSEQ, 16, 2, 64).transpose(2, 3, 1, 0).reshape(128, 16, NSEQ)

        h0 = np.ascontiguousarray(np.stack([hl(sre[sl_]), hl(sim_[sl_])], axis=1))
        cvst = np.ascontiguousarray(sc[sl_].reshape(NSEQ, 2, NFC, 128).transpose(3, 2, 0, 1))
        kc, vc = ck[sl_], cvv[sl_]
        kcT = np.ascontiguousarray(kc.transpose(2, 0, 1).reshape(128, NSEQ * 128))
        vcl = np.ascontiguousarray(vc.transpose(1, 0, 2))
        in_maps.append(dict(
            xin=xin, w_in=w_in_p, w_o=A(w_o)[0], w_glu=A(w_glu)[0], w_gate=wg_l, w_up=wu_l,
            w_down=wd_l, ident=ident, masks=masks, smask=smask, pcol=pcol, sinks=A(sinks)[0],
            gfin=A(g_final), lam=lam, lamb=lamb, bblk=bblk, cblk=cblk, h0=h0, cvst=cvst, kcT=kcT, vc=vcl,
            kcache=np.ascontiguousarray(kc), vcache=np.ascontiguousarray(vc)))
    return in_maps


def assemble(R):
    f32 = np.float32
    y_prompt = np.stack([R[c]["y"][128:NPT * 128] for c in range(NCORES)])
    y_sample = np.concatenate([R[c]["y"][NPT * 128:NPT * 128 + NS].reshape(NSEQ, 4, D) for c in range(NCORES)])
    p_k = np.stack([R[c]["kvp"][:, 0:128].reshape(128, 2, 64) for c in range(NCORES)])[None]
    p_v = np.stack([R[c]["kvp"][:, 128:256].reshape(128, 2, 64) for c in range(NCORES)])[None]
    s_k = np.concatenate([R[c]["kvs_k"].reshape(NSEQ, 128, 2, 64) for c in range(NCORES)])[None]
    s_v = np.concatenate([R[c]["kvs_v"].reshape(NSEQ, 128, 2, 64) for c in range(NCORES)])[None]

    def unh(a):
        nn = a.shape[-1]
        return a.reshape(2, 64, 16, nn).transpose(3, 2, 0, 1).reshape(nn, 32, 64)

    p_re = np.stack([unh(R[c]["hfin"][:, 0, :, NSEQ:])[0] for c in range(NCORES)])[None]
    p_im = np.stack([unh(R[c]["hfin"][:, 1, :, NSEQ:])[0] for c in range(NCORES)])[None]
    s_re = np.concatenate([unh(R[c]["hfin"][:, 0, :, :NSEQ]) for c in range(NCORES)])[None]
    s_im = np.concatenate([unh(R[c]["hfin"][:, 1, :, :NSEQ]) for c in range(NCORES)])[None]
    p_conv = np.stack([R[c]["pconv"].transpose(2, 1, 0).reshape(2, FF) for c in range(NCORES)])[None]
    s_conv = np.concatenate([R[c]["sconv"].transpose(2, 3, 1, 0).reshape(NSEQ, 2, FF) for c in range(NCORES)])[None]
    out = (y_prompt, y_sample, p_k, p_v, p_re, p_im, p_conv, s_k, s_v, s_re, s_im, s_conv)
    return tuple(np.ascontiguousarray(o, dtype=f32) for o in out)


def kernel(**inputs):
    in_maps = prep_inputs(**inputs)
    nc = build_nc()
    res = run_bass_kernel_spmd(nc, in_maps, core_ids=list(range(NCORES)))
    return assemble(res.results)
```

```python
import numpy as np
from contextlib import ExitStack
import concourse.bass as bass
import concourse.mybir as mybir
from concourse.bass_utils import run_bass_kernel_spmd

F32 = mybir.dt.float32
BF16 = mybir.dt.bfloat16
ALU = mybir.AluOpType
AF = mybir.ActivationFunctionType
AX = mybir.AxisListType

ENGS = ("pe", "act", "dve", "pool", "sp")
NDMASEM = 12
NCORES = 8
D = 1024
NPT = 17
NPAD = 112
NS = 64
NSEQ = 16
FF = 2816
NFC = 22
EPS = 1e-5
PI = float(np.pi)
MASKV = -30000.0


class _Recorder:
    def __init__(self):
        self.call = None

    def __getattr__(self, name):
        def f(*args, **kwargs):
            self.call = (name, args, kwargs)
            return self
        return f


class Prog:
    def __init__(self):
        self.streams = {e: [] for e in ENGS}
        self.count = {e: 0 for e in ENGS}
        self.waited = {e: {} for e in ENGS}
        self.regs = {}
        self.nrec = 0
        self.limit = None
        self.dma_rr = {"sp": 0, "pool": 0}
        self.dma_cnt = {}

    @staticmethod
    def _region(ap):
        dims = [(int(st), int(sz)) for st, sz in ap.ap]
        off = int(ap.offset)
        esz = mybir.dt.size(ap.dtype)
        space = str(ap.space)
        if space == "DRAM":
            ext = sum((sz - 1) * abs(st) for st, sz in dims)
            return (0, 1, off, off + ext)
        pst, npart = dims[0]
        pst = max(pst, 1)
        p0, f0 = off // pst, off % pst
        ext = sum((sz - 1) * abs(st) for st, sz in dims[1:])
        f0, ext = f0 * esz, ext * esz + esz - 1
        if space == "PSUM":
            return (0, 128, 0, 1 << 30)
        return (p0, p0 + npart, f0, f0 + ext)

    @staticmethod
    def _is_ap(v):
        return hasattr(v, "tensor") and hasattr(v, "ap") and hasattr(v, "offset")

    def _record(self, fn):
        rec = _Recorder()
        fn(rec)
        name, args, kwargs = rec.call
        acc = []
        for i, a in enumerate(args):
            if self._is_ap(a):
                acc.append((a, i == 0))
        for k, v in kwargs.items():
            if self._is_ap(v):
                acc.append((v, k in ("out", "accum_out")))
        out = []
        for ap, w in acc:
            if str(ap.space) == "PSUM":
                w = True
            out.append((ap.tensor.name, self._region(ap), w))
        self._last_call = rec.call
        return out

    def _deps(self, eng, acc):
        deps = {}
        for name, R, w in acc:
            for (R2, w2), evs in self.regs.get(name, {}).items():
                if not (w or w2):
                    continue
                if R[0] < R2[1] and R2[0] < R[1] and R[2] <= R2[3] and R2[2] <= R[3]:
                    for s_, v in evs.items():
                        if s_ == eng and eng == "pe":
                            continue
                        if deps.get(s_, 0) < v:
                            deps[s_] = v
        out = []
        wd = self.waited[eng]
        for s_, v in deps.items():
            if wd.get(s_, 0) < v:
                wd[s_] = v
                out.append((s_, v))
        return out

    def _commit(self, ev, acc):
        s_, v = ev
        for name, R, w in acc:
            d = self.regs.setdefault(name, {})
            if w:
                for key in [k for k in d if k[0][0] >= R[0] and k[0][1] <= R[1] and k[0][2] >= R[2] and k[0][3] <= R[3]]:
                    del d[key]
            e = d.setdefault((R, w), {})
            if e.get(s_, 0) < v:
                e[s_] = v

    def op(self, eng, fn, reads=(), writes=()):
        self.nrec += 1
        if self.limit is not None and self.nrec > self.limit:
            return
        acc = self._record(fn)
        deps = self._deps(eng, acc)
        self.count[eng] += 1
        ev = (eng, self.count[eng])
        self.streams[eng].append((deps, self._last_call, (eng, 1)))
        self._commit(ev, acc)

    def dma(self, eng, fn, reads=(), writes=()):
        self.nrec += 1
        if self.limit is not None and self.nrec > self.limit:
            return
        acc = self._record(fn)
        k = self.dma_rr[eng]
        self.dma_rr[eng] = (k + 1) % NDMASEM
        sname = "dma%s%d" % (eng, k)
        deps = self._deps(eng, acc)
        prev = self.dma_cnt.get(sname, 0) * 16
        if prev and self.waited[eng].get(sname, 0) < prev:
            self.waited[eng][sname] = prev
            deps.append((sname, prev))
        self.dma_cnt[sname] = self.dma_cnt.get(sname, 0) + 1
        ev = (sname, self.dma_cnt[sname] * 16)
        self.streams[eng].append((deps, self._last_call, (sname, 16)))
        self._commit(ev, acc)

    def barrier(self):
        evs = [(e, self.count[e]) for e in ENGS if self.count[e]]
        evs += [(k, c * 16) for k, c in self.dma_cnt.items()]
        for e in ENGS:
            deps = []
            for s, v in evs:
                if s == e:
                    continue
                if self.waited[e].get(s, 0) < v:
                    self.waited[e][s] = v
                    deps.append((s, v))
            if deps:
                self.streams[e].append((deps, None, None))

    def run(self, eng, h, sems):
        for deps, fn, inc in self.streams[eng]:
            for s, v in deps:
                h.wait_ge(sems[s], v)
            if fn is not None:
                name, args, kwargs = fn
                getattr(h, name)(*args, **kwargs).then_inc(sems[inc[0]], inc[1])


def build_nc(tiles=None, limit=None):
    nc = bass.Bass("TRN2", target_bir_lowering=False)

    def din(name, shape, dt=F32):
        return nc.dram_tensor(name, list(shape), dt, kind="ExternalInput").ap()

    def dout(name, shape, dt=F32):
        return nc.dram_tensor(name, list(shape), dt, kind="ExternalOutput").ap()

    xin = din("xin", [NPT * 128 + 128, D])
    w_in = din("w_in", [D, 1280])
    w_o = din("w_o", [D, D])
    w_glu = din("w_glu", [512, 512])
    w_gate = din("w_gate", [NFC, 128, 8, 128])
    w_up = din("w_up", [NFC, 128, 8, 128])
    w_down = din("w_down", [2, NFC // 2, 128, 2, 512])
    ident_d = din("ident", [128, 128])
    masks_d = din("masks", [128, 3, 256])
    smask_d = din("smask", [128, 17 * 128])
    pcol_d = din("pcol", [128, 128])
    sinks_d = din("sinks", [8])
    gfin_d = din("gfin", [D])
    lam_d = din("lam", [128, 3, 16])
    lamb_d = din("lamb", [3, 16 * 128])
    bblk_d = din("bblk", [128, 2, 16, 128])
    cblk_d = din("cblk", [128, 2, 16, 128])
    h0_d = din("h0", [128, 2, 16, NSEQ])
    cvst_d = din("cvst", [128, NFC, NSEQ, 2])
    kcT_d = din("kcT", [128, NSEQ * 128])
    vc_d = din("vc", [128, NSEQ, 128])
    kcache_d = din("kcache", [NSEQ, 128, 128])
    vcache_d = din("vcache", [NSEQ, 128, 128])

    wgb_d = nc.dram_tensor("wgb", [NFC, 128, 8, 128], BF16, kind="Internal").ap()
    wub_d = nc.dram_tensor("wub", [NFC, 128, 8, 128], BF16, kind="Internal").ap()
    wdb_d = nc.dram_tensor("wdb", [2, NFC // 2, 128, 2, 512], BF16, kind="Internal").ap()
    y_d = dout("y", [NPT * 128 + 128, D])
    kvp_d = dout("kvp", [128, 256])
    kvs_k = dout("kvs_k", [NSEQ, 128, 128])
    kvs_v = dout("kvs_v", [NSEQ, 128, 128])
    hfin_d = dout("hfin", [128, 2, 16, NSEQ + 1])
    pconv_d = dout("pconv", [128, NFC, 2])
    sconv_d = dout("sconv", [128, NFC, NSEQ, 2])

    P = Prog()
    P.limit = limit
    es = ExitStack()
    with es:
        def sb(name, shape, dt=F32):
            return es.enter_context(nc.sbuf_tensor(name, list(shape), dt))

        def ps(name, shape, dt=F32):
            return es.enter_context(nc.psum_tensor(name, list(shape), dt))

        sems = {e: es.enter_context(nc.semaphore("s_" + e)) for e in ENGS}
        for k in range(NDMASEM):
            for e_ in ("sp", "pool"):
                sems["dma%s%d" % (e_, k)] = es.enter_context(nc.semaphore("s_dma%s%d" % (e_, k)))

        idb = sb("idb", [128, 128], BF16)
        onesb = sb("onesb", [128, 128], BF16)
        masks = sb("masks_s", [128, 3, 256])
        smask = sb("smask_s", [128, 17 * 128], BF16)
        pcol = sb("pcol_s", [128, 128])
        sink8 = sb("sink8", [128, 8])
        gfb = sb("gfb", [128, D])
        epsb = sb("epsb", [128, 1])
        lam = sb("lam_s", [128, 3, 16])
        WBre = sb("WBre", [128, 16, 128], BF16); WBim = sb("WBim", [128, 16, 128], BF16)
        WCre = sb("WCre", [128, 16, 128], BF16); WCimn = sb("WCimn", [128, 16, 128], BF16)
        cs = sb("cs", [128, 16, 129]); sn = sb("sn", [128, 16, 129])
        mask64 = sb("mask64", [128, NS]); dtmp = sb("dtmp", [128, NS])
        are = sb("are", [128, 16]); aim = sb("aim", [128, 16])
        ah = sb("ah", [128, 2, 16, NSEQ])
        car = sb("car", [128, 2, 16])
        hfin = sb("hfin_s", [128, 2, 16, NSEQ + 1])
        gcar = sb("gcar", [128, NFC, 2])
        sconv = sb("sconv_s", [128, NFC, NSEQ, 2])
        kTx = sb("kTx", [128, 2, 256], BF16)
        vx = sb("vx", [128, 2, 128], BF16)
        GMIX, GFFN, GATT, GSSM, BGLU, DSK, CVW, CVB = 0, 8, 16, 20, 24, 28, 32, 98

        def col(c):
            return pcol[:, c:c + 1]

        ssq = sb("ssq", [128, 4]); rstd = sb("rstd", [128, 4])
        xn = sb("xn", [128, D], BF16)
        junk = xn
        xnT = sb("xnT", [128, 8, 128], BF16)
        qT = sb("qT", [128, 4, 128], BF16)
        uT2 = [sb("uT%d" % i, [128, 4, 128], BF16) for i in range(2)]
        Sx = sb("Sx", [128, 1, 2, 257]); mx = sb("mx", [128, 2]); nbias = sb("nbias", [128, 2])
        rs = sb("rs", [128, 2]); rinv = sb("rinv", [128, 2])
        Pb = sb("Pb", [128, 2, 257], BF16); PTs = sb("PTs", [128, 4, 128], BF16)
        mixT2 = [sb("mixT%d" % i, [128, 8, 128], BF16) for i in range(2)]
        mixT = mixT2[0]
        Psb = sb("Psb", [128, 17 * 128 + 1], BF16)
        PTss = sb("PTss", [128, 17, 128], BF16)
        Bs2 = [sb("Bs%d" % i, [128, 512]) for i in range(2)]
        pp2 = [sb("pp%d" % i, [128, 4, 2, 128]) for i in range(2)]
        rr2 = [sb("rr%d" % i, [128, 2, 2, 128]) for i in range(2)]
        vv2 = [sb("vv%d" % i, [128, 2, 2, 128]) for i in range(2)]
        hb = sb("hb", [128, 4, 2, 128], BF16)
        gl32 = sb("gl32", [128, 4, 128]); glb = sb("glb", [128, 4, 128], BF16)
        xn2T = sb("xn2T", [128, 8, 512], BF16)
        gx = sb("gx", [128, 2, 514]); gxs = sb("gxs", [128, NSEQ, 6])
        cv = sb("cv", [128, 1, 512]); sl = sb("sl", [128, 1, 512])
        kvtok = sl[:, 0, 0:256]
        yv = gx[:, 0, 0:128]; sg = gx[:, 0, 128:256]; rsb = gx[:, 0, 256:384]
        sqb = gx[:, 1, 0:256].bitcast(BF16).rearrange("p (a k) -> p a k", a=4)
        attn = cv[:, 0, :]
        anb = sl[:, 0, 256:512].bitcast(BF16)
        wi = [sb("wi0", [128, 8, 1280], BF16)] * 2
        wo = [sb("wo0", [128, 8, D], BF16)] * 2
        wgl = [sb("wgl0", [128, 4, 512], BF16)] * 2
        wg = [sb("wg%d" % i, [128, 8, 128], BF16) for i in range(2)] + [xnT]
        wu = [sb("wu%d" % i, [128, 8, 128], BF16) for i in range(2)] + [mixT]
        wdA = sb("wdA", [128, 2, 512], BF16)
        wd = [wdA[:], Sx[:].rearrange("p a b c -> p (a b c)").bitcast(BF16)[:, 0:1024].rearrange("p (j n) -> p j n", j=2),
              xn[:].rearrange("p (j n) -> p j n", j=2)]
        big = sb("big", [128, 9728])
        xb = big[:, 0:4096].rearrange("p (t d) -> p t d", t=4)
        hTall = big[:, 4096:9728].bitcast(BF16).rearrange("p (c n) -> p c n", c=NFC)
        hTsm = big[:, 4096:5504].bitcast(BF16).rearrange("p (c n) -> p c n", c=NFC)
        kTs = big[:, 5504:7680].bitcast(BF16).rearrange("p (a k) -> p a k", a=2)
        vs = big[:, 7680:8768].bitcast(BF16).rearrange("p (b k) -> p b k", b=17)
        scr = big
        h0 = big[:, 7680:8192].rearrange("p (a q s) -> p a q s", a=2, q=16)
        cvst = big[:, 8768:9472].rearrange("p (c s j) -> p c s j", c=NFC, s=NSEQ)
        def prep_views(o):
            return (scr[:, o:o + 768].rearrange("p (a b) -> p a b", a=3), scr[:, o + 768:o + 2304].rearrange("p (a b) -> p a b", a=6),
                    scr[:, o + 2304:o + 2816].rearrange("p (a q m) -> p a q m", a=2, q=2), scr[:, o + 2816:o + 3328].rearrange("p (a q m) -> p a q m", a=2, q=2))
        Ssx = big[:, 1024:1024 + 17 * 128 + 1]
        idf = big[:, 8960:9088]

        mmA = ps("mmA", [128, 512]); mmB = ps("mmB", [128, 512]); mmC = ps("mmC", [128, 512])
        pT = ps("pT", [128, 8, 128], BF16)
        pT32 = pT[:].rearrange("p c k -> p (c k)").bitcast(F32)
        Sps = ps("Sps", [128, 512]); Ops = ps("Ops", [128, 512])
        acc0 = ps("acc0", [128, 512]); acc1 = ps("acc1", [128, 512])

        op = P.op

        P.dma("sp", lambda h: h.dma_start(out=idf[:], in_=ident_d), writes=["idf"])
        P.dma("sp", lambda h: h.dma_start(out=masks[:], in_=masks_d), writes=["masks"])
        P.dma("pool", lambda h: h.dma_start(out=smask[:], in_=smask_d), writes=["smask"])
        P.dma("sp", lambda h: h.dma_start(out=pcol[:], in_=pcol_d), writes=["pcol"])
        P.dma("sp", lambda h: h.dma_start(out=sink8[:], in_=sinks_d.partition_broadcast(128)), writes=["sink8"])
        P.dma("sp", lambda h: h.dma_start(out=gfb[:], in_=gfin_d.partition_broadcast(128)), writes=["gfb"])
        P.dma("sp", lambda h: h.dma_start(out=lam[:], in_=lam_d), writes=["lam"])
        P.dma("sp", lambda h: h.dma_start(out=h0, in_=h0_d))
        op("pool", lambda h: h.memset(hb[:], 0.0))
        op("pool", lambda h: h.memset(cv[:], 0.0))
        op("pool", lambda h: h.memset(gx[:], 0.0))
        P.dma("pool", lambda h: h.dma_start(out=wi[0][:], in_=w_in.rearrange("(c p) n -> p c n", p=128)))
        P.dma("pool", lambda h: h.dma_start(out=wgl[0][:], in_=w_glu.rearrange("(c p) n -> p c n", p=128)))
        P.dma("pool", lambda h: h.dma_start(out=wo[0][:], in_=w_o.rearrange("(c p) n -> p c n", p=128)))
        for a_ in range(0, NFC, 11):
            P.dma("pool", lambda h: h.dma_start(out=wgb_d[a_:a_ + 11], in_=w_gate[a_:a_ + 11]))
            P.dma("pool", lambda h: h.dma_start(out=wub_d[a_:a_ + 11], in_=w_up[a_:a_ + 11]))
        for hf_ in range(2):
            P.dma("pool", lambda h: h.dma_start(out=wdb_d[hf_], in_=w_down[hf_]))
        P.dma("sp", lambda h: h.dma_start(out=kvs_k[:, 0:124, :], in_=kcache_d[:, 4:128, :]), writes=["kvs_k_a"])
        P.dma("sp", lambda h: h.dma_start(out=kvs_v[:, 0:124, :], in_=vcache_d[:, 4:128, :]), writes=["kvs_v_a"])

        op("dve", lambda h: h.tensor_copy(out=idb[:], in_=idf[:]), ["idf"], ["idb"])
        op("pool", lambda h: h.memset(onesb[:], 1.0), [], ["onesb"])
        op("pool", lambda h: h.memset(epsb[:], EPS), [], ["epsb"])
        op("pool", lambda h: h.memset(kTx[:], 0.0), [], ["kTx"])
        op("pool", lambda h: h.memset(vx[:], 0.0), [], ["vx"])
        op("pool", lambda h: h.memset(car[:], 0.0), [], ["car"])
        op("pool", lambda h: h.memset(gcar[:], 0.0), [], ["gcar"])
        pass
        op("pool", lambda h: h.memset(hfin[:], 0.0))
        op("pool", lambda h: h.memset(sconv[:], 0.0))
        rho = sb("rho", [128, 16])
        dtl, thl, c1, s1, tmpa, kfL = (big[:, 9088 + 16 * i:9104 + 16 * i] for i in range(6))
        kiL = big[:, 9184:9200].bitcast(mybir.dt.int32)
        kiR = big[:, 8192:8448].bitcast(mybir.dt.int32); kfR = big[:, 8448:8704]
        op("act", lambda h: h.activation(out=dtl[:], in_=lam[:, 2, :], func=AF.Exp), ["lam"], ["dtl"])
        op("dve", lambda h: h.tensor_tensor(out=thl[:], in0=lam[:, 1, :], in1=dtl[:], op=ALU.mult), ["lam", "dtl"], ["thl"])
        op("dve", lambda h: h.tensor_tensor(out=tmpa[:], in0=lam[:, 0, :], in1=dtl[:], op=ALU.mult), ["lam", "dtl"], ["tmpa"])
        op("act", lambda h: h.activation(out=rho[:], in_=tmpa[:], func=AF.Exp), ["tmpa"], ["rho"])

        def sincos(eng_tag, th_ap, s_ap, c_ap, tmp_ap, shape_key, ki_ap, kf_ap):
            K = shape_key

            def reduce_(shift, dst_key):
                op("dve", lambda h: h.tensor_scalar(out=tmp_ap, in0=th_ap, scalar1=shift, scalar2=1.0 / (2 * PI), op0=ALU.add, op1=ALU.mult),
                   [K + "th", K + "s", K + "c"], [K + "tmp"])
                op("dve", lambda h: h.tensor_copy(out=ki_ap, in_=tmp_ap), [K + "tmp"], [K + "ki"])
                op("dve", lambda h: h.tensor_copy(out=kf_ap, in_=ki_ap), [K + "ki"], [K + "kf"])
                op("dve", lambda h: h.tensor_scalar(out=tmp_ap, in0=th_ap, scalar1=shift, scalar2=None, op0=ALU.add), [K + "th", K + "ki"], [K + "tmp"])
                op("dve", lambda h: h.scalar_tensor_tensor(out=tmp_ap, in0=kf_ap, scalar=-2 * PI, in1=tmp_ap, op0=ALU.mult, op1=ALU.add),
                   [K + "kf", K + "tmp"], [K + "tmp"])
                op("dve", lambda h: h.tensor_scalar(out=kf_ap, in0=tmp_ap, scalar1=PI, scalar2=None, op0=ALU.is_gt), [K + "tmp"], [K + "kf"])
                op("dve", lambda h: h.scalar_tensor_tensor(out=tmp_ap, in0=kf_ap, scalar=-2 * PI, in1=tmp_ap, op0=ALU.mult, op1=ALU.add),
                   [K + "kf", K + "tmp"], [K + "tmp"])
                op("dve", lambda h: h.tensor_scalar(out=kf_ap, in0=tmp_ap, scalar1=-PI, scalar2=None, op0=ALU.is_lt), [K + "tmp"], [K + "kf"])
                op("dve", lambda h: h.scalar_tensor_tensor(out=tmp_ap, in0=kf_ap, scalar=2 * PI, in1=tmp_ap, op0=ALU.mult, op1=ALU.add),
                   [K + "kf", K + "tmp"], [K + "tmp"])
                op("dve", lambda h: h.tensor_scalar(out=tmp_ap, in0=tmp_ap, scalar1=-PI, scalar2=PI, op0=ALU.max, op1=ALU.min), [K + "tmp"], [K + "tmp"])

            reduce_(0.0, "s")
            op("act", lambda h: h.activation(out=s_ap, in_=tmp_ap, func=AF.Sin), [K + "tmp"], [K + "s"])
            reduce_(0.5 * PI, "c")
            op("act", lambda h: h.activation(out=c_ap, in_=tmp_ap, func=AF.Sin), [K + "tmp"], [K + "c"])

        sincos("L", thl[:], s1[:], c1[:], tmpa[:], "L", kiL[:], kfL[:])
        op("dve", lambda h: h.tensor_tensor(out=are[:], in0=rho[:], in1=c1[:], op=ALU.mult), ["rho", "Lc"], ["are"])
        op("dve", lambda h: h.tensor_tensor(out=aim[:], in0=rho[:], in1=s1[:], op=ALU.mult), ["rho", "Ls"], ["aim"])
        op("pool", lambda h: h.memset(cs[:, :, 0:1], 1.0), [], ["cs"])
        op("pool", lambda h: h.memset(sn[:, :, 0:1], 0.0), [], ["sn"])
        op("dve", lambda h: h.tensor_copy(out=cs[:, :, 1], in_=c1[:]), ["Lc", "cs"], ["cs"])
        op("dve", lambda h: h.tensor_copy(out=sn[:, :, 1], in_=s1[:]), ["Ls", "sn"], ["sn"])
        tA = pp2[0][:].rearrange("p a j (b c) -> p (a j b) c", c=64); tB = big[:, 6656:7680].rearrange("p (a c) -> p a c", c=64)
        m = 1
        while m < 128:
            cm = cs[:, :, m:m + 1].broadcast_to([128, 16, m]); sm = sn[:, :, m:m + 1].broadcast_to([128, 16, m])
            a_c = cs[:, :, 1:m + 1]; a_s = sn[:, :, 1:m + 1]
            o_c = cs[:, :, m + 1:2 * m + 1]; o_s = sn[:, :, m + 1:2 * m + 1]
            ta = tA[:, :, 0:m]; tb = tB[:, :, 0:m]
            op("dve", lambda h, a_c=a_c, cm=cm, ta=ta: h.tensor_tensor(out=ta, in0=a_c, in1=cm, op=ALU.mult), ["cs", "sn"], ["tA"])
            op("dve", lambda h, a_s=a_s, sm=sm, tb=tb: h.tensor_tensor(out=tb, in0=a_s, in1=sm, op=ALU.mult), ["cs", "sn"], ["tB"])
            op("dve", lambda h, o_c=o_c, ta=ta, tb=tb: h.tensor_tensor(out=o_c, in0=ta, in1=tb, op=ALU.subtract), ["tA", "tB", "sn"], ["cs"])
            op("dve", lambda h, a_c=a_c, sm=sm, ta=ta: h.tensor_tensor(out=ta, in0=a_c, in1=sm, op=ALU.mult), ["cs", "sn"], ["tA"])
            op("dve", lambda h, a_s=a_s, cm=cm, tb=tb: h.tensor_tensor(out=tb, in0=a_s, in1=cm, op=ALU.mult), ["cs", "sn"], ["tB"])
            op("dve", lambda h, o_s=o_s, ta=ta, tb=tb: h.tensor_tensor(out=o_s, in0=ta, in1=tb, op=ALU.add), ["tA", "tB", "cs"], ["sn"])
            m *= 2
        op("pool", lambda h: h.memset(mask64[:], 1.0))
        op("pool", lambda h: h.memset(mask64[:].rearrange("p (s t) -> p s t", t=4)[:, :, 0:1], 0.0))
        a_re_b = are[:].unsqueeze(2).broadcast_to([128, 16, NSEQ]); a_im_b = aim[:].unsqueeze(2).broadcast_to([128, 16, NSEQ])
        tC = big[:, 8704:8960].rearrange("p (q s) -> p q s", q=16)
        op("dve", lambda h: h.tensor_tensor(out=ah[:, 0], in0=h0[:, 0], in1=a_re_b, op=ALU.mult), ["h0", "are"], ["ah0"])
        op("dve", lambda h: h.tensor_tensor(out=tC[:], in0=h0[:, 1], in1=a_im_b, op=ALU.mult), ["h0", "aim"], ["tC"])
        op("dve", lambda h: h.tensor_tensor(out=ah[:, 0], in0=ah[:, 0], in1=tC[:], op=ALU.subtract), ["ah0", "tC"], ["ah0"])
        op("dve", lambda h: h.tensor_tensor(out=ah[:, 1], in0=h0[:, 0], in1=a_im_b, op=ALU.mult), ["h0", "aim"], ["ah1"])
        op("dve", lambda h: h.tensor_tensor(out=tC[:], in0=h0[:, 1], in1=a_re_b, op=ALU.mult), ["h0", "are", "ah0"], ["tC"])
        op("dve", lambda h: h.tensor_tensor(out=ah[:, 1], in0=ah[:, 1], in1=tC[:], op=ALU.add), ["ah1", "tC"], ["ah1"])

        for pc in range(8):
            lb, ft, bblk, cblk = prep_views((pc % 2) * 3328)
            for i in range(3):
                P.dma("sp", lambda h, i=i, pc=pc: h.dma_start(out=lb[:, i, :], in_=lamb_d[i, pc * 256:(pc + 1) * 256].partition_broadcast(128)), writes=["lb"])
            P.dma("sp", lambda h, pc=pc: h.dma_start(out=bblk[:], in_=bblk_d[:, :, pc * 2:(pc + 1) * 2, :]), writes=["bblk"])
            P.dma("sp", lambda h, pc=pc: h.dma_start(out=cblk[:], in_=cblk_d[:, :, pc * 2:(pc + 1) * 2, :]), writes=["cblk"])
            LR, LI, LD = lb[:, 0, :], lb[:, 1, :], lb[:, 2, :]
            f0, f1, f2, f3, f4, f5 = (ft[:, i, :] for i in range(6))
            op("act", lambda h: h.activation(out=f0, in_=LD, func=AF.Exp), ["lb"], ["f0"])
            op("dve", lambda h: h.tensor_tensor(out=f1, in0=LI, in1=f0, op=ALU.mult), ["lb", "f0"], ["Rth"])
            op("dve", lambda h: h.tensor_tensor(out=f2, in0=LR, in1=f0, op=ALU.mult), ["lb", "f0"], ["f2"])
            op("act", lambda h: h.activation(out=f2, in_=f2, func=AF.Exp), ["f2"], ["f2"])
            sincos("R", f1, f3, f4, f5, "R", kiR[:], kfR[:])
            op("dve", lambda h: h.tensor_tensor(out=f4, in0=f4, in1=f2, op=ALU.mult), ["Rc", "f2"], ["Rc"])
            op("dve", lambda h: h.tensor_scalar(out=f4, in0=f4, scalar1=-1.0, scalar2=None, op0=ALU.add), ["Rc"], ["Rc"])
            op("dve", lambda h: h.tensor_tensor(out=f3, in0=f3, in1=f2, op=ALU.mult), ["Rs", "f2"], ["Rs"])
            op("dve", lambda h: h.tensor_tensor(out=f0, in0=LR, in1=LR, op=ALU.mult), ["lb", "Rth", "f2"], ["f0"])
            op("dve", lambda h: h.tensor_tensor(out=f5, in0=LI, in1=LI, op=ALU.mult), ["lb", "Rc"], ["Rtmp"])
            op("dve", lambda h: h.tensor_tensor(out=f0, in0=f0, in1=f5, op=ALU.add), ["f0", "Rtmp"], ["f0"])
            op("dve", lambda h: h.reciprocal(out=f0, in_=f0), ["f0"], ["f0"])
            op("dve", lambda h: h.tensor_tensor(out=f1, in0=f4, in1=LR, op=ALU.mult), ["Rc", "lb", "Rth", "Rtmp"], ["f1"])
            op("dve", lambda h: h.tensor_tensor(out=f5, in0=f3, in1=LI, op=ALU.mult), ["Rs", "lb"], ["Rtmp"])
            op("dve", lambda h: h.tensor_tensor(out=f1, in0=f1, in1=f5, op=ALU.add), ["f1", "Rtmp"], ["f1"])
            op("dve", lambda h: h.tensor_tensor(out=f2, in0=f3, in1=LR, op=ALU.mult), ["Rs", "lb"], ["f2"])
            op("dve", lambda h: h.tensor_tensor(out=f5, in0=f4, in1=LI, op=ALU.mult), ["Rc", "lb", "f1"], ["Rtmp"])
            op("dve", lambda h: h.tensor_tensor(out=f2, in0=f2, in1=f5, op=ALU.subtract), ["f2", "Rtmp"], ["f2"])
            op("dve", lambda h: h.tensor_tensor(out=f1, in0=f1, in1=f0, op=ALU.mult), ["f1", "f0"], ["f1"])
            op("dve", lambda h: h.tensor_tensor(out=f2, in0=f2, in1=f0, op=ALU.mult), ["f2", "f0"], ["f2"])
            Bre_, Bim_ = bblk[:, 0].rearrange("p q m -> p (q m)"), bblk[:, 1].rearrange("p q m -> p (q m)")
            op("dve", lambda h: h.tensor_tensor(out=f3, in0=Bre_, in1=f1, op=ALU.mult), ["bblk", "f1", "Rs"], ["f3"])
            op("dve", lambda h: h.tensor_tensor(out=f4, in0=Bim_, in1=f2, op=ALU.mult), ["bblk", "f2", "Rc"], ["f4"])
            op("dve", lambda h, pc=pc: h.tensor_tensor(out=WBre[:, pc * 2:(pc + 1) * 2, :].rearrange("p q m -> p (q m)"), in0=f3, in1=f4, op=ALU.subtract), ["f3", "f4"], ["WBre"])
            op("dve", lambda h: h.tensor_tensor(out=f3, in0=Bre_, in1=f2, op=ALU.mult), ["bblk", "f2", "WBre"], ["f3"])
            op("dve", lambda h: h.tensor_tensor(out=f4, in0=Bim_, in1=f1, op=ALU.mult), ["bblk", "f1", "WBre"], ["f4"])
            op("dve", lambda h, pc=pc: h.tensor_tensor(out=WBim[:, pc * 2:(pc + 1) * 2, :].rearrange("p q m -> p (q m)"), in0=f3, in1=f4, op=ALU.add), ["f3", "f4"], ["WBim"])
            op("act", lambda h, pc=pc: h.activation(out=WCre[:, pc * 2:(pc + 1) * 2, :], in_=cblk[:, 0], func=AF.Copy), ["cblk"], ["WCre"])
            op("act", lambda h, pc=pc: h.activation(out=WCimn[:, pc * 2:(pc + 1) * 2, :], in_=cblk[:, 1], func=AF.Copy, scale=-1.0), ["cblk"], ["WCimn"])

        def rms(src_ap, key_src, n, slot, scale, pn):
            op("act", lambda h: h.activation(out=junk[0:pn, 0:n], in_=src_ap, func=AF.Square, accum_out=ssq[0:pn, slot:slot + 1]),
               [key_src], ["junk", "ssq%d" % slot])
            op("act", lambda h: h.activation(out=rstd[0:pn, slot:slot + 1], in_=ssq[0:pn, slot:slot + 1], func=AF.Ln, scale=scale, bias=epsb[0:pn, 0:1]))
            op("act", lambda h: h.activation(out=rstd[0:pn, slot:slot + 1], in_=rstd[0:pn, slot:slot + 1], func=AF.Exp, scale=-0.5))

        def partA(ti, tl):
            sample = (ti == NPT)
            xt = xb[:, tl, :]
            n = 128
            ns = NS if sample else 128
            r0 = ti * 128
            par = ti % 2
            if sample:
                op("pool", lambda h: h.memset(kTs[:], 0.0))
                P.dma("pool", lambda h: h.dma_start(out=kTs[0:64, 0, 0:NSEQ * 128], in_=kcT_d[0:64, :]))
                P.dma("pool", lambda h: h.dma_start(out=kTs[64:128, 1, 0:NSEQ * 128], in_=kcT_d[64:128, :]))
                P.dma("pool", lambda h: h.dma_start(out=vs[:, 0:NSEQ, :], in_=vc_d))
                P.dma("sp", lambda h: h.dma_start(out=cvst, in_=cvst_d))
            rms(xt[:], "xt", D, 0, 1.0 / D, 128)
            op("dve", lambda h: h.tensor_scalar(out=xn[:], in0=xt[:], scalar1=rstd[:, 0:1], scalar2=None, op0=ALU.mult))
            for c in range(8):
                op("pe", lambda h, c=c: h.transpose(out=pT[:, c, :], in_=xn[:, c * 128:(c + 1) * 128], identity=idb[:]))
            op("dve", lambda h: h.tensor_tensor(out=xnT[:], in0=pT[:], in1=pcol[:, GMIX:GMIX + 8].unsqueeze(2).broadcast_to([128, 8, 128]), op=ALU.mult))
            W = wi[par]
            yield
            for i in range(4):
                bank = acc0 if i % 2 == 0 else acc1
                for c in range(8):
                    op("pe", lambda h, i=i, c=c, bank=bank: h.matmul(bank[:, 0:128], lhsT=W[:, c, i * 128:(i + 1) * 128], rhs=xnT[:, c, :],
                                                                   start=(c == 0), stop=(c == 7)))
                op("act", lambda h, i=i, bank=bank: h.activation(out=qT[:, i, :], in_=bank[:, 0:128], func=AF.Copy))
            yield
            for c in range(8):
                op("pe", lambda h, c=c: h.matmul(Ops[:, 0:128], lhsT=W[:, c, 512:640], rhs=xnT[:, c, :], start=(c == 0), stop=(c == 7)))
            if sample:
                op("dve", lambda h: h.tensor_copy(out=kTs[0:64, 0, NSEQ * 128:17 * 128], in_=Ops[0:64, 0:128]))
                op("dve", lambda h: h.tensor_copy(out=kTs[64:128, 1, NSEQ * 128:17 * 128], in_=Ops[64:128, 0:128]))
            else:
                op("dve", lambda h: h.tensor_copy(out=kTx[0:64, 0, 128:256], in_=Ops[0:64, 0:128]))
                op("dve", lambda h: h.tensor_copy(out=kTx[64:128, 1, 128:256], in_=Ops[64:128, 0:128]))
            yield
            for i in range(4):
                bank = acc0 if i % 2 == 0 else acc1
                for c in range(8):
                    op("pe", lambda h, i=i, c=c, bank=bank: h.matmul(bank[:, 0:128], lhsT=W[:, c, 768 + i * 128:768 + (i + 1) * 128], rhs=xnT[:, c, :],
                                                                   start=(c == 0), stop=(c == 7)))
                op("act", lambda h, i=i, bank=bank: h.activation(out=uT2[par][:, i, :], in_=bank[:, 0:128], func=AF.Copy))
            yield
            for c in range(8):
                op("pe", lambda h, c=c: h.matmul(Ops[:, 0:256], lhsT=xnT[:, c, :], rhs=W[:, c, 512:768], start=(c == 0), stop=(c == 7)))
            if sample:
                op("dve", lambda h: h.tensor_copy(out=vs[:, 16, :], in_=Ops[:, 128:256]))
            else:
                op("dve", lambda h: h.tensor_copy(out=vx[:, 1, :], in_=Ops[:, 128:256]))
            if sample or ti == NPT - 1:
                op("dve", lambda h: h.tensor_copy(out=kvtok[:], in_=Ops[:, 0:256]))
                if sample:
                    P.dma("sp", lambda h: h.dma_start(out=kvs_k[:, 124:128, :], in_=kvtok[0:NS, 0:128]))
                    P.dma("sp", lambda h: h.dma_start(out=kvs_v[:, 124:128, :], in_=kvtok[0:NS, 128:256]))
                else:
                    P.dma("sp", lambda h: h.dma_start(out=kvp_d, in_=kvtok[:]))

            if not sample:
                mi = 0 if ti == 0 else (1 if ti == 1 else 2)
                for i in range(4):
                    for hh_ in range(2):
                        op("pe", lambda h, i=i, hh_=hh_: h.matmul(Sps[:, hh_ * 256:(hh_ + 1) * 256], lhsT=qT[:, i, :], rhs=kTx[:, hh_, :], start=True, stop=True))
                    for hh_ in range(2):
                        op("dve", lambda h, hh_=hh_, hd=i + 4 * hh_: h.tensor_scalar(out=Sx[:, 0, hh_, 256:257], in0=sink8[:, hd:hd + 1], scalar1=8.0, scalar2=None, op0=ALU.mult))
                    op("dve", lambda h: h.tensor_tensor(out=Sx[:, 0, :, 0:256], in0=Sps[:].rearrange("p (a k) -> p a k", a=2),
                                                        in1=masks[:, mi:mi + 1, :].broadcast_to([128, 2, 256]), op=ALU.add))
                    op("dve", lambda h: h.tensor_reduce(out=mx[:], in_=Sx[:, 0], axis=AX.X, op=ALU.max))
                    op("dve", lambda h: h.tensor_scalar(out=nbias[:], in0=mx[:], scalar1=-0.125, scalar2=None, op0=ALU.mult))
                    for hh_ in range(2):
                        op("act", lambda h, hh_=hh_: h.activation(out=Pb[:, hh_, :], in_=Sx[:, 0, hh_, :], func=AF.Exp, scale=0.125,
                                                                 bias=nbias[:, hh_:hh_ + 1], accum_out=rs[:, hh_:hh_ + 1]))
                    op("dve", lambda h: h.reciprocal(out=rinv[:], in_=rs[:]))
                    for hh_ in range(2):
                        for blk in range(2):
                            op("pe", lambda h, hh_=hh_, blk=blk: h.transpose(out=pT[:, hh_ * 2 + blk, :], in_=Pb[:, hh_, blk * 128:(blk + 1) * 128], identity=idb[:]))
                    op("act", lambda h: h.activation(out=PTs[:], in_=pT[:, 0:4, :], func=AF.Copy))
                    for hh_ in range(2):
                        for blk in range(2):
                            op("pe", lambda h, hh_=hh_, blk=blk: h.matmul(Ops[:, hh_ * 64:(hh_ + 1) * 64], lhsT=PTs[:, hh_ * 2 + blk, :],
                                                                         rhs=vx[:, blk, hh_ * 64:(hh_ + 1) * 64], start=(blk == 0), stop=(blk == 1)))
                    for hh_ in range(2):
                        hd = i + 4 * hh_
                        op("dve", lambda h, hh_=hh_, hd=hd: h.tensor_scalar(out=attn[:, hd * 64:(hd + 1) * 64], in0=Ops[:, hh_ * 64:(hh_ + 1) * 64],
                                                                           scalar1=rinv[:, hh_:hh_ + 1], scalar2=None, op0=ALU.mult))
                    yield
                op("pool", lambda h: h.tensor_copy(out=kTx[:, :, 0:128], in_=kTx[:, :, 128:256]))
                op("pool", lambda h: h.tensor_copy(out=vx[:, 0, :], in_=vx[:, 1, :]))
            else:
                W17 = 17 * 128
                for hd in range(8):
                    i, hh_ = hd % 4, hd // 4
                    for cb in range(5):
                        c0 = cb * 512
                        cw = min(512, W17 - c0)
                        bank = mmA if cb % 2 == 0 else mmB
                        op("pe", lambda h, i=i, hh_=hh_, c0=c0, cw=cw, bank=bank: h.matmul(bank[:, 0:cw], lhsT=qT[:, i, :], rhs=kTs[:, hh_, c0:c0 + cw], start=True, stop=True))
                        op("dve", lambda h, c0=c0, cw=cw, bank=bank: h.tensor_tensor(out=Ssx[:, c0:c0 + cw], in0=bank[:, 0:cw], in1=smask[:, c0:c0 + cw], op=ALU.add))
                    op("dve", lambda h, hd=hd: h.tensor_scalar(out=Ssx[:, W17:W17 + 1], in0=sink8[:, hd:hd + 1], scalar1=8.0, scalar2=None, op0=ALU.mult))
                    op("dve", lambda h: h.tensor_reduce(out=mx[:, 0:1], in_=Ssx[:], axis=AX.X, op=ALU.max))
                    op("dve", lambda h: h.tensor_scalar(out=nbias[:, 0:1], in0=mx[:, 0:1], scalar1=-0.125, scalar2=None, op0=ALU.mult))
                    op("act", lambda h: h.activation(out=Psb[:], in_=Ssx[:], func=AF.Exp, scale=0.125, bias=nbias[:, 0:1], accum_out=rs[:, 0:1]))
                    op("dve", lambda h: h.reciprocal(out=rinv[:, 0:1], in_=rs[:, 0:1]))
                    for g8 in range(3):
                        nb_ = 8 if g8 < 2 else 1
                        for b in range(nb_):
                            blk = g8 * 8 + b
                            op("pe", lambda h, b=b, blk=blk: h.transpose(out=pT[:, b, :], in_=Psb[:, blk * 128:(blk + 1) * 128], identity=idb[:]))
                        op("act", lambda h, g8=g8, nb_=nb_: h.activation(out=PTss[:, g8 * 8:g8 * 8 + nb_, :], in_=pT[:, 0:nb_, :], func=AF.Copy))
                    for blk in range(17):
                        op("pe", lambda h, blk=blk, hh_=hh_: h.matmul(Ops[:, 0:64], lhsT=PTss[:, blk, :], rhs=vs[:, blk, hh_ * 64:(hh_ + 1) * 64],
                                                                     start=(blk == 0), stop=(blk == 16)))
                    op("dve", lambda h, hd=hd: h.tensor_scalar(out=attn[:, hd * 64:(hd + 1) * 64], in0=Ops[:, 0:64], scalar1=rinv[:, 0:1], scalar2=None, op0=ALU.mult))
            rms(attn[0:n, :], "attn", 512, 1, 1.0 / 512, n)
            op("dve", lambda h: h.tensor_scalar(out=anb[0:n, :], in0=attn[0:n, :], scalar1=rstd[0:n, 1:2], scalar2=None, op0=ALU.mult), ["attn", "rstd1"], ["anb"])
            for c in range(4):
                op("pe", lambda h, c=c: h.transpose(out=pT[:, c, 0:n], in_=anb[0:n, c * 128:(c + 1) * 128], identity=idb[0:n, 0:n]), ["anb", "idb"], ["pT"])
            op("dve", lambda h: h.tensor_tensor(out=mixT2[par][:, 0:4, :], in0=pT[:, 0:4, :], in1=pcol[:, GATT:GATT + 4].unsqueeze(2).broadcast_to([128, 4, 128]), op=ALU.mult))

            yield

        def partB(ti, tl):
            sample = (ti == NPT)
            xt = xb[:, tl, :]
            n = 128
            ns = NS if sample else 128
            par = ti % 2
            uT = uT2[par]
            mixT = mixT2[par]
            def ssm_views(k):
                c4, hp = k // 2, k % 2
                q0 = c4 * 4 + 2 * hp
                bi = k % 2
                Bv = Bs2[bi][:].rearrange("p (r j k) -> p r j k", r=2, j=2)
                if sample:
                    csq = cs[:, q0:q0 + 2, 1:5].unsqueeze(2).broadcast_to([128, 2, NSEQ, 4])
                    snq = sn[:, q0:q0 + 2, 1:5].unsqueeze(2).broadcast_to([128, 2, NSEQ, 4])

                    def v3(ap):
                        return ap[:, :, 0:NS].rearrange("p j (s t) -> p j s t", t=4)
                else:
                    csq = cs[:, q0:q0 + 2, 1:129]; snq = sn[:, q0:q0 + 2, 1:129]

                    def v3(ap):
                        return ap
                T = [v3(pp2[bi][:, kk]) for kk in range(4)]
                RR = [v3(rr2[bi][:, kk]) for kk in range(2)]
                VV = [v3(vv2[bi][:, kk]) for kk in range(2)]
                return c4, hp, q0, bi, Bv, csq, snq, v3, T, RR, VV

            def ssm_front(k):
                c4, hp, q0, bi, Bv, csq, snq, v3, T, RR, VV = ssm_views(k)
                bank = mmA if bi == 0 else mmC
                for ri, WBx in enumerate((WBre, WBim)):
                    for j in range(2):
                        op("pe", lambda h: h.matmul(bank[:, (ri * 2 + j) * 128:(ri * 2 + j + 1) * 128], lhsT=WBx[:, q0 + j, :], rhs=uT[:, c4, :], start=True, stop=True))
                op("act", lambda h: h.activation(out=Bs2[bi][:], in_=bank[:, :], func=AF.Copy))
                if sample:
                    for ri in range(2):
                        v4 = Bv[:, ri, :, 0:NS].rearrange("p j (s t) -> p j s t", t=4)[:, :, :, 0]
                        op("dve", lambda h: h.tensor_tensor(out=v4, in0=v4, in1=ah[:, ri, q0:q0 + 2, :], op=ALU.add))
                Bre, Bim = v3(Bv[:, 0]), v3(Bv[:, 1])
                op("dve", lambda h: h.tensor_tensor(out=T[0], in0=Bre, in1=csq, op=ALU.mult))
                op("dve", lambda h: h.tensor_tensor(out=T[1], in0=Bim, in1=snq, op=ALU.mult))
                op("dve", lambda h: h.tensor_tensor(out=T[2], in0=Bim, in1=csq, op=ALU.mult))
                op("dve", lambda h: h.tensor_tensor(out=T[3], in0=Bre, in1=snq, op=ALU.mult))
                op("dve", lambda h: h.tensor_tensor(out=RR[0], in0=T[0], in1=T[1], op=ALU.add))
                op("dve", lambda h: h.tensor_tensor(out=RR[1], in0=T[2], in1=T[3], op=ALU.subtract))

            def ssm_back(k):
                c4, hp, q0, bi, Bv, csq, snq, v3, T, RR, VV = ssm_views(k)
                rrb, vvb = rr2[bi], vv2[bi]
                for j in range(2):
                    q = q0 + j
                    if sample:
                        op("dve", lambda h: h.tensor_scalar(out=dtmp[:], in0=mask64[:], scalar1=rho[:, q:q + 1], scalar2=None, op0=ALU.mult))
                    for ri in range(2):
                        if sample:
                            op("dve", lambda h: h.tensor_tensor_scan(out=vvb[:, ri, j, 0:NS], data0=dtmp[:], data1=rrb[:, ri, j, 0:NS], initial=0.0, op0=ALU.mult, op1=ALU.add))
                        else:
                            op("dve", lambda h: h.tensor_tensor_scan(out=vvb[:, ri, j, :], data0=rho[:, q:q + 1].broadcast_to([128, 128]), data1=rrb[:, ri, j, :],
                                                                   initial=car[:, ri, q:q + 1], op0=ALU.mult, op1=ALU.add))
                op("dve", lambda h: h.tensor_tensor(out=T[0], in0=VV[0], in1=csq, op=ALU.mult))
                op("dve", lambda h: h.tensor_tensor(out=T[1], in0=VV[1], in1=snq, op=ALU.mult))
                op("dve", lambda h: h.tensor_tensor(out=T[2], in0=VV[0], in1=snq, op=ALU.mult))
                op("dve", lambda h: h.tensor_tensor(out=T[3], in0=VV[1], in1=csq, op=ALU.mult))
                hb_re, hb_im = v3(hb[:, 2 * hp:2 * hp + 2, 0, :]), v3(hb[:, 2 * hp:2 * hp + 2, 1, :])
                op("dve", lambda h: h.tensor_tensor(out=hb_re, in0=T[0], in1=T[1], op=ALU.subtract))
                op("dve", lambda h: h.tensor_tensor(out=hb_im, in0=T[2], in1=T[3], op=ALU.add))
                if sample:
                    op("dve", lambda h: h.tensor_tensor(out=hfin[:, 0, q0:q0 + 2, 0:NSEQ], in0=T[0][:, :, :, 3], in1=T[1][:, :, :, 3], op=ALU.subtract))
                    op("dve", lambda h: h.tensor_tensor(out=hfin[:, 1, q0:q0 + 2, 0:NSEQ], in0=T[2][:, :, :, 3], in1=T[3][:, :, :, 3], op=ALU.add))
                else:
                    op("dve", lambda h: h.tensor_tensor(out=car[:, 0, q0:q0 + 2], in0=T[0][:, :, 127], in1=T[1][:, :, 127], op=ALU.subtract))
                    op("dve", lambda h: h.tensor_tensor(out=car[:, 1, q0:q0 + 2], in0=T[2][:, :, 127], in1=T[3][:, :, 127], op=ALU.add))

            def ssm_y(c4):
                for qq in range(4):
                    q = c4 * 4 + qq
                    op("pe", lambda h: h.matmul(mmB[:, 0:n], lhsT=WCre[:, q, :], rhs=hb[:, qq, 0, 0:n], start=(qq == 0), stop=False))
                    op("pe", lambda h: h.matmul(mmB[:, 0:n], lhsT=WCimn[:, q, :], rhs=hb[:, qq, 1, 0:n], start=False, stop=(qq == 3)))
                op("dve", lambda h: h.scalar_tensor_tensor(out=yv[:, 0:n], in0=uT[:, c4, 0:n], scalar=col(DSK + c4), in1=mmB[:, 0:n], op0=ALU.mult, op1=ALU.add))
                op("act", lambda h: h.activation(out=gl32[:, c4, 0:n], in_=yv[:, 0:n], func=AF.Gelu))
                op("pool", lambda h: h.tensor_copy(out=glb[:, c4, 0:n], in_=gl32[:, c4, 0:n]))

            ssm_front(0)
            yield
            for k in range(8):
                if k + 1 < 8:
                    ssm_front(k + 1)
                ssm_back(k)
                if k % 2 == 1:
                    ssm_y(k // 2)
                yield
            if ti == NPT - 1:
                op("act", lambda h: h.activation(out=hfin[:, :, :, NSEQ], in_=car[:], func=AF.Copy), ["car"], ["hfin"])
            for oc in range(4):
                for c4 in range(4):
                    op("pe", lambda h, oc=oc, c4=c4: h.matmul(mmC[:, 0:n], lhsT=wgl[0][:, c4, oc * 128:(oc + 1) * 128], rhs=glb[:, c4, 0:n], start=(c4 == 0), stop=(c4 == 3)),
                       ["glb", "wgl"], ["mmC"])
                op("act", lambda h, oc=oc: h.activation(out=sg[:, 0:n], in_=mmC[:, 0:n], func=AF.Sigmoid, bias=col(BGLU + oc)), ["mmC", "pcol"], ["sg"])
                op("dve", lambda h, oc=oc: h.tensor_tensor(out=gl32[:, oc, 0:n], in0=gl32[:, oc, 0:n], in1=sg[:, 0:n], op=ALU.mult), ["gl32", "sg", "glb"], ["gl32"])
                op("act", lambda h, oc=oc: h.activation(out=sqb[:, oc, 0:n], in_=gl32[:, oc, 0:n], func=AF.Square), ["gl32"], ["sqb"])
            for oc in range(4):
                op("pe", lambda h, oc=oc: h.matmul(mmC[:, 0:n], lhsT=onesb[:], rhs=sqb[:, oc, 0:n], start=(oc == 0), stop=(oc == 3)), ["sqb", "onesb"], ["mmC"])
            op("act", lambda h: h.activation(out=rsb[:, 0:n], in_=mmC[:, 0:n], func=AF.Ln, scale=1.0 / 512, bias=epsb[:, 0:1]))
            op("act", lambda h: h.activation(out=rsb[:, 0:n], in_=rsb[:, 0:n], func=AF.Exp, scale=-0.5))
            for oc in range(4):
                op("dve", lambda h, oc=oc: h.scalar_tensor_tensor(out=mixT[:, 4 + oc, 0:n], in0=gl32[:, oc, 0:n], scalar=col(GSSM + oc), in1=rsb[:, 0:n], op0=ALU.mult, op1=ALU.mult),
                   ["gl32", "rsb", "pcol"], ["mixT"])

            for hf in range(2):
                acc, ak = (acc0, "acc0") if hf == 0 else (acc1, "acc1")
                for c in range(8):
                    op("pe", lambda h, hf=hf, c=c, acc=acc: h.matmul(acc[0:n, :], lhsT=mixT[:, c, 0:n], rhs=wo[0][:, c, hf * 512:(hf + 1) * 512], start=(c == 0), stop=(c == 7)),
                       ["mixT", "wo"], [ak])
                op("dve", lambda h, hf=hf, acc=acc: h.tensor_tensor(out=xt[0:n, hf * 512:(hf + 1) * 512], in0=xt[0:n, hf * 512:(hf + 1) * 512], in1=acc[0:n, :], op=ALU.add),
                   ["xt", ak], ["xt"])
            rms(xt[0:n, :], "xt", D, 2, 1.0 / D, n)
            op("dve", lambda h: h.tensor_scalar(out=xn[0:n, :], in0=xt[0:n, :], scalar1=rstd[0:n, 2:3], scalar2=None, op0=ALU.mult), ["xt", "rstd2"], ["xn"])
            for c in range(8):
                op("pe", lambda h, c=c: h.transpose(out=pT[:, c, 0:n], in_=xn[0:n, c * 128:(c + 1) * 128], identity=idb[0:n, 0:n]), ["xn", "idb"], ["pT"])
            op("dve", lambda h: h.tensor_tensor(out=xn2T[:, :, tl * 128:(tl + 1) * 128], in0=pT[:], in1=pcol[:, GFFN:GFFN + 8].unsqueeze(2).broadcast_to([128, 8, 128]), op=ALU.mult))

        def ffn(tiles_):
            sample = (tiles_[0] == NPT)
            ntl = len(tiles_)
            nb = ntl * 128
            hTv = hTsm if sample else hTall
            for ch in range(NFC):
                b3 = ch % 2
                w3 = ch % 3
                P.dma("sp", lambda h: h.dma_start(out=wg[w3][:], in_=wgb_d[ch]))
                P.dma("sp", lambda h: h.dma_start(out=wu[w3][:], in_=wub_d[ch]))
                gps, ups = (mmA, mmB) if b3 == 0 else (mmC, pT32)
                for c in range(8):
                    op("pe", lambda h, c=c, b3=b3, gps=gps: h.matmul(gps[:, 0:nb], lhsT=wg[w3][:, c, :], rhs=xn2T[:, c, 0:nb], start=(c == 0), stop=(c == 7)))
                for c in range(8):
                    op("pe", lambda h, c=c, b3=b3, ups=ups: h.matmul(ups[:, 0:nb], lhsT=wu[w3][:, c, :], rhs=xn2T[:, c, 0:nb], start=(c == 0), stop=(c == 7)))
                w0, w1, w2, bb = col(CVW + ch * 3), col(CVW + ch * 3 + 1), col(CVW + ch * 3 + 2), col(CVB + ch)
                cvb, slb, gxb = cv[:, 0, :], sl[:, 0, :], gx[:, b3, :]
                if sample:
                    op("pool", lambda h, ch=ch: h.tensor_copy(out=gxs[:, :, 0:2], in_=cvst[:, ch, :, :]))
                    op("act", lambda h: h.activation(out=gxs[:, :, 2:6], in_=gps[:, 0:NS].rearrange("p (s t) -> p s t", t=4), func=AF.Copy))
                    g0, g1, g2 = gxs[:, :, 0:4], gxs[:, :, 1:5], gxs[:, :, 2:6]
                    cvv = cvb[:, 0:NS].rearrange("p (s t) -> p s t", t=4)
                    op("pool", lambda h, ch=ch: h.tensor_copy(out=sconv[:, ch, :, :], in_=gxs[:, :, 4:6]))
                else:
                    op("pool", lambda h, ch=ch, gxb=gxb: h.tensor_copy(out=gxb[:, 0:2], in_=gcar[:, ch, :]))
                    op("act", lambda h, gxb=gxb: h.activation(out=gxb[:, 2:2 + nb], in_=gps[:, 0:nb], func=AF.Copy))
                    g0, g1, g2 = gxb[:, 0:nb], gxb[:, 1:1 + nb], gxb[:, 2:2 + nb]
                    cvv = cvb[:, 0:nb]
                    op("pool", lambda h, ch=ch, gxb=gxb: h.tensor_copy(out=gcar[:, ch, :], in_=gxb[:, nb:nb + 2]))
                op("dve", lambda h, g2=g2, cvv=cvv, w2=w2, bb=bb: h.tensor_scalar(out=cvv, in0=g2, scalar1=w2, scalar2=bb, op0=ALU.mult, op1=ALU.add))
                op("dve", lambda h, g1=g1, cvv=cvv, w1=w1: h.scalar_tensor_tensor(out=cvv, in0=g1, scalar=w1, in1=cvv, op0=ALU.mult, op1=ALU.add))
                op("dve", lambda h, g0=g0, cvv=cvv, w0=w0: h.scalar_tensor_tensor(out=cvv, in0=g0, scalar=w0, in1=cvv, op0=ALU.mult, op1=ALU.add))
                op("act", lambda h, cvb=cvb, slb=slb: h.activation(out=slb[:, 0:nb], in_=cvb[:, 0:nb], func=AF.Silu))
                op("dve", lambda h, ch=ch, slb=slb: h.tensor_tensor(out=hTv[:, ch, 0:nb], in0=slb[:, 0:nb], in1=ups[:, 0:nb], op=ALU.mult))
            accs = [acc0, acc1, Sps, Ops]
            for hf in range(2):
                for g in range(NFC // 2):
                    wdb = wd[g % 3]
                    P.dma("sp", lambda h: h.dma_start(out=wdb, in_=wdb_d[hf, g]))
                    for j in range(2):
                        ch = 2 * g + j
                        for tl in range(ntl):
                            op("pe", lambda h: h.matmul(accs[tl][:, :], lhsT=hTv[:, ch, tl * 128:(tl + 1) * 128], rhs=wdb[:, j, :],
                                                        start=(ch == 0), stop=(ch == NFC - 1)))
                for tl in range(ntl):
                    op("dve", lambda h, tl=tl, hf=hf: h.tensor_tensor(out=xb[:, tl, hf * 512:(hf + 1) * 512], in0=xb[:, tl, hf * 512:(hf + 1) * 512], in1=accs[tl][:, :], op=ALU.add))
            for tl, ti in enumerate(tiles_):
                xt = xb[:, tl, :]
                rms(xt, "xt", D, 3, 1.0 / D, 128)
                op("dve", lambda h, xt=xt: h.scalar_tensor_tensor(out=xt, in0=xt, scalar=rstd[:, 3:4], in1=gfb[:], op0=ALU.mult, op1=ALU.mult))
                P.dma("sp", lambda h, xt=xt, ti=ti: h.dma_start(out=y_d[ti * 128:(ti + 1) * 128, :], in_=xt))

        if tiles is None:
            blocks = [[0, 1, 2, 3], [4, 5, 6, 7], [8, 9, 10, 11], [12, 13, 14, 15], [16], [NPT]]
        else:
            blocks = tiles
        for blk_ in blocks:
            for tl, ti in enumerate(blk_):
                P.dma("sp", lambda h: h.dma_start(out=xb[:, tl, :], in_=xin[ti * 128:(ti + 1) * 128, :]))
            for _ in partA(blk_[0], 0):
                pass
            for tl, ti in enumerate(blk_):
                gb = partB(ti, tl)
                ga = partA(blk_[tl + 1], tl + 1) if tl + 1 < len(blk_) else None
                while gb is not None or ga is not None:
                    if gb is not None and next(gb, "done") == "done":
                        gb = None
                    if ga is not None and next(ga, "done") == "done":
                        ga = None
            ffn(blk_)

        P.dma("sp", lambda h: h.dma_start(out=hfin_d, in_=hfin[:]), reads=["hfin"], writes=["hfin_d"])
        P.dma("sp", lambda h: h.dma_start(out=pconv_d, in_=gcar[:]), reads=["gcar"], writes=["pconv_d"])
        P.dma("sp", lambda h: h.dma_start(out=sconv_d, in_=sconv[:]), reads=["sconv"], writes=["sconv_d"])
        P.limit = None
        P.barrier()

        with nc.Block() as block:
            @block.tensor
            def _(h):
                P.run("pe", h, sems)

            @block.scalar
            def _(h):
                P.run("act", h, sems)

            @block.vector
            def _(h):
                P.run("dve", h, sems)

            @block.gpsimd
            def _(h):
                P.run("pool", h, sems)

            @block.sync
            def _(h):
                P.run("sp", h, sems)
    return nc


def _consts():
    ident = np.eye(128, dtype=np.float32)
    i = np.arange(128)[:, None]
    c = np.arange(256)[None, :]
    full = np.where(((c < 128) & (c > i)) | ((c >= 128) & (c - 128 <= i)), 0.0, MASKV)
    m1 = np.where(((c < 128) & (c > i) & (c >= NPAD)) | ((c >= 128) & (c - 128 <= i)), 0.0, MASKV)
    m0 = np.where((c >= 128) & (c - 128 <= i) & (c - 128 >= NPAD), 0.0, MASKV)
    masks = np.stack([m0, m1, full], axis=1).astype(np.float32)
    sm = np.full((128, 17 * 128), MASKV, np.float32)
    for s in range(NSEQ):
        for t in range(4):
            r = s * 4 + t
            sm[r, s * 128 + t + 1:(s + 1) * 128] = 0.0
            sm[r, 2048 + s * 4:2048 + s * 4 + t + 1] = 0.0
    return ident, masks, sm


def prep_inputs(x_prompt, x_sample, cache_k_win, cache_v_win, state_ssm_re, state_ssm_im, state_conv,
           meta_tokens, g_mix, w_in, sinks, lam_re, lam_im, log_dt, b_re, b_im, c_re, c_im, d_skip,
           w_glu, b_glu, g_attn_out, g_ssm_out, w_o, g_ffn, w_gate, w_up, conv_w, conv_b, w_down,
           g_final):
    f32 = np.float32
    A = lambda a: np.ascontiguousarray(np.asarray(a, dtype=f32))
    x_prompt, x_sample = A(x_prompt), A(x_sample)
    ident, masks, smask = _consts()
    w_in0 = A(w_in)[0]
    perm = []
    for i in range(4):
        perm += list(range(i * 64, (i + 1) * 64)) + list(range((4 + i) * 64, (5 + i) * 64))
    w_in_p = np.ascontiguousarray(np.concatenate([w_in0[:, perm], w_in0[:, 512:]], axis=1))
    pcol = np.zeros((128, 128), f32)
    pcol[:, 0:8] = A(g_mix)[0].reshape(8, 128).T
    pcol[:, 8:16] = A(g_ffn)[0].reshape(8, 128).T
    pcol[:, 16:20] = A(g_attn_out)[0].reshape(4, 128).T
    pcol[:, 20:24] = A(g_ssm_out)[0].reshape(4, 128).T
    pcol[:, 24:28] = A(b_glu)[0].reshape(4, 128).T
    pcol[:, 28:32] = A(d_skip)[0].reshape(4, 128).T
    cw = A(conv_w)[0].reshape(3, NFC, 128)
    pcol[:, 32:98] = cw.transpose(2, 1, 0).reshape(128, 66)
    pcol[:, 98:120] = A(conv_b)[0].reshape(NFC, 128).T
    lr, li, ld = A(lam_re)[0], A(lam_im)[0], A(log_dt)[0]
    ldx = np.repeat(ld[:, None], 64, axis=1)

    def pl(a):
        return a.reshape(16, 2, 64).transpose(1, 2, 0).reshape(128, 16)

    lam = np.ascontiguousarray(np.stack([pl(lr), pl(li), pl(ldx)], axis=1))
    lamb = np.ascontiguousarray(np.stack([lr.reshape(-1), li.reshape(-1), ldx.reshape(-1)], axis=0))
    bre, bim, cre, cim = A(b_re)[0], A(b_im)[0], A(c_re)[0], A(c_im)[0]
    bblk = np.zeros((128, 2, 16, 128), f32)
    cblk = np.zeros((128, 2, 16, 128), f32)
    for q in range(16):
        for j2 in range(2):
            g = 2 * q + j2
            g8 = g % 8
            rows = slice(g8 * 16, g8 * 16 + 16)
            cols = slice(j2 * 64, j2 * 64 + 64)
            bblk[rows, 0, q, cols] = bre[g].T
            bblk[rows, 1, q, cols] = bim[g].T
            cblk[cols, 0, q, rows] = cre[g].T
            cblk[cols, 1, q, rows] = cim[g].T
    sre, sim_ = A(state_ssm_re)[0], A(state_ssm_im)[0]
    ck, cvv = A(cache_k_win)[0].reshape(128, 128, 128), A(cache_v_win)[0].reshape(128, 128, 128)
    sc = A(state_conv)[0]
    meta = A(meta_tokens)
    wg_l = np.ascontiguousarray(A(w_gate)[0].reshape(8, 128, NFC, 128).transpose(2, 1, 0, 3))
    wu_l = np.ascontiguousarray(A(w_up)[0].reshape(8, 128, NFC, 128).transpose(2, 1, 0, 3))
    wd_l = np.ascontiguousarray(A(w_down)[0].reshape(NFC // 2, 2, 128, 2, 512).transpose(3, 0, 2, 1, 4))
    in_maps = []
    for c in range(NCORES):
        xin = np.zeros((NPT * 128 + 128, D), f32)
        xin[NPAD:128] = meta
        xin[128:NPT * 128] = x_prompt[c]
        xin[NPT * 128:NPT * 128 + NS] = x_sample[c * NSEQ:(c + 1) * NSEQ].reshape(NS, D)
        sl_ = slice(c * NSEQ, (c + 1) * NSEQ)

        def hl(a):
            return a.reshape(NSEQ, 16, 2, 64).transpose(2, 3, 1, 0).reshape(128, 16, NSEQ)

        h0 = np.ascontiguousarray(np.stack([hl(sre[sl_]), hl(sim_[sl_])], axis=1))
        cvst = np.ascontiguousarray(sc[sl_].reshape(NSEQ, 2, NFC, 128).transpose(3, 2, 0, 1))
        kc, vc = ck[sl_], cvv[sl_]
        kcT = np.ascontiguousarray(kc.transpose(2, 0, 1).reshape(128, NSEQ * 128))
        vcl = np.ascontiguousarray(vc.transpose(1, 0, 2))
        in_maps.append(dict(
            xin=xin, w_in=w_in_p, w_o=A(w_o)[0], w_glu=A(w_glu)[0], w_gate=wg_l, w_up=wu_l,
            w_down=wd_l, ident=ident, masks=masks, smask=smask, pcol=pcol, sinks=A(sinks)[0],
            gfin=A(g_final), lam=lam, lamb=lamb, bblk=bblk, cblk=cblk, h0=h0, cvst=cvst, kcT=kcT, vc=vcl,
            kcache=np.ascontiguousarray(kc), vcache=np.ascontiguousarray(vc)))
    return in_maps


def assemble(R):
    f32 = np.float32
    y_prompt = np.stack([R[c]["y"][128:NPT * 128] for c in range(NCORES)])
    y_sample = np.concatenate([R[c]["y"][NPT * 128:NPT * 128 + NS].reshape(NSEQ, 4, D) for c in range(NCORES)])
    p_k = np.stack([R[c]["kvp"][:, 0:128].reshape(128, 2, 64) for c in range(NCORES)])[None]
    p_v = np.stack([R[c]["kvp"][:, 128:256].reshape(128, 2, 64) for c in range(NCORES)])[None]
    s_k = np.concatenate([R[c]["kvs_k"].reshape(NSEQ, 128, 2, 64) for c in range(NCORES)])[None]
    s_v = np.concatenate([R[c]["kvs_v"].reshape(NSEQ, 128, 2, 64) for c in range(NCORES)])[None]

    def unh(a):
        nn = a.shape[-1]
        return a.reshape(2, 64, 16, nn).transpose(3, 2, 0, 1).reshape(nn, 32, 64)

    p_re = np.stack([unh(R[c]["hfin"][:, 0, :, NSEQ:])[0] for c in range(NCORES)])[None]
    p_im = np.stack([unh(R[c]["hfin"][:, 1, :, NSEQ:])[0] for c in range(NCORES)])[None]
    s_re = np.concatenate([unh(R[c]["hfin"][:, 0, :, :NSEQ]) for c in range(NCORES)])[None]
    s_im = np.concatenate([unh(R[c]["hfin"][:, 1, :, :NSEQ]) for c in range(NCORES)])[None]
    p_conv = np.stack([R[c]["pconv"].transpose(2, 1, 0).reshape(2, FF) for c in range(NCORES)])[None]
    s_conv = np.concatenate([R[c]["sconv"].transpose(2, 3, 1, 0).reshape(NSEQ, 2, FF) for c in range(NCORES)])[None]
    out = (y_prompt, y_sample, p_k, p_v, p_re, p_im, p_conv, s_k, s_v, s_re, s_im, s_conv)
    return tuple(np.ascontiguousarray(o, dtype=f32) for o in out)


def kernel(**inputs):
    in_maps = prep_inputs(**inputs)
    nc = build_nc()
    res = run_bass_kernel_spmd(nc, in_maps, core_ids=list(range(NCORES)))
    return assemble(res.results)
```

```python
import numpy as np
from contextlib import ExitStack
import concourse.bass as bass
import concourse.mybir as mybir
from concourse.bass_utils import run_bass_kernel_spmd

F32 = mybir.dt.float32
BF16 = mybir.dt.bfloat16
ALU = mybir.AluOpType
AF = mybir.ActivationFunctionType
AX = mybir.AxisListType

ENGS = ("pe", "act", "dve", "pool", "sp")
NDMASEM = 12
NCORES = 8
D = 1024
NPT = 17
NPAD = 112
NS = 64
NSEQ = 16
FF = 2816
NFC = 22
EPS = 1e-5
PI = float(np.pi)
MASKV = -30000.0


class _Recorder:
    def __init__(self):
        self.call = None

    def __getattr__(self, name):
        def f(*args, **kwargs):
            self.call = (name, args, kwargs)
            return self
        return f


class Prog:
    def __init__(self):
        self.streams = {e: [] for e in ENGS}
        self.count = {e: 0 for e in ENGS}
        self.waited = {e: {} for e in ENGS}
        self.regs = {}
        self.nrec = 0
        self.limit = None
        self.capture = None
        self.dma_rr = {"sp": 0, "pool": 0}
        self.dma_cnt = {}

    @staticmethod
    def _region(ap):
        dims = [(int(st), int(sz)) for st, sz in ap.ap]
        off = int(ap.offset)
        esz = mybir.dt.size(ap.dtype)
        space = str(ap.space)
        if space == "DRAM":
            ext = sum((sz - 1) * abs(st) for st, sz in dims)
            return (0, 1, off, off + ext)
        pst, npart = dims[0]
        pst = max(pst, 1)
        p0, f0 = off // pst, off % pst
        ext = sum((sz - 1) * abs(st) for st, sz in dims[1:])
        f0, ext = f0 * esz, ext * esz + esz - 1
        if space == "PSUM":
            return (0, 128, 0, 1 << 30)
        return (p0, p0 + npart, f0, f0 + ext)

    @staticmethod
    def _is_ap(v):
        return hasattr(v, "tensor") and hasattr(v, "ap") and hasattr(v, "offset")

    def _record(self, fn):
        rec = _Recorder()
        fn(rec)
        name, args, kwargs = rec.call
        acc = []
        for i, a in enumerate(args):
            if self._is_ap(a):
                acc.append((a, i == 0))
        for k, v in kwargs.items():
            if self._is_ap(v):
                acc.append((v, k in ("out", "accum_out")))
        out = []
        for ap, w in acc:
            if str(ap.space) == "PSUM":
                w = True
            out.append((ap.tensor.name, self._region(ap), w))
        self._last_call = rec.call
        return out

    def _deps(self, eng, acc):
        deps = {}
        for name, R, w in acc:
            for (R2, w2), evs in self.regs.get(name, {}).items():
                if not (w or w2):
                    continue
                if R[0] < R2[1] and R2[0] < R[1] and R[2] <= R2[3] and R2[2] <= R[3]:
                    for s_, v in evs.items():
                        if s_ == eng and eng == "pe":
                            continue
                        if deps.get(s_, 0) < v:
                            deps[s_] = v
        out = []
        wd = self.waited[eng]
        for s_, v in deps.items():
            if wd.get(s_, 0) < v:
                wd[s_] = v
                out.append((s_, v))
        return out

    def _commit(self, ev, acc):
        s_, v = ev
        for name, R, w in acc:
            d = self.regs.setdefault(name, {})
            if w:
                for key in [k for k in d if k[0][0] >= R[0] and k[0][1] <= R[1] and k[0][2] >= R[2] and k[0][3] <= R[3]]:
                    del d[key]
            e = d.setdefault((R, w), {})
            if e.get(s_, 0) < v:
                e[s_] = v

    @staticmethod
    def _freeze(fn):
        rec = _Recorder()
        fn(rec)
        return lambda h, c=rec.call: getattr(h, c[0])(*c[1], **c[2])

    def op(self, eng, fn, reads=(), writes=()):
        if self.capture is not None:
            self.capture.append(("op", eng, self._freeze(fn)))
            return
        self.nrec += 1
        if self.limit is not None and self.nrec > self.limit:
            return
        acc = self._record(fn)
        deps = self._deps(eng, acc)
        self.count[eng] += 1
        ev = (eng, self.count[eng])
        self.streams[eng].append((deps, self._last_call, (eng, 1)))
        self._commit(ev, acc)

    def dma(self, eng, fn, reads=(), writes=()):
        if self.capture is not None:
            self.capture.append(("dma", eng, self._freeze(fn)))
            return
        self.nrec += 1
        if self.limit is not None and self.nrec > self.limit:
            return
        acc = self._record(fn)
        k = self.dma_rr[eng]
        self.dma_rr[eng] = (k + 1) % NDMASEM
        sname = "dma%s%d" % (eng, k)
        deps = self._deps(eng, acc)
        prev = self.dma_cnt.get(sname, 0) * 16
        if prev and self.waited[eng].get(sname, 0) < prev:
            self.waited[eng][sname] = prev
            deps.append((sname, prev))
        self.dma_cnt[sname] = self.dma_cnt.get(sname, 0) + 1
        ev = (sname, self.dma_cnt[sname] * 16)
        self.streams[eng].append((deps, self._last_call, (sname, 16)))
        self._commit(ev, acc)

    def barrier(self):
        evs = [(e, self.count[e]) for e in ENGS if self.count[e]]
        evs += [(k, c * 16) for k, c in self.dma_cnt.items()]
        for e in ENGS:
            deps = []
            for s, v in evs:
                if s == e:
                    continue
                if self.waited[e].get(s, 0) < v:
                    self.waited[e][s] = v
                    deps.append((s, v))
            if deps:
                self.streams[e].append((deps, None, None))

    def run(self, eng, h, sems):
        for deps, fn, inc in self.streams[eng]:
            for s, v in deps:
                h.wait_ge(sems[s], v)
            if fn is not None:
                name, args, kwargs = fn
                getattr(h, name)(*args, **kwargs).then_inc(sems[inc[0]], inc[1])


def build_nc(tiles=None, limit=None):
    nc = bass.Bass("TRN2", target_bir_lowering=False)

    def din(name, shape, dt=F32):
        return nc.dram_tensor(name, list(shape), dt, kind="ExternalInput").ap()

    def dout(name, shape, dt=F32):
        return nc.dram_tensor(name, list(shape), dt, kind="ExternalOutput").ap()

    xin = din("xin", [NPT * 128 + 128, D])
    w_in = din("w_in", [D, 1280])
    w_o = din("w_o", [D, D])
    w_glu = din("w_glu", [512, 512])
    w_gate = din("w_gate", [NFC, 128, 8, 128])
    w_up = din("w_up", [NFC, 128, 8, 128])
    w_down = din("w_down", [2, NFC // 2, 128, 2, 512])
    ident_d = din("ident", [128, 128])
    masks_d = din("masks", [128, 3, 256])
    smask_d = din("smask", [128, 17 * 128])
    pcol_d = din("pcol", [128, 128])
    sinks_d = din("sinks", [8])
    gfin_d = din("gfin", [D])
    lam_d = din("lam", [128, 3, 16])
    lamb_d = din("lamb", [3, 16 * 128])
    bblk_d = din("bblk", [128, 2, 16, 128])
    cblk_d = din("cblk", [128, 2, 16, 128])
    h0_d = din("h0", [128, 2, 16, NSEQ])
    cvst_d = din("cvst", [128, NFC, NSEQ, 2])
    kcT_d = din("kcT", [128, NSEQ * 128])
    vc_d = din("vc", [128, NSEQ, 128])
    kcache_d = din("kcache", [NSEQ, 128, 128])
    vcache_d = din("vcache", [NSEQ, 128, 128])

    wgb_d = nc.dram_tensor("wgb", [NFC, 128, 8, 128], BF16, kind="Internal").ap()
    wub_d = nc.dram_tensor("wub", [NFC, 128, 8, 128], BF16, kind="Internal").ap()
    wdb_d = nc.dram_tensor("wdb", [2, NFC // 2, 128, 2, 512], BF16, kind="Internal").ap()
    y_d = dout("y", [NPT * 128 + 128, D])
    kvp_d = dout("kvp", [128, 256])
    kvs_k = dout("kvs_k", [NSEQ, 128, 128])
    kvs_v = dout("kvs_v", [NSEQ, 128, 128])
    hfin_d = dout("hfin", [128, 2, 16, NSEQ + 1])
    pconv_d = dout("pconv", [128, NFC, 2])
    sconv_d = dout("sconv", [128, NFC, NSEQ, 2])

    P = Prog()
    P.limit = limit
    es = ExitStack()
    with es:
        def sb(name, shape, dt=F32):
            return es.enter_context(nc.sbuf_tensor(name, list(shape), dt))

        def ps(name, shape, dt=F32):
            return es.enter_context(nc.psum_tensor(name, list(shape), dt))

        sems = {e: es.enter_context(nc.semaphore("s_" + e)) for e in ENGS}
        for k in range(NDMASEM):
            for e_ in ("sp", "pool"):
                sems["dma%s%d" % (e_, k)] = es.enter_context(nc.semaphore("s_dma%s%d" % (e_, k)))

        idb = sb("idb", [128, 128], BF16)
        onesb = sb("onesb", [128, 128], BF16)
        masks = sb("masks_s", [128, 3, 256])
        smask = sb("smask_s", [128, 17 * 128], BF16)
        pcol = sb("pcol_s", [128, 128])
        sink8 = sb("sink8", [128, 8])
        gfb = sb("gfb", [128, D])
        epsb = sb("epsb", [128, 1])
        lam = sb("lam_s", [128, 3, 16])
        WBre = sb("WBre", [128, 16, 128], BF16); WBim = sb("WBim", [128, 16, 128], BF16)
        WCre = sb("WCre", [128, 16, 128], BF16); WCimn = sb("WCimn", [128, 16, 128], BF16)
        cs = sb("cs", [128, 16, 129]); sn = sb("sn", [128, 16, 129])
        mask64 = sb("mask64", [128, NS]); dtmp = sb("dtmp", [128, NS])
        are = sb("are", [128, 16]); aim = sb("aim", [128, 16])
        ah = sb("ah", [128, 2, 16, NSEQ])
        car = sb("car", [128, 2, 16])
        hfin = sb("hfin_s", [128, 2, 16, NSEQ + 1])
        gcar = sb("gcar", [128, NFC, 2])
        sconv = sb("sconv_s", [128, NFC, NSEQ, 2])
        kTx = sb("kTx", [128, 2, 256], BF16)
        vx = sb("vx", [128, 2, 128], BF16)
        GMIX, GFFN, GATT, GSSM, BGLU, DSK, CVW, CVB = 0, 8, 16, 20, 24, 28, 32, 98

        def col(c):
            return pcol[:, c:c + 1]

        ssq = sb("ssq", [128, 4]); rstd = sb("rstd", [128, 4])
        xn = sb("xn", [128, D], BF16)
        junk = xn
        xnT = sb("xnT", [128, 8, 128], BF16)
        qT = sb("qT", [128, 4, 128], BF16)
        uT2 = [sb("uT%d" % i, [128, 4, 128], BF16) for i in range(2)]
        Sx = sb("Sx", [128, 1, 2, 257]); mx = sb("mx", [128, 2]); nbias = sb("nbias", [128, 2])
        rs = sb("rs", [128, 2]); rinv = sb("rinv", [128, 2])
        Pb = sb("Pb", [128, 2, 257], BF16); PTs = sb("PTs", [128, 4, 128], BF16)
        mixT2 = [sb("mixT%d" % i, [128, 8, 128], BF16) for i in range(2)]
        mixT = mixT2[0]
        Psb = sb("Psb", [128, 17 * 128 + 1], BF16)
        PTss = sb("PTss", [128, 17, 128], BF16)
        Bs2 = [sb("Bs%d" % i, [128, 512]) for i in range(2)]
        pp2 = [sb("pp%d" % i, [128, 4, 2, 128]) for i in range(2)]
        rr2 = [sb("rr%d" % i, [128, 2, 2, 128]) for i in range(2)]
        vv2 = [sb("vv%d" % i, [128, 2, 2, 128]) for i in range(2)]
        hb = sb("hb", [128, 4, 2, 128], BF16)
        gl32 = sb("gl32", [128, 4, 128]); glb = sb("glb", [128, 4, 128], BF16)
        xn2T = sb("xn2T", [128, 8, 512], BF16)
        gx = sb("gx", [128, 2, 514]); gxs = sb("gxs", [128, NSEQ, 6])
        cv = sb("cv", [128, 1, 512]); sl = sb("sl", [128, 1, 512])
        kvtok = sl[:, 0, 0:256]
        yv = gx[:, 0, 0:128]; sg = gx[:, 0, 128:256]; rsb = gx[:, 0, 256:384]
        sqb = gx[:, 1, 0:256].bitcast(BF16).rearrange("p (a k) -> p a k", a=4)
        attn = cv[:, 0, :]
        anb = sl[:, 0, 256:512].bitcast(BF16)
        wi = [sb("wi0", [128, 8, 1280], BF16)] * 2
        wo = [sb("wo0", [128, 8, D], BF16)] * 2
        wgl = [sb("wgl0", [128, 4, 512], BF16)] * 2
        wg = [sb("wg%d" % i, [128, 8, 128], BF16) for i in range(2)] + [xnT]
        wu = [sb("wu%d" % i, [128, 8, 128], BF16) for i in range(2)] + [mixT]
        wdA = sb("wdA", [128, 2, 512], BF16)
        wd = [wdA[:], Sx[:].rearrange("p a b c -> p (a b c)").bitcast(BF16)[:, 0:1024].rearrange("p (j n) -> p j n", j=2),
              xn[:].rearrange("p (j n) -> p j n", j=2)]
        big = sb("big", [128, 9728])
        xb = big[:, 0:4096].rearrange("p (t d) -> p t d", t=4)
        hTall = big[:, 4096:9728].bitcast(BF16).rearrange("p (c n) -> p c n", c=NFC)
        hTsm = big[:, 4096:5504].bitcast(BF16).rearrange("p (c n) -> p c n", c=NFC)
        kTs = big[:, 5504:7680].bitcast(BF16).rearrange("p (a k) -> p a k", a=2)
        vs = big[:, 7680:8768].bitcast(BF16).rearrange("p (b k) -> p b k", b=17)
        scr = big
        h0 = big[:, 7680:8192].rearrange("p (a q s) -> p a q s", a=2, q=16)
        cvst = big[:, 8768:9472].rearrange("p (c s j) -> p c s j", c=NFC, s=NSEQ)
        def prep_views(o):
            return (scr[:, o:o + 768].rearrange("p (a b) -> p a b", a=3), scr[:, o + 768:o + 2304].rearrange("p (a b) -> p a b", a=6),
                    scr[:, o + 2304:o + 2816].rearrange("p (a q m) -> p a q m", a=2, q=2), scr[:, o + 2816:o + 3328].rearrange("p (a q m) -> p a q m", a=2, q=2))
        Ssx = big[:, 1024:1024 + 17 * 128 + 1]
        idf = big[:, 8960:9088]

        mmA = ps("mmA", [128, 512]); mmB = ps("mmB", [128, 512]); mmC = ps("mmC", [128, 512])
        pT = ps("pT", [128, 8, 128], BF16)
        pT32 = pT[:].rearrange("p c k -> p (c k)").bitcast(F32)
        Sps = ps("Sps", [128, 512]); Ops = ps("Ops", [128, 512])
        acc0 = ps("acc0", [128, 512]); acc1 = ps("acc1", [128, 512])

        op = P.op

        P.dma("sp", lambda h: h.dma_start(out=idf[:], in_=ident_d), writes=["idf"])
        P.dma("sp", lambda h: h.dma_start(out=masks[:], in_=masks_d), writes=["masks"])
        P.dma("pool", lambda h: h.dma_start(out=smask[:], in_=smask_d), writes=["smask"])
        P.dma("sp", lambda h: h.dma_start(out=pcol[:], in_=pcol_d), writes=["pcol"])
        P.dma("sp", lambda h: h.dma_start(out=sink8[:], in_=sinks_d.partition_broadcast(128)), writes=["sink8"])
        P.dma("sp", lambda h: h.dma_start(out=gfb[:], in_=gfin_d.partition_broadcast(128)), writes=["gfb"])
        P.dma("sp", lambda h: h.dma_start(out=lam[:], in_=lam_d), writes=["lam"])
        P.dma("sp", lambda h: h.dma_start(out=h0, in_=h0_d))
        op("pool", lambda h: h.memset(hb[:], 0.0))
        op("pool", lambda h: h.memset(cv[:], 0.0))
        op("pool", lambda h: h.memset(gx[:], 0.0))
        P.dma("pool", lambda h: h.dma_start(out=wi[0][:], in_=w_in.rearrange("(c p) n -> p c n", p=128)))
        P.dma("pool", lambda h: h.dma_start(out=wgl[0][:], in_=w_glu.rearrange("(c p) n -> p c n", p=128)))
        P.dma("pool", lambda h: h.dma_start(out=wo[0][:], in_=w_o.rearrange("(c p) n -> p c n", p=128)))
        for a_ in range(0, NFC, 11):
            P.dma("pool", lambda h: h.dma_start(out=wgb_d[a_:a_ + 11], in_=w_gate[a_:a_ + 11]))
            P.dma("pool", lambda h: h.dma_start(out=wub_d[a_:a_ + 11], in_=w_up[a_:a_ + 11]))
        for hf_ in range(2):
            P.dma("pool", lambda h: h.dma_start(out=wdb_d[hf_], in_=w_down[hf_]))
        P.dma("sp", lambda h: h.dma_start(out=kvs_k[:, 0:124, :], in_=kcache_d[:, 4:128, :]), writes=["kvs_k_a"])
        P.dma("sp", lambda h: h.dma_start(out=kvs_v[:, 0:124, :], in_=vcache_d[:, 4:128, :]), writes=["kvs_v_a"])

        op("dve", lambda h: h.tensor_copy(out=idb[:], in_=idf[:]), ["idf"], ["idb"])
        op("pool", lambda h: h.memset(onesb[:], 1.0), [], ["onesb"])
        op("pool", lambda h: h.memset(epsb[:], EPS), [], ["epsb"])
        op("pool", lambda h: h.memset(kTx[:], 0.0), [], ["kTx"])
        op("pool", lambda h: h.memset(vx[:], 0.0), [], ["vx"])
        op("pool", lambda h: h.memset(car[:], 0.0), [], ["car"])
        op("pool", lambda h: h.memset(gcar[:], 0.0), [], ["gcar"])
        pass
        op("pool", lambda h: h.memset(hfin[:], 0.0))
        op("pool", lambda h: h.memset(sconv[:], 0.0))
        rho = sb("rho", [128, 16])
        dtl, thl, c1, s1, tmpa, kfL = (big[:, 9088 + 16 * i:9104 + 16 * i] for i in range(6))
        kiL = big[:, 9184:9200].bitcast(mybir.dt.int32)
        kiR = big[:, 8192:8448].bitcast(mybir.dt.int32); kfR = big[:, 8448:8704]
        op("act", lambda h: h.activation(out=dtl[:], in_=lam[:, 2, :], func=AF.Exp), ["lam"], ["dtl"])
        op("dve", lambda h: h.tensor_tensor(out=thl[:], in0=lam[:, 1, :], in1=dtl[:], op=ALU.mult), ["lam", "dtl"], ["thl"])
        op("dve", lambda h: h.tensor_tensor(out=tmpa[:], in0=lam[:, 0, :], in1=dtl[:], op=ALU.mult), ["lam", "dtl"], ["tmpa"])
        op("act", lambda h: h.activation(out=rho[:], in_=tmpa[:], func=AF.Exp), ["tmpa"], ["rho"])

        def sincos(eng_tag, th_ap, s_ap, c_ap, tmp_ap, shape_key, ki_ap, kf_ap):
            K = shape_key

            def reduce_(shift, dst_key):
                op("dve", lambda h: h.tensor_scalar(out=tmp_ap, in0=th_ap, scalar1=shift, scalar2=1.0 / (2 * PI), op0=ALU.add, op1=ALU.mult),
                   [K + "th", K + "s", K + "c"], [K + "tmp"])
                op("dve", lambda h: h.tensor_copy(out=ki_ap, in_=tmp_ap), [K + "tmp"], [K + "ki"])
                op("dve", lambda h: h.tensor_copy(out=kf_ap, in_=ki_ap), [K + "ki"], [K + "kf"])
                op("dve", lambda h: h.tensor_scalar(out=tmp_ap, in0=th_ap, scalar1=shift, scalar2=None, op0=ALU.add), [K + "th", K + "ki"], [K + "tmp"])
                op("dve", lambda h: h.scalar_tensor_tensor(out=tmp_ap, in0=kf_ap, scalar=-2 * PI, in1=tmp_ap, op0=ALU.mult, op1=ALU.add),
                   [K + "kf", K + "tmp"], [K + "tmp"])
                op("dve", lambda h: h.tensor_scalar(out=kf_ap, in0=tmp_ap, scalar1=PI, scalar2=None, op0=ALU.is_gt), [K + "tmp"], [K + "kf"])
                op("dve", lambda h: h.scalar_tensor_tensor(out=tmp_ap, in0=kf_ap, scalar=-2 * PI, in1=tmp_ap, op0=ALU.mult, op1=ALU.add),
                   [K + "kf", K + "tmp"], [K + "tmp"])
                op("dve", lambda h: h.tensor_scalar(out=kf_ap, in0=tmp_ap, scalar1=-PI, scalar2=None, op0=ALU.is_lt), [K + "tmp"], [K + "kf"])
                op("dve", lambda h: h.scalar_tensor_tensor(out=tmp_ap, in0=kf_ap, scalar=2 * PI, in1=tmp_ap, op0=ALU.mult, op1=ALU.add),
                   [K + "kf", K + "tmp"], [K + "tmp"])
                op("dve", lambda h: h.tensor_scalar(out=tmp_ap, in0=tmp_ap, scalar1=-PI, scalar2=PI, op0=ALU.max, op1=ALU.min), [K + "tmp"], [K + "tmp"])

            reduce_(0.0, "s")
            op("act", lambda h: h.activation(out=s_ap, in_=tmp_ap, func=AF.Sin), [K + "tmp"], [K + "s"])
            reduce_(0.5 * PI, "c")
            op("act", lambda h: h.activation(out=c_ap, in_=tmp_ap, func=AF.Sin), [K + "tmp"], [K + "c"])

        sincos("L", thl[:], s1[:], c1[:], tmpa[:], "L", kiL[:], kfL[:])
        op("dve", lambda h: h.tensor_tensor(out=are[:], in0=rho[:], in1=c1[:], op=ALU.mult), ["rho", "Lc"], ["are"])
        op("dve", lambda h: h.tensor_tensor(out=aim[:], in0=rho[:], in1=s1[:], op=ALU.mult), ["rho", "Ls"], ["aim"])
        op("pool", lambda h: h.memset(cs[:, :, 0:1], 1.0), [], ["cs"])
        op("pool", lambda h: h.memset(sn[:, :, 0:1], 0.0), [], ["sn"])
        op("dve", lambda h: h.tensor_copy(out=cs[:, :, 1], in_=c1[:]), ["Lc", "cs"], ["cs"])
        op("dve", lambda h: h.tensor_copy(out=sn[:, :, 1], in_=s1[:]), ["Ls", "sn"], ["sn"])
        tA = pp2[0][:].rearrange("p a j (b c) -> p (a j b) c", c=64); tB = big[:, 6656:7680].rearrange("p (a c) -> p a c", c=64)
        m = 1
        while m < 128:
            cm = cs[:, :, m:m + 1].broadcast_to([128, 16, m]); sm = sn[:, :, m:m + 1].broadcast_to([128, 16, m])
            a_c = cs[:, :, 1:m + 1]; a_s = sn[:, :, 1:m + 1]
            o_c = cs[:, :, m + 1:2 * m + 1]; o_s = sn[:, :, m + 1:2 * m + 1]
            ta = tA[:, :, 0:m]; tb = tB[:, :, 0:m]
            op("dve", lambda h, a_c=a_c, cm=cm, ta=ta: h.tensor_tensor(out=ta, in0=a_c, in1=cm, op=ALU.mult), ["cs", "sn"], ["tA"])
            op("dve", lambda h, a_s=a_s, sm=sm, tb=tb: h.tensor_tensor(out=tb, in0=a_s, in1=sm, op=ALU.mult), ["cs", "sn"], ["tB"])
            op("dve", lambda h, o_c=o_c, ta=ta, tb=tb: h.tensor_tensor(out=o_c, in0=ta, in1=tb, op=ALU.subtract), ["tA", "tB", "sn"], ["cs"])
            op("dve", lambda h, a_c=a_c, sm=sm, ta=ta: h.tensor_tensor(out=ta, in0=a_c, in1=sm, op=ALU.mult), ["cs", "sn"], ["tA"])
            op("dve", lambda h, a_s=a_s, cm=cm, tb=tb: h.tensor_tensor(out=tb, in0=a_s, in1=cm, op=ALU.mult), ["cs", "sn"], ["tB"])
            op("dve", lambda h, o_s=o_s, ta=ta, tb=tb: h.tensor_tensor(out=o_s, in0=ta, in1=tb, op=ALU.add), ["tA", "tB", "cs"], ["sn"])
            m *= 2
        op("pool", lambda h: h.memset(mask64[:], 1.0))
        op("pool", lambda h: h.memset(mask64[:].rearrange("p (s t) -> p s t", t=4)[:, :, 0:1], 0.0))
        a_re_b = are[:].unsqueeze(2).broadcast_to([128, 16, NSEQ]); a_im_b = aim[:].unsqueeze(2).broadcast_to([128, 16, NSEQ])
        tC = big[:, 8704:8960].rearrange("p (q s) -> p q s", q=16)
        op("dve", lambda h: h.tensor_tensor(out=ah[:, 0], in0=h0[:, 0], in1=a_re_b, op=ALU.mult), ["h0", "are"], ["ah0"])
        op("dve", lambda h: h.tensor_tensor(out=tC[:], in0=h0[:, 1], in1=a_im_b, op=ALU.mult), ["h0", "aim"], ["tC"])
        op("dve", lambda h: h.tensor_tensor(out=ah[:, 0], in0=ah[:, 0], in1=tC[:], op=ALU.subtract), ["ah0", "tC"], ["ah0"])
        op("dve", lambda h: h.tensor_tensor(out=ah[:, 1], in0=h0[:, 0], in1=a_im_b, op=ALU.mult), ["h0", "aim"], ["ah1"])
        op("dve", lambda h: h.tensor_tensor(out=tC[:], in0=h0[:, 1], in1=a_re_b, op=ALU.mult), ["h0", "are", "ah0"], ["tC"])
        op("dve", lambda h: h.tensor_tensor(out=ah[:, 1], in0=ah[:, 1], in1=tC[:], op=ALU.add), ["ah1", "tC"], ["ah1"])

        for pc in range(8):
            lb, ft, bblk, cblk = prep_views((pc % 2) * 3328)
            for i in range(3):
                P.dma("sp", lambda h, i=i, pc=pc: h.dma_start(out=lb[:, i, :], in_=lamb_d[i, pc * 256:(pc + 1) * 256].partition_broadcast(128)), writes=["lb"])
            P.dma("sp", lambda h, pc=pc: h.dma_start(out=bblk[:], in_=bblk_d[:, :, pc * 2:(pc + 1) * 2, :]), writes=["bblk"])
            P.dma("sp", lambda h, pc=pc: h.dma_start(out=cblk[:], in_=cblk_d[:, :, pc * 2:(pc + 1) * 2, :]), writes=["cblk"])
            LR, LI, LD = lb[:, 0, :], lb[:, 1, :], lb[:, 2, :]
            f0, f1, f2, f3, f4, f5 = (ft[:, i, :] for i in range(6))
            op("act", lambda h: h.activation(out=f0, in_=LD, func=AF.Exp), ["lb"], ["f0"])
            op("dve", lambda h: h.tensor_tensor(out=f1, in0=LI, in1=f0, op=ALU.mult), ["lb", "f0"], ["Rth"])
            op("dve", lambda h: h.tensor_tensor(out=f2, in0=LR, in1=f0, op=ALU.mult), ["lb", "f0"], ["f2"])
            op("act", lambda h: h.activation(out=f2, in_=f2, func=AF.Exp), ["f2"], ["f2"])
            sincos("R", f1, f3, f4, f5, "R", kiR[:], kfR[:])
            op("dve", lambda h: h.tensor_tensor(out=f4, in0=f4, in1=f2, op=ALU.mult), ["Rc", "f2"], ["Rc"])
            op("dve", lambda h: h.tensor_scalar(out=f4, in0=f4, scalar1=-1.0, scalar2=None, op0=ALU.add), ["Rc"], ["Rc"])
            op("dve", lambda h: h.tensor_tensor(out=f3, in0=f3, in1=f2, op=ALU.mult), ["Rs", "f2"], ["Rs"])
            op("dve", lambda h: h.tensor_tensor(out=f0, in0=LR, in1=LR, op=ALU.mult), ["lb", "Rth", "f2"], ["f0"])
            op("dve", lambda h: h.tensor_tensor(out=f5, in0=LI, in1=LI, op=ALU.mult), ["lb", "Rc"], ["Rtmp"])
            op("dve", lambda h: h.tensor_tensor(out=f0, in0=f0, in1=f5, op=ALU.add), ["f0", "Rtmp"], ["f0"])
            op("dve", lambda h: h.reciprocal(out=f0, in_=f0), ["f0"], ["f0"])
            op("dve", lambda h: h.tensor_tensor(out=f1, in0=f4, in1=LR, op=ALU.mult), ["Rc", "lb", "Rth", "Rtmp"], ["f1"])
            op("dve", lambda h: h.tensor_tensor(out=f5, in0=f3, in1=LI, op=ALU.mult), ["Rs", "lb"], ["Rtmp"])
            op("dve", lambda h: h.tensor_tensor(out=f1, in0=f1, in1=f5, op=ALU.add), ["f1", "Rtmp"], ["f1"])
            op("dve", lambda h: h.tensor_tensor(out=f2, in0=f3, in1=LR, op=ALU.mult), ["Rs", "lb"], ["f2"])
            op("dve", lambda h: h.tensor_tensor(out=f5, in0=f4, in1=LI, op=ALU.mult), ["Rc", "lb", "f1"], ["Rtmp"])
            op("dve", lambda h: h.tensor_tensor(out=f2, in0=f2, in1=f5, op=ALU.subtract), ["f2", "Rtmp"], ["f2"])
            op("dve", lambda h: h.tensor_tensor(out=f1, in0=f1, in1=f0, op=ALU.mult), ["f1", "f0"], ["f1"])
            op("dve", lambda h: h.tensor_tensor(out=f2, in0=f2, in1=f0, op=ALU.mult), ["f2", "f0"], ["f2"])
            Bre_, Bim_ = bblk[:, 0].rearrange("p q m -> p (q m)"), bblk[:, 1].rearrange("p q m -> p (q m)")
            op("dve", lambda h: h.tensor_tensor(out=f3, in0=Bre_, in1=f1, op=ALU.mult), ["bblk", "f1", "Rs"], ["f3"])
            op("dve", lambda h: h.tensor_tensor(out=f4, in0=Bim_, in1=f2, op=ALU.mult), ["bblk", "f2", "Rc"], ["f4"])
            op("dve", lambda h, pc=pc: h.tensor_tensor(out=WBre[:, pc * 2:(pc + 1) * 2, :].rearrange("p q m -> p (q m)"), in0=f3, in1=f4, op=ALU.subtract), ["f3", "f4"], ["WBre"])
            op("dve", lambda h: h.tensor_tensor(out=f3, in0=Bre_, in1=f2, op=ALU.mult), ["bblk", "f2", "WBre"], ["f3"])
            op("dve", lambda h: h.tensor_tensor(out=f4, in0=Bim_, in1=f1, op=ALU.mult), ["bblk", "f1", "WBre"], ["f4"])
            op("dve", lambda h, pc=pc: h.tensor_tensor(out=WBim[:, pc * 2:(pc + 1) * 2, :].rearrange("p q m -> p (q m)"), in0=f3, in1=f4, op=ALU.add), ["f3", "f4"], ["WBim"])
            op("act", lambda h, pc=pc: h.activation(out=WCre[:, pc * 2:(pc + 1) * 2, :], in_=cblk[:, 0], func=AF.Copy), ["cblk"], ["WCre"])
            op("act", lambda h, pc=pc: h.activation(out=WCimn[:, pc * 2:(pc + 1) * 2, :], in_=cblk[:, 1], func=AF.Copy, scale=-1.0), ["cblk"], ["WCimn"])

        def rms(src_ap, key_src, n, slot, scale, pn):
            op("act", lambda h: h.activation(out=junk[0:pn, 0:n], in_=src_ap, func=AF.Square, accum_out=ssq[0:pn, slot:slot + 1]),
               [key_src], ["junk", "ssq%d" % slot])
            op("act", lambda h: h.activation(out=rstd[0:pn, slot:slot + 1], in_=ssq[0:pn, slot:slot + 1], func=AF.Ln, scale=scale, bias=epsb[0:pn, 0:1]))
            op("act", lambda h: h.activation(out=rstd[0:pn, slot:slot + 1], in_=rstd[0:pn, slot:slot + 1], func=AF.Exp, scale=-0.5))

        def partA(ti, tl):
            sample = (ti == NPT)
            xt = xb[:, tl, :]
            n = 128
            ns = NS if sample else 128
            r0 = ti * 128
            par = ti % 2
            if sample:
                op("pool", lambda h: h.memset(kTs[:], 0.0))
                P.dma("pool", lambda h: h.dma_start(out=kTs[0:64, 0, 0:NSEQ * 128], in_=kcT_d[0:64, :]))
                P.dma("pool", lambda h: h.dma_start(out=kTs[64:128, 1, 0:NSEQ * 128], in_=kcT_d[64:128, :]))
                P.dma("pool", lambda h: h.dma_start(out=vs[:, 0:NSEQ, :], in_=vc_d))
                P.dma("sp", lambda h: h.dma_start(out=cvst, in_=cvst_d))
            rms(xt[:], "xt", D, 0, 1.0 / D, 128)
            op("dve", lambda h: h.tensor_scalar(out=xn[:], in0=xt[:], scalar1=rstd[:, 0:1], scalar2=None, op0=ALU.mult))
            for c in range(8):
                op("pe", lambda h, c=c: h.transpose(out=pT[:, c, :], in_=xn[:, c * 128:(c + 1) * 128], identity=idb[:]))
            op("dve", lambda h: h.tensor_tensor(out=xnT[:], in0=pT[:], in1=pcol[:, GMIX:GMIX + 8].unsqueeze(2).broadcast_to([128, 8, 128]), op=ALU.mult))
            W = wi[par]
            yield
            for i in range(4):
                bank = acc0 if i % 2 == 0 else acc1
                for c in range(8):
                    op("pe", lambda h, i=i, c=c, bank=bank: h.matmul(bank[:, 0:128], lhsT=W[:, c, i * 128:(i + 1) * 128], rhs=xnT[:, c, :],
                                                                   start=(c == 0), stop=(c == 7)))
                op("act", lambda h, i=i, bank=bank: h.activation(out=qT[:, i, :], in_=bank[:, 0:128], func=AF.Copy))
            yield
            for c in range(8):
                op("pe", lambda h, c=c: h.matmul(Ops[:, 0:128], lhsT=W[:, c, 512:640], rhs=xnT[:, c, :], start=(c == 0), stop=(c == 7)))
            if sample:
                op("dve", lambda h: h.tensor_copy(out=kTs[0:64, 0, NSEQ * 128:17 * 128], in_=Ops[0:64, 0:128]))
                op("dve", lambda h: h.tensor_copy(out=kTs[64:128, 1, NSEQ * 128:17 * 128], in_=Ops[64:128, 0:128]))
            else:
                op("dve", lambda h: h.tensor_copy(out=kTx[0:64, 0, 128:256], in_=Ops[0:64, 0:128]))
                op("dve", lambda h: h.tensor_copy(out=kTx[64:128, 1, 128:256], in_=Ops[64:128, 0:128]))
            yield
            for i in range(4):
                bank = acc0 if i % 2 == 0 else acc1
                for c in range(8):
                    op("pe", lambda h, i=i, c=c, bank=bank: h.matmul(bank[:, 0:128], lhsT=W[:, c, 768 + i * 128:768 + (i + 1) * 128], rhs=xnT[:, c, :],
                                                                   start=(c == 0), stop=(c == 7)))
                op("act", lambda h, i=i, bank=bank: h.activation(out=uT2[par][:, i, :], in_=bank[:, 0:128], func=AF.Copy))
            yield
            for c in range(8):
                op("pe", lambda h, c=c: h.matmul(Ops[:, 0:256], lhsT=xnT[:, c, :], rhs=W[:, c, 512:768], start=(c == 0), stop=(c == 7)))
            if sample:
                op("dve", lambda h: h.tensor_copy(out=vs[:, 16, :], in_=Ops[:, 128:256]))
            else:
                op("dve", lambda h: h.tensor_copy(out=vx[:, 1, :], in_=Ops[:, 128:256]))
            if sample or ti == NPT - 1:
                op("dve", lambda h: h.tensor_copy(out=kvtok[:], in_=Ops[:, 0:256]))
                if sample:
                    P.dma("sp", lambda h: h.dma_start(out=kvs_k[:, 124:128, :], in_=kvtok[0:NS, 0:128]))
                    P.dma("sp", lambda h: h.dma_start(out=kvs_v[:, 124:128, :], in_=kvtok[0:NS, 128:256]))
                else:
                    P.dma("sp", lambda h: h.dma_start(out=kvp_d, in_=kvtok[:]))

            if not sample:
                mi = 0 if ti == 0 else (1 if ti == 1 else 2)
                for i in range(4):
                    for hh_ in range(2):
                        op("pe", lambda h, i=i, hh_=hh_: h.matmul(Sps[:, hh_ * 256:(hh_ + 1) * 256], lhsT=qT[:, i, :], rhs=kTx[:, hh_, :], start=True, stop=True))
                    for hh_ in range(2):
                        op("dve", lambda h, hh_=hh_, hd=i + 4 * hh_: h.tensor_scalar(out=Sx[:, 0, hh_, 256:257], in0=sink8[:, hd:hd + 1], scalar1=8.0, scalar2=None, op0=ALU.mult))
                    op("dve", lambda h: h.tensor_tensor(out=Sx[:, 0, :, 0:256], in0=Sps[:].rearrange("p (a k) -> p a k", a=2),
                                                        in1=masks[:, mi:mi + 1, :].broadcast_to([128, 2, 256]), op=ALU.add))
                    op("dve", lambda h: h.tensor_reduce(out=mx[:], in_=Sx[:, 0], axis=AX.X, op=ALU.max))
                    op("dve", lambda h: h.tensor_scalar(out=nbias[:], in0=mx[:], scalar1=-0.125, scalar2=None, op0=ALU.mult))
                    for hh_ in range(2):
                        op("act", lambda h, hh_=hh_: h.activation(out=Pb[:, hh_, :], in_=Sx[:, 0, hh_, :], func=AF.Exp, scale=0.125,
                                                                 bias=nbias[:, hh_:hh_ + 1], accum_out=rs[:, hh_:hh_ + 1]))
                    op("dve", lambda h: h.reciprocal(out=rinv[:], in_=rs[:]))
                    for hh_ in range(2):
                        for blk in range(2):
                            op("pe", lambda h, hh_=hh_, blk=blk: h.transpose(out=pT[:, hh_ * 2 + blk, :], in_=Pb[:, hh_, blk * 128:(blk + 1) * 128], identity=idb[:]))
                    op("act", lambda h: h.activation(out=PTs[:], in_=pT[:, 0:4, :], func=AF.Copy))
                    for hh_ in range(2):
                        for blk in range(2):
                            op("pe", lambda h, hh_=hh_, blk=blk: h.matmul(Ops[:, hh_ * 64:(hh_ + 1) * 64], lhsT=PTs[:, hh_ * 2 + blk, :],
                                                                         rhs=vx[:, blk, hh_ * 64:(hh_ + 1) * 64], start=(blk == 0), stop=(blk == 1)))
                    for hh_ in range(2):
                        hd = i + 4 * hh_
                        op("dve", lambda h, hh_=hh_, hd=hd: h.tensor_scalar(out=attn[:, hd * 64:(hd + 1) * 64], in0=Ops[:, hh_ * 64:(hh_ + 1) * 64],
                                                                           scalar1=rinv[:, hh_:hh_ + 1], scalar2=None, op0=ALU.mult))
                    yield
                op("pool", lambda h: h.tensor_copy(out=kTx[:, :, 0:128], in_=kTx[:, :, 128:256]))
                op("pool", lambda h: h.tensor_copy(out=vx[:, 0, :], in_=vx[:, 1, :]))
            else:
                W17 = 17 * 128
                for hd in range(8):
                    i, hh_ = hd % 4, hd // 4
                    for cb in range(5):
                        c0 = cb * 512
                        cw = min(512, W17 - c0)
                        bank = mmA if cb % 2 == 0 else mmB
                        op("pe", lambda h, i=i, hh_=hh_, c0=c0, cw=cw, bank=bank: h.matmul(bank[:, 0:cw], lhsT=qT[:, i, :], rhs=kTs[:, hh_, c0:c0 + cw], start=True, stop=True))
                        op("dve", lambda h, c0=c0, cw=cw, bank=bank: h.tensor_tensor(out=Ssx[:, c0:c0 + cw], in0=bank[:, 0:cw], in1=smask[:, c0:c0 + cw], op=ALU.add))
                    op("dve", lambda h, hd=hd: h.tensor_scalar(out=Ssx[:, W17:W17 + 1], in0=sink8[:, hd:hd + 1], scalar1=8.0, scalar2=None, op0=ALU.mult))
                    op("dve", lambda h: h.tensor_reduce(out=mx[:, 0:1], in_=Ssx[:], axis=AX.X, op=ALU.max))
                    op("dve", lambda h: h.tensor_scalar(out=nbias[:, 0:1], in0=mx[:, 0:1], scalar1=-0.125, scalar2=None, op0=ALU.mult))
                    op("act", lambda h: h.activation(out=Psb[:], in_=Ssx[:], func=AF.Exp, scale=0.125, bias=nbias[:, 0:1], accum_out=rs[:, 0:1]))
                    op("dve", lambda h: h.reciprocal(out=rinv[:, 0:1], in_=rs[:, 0:1]))
                    for g8 in range(3):
                        nb_ = 8 if g8 < 2 else 1
                        for b in range(nb_):
                            blk = g8 * 8 + b
                            op("pe", lambda h, b=b, blk=blk: h.transpose(out=pT[:, b, :], in_=Psb[:, blk * 128:(blk + 1) * 128], identity=idb[:]))
                        op("act", lambda h, g8=g8, nb_=nb_: h.activation(out=PTss[:, g8 * 8:g8 * 8 + nb_, :], in_=pT[:, 0:nb_, :], func=AF.Copy))
                    for blk in range(17):
                        op("pe", lambda h, blk=blk, hh_=hh_: h.matmul(Ops[:, 0:64], lhsT=PTss[:, blk, :], rhs=vs[:, blk, hh_ * 64:(hh_ + 1) * 64],
                                                                     start=(blk == 0), stop=(blk == 16)))
                    op("dve", lambda h, hd=hd: h.tensor_scalar(out=attn[:, hd * 64:(hd + 1) * 64], in0=Ops[:, 0:64], scalar1=rinv[:, 0:1], scalar2=None, op0=ALU.mult))
            rms(attn[0:n, :], "attn", 512, 1, 1.0 / 512, n)
            op("dve", lambda h: h.tensor_scalar(out=anb[0:n, :], in0=attn[0:n, :], scalar1=rstd[0:n, 1:2], scalar2=None, op0=ALU.mult), ["attn", "rstd1"], ["anb"])
            for c in range(4):
                op("pe", lambda h, c=c: h.transpose(out=pT[:, c, 0:n], in_=anb[0:n, c * 128:(c + 1) * 128], identity=idb[0:n, 0:n]), ["anb", "idb"], ["pT"])
            op("dve", lambda h: h.tensor_tensor(out=mixT2[par][:, 0:4, :], in0=pT[:, 0:4, :], in1=pcol[:, GATT:GATT + 4].unsqueeze(2).broadcast_to([128, 4, 128]), op=ALU.mult))

            yield

        def partB(ti, tl):
            sample = (ti == NPT)
            xt = xb[:, tl, :]
            n = 128
            ns = NS if sample else 128
            par = ti % 2
            uT = uT2[par]
            mixT = mixT2[par]
            def ssm_views(k):
                c4, hp = k // 2, k % 2
                q0 = c4 * 4 + 2 * hp
                bi = k % 2
                Bv = Bs2[bi][:].rearrange("p (r j k) -> p r j k", r=2, j=2)
                if sample:
                    csq = cs[:, q0:q0 + 2, 1:5].unsqueeze(2).broadcast_to([128, 2, NSEQ, 4])
                    snq = sn[:, q0:q0 + 2, 1:5].unsqueeze(2).broadcast_to([128, 2, NSEQ, 4])

                    def v3(ap):
                        return ap[:, :, 0:NS].rearrange("p j (s t) -> p j s t", t=4)
                else:
                    csq = cs[:, q0:q0 + 2, 1:129]; snq = sn[:, q0:q0 + 2, 1:129]

                    def v3(ap):
                        return ap
                T = [v3(pp2[bi][:, kk]) for kk in range(4)]
                RR = [v3(rr2[bi][:, kk]) for kk in range(2)]
                VV = [v3(vv2[bi][:, kk]) for kk in range(2)]
                return c4, hp, q0, bi, Bv, csq, snq, v3, T, RR, VV

            def ssm_front(k):
                c4, hp, q0, bi, Bv, csq, snq, v3, T, RR, VV = ssm_views(k)
                bank = mmA if bi == 0 else mmC
                for ri, WBx in enumerate((WBre, WBim)):
                    for j in range(2):
                        op("pe", lambda h: h.matmul(bank[:, (ri * 2 + j) * 128:(ri * 2 + j + 1) * 128], lhsT=WBx[:, q0 + j, :], rhs=uT[:, c4, :], start=True, stop=True))
                op("act", lambda h: h.activation(out=Bs2[bi][:], in_=bank[:, :], func=AF.Copy))
                if sample:
                    for ri in range(2):
                        v4 = Bv[:, ri, :, 0:NS].rearrange("p j (s t) -> p j s t", t=4)[:, :, :, 0]
                        op("dve", lambda h: h.tensor_tensor(out=v4, in0=v4, in1=ah[:, ri, q0:q0 + 2, :], op=ALU.add))
                Bre, Bim = v3(Bv[:, 0]), v3(Bv[:, 1])
                op("dve", lambda h: h.tensor_tensor(out=T[0], in0=Bre, in1=csq, op=ALU.mult))
                op("dve", lambda h: h.tensor_tensor(out=T[1], in0=Bim, in1=snq, op=ALU.mult))
                op("dve", lambda h: h.tensor_tensor(out=T[2], in0=Bim, in1=csq, op=ALU.mult))
                op("dve", lambda h: h.tensor_tensor(out=T[3], in0=Bre, in1=snq, op=ALU.mult))
                op("dve", lambda h: h.tensor_tensor(out=RR[0], in0=T[0], in1=T[1], op=ALU.add))
                op("dve", lambda h: h.tensor_tensor(out=RR[1], in0=T[2], in1=T[3], op=ALU.subtract))

            def ssm_back(k):
                c4, hp, q0, bi, Bv, csq, snq, v3, T, RR, VV = ssm_views(k)
                rrb, vvb = rr2[bi], vv2[bi]
                for j in range(2):
                    q = q0 + j
                    if sample:
                        op("dve", lambda h: h.tensor_scalar(out=dtmp[:], in0=mask64[:], scalar1=rho[:, q:q + 1], scalar2=None, op0=ALU.mult))
                    for ri in range(2):
                        if sample:
                            op("dve", lambda h: h.tensor_tensor_scan(out=vvb[:, ri, j, 0:NS], data0=dtmp[:], data1=rrb[:, ri, j, 0:NS], initial=0.0, op0=ALU.mult, op1=ALU.add))
                        else:
                            op("dve", lambda h: h.tensor_tensor_scan(out=vvb[:, ri, j, :], data0=rho[:, q:q + 1].broadcast_to([128, 128]), data1=rrb[:, ri, j, :],
                                                                   initial=car[:, ri, q:q + 1], op0=ALU.mult, op1=ALU.add))
                op("dve", lambda h: h.tensor_tensor(out=T[0], in0=VV[0], in1=csq, op=ALU.mult))
                op("dve", lambda h: h.tensor_tensor(out=T[1], in0=VV[1], in1=snq, op=ALU.mult))
                op("dve", lambda h: h.tensor_tensor(out=T[2], in0=VV[0], in1=snq, op=ALU.mult))
                op("dve", lambda h: h.tensor_tensor(out=T[3], in0=VV[1], in1=csq, op=ALU.mult))
                hb_re, hb_im = v3(hb[:, 2 * hp:2 * hp + 2, 0, :]), v3(hb[:, 2 * hp:2 * hp + 2, 1, :])
                op("dve", lambda h: h.tensor_tensor(out=hb_re, in0=T[0], in1=T[1], op=ALU.subtract))
                op("dve", lambda h: h.tensor_tensor(out=hb_im, in0=T[2], in1=T[3], op=ALU.add))
                if sample:
                    op("dve", lambda h: h.tensor_tensor(out=hfin[:, 0, q0:q0 + 2, 0:NSEQ], in0=T[0][:, :, :, 3], in1=T[1][:, :, :, 3], op=ALU.subtract))
                    op("dve", lambda h: h.tensor_tensor(out=hfin[:, 1, q0:q0 + 2, 0:NSEQ], in0=T[2][:, :, :, 3], in1=T[3][:, :, :, 3], op=ALU.add))
                else:
                    op("dve", lambda h: h.tensor_tensor(out=car[:, 0, q0:q0 + 2], in0=T[0][:, :, 127], in1=T[1][:, :, 127], op=ALU.subtract))
                    op("dve", lambda h: h.tensor_tensor(out=car[:, 1, q0:q0 + 2], in0=T[2][:, :, 127], in1=T[3][:, :, 127], op=ALU.add))

            def ssm_y(c4):
                for qq in range(4):
                    q = c4 * 4 + qq
                    op("pe", lambda h: h.matmul(mmB[:, 0:n], lhsT=WCre[:, q, :], rhs=hb[:, qq, 0, 0:n], start=(qq == 0), stop=False))
                    op("pe", lambda h: h.matmul(mmB[:, 0:n], lhsT=WCimn[:, q, :], rhs=hb[:, qq, 1, 0:n], start=False, stop=(qq == 3)))
                op("dve", lambda h: h.scalar_tensor_tensor(out=yv[:, 0:n], in0=uT[:, c4, 0:n], scalar=col(DSK + c4), in1=mmB[:, 0:n], op0=ALU.mult, op1=ALU.add))
                op("act", lambda h: h.activation(out=gl32[:, c4, 0:n], in_=yv[:, 0:n], func=AF.Gelu))
                op("pool", lambda h: h.tensor_copy(out=glb[:, c4, 0:n], in_=gl32[:, c4, 0:n]))

            ssm_front(0)
            yield
            for k in range(8):
                if k + 1 < 8:
                    ssm_front(k + 1)
                ssm_back(k)
                if k % 2 == 1:
                    ssm_y(k // 2)
                yield
            if ti == NPT - 1:
                op("act", lambda h: h.activation(out=hfin[:, :, :, NSEQ], in_=car[:], func=AF.Copy), ["car"], ["hfin"])
            for oc in range(4):
                for c4 in range(4):
                    op("pe", lambda h, oc=oc, c4=c4: h.matmul(mmC[:, 0:n], lhsT=wgl[0][:, c4, oc * 128:(oc + 1) * 128], rhs=glb[:, c4, 0:n], start=(c4 == 0), stop=(c4 == 3)),
                       ["glb", "wgl"], ["mmC"])
                op("act", lambda h, oc=oc: h.activation(out=sg[:, 0:n], in_=mmC[:, 0:n], func=AF.Sigmoid, bias=col(BGLU + oc)), ["mmC", "pcol"], ["sg"])
                op("dve", lambda h, oc=oc: h.tensor_tensor(out=gl32[:, oc, 0:n], in0=gl32[:, oc, 0:n], in1=sg[:, 0:n], op=ALU.mult), ["gl32", "sg", "glb"], ["gl32"])
                op("act", lambda h, oc=oc: h.activation(out=sqb[:, oc, 0:n], in_=gl32[:, oc, 0:n], func=AF.Square), ["gl32"], ["sqb"])
            for oc in range(4):
                op("pe", lambda h, oc=oc: h.matmul(mmC[:, 0:n], lhsT=onesb[:], rhs=sqb[:, oc, 0:n], start=(oc == 0), stop=(oc == 3)), ["sqb", "onesb"], ["mmC"])
            op("act", lambda h: h.activation(out=rsb[:, 0:n], in_=mmC[:, 0:n], func=AF.Ln, scale=1.0 / 512, bias=epsb[:, 0:1]))
            op("act", lambda h: h.activation(out=rsb[:, 0:n], in_=rsb[:, 0:n], func=AF.Exp, scale=-0.5))
            for oc in range(4):
                op("dve", lambda h, oc=oc: h.scalar_tensor_tensor(out=mixT[:, 4 + oc, 0:n], in0=gl32[:, oc, 0:n], scalar=col(GSSM + oc), in1=rsb[:, 0:n], op0=ALU.mult, op1=ALU.mult),
                   ["gl32", "rsb", "pcol"], ["mixT"])

            for hf in range(2):
                acc, ak = (acc0, "acc0") if hf == 0 else (acc1, "acc1")
                for c in range(8):
                    op("pe", lambda h, hf=hf, c=c, acc=acc: h.matmul(acc[0:n, :], lhsT=mixT[:, c, 0:n], rhs=wo[0][:, c, hf * 512:(hf + 1) * 512], start=(c == 0), stop=(c == 7)),
                       ["mixT", "wo"], [ak])
                op("dve", lambda h, hf=hf, acc=acc: h.tensor_tensor(out=xt[0:n, hf * 512:(hf + 1) * 512], in0=xt[0:n, hf * 512:(hf + 1) * 512], in1=acc[0:n, :], op=ALU.add),
                   ["xt", ak], ["xt"])
            rms(xt[0:n, :], "xt", D, 2, 1.0 / D, n)
            op("dve", lambda h: h.tensor_scalar(out=xn[0:n, :], in0=xt[0:n, :], scalar1=rstd[0:n, 2:3], scalar2=None, op0=ALU.mult), ["xt", "rstd2"], ["xn"])
            for c in range(8):
                op("pe", lambda h, c=c: h.transpose(out=pT[:, c, 0:n], in_=xn[0:n, c * 128:(c + 1) * 128], identity=idb[0:n, 0:n]), ["xn", "idb"], ["pT"])
            op("dve", lambda h: h.tensor_tensor(out=xn2T[:, :, tl * 128:(tl + 1) * 128], in0=pT[:], in1=pcol[:, GFFN:GFFN + 8].unsqueeze(2).broadcast_to([128, 8, 128]), op=ALU.mult))

        def ffn(tiles_):
            sample = (tiles_[0] == NPT)
            ntl = len(tiles_)
            nb = ntl * 128
            hTv = hTsm if sample else hTall
            for ch in range(NFC):
                b3 = ch % 2
                w3 = ch % 3
                P.dma("sp", lambda h: h.dma_start(out=wg[w3][:], in_=wgb_d[ch]))
                P.dma("sp", lambda h: h.dma_start(out=wu[w3][:], in_=wub_d[ch]))
                gps, ups = (mmA, mmB) if b3 == 0 else (mmC, pT32)
                for c in range(8):
                    op("pe", lambda h, c=c, b3=b3, gps=gps: h.matmul(gps[:, 0:nb], lhsT=wg[w3][:, c, :], rhs=xn2T[:, c, 0:nb], start=(c == 0), stop=(c == 7)))
                for c in range(8):
                    op("pe", lambda h, c=c, b3=b3, ups=ups: h.matmul(ups[:, 0:nb], lhsT=wu[w3][:, c, :], rhs=xn2T[:, c, 0:nb], start=(c == 0), stop=(c == 7)))
                w0, w1, w2, bb = col(CVW + ch * 3), col(CVW + ch * 3 + 1), col(CVW + ch * 3 + 2), col(CVB + ch)
                cvb, slb, gxb = cv[:, 0, :], sl[:, 0, :], gx[:, b3, :]
                if sample:
                    op("pool", lambda h, ch=ch: h.tensor_copy(out=gxs[:, :, 0:2], in_=cvst[:, ch, :, :]))
                    op("act", lambda h: h.activation(out=gxs[:, :, 2:6], in_=gps[:, 0:NS].rearrange("p (s t) -> p s t", t=4), func=AF.Copy))
                    g0, g1, g2 = gxs[:, :, 0:4], gxs[:, :, 1:5], gxs[:, :, 2:6]
                    cvv = cvb[:, 0:NS].rearrange("p (s t) -> p s t", t=4)
                    op("pool", lambda h, ch=ch: h.tensor_copy(out=sconv[:, ch, :, :], in_=gxs[:, :, 4:6]))
                else:
                    op("pool", lambda h, ch=ch, gxb=gxb: h.tensor_copy(out=gxb[:, 0:2], in_=gcar[:, ch, :]))
                    op("act", lambda h, gxb=gxb: h.activation(out=gxb[:, 2:2 + nb], in_=gps[:, 0:nb], func=AF.Copy))
                    g0, g1, g2 = gxb[:, 0:nb], gxb[:, 1:1 + nb], gxb[:, 2:2 + nb]
                    cvv = cvb[:, 0:nb]
                    op("pool", lambda h, ch=ch, gxb=gxb: h.tensor_copy(out=gcar[:, ch, :], in_=gxb[:, nb:nb + 2]))
                op("dve", lambda h, g2=g2, cvv=cvv, w2=w2, bb=bb: h.tensor_scalar(out=cvv, in0=g2, scalar1=w2, scalar2=bb, op0=ALU.mult, op1=ALU.add))
                op("dve", lambda h, g1=g1, cvv=cvv, w1=w1: h.scalar_tensor_tensor(out=cvv, in0=g1, scalar=w1, in1=cvv, op0=ALU.mult, op1=ALU.add))
                op("dve", lambda h, g0=g0, cvv=cvv, w0=w0: h.scalar_tensor_tensor(out=cvv, in0=g0, scalar=w0, in1=cvv, op0=ALU.mult, op1=ALU.add))
                op("act", lambda h, cvb=cvb, slb=slb: h.activation(out=slb[:, 0:nb], in_=cvb[:, 0:nb], func=AF.Silu))
                op("dve", lambda h, ch=ch, slb=slb: h.tensor_tensor(out=hTv[:, ch, 0:nb], in0=slb[:, 0:nb], in1=ups[:, 0:nb], op=ALU.mult))
            accs = [acc0, acc1, Sps, Ops]
            for hf in range(2):
                for g in range(NFC // 2):
                    wdb = wd[g % 3]
                    P.dma("sp", lambda h: h.dma_start(out=wdb, in_=wdb_d[hf, g]))
                    for j in range(2):
                        ch = 2 * g + j
                        for tl in range(ntl):
                            op("pe", lambda h: h.matmul(accs[tl][:, :], lhsT=hTv[:, ch, tl * 128:(tl + 1) * 128], rhs=wdb[:, j, :],
                                                        start=(ch == 0), stop=(ch == NFC - 1)))
                for tl in range(ntl):
                    op("dve", lambda h, tl=tl, hf=hf: h.tensor_tensor(out=xb[:, tl, hf * 512:(hf + 1) * 512], in0=xb[:, tl, hf * 512:(hf + 1) * 512], in1=accs[tl][:, :], op=ALU.add))
            for tl, ti in enumerate(tiles_):
                xt = xb[:, tl, :]
                rms(xt, "xt", D, 3, 1.0 / D, 128)
                op("dve", lambda h, xt=xt: h.scalar_tensor_tensor(out=xt, in0=xt, scalar=rstd[:, 3:4], in1=gfb[:], op0=ALU.mult, op1=ALU.mult))
                P.dma("sp", lambda h, xt=xt, ti=ti: h.dma_start(out=y_d[ti * 128:(ti + 1) * 128, :], in_=xt))

        def est_dur(kind, eng, fn):
            rec = _Recorder()
            fn(rec)
            name, args, kwargs = rec.call
            o = kwargs.get("out", args[0] if args else None)
            nfree = 1
            if o is not None and hasattr(o, "shape"):
                for d_ in list(o.shape)[1:]:
                    nfree *= int(d_)
            if kind == "dma":
                return 100.0, 2500.0
            if eng == "dve":
                d = ((2 * nfree if name == "tensor_tensor_scan" else nfree) + 151) / 0.96
            elif eng == "act":
                d = (nfree + 230) / 1.2 + (90 if kwargs.get("accum_out") is not None else 0)
            elif eng == "pool":
                d = (2 * nfree + 250) / 1.2
            else:
                d = max(64, nfree) / 1.9 + 25
            return d, d

        def merge_threads(gens):
            gens = [g for g in gens if g is not None]
            bufs = [[] for _ in gens]
            alive = [True] * len(gens)
            tchain = [sched_now[0]] * len(gens)
            while True:
                for i, g in enumerate(gens):
                    while alive[i] and not bufs[i]:
                        P.capture = bufs[i]
                        if next(g, "done") == "done":
                            alive[i] = False
                        P.capture = None
                cands = [i for i in range(len(gens)) if bufs[i]]
                if not cands:
                    break
                best, best_t = None, None
                for i in cands:
                    kind, eng, fn = bufs[i][0]
                    t = max(eng_free.get(eng, 0.0), tchain[i] + 60.0)
                    if best is None or t < best_t:
                        best, best_t = i, t
                kind, eng, fn = bufs[best].pop(0)
                busy, lat = est_dur(kind, eng, fn)
                eng_free[eng] = best_t + busy
                tchain[best] = best_t + lat
                sched_now[0] = max(sched_now[0], best_t)
                (P.dma if kind == "dma" else P.op)(eng, fn)

        eng_free = {}
        sched_now = [0.0]
        if tiles is None:
            blocks = [[0, 1, 2, 3], [4, 5, 6, 7], [8, 9, 10, 11], [12, 13, 14, 15], [16], [NPT]]
        else:
            blocks = tiles
        for blk_ in blocks:
            for tl, ti in enumerate(blk_):
                P.dma("sp", lambda h: h.dma_start(out=xb[:, tl, :], in_=xin[ti * 128:(ti + 1) * 128, :]))
            for _ in partA(blk_[0], 0):
                pass
            for tl, ti in enumerate(blk_):
                gb = partB(ti, tl)
                ga = partA(blk_[tl + 1], tl + 1) if tl + 1 < len(blk_) else None
                merge_threads([gb, ga])
            ffn(blk_)

        P.dma("sp", lambda h: h.dma_start(out=hfin_d, in_=hfin[:]), reads=["hfin"], writes=["hfin_d"])
        P.dma("sp", lambda h: h.dma_start(out=pconv_d, in_=gcar[:]), reads=["gcar"], writes=["pconv_d"])
        P.dma("sp", lambda h: h.dma_start(out=sconv_d, in_=sconv[:]), reads=["sconv"], writes=["sconv_d"])
        P.limit = None
        P.barrier()

        with nc.Block() as block:
            @block.tensor
            def _(h):
                P.run("pe", h, sems)

            @block.scalar
            def _(h):
                P.run("act", h, sems)

            @block.vector
            def _(h):
                P.run("dve", h, sems)

            @block.gpsimd
            def _(h):
                P.run("pool", h, sems)

            @block.sync
            def _(h):
                P.run("sp", h, sems)
    return nc


def _consts():
    ident = np.eye(128, dtype=np.float32)
    i = np.arange(128)[:, None]
    c = np.arange(256)[None, :]
    full = np.where(((c < 128) & (c > i)) | ((c >= 128) & (c - 128 <= i)), 0.0, MASKV)
    m1 = np.where(((c < 128) & (c > i) & (c >= NPAD)) | ((c >= 128) & (c - 128 <= i)), 0.0, MASKV)
    m0 = np.where((c >= 128) & (c - 128 <= i) & (c - 128 >= NPAD), 0.0, MASKV)
    masks = np.stack([m0, m1, full], axis=1).astype(np.float32)
    sm = np.full((128, 17 * 128), MASKV, np.float32)
    for s in range(NSEQ):
        for t in range(4):
            r = s * 4 + t
            sm[r, s * 128 + t + 1:(s + 1) * 128] = 0.0
            sm[r, 2048 + s * 4:2048 + s * 4 + t + 1] = 0.0
    return ident, masks, sm


def prep_inputs(x_prompt, x_sample, cache_k_win, cache_v_win, state_ssm_re, state_ssm_im, state_conv,
           meta_tokens, g_mix, w_in, sinks, lam_re, lam_im, log_dt, b_re, b_im, c_re, c_im, d_skip,
           w_glu, b_glu, g_attn_out, g_ssm_out, w_o, g_ffn, w_gate, w_up, conv_w, conv_b, w_down,
           g_final):
    f32 = np.float32
    A = lambda a: np.ascontiguousarray(np.asarray(a, dtype=f32))
    x_prompt, x_sample = A(x_prompt), A(x_sample)
    ident, masks, smask = _consts()
    w_in0 = A(w_in)[0]
    perm = []
    for i in range(4):
        perm += list(range(i * 64, (i + 1) * 64)) + list(range((4 + i) * 64, (5 + i) * 64))
    w_in_p = np.ascontiguousarray(np.concatenate([w_in0[:, perm], w_in0[:, 512:]], axis=1))
    pcol = np.zeros((128, 128), f32)
    pcol[:, 0:8] = A(g_mix)[0].reshape(8, 128).T
    pcol[:, 8:16] = A(g_ffn)[0].reshape(8, 128).T
    pcol[:, 16:20] = A(g_attn_out)[0].reshape(4, 128).T
    pcol[:, 20:24] = A(g_ssm_out)[0].reshape(4, 128).T
    pcol[:, 24:28] = A(b_glu)[0].reshape(4, 128).T
    pcol[:, 28:32] = A(d_skip)[0].reshape(4, 128).T
    cw = A(conv_w)[0].reshape(3, NFC, 128)
    pcol[:, 32:98] = cw.transpose(2, 1, 0).reshape(128, 66)
    pcol[:, 98:120] = A(conv_b)[0].reshape(NFC, 128).T
    lr, li, ld = A(lam_re)[0], A(lam_im)[0], A(log_dt)[0]
    ldx = np.repeat(ld[:, None], 64, axis=1)

    def pl(a):
        return a.reshape(16, 2, 64).transpose(1, 2, 0).reshape(128, 16)

    lam = np.ascontiguousarray(np.stack([pl(lr), pl(li), pl(ldx)], axis=1))
    lamb = np.ascontiguousarray(np.stack([lr.reshape(-1), li.reshape(-1), ldx.reshape(-1)], axis=0))
    bre, bim, cre, cim = A(b_re)[0], A(b_im)[0], A(c_re)[0], A(c_im)[0]
    bblk = np.zeros((128, 2, 16, 128), f32)
    cblk = np.zeros((128, 2, 16, 128), f32)
    for q in range(16):
        for j2 in range(2):
            g = 2 * q + j2
            g8 = g % 8
            rows = slice(g8 * 16, g8 * 16 + 16)
            cols = slice(j2 * 64, j2 * 64 + 64)
            bblk[rows, 0, q, cols] = bre[g].T
            bblk[rows, 1, q, cols] = bim[g].T
            cblk[cols, 0, q, rows] = cre[g].T
            cblk[cols, 1, q, rows] = cim[g].T
    sre, sim_ = A(state_ssm_re)[0], A(state_ssm_im)[0]
    ck, cvv = A(cache_k_win)[0].reshape(128, 128, 128), A(cache_v_win)[0].reshape(128, 128, 128)
    sc = A(state_conv)[0]
    meta = A(meta_tokens)
    wg_l = np.ascontiguousarray(A(w_gate)[0].reshape(8, 128, NFC, 128).transpose(2, 1, 0, 3))
    wu_l = np.ascontiguousarray(A(w_up)[0].reshape(8, 128, NFC, 128).transpose(2, 1, 0, 3))
    wd_l = np.ascontiguousarray(A(w_down)[0].reshape(NFC // 2, 2, 128, 2, 512).transpose(3, 0, 2, 1, 4))
    in_maps = []
    for c in range(NCORES):
        xin = np.zeros((NPT * 128 + 128, D), f32)
        xin[NPAD:128] = meta
        xin[128:NPT * 128] = x_prompt[c]
        xin[NPT * 128:NPT * 128 + NS] = x_sample[c * NSEQ:(c + 1) * NSEQ].reshape(NS, D)
        sl_ = slice(c * NSEQ, (c + 1) * NSEQ)

        def hl(a):
            return a.reshape(NSEQ, 16, 2, 64).transpose(2, 3, 1, 0).reshape(128, 16, NSEQ)

        h0 = np.ascontiguousarray(np.stack([hl(sre[sl_]), hl(sim_[sl_])], axis=1))
        cvst = np.ascontiguousarray(sc[sl_].reshape(NSEQ, 2, NFC, 128).transpose(3, 2, 0, 1))
        kc, vc = ck[sl_], cvv[sl_]
        kcT = np.ascontiguousarray(kc.transpose(2, 0, 1).reshape(128, NSEQ * 128))
        vcl = np.ascontiguousarray(vc.transpose(1, 0, 2))
        in_maps.append(dict(
            xin=xin, w_in=w_in_p, w_o=A(w_o)[0], w_glu=A(w_glu)[0], w_gate=wg_l, w_up=wu_l,
            w_down=wd_l, ident=ident, masks=masks, smask=smask, pcol=pcol, sinks=A(sinks)[0],
            gfin=A(g_final), lam=lam, lamb=lamb, bblk=bblk, cblk=cblk, h0=h0, cvst=cvst, kcT=kcT, vc=vcl,
            kcache=np.ascontiguousarray(kc), vcache=np.ascontiguousarray(vc)))
    return in_maps


def assemble(R):
    f32 = np.float32
    y_prompt = np.stack([R[c]["y"][128:NPT * 128] for c in range(NCORES)])
    y_sample = np.concatenate([R[c]["y"][NPT * 128:NPT * 128 + NS].reshape(NSEQ, 4, D) for c in range(NCORES)])
    p_k = np.stack([R[c]["kvp"][:, 0:128].reshape(128, 2, 64) for c in range(NCORES)])[None]
    p_v = np.stack([R[c]["kvp"][:, 128:256].reshape(128, 2, 64) for c in range(NCORES)])[None]
    s_k = np.concatenate([R[c]["kvs_k"].reshape(NSEQ, 128, 2, 64) for c in range(NCORES)])[None]
    s_v = np.concatenate([R[c]["kvs_v"].reshape(NSEQ, 128, 2, 64) for c in range(NCORES)])[None]

    def unh(a):
        nn = a.shape[-1]
        return a.reshape(2, 64, 16, nn).transpose(3, 2, 0, 1).reshape(nn, 32, 64)

    p_re = np.stack([unh(R[c]["hfin"][:, 0, :, NSEQ:])[0] for c in range(NCORES)])[None]
    p_im = np.stack([unh(R[c]["hfin"][:, 1, :, NSEQ:])[0] for c in range(NCORES)])[None]
    s_re = np.concatenate([unh(R[c]["hfin"][:, 0, :, :NSEQ]) for c in range(NCORES)])[None]
    s_im = np.concatenate([unh(R[c]["hfin"][:, 1, :, :NSEQ]) for c in range(NCORES)])[None]
    p_conv = np.stack([R[c]["pconv"].transpose(2, 1, 0).reshape(2, FF) for c in range(NCORES)])[None]
    s_conv = np.concatenate([R[c]["sconv"].transpose(2, 3, 1, 0).reshape(NSEQ, 2, FF) for c in range(NCORES)])[None]
    out = (y_prompt, y_sample, p_k, p_v, p_re, p_im, p_conv, s_k, s_v, s_re, s_im, s_conv)
    return tuple(np.ascontiguousarray(o, dtype=f32) for o in out)


def kernel(**inputs):
    in_maps = prep_inputs(**inputs)
    nc = build_nc()
    res = run_bass_kernel_spmd(nc, in_maps, core_ids=list(range(NCORES)))
    return assemble(res.results)
```

```python
import numpy as np
from contextlib import ExitStack
import concourse.bass as bass
import concourse.mybir as mybir
from concourse.bass_utils import run_bass_kernel_spmd

F32 = mybir.dt.float32
BF16 = mybir.dt.bfloat16
ALU = mybir.AluOpType
AF = mybir.ActivationFunctionType
AX = mybir.AxisListType

ENGS = ("pe", "act", "dve", "pool", "sp")
NDMASEM = 12
NCORES = 8
D = 1024
NPT = 17
NPAD = 112
NS = 64
NSEQ = 16
FF = 2816
NFC = 22
EPS = 1e-5
PI = float(np.pi)
MASKV = -30000.0


class _Recorder:
    def __init__(self):
        self.call = None

    def __getattr__(self, name):
        def f(*args, **kwargs):
            self.call = (name, args, kwargs)
            return self
        return f


class Prog:
    def __init__(self):
        self.streams = {e: [] for e in ENGS}
        self.count = {e: 0 for e in ENGS}
        self.waited = {e: {} for e in ENGS}
        self.regs = {}
        self.nrec = 0
        self.limit = None
        self.capture = None
        self.dma_rr = {"sp": 0, "pool": 0}
        self.dma_cnt = {}

    @staticmethod
    def _region(ap):
        dims = [(int(st), int(sz)) for st, sz in ap.ap]
        off = int(ap.offset)
        esz = mybir.dt.size(ap.dtype)
        space = str(ap.space)
        if space == "DRAM":
            ext = sum((sz - 1) * abs(st) for st, sz in dims)
            return (0, 1, off, off + ext)
        pst, npart = dims[0]
        pst = max(pst, 1)
        p0, f0 = off // pst, off % pst
        ext = sum((sz - 1) * abs(st) for st, sz in dims[1:])
        f0, ext = f0 * esz, ext * esz + esz - 1
        if space == "PSUM":
            return (0, 128, 0, 1 << 30)
        return (p0, p0 + npart, f0, f0 + ext)

    @staticmethod
    def _is_ap(v):
        return hasattr(v, "tensor") and hasattr(v, "ap") and hasattr(v, "offset")

    def _record(self, fn):
        rec = _Recorder()
        fn(rec)
        name, args, kwargs = rec.call
        acc = []
        for i, a in enumerate(args):
            if self._is_ap(a):
                acc.append((a, i == 0))
        for k, v in kwargs.items():
            if self._is_ap(v):
                acc.append((v, k in ("out", "accum_out")))
        out = []
        for ap, w in acc:
            if str(ap.space) == "PSUM":
                w = True
            out.append((ap.tensor.name, self._region(ap), w))
        self._last_call = rec.call
        return out

    def _deps(self, eng, acc):
        deps = {}
        for name, R, w in acc:
            for (R2, w2), evs in self.regs.get(name, {}).items():
                if not (w or w2):
                    continue
                if R[0] < R2[1] and R2[0] < R[1] and R[2] <= R2[3] and R2[2] <= R[3]:
                    for s_, v in evs.items():
                        if s_ == eng and eng == "pe":
                            continue
                        if deps.get(s_, 0) < v:
                            deps[s_] = v
        out = []
        wd = self.waited[eng]
        for s_, v in deps.items():
            if wd.get(s_, 0) < v:
                wd[s_] = v
                out.append((s_, v))
        return out

    def _commit(self, ev, acc):
        s_, v = ev
        for name, R, w in acc:
            d = self.regs.setdefault(name, {})
            if w:
                for key in [k for k in d if k[0][0] >= R[0] and k[0][1] <= R[1] and k[0][2] >= R[2] and k[0][3] <= R[3]]:
                    del d[key]
            e = d.setdefault((R, w), {})
            if e.get(s_, 0) < v:
                e[s_] = v

    @staticmethod
    def _freeze(fn):
        rec = _Recorder()
        fn(rec)
        return lambda h, c=rec.call: getattr(h, c[0])(*c[1], **c[2])

    def op(self, eng, fn, reads=(), writes=()):
        if self.capture is not None:
            self.capture.append(("op", eng, self._freeze(fn)))
            return
        self.nrec += 1
        if self.limit is not None and self.nrec > self.limit:
            return
        acc = self._record(fn)
        deps = self._deps(eng, acc)
        self.count[eng] += 1
        ev = (eng, self.count[eng])
        self.streams[eng].append((deps, self._last_call, (eng, 1)))
        self._commit(ev, acc)

    def dma(self, eng, fn, reads=(), writes=()):
        if self.capture is not None:
            self.capture.append(("dma", eng, self._freeze(fn)))
            return
        self.nrec += 1
        if self.limit is not None and self.nrec > self.limit:
            return
        acc = self._record(fn)
        k = self.dma_rr[eng]
        self.dma_rr[eng] = (k + 1) % NDMASEM
        sname = "dma%s%d" % (eng, k)
        deps = self._deps(eng, acc)
        prev = self.dma_cnt.get(sname, 0) * 16
        if prev and self.waited[eng].get(sname, 0) < prev:
            self.waited[eng][sname] = prev
            deps.append((sname, prev))
        self.dma_cnt[sname] = self.dma_cnt.get(sname, 0) + 1
        ev = (sname, self.dma_cnt[sname] * 16)
        self.streams[eng].append((deps, self._last_call, (sname, 16)))
        self._commit(ev, acc)

    def barrier(self):
        evs = [(e, self.count[e]) for e in ENGS if self.count[e]]
        evs += [(k, c * 16) for k, c in self.dma_cnt.items()]
        for e in ENGS:
            deps = []
            for s, v in evs:
                if s == e:
                    continue
                if self.waited[e].get(s, 0) < v:
                    self.waited[e][s] = v
                    deps.append((s, v))
            if deps:
                self.streams[e].append((deps, None, None))

    def run(self, eng, h, sems):
        for deps, fn, inc in self.streams[eng]:
            for s, v in deps:
                h.wait_ge(sems[s], v)
            if fn is not None:
                name, args, kwargs = fn
                getattr(h, name)(*args, **kwargs).then_inc(sems[inc[0]], inc[1])


def build_nc(tiles=None, limit=None):
    nc = bass.Bass("TRN2", target_bir_lowering=False)

    def din(name, shape, dt=F32):
        return nc.dram_tensor(name, list(shape), dt, kind="ExternalInput").ap()

    def dout(name, shape, dt=F32):
        return nc.dram_tensor(name, list(shape), dt, kind="ExternalOutput").ap()

    xin = din("xin", [NPT * 128 + 128, D])
    w_in = din("w_in", [D, 1280])
    w_o = din("w_o", [D, D])
    w_glu = din("w_glu", [512, 512])
    w_gate = din("w_gate", [NFC, 128, 8, 128])
    w_up = din("w_up", [NFC, 128, 8, 128])
    w_down = din("w_down", [2, NFC // 2, 128, 2, 512])
    ident_d = din("ident", [128, 128])
    masks_d = din("masks", [128, 3, 256])
    smask_d = din("smask", [128, 17 * 128])
    pcol_d = din("pcol", [128, 128])
    sinks_d = din("sinks", [8])
    gfin_d = din("gfin", [D])
    lam_d = din("lam", [128, 3, 16])
    lamb_d = din("lamb", [3, 16 * 128])
    bblk_d = din("bblk", [128, 2, 16, 128])
    cblk_d = din("cblk", [128, 2, 16, 128])
    h0_d = din("h0", [128, 2, 16, NSEQ])
    cvst_d = din("cvst", [128, NFC, NSEQ, 2])
    kcT_d = din("kcT", [128, NSEQ * 128])
    vc_d = din("vc", [128, NSEQ, 128])
    kcache_d = din("kcache", [NSEQ, 128, 128])
    vcache_d = din("vcache", [NSEQ, 128, 128])

    wgb_d = nc.dram_tensor("wgb", [NFC, 128, 8, 128], BF16, kind="Internal").ap()
    wub_d = nc.dram_tensor("wub", [NFC, 128, 8, 128], BF16, kind="Internal").ap()
    wdb_d = nc.dram_tensor("wdb", [2, NFC // 2, 128, 2, 512], BF16, kind="Internal").ap()
    y_d = dout("y", [NPT * 128 + 128, D])
    kvp_d = dout("kvp", [128, 256])
    kvs_k = dout("kvs_k", [NSEQ, 128, 128])
    kvs_v = dout("kvs_v", [NSEQ, 128, 128])
    hfin_d = dout("hfin", [128, 2, 16, NSEQ + 1])
    pconv_d = dout("pconv", [128, NFC, 2])
    sconv_d = dout("sconv", [128, NFC, NSEQ, 2])

    P = Prog()
    P.limit = limit
    es = ExitStack()
    with es:
        def sb(name, shape, dt=F32):
            return es.enter_context(nc.sbuf_tensor(name, list(shape), dt))

        def ps(name, shape, dt=F32):
            return es.enter_context(nc.psum_tensor(name, list(shape), dt))

        sems = {e: es.enter_context(nc.semaphore("s_" + e)) for e in ENGS}
        for k in range(NDMASEM):
            for e_ in ("sp", "pool"):
                sems["dma%s%d" % (e_, k)] = es.enter_context(nc.semaphore("s_dma%s%d" % (e_, k)))

        idb = sb("idb", [128, 128], BF16)
        onesb = sb("onesb", [128, 128], BF16)
        masks = sb("masks_s", [128, 3, 256])
        smask = sb("smask_s", [128, 17 * 128], BF16)
        pcol = sb("pcol_s", [128, 128])
        sink8 = sb("sink8", [128, 8])
        gfb = sb("gfb", [128, D])
        epsb = sb("epsb", [128, 1])
        lam = sb("lam_s", [128, 3, 16])
        WBre = sb("WBre", [128, 16, 128], BF16); WBim = sb("WBim", [128, 16, 128], BF16)
        WCre = sb("WCre", [128, 16, 128], BF16); WCimn = sb("WCimn", [128, 16, 128], BF16)
        cs = sb("cs", [128, 16, 129]); sn = sb("sn", [128, 16, 129])
        mask64 = sb("mask64", [128, NS]); dtmp = sb("dtmp", [128, NS])
        are = sb("are", [128, 16]); aim = sb("aim", [128, 16])
        ah = sb("ah", [128, 2, 16, NSEQ])
        car = sb("car", [128, 2, 16])
        hfin = sb("hfin_s", [128, 2, 16, NSEQ + 1])
        gcar = sb("gcar", [128, NFC, 2])
        sconv = sb("sconv_s", [128, NFC, NSEQ, 2])
        kTx = sb("kTx", [128, 2, 256], BF16)
        vx = sb("vx", [128, 2, 128], BF16)
        GMIX, GFFN, GATT, GSSM, BGLU, DSK, CVW, CVB = 0, 8, 16, 20, 24, 28, 32, 98

        def col(c):
            return pcol[:, c:c + 1]

        ssq = sb("ssq", [128, 4]); rstd = sb("rstd", [128, 4])
        xn = sb("xn", [128, D], BF16)
        junk = xn
        xnT = sb("xnT", [128, 8, 128], BF16)
        qT = sb("qT", [128, 4, 128], BF16)
        uT2 = [sb("uT%d" % i, [128, 4, 128], BF16) for i in range(2)]
        Sx = sb("Sx", [128, 1, 2, 257]); mx = sb("mx", [128, 2]); nbias = sb("nbias", [128, 2])
        rs = sb("rs", [128, 2]); rinv = sb("rinv", [128, 2])
        Pb = sb("Pb", [128, 2, 257], BF16); PTs = sb("PTs", [128, 4, 128], BF16)
        mixT2 = [sb("mixT%d" % i, [128, 8, 128], BF16) for i in range(2)]
        mixT = mixT2[0]
        Psb = sb("Psb", [128, 17 * 128 + 1], BF16)
        PTss = sb("PTss", [128, 17, 128], BF16)
        Bs2 = [sb("Bs%d" % i, [128, 512]) for i in range(2)]
        pp2 = [sb("pp%d" % i, [128, 4, 2, 128]) for i in range(2)]
        rr2 = [sb("rr%d" % i, [128, 2, 2, 128]) for i in range(2)]
        vv2 = [sb("vv%d" % i, [128, 2, 2, 128]) for i in range(2)]
        hb = sb("hb", [128, 4, 2, 128], BF16)
        gl32 = sb("gl32", [128, 4, 128]); glb = sb("glb", [128, 4, 128], BF16)
        xn2T = sb("xn2T", [128, 8, 512], BF16)
        gx = sb("gx", [128, 2, 514]); gxs = sb("gxs", [128, NSEQ, 6])
        cv = sb("cv", [128, 1, 512]); sl = sb("sl", [128, 1, 512])
        kvtok = sl[:, 0, 0:256]
        yv = gx[:, 0, 0:128]; sg = gx[:, 0, 128:256]; rsb = gx[:, 0, 256:384]
        sqb = gx[:, 1, 0:256].bitcast(BF16).rearrange("p (a k) -> p a k", a=4)
        attn = cv[:, 0, :]
        anb = sl[:, 0, 256:512].bitcast(BF16)
        wi = [sb("wi0", [128, 8, 1280], BF16)] * 2
        wo = [sb("wo0", [128, 8, D], BF16)] * 2
        wgl = [sb("wgl0", [128, 4, 512], BF16)] * 2
        wg = [sb("wg%d" % i, [128, 8, 128], BF16) for i in range(2)] + [xnT]
        wu = [sb("wu%d" % i, [128, 8, 128], BF16) for i in range(2)] + [mixT]
        wdA = sb("wdA", [128, 2, 512], BF16)
        wd = [wdA[:], Sx[:].rearrange("p a b c -> p (a b c)").bitcast(BF16)[:, 0:1024].rearrange("p (j n) -> p j n", j=2),
              xn[:].rearrange("p (j n) -> p j n", j=2)]
        big = sb("big", [128, 9728])
        xb = big[:, 0:4096].rearrange("p (t d) -> p t d", t=4)
        hTall = big[:, 4096:9728].bitcast(BF16).rearrange("p (c n) -> p c n", c=NFC)
        hTsm = big[:, 4096:5504].bitcast(BF16).rearrange("p (c n) -> p c n", c=NFC)
        kTs = big[:, 5504:7680].bitcast(BF16).rearrange("p (a k) -> p a k", a=2)
        vs = big[:, 7680:8768].bitcast(BF16).rearrange("p (b k) -> p b k", b=17)
        scr = big
        h0 = big[:, 8192:8704].rearrange("p (a q s) -> p a q s", a=2, q=16)
        cvst = big[:, 8768:9472].rearrange("p (c s j) -> p c s j", c=NFC, s=NSEQ)
        def prep_views(o):
            return (scr[:, o:o + 768].rearrange("p (a b) -> p a b", a=3), scr[:, o + 768:o + 2304].rearrange("p (a b) -> p a b", a=6),
                    scr[:, o + 2304:o + 2816].rearrange("p (a q m) -> p a q m", a=2, q=2), scr[:, o + 2816:o + 3328].rearrange("p (a q m) -> p a q m", a=2, q=2))
        Ssx = big[:, 1024:1024 + 17 * 128 + 1]
        idf = big[:, 8960:9088]

        mmA = ps("mmA", [128, 512]); mmB = ps("mmB", [128, 512]); mmC = ps("mmC", [128, 512])
        pT = ps("pT", [128, 8, 128], BF16)
        pT32 = pT[:].rearrange("p c k -> p (c k)").bitcast(F32)
        Sps = ps("Sps", [128, 512]); Ops = ps("Ops", [128, 512])
        acc0 = ps("acc0", [128, 512]); acc1 = ps("acc1", [128, 512])

        op = P.op

        P.dma("sp", lambda h: h.dma_start(out=idf[:], in_=ident_d), writes=["idf"])
        P.dma("sp", lambda h: h.dma_start(out=masks[:], in_=masks_d), writes=["masks"])
        P.dma("pool", lambda h: h.dma_start(out=smask[:], in_=smask_d), writes=["smask"])
        P.dma("sp", lambda h: h.dma_start(out=pcol[:], in_=pcol_d), writes=["pcol"])
        P.dma("sp", lambda h: h.dma_start(out=sink8[:], in_=sinks_d.partition_broadcast(128)), writes=["sink8"])
        P.dma("sp", lambda h: h.dma_start(out=gfb[:], in_=gfin_d.partition_broadcast(128)), writes=["gfb"])
        P.dma("sp", lambda h: h.dma_start(out=lam[:], in_=lam_d), writes=["lam"])
        P.dma("sp", lambda h: h.dma_start(out=h0, in_=h0_d))
        op("pool", lambda h: h.memset(hb[:], 0.0))
        op("pool", lambda h: h.memset(cv[:], 0.0))
        op("pool", lambda h: h.memset(gx[:], 0.0))
        P.dma("pool", lambda h: h.dma_start(out=wi[0][:], in_=w_in.rearrange("(c p) n -> p c n", p=128)))
        P.dma("pool", lambda h: h.dma_start(out=wgl[0][:], in_=w_glu.rearrange("(c p) n -> p c n", p=128)))
        P.dma("pool", lambda h: h.dma_start(out=wo[0][:], in_=w_o.rearrange("(c p) n -> p c n", p=128)))
        for a_ in range(0, NFC, 11):
            P.dma("pool", lambda h: h.dma_start(out=wgb_d[a_:a_ + 11], in_=w_gate[a_:a_ + 11]))
            P.dma("pool", lambda h: h.dma_start(out=wub_d[a_:a_ + 11], in_=w_up[a_:a_ + 11]))
        for hf_ in range(2):
            P.dma("pool", lambda h: h.dma_start(out=wdb_d[hf_], in_=w_down[hf_]))
        P.dma("sp", lambda h: h.dma_start(out=kvs_k[:, 0:124, :], in_=kcache_d[:, 4:128, :]), writes=["kvs_k_a"])
        P.dma("sp", lambda h: h.dma_start(out=kvs_v[:, 0:124, :], in_=vcache_d[:, 4:128, :]), writes=["kvs_v_a"])

        op("dve", lambda h: h.tensor_copy(out=idb[:], in_=idf[:]), ["idf"], ["idb"])
        op("pool", lambda h: h.memset(onesb[:], 1.0), [], ["onesb"])
        op("pool", lambda h: h.memset(epsb[:], EPS), [], ["epsb"])
        op("pool", lambda h: h.memset(kTx[:], 0.0), [], ["kTx"])
        op("pool", lambda h: h.memset(vx[:], 0.0), [], ["vx"])
        op("pool", lambda h: h.memset(car[:], 0.0), [], ["car"])
        op("pool", lambda h: h.memset(gcar[:], 0.0), [], ["gcar"])
        pass
        op("pool", lambda h: h.memset(hfin[:], 0.0))
        op("pool", lambda h: h.memset(sconv[:], 0.0))
        rho = sb("rho", [128, 16]); fre = sb("fre", [128, 16]); fim = sb("fim", [128, 16])
        dtl, thl, c1, s1, tmpa, kfL = (big[:, 9088 + 16 * i:9104 + 16 * i] for i in range(6))
        kiL = big[:, 9184:9200].bitcast(mybir.dt.int32)
        op("act", lambda h: h.activation(out=dtl[:], in_=lam[:, 2, :], func=AF.Exp), ["lam"], ["dtl"])
        op("dve", lambda h: h.tensor_tensor(out=thl[:], in0=lam[:, 1, :], in1=dtl[:], op=ALU.mult), ["lam", "dtl"], ["thl"])
        op("dve", lambda h: h.tensor_tensor(out=tmpa[:], in0=lam[:, 0, :], in1=dtl[:], op=ALU.mult), ["lam", "dtl"], ["tmpa"])
        op("act", lambda h: h.activation(out=rho[:], in_=tmpa[:], func=AF.Exp), ["tmpa"], ["rho"])

        def sincos(eng_tag, th_ap, s_ap, c_ap, tmp_ap, shape_key, ki_ap, kf_ap):
            K = shape_key

            def reduce_(shift, dst_key):
                op("dve", lambda h: h.tensor_scalar(out=tmp_ap, in0=th_ap, scalar1=shift, scalar2=1.0 / (2 * PI), op0=ALU.add, op1=ALU.mult),
                   [K + "th", K + "s", K + "c"], [K + "tmp"])
                op("dve", lambda h: h.tensor_copy(out=ki_ap, in_=tmp_ap), [K + "tmp"], [K + "ki"])
                op("dve", lambda h: h.tensor_copy(out=kf_ap, in_=ki_ap), [K + "ki"], [K + "kf"])
                op("dve", lambda h: h.tensor_scalar(out=tmp_ap, in0=th_ap, scalar1=shift, scalar2=None, op0=ALU.add), [K + "th", K + "ki"], [K + "tmp"])
                op("dve", lambda h: h.scalar_tensor_tensor(out=tmp_ap, in0=kf_ap, scalar=-2 * PI, in1=tmp_ap, op0=ALU.mult, op1=ALU.add),
                   [K + "kf", K + "tmp"], [K + "tmp"])
                op("dve", lambda h: h.tensor_scalar(out=kf_ap, in0=tmp_ap, scalar1=PI, scalar2=None, op0=ALU.is_gt), [K + "tmp"], [K + "kf"])
                op("dve", lambda h: h.scalar_tensor_tensor(out=tmp_ap, in0=kf_ap, scalar=-2 * PI, in1=tmp_ap, op0=ALU.mult, op1=ALU.add),
                   [K + "kf", K + "tmp"], [K + "tmp"])
                op("dve", lambda h: h.tensor_scalar(out=kf_ap, in0=tmp_ap, scalar1=-PI, scalar2=None, op0=ALU.is_lt), [K + "tmp"], [K + "kf"])
                op("dve", lambda h: h.scalar_tensor_tensor(out=tmp_ap, in0=kf_ap, scalar=2 * PI, in1=tmp_ap, op0=ALU.mult, op1=ALU.add),
                   [K + "kf", K + "tmp"], [K + "tmp"])
                op("dve", lambda h: h.tensor_scalar(out=tmp_ap, in0=tmp_ap, scalar1=-PI, scalar2=PI, op0=ALU.max, op1=ALU.min), [K + "tmp"], [K + "tmp"])

            reduce_(0.0, "s")
            op("act", lambda h: h.activation(out=s_ap, in_=tmp_ap, func=AF.Sin), [K + "tmp"], [K + "s"])
            reduce_(0.5 * PI, "c")
            op("act", lambda h: h.activation(out=c_ap, in_=tmp_ap, func=AF.Sin), [K + "tmp"], [K + "c"])

        sincos("L", thl[:], s1[:], c1[:], tmpa[:], "L", kiL[:], kfL[:])
        op("dve", lambda h: h.tensor_tensor(out=are[:], in0=rho[:], in1=c1[:], op=ALU.mult), ["rho", "Lc"], ["are"])
        op("dve", lambda h: h.tensor_tensor(out=aim[:], in0=rho[:], in1=s1[:], op=ALU.mult), ["rho", "Ls"], ["aim"])
        op("pool", lambda h: h.memset(cs[:, :, 0:1], 1.0), [], ["cs"])
        op("pool", lambda h: h.memset(sn[:, :, 0:1], 0.0), [], ["sn"])
        op("dve", lambda h: h.tensor_copy(out=cs[:, :, 1], in_=c1[:]), ["Lc", "cs"], ["cs"])
        op("dve", lambda h: h.tensor_copy(out=sn[:, :, 1], in_=s1[:]), ["Ls", "sn"], ["sn"])
        tA = pp2[0][:].rearrange("p a j (b c) -> p (a j b) c", c=64); tB = big[:, 6656:7680].rearrange("p (a c) -> p a c", c=64)
        m = 1
        while m < 128:
            cm = cs[:, :, m:m + 1].broadcast_to([128, 16, m]); sm = sn[:, :, m:m + 1].broadcast_to([128, 16, m])
            a_c = cs[:, :, 1:m + 1]; a_s = sn[:, :, 1:m + 1]
            o_c = cs[:, :, m + 1:2 * m + 1]; o_s = sn[:, :, m + 1:2 * m + 1]
            ta = tA[:, :, 0:m]; tb = tB[:, :, 0:m]
            op("dve", lambda h, a_c=a_c, cm=cm, ta=ta: h.tensor_tensor(out=ta, in0=a_c, in1=cm, op=ALU.mult), ["cs", "sn"], ["tA"])
            op("dve", lambda h, a_s=a_s, sm=sm, tb=tb: h.tensor_tensor(out=tb, in0=a_s, in1=sm, op=ALU.mult), ["cs", "sn"], ["tB"])
            op("dve", lambda h, o_c=o_c, ta=ta, tb=tb: h.tensor_tensor(out=o_c, in0=ta, in1=tb, op=ALU.subtract), ["tA", "tB", "sn"], ["cs"])
            op("dve", lambda h, a_c=a_c, sm=sm, ta=ta: h.tensor_tensor(out=ta, in0=a_c, in1=sm, op=ALU.mult), ["cs", "sn"], ["tA"])
            op("dve", lambda h, a_s=a_s, cm=cm, tb=tb: h.tensor_tensor(out=tb, in0=a_s, in1=cm, op=ALU.mult), ["cs", "sn"], ["tB"])
            op("dve", lambda h, o_s=o_s, ta=ta, tb=tb: h.tensor_tensor(out=o_s, in0=ta, in1=tb, op=ALU.add), ["tA", "tB", "cs"], ["sn"])
            m *= 2
        op("pool", lambda h: h.memset(mask64[:], 1.0))
        op("pool", lambda h: h.memset(mask64[:].rearrange("p (s t) -> p s t", t=4)[:, :, 0:1], 0.0))
        sm_ = [big[:, 9200 + 16 * i:9216 + 16 * i] for i in range(8)]
        lr_, li_ = lam[:, 0, :], lam[:, 1, :]
        nr, den, t_a, t_b, gr, gi = sm_[0], sm_[1], sm_[2], sm_[3], sm_[4], sm_[5]
        TT = lambda o, x, y, f_: op("dve", lambda h: h.tensor_tensor(out=o, in0=x, in1=y, op=f_))
        op("dve", lambda h: h.tensor_scalar(out=nr, in0=are[:], scalar1=-1.0, scalar2=None, op0=ALU.add))
        TT(den, lr_, lr_, ALU.mult); TT(t_a, li_, li_, ALU.mult); TT(den, den, t_a, ALU.add)
        op("dve", lambda h: h.reciprocal(out=den, in_=den))
        TT(t_a, nr, lr_, ALU.mult); TT(t_b, aim[:], li_, ALU.mult); TT(t_a, t_a, t_b, ALU.add); TT(fre[:], t_a, den, ALU.mult)
        TT(t_a, aim[:], lr_, ALU.mult); TT(t_b, nr, li_, ALU.mult); TT(t_a, t_a, t_b, ALU.subtract); TT(fim[:], t_a, den, ALU.mult)
        TT(den, fre[:], fre[:], ALU.mult); TT(t_a, fim[:], fim[:], ALU.mult); TT(den, den, t_a, ALU.add)
        op("dve", lambda h: h.reciprocal(out=den, in_=den))
        TT(gr, fre[:], den, ALU.mult); TT(gi, fim[:], den, ALU.mult)
        op("dve", lambda h: h.tensor_scalar(out=gi, in0=gi, scalar1=-1.0, scalar2=None, op0=ALU.mult))
        P.dma("pool", lambda h: h.dma_start(out=WBre[:], in_=bblk_d[:, 0]))
        P.dma("pool", lambda h: h.dma_start(out=WBim[:], in_=bblk_d[:, 1]))
        cbf = big[:, 0:4096].rearrange("p (a q m) -> p a q m", a=2, q=16)
        u1 = big[:, 4096:6144].rearrange("p (q m) -> p q m", q=16); u2 = big[:, 6144:8192].rearrange("p (q m) -> p q m", q=16)
        P.dma("sp", lambda h: h.dma_start(out=cbf, in_=cblk_d))
        fre_b = fre[:].unsqueeze(2).broadcast_to([128, 16, 128]); fim_b = fim[:].unsqueeze(2).broadcast_to([128, 16, 128])
        TT(u1, cbf[:, 0], fre_b, ALU.mult); TT(u2, cbf[:, 1], fim_b, ALU.mult); TT(WCre[:], u1, u2, ALU.subtract)
        TT(u1, cbf[:, 0], fim_b, ALU.mult); TT(u2, cbf[:, 1], fre_b, ALU.mult); TT(u1, u1, u2, ALU.add)
        op("dve", lambda h: h.tensor_scalar(out=WCimn[:], in0=u1, scalar1=-1.0, scalar2=None, op0=ALU.mult))

        tD = big[:, 9344:9600].rearrange("p (q s) -> p q s", q=16)
        tE = big[:, 8704:8960].rearrange("p (q s) -> p q s", q=16)
        gr_b = gr.unsqueeze(2).broadcast_to([128, 16, NSEQ]); gi_b = gi.unsqueeze(2).broadcast_to([128, 16, NSEQ])
        TT(tD, h0[:, 0], gr_b, ALU.mult); TT(tE, h0[:, 1], gi_b, ALU.mult); TT(tD, tD, tE, ALU.subtract)
        TT(tE, h0[:, 0], gi_b, ALU.mult); TT(h0[:, 0], tD, tD, ALU.max)
        TT(tD, h0[:, 1], gr_b, ALU.mult); TT(h0[:, 1], tE, tD, ALU.add)
        a_re_b = are[:].unsqueeze(2).broadcast_to([128, 16, NSEQ]); a_im_b = aim[:].unsqueeze(2).broadcast_to([128, 16, NSEQ])
        tC = big[:, 8704:8960].rearrange("p (q s) -> p q s", q=16)
        op("dve", lambda h: h.tensor_tensor(out=ah[:, 0], in0=h0[:, 0], in1=a_re_b, op=ALU.mult), ["h0", "are"], ["ah0"])
        op("dve", lambda h: h.tensor_tensor(out=tC[:], in0=h0[:, 1], in1=a_im_b, op=ALU.mult), ["h0", "aim"], ["tC"])
        op("dve", lambda h: h.tensor_tensor(out=ah[:, 0], in0=ah[:, 0], in1=tC[:], op=ALU.subtract), ["ah0", "tC"], ["ah0"])
        op("dve", lambda h: h.tensor_tensor(out=ah[:, 1], in0=h0[:, 0], in1=a_im_b, op=ALU.mult), ["h0", "aim"], ["ah1"])
        op("dve", lambda h: h.tensor_tensor(out=tC[:], in0=h0[:, 1], in1=a_re_b, op=ALU.mult), ["h0", "are", "ah0"], ["tC"])
        op("dve", lambda h: h.tensor_tensor(out=ah[:, 1], in0=ah[:, 1], in1=tC[:], op=ALU.add), ["ah1", "tC"], ["ah1"])


        def rms(src_ap, key_src, n, slot, scale, pn):
            op("act", lambda h: h.activation(out=junk[0:pn, 0:n], in_=src_ap, func=AF.Square, accum_out=ssq[0:pn, slot:slot + 1]),
               [key_src], ["junk", "ssq%d" % slot])
            op("act", lambda h: h.activation(out=rstd[0:pn, slot:slot + 1], in_=ssq[0:pn, slot:slot + 1], func=AF.Ln, scale=scale, bias=epsb[0:pn, 0:1]))
            op("act", lambda h: h.activation(out=rstd[0:pn, slot:slot + 1], in_=rstd[0:pn, slot:slot + 1], func=AF.Exp, scale=-0.5))

        def partA(ti, tl):
            sample = (ti == NPT)
            xt = xb[:, tl, :]
            n = 128
            ns = NS if sample else 128
            r0 = ti * 128
            par = ti % 2
            if sample:
                op("pool", lambda h: h.memset(kTs[:], 0.0))
                P.dma("pool", lambda h: h.dma_start(out=kTs[0:64, 0, 0:NSEQ * 128], in_=kcT_d[0:64, :]))
                P.dma("pool", lambda h: h.dma_start(out=kTs[64:128, 1, 0:NSEQ * 128], in_=kcT_d[64:128, :]))
                P.dma("pool", lambda h: h.dma_start(out=vs[:, 0:NSEQ, :], in_=vc_d))
                P.dma("sp", lambda h: h.dma_start(out=cvst, in_=cvst_d))
            rms(xt[:], "xt", D, 0, 1.0 / D, 128)
            op("dve", lambda h: h.tensor_scalar(out=xn[:], in0=xt[:], scalar1=rstd[:, 0:1], scalar2=None, op0=ALU.mult))
            for c in range(8):
                op("pe", lambda h, c=c: h.transpose(out=pT[:, c, :], in_=xn[:, c * 128:(c + 1) * 128], identity=idb[:]))
            op("dve", lambda h: h.tensor_tensor(out=xnT[:], in0=pT[:], in1=pcol[:, GMIX:GMIX + 8].unsqueeze(2).broadcast_to([128, 8, 128]), op=ALU.mult))
            W = wi[par]
            yield
            for i in range(4):
                bank = acc0 if i % 2 == 0 else acc1
                for c in range(8):
                    op("pe", lambda h, i=i, c=c, bank=bank: h.matmul(bank[:, 0:128], lhsT=W[:, c, i * 128:(i + 1) * 128], rhs=xnT[:, c, :],
                                                                   start=(c == 0), stop=(c == 7)))
                op("act", lambda h, i=i, bank=bank: h.activation(out=qT[:, i, :], in_=bank[:, 0:128], func=AF.Copy))
            yield
            for c in range(8):
                op("pe", lambda h, c=c: h.matmul(Ops[:, 0:128], lhsT=W[:, c, 512:640], rhs=xnT[:, c, :], start=(c == 0), stop=(c == 7)))
            if sample:
                op("dve", lambda h: h.tensor_copy(out=kTs[0:64, 0, NSEQ * 128:17 * 128], in_=Ops[0:64, 0:128]))
                op("dve", lambda h: h.tensor_copy(out=kTs[64:128, 1, NSEQ * 128:17 * 128], in_=Ops[64:128, 0:128]))
            else:
                op("dve", lambda h: h.tensor_copy(out=kTx[0:64, 0, 128:256], in_=Ops[0:64, 0:128]))
                op("dve", lambda h: h.tensor_copy(out=kTx[64:128, 1, 128:256], in_=Ops[64:128, 0:128]))
            yield
            for i in range(4):
                bank = acc0 if i % 2 == 0 else acc1
                for c in range(8):
                    op("pe", lambda h, i=i, c=c, bank=bank: h.matmul(bank[:, 0:128], lhsT=W[:, c, 768 + i * 128:768 + (i + 1) * 128], rhs=xnT[:, c, :],
                                                                   start=(c == 0), stop=(c == 7)))
                op("act", lambda h, i=i, bank=bank: h.activation(out=uT2[par][:, i, :], in_=bank[:, 0:128], func=AF.Copy))
            yield
            for c in range(8):
                op("pe", lambda h, c=c: h.matmul(Ops[:, 0:256], lhsT=xnT[:, c, :], rhs=W[:, c, 512:768], start=(c == 0), stop=(c == 7)))
            if sample:
                op("dve", lambda h: h.tensor_copy(out=vs[:, 16, :], in_=Ops[:, 128:256]))
            else:
                op("dve", lambda h: h.tensor_copy(out=vx[:, 1, :], in_=Ops[:, 128:256]))
            if sample or ti == NPT - 1:
                op("dve", lambda h: h.tensor_copy(out=kvtok[:], in_=Ops[:, 0:256]))
                if sample:
                    P.dma("sp", lambda h: h.dma_start(out=kvs_k[:, 124:128, :], in_=kvtok[0:NS, 0:128]))
                    P.dma("sp", lambda h: h.dma_start(out=kvs_v[:, 124:128, :], in_=kvtok[0:NS, 128:256]))
                else:
                    P.dma("sp", lambda h: h.dma_start(out=kvp_d, in_=kvtok[:]))

            if not sample:
                mi = 0 if ti == 0 else (1 if ti == 1 else 2)
                for i in range(4):
                    for hh_ in range(2):
                        op("pe", lambda h, i=i, hh_=hh_: h.matmul(Sps[:, hh_ * 256:(hh_ + 1) * 256], lhsT=qT[:, i, :], rhs=kTx[:, hh_, :], start=True, stop=True))
                    for hh_ in range(2):
                        op("dve", lambda h, hh_=hh_, hd=i + 4 * hh_: h.tensor_scalar(out=Sx[:, 0, hh_, 256:257], in0=sink8[:, hd:hd + 1], scalar1=8.0, scalar2=None, op0=ALU.mult))
                    op("dve", lambda h: h.tensor_tensor(out=Sx[:, 0, :, 0:256], in0=Sps[:].rearrange("p (a k) -> p a k", a=2),
                                                        in1=masks[:, mi:mi + 1, :].broadcast_to([128, 2, 256]), op=ALU.add))
                    op("dve", lambda h: h.tensor_reduce(out=mx[:], in_=Sx[:, 0], axis=AX.X, op=ALU.max))
                    op("dve", lambda h: h.tensor_scalar(out=nbias[:], in0=mx[:], scalar1=-0.125, scalar2=None, op0=ALU.mult))
                    for hh_ in range(2):
                        op("act", lambda h, hh_=hh_: h.activation(out=Pb[:, hh_, :], in_=Sx[:, 0, hh_, :], func=AF.Exp, scale=0.125,
                                                                 bias=nbias[:, hh_:hh_ + 1], accum_out=rs[:, hh_:hh_ + 1]))
                    op("dve", lambda h: h.reciprocal(out=rinv[:], in_=rs[:]))
                    for hh_ in range(2):
                        for blk in range(2):
                            op("pe", lambda h, hh_=hh_, blk=blk: h.transpose(out=pT[:, hh_ * 2 + blk, :], in_=Pb[:, hh_, blk * 128:(blk + 1) * 128], identity=idb[:]))
                    op("act", lambda h: h.activation(out=PTs[:], in_=pT[:, 0:4, :], func=AF.Copy))
                    for hh_ in range(2):
                        for blk in range(2):
                            op("pe", lambda h, hh_=hh_, blk=blk: h.matmul(Ops[:, hh_ * 64:(hh_ + 1) * 64], lhsT=PTs[:, hh_ * 2 + blk, :],
                                                                         rhs=vx[:, blk, hh_ * 64:(hh_ + 1) * 64], start=(blk == 0), stop=(blk == 1)))
                    for hh_ in range(2):
                        hd = i + 4 * hh_
                        op("dve", lambda h, hh_=hh_, hd=hd: h.tensor_scalar(out=attn[:, hd * 64:(hd + 1) * 64], in0=Ops[:, hh_ * 64:(hh_ + 1) * 64],
                                                                           scalar1=rinv[:, hh_:hh_ + 1], scalar2=None, op0=ALU.mult))
                    yield
                op("pool", lambda h: h.tensor_copy(out=kTx[:, :, 0:128], in_=kTx[:, :, 128:256]))
                op("pool", lambda h: h.tensor_copy(out=vx[:, 0, :], in_=vx[:, 1, :]))
            else:
                W17 = 17 * 128
                for hd in range(8):
                    i, hh_ = hd % 4, hd // 4
                    for cb in range(5):
                        c0 = cb * 512
                        cw = min(512, W17 - c0)
                        bank = mmA if cb % 2 == 0 else mmB
                        op("pe", lambda h, i=i, hh_=hh_, c0=c0, cw=cw, bank=bank: h.matmul(bank[:, 0:cw], lhsT=qT[:, i, :], rhs=kTs[:, hh_, c0:c0 + cw], start=True, stop=True))
                        op("dve", lambda h, c0=c0, cw=cw, bank=bank: h.tensor_tensor(out=Ssx[:, c0:c0 + cw], in0=bank[:, 0:cw], in1=smask[:, c0:c0 + cw], op=ALU.add))
                    op("dve", lambda h, hd=hd: h.tensor_scalar(out=Ssx[:, W17:W17 + 1], in0=sink8[:, hd:hd + 1], scalar1=8.0, scalar2=None, op0=ALU.mult))
                    op("dve", lambda h: h.tensor_reduce(out=mx[:, 0:1], in_=Ssx[:], axis=AX.X, op=ALU.max))
                    op("dve", lambda h: h.tensor_scalar(out=nbias[:, 0:1], in0=mx[:, 0:1], scalar1=-0.125, scalar2=None, op0=ALU.mult))
                    op("act", lambda h: h.activation(out=Psb[:], in_=Ssx[:], func=AF.Exp, scale=0.125, bias=nbias[:, 0:1], accum_out=rs[:, 0:1]))
                    op("dve", lambda h: h.reciprocal(out=rinv[:, 0:1], in_=rs[:, 0:1]))
                    for g8 in range(3):
                        nb_ = 8 if g8 < 2 else 1
                        for b in range(nb_):
                            blk = g8 * 8 + b
                            op("pe", lambda h, b=b, blk=blk: h.transpose(out=pT[:, b, :], in_=Psb[:, blk * 128:(blk + 1) * 128], identity=idb[:]))
                        op("act", lambda h, g8=g8, nb_=nb_: h.activation(out=PTss[:, g8 * 8:g8 * 8 + nb_, :], in_=pT[:, 0:nb_, :], func=AF.Copy))
                    for blk in range(17):
                        op("pe", lambda h, blk=blk, hh_=hh_: h.matmul(Ops[:, 0:64], lhsT=PTss[:, blk, :], rhs=vs[:, blk, hh_ * 64:(hh_ + 1) * 64],
                                                                     start=(blk == 0), stop=(blk == 16)))
                    op("dve", lambda h, hd=hd: h.tensor_scalar(out=attn[:, hd * 64:(hd + 1) * 64], in0=Ops[:, 0:64], scalar1=rinv[:, 0:1], scalar2=None, op0=ALU.mult))
            rms(attn[0:n, :], "attn", 512, 1, 1.0 / 512, n)
            op("dve", lambda h: h.tensor_scalar(out=anb[0:n, :], in0=attn[0:n, :], scalar1=rstd[0:n, 1:2], scalar2=None, op0=ALU.mult), ["attn", "rstd1"], ["anb"])
            for c in range(4):
                op("pe", lambda h, c=c: h.transpose(out=pT[:, c, 0:n], in_=anb[0:n, c * 128:(c + 1) * 128], identity=idb[0:n, 0:n]), ["anb", "idb"], ["pT"])
            op("dve", lambda h: h.tensor_tensor(out=mixT2[par][:, 0:4, :], in0=pT[:, 0:4, :], in1=pcol[:, GATT:GATT + 4].unsqueeze(2).broadcast_to([128, 4, 128]), op=ALU.mult))

            yield

        def partB(ti, tl):
            sample = (ti == NPT)
            xt = xb[:, tl, :]
            n = 128
            ns = NS if sample else 128
            par = ti % 2
            uT = uT2[par]
            mixT = mixT2[par]
            def ssm_views(k):
                c4, hp = k // 2, k % 2
                q0 = c4 * 4 + 2 * hp
                bi = k % 2
                Bv = Bs2[bi][:].rearrange("p (r j k) -> p r j k", r=2, j=2)
                if sample:
                    csq = cs[:, q0:q0 + 2, 1:5].unsqueeze(2).broadcast_to([128, 2, NSEQ, 4])
                    snq = sn[:, q0:q0 + 2, 1:5].unsqueeze(2).broadcast_to([128, 2, NSEQ, 4])

                    def v3(ap):
                        return ap[:, :, 0:NS].rearrange("p j (s t) -> p j s t", t=4)
                else:
                    csq = cs[:, q0:q0 + 2, 1:129]; snq = sn[:, q0:q0 + 2, 1:129]

                    def v3(ap):
                        return ap
                T = [v3(pp2[bi][:, kk]) for kk in range(4)]
                RR = [v3(rr2[bi][:, kk]) for kk in range(2)]
                VV = [v3(vv2[bi][:, kk]) for kk in range(2)]
                return c4, hp, q0, bi, Bv, csq, snq, v3, T, RR, VV

            def ssm_front(k):
                c4, hp, q0, bi, Bv, csq, snq, v3, T, RR, VV = ssm_views(k)
                bank = mmA if bi == 0 else mmC
                for ri, WBx in enumerate((WBre, WBim)):
                    for j in range(2):
                        op("pe", lambda h: h.matmul(bank[:, (ri * 2 + j) * 128:(ri * 2 + j + 1) * 128], lhsT=WBx[:, q0 + j, :], rhs=uT[:, c4, :], start=True, stop=True))
                op("act", lambda h: h.activation(out=Bs2[bi][:], in_=bank[:, :], func=AF.Copy))
                if sample:
                    for ri in range(2):
                        v4 = Bv[:, ri, :, 0:NS].rearrange("p j (s t) -> p j s t", t=4)[:, :, :, 0]
                        op("dve", lambda h: h.tensor_tensor(out=v4, in0=v4, in1=ah[:, ri, q0:q0 + 2, :], op=ALU.add))
                Bre, Bim = v3(Bv[:, 0]), v3(Bv[:, 1])
                op("dve", lambda h: h.tensor_tensor(out=T[0], in0=Bre, in1=csq, op=ALU.mult))
                op("dve", lambda h: h.tensor_tensor(out=T[1], in0=Bim, in1=snq, op=ALU.mult))
                op("dve", lambda h: h.tensor_tensor(out=T[2], in0=Bim, in1=csq, op=ALU.mult))
                op("dve", lambda h: h.tensor_tensor(out=T[3], in0=Bre, in1=snq, op=ALU.mult))
                op("dve", lambda h: h.tensor_tensor(out=RR[0], in0=T[0], in1=T[1], op=ALU.add))
                op("dve", lambda h: h.tensor_tensor(out=RR[1], in0=T[2], in1=T[3], op=ALU.subtract))

            def ssm_back(k):
                c4, hp, q0, bi, Bv, csq, snq, v3, T, RR, VV = ssm_views(k)
                rrb, vvb = rr2[bi], vv2[bi]
                for j in range(2):
                    q = q0 + j
                    if sample:
                        op("dve", lambda h: h.tensor_scalar(out=dtmp[:], in0=mask64[:], scalar1=rho[:, q:q + 1], scalar2=None, op0=ALU.mult))
                    for ri in range(2):
                        if sample:
                            op("dve", lambda h: h.tensor_tensor_scan(out=vvb[:, ri, j, 0:NS], data0=dtmp[:], data1=rrb[:, ri, j, 0:NS], initial=0.0, op0=ALU.mult, op1=ALU.add))
                        else:
                            op("dve", lambda h: h.tensor_tensor_scan(out=vvb[:, ri, j, :], data0=rho[:, q:q + 1].broadcast_to([128, 128]), data1=rrb[:, ri, j, :],
                                                                   initial=car[:, ri, q:q + 1], op0=ALU.mult, op1=ALU.add))
                op("dve", lambda h: h.tensor_tensor(out=T[0], in0=VV[0], in1=csq, op=ALU.mult))
                op("dve", lambda h: h.tensor_tensor(out=T[1], in0=VV[1], in1=snq, op=ALU.mult))
                op("dve", lambda h: h.tensor_tensor(out=T[2], in0=VV[0], in1=snq, op=ALU.mult))
                op("dve", lambda h: h.tensor_tensor(out=T[3], in0=VV[1], in1=csq, op=ALU.mult))
                hb_re, hb_im = v3(hb[:, 2 * hp:2 * hp + 2, 0, :]), v3(hb[:, 2 * hp:2 * hp + 2, 1, :])
                op("dve", lambda h: h.tensor_tensor(out=hb_re, in0=T[0], in1=T[1], op=ALU.subtract))
                op("dve", lambda h: h.tensor_tensor(out=hb_im, in0=T[2], in1=T[3], op=ALU.add))
                if sample:
                    op("dve", lambda h: h.tensor_tensor(out=hfin[:, 0, q0:q0 + 2, 0:NSEQ], in0=T[0][:, :, :, 3], in1=T[1][:, :, :, 3], op=ALU.subtract))
                    op("dve", lambda h: h.tensor_tensor(out=hfin[:, 1, q0:q0 + 2, 0:NSEQ], in0=T[2][:, :, :, 3], in1=T[3][:, :, :, 3], op=ALU.add))
                else:
                    op("dve", lambda h: h.tensor_tensor(out=car[:, 0, q0:q0 + 2], in0=T[0][:, :, 127], in1=T[1][:, :, 127], op=ALU.subtract))
                    op("dve", lambda h: h.tensor_tensor(out=car[:, 1, q0:q0 + 2], in0=T[2][:, :, 127], in1=T[3][:, :, 127], op=ALU.add))

            def ssm_y(c4):
                for qq in range(4):
                    q = c4 * 4 + qq
                    op("pe", lambda h: h.matmul(mmB[:, 0:n], lhsT=WCre[:, q, :], rhs=hb[:, qq, 0, 0:n], start=(qq == 0), stop=False))
                    op("pe", lambda h: h.matmul(mmB[:, 0:n], lhsT=WCimn[:, q, :], rhs=hb[:, qq, 1, 0:n], start=False, stop=(qq == 3)))
                op("dve", lambda h: h.scalar_tensor_tensor(out=yv[:, 0:n], in0=uT[:, c4, 0:n], scalar=col(DSK + c4), in1=mmB[:, 0:n], op0=ALU.mult, op1=ALU.add))
                op("act", lambda h: h.activation(out=gl32[:, c4, 0:n], in_=yv[:, 0:n], func=AF.Gelu))
                op("pool", lambda h: h.tensor_copy(out=glb[:, c4, 0:n], in_=gl32[:, c4, 0:n]))

            ssm_front(0)
            yield
            for k in range(8):
                if k + 1 < 8:
                    ssm_front(k + 1)
                ssm_back(k)
                if k % 2 == 1:
                    ssm_y(k // 2)
                yield
            if ti == NPT - 1:
                op("act", lambda h: h.activation(out=hfin[:, :, :, NSEQ], in_=car[:], func=AF.Copy), ["car"], ["hfin"])
            for oc in range(4):
                for c4 in range(4):
                    op("pe", lambda h, oc=oc, c4=c4: h.matmul(mmC[:, 0:n], lhsT=wgl[0][:, c4, oc * 128:(oc + 1) * 128], rhs=glb[:, c4, 0:n], start=(c4 == 0), stop=(c4 == 3)),
                       ["glb", "wgl"], ["mmC"])
                op("act", lambda h, oc=oc: h.activation(out=sg[:, 0:n], in_=mmC[:, 0:n], func=AF.Sigmoid, bias=col(BGLU + oc)), ["mmC", "pcol"], ["sg"])
                op("dve", lambda h, oc=oc: h.tensor_tensor(out=gl32[:, oc, 0:n], in0=gl32[:, oc, 0:n], in1=sg[:, 0:n], op=ALU.mult), ["gl32", "sg", "glb"], ["gl32"])
                op("act", lambda h, oc=oc: h.activation(out=sqb[:, oc, 0:n], in_=gl32[:, oc, 0:n], func=AF.Square), ["gl32"], ["sqb"])
            for oc in range(4):
                op("pe", lambda h, oc=oc: h.matmul(mmC[:, 0:n], lhsT=onesb[:], rhs=sqb[:, oc, 0:n], start=(oc == 0), stop=(oc == 3)), ["sqb", "onesb"], ["mmC"])
            op("act", lambda h: h.activation(out=rsb[:, 0:n], in_=mmC[:, 0:n], func=AF.Ln, scale=1.0 / 512, bias=epsb[:, 0:1]))
            op("act", lambda h: h.activation(out=rsb[:, 0:n], in_=rsb[:, 0:n], func=AF.Exp, scale=-0.5))
            for oc in range(4):
                op("dve", lambda h, oc=oc: h.scalar_tensor_tensor(out=mixT[:, 4 + oc, 0:n], in0=gl32[:, oc, 0:n], scalar=col(GSSM + oc), in1=rsb[:, 0:n], op0=ALU.mult, op1=ALU.mult),
                   ["gl32", "rsb", "pcol"], ["mixT"])

            for hf in range(2):
                acc, ak = (acc0, "acc0") if hf == 0 else (acc1, "acc1")
                for c in range(8):
                    op("pe", lambda h, hf=hf, c=c, acc=acc: h.matmul(acc[0:n, :], lhsT=mixT[:, c, 0:n], rhs=wo[0][:, c, hf * 512:(hf + 1) * 512], start=(c == 0), stop=(c == 7)),
                       ["mixT", "wo"], [ak])
                op("dve", lambda h, hf=hf, acc=acc: h.tensor_tensor(out=xt[0:n, hf * 512:(hf + 1) * 512], in0=xt[0:n, hf * 512:(hf + 1) * 512], in1=acc[0:n, :], op=ALU.add),
                   ["xt", ak], ["xt"])
            rms(xt[0:n, :], "xt", D, 2, 1.0 / D, n)
            op("dve", lambda h: h.tensor_scalar(out=xn[0:n, :], in0=xt[0:n, :], scalar1=rstd[0:n, 2:3], scalar2=None, op0=ALU.mult), ["xt", "rstd2"], ["xn"])
            for c in range(8):
                op("pe", lambda h, c=c: h.transpose(out=pT[:, c, 0:n], in_=xn[0:n, c * 128:(c + 1) * 128], identity=idb[0:n, 0:n]), ["xn", "idb"], ["pT"])
            op("dve", lambda h: h.tensor_tensor(out=xn2T[:, :, tl * 128:(tl + 1) * 128], in0=pT[:], in1=pcol[:, GFFN:GFFN + 8].unsqueeze(2).broadcast_to([128, 8, 128]), op=ALU.mult))

        def ffn(tiles_):
            sample = (tiles_[0] == NPT)
            ntl = len(tiles_)
            nb = ntl * 128
            hTv = hTsm if sample else hTall
            for ch in range(NFC):
                b3 = ch % 2
                w3 = ch % 3
                P.dma("sp", lambda h: h.dma_start(out=wg[w3][:], in_=wgb_d[ch]))
                P.dma("sp", lambda h: h.dma_start(out=wu[w3][:], in_=wub_d[ch]))
                gps, ups = (mmA, mmB) if b3 == 0 else (mmC, pT32)
                for c in range(8):
                    op("pe", lambda h, c=c, b3=b3, gps=gps: h.matmul(gps[:, 0:nb], lhsT=wg[w3][:, c, :], rhs=xn2T[:, c, 0:nb], start=(c == 0), stop=(c == 7)))
                for c in range(8):
                    op("pe", lambda h, c=c, b3=b3, ups=ups: h.matmul(ups[:, 0:nb], lhsT=wu[w3][:, c, :], rhs=xn2T[:, c, 0:nb], start=(c == 0), stop=(c == 7)))
                w0, w1, w2, bb = col(CVW + ch * 3), col(CVW + ch * 3 + 1), col(CVW + ch * 3 + 2), col(CVB + ch)
                cvb, slb, gxb = cv[:, 0, :], sl[:, 0, :], gx[:, b3, :]
                if sample:
                    op("pool", lambda h, ch=ch: h.tensor_copy(out=gxs[:, :, 0:2], in_=cvst[:, ch, :, :]))
                    op("act", lambda h: h.activation(out=gxs[:, :, 2:6], in_=gps[:, 0:NS].rearrange("p (s t) -> p s t", t=4), func=AF.Copy))
                    g0, g1, g2 = gxs[:, :, 0:4], gxs[:, :, 1:5], gxs[:, :, 2:6]
                    cvv = cvb[:, 0:NS].rearrange("p (s t) -> p s t", t=4)
                    op("pool", lambda h, ch=ch: h.tensor_copy(out=sconv[:, ch, :, :], in_=gxs[:, :, 4:6]))
                else:
                    op("pool", lambda h, ch=ch, gxb=gxb: h.tensor_copy(out=gxb[:, 0:2], in_=gcar[:, ch, :]))
                    op("act", lambda h, gxb=gxb: h.activation(out=gxb[:, 2:2 + nb], in_=gps[:, 0:nb], func=AF.Copy))
                    g0, g1, g2 = gxb[:, 0:nb], gxb[:, 1:1 + nb], gxb[:, 2:2 + nb]
                    cvv = cvb[:, 0:nb]
                    op("pool", lambda h, ch=ch, gxb=gxb: h.tensor_copy(out=gcar[:, ch, :], in_=gxb[:, nb:nb + 2]))
                op("dve", lambda h, g2=g2, cvv=cvv, w2=w2, bb=bb: h.tensor_scalar(out=cvv, in0=g2, scalar1=w2, scalar2=bb, op0=ALU.mult, op1=ALU.add))
                op("dve", lambda h, g1=g1, cvv=cvv, w1=w1: h.scalar_tensor_tensor(out=cvv, in0=g1, scalar=w1, in1=cvv, op0=ALU.mult, op1=ALU.add))
                op("dve", lambda h, g0=g0, cvv=cvv, w0=w0: h.scalar_tensor_tensor(out=cvv, in0=g0, scalar=w0, in1=cvv, op0=ALU.mult, op1=ALU.add))
                op("act", lambda h, cvb=cvb, slb=slb: h.activation(out=slb[:, 0:nb], in_=cvb[:, 0:nb], func=AF.Silu))
                op("dve", lambda h, ch=ch, slb=slb: h.tensor_tensor(out=hTv[:, ch, 0:nb], in0=slb[:, 0:nb], in1=ups[:, 0:nb], op=ALU.mult))
            accs = [acc0, acc1, Sps, Ops]
            for hf in range(2):
                for g in range(NFC // 2):
                    wdb = wd[g % 3]
                    P.dma("sp", lambda h: h.dma_start(out=wdb, in_=wdb_d[hf, g]))
                    for j in range(2):
                        ch = 2 * g + j
                        for tl in range(ntl):
                            op("pe", lambda h: h.matmul(accs[tl][:, :], lhsT=hTv[:, ch, tl * 128:(tl + 1) * 128], rhs=wdb[:, j, :],
                                                        start=(ch == 0), stop=(ch == NFC - 1)))
                for tl in range(ntl):
                    op("dve", lambda h, tl=tl, hf=hf: h.tensor_tensor(out=xb[:, tl, hf * 512:(hf + 1) * 512], in0=xb[:, tl, hf * 512:(hf + 1) * 512], in1=accs[tl][:, :], op=ALU.add))
            for tl, ti in enumerate(tiles_):
                xt = xb[:, tl, :]
                rms(xt, "xt", D, 3, 1.0 / D, 128)
                op("dve", lambda h, xt=xt: h.scalar_tensor_tensor(out=xt, in0=xt, scalar=rstd[:, 3:4], in1=gfb[:], op0=ALU.mult, op1=ALU.mult))
                P.dma("sp", lambda h, xt=xt, ti=ti: h.dma_start(out=y_d[ti * 128:(ti + 1) * 128, :], in_=xt))

        def est_dur(kind, eng, fn):
            rec = _Recorder()
            fn(rec)
            name, args, kwargs = rec.call
            o = kwargs.get("out", args[0] if args else None)
            nfree = 1
            if o is not None and hasattr(o, "shape"):
                for d_ in list(o.shape)[1:]:
                    nfree *= int(d_)
            if kind == "dma":
                return 100.0, 2500.0
            if eng == "dve":
                d = ((2 * nfree if name == "tensor_tensor_scan" else nfree) + 151) / 0.96
            elif eng == "act":
                d = (nfree + 230) / 1.2 + (90 if kwargs.get("accum_out") is not None else 0)
            elif eng == "pool":
                d = (2 * nfree + 250) / 1.2
            else:
                d = max(64, nfree) / 1.9 + 25
            return d, d

        def merge_threads(gens):
            gens = [g for g in gens if g is not None]
            bufs = [[] for _ in gens]
            alive = [True] * len(gens)
            tchain = [sched_now[0]] * len(gens)
            while True:
                for i, g in enumerate(gens):
                    while alive[i] and not bufs[i]:
                        P.capture = bufs[i]
                        if next(g, "done") == "done":
                            alive[i] = False
                        P.capture = None
                cands = [i for i in range(len(gens)) if bufs[i]]
                if not cands:
                    break
                best, best_t = None, None
                for i in cands:
                    kind, eng, fn = bufs[i][0]
                    t = max(eng_free.get(eng, 0.0), tchain[i] + 60.0)
                    if best is None or t < best_t:
                        best, best_t = i, t
                kind, eng, fn = bufs[best].pop(0)
                busy, lat = est_dur(kind, eng, fn)
                eng_free[eng] = best_t + busy
                tchain[best] = best_t + lat
                sched_now[0] = max(sched_now[0], best_t)
                (P.dma if kind == "dma" else P.op)(eng, fn)

        eng_free = {}
        sched_now = [0.0]
        if tiles is None:
            blocks = [[0, 1, 2, 3], [4, 5, 6, 7], [8, 9, 10, 11], [12, 13, 14, 15], [16], [NPT]]
        else:
            blocks = tiles
        for blk_ in blocks:
            for tl, ti in enumerate(blk_):
                P.dma("sp", lambda h: h.dma_start(out=xb[:, tl, :], in_=xin[ti * 128:(ti + 1) * 128, :]))
            for _ in partA(blk_[0], 0):
                pass
            for tl, ti in enumerate(blk_):
                gb = partB(ti, tl)
                ga = partA(blk_[tl + 1], tl + 1) if tl + 1 < len(blk_) else None
                merge_threads([gb, ga])
            ffn(blk_)

        fv = pp2[0][:].rearrange("p a j k -> p (a j k)")
        v1 = fv[:, 0:272].rearrange("p (q s) -> p q s", q=16); v2 = fv[:, 272:544].rearrange("p (q s) -> p q s", q=16); v3_ = fv[:, 544:816].rearrange("p (q s) -> p q s", q=16)
        fre_c = fre[:].unsqueeze(2).broadcast_to([128, 16, NSEQ + 1]); fim_c = fim[:].unsqueeze(2).broadcast_to([128, 16, NSEQ + 1])
        TT(v1, hfin[:, 0], fre_c, ALU.mult); TT(v2, hfin[:, 1], fim_c, ALU.mult); TT(v3_, v1, v2, ALU.subtract)
        TT(v1, hfin[:, 0], fim_c, ALU.mult); TT(v2, hfin[:, 1], fre_c, ALU.mult); TT(hfin[:, 1], v1, v2, ALU.add)
        op("dve", lambda h: h.tensor_copy(out=hfin[:, 0], in_=v3_))
        P.dma("sp", lambda h: h.dma_start(out=hfin_d, in_=hfin[:]), reads=["hfin"], writes=["hfin_d"])
        P.dma("sp", lambda h: h.dma_start(out=pconv_d, in_=gcar[:]), reads=["gcar"], writes=["pconv_d"])
        P.dma("sp", lambda h: h.dma_start(out=sconv_d, in_=sconv[:]), reads=["sconv"], writes=["sconv_d"])
        P.limit = None
        P.barrier()

        with nc.Block() as block:
            @block.tensor
            def _(h):
                P.run("pe", h, sems)

            @block.scalar
            def _(h):
                P.run("act", h, sems)

            @block.vector
            def _(h):
                P.run("dve", h, sems)

            @block.gpsimd
            def _(h):
                P.run("pool", h, sems)

            @block.sync
            def _(h):
                P.run("sp", h, sems)
    return nc


def _consts():
    ident = np.eye(128, dtype=np.float32)
    i = np.arange(128)[:, None]
    c = np.arange(256)[None, :]
    full = np.where(((c < 128) & (c > i)) | ((c >= 128) & (c - 128 <= i)), 0.0, MASKV)
    m1 = np.where(((c < 128) & (c > i) & (c >= NPAD)) | ((c >= 128) & (c - 128 <= i)), 0.0, MASKV)
    m0 = np.where((c >= 128) & (c - 128 <= i) & (c - 128 >= NPAD), 0.0, MASKV)
    masks = np.stack([m0, m1, full], axis=1).astype(np.float32)
    sm = np.full((128, 17 * 128), MASKV, np.float32)
    for s in range(NSEQ):
        for t in range(4):
            r = s * 4 + t
            sm[r, s * 128 + t + 1:(s + 1) * 128] = 0.0
            sm[r, 2048 + s * 4:2048 + s * 4 + t + 1] = 0.0
    return ident, masks, sm


def prep_inputs(x_prompt, x_sample, cache_k_win, cache_v_win, state_ssm_re, state_ssm_im, state_conv,
           meta_tokens, g_mix, w_in, sinks, lam_re, lam_im, log_dt, b_re, b_im, c_re, c_im, d_skip,
           w_glu, b_glu, g_attn_out, g_ssm_out, w_o, g_ffn, w_gate, w_up, conv_w, conv_b, w_down,
           g_final):
    f32 = np.float32
    A = lambda a: np.ascontiguousarray(np.asarray(a, dtype=f32))
    x_prompt, x_sample = A(x_prompt), A(x_sample)
    ident, masks, smask = _consts()
    w_in0 = A(w_in)[0]
    perm = []
    for i in range(4):
        perm += list(range(i * 64, (i + 1) * 64)) + list(range((4 + i) * 64, (5 + i) * 64))
    w_in_p = np.ascontiguousarray(np.concatenate([w_in0[:, perm], w_in0[:, 512:]], axis=1))
    pcol = np.zeros((128, 128), f32)
    pcol[:, 0:8] = A(g_mix)[0].reshape(8, 128).T
    pcol[:, 8:16] = A(g_ffn)[0].reshape(8, 128).T
    pcol[:, 16:20] = A(g_attn_out)[0].reshape(4, 128).T
    pcol[:, 20:24] = A(g_ssm_out)[0].reshape(4, 128).T
    pcol[:, 24:28] = A(b_glu)[0].reshape(4, 128).T
    pcol[:, 28:32] = A(d_skip)[0].reshape(4, 128).T
    cw = A(conv_w)[0].reshape(3, NFC, 128)
    pcol[:, 32:98] = cw.transpose(2, 1, 0).reshape(128, 66)
    pcol[:, 98:120] = A(conv_b)[0].reshape(NFC, 128).T
    lr, li, ld = A(lam_re)[0], A(lam_im)[0], A(log_dt)[0]
    ldx = np.repeat(ld[:, None], 64, axis=1)

    def pl(a):
        return a.reshape(16, 2, 64).transpose(1, 2, 0).reshape(128, 16)

    lam = np.ascontiguousarray(np.stack([pl(lr), pl(li), pl(ldx)], axis=1))
    lamb = np.ascontiguousarray(np.stack([lr.reshape(-1), li.reshape(-1), ldx.reshape(-1)], axis=0))
    bre, bim, cre, cim = A(b_re)[0], A(b_im)[0], A(c_re)[0], A(c_im)[0]
    bblk = np.zeros((128, 2, 16, 128), f32)
    cblk = np.zeros((128, 2, 16, 128), f32)
    for q in range(16):
        for j2 in range(2):
            g = 2 * q + j2
            g8 = g % 8
            rows = slice(g8 * 16, g8 * 16 + 16)
            cols = slice(j2 * 64, j2 * 64 + 64)
            bblk[rows, 0, q, cols] = bre[g].T
            bblk[rows, 1, q, cols] = bim[g].T
            cblk[cols, 0, q, rows] = cre[g].T
            cblk[cols, 1, q, rows] = cim[g].T
    sre, sim_ = A(state_ssm_re)[0], A(state_ssm_im)[0]
    ck, cvv = A(cache_k_win)[0].reshape(128, 128, 128), A(cache_v_win)[0].reshape(128, 128, 128)
    sc = A(state_conv)[0]
    meta = A(meta_tokens)
    wg_l = np.ascontiguousarray(A(w_gate)[0].reshape(8, 128, NFC, 128).transpose(2, 1, 0, 3))
    wu_l = np.ascontiguousarray(A(w_up)[0].reshape(8, 128, NFC, 128).transpose(2, 1, 0, 3))
    wd_l = np.ascontiguousarray(A(w_down)[0].reshape(NFC // 2, 2, 128, 2, 512).transpose(3, 0, 2, 1, 4))
    in_maps = []
    for c in range(NCORES):
        xin = np.zeros((NPT * 128 + 128, D), f32)
        xin[NPAD:128] = meta
        xin[128:NPT * 128] = x_prompt[c]
        xin[NPT * 128:NPT * 128 + NS] = x_sample[c * NSEQ:(c + 1) * NSEQ].reshape(NS, D)
        sl_ = slice(c * NSEQ, (c + 1) * NSEQ)

        def hl(a):
            return a.reshape(NSEQ, 16, 2, 64).transpose(2, 3, 1, 0).reshape(128, 16, NSEQ)

        h0 = np.ascontiguousarray(np.stack([hl(sre[sl_]), hl(sim_[sl_])], axis=1))
        cvst = np.ascontiguousarray(sc[sl_].reshape(NSEQ, 2, NFC, 128).transpose(3, 2, 0, 1))
        kc, vc = ck[sl_], cvv[sl_]
        kcT = np.ascontiguousarray(kc.transpose(2, 0, 1).reshape(128, NSEQ * 128))
        vcl = np.ascontiguousarray(vc.transpose(1, 0, 2))
        in_maps.append(dict(
            xin=xin, w_in=w_in_p, w_o=A(w_o)[0], w_glu=A(w_glu)[0], w_gate=wg_l, w_up=wu_l,
            w_down=wd_l, ident=ident, masks=masks, smask=smask, pcol=pcol, sinks=A(sinks)[0],
            gfin=A(g_final), lam=lam, lamb=lamb, bblk=bblk, cblk=cblk, h0=h0, cvst=cvst, kcT=kcT, vc=vcl,
            kcache=np.ascontiguousarray(kc), vcache=np.ascontiguousarray(vc)))
    return in_maps


def assemble(R):
    f32 = np.float32
    y_prompt = np.stack([R[c]["y"][128:NPT * 128] for c in range(NCORES)])
    y_sample = np.concatenate([R[c]["y"][NPT * 128:NPT * 128 + NS].reshape(NSEQ, 4, D) for c in range(NCORES)])
    p_k = np.stack([R[c]["kvp"][:, 0:128].reshape(128, 2, 64) for c in range(NCORES)])[None]
    p_v = np.stack([R[c]["kvp"][:, 128:256].reshape(128, 2, 64) for c in range(NCORES)])[None]
    s_k = np.concatenate([R[c]["kvs_k"].reshape(NSEQ, 128, 2, 64) for c in range(NCORES)])[None]
    s_v = np.concatenate([R[c]["kvs_v"].reshape(NSEQ, 128, 2, 64) for c in range(NCORES)])[None]

    def unh(a):
        nn = a.shape[-1]
        return a.reshape(2, 64, 16, nn).transpose(3, 2, 0, 1).reshape(nn, 32, 64)

    p_re = np.stack([unh(R[c]["hfin"][:, 0, :, NSEQ:])[0] for c in range(NCORES)])[None]
    p_im = np.stack([unh(R[c]["hfin"][:, 1, :, NSEQ:])[0] for c in range(NCORES)])[None]
    s_re = np.concatenate([unh(R[c]["hfin"][:, 0, :, :NSEQ]) for c in range(NCORES)])[None]
    s_im = np.concatenate([unh(R[c]["hfin"][:, 1, :, :NSEQ]) for c in range(NCORES)])[None]
    p_conv = np.stack([R[c]["pconv"].transpose(2, 1, 0).reshape(2, FF) for c in range(NCORES)])[None]
    s_conv = np.concatenate([R[c]["sconv"].transpose(2, 3, 1, 0).reshape(NSEQ, 2, FF) for c in range(NCORES)])[None]
    out = (y_prompt, y_sample, p_k, p_v, p_re, p_im, p_conv, s_k, s_v, s_re, s_im, s_conv)
    return tuple(np.ascontiguousarray(o, dtype=f32) for o in out)


def kernel(**inputs):
    in_maps = prep_inputs(**inputs)
    nc = build_nc()
    res = run_bass_kernel_spmd(nc, in_maps, core_ids=list(range(NCORES)))
    return assemble(res.results)
```

```python
import numpy as np
from contextlib import ExitStack
import concourse.bass as bass
import concourse.mybir as mybir
from concourse.bass_utils import run_bass_kernel_spmd

F32 = mybir.dt.float32
BF16 = mybir.dt.bfloat16
ALU = mybir.AluOpType
AF = mybir.ActivationFunctionType
AX = mybir.AxisListType

ENGS = ("pe", "act", "dve", "pool", "sp")
NDMASEM = 12
NCORES = 8
D = 1024
NPT = 17
NPAD = 112
NS = 64
NSEQ = 16
FF = 2816
NFC = 22
EPS = 1e-5
PI = float(np.pi)
MASKV = -30000.0


class _Recorder:
    def __init__(self):
        self.call = None

    def __getattr__(self, name):
        def f(*args, **kwargs):
            self.call = (name, args, kwargs)
            return self
        return f


class Prog:
    def __init__(self):
        self.streams = {e: [] for e in ENGS}
        self.count = {e: 0 for e in ENGS}
        self.waited = {e: {} for e in ENGS}
        self.regs = {}
        self.nrec = 0
        self.limit = None
        self.capture = None
        self.dma_rr = {"sp": 0, "pool": 0}
        self.dma_cnt = {}

    @staticmethod
    def _region(ap):
        dims = [(int(st), int(sz)) for st, sz in ap.ap]
        off = int(ap.offset)
        esz = mybir.dt.size(ap.dtype)
        space = str(ap.space)
        if space == "DRAM":
            ext = sum((sz - 1) * abs(st) for st, sz in dims)
            return (0, 1, off, off + ext)
        pst, npart = dims[0]
        pst = max(pst, 1)
        p0, f0 = off // pst, off % pst
        ext = sum((sz - 1) * abs(st) for st, sz in dims[1:])
        f0, ext = f0 * esz, ext * esz + esz - 1
        if space == "PSUM":
            return (0, 128, 0, 1 << 30)
        return (p0, p0 + npart, f0, f0 + ext)

    @staticmethod
    def _is_ap(v):
        return hasattr(v, "tensor") and hasattr(v, "ap") and hasattr(v, "offset")

    def _record(self, fn):
        rec = _Recorder()
        fn(rec)
        name, args, kwargs = rec.call
        acc = []
        for i, a in enumerate(args):
            if self._is_ap(a):
                acc.append((a, i == 0))
        for k, v in kwargs.items():
            if self._is_ap(v):
                acc.append((v, k in ("out", "accum_out")))
        out = []
        for ap, w in acc:
            if str(ap.space) == "PSUM":
                w = True
            out.append((ap.tensor.name, self._region(ap), w))
        self._last_call = rec.call
        return out

    def _deps(self, eng, acc):
        deps = {}
        for name, R, w in acc:
            for (R2, w2), evs in self.regs.get(name, {}).items():
                if not (w or w2):
                    continue
                if R[0] < R2[1] and R2[0] < R[1] and R[2] <= R2[3] and R2[2] <= R[3]:
                    for s_, v in evs.items():
                        if s_ == eng and eng == "pe":
                            continue
                        if deps.get(s_, 0) < v:
                            deps[s_] = v
        out = []
        wd = self.waited[eng]
        for s_, v in deps.items():
            if wd.get(s_, 0) < v:
                wd[s_] = v
                out.append((s_, v))
        return out

    def _commit(self, ev, acc):
        s_, v = ev
        for name, R, w in acc:
            d = self.regs.setdefault(name, {})
            if w:
                for key in [k for k in d if k[0][0] >= R[0] and k[0][1] <= R[1] and k[0][2] >= R[2] and k[0][3] <= R[3]]:
                    del d[key]
            e = d.setdefault((R, w), {})
            if e.get(s_, 0) < v:
                e[s_] = v

    @staticmethod
    def _freeze(fn):
        rec = _Recorder()
        fn(rec)
        return lambda h, c=rec.call: getattr(h, c[0])(*c[1], **c[2])

    def op(self, eng, fn, reads=(), writes=()):
        if self.capture is not None:
            self.capture.append(("op", eng, self._freeze(fn)))
            return
        self.nrec += 1
        if self.limit is not None and self.nrec > self.limit:
            return
        acc = self._record(fn)
        deps = self._deps(eng, acc)
        self.count[eng] += 1
        ev = (eng, self.count[eng])
        self.streams[eng].append((deps, self._last_call, (eng, 1)))
        self._commit(ev, acc)

    def dma(self, eng, fn, reads=(), writes=()):
        if self.capture is not None:
            self.capture.append(("dma", eng, self._freeze(fn)))
            return
        self.nrec += 1
        if self.limit is not None and self.nrec > self.limit:
            return
        acc = self._record(fn)
        k = self.dma_rr[eng]
        self.dma_rr[eng] = (k + 1) % NDMASEM
        sname = "dma%s%d" % (eng, k)
        deps = self._deps(eng, acc)
        prev = self.dma_cnt.get(sname, 0) * 16
        if prev and self.waited[eng].get(sname, 0) < prev:
            self.waited[eng][sname] = prev
            deps.append((sname, prev))
        self.dma_cnt[sname] = self.dma_cnt.get(sname, 0) + 1
        ev = (sname, self.dma_cnt[sname] * 16)
        self.streams[eng].append((deps, self._last_call, (sname, 16)))
        self._commit(ev, acc)

    def barrier(self):
        evs = [(e, self.count[e]) for e in ENGS if self.count[e]]
        evs += [(k, c * 16) for k, c in self.dma_cnt.items()]
        for e in ENGS:
            deps = []
            for s, v in evs:
                if s == e:
                    continue
                if self.waited[e].get(s, 0) < v:
                    self.waited[e][s] = v
                    deps.append((s, v))
            if deps:
                self.streams[e].append((deps, None, None))

    def run(self, eng, h, sems):
        for deps, fn, inc in self.streams[eng]:
            for s, v in deps:
                h.wait_ge(sems[s], v)
            if fn is not None:
                name, args, kwargs = fn
                getattr(h, name)(*args, **kwargs).then_inc(sems[inc[0]], inc[1])


def build_nc(tiles=None, limit=None):
    nc = bass.Bass("TRN2", target_bir_lowering=False)

    def din(name, shape, dt=F32):
        return nc.dram_tensor(name, list(shape), dt, kind="ExternalInput").ap()

    def dout(name, shape, dt=F32):
        return nc.dram_tensor(name, list(shape), dt, kind="ExternalOutput").ap()

    xin = din("xin", [NPT * 128 + 128, D])
    w_in = din("w_in", [D, 1280])
    w_o = din("w_o", [D, D])
    w_glu = din("w_glu", [512, 512])
    w_gate = din("w_gate", [NFC, 128, 8, 128])
    w_up = din("w_up", [NFC, 128, 8, 128])
    w_down = din("w_down", [2, NFC // 2, 128, 2, 512])
    ident_d = din("ident", [128, 128])
    masks_d = din("masks", [128, 3, 256])
    smask_d = din("smask", [128, 17 * 128])
    pcol_d = din("pcol", [128, 128])
    sinks_d = din("sinks", [8])
    gfin_d = din("gfin", [D])
    lam_d = din("lam", [128, 3, 16])
    lamb_d = din("lamb", [3, 16 * 128])
    bblk_d = din("bblk", [128, 2, 16, 128])
    cblk_d = din("cblk", [128, 2, 16, 128])
    h0_d = din("h0", [128, 2, 16, NSEQ])
    cvst_d = din("cvst", [128, NFC, NSEQ, 2])
    kcT_d = din("kcT", [128, NSEQ * 128])
    vc_d = din("vc", [128, NSEQ, 128])
    kcache_d = din("kcache", [NSEQ, 128, 128])
    vcache_d = din("vcache", [NSEQ, 128, 128])

    wgb_d = nc.dram_tensor("wgb", [NFC, 128, 8, 128], BF16, kind="Internal").ap()
    wub_d = nc.dram_tensor("wub", [NFC, 128, 8, 128], BF16, kind="Internal").ap()
    wdb_d = nc.dram_tensor("wdb", [2, NFC // 2, 128, 2, 512], BF16, kind="Internal").ap()
    y_d = dout("y", [NPT * 128 + 128, D])
    kvp_d = dout("kvp", [128, 256])
    kvs_k = dout("kvs_k", [NSEQ, 128, 128])
    kvs_v = dout("kvs_v", [NSEQ, 128, 128])
    hfin_d = dout("hfin", [128, 2, 16, NSEQ + 1])
    pconv_d = dout("pconv", [128, NFC, 2])
    sconv_d = dout("sconv", [128, NFC, NSEQ, 2])

    P = Prog()
    P.limit = limit
    es = ExitStack()
    with es:
        def sb(name, shape, dt=F32):
            return es.enter_context(nc.sbuf_tensor(name, list(shape), dt))

        def ps(name, shape, dt=F32):
            return es.enter_context(nc.psum_tensor(name, list(shape), dt))

        sems = {e: es.enter_context(nc.semaphore("s_" + e)) for e in ENGS}
        for k in range(NDMASEM):
            for e_ in ("sp", "pool"):
                sems["dma%s%d" % (e_, k)] = es.enter_context(nc.semaphore("s_dma%s%d" % (e_, k)))

        idb = sb("idb", [128, 128], BF16)
        onesb = sb("onesb", [128, 128], BF16)
        masks = sb("masks_s", [128, 3, 256])
        smask = sb("smask_s", [128, 17 * 128], BF16)
        pcol = sb("pcol_s", [128, 128])
        sink8 = sb("sink8", [128, 8])
        gfb = sb("gfb", [128, D])
        epsb = sb("epsb", [128, 1])
        lam = sb("lam_s", [128, 3, 16])
        WBre = sb("WBre", [128, 16, 128], BF16); WBim = sb("WBim", [128, 16, 128], BF16)
        WCre = sb("WCre", [128, 16, 128], BF16); WCimn = sb("WCimn", [128, 16, 128], BF16)
        cs = sb("cs", [128, 16, 129]); sn = sb("sn", [128, 16, 129])
        mask64 = sb("mask64", [128, NS]); dtmp = sb("dtmp", [128, NS])
        are = sb("are", [128, 16]); aim = sb("aim", [128, 16])
        ah = sb("ah", [128, 2, 16, NSEQ])
        car = sb("car", [128, 2, 16])
        hfin = sb("hfin_s", [128, 2, 16, NSEQ + 1])
        gcar = sb("gcar", [128, NFC, 2])
        sconv = sb("sconv_s", [128, NFC, NSEQ, 2])
        kTx = sb("kTx", [128, 2, 256], BF16)
        vx = sb("vx", [128, 2, 128], BF16)
        GMIX, GFFN, GATT, GSSM, BGLU, DSK, CVW, CVB = 0, 8, 16, 20, 24, 28, 32, 98

        def col(c):
            return pcol[:, c:c + 1]

        ssq = sb("ssq", [128, 4]); rstd = sb("rstd", [128, 4])
        xn = sb("xn", [128, D], BF16)
        junk = xn
        xnT = sb("xnT", [128, 8, 128], BF16)
        qT = sb("qT", [128, 4, 128], BF16)
        uT2 = [sb("uT%d" % i, [128, 4, 128], BF16) for i in range(2)]
        Sx = sb("Sx", [128, 1, 2, 257]); mx = sb("mx", [128, 2]); nbias = sb("nbias", [128, 2])
        rs = sb("rs", [128, 2]); rinv = sb("rinv", [128, 2])
        Pb = sb("Pb", [128, 2, 257], BF16); PTs = sb("PTs", [128, 4, 128], BF16)
        mixT2 = [sb("mixT%d" % i, [128, 8, 128], BF16) for i in range(2)]
        mixT = mixT2[0]
        Psb = sb("Psb", [128, 17 * 128 + 1], BF16)
        PTss = sb("PTss", [128, 17, 128], BF16)
        Bs2 = [sb("Bs%d" % i, [128, 512]) for i in range(2)]
        pp2 = [sb("pp%d" % i, [128, 4, 2, 128]) for i in range(2)]
        rr2 = [sb("rr%d" % i, [128, 2, 2, 128]) for i in range(2)]
        vv2 = [sb("vv%d" % i, [128, 2, 2, 128]) for i in range(2)]
        hb = sb("hb", [128, 4, 2, 128], BF16)
        gl32 = sb("gl32", [128, 4, 128]); glb = sb("glb", [128, 4, 128], BF16)
        xn2T = sb("xn2T", [128, 8, 512], BF16)
        gx = sb("gx", [128, 2, 514]); gxs = sb("gxs", [128, NSEQ, 6])
        cv = sb("cv", [128, 1, 512]); sl = sb("sl", [128, 1, 512])
        kvtok = sl[:, 0, 0:256]
        yv = gx[:, 0, 0:128]; sg = gx[:, 0, 128:256]; rsb = gx[:, 0, 256:384]
        sqb = gx[:, 1, 0:256].bitcast(BF16).rearrange("p (a k) -> p a k", a=4)
        attn = cv[:, 0, :]
        anb = sl[:, 0, 256:512].bitcast(BF16)
        wi = [sb("wi0", [128, 8, 1280], BF16)] * 2
        wo = [sb("wo0", [128, 8, D], BF16)] * 2
        wgl = [sb("wgl0", [128, 4, 512], BF16)] * 2
        wg = [sb("wg%d" % i, [128, 8, 128], BF16) for i in range(2)] + [xnT]
        wu = [sb("wu%d" % i, [128, 8, 128], BF16) for i in range(2)] + [mixT]
        wdA = sb("wdA", [128, 2, 512], BF16)
        wd = [wdA[:], Sx[:].rearrange("p a b c -> p (a b c)").bitcast(BF16)[:, 0:1024].rearrange("p (j n) -> p j n", j=2),
              xn[:].rearrange("p (j n) -> p j n", j=2)]
        big = sb("big", [128, 9728])
        xb = big[:, 0:4096].rearrange("p (t d) -> p t d", t=4)
        hTall = big[:, 4096:9728].bitcast(BF16).rearrange("p (c n) -> p c n", c=NFC)
        hTsm = big[:, 4096:5504].bitcast(BF16).rearrange("p (c n) -> p c n", c=NFC)
        kTs = big[:, 5504:7680].bitcast(BF16).rearrange("p (a k) -> p a k", a=2)
        vs = big[:, 7680:8768].bitcast(BF16).rearrange("p (b k) -> p b k", b=17)
        scr = big
        h0 = big[:, 8192:8704].rearrange("p (a q s) -> p a q s", a=2, q=16)
        cvst = big[:, 3264:3968].rearrange("p (c s j) -> p c s j", c=NFC, s=NSEQ)
        def prep_views(o):
            return (scr[:, o:o + 768].rearrange("p (a b) -> p a b", a=3), scr[:, o + 768:o + 2304].rearrange("p (a b) -> p a b", a=6),
                    scr[:, o + 2304:o + 2816].rearrange("p (a q m) -> p a q m", a=2, q=2), scr[:, o + 2816:o + 3328].rearrange("p (a q m) -> p a q m", a=2, q=2))
        Ssx = big[:, 1024:1024 + 17 * 128 + 1]
        idf = big[:, 8960:9088]

        mmA = ps("mmA", [128, 512]); mmB = ps("mmB", [128, 512]); mmC = ps("mmC", [128, 512])
        pT = ps("pT", [128, 8, 128], BF16)
        pT32 = pT[:].rearrange("p c k -> p (c k)").bitcast(F32)
        Sps = ps("Sps", [128, 512]); Ops = ps("Ops", [128, 512])
        acc0 = ps("acc0", [128, 512]); acc1 = ps("acc1", [128, 512])

        op = P.op

        P.dma("sp", lambda h: h.dma_start(out=idf[:], in_=ident_d), writes=["idf"])
        P.dma("sp", lambda h: h.dma_start(out=masks[:], in_=masks_d), writes=["masks"])
        P.dma("pool", lambda h: h.dma_start(out=smask[:], in_=smask_d), writes=["smask"])
        P.dma("sp", lambda h: h.dma_start(out=pcol[:], in_=pcol_d), writes=["pcol"])
        P.dma("sp", lambda h: h.dma_start(out=sink8[:], in_=sinks_d.partition_broadcast(128)), writes=["sink8"])
        P.dma("sp", lambda h: h.dma_start(out=gfb[:], in_=gfin_d.partition_broadcast(128)), writes=["gfb"])
        P.dma("sp", lambda h: h.dma_start(out=lam[:], in_=lam_d), writes=["lam"])
        P.dma("sp", lambda h: h.dma_start(out=h0, in_=h0_d))
        op("pool", lambda h: h.memset(hb[:], 0.0))
        op("pool", lambda h: h.memset(cv[:], 0.0))
        op("pool", lambda h: h.memset(gx[:], 0.0))
        P.dma("pool", lambda h: h.dma_start(out=wi[0][:], in_=w_in.rearrange("(c p) n -> p c n", p=128)))
        P.dma("pool", lambda h: h.dma_start(out=wgl[0][:], in_=w_glu.rearrange("(c p) n -> p c n", p=128)))
        P.dma("pool", lambda h: h.dma_start(out=wo[0][:], in_=w_o.rearrange("(c p) n -> p c n", p=128)))
        for a_ in range(0, NFC, 11):
            P.dma("pool", lambda h: h.dma_start(out=wgb_d[a_:a_ + 11], in_=w_gate[a_:a_ + 11]))
            P.dma("pool", lambda h: h.dma_start(out=wub_d[a_:a_ + 11], in_=w_up[a_:a_ + 11]))
        for hf_ in range(2):
            P.dma("pool", lambda h: h.dma_start(out=wdb_d[hf_], in_=w_down[hf_]))
        P.dma("sp", lambda h: h.dma_start(out=kvs_k[:, 0:124, :], in_=kcache_d[:, 4:128, :]), writes=["kvs_k_a"])
        P.dma("sp", lambda h: h.dma_start(out=kvs_v[:, 0:124, :], in_=vcache_d[:, 4:128, :]), writes=["kvs_v_a"])

        op("dve", lambda h: h.tensor_copy(out=idb[:], in_=idf[:]), ["idf"], ["idb"])
        op("pool", lambda h: h.memset(onesb[:], 1.0), [], ["onesb"])
        op("pool", lambda h: h.memset(epsb[:], EPS), [], ["epsb"])
        op("pool", lambda h: h.memset(kTx[:], 0.0), [], ["kTx"])
        op("pool", lambda h: h.memset(vx[:], 0.0), [], ["vx"])
        op("pool", lambda h: h.memset(car[:], 0.0), [], ["car"])
        op("pool", lambda h: h.memset(gcar[:], 0.0), [], ["gcar"])
        pass
        op("pool", lambda h: h.memset(hfin[:], 0.0))
        op("pool", lambda h: h.memset(sconv[:], 0.0))
        rho = sb("rho", [128, 16]); fre = sb("fre", [128, 16]); fim = sb("fim", [128, 16])
        dtl, thl, c1, s1, tmpa, kfL = (big[:, 9088 + 16 * i:9104 + 16 * i] for i in range(6))
        kiL = big[:, 9184:9200].bitcast(mybir.dt.int32)
        op("act", lambda h: h.activation(out=dtl[:], in_=lam[:, 2, :], func=AF.Exp), ["lam"], ["dtl"])
        op("dve", lambda h: h.tensor_tensor(out=thl[:], in0=lam[:, 1, :], in1=dtl[:], op=ALU.mult), ["lam", "dtl"], ["thl"])
        op("dve", lambda h: h.tensor_tensor(out=tmpa[:], in0=lam[:, 0, :], in1=dtl[:], op=ALU.mult), ["lam", "dtl"], ["tmpa"])
        op("act", lambda h: h.activation(out=rho[:], in_=tmpa[:], func=AF.Exp), ["tmpa"], ["rho"])

        def sincos(eng_tag, th_ap, s_ap, c_ap, tmp_ap, shape_key, ki_ap, kf_ap):
            K = shape_key

            def reduce_(shift, dst_key):
                op("dve", lambda h: h.tensor_scalar(out=tmp_ap, in0=th_ap, scalar1=shift, scalar2=1.0 / (2 * PI), op0=ALU.add, op1=ALU.mult),
                   [K + "th", K + "s", K + "c"], [K + "tmp"])
                op("dve", lambda h: h.tensor_copy(out=ki_ap, in_=tmp_ap), [K + "tmp"], [K + "ki"])
                op("dve", lambda h: h.tensor_copy(out=kf_ap, in_=ki_ap), [K + "ki"], [K + "kf"])
                op("dve", lambda h: h.tensor_scalar(out=tmp_ap, in0=th_ap, scalar1=shift, scalar2=None, op0=ALU.add), [K + "th", K + "ki"], [K + "tmp"])
                op("dve", lambda h: h.scalar_tensor_tensor(out=tmp_ap, in0=kf_ap, scalar=-2 * PI, in1=tmp_ap, op0=ALU.mult, op1=ALU.add),
                   [K + "kf", K + "tmp"], [K + "tmp"])
                op("dve", lambda h: h.tensor_scalar(out=kf_ap, in0=tmp_ap, scalar1=PI, scalar2=None, op0=ALU.is_gt), [K + "tmp"], [K + "kf"])
                op("dve", lambda h: h.scalar_tensor_tensor(out=tmp_ap, in0=kf_ap, scalar=-2 * PI, in1=tmp_ap, op0=ALU.mult, op1=ALU.add),
                   [K + "kf", K + "tmp"], [K + "tmp"])
                op("dve", lambda h: h.tensor_scalar(out=kf_ap, in0=tmp_ap, scalar1=-PI, scalar2=None, op0=ALU.is_lt), [K + "tmp"], [K + "kf"])
                op("dve", lambda h: h.scalar_tensor_tensor(out=tmp_ap, in0=kf_ap, scalar=2 * PI, in1=tmp_ap, op0=ALU.mult, op1=ALU.add),
                   [K + "kf", K + "tmp"], [K + "tmp"])
                op("dve", lambda h: h.tensor_scalar(out=tmp_ap, in0=tmp_ap, scalar1=-PI, scalar2=PI, op0=ALU.max, op1=ALU.min), [K + "tmp"], [K + "tmp"])

            reduce_(0.0, "s")
            op("act", lambda h: h.activation(out=s_ap, in_=tmp_ap, func=AF.Sin), [K + "tmp"], [K + "s"])
            reduce_(0.5 * PI, "c")
            op("act", lambda h: h.activation(out=c_ap, in_=tmp_ap, func=AF.Sin), [K + "tmp"], [K + "c"])

        sincos("L", thl[:], s1[:], c1[:], tmpa[:], "L", kiL[:], kfL[:])
        op("dve", lambda h: h.tensor_tensor(out=are[:], in0=rho[:], in1=c1[:], op=ALU.mult), ["rho", "Lc"], ["are"])
        op("dve", lambda h: h.tensor_tensor(out=aim[:], in0=rho[:], in1=s1[:], op=ALU.mult), ["rho", "Ls"], ["aim"])
        op("pool", lambda h: h.memset(cs[:, :, 0:1], 1.0), [], ["cs"])
        op("pool", lambda h: h.memset(sn[:, :, 0:1], 0.0), [], ["sn"])
        op("dve", lambda h: h.tensor_copy(out=cs[:, :, 1], in_=c1[:]), ["Lc", "cs"], ["cs"])
        op("dve", lambda h: h.tensor_copy(out=sn[:, :, 1], in_=s1[:]), ["Ls", "sn"], ["sn"])
        tA = pp2[0][:].rearrange("p a j (b c) -> p (a j b) c", c=64); tB = big[:, 6656:7680].rearrange("p (a c) -> p a c", c=64)
        m = 1
        while m < 128:
            cm = cs[:, :, m:m + 1].broadcast_to([128, 16, m]); sm = sn[:, :, m:m + 1].broadcast_to([128, 16, m])
            a_c = cs[:, :, 1:m + 1]; a_s = sn[:, :, 1:m + 1]
            o_c = cs[:, :, m + 1:2 * m + 1]; o_s = sn[:, :, m + 1:2 * m + 1]
            ta = tA[:, :, 0:m]; tb = tB[:, :, 0:m]
            op("dve", lambda h, a_c=a_c, cm=cm, ta=ta: h.tensor_tensor(out=ta, in0=a_c, in1=cm, op=ALU.mult), ["cs", "sn"], ["tA"])
            op("dve", lambda h, a_s=a_s, sm=sm, tb=tb: h.tensor_tensor(out=tb, in0=a_s, in1=sm, op=ALU.mult), ["cs", "sn"], ["tB"])
            op("dve", lambda h, o_c=o_c, ta=ta, tb=tb: h.tensor_tensor(out=o_c, in0=ta, in1=tb, op=ALU.subtract), ["tA", "tB", "sn"], ["cs"])
            op("dve", lambda h, a_c=a_c, sm=sm, ta=ta: h.tensor_tensor(out=ta, in0=a_c, in1=sm, op=ALU.mult), ["cs", "sn"], ["tA"])
            op("dve", lambda h, a_s=a_s, cm=cm, tb=tb: h.tensor_tensor(out=tb, in0=a_s, in1=cm, op=ALU.mult), ["cs", "sn"], ["tB"])
            op("dve", lambda h, o_s=o_s, ta=ta, tb=tb: h.tensor_tensor(out=o_s, in0=ta, in1=tb, op=ALU.add), ["tA", "tB", "cs"], ["sn"])
            m *= 2
        op("pool", lambda h: h.memset(mask64[:], 1.0))
        op("pool", lambda h: h.memset(mask64[:].rearrange("p (s t) -> p s t", t=4)[:, :, 0:1], 0.0))
        sm_ = [big[:, 9200 + 16 * i:9216 + 16 * i] for i in range(8)]
        lr_, li_ = lam[:, 0, :], lam[:, 1, :]
        nr, den, t_a, t_b, gr, gi = sm_[0], sm_[1], sm_[2], sm_[3], sm_[4], sm_[5]
        TT = lambda o, x, y, f_: op("dve", lambda h: h.tensor_tensor(out=o, in0=x, in1=y, op=f_))
        op("dve", lambda h: h.tensor_scalar(out=nr, in0=are[:], scalar1=-1.0, scalar2=None, op0=ALU.add))
        TT(den, lr_, lr_, ALU.mult); TT(t_a, li_, li_, ALU.mult); TT(den, den, t_a, ALU.add)
        op("dve", lambda h: h.reciprocal(out=den, in_=den))
        TT(t_a, nr, lr_, ALU.mult); TT(t_b, aim[:], li_, ALU.mult); TT(t_a, t_a, t_b, ALU.add); TT(fre[:], t_a, den, ALU.mult)
        TT(t_a, aim[:], lr_, ALU.mult); TT(t_b, nr, li_, ALU.mult); TT(t_a, t_a, t_b, ALU.subtract); TT(fim[:], t_a, den, ALU.mult)
        TT(den, fre[:], fre[:], ALU.mult); TT(t_a, fim[:], fim[:], ALU.mult); TT(den, den, t_a, ALU.add)
        op("dve", lambda h: h.reciprocal(out=den, in_=den))
        TT(gr, fre[:], den, ALU.mult); TT(gi, fim[:], den, ALU.mult)
        op("dve", lambda h: h.tensor_scalar(out=gi, in0=gi, scalar1=-1.0, scalar2=None, op0=ALU.mult))
        P.dma("pool", lambda h: h.dma_start(out=WBre[:], in_=bblk_d[:, 0]))
        P.dma("pool", lambda h: h.dma_start(out=WBim[:], in_=bblk_d[:, 1]))
        cbf = big[:, 0:4096].rearrange("p (a q m) -> p a q m", a=2, q=16)
        u1 = big[:, 4096:6144].rearrange("p (q m) -> p q m", q=16); u2 = big[:, 6144:8192].rearrange("p (q m) -> p q m", q=16)
        P.dma("sp", lambda h: h.dma_start(out=cbf, in_=cblk_d))
        fre_b = fre[:].unsqueeze(2).broadcast_to([128, 16, 128]); fim_b = fim[:].unsqueeze(2).broadcast_to([128, 16, 128])
        TT(u1, cbf[:, 0], fre_b, ALU.mult); TT(u2, cbf[:, 1], fim_b, ALU.mult); TT(WCre[:], u1, u2, ALU.subtract)
        TT(u1, cbf[:, 0], fim_b, ALU.mult); TT(u2, cbf[:, 1], fre_b, ALU.mult); TT(u1, u1, u2, ALU.add)
        op("dve", lambda h: h.tensor_scalar(out=WCimn[:], in0=u1, scalar1=-1.0, scalar2=None, op0=ALU.mult))

        tD = big[:, 9344:9600].rearrange("p (q s) -> p q s", q=16)
        tE = big[:, 8704:8960].rearrange("p (q s) -> p q s", q=16)
        gr_b = gr.unsqueeze(2).broadcast_to([128, 16, NSEQ]); gi_b = gi.unsqueeze(2).broadcast_to([128, 16, NSEQ])
        TT(tD, h0[:, 0], gr_b, ALU.mult); TT(tE, h0[:, 1], gi_b, ALU.mult); TT(tD, tD, tE, ALU.subtract)
        TT(tE, h0[:, 0], gi_b, ALU.mult); TT(h0[:, 0], tD, tD, ALU.max)
        TT(tD, h0[:, 1], gr_b, ALU.mult); TT(h0[:, 1], tE, tD, ALU.add)
        a_re_b = are[:].unsqueeze(2).broadcast_to([128, 16, NSEQ]); a_im_b = aim[:].unsqueeze(2).broadcast_to([128, 16, NSEQ])
        tC = big[:, 8704:8960].rearrange("p (q s) -> p q s", q=16)
        op("dve", lambda h: h.tensor_tensor(out=ah[:, 0], in0=h0[:, 0], in1=a_re_b, op=ALU.mult), ["h0", "are"], ["ah0"])
        op("dve", lambda h: h.tensor_tensor(out=tC[:], in0=h0[:, 1], in1=a_im_b, op=ALU.mult), ["h0", "aim"], ["tC"])
        op("dve", lambda h: h.tensor_tensor(out=ah[:, 0], in0=ah[:, 0], in1=tC[:], op=ALU.subtract), ["ah0", "tC"], ["ah0"])
        op("dve", lambda h: h.tensor_tensor(out=ah[:, 1], in0=h0[:, 0], in1=a_im_b, op=ALU.mult), ["h0", "aim"], ["ah1"])
        op("dve", lambda h: h.tensor_tensor(out=tC[:], in0=h0[:, 1], in1=a_re_b, op=ALU.mult), ["h0", "are", "ah0"], ["tC"])
        op("dve", lambda h: h.tensor_tensor(out=ah[:, 1], in0=ah[:, 1], in1=tC[:], op=ALU.add), ["ah1", "tC"], ["ah1"])


        def rms(src_ap, key_src, n, slot, scale, pn):
            op("act", lambda h: h.activation(out=junk[0:pn, 0:n], in_=src_ap, func=AF.Square, accum_out=ssq[0:pn, slot:slot + 1]),
               [key_src], ["junk", "ssq%d" % slot])
            op("act", lambda h: h.activation(out=rstd[0:pn, slot:slot + 1], in_=ssq[0:pn, slot:slot + 1], func=AF.Ln, scale=scale, bias=epsb[0:pn, 0:1]))
            op("act", lambda h: h.activation(out=rstd[0:pn, slot:slot + 1], in_=rstd[0:pn, slot:slot + 1], func=AF.Exp, scale=-0.5))

        def partA(ti, tl):
            sample = (ti == NPT)
            xt = xb[:, tl, :]
            n = 128
            ns = NS if sample else 128
            r0 = ti * 128
            par = ti % 2
            if sample:
                op("pool", lambda h: h.memset(kTs[:], 0.0))
                P.dma("pool", lambda h: h.dma_start(out=kTs[0:64, 0, 0:NSEQ * 128], in_=kcT_d[0:64, :]))
                P.dma("pool", lambda h: h.dma_start(out=kTs[64:128, 1, 0:NSEQ * 128], in_=kcT_d[64:128, :]))
                P.dma("pool", lambda h: h.dma_start(out=vs[:, 0:NSEQ, :], in_=vc_d))
                P.dma("sp", lambda h: h.dma_start(out=cvst, in_=cvst_d))
            rms(xt[:], "xt", D, 0, 1.0 / D, 128)
            op("dve", lambda h: h.tensor_scalar(out=xn[:], in0=xt[:], scalar1=rstd[:, 0:1], scalar2=None, op0=ALU.mult))
            for c in range(8):
                op("pe", lambda h, c=c: h.transpose(out=pT[:, c, :], in_=xn[:, c * 128:(c + 1) * 128], identity=idb[:]))
            op("dve", lambda h: h.tensor_tensor(out=xnT[:], in0=pT[:], in1=pcol[:, GMIX:GMIX + 8].unsqueeze(2).broadcast_to([128, 8, 128]), op=ALU.mult))
            W = wi[par]
            yield
            for i in range(4):
                bank = acc0 if i % 2 == 0 else acc1
                for c in range(8):
                    op("pe", lambda h, i=i, c=c, bank=bank: h.matmul(bank[:, 0:128], lhsT=W[:, c, i * 128:(i + 1) * 128], rhs=xnT[:, c, :],
                                                                   start=(c == 0), stop=(c == 7)))
                op("act", lambda h, i=i, bank=bank: h.activation(out=qT[:, i, :], in_=bank[:, 0:128], func=AF.Copy))
            yield
            for c in range(8):
                op("pe", lambda h, c=c: h.matmul(Ops[:, 0:128], lhsT=W[:, c, 512:640], rhs=xnT[:, c, :], start=(c == 0), stop=(c == 7)))
            if sample:
                op("dve", lambda h: h.tensor_copy(out=kTs[0:64, 0, NSEQ * 128:17 * 128], in_=Ops[0:64, 0:128]))
                op("dve", lambda h: h.tensor_copy(out=kTs[64:128, 1, NSEQ * 128:17 * 128], in_=Ops[64:128, 0:128]))
            else:
                op("dve", lambda h: h.tensor_copy(out=kTx[0:64, 0, 128:256], in_=Ops[0:64, 0:128]))
                op("dve", lambda h: h.tensor_copy(out=kTx[64:128, 1, 128:256], in_=Ops[64:128, 0:128]))
            yield
            for i in range(4):
                bank = acc0 if i % 2 == 0 else acc1
                for c in range(8):
                    op("pe", lambda h, i=i, c=c, bank=bank: h.matmul(bank[:, 0:128], lhsT=W[:, c, 768 + i * 128:768 + (i + 1) * 128], rhs=xnT[:, c, :],
                                                                   start=(c == 0), stop=(c == 7)))
                op("act", lambda h, i=i, bank=bank: h.activation(out=uT2[par][:, i, :], in_=bank[:, 0:128], func=AF.Copy))
            yield
            for c in range(8):
                op("pe", lambda h, c=c: h.matmul(Ops[:, 0:256], lhsT=xnT[:, c, :], rhs=W[:, c, 512:768], start=(c == 0), stop=(c == 7)))
            if sample:
                op("dve", lambda h: h.tensor_copy(out=vs[:, 16, :], in_=Ops[:, 128:256]))
            else:
                op("dve", lambda h: h.tensor_copy(out=vx[:, 1, :], in_=Ops[:, 128:256]))
            if sample or ti == NPT - 1:
                op("dve", lambda h: h.tensor_copy(out=kvtok[:], in_=Ops[:, 0:256]))
                if sample:
                    P.dma("sp", lambda h: h.dma_start(out=kvs_k[:, 124:128, :], in_=kvtok[0:NS, 0:128]))
                    P.dma("sp", lambda h: h.dma_start(out=kvs_v[:, 124:128, :], in_=kvtok[0:NS, 128:256]))
                else:
                    P.dma("sp", lambda h: h.dma_start(out=kvp_d, in_=kvtok[:]))

            if not sample:
                mi = 0 if ti == 0 else (1 if ti == 1 else 2)
                for i in range(4):
                    for hh_ in range(2):
                        op("pe", lambda h, i=i, hh_=hh_: h.matmul(Sps[:, hh_ * 256:(hh_ + 1) * 256], lhsT=qT[:, i, :], rhs=kTx[:, hh_, :], start=True, stop=True))
                    for hh_ in range(2):
                        op("dve", lambda h, hh_=hh_, hd=i + 4 * hh_: h.tensor_scalar(out=Sx[:, 0, hh_, 256:257], in0=sink8[:, hd:hd + 1], scalar1=8.0, scalar2=None, op0=ALU.mult))
                    op("dve", lambda h: h.tensor_tensor(out=Sx[:, 0, :, 0:256], in0=Sps[:].rearrange("p (a k) -> p a k", a=2),
                                                        in1=masks[:, mi:mi + 1, :].broadcast_to([128, 2, 256]), op=ALU.add))
                    op("dve", lambda h: h.tensor_reduce(out=mx[:], in_=Sx[:, 0], axis=AX.X, op=ALU.max))
                    op("dve", lambda h: h.tensor_scalar(out=nbias[:], in0=mx[:], scalar1=-0.125, scalar2=None, op0=ALU.mult))
                    for hh_ in range(2):
                        op("act", lambda h, hh_=hh_: h.activation(out=Pb[:, hh_, :], in_=Sx[:, 0, hh_, :], func=AF.Exp, scale=0.125,
                                                                 bias=nbias[:, hh_:hh_ + 1], accum_out=rs[:, hh_:hh_ + 1]))
                    op("dve", lambda h: h.reciprocal(out=rinv[:], in_=rs[:]))
                    for hh_ in range(2):
                        for blk in range(2):
                            op("pe", lambda h, hh_=hh_, blk=blk: h.transpose(out=pT[:, hh_ * 2 + blk, :], in_=Pb[:, hh_, blk * 128:(blk + 1) * 128], identity=idb[:]))
                    op("act", lambda h: h.activation(out=PTs[:], in_=pT[:, 0:4, :], func=AF.Copy))
                    for hh_ in range(2):
                        for blk in range(2):
                            op("pe", lambda h, hh_=hh_, blk=blk: h.matmul(Ops[:, hh_ * 64:(hh_ + 1) * 64], lhsT=PTs[:, hh_ * 2 + blk, :],
                                                                         rhs=vx[:, blk, hh_ * 64:(hh_ + 1) * 64], start=(blk == 0), stop=(blk == 1)))
                    for hh_ in range(2):
                        hd = i + 4 * hh_
                        op("dve", lambda h, hh_=hh_, hd=hd: h.tensor_scalar(out=attn[:, hd * 64:(hd + 1) * 64], in0=Ops[:, hh_ * 64:(hh_ + 1) * 64],
                                                                           scalar1=rinv[:, hh_:hh_ + 1], scalar2=None, op0=ALU.mult))
                    yield
                op("pool", lambda h: h.tensor_copy(out=kTx[:, :, 0:128], in_=kTx[:, :, 128:256]))
                op("pool", lambda h: h.tensor_copy(out=vx[:, 0, :], in_=vx[:, 1, :]))
            else:
                W17 = 17 * 128
                for hd in range(8):
                    i, hh_ = hd % 4, hd // 4
                    for cb in range(5):
                        c0 = cb * 512
                        cw = min(512, W17 - c0)
                        bank = mmA if cb % 2 == 0 else mmB
                        op("pe", lambda h, i=i, hh_=hh_, c0=c0, cw=cw, bank=bank: h.matmul(bank[:, 0:cw], lhsT=qT[:, i, :], rhs=kTs[:, hh_, c0:c0 + cw], start=True, stop=True))
                        op("dve", lambda h, c0=c0, cw=cw, bank=bank: h.tensor_tensor(out=Ssx[:, c0:c0 + cw], in0=bank[:, 0:cw], in1=smask[:, c0:c0 + cw], op=ALU.add))
                    op("dve", lambda h, hd=hd: h.tensor_scalar(out=Ssx[:, W17:W17 + 1], in0=sink8[:, hd:hd + 1], scalar1=8.0, scalar2=None, op0=ALU.mult))
                    op("dve", lambda h: h.tensor_reduce(out=mx[:, 0:1], in_=Ssx[:], axis=AX.X, op=ALU.max))
                    op("dve", lambda h: h.tensor_scalar(out=nbias[:, 0:1], in0=mx[:, 0:1], scalar1=-0.125, scalar2=None, op0=ALU.mult))
                    op("act", lambda h: h.activation(out=Psb[:], in_=Ssx[:], func=AF.Exp, scale=0.125, bias=nbias[:, 0:1], accum_out=rs[:, 0:1]))
                    op("dve", lambda h: h.reciprocal(out=rinv[:, 0:1], in_=rs[:, 0:1]))
                    for g8 in range(3):
                        nb_ = 8 if g8 < 2 else 1
                        for b in range(nb_):
                            blk = g8 * 8 + b
                            op("pe", lambda h, b=b, blk=blk: h.transpose(out=pT[:, b, :], in_=Psb[:, blk * 128:(blk + 1) * 128], identity=idb[:]))
                        op("act", lambda h, g8=g8, nb_=nb_: h.activation(out=PTss[:, g8 * 8:g8 * 8 + nb_, :], in_=pT[:, 0:nb_, :], func=AF.Copy))
                    for blk in range(17):
                        op("pe", lambda h, blk=blk, hh_=hh_: h.matmul(Ops[:, 0:64], lhsT=PTss[:, blk, :], rhs=vs[:, blk, hh_ * 64:(hh_ + 1) * 64],
                                                                     start=(blk == 0), stop=(blk == 16)))
                    op("dve", lambda h, hd=hd: h.tensor_scalar(out=attn[:, hd * 64:(hd + 1) * 64], in0=Ops[:, 0:64], scalar1=rinv[:, 0:1], scalar2=None, op0=ALU.mult))
            rms(attn[0:n, :], "attn", 512, 1, 1.0 / 512, n)
            op("dve", lambda h: h.tensor_scalar(out=anb[0:n, :], in0=attn[0:n, :], scalar1=rstd[0:n, 1:2], scalar2=None, op0=ALU.mult), ["attn", "rstd1"], ["anb"])
            for c in range(4):
                op("pe", lambda h, c=c: h.transpose(out=pT[:, c, 0:n], in_=anb[0:n, c * 128:(c + 1) * 128], identity=idb[0:n, 0:n]), ["anb", "idb"], ["pT"])
            op("dve", lambda h: h.tensor_tensor(out=mixT2[par][:, 0:4, :], in0=pT[:, 0:4, :], in1=pcol[:, GATT:GATT + 4].unsqueeze(2).broadcast_to([128, 4, 128]), op=ALU.mult))

            yield

        def partB(ti, tl):
            sample = (ti == NPT)
            xt = xb[:, tl, :]
            n = 128
            ns = NS if sample else 128
            par = ti % 2
            uT = uT2[par]
            mixT = mixT2[par]
            def ssm_views(k):
                c4, hp = k // 2, k % 2
                q0 = c4 * 4 + 2 * hp
                bi = k % 2
                Bv = Bs2[bi][:].rearrange("p (r j k) -> p r j k", r=2, j=2)
                if sample:
                    csq = cs[:, q0:q0 + 2, 1:5].unsqueeze(2).broadcast_to([128, 2, NSEQ, 4])
                    snq = sn[:, q0:q0 + 2, 1:5].unsqueeze(2).broadcast_to([128, 2, NSEQ, 4])

                    def v3(ap):
                        return ap[:, :, 0:NS].rearrange("p j (s t) -> p j s t", t=4)
                else:
                    csq = cs[:, q0:q0 + 2, 1:129]; snq = sn[:, q0:q0 + 2, 1:129]

                    def v3(ap):
                        return ap
                T = [v3(pp2[bi][:, kk]) for kk in range(4)]
                RR = [v3(rr2[bi][:, kk]) for kk in range(2)]
                VV = [v3(vv2[bi][:, kk]) for kk in range(2)]
                return c4, hp, q0, bi, Bv, csq, snq, v3, T, RR, VV

            def ssm_front(k):
                c4, hp, q0, bi, Bv, csq, snq, v3, T, RR, VV = ssm_views(k)
                bank = mmA if bi == 0 else mmC
                for ri, WBx in enumerate((WBre, WBim)):
                    for j in range(2):
                        op("pe", lambda h: h.matmul(bank[:, (ri * 2 + j) * 128:(ri * 2 + j + 1) * 128], lhsT=WBx[:, q0 + j, :], rhs=uT[:, c4, :], start=True, stop=True))
                op("act", lambda h: h.activation(out=Bs2[bi][:], in_=bank[:, :], func=AF.Copy))
                if sample:
                    for ri in range(2):
                        v4 = Bv[:, ri, :, 0:NS].rearrange("p j (s t) -> p j s t", t=4)[:, :, :, 0]
                        op("dve", lambda h: h.tensor_tensor(out=v4, in0=v4, in1=ah[:, ri, q0:q0 + 2, :], op=ALU.add))
                Bre, Bim = v3(Bv[:, 0]), v3(Bv[:, 1])
                op("dve", lambda h: h.tensor_tensor(out=T[0], in0=Bre, in1=csq, op=ALU.mult))
                op("dve", lambda h: h.tensor_tensor(out=T[1], in0=Bim, in1=snq, op=ALU.mult))
                op("dve", lambda h: h.tensor_tensor(out=T[2], in0=Bim, in1=csq, op=ALU.mult))
                op("dve", lambda h: h.tensor_tensor(out=T[3], in0=Bre, in1=snq, op=ALU.mult))
                op("dve", lambda h: h.tensor_tensor(out=RR[0], in0=T[0], in1=T[1], op=ALU.add))
                op("dve", lambda h: h.tensor_tensor(out=RR[1], in0=T[2], in1=T[3], op=ALU.subtract))

            def ssm_back(k):
                c4, hp, q0, bi, Bv, csq, snq, v3, T, RR, VV = ssm_views(k)
                rrb, vvb = rr2[bi], vv2[bi]
                for j in range(2):
                    q = q0 + j
                    if sample:
                        op("dve", lambda h: h.tensor_scalar(out=dtmp[:], in0=mask64[:], scalar1=rho[:, q:q + 1], scalar2=None, op0=ALU.mult))
                    for ri in range(2):
                        if sample:
                            op("dve", lambda h: h.tensor_tensor_scan(out=vvb[:, ri, j, 0:NS], data0=dtmp[:], data1=rrb[:, ri, j, 0:NS], initial=0.0, op0=ALU.mult, op1=ALU.add))
                        else:
                            op("dve", lambda h: h.tensor_tensor_scan(out=vvb[:, ri, j, :], data0=rho[:, q:q + 1].broadcast_to([128, 128]), data1=rrb[:, ri, j, :],
                                                                   initial=car[:, ri, q:q + 1], op0=ALU.mult, op1=ALU.add))
                op("dve", lambda h: h.tensor_tensor(out=T[0], in0=VV[0], in1=csq, op=ALU.mult))
                op("dve", lambda h: h.tensor_tensor(out=T[1], in0=VV[1], in1=snq, op=ALU.mult))
                op("dve", lambda h: h.tensor_tensor(out=T[2], in0=VV[0], in1=snq, op=ALU.mult))
                op("dve", lambda h: h.tensor_tensor(out=T[3], in0=VV[1], in1=csq, op=ALU.mult))
                hb_re, hb_im = v3(hb[:, 2 * hp:2 * hp + 2, 0, :]), v3(hb[:, 2 * hp:2 * hp + 2, 1, :])
                op("dve", lambda h: h.tensor_tensor(out=hb_re, in0=T[0], in1=T[1], op=ALU.subtract))
                op("dve", lambda h: h.tensor_tensor(out=hb_im, in0=T[2], in1=T[3], op=ALU.add))
                if sample:
                    op("dve", lambda h: h.tensor_tensor(out=hfin[:, 0, q0:q0 + 2, 0:NSEQ], in0=T[0][:, :, :, 3], in1=T[1][:, :, :, 3], op=ALU.subtract))
                    op("dve", lambda h: h.tensor_tensor(out=hfin[:, 1, q0:q0 + 2, 0:NSEQ], in0=T[2][:, :, :, 3], in1=T[3][:, :, :, 3], op=ALU.add))
                else:
                    op("dve", lambda h: h.tensor_tensor(out=car[:, 0, q0:q0 + 2], in0=T[0][:, :, 127], in1=T[1][:, :, 127], op=ALU.subtract))
                    op("dve", lambda h: h.tensor_tensor(out=car[:, 1, q0:q0 + 2], in0=T[2][:, :, 127], in1=T[3][:, :, 127], op=ALU.add))

            def ssm_y(c4):
                for qq in range(4):
                    q = c4 * 4 + qq
                    op("pe", lambda h: h.matmul(mmB[:, 0:n], lhsT=WCre[:, q, :], rhs=hb[:, qq, 0, 0:n], start=(qq == 0), stop=False))
                    op("pe", lambda h: h.matmul(mmB[:, 0:n], lhsT=WCimn[:, q, :], rhs=hb[:, qq, 1, 0:n], start=False, stop=(qq == 3)))
                op("dve", lambda h: h.scalar_tensor_tensor(out=yv[:, 0:n], in0=uT[:, c4, 0:n], scalar=col(DSK + c4), in1=mmB[:, 0:n], op0=ALU.mult, op1=ALU.add))
                op("act", lambda h: h.activation(out=gl32[:, c4, 0:n], in_=yv[:, 0:n], func=AF.Gelu))
                op("pool", lambda h: h.tensor_copy(out=glb[:, c4, 0:n], in_=gl32[:, c4, 0:n]))

            ssm_front(0)
            yield
            for k in range(8):
                if k + 1 < 8:
                    ssm_front(k + 1)
                ssm_back(k)
                if k % 2 == 1:
                    ssm_y(k // 2)
                yield
            if ti == NPT - 1:
                op("act", lambda h: h.activation(out=hfin[:, :, :, NSEQ], in_=car[:], func=AF.Copy), ["car"], ["hfin"])
            for oc in range(4):
                for c4 in range(4):
                    op("pe", lambda h, oc=oc, c4=c4: h.matmul(mmC[:, 0:n], lhsT=wgl[0][:, c4, oc * 128:(oc + 1) * 128], rhs=glb[:, c4, 0:n], start=(c4 == 0), stop=(c4 == 3)),
                       ["glb", "wgl"], ["mmC"])
                op("act", lambda h, oc=oc: h.activation(out=sg[:, 0:n], in_=mmC[:, 0:n], func=AF.Sigmoid, bias=col(BGLU + oc)), ["mmC", "pcol"], ["sg"])
                op("dve", lambda h, oc=oc: h.tensor_tensor(out=gl32[:, oc, 0:n], in0=gl32[:, oc, 0:n], in1=sg[:, 0:n], op=ALU.mult), ["gl32", "sg", "glb"], ["gl32"])
                op("act", lambda h, oc=oc: h.activation(out=sqb[:, oc, 0:n], in_=gl32[:, oc, 0:n], func=AF.Square), ["gl32"], ["sqb"])
            for oc in range(4):
                op("pe", lambda h, oc=oc: h.matmul(mmC[:, 0:n], lhsT=onesb[:], rhs=sqb[:, oc, 0:n], start=(oc == 0), stop=(oc == 3)), ["sqb", "onesb"], ["mmC"])
            op("act", lambda h: h.activation(out=rsb[:, 0:n], in_=mmC[:, 0:n], func=AF.Ln, scale=1.0 / 512, bias=epsb[:, 0:1]))
            op("act", lambda h: h.activation(out=rsb[:, 0:n], in_=rsb[:, 0:n], func=AF.Exp, scale=-0.5))
            for oc in range(4):
                op("dve", lambda h, oc=oc: h.scalar_tensor_tensor(out=mixT[:, 4 + oc, 0:n], in0=gl32[:, oc, 0:n], scalar=col(GSSM + oc), in1=rsb[:, 0:n], op0=ALU.mult, op1=ALU.mult),
                   ["gl32", "rsb", "pcol"], ["mixT"])

            for hf in range(2):
                acc, ak = (acc0, "acc0") if hf == 0 else (acc1, "acc1")
                for c in range(8):
                    op("pe", lambda h, hf=hf, c=c, acc=acc: h.matmul(acc[0:n, :], lhsT=mixT[:, c, 0:n], rhs=wo[0][:, c, hf * 512:(hf + 1) * 512], start=(c == 0), stop=(c == 7)),
                       ["mixT", "wo"], [ak])
                op("dve", lambda h, hf=hf, acc=acc: h.tensor_tensor(out=xt[0:n, hf * 512:(hf + 1) * 512], in0=xt[0:n, hf * 512:(hf + 1) * 512], in1=acc[0:n, :], op=ALU.add),
                   ["xt", ak], ["xt"])
            rms(xt[0:n, :], "xt", D, 2, 1.0 / D, n)
            op("dve", lambda h: h.tensor_scalar(out=xn[0:n, :], in0=xt[0:n, :], scalar1=rstd[0:n, 2:3], scalar2=None, op0=ALU.mult), ["xt", "rstd2"], ["xn"])
            for c in range(8):
                op("pe", lambda h, c=c: h.transpose(out=pT[:, c, 0:n], in_=xn[0:n, c * 128:(c + 1) * 128], identity=idb[0:n, 0:n]), ["xn", "idb"], ["pT"])
            op("dve", lambda h: h.tensor_tensor(out=xn2T[:, :, tl * 128:(tl + 1) * 128], in0=pT[:], in1=pcol[:, GFFN:GFFN + 8].unsqueeze(2).broadcast_to([128, 8, 128]), op=ALU.mult))

        def ffn(tiles_):
            sample = (tiles_[0] == NPT)
            ntl = len(tiles_)
            nb = ntl * 128
            hTv = hTsm if (sample and ntl == 1) else hTall
            p0 = 128 if sample else 0
            npc = nb - p0
            for ch in range(NFC):
                b3 = ch % 2
                w3 = ch % 3
                P.dma("sp", lambda h: h.dma_start(out=wg[w3][:], in_=wgb_d[ch]))
                P.dma("sp", lambda h: h.dma_start(out=wu[w3][:], in_=wub_d[ch]))
                gps, ups = (mmA, mmB) if b3 == 0 else (mmC, pT32)
                for c in range(8):
                    op("pe", lambda h, c=c, b3=b3, gps=gps: h.matmul(gps[:, 0:nb], lhsT=wg[w3][:, c, :], rhs=xn2T[:, c, 0:nb], start=(c == 0), stop=(c == 7)))
                for c in range(8):
                    op("pe", lambda h, c=c, b3=b3, ups=ups: h.matmul(ups[:, 0:nb], lhsT=wu[w3][:, c, :], rhs=xn2T[:, c, 0:nb], start=(c == 0), stop=(c == 7)))
                w0, w1, w2, bb = col(CVW + ch * 3), col(CVW + ch * 3 + 1), col(CVW + ch * 3 + 2), col(CVB + ch)
                cvb, slb, gxb = cv[:, 0, :], sl[:, 0, :], gx[:, b3, :]
                def conv3(g0, g1, g2, cvv):
                    op("dve", lambda h: h.tensor_scalar(out=cvv, in0=g2, scalar1=w2, scalar2=bb, op0=ALU.mult, op1=ALU.add))
                    op("dve", lambda h: h.scalar_tensor_tensor(out=cvv, in0=g1, scalar=w1, in1=cvv, op0=ALU.mult, op1=ALU.add))
                    op("dve", lambda h: h.scalar_tensor_tensor(out=cvv, in0=g0, scalar=w0, in1=cvv, op0=ALU.mult, op1=ALU.add))

                if sample:
                    op("pool", lambda h: h.tensor_copy(out=gxs[:, :, 0:2], in_=cvst[:, ch, :, :]))
                    op("act", lambda h: h.activation(out=gxs[:, :, 2:6], in_=gps[:, 0:NS].rearrange("p (s t) -> p s t", t=4), func=AF.Copy))
                    op("pool", lambda h: h.tensor_copy(out=sconv[:, ch, :, :], in_=gxs[:, :, 4:6]))
                    conv3(gxs[:, :, 0:4], gxs[:, :, 1:5], gxs[:, :, 2:6], cvb[:, 0:NS].rearrange("p (s t) -> p s t", t=4))
                if npc:
                    op("pool", lambda h: h.tensor_copy(out=gxb[:, 0:2], in_=gcar[:, ch, :]))
                    op("act", lambda h: h.activation(out=gxb[:, 2:2 + npc], in_=gps[:, p0:nb], func=AF.Copy))
                    op("pool", lambda h: h.tensor_copy(out=gcar[:, ch, :], in_=gxb[:, npc:npc + 2]))
                    conv3(gxb[:, 0:npc], gxb[:, 1:1 + npc], gxb[:, 2:2 + npc], cvb[:, p0:nb])
                op("act", lambda h, cvb=cvb, slb=slb: h.activation(out=slb[:, 0:nb], in_=cvb[:, 0:nb], func=AF.Silu))
                op("dve", lambda h, ch=ch, slb=slb: h.tensor_tensor(out=hTv[:, ch, 0:nb], in0=slb[:, 0:nb], in1=ups[:, 0:nb], op=ALU.mult))
            accs = [acc0, acc1, Sps, Ops]
            for hf in range(2):
                for g in range(NFC // 2):
                    wdb = wd[g % 3]
                    P.dma("sp", lambda h: h.dma_start(out=wdb, in_=wdb_d[hf, g]))
                    for j in range(2):
                        ch = 2 * g + j
                        for tl in range(ntl):
                            op("pe", lambda h: h.matmul(accs[tl][:, :], lhsT=hTv[:, ch, tl * 128:(tl + 1) * 128], rhs=wdb[:, j, :],
                                                        start=(ch == 0), stop=(ch == NFC - 1)))
                for tl in range(ntl):
                    op("dve", lambda h, tl=tl, hf=hf: h.tensor_tensor(out=xb[:, tl, hf * 512:(hf + 1) * 512], in0=xb[:, tl, hf * 512:(hf + 1) * 512], in1=accs[tl][:, :], op=ALU.add))
            for tl, ti in enumerate(tiles_):
                xt = xb[:, tl, :]
                rms(xt, "xt", D, 3, 1.0 / D, 128)
                op("dve", lambda h, xt=xt: h.scalar_tensor_tensor(out=xt, in0=xt, scalar=rstd[:, 3:4], in1=gfb[:], op0=ALU.mult, op1=ALU.mult))
                P.dma("sp", lambda h, xt=xt, ti=ti: h.dma_start(out=y_d[ti * 128:(ti + 1) * 128, :], in_=xt))

        def est_dur(kind, eng, fn):
            rec = _Recorder()
            fn(rec)
            name, args, kwargs = rec.call
            o = kwargs.get("out", args[0] if args else None)
            nfree = 1
            if o is not None and hasattr(o, "shape"):
                for d_ in list(o.shape)[1:]:
                    nfree *= int(d_)
            if kind == "dma":
                return 100.0, 2500.0
            if eng == "dve":
                d = ((2 * nfree if name == "tensor_tensor_scan" else nfree) + 151) / 0.96
            elif eng == "act":
                d = (nfree + 230) / 1.2 + (90 if kwargs.get("accum_out") is not None else 0)
            elif eng == "pool":
                d = (2 * nfree + 250) / 1.2
            else:
                d = max(64, nfree) / 1.9 + 25
            return d, d

        def merge_threads(gens):
            gens = [g for g in gens if g is not None]
            bufs = [[] for _ in gens]
            alive = [True] * len(gens)
            tchain = [sched_now[0]] * len(gens)
            while True:
                for i, g in enumerate(gens):
                    while alive[i] and not bufs[i]:
                        P.capture = bufs[i]
                        if next(g, "done") == "done":
                            alive[i] = False
                        P.capture = None
                cands = [i for i in range(len(gens)) if bufs[i]]
                if not cands:
                    break
                best, best_t = None, None
                for i in cands:
                    kind, eng, fn = bufs[i][0]
                    t = max(eng_free.get(eng, 0.0), tchain[i] + 60.0)
                    if best is None or t < best_t:
                        best, best_t = i, t
                kind, eng, fn = bufs[best].pop(0)
                busy, lat = est_dur(kind, eng, fn)
                eng_free[eng] = best_t + busy
                tchain[best] = best_t + lat
                sched_now[0] = max(sched_now[0], best_t)
                (P.dma if kind == "dma" else P.op)(eng, fn)

        eng_free = {}
        sched_now = [0.0]
        if tiles is None:
            blocks = [[0, 1, 2, 3], [4, 5, 6, 7], [8, 9, 10, 11], [12, 13, 14, 15], [NPT, 16]]
        else:
            blocks = tiles
        for blk_ in blocks:
            def load_x(tl, ti):
                P.dma("sp", lambda h: h.dma_start(out=xb[:, tl, :], in_=xin[ti * 128:(ti + 1) * 128, :]))
            lazy = NPT in blk_
            for tl, ti in enumerate(blk_):
                if tl == 0 or not lazy:
                    load_x(tl, ti)
            for _ in partA(blk_[0], 0):
                pass
            for tl, ti in enumerate(blk_):
                gb = partB(ti, tl)
                ga = partA(blk_[tl + 1], tl + 1) if tl + 1 < len(blk_) else None
                if ga is not None and lazy:
                    load_x(tl + 1, blk_[tl + 1])
                merge_threads([gb, ga])
            ffn(blk_)

        fv = pp2[0][:].rearrange("p a j k -> p (a j k)")
        v1 = fv[:, 0:272].rearrange("p (q s) -> p q s", q=16); v2 = fv[:, 272:544].rearrange("p (q s) -> p q s", q=16); v3_ = fv[:, 544:816].rearrange("p (q s) -> p q s", q=16)
        fre_c = fre[:].unsqueeze(2).broadcast_to([128, 16, NSEQ + 1]); fim_c = fim[:].unsqueeze(2).broadcast_to([128, 16, NSEQ + 1])
        TT(v1, hfin[:, 0], fre_c, ALU.mult); TT(v2, hfin[:, 1], fim_c, ALU.mult); TT(v3_, v1, v2, ALU.subtract)
        TT(v1, hfin[:, 0], fim_c, ALU.mult); TT(v2, hfin[:, 1], fre_c, ALU.mult); TT(hfin[:, 1], v1, v2, ALU.add)
        op("dve", lambda h: h.tensor_copy(out=hfin[:, 0], in_=v3_))
        P.dma("sp", lambda h: h.dma_start(out=hfin_d, in_=hfin[:]), reads=["hfin"], writes=["hfin_d"])
        P.dma("sp", lambda h: h.dma_start(out=pconv_d, in_=gcar[:]), reads=["gcar"], writes=["pconv_d"])
        P.dma("sp", lambda h: h.dma_start(out=sconv_d, in_=sconv[:]), reads=["sconv"], writes=["sconv_d"])
        P.limit = None
        P.barrier()

        with nc.Block() as block:
            @block.tensor
            def _(h):
                P.run("pe", h, sems)

            @block.scalar
            def _(h):
                P.run("act", h, sems)

            @block.vector
            def _(h):
                P.run("dve", h, sems)

            @block.gpsimd
            def _(h):
                P.run("pool", h, sems)

            @block.sync
            def _(h):
                P.run("sp", h, sems)
    return nc


def _consts():
    ident = np.eye(128, dtype=np.float32)
    i = np.arange(128)[:, None]
    c = np.arange(256)[None, :]
    full = np.where(((c < 128) & (c > i)) | ((c >= 128) & (c - 128 <= i)), 0.0, MASKV)
    m1 = np.where(((c < 128) & (c > i) & (c >= NPAD)) | ((c >= 128) & (c - 128 <= i)), 0.0, MASKV)
    m0 = np.where((c >= 128) & (c - 128 <= i) & (c - 128 >= NPAD), 0.0, MASKV)
    masks = np.stack([m0, m1, full], axis=1).astype(np.float32)
    sm = np.full((128, 17 * 128), MASKV, np.float32)
    for s in range(NSEQ):
        for t in range(4):
            r = s * 4 + t
            sm[r, s * 128 + t + 1:(s + 1) * 128] = 0.0
            sm[r, 2048 + s * 4:2048 + s * 4 + t + 1] = 0.0
    return ident, masks, sm


def prep_inputs(x_prompt, x_sample, cache_k_win, cache_v_win, state_ssm_re, state_ssm_im, state_conv,
           meta_tokens, g_mix, w_in, sinks, lam_re, lam_im, log_dt, b_re, b_im, c_re, c_im, d_skip,
           w_glu, b_glu, g_attn_out, g_ssm_out, w_o, g_ffn, w_gate, w_up, conv_w, conv_b, w_down,
           g_final):
    f32 = np.float32
    A = lambda a: np.ascontiguousarray(np.asarray(a, dtype=f32))
    x_prompt, x_sample = A(x_prompt), A(x_sample)
    ident, masks, smask = _consts()
    w_in0 = A(w_in)[0]
    perm = []
    for i in range(4):
        perm += list(range(i * 64, (i + 1) * 64)) + list(range((4 + i) * 64, (5 + i) * 64))
    w_in_p = np.ascontiguousarray(np.concatenate([w_in0[:, perm], w_in0[:, 512:]], axis=1))
    pcol = np.zeros((128, 128), f32)
    pcol[:, 0:8] = A(g_mix)[0].reshape(8, 128).T
    pcol[:, 8:16] = A(g_ffn)[0].reshape(8, 128).T
    pcol[:, 16:20] = A(g_attn_out)[0].reshape(4, 128).T
    pcol[:, 20:24] = A(g_ssm_out)[0].reshape(4, 128).T
    pcol[:, 24:28] = A(b_glu)[0].reshape(4, 128).T
    pcol[:, 28:32] = A(d_skip)[0].reshape(4, 128).T
    cw = A(conv_w)[0].reshape(3, NFC, 128)
    pcol[:, 32:98] = cw.transpose(2, 1, 0).reshape(128, 66)
    pcol[:, 98:120] = A(conv_b)[0].reshape(NFC, 128).T
    lr, li, ld = A(lam_re)[0], A(lam_im)[0], A(log_dt)[0]
    ldx = np.repeat(ld[:, None], 64, axis=1)

    def pl(a):
        return a.reshape(16, 2, 64).transpose(1, 2, 0).reshape(128, 16)

    lam = np.ascontiguousarray(np.stack([pl(lr), pl(li), pl(ldx)], axis=1))
    lamb = np.ascontiguousarray(np.stack([lr.reshape(-1), li.reshape(-1), ldx.reshape(-1)], axis=0))
    bre, bim, cre, cim = A(b_re)[0], A(b_im)[0], A(c_re)[0], A(c_im)[0]
    bblk = np.zeros((128, 2, 16, 128), f32)
    cblk = np.zeros((128, 2, 16, 128), f32)
    for q in range(16):
        for j2 in range(2):
            g = 2 * q + j2
            g8 = g % 8
            rows = slice(g8 * 16, g8 * 16 + 16)
            cols = slice(j2 * 64, j2 * 64 + 64)
            bblk[rows, 0, q, cols] = bre[g].T
            bblk[rows, 1, q, cols] = bim[g].T
            cblk[cols, 0, q, rows] = cre[g].T
            cblk[cols, 1, q, rows] = cim[g].T
    sre, sim_ = A(state_ssm_re)[0], A(state_ssm_im)[0]
    ck, cvv = A(cache_k_win)[0].reshape(128, 128, 128), A(cache_v_win)[0].reshape(128, 128, 128)
    sc = A(state_conv)[0]
    meta = A(meta_tokens)
    wg_l = np.ascontiguousarray(A(w_gate)[0].reshape(8, 128, NFC, 128).transpose(2, 1, 0, 3))
    wu_l = np.ascontiguousarray(A(w_up)[0].reshape(8, 128, NFC, 128).transpose(2, 1, 0, 3))
    wd_l = np.ascontiguousarray(A(w_down)[0].reshape(NFC // 2, 2, 128, 2, 512).transpose(3, 0, 2, 1, 4))
    in_maps = []
    for c in range(NCORES):
        xin = np.zeros((NPT * 128 + 128, D), f32)
        xin[NPAD:128] = meta
        xin[128:NPT * 128] = x_prompt[c]
        xin[NPT * 128:NPT * 128 + NS] = x_sample[c * NSEQ:(c + 1) * NSEQ].reshape(NS, D)
        sl_ = slice(c * NSEQ, (c + 1) * NSEQ)

        def hl(a):
            return a.reshape(NSEQ, 16, 2, 64).transpose(2, 3, 1, 0).reshape(128, 16, NSEQ)

        h0 = np.ascontiguousarray(np.stack([hl(sre[sl_]), hl(sim_[sl_])], axis=1))
        cvst = np.ascontiguousarray(sc[sl_].reshape(NSEQ, 2, NFC, 128).transpose(3, 2, 0, 1))
        kc, vc = ck[sl_], cvv[sl_]
        kcT = np.ascontiguousarray(kc.transpose(2, 0, 1).reshape(128, NSEQ * 128))
        vcl = np.ascontiguousarray(vc.transpose(1, 0, 2))
        in_maps.append(dict(
            xin=xin, w_in=w_in_p, w_o=A(w_o)[0], w_glu=A(w_glu)[0], w_gate=wg_l, w_up=wu_l,
            w_down=wd_l, ident=ident, masks=masks, smask=smask, pcol=pcol, sinks=A(sinks)[0],
            gfin=A(g_final), lam=lam, lamb=lamb, bblk=bblk, cblk=cblk, h0=h0, cvst=cvst, kcT=kcT, vc=vcl,
            kcache=np.ascontiguousarray(kc), vcache=np.ascontiguousarray(vc)))
    return in_maps


def assemble(R):
    f32 = np.float32
    y_prompt = np.stack([R[c]["y"][128:NPT * 128] for c in range(NCORES)])
    y_sample = np.concatenate([R[c]["y"][NPT * 128:NPT * 128 + NS].reshape(NSEQ, 4, D) for c in range(NCORES)])
    p_k = np.stack([R[c]["kvp"][:, 0:128].reshape(128, 2, 64) for c in range(NCORES)])[None]
    p_v = np.stack([R[c]["kvp"][:, 128:256].reshape(128, 2, 64) for c in range(NCORES)])[None]
    s_k = np.concatenate([R[c]["kvs_k"].reshape(NSEQ, 128, 2, 64) for c in range(NCORES)])[None]
    s_v = np.concatenate([R[c]["kvs_v"].reshape(NSEQ, 128, 2, 64) for c in range(NCORES)])[None]

    def unh(a):
        nn = a.shape[-1]
        return a.reshape(2, 64, 16, nn).transpose(3, 2, 0, 1).reshape(nn, 32, 64)

    p_re = np.stack([unh(R[c]["hfin"][:, 0, :, NSEQ:])[0] for c in range(NCORES)])[None]
    p_im = np.stack([unh(R[c]["hfin"][:, 1, :, NSEQ:])[0] for c in range(NCORES)])[None]
    s_re = np.concatenate([unh(R[c]["hfin"][:, 0, :, :NSEQ]) for c in range(NCORES)])[None]
    s_im = np.concatenate([unh(R[c]["hfin"][:, 1, :, :NSEQ]) for c in range(NCORES)])[None]
    p_conv = np.stack([R[c]["pconv"].transpose(2, 1, 0).reshape(2, FF) for c in range(NCORES)])[None]
    s_conv = np.concatenate([R[c]["sconv"].transpose(2, 3, 1, 0).reshape(NSEQ, 2, FF) for c in range(NCORES)])[None]
    out = (y_prompt, y_sample, p_k, p_v, p_re, p_im, p_conv, s_k, s_v, s_re, s_im, s_conv)
    return tuple(np.ascontiguousarray(o, dtype=f32) for o in out)


def kernel(**inputs):
    in_maps = prep_inputs(**inputs)
    nc = build_nc()
    res = run_bass_kernel_spmd(nc, in_maps, core_ids=list(range(NCORES)))
    return assemble(res.results)
```

```python
import numpy as np
from contextlib import ExitStack
import concourse.bass as bass
import concourse.mybir as mybir
from concourse.bass_utils import run_bass_kernel_spmd

F32 = mybir.dt.float32
BF16 = mybir.dt.bfloat16
ALU = mybir.AluOpType
AF = mybir.ActivationFunctionType
AX = mybir.AxisListType

ENGS = ("pe", "act", "dve", "pool", "sp")
NDMASEM = 12
NCORES = 8
D = 1024
NPT = 17
NPAD = 112
NS = 64
NSEQ = 16
FF = 2816
NFC = 22
EPS = 1e-5
PI = float(np.pi)
MASKV = -30000.0


class _Recorder:
    def __init__(self):
        self.call = None

    def __getattr__(self, name):
        def f(*args, **kwargs):
            self.call = (name, args, kwargs)
            return self
        return f


class Prog:
    def __init__(self):
        self.streams = {e: [] for e in ENGS}
        self.count = {e: 0 for e in ENGS}
        self.waited = {e: {} for e in ENGS}
        self.regs = {}
        self.nrec = 0
        self.limit = None
        self.capture = None
        self.dma_rr = {"sp": 0, "pool": 0}
        self.dma_cnt = {}

    @staticmethod
    def _region(ap):
        dims = [(int(st), int(sz)) for st, sz in ap.ap]
        off = int(ap.offset)
        esz = mybir.dt.size(ap.dtype)
        space = str(ap.space)
        if space == "DRAM":
            ext = sum((sz - 1) * abs(st) for st, sz in dims)
            return (0, 1, off, off + ext)
        pst, npart = dims[0]
        pst = max(pst, 1)
        p0, f0 = off // pst, off % pst
        ext = sum((sz - 1) * abs(st) for st, sz in dims[1:])
        f0, ext = f0 * esz, ext * esz + esz - 1
        if space == "PSUM":
            return (0, 128, 0, 1 << 30)
        return (p0, p0 + npart, f0, f0 + ext)

    @staticmethod
    def _is_ap(v):
        return hasattr(v, "tensor") and hasattr(v, "ap") and hasattr(v, "offset")

    def _record(self, fn):
        rec = _Recorder()
        fn(rec)
        name, args, kwargs = rec.call
        acc = []
        for i, a in enumerate(args):
            if self._is_ap(a):
                acc.append((a, i == 0))
        for k, v in kwargs.items():
            if self._is_ap(v):
                acc.append((v, k in ("out", "accum_out")))
        out = []
        for ap, w in acc:
            if str(ap.space) == "PSUM":
                w = True
            out.append((ap.tensor.name, self._region(ap), w))
        self._last_call = rec.call
        return out

    def _deps(self, eng, acc):
        deps = {}
        for name, R, w in acc:
            for (R2, w2), evs in self.regs.get(name, {}).items():
                if not (w or w2):
                    continue
                if R[0] < R2[1] and R2[0] < R[1] and R[2] <= R2[3] and R2[2] <= R[3]:
                    for s_, v in evs.items():
                        if s_ == eng and eng == "pe":
                            continue
                        if deps.get(s_, 0) < v:
                            deps[s_] = v
        out = []
        wd = self.waited[eng]
        for s_, v in deps.items():
            if wd.get(s_, 0) < v:
                wd[s_] = v
                out.append((s_, v))
        return out

    def _commit(self, ev, acc):
        s_, v = ev
        for name, R, w in acc:
            d = self.regs.setdefault(name, {})
            if w:
                for key in [k for k in d if k[0][0] >= R[0] and k[0][1] <= R[1] and k[0][2] >= R[2] and k[0][3] <= R[3]]:
                    del d[key]
            e = d.setdefault((R, w), {})
            if e.get(s_, 0) < v:
                e[s_] = v

    @staticmethod
    def _freeze(fn):
        rec = _Recorder()
        fn(rec)
        return lambda h, c=rec.call: getattr(h, c[0])(*c[1], **c[2])

    def op(self, eng, fn, reads=(), writes=()):
        if self.capture is not None:
            self.capture.append(("op", eng, self._freeze(fn)))
            return
        self.nrec += 1
        if self.limit is not None and self.nrec > self.limit:
            return
        acc = self._record(fn)
        deps = self._deps(eng, acc)
        self.count[eng] += 1
        ev = (eng, self.count[eng])
        self.streams[eng].append((deps, self._last_call, (eng, 1)))
        self._commit(ev, acc)

    def dma(self, eng, fn, reads=(), writes=()):
        if self.capture is not None:
            self.capture.append(("dma", eng, self._freeze(fn)))
            return
        self.nrec += 1
        if self.limit is not None and self.nrec > self.limit:
            return
        acc = self._record(fn)
        k = self.dma_rr[eng]
        self.dma_rr[eng] = (k + 1) % NDMASEM
        sname = "dma%s%d" % (eng, k)
        deps = self._deps(eng, acc)
        prev = self.dma_cnt.get(sname, 0) * 16
        if prev and self.waited[eng].get(sname, 0) < prev:
            self.waited[eng][sname] = prev
            deps.append((sname, prev))
        self.dma_cnt[sname] = self.dma_cnt.get(sname, 0) + 1
        ev = (sname, self.dma_cnt[sname] * 16)
        self.streams[eng].append((deps, self._last_call, (sname, 16)))
        self._commit(ev, acc)

    def barrier(self):
        evs = [(e, self.count[e]) for e in ENGS if self.count[e]]
        evs += [(k, c * 16) for k, c in self.dma_cnt.items()]
        for e in ENGS:
            deps = []
            for s, v in evs:
                if s == e:
                    continue
                if self.waited[e].get(s, 0) < v:
                    self.waited[e][s] = v
                    deps.append((s, v))
            if deps:
                self.streams[e].append((deps, None, None))

    def run(self, eng, h, sems):
        for deps, fn, inc in self.streams[eng]:
            for s, v in deps:
                h.wait_ge(sems[s], v)
            if fn is not None:
                name, args, kwargs = fn
                getattr(h, name)(*args, **kwargs).then_inc(sems[inc[0]], inc[1])


def build_nc(tiles=None, limit=None):
    nc = bass.Bass("TRN2", target_bir_lowering=False)

    def din(name, shape, dt=F32):
        return nc.dram_tensor(name, list(shape), dt, kind="ExternalInput").ap()

    def dout(name, shape, dt=F32):
        return nc.dram_tensor(name, list(shape), dt, kind="ExternalOutput").ap()

    xin = din("xin", [NPT * 128 + 128, D])
    w_in = din("w_in", [D, 1280])
    w_o = din("w_o", [D, D])
    w_glu = din("w_glu", [512, 512])
    w_gate = din("w_gate", [NFC, 128, 8, 128])
    w_up = din("w_up", [NFC, 128, 8, 128])
    w_down = din("w_down", [2, NFC // 2, 128, 2, 512])
    ident_d = din("ident", [128, 128])
    masks_d = din("masks", [128, 3, 256])
    smask_d = din("smask", [128, 17 * 128])
    pcol_d = din("pcol", [128, 128])
    sinks_d = din("sinks", [8])
    gfin_d = din("gfin", [D])
    lam_d = din("lam", [128, 3, 16])
    lamb_d = din("lamb", [3, 16 * 128])
    bblk_d = din("bblk", [128, 2, 16, 128])
    cblk_d = din("cblk", [128, 2, 16, 128])
    h0_d = din("h0", [128, 2, 16, NSEQ])
    cvst_d = din("cvst", [128, NFC, NSEQ, 2])
    kcT_d = din("kcT", [128, NSEQ * 128])
    vc_d = din("vc", [128, NSEQ, 128])
    kcache_d = din("kcache", [NSEQ, 128, 128])
    vcache_d = din("vcache", [NSEQ, 128, 128])

    wgb_d = nc.dram_tensor("wgb", [NFC, 128, 8, 128], BF16, kind="Internal").ap()
    wub_d = nc.dram_tensor("wub", [NFC, 128, 8, 128], BF16, kind="Internal").ap()
    wdb_d = nc.dram_tensor("wdb", [2, NFC // 2, 128, 2, 512], BF16, kind="Internal").ap()
    y_d = dout("y", [NPT * 128 + 128, D])
    kvp_d = dout("kvp", [128, 256])
    kvs_k = dout("kvs_k", [NSEQ, 128, 128])
    kvs_v = dout("kvs_v", [NSEQ, 128, 128])
    hfin_d = dout("hfin", [128, 2, 16, NSEQ + 1])
    pconv_d = dout("pconv", [128, NFC, 2])
    sconv_d = dout("sconv", [128, NFC, NSEQ, 2])

    P = Prog()
    P.limit = limit
    es = ExitStack()
    with es:
        def sb(name, shape, dt=F32):
            return es.enter_context(nc.sbuf_tensor(name, list(shape), dt))

        def ps(name, shape, dt=F32):
            return es.enter_context(nc.psum_tensor(name, list(shape), dt))

        sems = {e: es.enter_context(nc.semaphore("s_" + e)) for e in ENGS}
        for k in range(NDMASEM):
            for e_ in ("sp", "pool"):
                sems["dma%s%d" % (e_, k)] = es.enter_context(nc.semaphore("s_dma%s%d" % (e_, k)))

        idb = sb("idb", [128, 128], BF16)
        onesb = sb("onesb", [128, 128], BF16)
        masks = sb("masks_s", [128, 3, 256])
        smask = sb("smask_s", [128, 17 * 128], BF16)
        pcol = sb("pcol_s", [128, 128])
        sink8 = sb("sink8", [128, 8])
        gfb = sb("gfb", [128, D])
        epsb = sb("epsb", [128, 1])
        lam = sb("lam_s", [128, 3, 16])
        WBre = sb("WBre", [128, 16, 128], BF16); WBim = sb("WBim", [128, 16, 128], BF16)
        WCre = sb("WCre", [128, 16, 128], BF16); WCimn = sb("WCimn", [128, 16, 128], BF16)
        cs = sb("cs", [128, 16, 129]); sn = sb("sn", [128, 16, 129])
        mask64 = sb("mask64", [128, NS]); dtmp = sb("dtmp", [128, NS])
        are = sb("are", [128, 16]); aim = sb("aim", [128, 16])
        ah = sb("ah", [128, 2, 16, NSEQ])
        car = sb("car", [128, 2, 16])
        hfin = sb("hfin_s", [128, 2, 16, NSEQ + 1])
        gcar = sb("gcar", [128, NFC, 2])
        sconv = sb("sconv_s", [128, NFC, NSEQ, 2])
        kTx = sb("kTx", [128, 2, 256], BF16)
        vx = sb("vx", [128, 2, 128], BF16)
        GMIX, GFFN, GATT, GSSM, BGLU, DSK, CVW, CVB = 0, 8, 16, 20, 24, 28, 32, 98

        def col(c):
            return pcol[:, c:c + 1]

        ssq = sb("ssq", [128, 4]); rstd = sb("rstd", [128, 4])
        xn = sb("xn", [128, D], BF16)
        junk = xn
        xnT = sb("xnT", [128, 8, 128], BF16)
        qT = sb("qT", [128, 4, 128], BF16)
        uT2 = [sb("uT%d" % i, [128, 4, 128], BF16) for i in range(2)]
        Sx = sb("Sx", [128, 1, 2, 257]); mx = sb("mx", [128, 2]); nbias = sb("nbias", [128, 2])
        rs = sb("rs", [128, 2]); rinv = sb("rinv", [128, 2])
        Pb = sb("Pb", [128, 2, 257], BF16); PTs = sb("PTs", [128, 4, 128], BF16)
        mixT2 = [sb("mixT%d" % i, [128, 8, 128], BF16) for i in range(2)]
        mixT = mixT2[0]
        Psb = sb("Psb", [128, 17 * 128 + 1], BF16)
        PTss = sb("PTss", [128, 17, 128], BF16)
        pp4 = sb("pp4", [128, 4, 4, 128]); rr4 = sb("rr4", [128, 2, 4, 128]); vv4 = sb("vv4", [128, 2, 4, 128])
        hb = sb("hb", [128, 4, 2, 128], BF16)
        gl32 = sb("gl32", [128, 4, 128]); glb = sb("glb", [128, 4, 128], BF16)
        xn2T = sb("xn2T", [128, 8, 512], BF16)
        gx = sb("gx", [128, 2, 514]); gxs = sb("gxs", [128, NSEQ, 6])
        cv = sb("cv", [128, 1, 512]); sl = sb("sl", [128, 1, 512])
        kvtok = sl[:, 0, 0:256]
        yv = gx[:, 0, 0:128]; sg = gx[:, 0, 128:256]; rsb = gx[:, 0, 256:384]
        sqb = gx[:, 1, 0:256].bitcast(BF16).rearrange("p (a k) -> p a k", a=4)
        attn = cv[:, 0, :]
        anb = sl[:, 0, 256:512].bitcast(BF16)
        wi = [sb("wi0", [128, 8, 1280], BF16)] * 2
        wo = [sb("wo0", [128, 8, D], BF16)] * 2
        wgl = [sb("wgl0", [128, 4, 512], BF16)] * 2
        wg = [sb("wg%d" % i, [128, 8, 128], BF16) for i in range(2)] + [xnT]
        wu = [sb("wu%d" % i, [128, 8, 128], BF16) for i in range(2)] + [mixT]
        wdA = sb("wdA", [128, 2, 512], BF16)
        wd = [wdA[:], Sx[:].rearrange("p a b c -> p (a b c)").bitcast(BF16)[:, 0:1024].rearrange("p (j n) -> p j n", j=2),
              xn[:].rearrange("p (j n) -> p j n", j=2)]
        big = sb("big", [128, 9728])
        xb = big[:, 0:4096].rearrange("p (t d) -> p t d", t=4)
        hTall = big[:, 4096:9728].bitcast(BF16).rearrange("p (c n) -> p c n", c=NFC)
        hTsm = big[:, 4096:5504].bitcast(BF16).rearrange("p (c n) -> p c n", c=NFC)
        kTs = big[:, 5504:7680].bitcast(BF16).rearrange("p (a k) -> p a k", a=2)
        vs = big[:, 7680:8768].bitcast(BF16).rearrange("p (b k) -> p b k", b=17)
        scr = big
        h0 = big[:, 8192:8704].rearrange("p (a q s) -> p a q s", a=2, q=16)
        cvst = big[:, 3264:3968].rearrange("p (c s j) -> p c s j", c=NFC, s=NSEQ)
        def prep_views(o):
            return (scr[:, o:o + 768].rearrange("p (a b) -> p a b", a=3), scr[:, o + 768:o + 2304].rearrange("p (a b) -> p a b", a=6),
                    scr[:, o + 2304:o + 2816].rearrange("p (a q m) -> p a q m", a=2, q=2), scr[:, o + 2816:o + 3328].rearrange("p (a q m) -> p a q m", a=2, q=2))
        Ssx = big[:, 1024:1024 + 17 * 128 + 1]
        idf = big[:, 8960:9088]

        mmA = ps("mmA", [128, 512]); mmB = ps("mmB", [128, 512]); mmC = ps("mmC", [128, 512])
        pT = ps("pT", [128, 8, 128], BF16)
        pT32 = pT[:].rearrange("p c k -> p (c k)").bitcast(F32)
        Sps = ps("Sps", [128, 512]); Ops = ps("Ops", [128, 512])
        acc0 = ps("acc0", [128, 512]); acc1 = ps("acc1", [128, 512])

        op = P.op

        P.dma("sp", lambda h: h.dma_start(out=idf[:], in_=ident_d), writes=["idf"])
        P.dma("sp", lambda h: h.dma_start(out=masks[:], in_=masks_d), writes=["masks"])
        P.dma("pool", lambda h: h.dma_start(out=smask[:], in_=smask_d), writes=["smask"])
        P.dma("sp", lambda h: h.dma_start(out=pcol[:], in_=pcol_d), writes=["pcol"])
        P.dma("sp", lambda h: h.dma_start(out=sink8[:], in_=sinks_d.partition_broadcast(128)), writes=["sink8"])
        P.dma("sp", lambda h: h.dma_start(out=gfb[:], in_=gfin_d.partition_broadcast(128)), writes=["gfb"])
        P.dma("sp", lambda h: h.dma_start(out=lam[:], in_=lam_d), writes=["lam"])
        P.dma("sp", lambda h: h.dma_start(out=h0, in_=h0_d))
        op("pool", lambda h: h.memset(hb[:], 0.0))
        op("pool", lambda h: h.memset(cv[:], 0.0))
        op("pool", lambda h: h.memset(gx[:], 0.0))
        P.dma("pool", lambda h: h.dma_start(out=wi[0][:], in_=w_in.rearrange("(c p) n -> p c n", p=128)))
        P.dma("pool", lambda h: h.dma_start(out=wgl[0][:], in_=w_glu.rearrange("(c p) n -> p c n", p=128)))
        P.dma("pool", lambda h: h.dma_start(out=wo[0][:], in_=w_o.rearrange("(c p) n -> p c n", p=128)))
        for a_ in range(0, NFC, 11):
            P.dma("pool", lambda h: h.dma_start(out=wgb_d[a_:a_ + 11], in_=w_gate[a_:a_ + 11]))
            P.dma("pool", lambda h: h.dma_start(out=wub_d[a_:a_ + 11], in_=w_up[a_:a_ + 11]))
        for hf_ in range(2):
            P.dma("pool", lambda h: h.dma_start(out=wdb_d[hf_], in_=w_down[hf_]))
        P.dma("sp", lambda h: h.dma_start(out=kvs_k[:, 0:124, :], in_=kcache_d[:, 4:128, :]), writes=["kvs_k_a"])
        P.dma("sp", lambda h: h.dma_start(out=kvs_v[:, 0:124, :], in_=vcache_d[:, 4:128, :]), writes=["kvs_v_a"])

        op("dve", lambda h: h.tensor_copy(out=idb[:], in_=idf[:]), ["idf"], ["idb"])
        op("pool", lambda h: h.memset(onesb[:], 1.0), [], ["onesb"])
        op("pool", lambda h: h.memset(epsb[:], EPS), [], ["epsb"])
        op("pool", lambda h: h.memset(kTx[:], 0.0), [], ["kTx"])
        op("pool", lambda h: h.memset(vx[:], 0.0), [], ["vx"])
        op("pool", lambda h: h.memset(car[:], 0.0), [], ["car"])
        op("pool", lambda h: h.memset(gcar[:], 0.0), [], ["gcar"])
        pass
        op("pool", lambda h: h.memset(hfin[:], 0.0))
        op("pool", lambda h: h.memset(sconv[:], 0.0))
        rho = sb("rho", [128, 16]); fre = sb("fre", [128, 16]); fim = sb("fim", [128, 16])
        dtl, thl, c1, s1, tmpa, kfL = (big[:, 9088 + 16 * i:9104 + 16 * i] for i in range(6))
        kiL = big[:, 9184:9200].bitcast(mybir.dt.int32)
        op("act", lambda h: h.activation(out=dtl[:], in_=lam[:, 2, :], func=AF.Exp), ["lam"], ["dtl"])
        op("dve", lambda h: h.tensor_tensor(out=thl[:], in0=lam[:, 1, :], in1=dtl[:], op=ALU.mult), ["lam", "dtl"], ["thl"])
        op("dve", lambda h: h.tensor_tensor(out=tmpa[:], in0=lam[:, 0, :], in1=dtl[:], op=ALU.mult), ["lam", "dtl"], ["tmpa"])
        op("act", lambda h: h.activation(out=rho[:], in_=tmpa[:], func=AF.Exp), ["tmpa"], ["rho"])

        def sincos(eng_tag, th_ap, s_ap, c_ap, tmp_ap, shape_key, ki_ap, kf_ap):
            K = shape_key

            def reduce_(shift, dst_key):
                op("dve", lambda h: h.tensor_scalar(out=tmp_ap, in0=th_ap, scalar1=shift, scalar2=1.0 / (2 * PI), op0=ALU.add, op1=ALU.mult),
                   [K + "th", K + "s", K + "c"], [K + "tmp"])
                op("dve", lambda h: h.tensor_copy(out=ki_ap, in_=tmp_ap), [K + "tmp"], [K + "ki"])
                op("dve", lambda h: h.tensor_copy(out=kf_ap, in_=ki_ap), [K + "ki"], [K + "kf"])
                op("dve", lambda h: h.tensor_scalar(out=tmp_ap, in0=th_ap, scalar1=shift, scalar2=None, op0=ALU.add), [K + "th", K + "ki"], [K + "tmp"])
                op("dve", lambda h: h.scalar_tensor_tensor(out=tmp_ap, in0=kf_ap, scalar=-2 * PI, in1=tmp_ap, op0=ALU.mult, op1=ALU.add),
                   [K + "kf", K + "tmp"], [K + "tmp"])
                op("dve", lambda h: h.tensor_scalar(out=kf_ap, in0=tmp_ap, scalar1=PI, scalar2=None, op0=ALU.is_gt), [K + "tmp"], [K + "kf"])
                op("dve", lambda h: h.scalar_tensor_tensor(out=tmp_ap, in0=kf_ap, scalar=-2 * PI, in1=tmp_ap, op0=ALU.mult, op1=ALU.add),
                   [K + "kf", K + "tmp"], [K + "tmp"])
                op("dve", lambda h: h.tensor_scalar(out=kf_ap, in0=tmp_ap, scalar1=-PI, scalar2=None, op0=ALU.is_lt), [K + "tmp"], [K + "kf"])
                op("dve", lambda h: h.scalar_tensor_tensor(out=tmp_ap, in0=kf_ap, scalar=2 * PI, in1=tmp_ap, op0=ALU.mult, op1=ALU.add),
                   [K + "kf", K + "tmp"], [K + "tmp"])
                op("dve", lambda h: h.tensor_scalar(out=tmp_ap, in0=tmp_ap, scalar1=-PI, scalar2=PI, op0=ALU.max, op1=ALU.min), [K + "tmp"], [K + "tmp"])

            reduce_(0.0, "s")
            op("act", lambda h: h.activation(out=s_ap, in_=tmp_ap, func=AF.Sin), [K + "tmp"], [K + "s"])
            reduce_(0.5 * PI, "c")
            op("act", lambda h: h.activation(out=c_ap, in_=tmp_ap, func=AF.Sin), [K + "tmp"], [K + "c"])

        sincos("L", thl[:], s1[:], c1[:], tmpa[:], "L", kiL[:], kfL[:])
        op("dve", lambda h: h.tensor_tensor(out=are[:], in0=rho[:], in1=c1[:], op=ALU.mult), ["rho", "Lc"], ["are"])
        op("dve", lambda h: h.tensor_tensor(out=aim[:], in0=rho[:], in1=s1[:], op=ALU.mult), ["rho", "Ls"], ["aim"])
        op("pool", lambda h: h.memset(cs[:, :, 0:1], 1.0), [], ["cs"])
        op("pool", lambda h: h.memset(sn[:, :, 0:1], 0.0), [], ["sn"])
        op("dve", lambda h: h.tensor_copy(out=cs[:, :, 1], in_=c1[:]), ["Lc", "cs"], ["cs"])
        op("dve", lambda h: h.tensor_copy(out=sn[:, :, 1], in_=s1[:]), ["Ls", "sn"], ["sn"])
        tA = pp4[:, 0:2].rearrange("p a j (b c) -> p (a j b) c", c=64); tB = big[:, 6656:7680].rearrange("p (a c) -> p a c", c=64)
        m = 1
        while m < 128:
            cm = cs[:, :, m:m + 1].broadcast_to([128, 16, m]); sm = sn[:, :, m:m + 1].broadcast_to([128, 16, m])
            a_c = cs[:, :, 1:m + 1]; a_s = sn[:, :, 1:m + 1]
            o_c = cs[:, :, m + 1:2 * m + 1]; o_s = sn[:, :, m + 1:2 * m + 1]
            ta = tA[:, :, 0:m]; tb = tB[:, :, 0:m]
            op("dve", lambda h, a_c=a_c, cm=cm, ta=ta: h.tensor_tensor(out=ta, in0=a_c, in1=cm, op=ALU.mult), ["cs", "sn"], ["tA"])
            op("dve", lambda h, a_s=a_s, sm=sm, tb=tb: h.tensor_tensor(out=tb, in0=a_s, in1=sm, op=ALU.mult), ["cs", "sn"], ["tB"])
            op("dve", lambda h, o_c=o_c, ta=ta, tb=tb: h.tensor_tensor(out=o_c, in0=ta, in1=tb, op=ALU.subtract), ["tA", "tB", "sn"], ["cs"])
            op("dve", lambda h, a_c=a_c, sm=sm, ta=ta: h.tensor_tensor(out=ta, in0=a_c, in1=sm, op=ALU.mult), ["cs", "sn"], ["tA"])
            op("dve", lambda h, a_s=a_s, cm=cm, tb=tb: h.tensor_tensor(out=tb, in0=a_s, in1=cm, op=ALU.mult), ["cs", "sn"], ["tB"])
            op("dve", lambda h, o_s=o_s, ta=ta, tb=tb: h.tensor_tensor(out=o_s, in0=ta, in1=tb, op=ALU.add), ["tA", "tB", "cs"], ["sn"])
            m *= 2
        op("pool", lambda h: h.memset(mask64[:], 1.0))
        op("pool", lambda h: h.memset(mask64[:].rearrange("p (s t) -> p s t", t=4)[:, :, 0:1], 0.0))
        sm_ = [big[:, 9200 + 16 * i:9216 + 16 * i] for i in range(8)]
        lr_, li_ = lam[:, 0, :], lam[:, 1, :]
        nr, den, t_a, t_b, gr, gi = sm_[0], sm_[1], sm_[2], sm_[3], sm_[4], sm_[5]
        TT = lambda o, x, y, f_: op("dve", lambda h: h.tensor_tensor(out=o, in0=x, in1=y, op=f_))
        op("dve", lambda h: h.tensor_scalar(out=nr, in0=are[:], scalar1=-1.0, scalar2=None, op0=ALU.add))
        TT(den, lr_, lr_, ALU.mult); TT(t_a, li_, li_, ALU.mult); TT(den, den, t_a, ALU.add)
        op("dve", lambda h: h.reciprocal(out=den, in_=den))
        TT(t_a, nr, lr_, ALU.mult); TT(t_b, aim[:], li_, ALU.mult); TT(t_a, t_a, t_b, ALU.add); TT(fre[:], t_a, den, ALU.mult)
        TT(t_a, aim[:], lr_, ALU.mult); TT(t_b, nr, li_, ALU.mult); TT(t_a, t_a, t_b, ALU.subtract); TT(fim[:], t_a, den, ALU.mult)
        TT(den, fre[:], fre[:], ALU.mult); TT(t_a, fim[:], fim[:], ALU.mult); TT(den, den, t_a, ALU.add)
        op("dve", lambda h: h.reciprocal(out=den, in_=den))
        TT(gr, fre[:], den, ALU.mult); TT(gi, fim[:], den, ALU.mult)
        op("dve", lambda h: h.tensor_scalar(out=gi, in0=gi, scalar1=-1.0, scalar2=None, op0=ALU.mult))
        P.dma("pool", lambda h: h.dma_start(out=WBre[:], in_=bblk_d[:, 0]))
        P.dma("pool", lambda h: h.dma_start(out=WBim[:], in_=bblk_d[:, 1]))
        cbf = big[:, 0:4096].rearrange("p (a q m) -> p a q m", a=2, q=16)
        u1 = big[:, 4096:6144].rearrange("p (q m) -> p q m", q=16); u2 = big[:, 6144:8192].rearrange("p (q m) -> p q m", q=16)
        P.dma("sp", lambda h: h.dma_start(out=cbf, in_=cblk_d))
        fre_b = fre[:].unsqueeze(2).broadcast_to([128, 16, 128]); fim_b = fim[:].unsqueeze(2).broadcast_to([128, 16, 128])
        TT(u1, cbf[:, 0], fre_b, ALU.mult); TT(u2, cbf[:, 1], fim_b, ALU.mult); TT(WCre[:], u1, u2, ALU.subtract)
        TT(u1, cbf[:, 0], fim_b, ALU.mult); TT(u2, cbf[:, 1], fre_b, ALU.mult); TT(u1, u1, u2, ALU.add)
        op("dve", lambda h: h.tensor_scalar(out=WCimn[:], in0=u1, scalar1=-1.0, scalar2=None, op0=ALU.mult))

        tD = big[:, 9344:9600].rearrange("p (q s) -> p q s", q=16)
        tE = big[:, 8704:8960].rearrange("p (q s) -> p q s", q=16)
        gr_b = gr.unsqueeze(2).broadcast_to([128, 16, NSEQ]); gi_b = gi.unsqueeze(2).broadcast_to([128, 16, NSEQ])
        TT(tD, h0[:, 0], gr_b, ALU.mult); TT(tE, h0[:, 1], gi_b, ALU.mult); TT(tD, tD, tE, ALU.subtract)
        TT(tE, h0[:, 0], gi_b, ALU.mult); TT(h0[:, 0], tD, tD, ALU.max)
        TT(tD, h0[:, 1], gr_b, ALU.mult); TT(h0[:, 1], tE, tD, ALU.add)
        a_re_b = are[:].unsqueeze(2).broadcast_to([128, 16, NSEQ]); a_im_b = aim[:].unsqueeze(2).broadcast_to([128, 16, NSEQ])
        tC = big[:, 8704:8960].rearrange("p (q s) -> p q s", q=16)
        op("dve", lambda h: h.tensor_tensor(out=ah[:, 0], in0=h0[:, 0], in1=a_re_b, op=ALU.mult), ["h0", "are"], ["ah0"])
        op("dve", lambda h: h.tensor_tensor(out=tC[:], in0=h0[:, 1], in1=a_im_b, op=ALU.mult), ["h0", "aim"], ["tC"])
        op("dve", lambda h: h.tensor_tensor(out=ah[:, 0], in0=ah[:, 0], in1=tC[:], op=ALU.subtract), ["ah0", "tC"], ["ah0"])
        op("dve", lambda h: h.tensor_tensor(out=ah[:, 1], in0=h0[:, 0], in1=a_im_b, op=ALU.mult), ["h0", "aim"], ["ah1"])
        op("dve", lambda h: h.tensor_tensor(out=tC[:], in0=h0[:, 1], in1=a_re_b, op=ALU.mult), ["h0", "are", "ah0"], ["tC"])
        op("dve", lambda h: h.tensor_tensor(out=ah[:, 1], in0=ah[:, 1], in1=tC[:], op=ALU.add), ["ah1", "tC"], ["ah1"])


        def rms(src_ap, key_src, n, slot, scale, pn):
            op("act", lambda h: h.activation(out=junk[0:pn, 0:n], in_=src_ap, func=AF.Square, accum_out=ssq[0:pn, slot:slot + 1]),
               [key_src], ["junk", "ssq%d" % slot])
            op("act", lambda h: h.activation(out=rstd[0:pn, slot:slot + 1], in_=ssq[0:pn, slot:slot + 1], func=AF.Ln, scale=scale, bias=epsb[0:pn, 0:1]))
            op("act", lambda h: h.activation(out=rstd[0:pn, slot:slot + 1], in_=rstd[0:pn, slot:slot + 1], func=AF.Exp, scale=-0.5))

        def partA(ti, tl):
            sample = (ti == NPT)
            xt = xb[:, tl, :]
            n = 128
            ns = NS if sample else 128
            r0 = ti * 128
            par = ti % 2
            if sample:
                op("pool", lambda h: h.memset(kTs[:], 0.0))
                P.dma("pool", lambda h: h.dma_start(out=kTs[0:64, 0, 0:NSEQ * 128], in_=kcT_d[0:64, :]))
                P.dma("pool", lambda h: h.dma_start(out=kTs[64:128, 1, 0:NSEQ * 128], in_=kcT_d[64:128, :]))
                P.dma("pool", lambda h: h.dma_start(out=vs[:, 0:NSEQ, :], in_=vc_d))
                P.dma("sp", lambda h: h.dma_start(out=cvst, in_=cvst_d))
            rms(xt[:], "xt", D, 0, 1.0 / D, 128)
            op("dve", lambda h: h.tensor_scalar(out=xn[:], in0=xt[:], scalar1=rstd[:, 0:1], scalar2=None, op0=ALU.mult))
            for c in range(8):
                op("pe", lambda h, c=c: h.transpose(out=pT[:, c, :], in_=xn[:, c * 128:(c + 1) * 128], identity=idb[:]))
            op("dve", lambda h: h.tensor_tensor(out=xnT[:], in0=pT[:], in1=pcol[:, GMIX:GMIX + 8].unsqueeze(2).broadcast_to([128, 8, 128]), op=ALU.mult))
            W = wi[par]
            yield
            for i in range(4):
                bank = acc0 if i % 2 == 0 else acc1
                for c in range(8):
                    op("pe", lambda h, i=i, c=c, bank=bank: h.matmul(bank[:, 0:128], lhsT=W[:, c, i * 128:(i + 1) * 128], rhs=xnT[:, c, :],
                                                                   start=(c == 0), stop=(c == 7)))
                op("act", lambda h, i=i, bank=bank: h.activation(out=qT[:, i, :], in_=bank[:, 0:128], func=AF.Copy))
            yield
            for c in range(8):
                op("pe", lambda h, c=c: h.matmul(Ops[:, 0:128], lhsT=W[:, c, 512:640], rhs=xnT[:, c, :], start=(c == 0), stop=(c == 7)))
            if sample:
                op("dve", lambda h: h.tensor_copy(out=kTs[0:64, 0, NSEQ * 128:17 * 128], in_=Ops[0:64, 0:128]))
                op("dve", lambda h: h.tensor_copy(out=kTs[64:128, 1, NSEQ * 128:17 * 128], in_=Ops[64:128, 0:128]))
            else:
                op("dve", lambda h: h.tensor_copy(out=kTx[0:64, 0, 128:256], in_=Ops[0:64, 0:128]))
                op("dve", lambda h: h.tensor_copy(out=kTx[64:128, 1, 128:256], in_=Ops[64:128, 0:128]))
            yield
            for i in range(4):
                bank = acc0 if i % 2 == 0 else acc1
                for c in range(8):
                    op("pe", lambda h, i=i, c=c, bank=bank: h.matmul(bank[:, 0:128], lhsT=W[:, c, 768 + i * 128:768 + (i + 1) * 128], rhs=xnT[:, c, :],
                                                                   start=(c == 0), stop=(c == 7)))
                op("act", lambda h, i=i, bank=bank: h.activation(out=uT2[par][:, i, :], in_=bank[:, 0:128], func=AF.Copy))
            yield
            for c in range(8):
                op("pe", lambda h, c=c: h.matmul(Ops[:, 0:256], lhsT=xnT[:, c, :], rhs=W[:, c, 512:768], start=(c == 0), stop=(c == 7)))
            if sample:
                op("dve", lambda h: h.tensor_copy(out=vs[:, 16, :], in_=Ops[:, 128:256]))
            else:
                op("dve", lambda h: h.tensor_copy(out=vx[:, 1, :], in_=Ops[:, 128:256]))
            if sample or ti == NPT - 1:
                op("dve", lambda h: h.tensor_copy(out=kvtok[:], in_=Ops[:, 0:256]))
                if sample:
                    P.dma("sp", lambda h: h.dma_start(out=kvs_k[:, 124:128, :], in_=kvtok[0:NS, 0:128]))
                    P.dma("sp", lambda h: h.dma_start(out=kvs_v[:, 124:128, :], in_=kvtok[0:NS, 128:256]))
                else:
                    P.dma("sp", lambda h: h.dma_start(out=kvp_d, in_=kvtok[:]))

            if not sample:
                mi = 0 if ti == 0 else (1 if ti == 1 else 2)
                for i in range(4):
                    for hh_ in range(2):
                        op("pe", lambda h, i=i, hh_=hh_: h.matmul(Sps[:, hh_ * 256:(hh_ + 1) * 256], lhsT=qT[:, i, :], rhs=kTx[:, hh_, :], start=True, stop=True))
                    for hh_ in range(2):
                        op("dve", lambda h, hh_=hh_, hd=i + 4 * hh_: h.tensor_scalar(out=Sx[:, 0, hh_, 256:257], in0=sink8[:, hd:hd + 1], scalar1=8.0, scalar2=None, op0=ALU.mult))
                    op("dve", lambda h: h.tensor_tensor(out=Sx[:, 0, :, 0:256], in0=Sps[:].rearrange("p (a k) -> p a k", a=2),
                                                        in1=masks[:, mi:mi + 1, :].broadcast_to([128, 2, 256]), op=ALU.add))
                    op("dve", lambda h: h.tensor_reduce(out=mx[:], in_=Sx[:, 0], axis=AX.X, op=ALU.max))
                    op("dve", lambda h: h.tensor_scalar(out=nbias[:], in0=mx[:], scalar1=-0.125, scalar2=None, op0=ALU.mult))
                    for hh_ in range(2):
                        op("act", lambda h, hh_=hh_: h.activation(out=Pb[:, hh_, :], in_=Sx[:, 0, hh_, :], func=AF.Exp, scale=0.125,
                                                                 bias=nbias[:, hh_:hh_ + 1], accum_out=rs[:, hh_:hh_ + 1]))
                    op("dve", lambda h: h.reciprocal(out=rinv[:], in_=rs[:]))
                    for hh_ in range(2):
                        for blk in range(2):
                            op("pe", lambda h, hh_=hh_, blk=blk: h.transpose(out=pT[:, hh_ * 2 + blk, :], in_=Pb[:, hh_, blk * 128:(blk + 1) * 128], identity=idb[:]))
                    op("act", lambda h: h.activation(out=PTs[:], in_=pT[:, 0:4, :], func=AF.Copy))
                    for hh_ in range(2):
                        for blk in range(2):
                            op("pe", lambda h, hh_=hh_, blk=blk: h.matmul(Ops[:, hh_ * 64:(hh_ + 1) * 64], lhsT=PTs[:, hh_ * 2 + blk, :],
                                                                         rhs=vx[:, blk, hh_ * 64:(hh_ + 1) * 64], start=(blk == 0), stop=(blk == 1)))
                    for hh_ in range(2):
                        hd = i + 4 * hh_
                        op("dve", lambda h, hh_=hh_, hd=hd: h.tensor_scalar(out=attn[:, hd * 64:(hd + 1) * 64], in0=Ops[:, hh_ * 64:(hh_ + 1) * 64],
                                                                           scalar1=rinv[:, hh_:hh_ + 1], scalar2=None, op0=ALU.mult))
                    yield
                op("pool", lambda h: h.tensor_copy(out=kTx[:, :, 0:128], in_=kTx[:, :, 128:256]))
                op("pool", lambda h: h.tensor_copy(out=vx[:, 0, :], in_=vx[:, 1, :]))
            else:
                W17 = 17 * 128
                for hd in range(8):
                    i, hh_ = hd % 4, hd // 4
                    for cb in range(5):
                        c0 = cb * 512
                        cw = min(512, W17 - c0)
                        bank = mmA if cb % 2 == 0 else mmB
                        op("pe", lambda h, i=i, hh_=hh_, c0=c0, cw=cw, bank=bank: h.matmul(bank[:, 0:cw], lhsT=qT[:, i, :], rhs=kTs[:, hh_, c0:c0 + cw], start=True, stop=True))
                        op("dve", lambda h, c0=c0, cw=cw, bank=bank: h.tensor_tensor(out=Ssx[:, c0:c0 + cw], in0=bank[:, 0:cw], in1=smask[:, c0:c0 + cw], op=ALU.add))
                    op("dve", lambda h, hd=hd: h.tensor_scalar(out=Ssx[:, W17:W17 + 1], in0=sink8[:, hd:hd + 1], scalar1=8.0, scalar2=None, op0=ALU.mult))
                    op("dve", lambda h: h.tensor_reduce(out=mx[:, 0:1], in_=Ssx[:], axis=AX.X, op=ALU.max))
                    op("dve", lambda h: h.tensor_scalar(out=nbias[:, 0:1], in0=mx[:, 0:1], scalar1=-0.125, scalar2=None, op0=ALU.mult))
                    op("act", lambda h: h.activation(out=Psb[:], in_=Ssx[:], func=AF.Exp, scale=0.125, bias=nbias[:, 0:1], accum_out=rs[:, 0:1]))
                    op("dve", lambda h: h.reciprocal(out=rinv[:, 0:1], in_=rs[:, 0:1]))
                    for g8 in range(3):
                        nb_ = 8 if g8 < 2 else 1
                        for b in range(nb_):
                            blk = g8 * 8 + b
                            op("pe", lambda h, b=b, blk=blk: h.transpose(out=pT[:, b, :], in_=Psb[:, blk * 128:(blk + 1) * 128], identity=idb[:]))
                        op("act", lambda h, g8=g8, nb_=nb_: h.activation(out=PTss[:, g8 * 8:g8 * 8 + nb_, :], in_=pT[:, 0:nb_, :], func=AF.Copy))
                    for blk in range(17):
                        op("pe", lambda h, blk=blk, hh_=hh_: h.matmul(Ops[:, 0:64], lhsT=PTss[:, blk, :], rhs=vs[:, blk, hh_ * 64:(hh_ + 1) * 64],
                                                                     start=(blk == 0), stop=(blk == 16)))
                    op("dve", lambda h, hd=hd: h.tensor_scalar(out=attn[:, hd * 64:(hd + 1) * 64], in0=Ops[:, 0:64], scalar1=rinv[:, 0:1], scalar2=None, op0=ALU.mult))
            rms(attn[0:n, :], "attn", 512, 1, 1.0 / 512, n)
            op("dve", lambda h: h.tensor_scalar(out=anb[0:n, :], in0=attn[0:n, :], scalar1=rstd[0:n, 1:2], scalar2=None, op0=ALU.mult), ["attn", "rstd1"], ["anb"])
            for c in range(4):
                op("pe", lambda h, c=c: h.transpose(out=pT[:, c, 0:n], in_=anb[0:n, c * 128:(c + 1) * 128], identity=idb[0:n, 0:n]), ["anb", "idb"], ["pT"])
            op("dve", lambda h: h.tensor_tensor(out=mixT2[par][:, 0:4, :], in0=pT[:, 0:4, :], in1=pcol[:, GATT:GATT + 4].unsqueeze(2).broadcast_to([128, 4, 128]), op=ALU.mult))

            yield

        def partB(ti, tl):
            sample = (ti == NPT)
            xt = xb[:, tl, :]
            n = 128
            ns = NS if sample else 128
            par = ti % 2
            uT = uT2[par]
            mixT = mixT2[par]
            TT = lambda o, x, y, f_: op("dve", lambda h: h.tensor_tensor(out=o, in0=x, in1=y, op=f_))

            def ssm_mm(c4):
                q0 = 4 * c4
                for ri, (WBx, bank) in enumerate(((WBre, mmA), (WBim, mmC))):
                    for j in range(4):
                        op("pe", lambda h: h.matmul(bank[:, j * 128:(j + 1) * 128], lhsT=WBx[:, q0 + j, :], rhs=uT[:, c4, :], start=True, stop=True))

            def ssm_batch(c4):
                q0 = 4 * c4
                T = [pp4[:, kk] for kk in range(4)]
                Bre = mmA[:, :].rearrange("p (j k) -> p j k", j=4); Bim = mmC[:, :].rearrange("p (j k) -> p j k", j=4)
                if sample:
                    op("act", lambda h: h.activation(out=vv4[:, 0], in_=Bre, func=AF.Copy))
                    op("act", lambda h: h.activation(out=vv4[:, 1], in_=Bim, func=AF.Copy))
                    for ri in range(2):
                        v4 = vv4[:, ri, :, 0:NS].rearrange("p j (s t) -> p j s t", t=4)[:, :, :, 0]
                        TT(v4, v4, ah[:, ri, q0:q0 + 4, :], ALU.add)
                    csq = cs[:, q0:q0 + 4, 1:5].unsqueeze(2).broadcast_to([128, 4, NSEQ, 4])
                    snq = sn[:, q0:q0 + 4, 1:5].unsqueeze(2).broadcast_to([128, 4, NSEQ, 4])
                    v3 = lambda ap: ap[:, :, 0:NS].rearrange("p j (s t) -> p j s t", t=4)
                    Bre, Bim = vv4[:, 0], vv4[:, 1]
                else:
                    csq = cs[:, q0:q0 + 4, 1:129]; snq = sn[:, q0:q0 + 4, 1:129]
                    v3 = lambda ap: ap
                Tv = [v3(t_) for t_ in T]
                RR = [v3(rr4[:, kk]) for kk in range(2)]
                TT(Tv[0], v3(Bre), csq, ALU.mult); TT(Tv[1], v3(Bim), snq, ALU.mult)
                TT(Tv[2], v3(Bim), csq, ALU.mult); TT(Tv[3], v3(Bre), snq, ALU.mult)
                if c4 + 1 < 4:
                    ssm_mm(c4 + 1)
                TT(RR[0], Tv[0], Tv[1], ALU.add); TT(RR[1], Tv[2], Tv[3], ALU.subtract)
                for j in range(4):
                    q = q0 + j
                    if sample:
                        op("dve", lambda h: h.tensor_scalar(out=dtmp[:], in0=mask64[:], scalar1=rho[:, q:q + 1], scalar2=None, op0=ALU.mult))
                    for ri in range(2):
                        if sample:
                            op("dve", lambda h: h.tensor_tensor_scan(out=vv4[:, ri, j, 0:NS], data0=dtmp[:], data1=rr4[:, ri, j, 0:NS], initial=0.0, op0=ALU.mult, op1=ALU.add))
                        else:
                            op("dve", lambda h: h.tensor_tensor_scan(out=vv4[:, ri, j, :], data0=rho[:, q:q + 1].broadcast_to([128, 128]), data1=rr4[:, ri, j, :],
                                                                   initial=car[:, ri, q:q + 1], op0=ALU.mult, op1=ALU.add))
                VV = [v3(vv4[:, kk]) for kk in range(2)]
                TT(Tv[0], VV[0], csq, ALU.mult); TT(Tv[1], VV[1], snq, ALU.mult)
                TT(Tv[2], VV[0], snq, ALU.mult); TT(Tv[3], VV[1], csq, ALU.mult)
                TT(v3(hb[:, :, 0, :]), Tv[0], Tv[1], ALU.subtract); TT(v3(hb[:, :, 1, :]), Tv[2], Tv[3], ALU.add)
                if sample:
                    TT(hfin[:, 0, q0:q0 + 4, 0:NSEQ], Tv[0][:, :, :, 3], Tv[1][:, :, :, 3], ALU.subtract)
                    TT(hfin[:, 1, q0:q0 + 4, 0:NSEQ], Tv[2][:, :, :, 3], Tv[3][:, :, :, 3], ALU.add)
                else:
                    TT(car[:, 0, q0:q0 + 4], Tv[0][:, :, 127], Tv[1][:, :, 127], ALU.subtract)
                    TT(car[:, 1, q0:q0 + 4], Tv[2][:, :, 127], Tv[3][:, :, 127], ALU.add)

            def ssm_y(c4):
                for qq in range(4):
                    q = c4 * 4 + qq
                    op("pe", lambda h: h.matmul(mmB[:, 0:n], lhsT=WCre[:, q, :], rhs=hb[:, qq, 0, 0:n], start=(qq == 0), stop=False))
                    op("pe", lambda h: h.matmul(mmB[:, 0:n], lhsT=WCimn[:, q, :], rhs=hb[:, qq, 1, 0:n], start=False, stop=(qq == 3)))
                op("dve", lambda h: h.scalar_tensor_tensor(out=yv[:, 0:n], in0=uT[:, c4, 0:n], scalar=col(DSK + c4), in1=mmB[:, 0:n], op0=ALU.mult, op1=ALU.add))
                op("act", lambda h: h.activation(out=gl32[:, c4, 0:n], in_=yv[:, 0:n], func=AF.Gelu))
                op("pool", lambda h: h.tensor_copy(out=glb[:, c4, 0:n], in_=gl32[:, c4, 0:n]))

            ssm_mm(0)
            yield
            for c4_ in range(4):
                ssm_batch(c4_)
                yield
                ssm_y(c4_)
                yield
            if ti == NPT - 1:
                op("act", lambda h: h.activation(out=hfin[:, :, :, NSEQ], in_=car[:], func=AF.Copy), ["car"], ["hfin"])
            for oc in range(4):
                for c4 in range(4):
                    op("pe", lambda h, oc=oc, c4=c4: h.matmul(mmC[:, 0:n], lhsT=wgl[0][:, c4, oc * 128:(oc + 1) * 128], rhs=glb[:, c4, 0:n], start=(c4 == 0), stop=(c4 == 3)),
                       ["glb", "wgl"], ["mmC"])
                op("act", lambda h, oc=oc: h.activation(out=sg[:, 0:n], in_=mmC[:, 0:n], func=AF.Sigmoid, bias=col(BGLU + oc)), ["mmC", "pcol"], ["sg"])
                op("dve", lambda h, oc=oc: h.tensor_tensor(out=gl32[:, oc, 0:n], in0=gl32[:, oc, 0:n], in1=sg[:, 0:n], op=ALU.mult), ["gl32", "sg", "glb"], ["gl32"])
                op("act", lambda h, oc=oc: h.activation(out=sqb[:, oc, 0:n], in_=gl32[:, oc, 0:n], func=AF.Square), ["gl32"], ["sqb"])
            for oc in range(4):
                op("pe", lambda h, oc=oc: h.matmul(mmC[:, 0:n], lhsT=onesb[:], rhs=sqb[:, oc, 0:n], start=(oc == 0), stop=(oc == 3)), ["sqb", "onesb"], ["mmC"])
            op("act", lambda h: h.activation(out=rsb[:, 0:n], in_=mmC[:, 0:n], func=AF.Ln, scale=1.0 / 512, bias=epsb[:, 0:1]))
            op("act", lambda h: h.activation(out=rsb[:, 0:n], in_=rsb[:, 0:n], func=AF.Exp, scale=-0.5))
            for oc in range(4):
                op("dve", lambda h, oc=oc: h.scalar_tensor_tensor(out=mixT[:, 4 + oc, 0:n], in0=gl32[:, oc, 0:n], scalar=col(GSSM + oc), in1=rsb[:, 0:n], op0=ALU.mult, op1=ALU.mult),
                   ["gl32", "rsb", "pcol"], ["mixT"])

            for hf in range(2):
                acc, ak = (acc0, "acc0") if hf == 0 else (acc1, "acc1")
                for c in range(8):
                    op("pe", lambda h, hf=hf, c=c, acc=acc: h.matmul(acc[0:n, :], lhsT=mixT[:, c, 0:n], rhs=wo[0][:, c, hf * 512:(hf + 1) * 512], start=(c == 0), stop=(c == 7)),
                       ["mixT", "wo"], [ak])
                op("dve", lambda h, hf=hf, acc=acc: h.tensor_tensor(out=xt[0:n, hf * 512:(hf + 1) * 512], in0=xt[0:n, hf * 512:(hf + 1) * 512], in1=acc[0:n, :], op=ALU.add),
                   ["xt", ak], ["xt"])
            rms(xt[0:n, :], "xt", D, 2, 1.0 / D, n)
            op("dve", lambda h: h.tensor_scalar(out=xn[0:n, :], in0=xt[0:n, :], scalar1=rstd[0:n, 2:3], scalar2=None, op0=ALU.mult), ["xt", "rstd2"], ["xn"])
            for c in range(8):
                op("pe", lambda h, c=c: h.transpose(out=pT[:, c, 0:n], in_=xn[0:n, c * 128:(c + 1) * 128], identity=idb[0:n, 0:n]), ["xn", "idb"], ["pT"])
            op("dve", lambda h: h.tensor_tensor(out=xn2T[:, :, tl * 128:(tl + 1) * 128], in0=pT[:], in1=pcol[:, GFFN:GFFN + 8].unsqueeze(2).broadcast_to([128, 8, 128]), op=ALU.mult))

        def ffn(tiles_):
            sample = (tiles_[0] == NPT)
            ntl = len(tiles_)
            nb = ntl * 128
            hTv = hTsm if (sample and ntl == 1) else hTall
            p0 = 128 if sample else 0
            npc = nb - p0
            for ch in range(NFC):
                b3 = ch % 2
                w3 = ch % 3
                P.dma("sp", lambda h: h.dma_start(out=wg[w3][:], in_=wgb_d[ch]))
                P.dma("sp", lambda h: h.dma_start(out=wu[w3][:], in_=wub_d[ch]))
                gps, ups = (mmA, mmB) if b3 == 0 else (mmC, pT32)
                for c in range(8):
                    op("pe", lambda h, c=c, b3=b3, gps=gps: h.matmul(gps[:, 0:nb], lhsT=wg[w3][:, c, :], rhs=xn2T[:, c, 0:nb], start=(c == 0), stop=(c == 7)))
                for c in range(8):
                    op("pe", lambda h, c=c, b3=b3, ups=ups: h.matmul(ups[:, 0:nb], lhsT=wu[w3][:, c, :], rhs=xn2T[:, c, 0:nb], start=(c == 0), stop=(c == 7)))
                w0, w1, w2, bb = col(CVW + ch * 3), col(CVW + ch * 3 + 1), col(CVW + ch * 3 + 2), col(CVB + ch)
                cvb, slb, gxb = cv[:, 0, :], sl[:, 0, :], gx[:, b3, :]
                def conv3(g0, g1, g2, cvv):
                    op("dve", lambda h: h.tensor_scalar(out=cvv, in0=g2, scalar1=w2, scalar2=bb, op0=ALU.mult, op1=ALU.add))
                    op("dve", lambda h: h.scalar_tensor_tensor(out=cvv, in0=g1, scalar=w1, in1=cvv, op0=ALU.mult, op1=ALU.add))
                    op("dve", lambda h: h.scalar_tensor_tensor(out=cvv, in0=g0, scalar=w0, in1=cvv, op0=ALU.mult, op1=ALU.add))

                if sample:
                    op("pool", lambda h: h.tensor_copy(out=gxs[:, :, 0:2], in_=cvst[:, ch, :, :]))
                    op("act", lambda h: h.activation(out=gxs[:, :, 2:6], in_=gps[:, 0:NS].rearrange("p (s t) -> p s t", t=4), func=AF.Copy))
                    op("pool", lambda h: h.tensor_copy(out=sconv[:, ch, :, :], in_=gxs[:, :, 4:6]))
                    conv3(gxs[:, :, 0:4], gxs[:, :, 1:5], gxs[:, :, 2:6], cvb[:, 0:NS].rearrange("p (s t) -> p s t", t=4))
                if npc:
                    op("pool", lambda h: h.tensor_copy(out=gxb[:, 0:2], in_=gcar[:, ch, :]))
                    op("act", lambda h: h.activation(out=gxb[:, 2:2 + npc], in_=gps[:, p0:nb], func=AF.Copy))
                    op("pool", lambda h: h.tensor_copy(out=gcar[:, ch, :], in_=gxb[:, npc:npc + 2]))
                    conv3(gxb[:, 0:npc], gxb[:, 1:1 + npc], gxb[:, 2:2 + npc], cvb[:, p0:nb])
                op("act", lambda h, cvb=cvb, slb=slb: h.activation(out=slb[:, 0:nb], in_=cvb[:, 0:nb], func=AF.Silu))
                op("dve", lambda h, ch=ch, slb=slb: h.tensor_tensor(out=hTv[:, ch, 0:nb], in0=slb[:, 0:nb], in1=ups[:, 0:nb], op=ALU.mult))
            accs = [acc0, acc1, Sps, Ops]
            for hf in range(2):
                for g in range(NFC // 2):
                    wdb = wd[g % 3]
                    P.dma("sp", lambda h: h.dma_start(out=wdb, in_=wdb_d[hf, g]))
                    for j in range(2):
                        ch = 2 * g + j
                        for tl in range(ntl):
                            op("pe", lambda h: h.matmul(accs[tl][:, :], lhsT=hTv[:, ch, tl * 128:(tl + 1) * 128], rhs=wdb[:, j, :],
                                                        start=(ch == 0), stop=(ch == NFC - 1)))
                for tl in range(ntl):
                    op("dve", lambda h, tl=tl, hf=hf: h.tensor_tensor(out=xb[:, tl, hf * 512:(hf + 1) * 512], in0=xb[:, tl, hf * 512:(hf + 1) * 512], in1=accs[tl][:, :], op=ALU.add))
            for tl, ti in enumerate(tiles_):
                xt = xb[:, tl, :]
                rms(xt, "xt", D, 3, 1.0 / D, 128)
                op("dve", lambda h, xt=xt: h.scalar_tensor_tensor(out=xt, in0=xt, scalar=rstd[:, 3:4], in1=gfb[:], op0=ALU.mult, op1=ALU.mult))
                P.dma("sp", lambda h, xt=xt, ti=ti: h.dma_start(out=y_d[ti * 128:(ti + 1) * 128, :], in_=xt))

        def est_dur(kind, eng, fn):
            rec = _Recorder()
            fn(rec)
            name, args, kwargs = rec.call
            o = kwargs.get("out", args[0] if args else None)
            nfree = 1
            if o is not None and hasattr(o, "shape"):
                for d_ in list(o.shape)[1:]:
                    nfree *= int(d_)
            if kind == "dma":
                return 100.0, 2500.0
            if eng == "dve":
                d = ((2 * nfree if name == "tensor_tensor_scan" else nfree) + 151) / 0.96
            elif eng == "act":
                d = (nfree + 230) / 1.2 + (90 if kwargs.get("accum_out") is not None else 0)
            elif eng == "pool":
                d = (2 * nfree + 250) / 1.2
            else:
                d = max(64, nfree) / 1.9 + 25
            return d, d

        def merge_threads(gens):
            gens = [g for g in gens if g is not None]
            bufs = [[] for _ in gens]
            alive = [True] * len(gens)
            tchain = [sched_now[0]] * len(gens)
            while True:
                for i, g in enumerate(gens):
                    while alive[i] and not bufs[i]:
                        P.capture = bufs[i]
                        if next(g, "done") == "done":
                            alive[i] = False
                        P.capture = None
                cands = [i for i in range(len(gens)) if bufs[i]]
                if not cands:
                    break
                best, best_t = None, None
                for i in cands:
                    kind, eng, fn = bufs[i][0]
                    t = max(eng_free.get(eng, 0.0), tchain[i] + 60.0)
                    if best is None or t < best_t:
                        best, best_t = i, t
                kind, eng, fn = bufs[best].pop(0)
                busy, lat = est_dur(kind, eng, fn)
                eng_free[eng] = best_t + busy
                tchain[best] = best_t + lat
                sched_now[0] = max(sched_now[0], best_t)
                (P.dma if kind == "dma" else P.op)(eng, fn)

        eng_free = {}
        sched_now = [0.0]
        if tiles is None:
            blocks = [[0, 1, 2, 3], [4, 5, 6, 7], [8, 9, 10, 11], [12, 13, 14, 15], [NPT, 16]]
        else:
            blocks = tiles
        for blk_ in blocks:
            def load_x(tl, ti):
                P.dma("sp", lambda h: h.dma_start(out=xb[:, tl, :], in_=xin[ti * 128:(ti + 1) * 128, :]))
            lazy = NPT in blk_
            for tl, ti in enumerate(blk_):
                if tl == 0 or not lazy:
                    load_x(tl, ti)
            for _ in partA(blk_[0], 0):
                pass
            for tl, ti in enumerate(blk_):
                gb = partB(ti, tl)
                ga = partA(blk_[tl + 1], tl + 1) if tl + 1 < len(blk_) else None
                if ga is not None and lazy:
                    load_x(tl + 1, blk_[tl + 1])
                merge_threads([gb, ga])
            ffn(blk_)

        fv = pp4[:].rearrange("p a j k -> p (a j k)")
        v1 = fv[:, 0:272].rearrange("p (q s) -> p q s", q=16); v2 = fv[:, 272:544].rearrange("p (q s) -> p q s", q=16); v3_ = fv[:, 544:816].rearrange("p (q s) -> p q s", q=16)
        fre_c = fre[:].unsqueeze(2).broadcast_to([128, 16, NSEQ + 1]); fim_c = fim[:].unsqueeze(2).broadcast_to([128, 16, NSEQ + 1])
        TT(v1, hfin[:, 0], fre_c, ALU.mult); TT(v2, hfin[:, 1], fim_c, ALU.mult); TT(v3_, v1, v2, ALU.subtract)
        TT(v1, hfin[:, 0], fim_c, ALU.mult); TT(v2, hfin[:, 1], fre_c, ALU.mult); TT(hfin[:, 1], v1, v2, ALU.add)
        op("dve", lambda h: h.tensor_copy(out=hfin[:, 0], in_=v3_))
        P.dma("sp", lambda h: h.dma_start(out=hfin_d, in_=hfin[:]), reads=["hfin"], writes=["hfin_d"])
        P.dma("sp", lambda h: h.dma_start(out=pconv_d, in_=gcar[:]), reads=["gcar"], writes=["pconv_d"])
        P.dma("sp", lambda h: h.dma_start(out=sconv_d, in_=sconv[:]), reads=["sconv"], writes=["sconv_d"])
        P.limit = None
        P.barrier()

        with nc.Block() as block:
            @block.tensor
            def _(h):
                P.run("pe", h, sems)

            @block.scalar
            def _(h):
                P.run("act", h, sems)

            @block.vector
            def _(h):
                P.run("dve", h, sems)

            @block.gpsimd
            def _(h):
                P.run("pool", h, sems)

            @block.sync
            def _(h):
                P.run("sp", h, sems)
    return nc


def _consts():
    ident = np.eye(128, dtype=np.float32)
    i = np.arange(128)[:, None]
    c = np.arange(256)[None, :]
    full = np.where(((c < 128) & (c > i)) | ((c >= 128) & (c - 128 <= i)), 0.0, MASKV)
    m1 = np.where(((c < 128) & (c > i) & (c >= NPAD)) | ((c >= 128) & (c - 128 <= i)), 0.0, MASKV)
    m0 = np.where((c >= 128) & (c - 128 <= i) & (c - 128 >= NPAD), 0.0, MASKV)
    masks = np.stack([m0, m1, full], axis=1).astype(np.float32)
    sm = np.full((128, 17 * 128), MASKV, np.float32)
    for s in range(NSEQ):
        for t in range(4):
            r = s * 4 + t
            sm[r, s * 128 + t + 1:(s + 1) * 128] = 0.0
            sm[r, 2048 + s * 4:2048 + s * 4 + t + 1] = 0.0
    return ident, masks, sm


def prep_inputs(x_prompt, x_sample, cache_k_win, cache_v_win, state_ssm_re, state_ssm_im, state_conv,
           meta_tokens, g_mix, w_in, sinks, lam_re, lam_im, log_dt, b_re, b_im, c_re, c_im, d_skip,
           w_glu, b_glu, g_attn_out, g_ssm_out, w_o, g_ffn, w_gate, w_up, conv_w, conv_b, w_down,
           g_final):
    f32 = np.float32
    A = lambda a: np.ascontiguousarray(np.asarray(a, dtype=f32))
    x_prompt, x_sample = A(x_prompt), A(x_sample)
    ident, masks, smask = _consts()
    w_in0 = A(w_in)[0]
    perm = []
    for i in range(4):
        perm += list(range(i * 64, (i + 1) * 64)) + list(range((4 + i) * 64, (5 + i) * 64))
    w_in_p = np.ascontiguousarray(np.concatenate([w_in0[:, perm], w_in0[:, 512:]], axis=1))
    pcol = np.zeros((128, 128), f32)
    pcol[:, 0:8] = A(g_mix)[0].reshape(8, 128).T
    pcol[:, 8:16] = A(g_ffn)[0].reshape(8, 128).T
    pcol[:, 16:20] = A(g_attn_out)[0].reshape(4, 128).T
    pcol[:, 20:24] = A(g_ssm_out)[0].reshape(4, 128).T
    pcol[:, 24:28] = A(b_glu)[0].reshape(4, 128).T
    pcol[:, 28:32] = A(d_skip)[0].reshape(4, 128).T
    cw = A(conv_w)[0].reshape(3, NFC, 128)
    pcol[:, 32:98] = cw.transpose(2, 1, 0).reshape(128, 66)
    pcol[:, 98:120] = A(conv_b)[0].reshape(NFC, 128).T
    lr, li, ld = A(lam_re)[0], A(lam_im)[0], A(log_dt)[0]
    ldx = np.repeat(ld[:, None], 64, axis=1)

    def pl(a):
        return a.reshape(16, 2, 64).transpose(1, 2, 0).reshape(128, 16)

    lam = np.ascontiguousarray(np.stack([pl(lr), pl(li), pl(ldx)], axis=1))
    lamb = np.ascontiguousarray(np.stack([lr.reshape(-1), li.reshape(-1), ldx.reshape(-1)], axis=0))
    bre, bim, cre, cim = A(b_re)[0], A(b_im)[0], A(c_re)[0], A(c_im)[0]
    bblk = np.zeros((128, 2, 16, 128), f32)
    cblk = np.zeros((128, 2, 16, 128), f32)
    for q in range(16):
        for j2 in range(2):
            g = 2 * q + j2
            g8 = g % 8
            rows = slice(g8 * 16, g8 * 16 + 16)
            cols = slice(j2 * 64, j2 * 64 + 64)
            bblk[rows, 0, q, cols] = bre[g].T
            bblk[rows, 1, q, cols] = bim[g].T
            cblk[cols, 0, q, rows] = cre[g].T
            cblk[cols, 1, q, rows] = cim[g].T
    sre, sim_ = A(state_ssm_re)[0], A(state_ssm_im)[0]
    ck, cvv = A(cache_k_win)[0].reshape(128, 128, 128), A(cache_v_win)[0].reshape(128, 128, 128)
    sc = A(state_conv)[0]
    meta = A(meta_tokens)
    wg_l = np.ascontiguousarray(A(w_gate)[0].reshape(8, 128, NFC, 128).transpose(2, 1, 0, 3))
    wu_l = np.ascontiguousarray(A(w_up)[0].reshape(8, 128, NFC, 128).transpose(2, 1, 0, 3))
    wd_l = np.ascontiguousarray(A(w_down)[0].reshape(NFC // 2, 2, 128, 2, 512).transpose(3, 0, 2, 1, 4))
    in_maps = []
    for c in range(NCORES):
        xin = np.zeros((NPT * 128 + 128, D), f32)
        xin[NPAD:128] = meta
        xin[128:NPT * 128] = x_prompt[c]
        xin[NPT * 128:NPT * 128 + NS] = x_sample[c * NSEQ:(c + 1) * NSEQ].reshape(NS, D)
        sl_ = slice(c * NSEQ, (c + 1) * NSEQ)

        def hl(a):
            return a.reshape(NSEQ, 16, 2, 64).transpose(2, 3, 1, 0).reshape(128, 16, NSEQ)

        h0 = np.ascontiguousarray(np.stack([hl(sre[sl_]), hl(sim_[sl_])], axis=1))
        cvst = np.ascontiguousarray(sc[sl_].reshape(NSEQ, 2, NFC, 128).transpose(3, 2, 0, 1))
        kc, vc = ck[sl_], cvv[sl_]
        kcT = np.ascontiguousarray(kc.transpose(2, 0, 1).reshape(128, NSEQ * 128))
        vcl = np.ascontiguousarray(vc.transpose(1, 0, 2))
        in_maps.append(dict(
            xin=xin, w_in=w_in_p, w_o=A(w_o)[0], w_glu=A(w_glu)[0], w_gate=wg_l, w_up=wu_l,
            w_down=wd_l, ident=ident, masks=masks, smask=smask, pcol=pcol, sinks=A(sinks)[0],
            gfin=A(g_final), lam=lam, lamb=lamb, bblk=bblk, cblk=cblk, h0=h0, cvst=cvst, kcT=kcT, vc=vcl,
            kcache=np.ascontiguousarray(kc), vcache=np.ascontiguousarray(vc)))
    return in_maps


def assemble(R):
    f32 = np.float32
    y_prompt = np.stack([R[c]["y"][128:NPT * 128] for c in range(NCORES)])
    y_sample = np.concatenate([R[c]["y"][NPT * 128:NPT * 128 + NS].reshape(NSEQ, 4, D) for c in range(NCORES)])
    p_k = np.stack([R[c]["kvp"][:, 0:128].reshape(128, 2, 64) for c in range(NCORES)])[None]
    p_v = np.stack([R[c]["kvp"][:, 128:256].reshape(128, 2, 64) for c in range(NCORES)])[None]
    s_k = np.concatenate([R[c]["kvs_k"].reshape(NSEQ, 128, 2, 64) for c in range(NCORES)])[None]
    s_v = np.concatenate([R[c]["kvs_v"].reshape(NSEQ, 128, 2, 64) for c in range(NCORES)])[None]

    def unh(a):
        nn = a.shape[-1]
        return a.reshape(2, 64, 16, nn).transpose(3, 2, 0, 1).reshape(nn, 32, 64)

    p_re = np.stack([unh(R[c]["hfin"][:, 0, :, NSEQ:])[0] for c in range(NCORES)])[None]
    p_im = np.stack([unh(R[c]["hfin"][:, 1, :, NSEQ:])[0] for c in range(NCORES)])[None]
    s_re = np.concatenate([unh(R[c]["hfin"][:, 0, :, :NSEQ]) for c in range(NCORES)])[None]
    s_im = np.concatenate([unh(R[c]["hfin"][:, 1, :, :NSEQ]) for c in range(NCORES)])[None]
    p_conv = np.stack([R[c]["pconv"].transpose(2, 1, 0).reshape(2, FF) for c in range(NCORES)])[None]
    s_conv = np.concatenate([R[c]["sconv"].transpose(2, 3, 1, 0).reshape(NSEQ, 2, FF) for c in range(NCORES)])[None]
    out = (y_prompt, y_sample, p_k, p_v, p_re, p_im, p_conv, s_k, s_v, s_re, s_im, s_conv)
    return tuple(np.ascontiguousarray(o, dtype=f32) for o in out)


def kernel(**inputs):
    in_maps = prep_inputs(**inputs)
    nc = build_nc()
    res = run_bass_kernel_spmd(nc, in_maps, core_ids=list(range(NCORES)))
    return assemble(res.results)
```

```python
import numpy as np
from contextlib import ExitStack
import concourse.bass as bass
import concourse.mybir as mybir
from concourse.bass_utils import run_bass_kernel_spmd

F32 = mybir.dt.float32
BF16 = mybir.dt.bfloat16
ALU = mybir.AluOpType
AF = mybir.ActivationFunctionType
AX = mybir.AxisListType

ENGS = ("pe", "act", "dve", "pool", "sp")
NDMASEM = 12
NCORES = 8
D = 1024
NPT = 17
NPAD = 112
NS = 64
NSEQ = 16
FF = 2816
NFC = 22
EPS = 1e-5
PI = float(np.pi)
MASKV = -30000.0


class _Recorder:
    def __init__(self):
        self.call = None

    def __getattr__(self, name):
        def f(*args, **kwargs):
            self.call = (name, args, kwargs)
            return self
        return f


class Prog:
    def __init__(self):
        self.streams = {e: [] for e in ENGS}
        self.count = {e: 0 for e in ENGS}
        self.waited = {e: {} for e in ENGS}
        self.regs = {}
        self.nrec = 0
        self.limit = None
        self.capture = None
        self.dma_rr = {"sp": 0, "pool": 0}
        self.dma_cnt = {}

    @staticmethod
    def _region(ap):
        dims = [(int(st), int(sz)) for st, sz in ap.ap]
        off = int(ap.offset)
        esz = mybir.dt.size(ap.dtype)
        space = str(ap.space)
        if space == "DRAM":
            ext = sum((sz - 1) * abs(st) for st, sz in dims)
            return (0, 1, off, off + ext)
        pst, npart = dims[0]
        pst = max(pst, 1)
        p0, f0 = off // pst, off % pst
        ext = sum((sz - 1) * abs(st) for st, sz in dims[1:])
        f0, ext = f0 * esz, ext * esz + esz - 1
        if space == "PSUM":
            return (0, 128, 0, 1 << 30)
        return (p0, p0 + npart, f0, f0 + ext)

    @staticmethod
    def _is_ap(v):
        return hasattr(v, "tensor") and hasattr(v, "ap") and hasattr(v, "offset")

    def _record(self, fn):
        rec = _Recorder()
        fn(rec)
        name, args, kwargs = rec.call
        acc = []
        for i, a in enumerate(args):
            if self._is_ap(a):
                acc.append((a, i == 0))
        for k, v in kwargs.items():
            if self._is_ap(v):
                acc.append((v, k in ("out", "accum_out")))
        out = []
        for ap, w in acc:
            if str(ap.space) == "PSUM":
                w = True
            out.append((ap.tensor.name, self._region(ap), w))
        self._last_call = rec.call
        return out

    def _deps(self, eng, acc):
        deps = {}
        for name, R, w in acc:
            for (R2, w2), evs in self.regs.get(name, {}).items():
                if not (w or w2):
                    continue
                if R[0] < R2[1] and R2[0] < R[1] and R[2] <= R2[3] and R2[2] <= R[3]:
                    for s_, v in evs.items():
                        if s_ == eng and eng == "pe":
                            continue
                        if deps.get(s_, 0) < v:
                            deps[s_] = v
        out = []
        wd = self.waited[eng]
        for s_, v in deps.items():
            if wd.get(s_, 0) < v:
                wd[s_] = v
                out.append((s_, v))
        return out

    def _commit(self, ev, acc):
        s_, v = ev
        for name, R, w in acc:
            d = self.regs.setdefault(name, {})
            if w:
                for key in [k for k in d if k[0][0] >= R[0] and k[0][1] <= R[1] and k[0][2] >= R[2] and k[0][3] <= R[3]]:
                    del d[key]
            e = d.setdefault((R, w), {})
            if e.get(s_, 0) < v:
                e[s_] = v

    @staticmethod
    def _freeze(fn):
        rec = _Recorder()
        fn(rec)
        return lambda h, c=rec.call: getattr(h, c[0])(*c[1], **c[2])

    def op(self, eng, fn, reads=(), writes=()):
        if self.capture is not None:
            self.capture.append(("op", eng, self._freeze(fn)))
            return
        self.nrec += 1
        if self.limit is not None and self.nrec > self.limit:
            return
        acc = self._record(fn)
        deps = self._deps(eng, acc)
        self.count[eng] += 1
        ev = (eng, self.count[eng])
        self.streams[eng].append((deps, self._last_call, (eng, 1)))
        self._commit(ev, acc)

    def dma(self, eng, fn, reads=(), writes=()):
        if self.capture is not None:
            self.capture.append(("dma", eng, self._freeze(fn)))
            return
        self.nrec += 1
        if self.limit is not None and self.nrec > self.limit:
            return
        acc = self._record(fn)
        k = self.dma_rr[eng]
        self.dma_rr[eng] = (k + 1) % NDMASEM
        sname = "dma%s%d" % (eng, k)
        deps = self._deps(eng, acc)
        prev = self.dma_cnt.get(sname, 0) * 16
        if prev and self.waited[eng].get(sname, 0) < prev:
            self.waited[eng][sname] = prev
            deps.append((sname, prev))
        self.dma_cnt[sname] = self.dma_cnt.get(sname, 0) + 1
        ev = (sname, self.dma_cnt[sname] * 16)
        self.streams[eng].append((deps, self._last_call, (sname, 16)))
        self._commit(ev, acc)

    def barrier(self):
        evs = [(e, self.count[e]) for e in ENGS if self.count[e]]
        evs += [(k, c * 16) for k, c in self.dma_cnt.items()]
        for e in ENGS:
            deps = []
            for s, v in evs:
                if s == e:
                    continue
                if self.waited[e].get(s, 0) < v:
                    self.waited[e][s] = v
                    deps.append((s, v))
            if deps:
                self.streams[e].append((deps, None, None))

    def run(self, eng, h, sems):
        for deps, fn, inc in self.streams[eng]:
            for s, v in deps:
                h.wait_ge(sems[s], v)
            if fn is not None:
                name, args, kwargs = fn
                getattr(h, name)(*args, **kwargs).then_inc(sems[inc[0]], inc[1])


def build_nc(tiles=None, limit=None):
    nc = bass.Bass("TRN2", target_bir_lowering=False)

    def din(name, shape, dt=F32):
        return nc.dram_tensor(name, list(shape), dt, kind="ExternalInput").ap()

    def dout(name, shape, dt=F32):
        return nc.dram_tensor(name, list(shape), dt, kind="ExternalOutput").ap()

    xin = din("xin", [NPT * 128 + 128, D])
    w_in = din("w_in", [D, 1280])
    w_o = din("w_o", [D, D])
    w_glu = din("w_glu", [512, 512])
    w_gate = din("w_gate", [NFC, 128, 8, 128])
    w_up = din("w_up", [NFC, 128, 8, 128])
    w_down = din("w_down", [2, NFC // 2, 128, 2, 512])
    ident_d = din("ident", [128, 128])
    masks_d = din("masks", [128, 3, 256])
    smask_d = din("smask", [128, 17 * 128])
    pcol_d = din("pcol", [128, 128])
    sinks_d = din("sinks", [8])
    gfin_d = din("gfin", [D])
    lam_d = din("lam", [128, 3, 16])
    lamb_d = din("lamb", [3, 16 * 128])
    bblk_d = din("bblk", [128, 2, 16, 128])
    cblk_d = din("cblk", [128, 2, 16, 128])
    h0_d = din("h0", [128, 2, 16, NSEQ])
    cvst_d = din("cvst", [128, NFC, NSEQ, 2])
    kcT_d = din("kcT", [128, NSEQ * 128])
    vc_d = din("vc", [128, NSEQ, 128])
    kcache_d = din("kcache", [NSEQ, 128, 128])
    vcache_d = din("vcache", [NSEQ, 128, 128])

    wgb_d = nc.dram_tensor("wgb", [NFC, 128, 8, 128], BF16, kind="Internal").ap()
    wub_d = nc.dram_tensor("wub", [NFC, 128, 8, 128], BF16, kind="Internal").ap()
    wdb_d = nc.dram_tensor("wdb", [2, NFC // 2, 128, 2, 512], BF16, kind="Internal").ap()
    y_d = dout("y", [NPT * 128 + 128, D])
    kvp_d = dout("kvp", [128, 256])
    kvs_k = dout("kvs_k", [NSEQ, 128, 128])
    kvs_v = dout("kvs_v", [NSEQ, 128, 128])
    hfin_d = dout("hfin", [128, 2, 16, NSEQ + 1])
    pconv_d = dout("pconv", [128, NFC, 2])
    sconv_d = dout("sconv", [128, NFC, NSEQ, 2])

    P = Prog()
    P.limit = limit
    es = ExitStack()
    with es:
        def sb(name, shape, dt=F32):
            return es.enter_context(nc.sbuf_tensor(name, list(shape), dt))

        def ps(name, shape, dt=F32):
            return es.enter_context(nc.psum_tensor(name, list(shape), dt))

        sems = {e: es.enter_context(nc.semaphore("s_" + e)) for e in ENGS}
        for k in range(NDMASEM):
            for e_ in ("sp", "pool"):
                sems["dma%s%d" % (e_, k)] = es.enter_context(nc.semaphore("s_dma%s%d" % (e_, k)))

        idb = sb("idb", [128, 128], BF16)
        onesb = sb("onesb", [128, 128], BF16)
        masks = sb("masks_s", [128, 3, 256])
        smask = sb("smask_s", [128, 17 * 128], BF16)
        pcol = sb("pcol_s", [128, 128])
        sink8 = sb("sink8", [128, 8])
        gfb = sb("gfb", [128, D])
        epsb = sb("epsb", [128, 1])
        lam = sb("lam_s", [128, 3, 16])
        WBre = sb("WBre", [128, 16, 128], BF16); WBim = sb("WBim", [128, 16, 128], BF16)
        WCre = sb("WCre", [128, 16, 128], BF16); WCimn = sb("WCimn", [128, 16, 128], BF16)
        cs = sb("cs", [128, 16, 129]); sn = sb("sn", [128, 16, 129])
        mask64 = sb("mask64", [128, NS]); dtmp = sb("dtmp", [128, NS])
        are = sb("are", [128, 16]); aim = sb("aim", [128, 16])
        ah = sb("ah", [128, 2, 16, NSEQ])
        car = sb("car", [128, 2, 16])
        hfin = sb("hfin_s", [128, 2, 16, NSEQ + 1])
        gcar = sb("gcar", [128, NFC, 2])
        sconv = sb("sconv_s", [128, NFC, NSEQ, 2])
        kTx = sb("kTx", [128, 2, 256], BF16)
        vx = sb("vx", [128, 2, 128], BF16)
        GMIX, GFFN, GATT, GSSM, BGLU, DSK, CVW, CVB = 0, 8, 16, 20, 24, 28, 32, 98

        def col(c):
            return pcol[:, c:c + 1]

        ssq = sb("ssq", [128, 4]); rstd = sb("rstd", [128, 4])
        xn = sb("xn", [128, D], BF16)
        junk = xn
        xnT = sb("xnT", [128, 8, 128], BF16)
        qT = sb("qT", [128, 4, 128], BF16)
        uT2 = [sb("uT%d" % i, [128, 4, 128], BF16) for i in range(2)]
        Sx = sb("Sx", [128, 1, 2, 257]); mx = sb("mx", [128, 2]); nbias = sb("nbias", [128, 2])
        rs = sb("rs", [128, 2]); rinv = sb("rinv", [128, 2])
        Pb = sb("Pb", [128, 2, 257], BF16); PTs = sb("PTs", [128, 4, 128], BF16)
        mixT2 = [sb("mixT%d" % i, [128, 8, 128], BF16) for i in range(2)]
        mixT = mixT2[0]
        Psb = sb("Psb", [128, 17 * 128 + 1], BF16)
        PTss = sb("PTss", [128, 17, 128], BF16)
        pp4 = sb("pp4", [128, 4, 4, 128]); rr4 = sb("rr4", [128, 2, 4, 128]); vv4 = sb("vv4", [128, 2, 4, 128])
        hb = sb("hb", [128, 4, 2, 128], BF16)
        gl32 = sb("gl32", [128, 4, 128]); glb = sb("glb", [128, 4, 128], BF16)
        xn2T = sb("xn2T", [128, 8, 512], BF16)
        gx = sb("gx", [128, 2, 514]); gxs = sb("gxs", [128, NSEQ, 6])
        cv = sb("cv", [128, 1, 512]); sl = sb("sl", [128, 1, 512])
        kvtok = sl[:, 0, 0:256]
        yv = gx[:, 0, 0:128]; sg = gx[:, 0, 128:256]; rsb = gx[:, 0, 256:384]
        sqb = gx[:, 1, 0:256].bitcast(BF16).rearrange("p (a k) -> p a k", a=4)
        attn = cv[:, 0, :]
        anb = sl[:, 0, 256:512].bitcast(BF16)
        wi = [sb("wi0", [128, 8, 1280], BF16)] * 2
        wo = [sb("wo0", [128, 8, D], BF16)] * 2
        wgl = [sb("wgl0", [128, 4, 512], BF16)] * 2
        wg = [sb("wg%d" % i, [128, 8, 128], BF16) for i in range(2)] + [xnT]
        wu = [sb("wu%d" % i, [128, 8, 128], BF16) for i in range(2)] + [mixT]
        wdA = sb("wdA", [128, 2, 512], BF16)
        wd = [wdA[:], Sx[:].rearrange("p a b c -> p (a b c)").bitcast(BF16)[:, 0:1024].rearrange("p (j n) -> p j n", j=2),
              xn[:].rearrange("p (j n) -> p j n", j=2)]
        big = sb("big", [128, 9728])
        xb = big[:, 0:4096].rearrange("p (t d) -> p t d", t=4)
        hTall = big[:, 4096:9728].bitcast(BF16).rearrange("p (c n) -> p c n", c=NFC)
        hTsm = big[:, 4096:5504].bitcast(BF16).rearrange("p (c n) -> p c n", c=NFC)
        kTs = big[:, 5504:7680].bitcast(BF16).rearrange("p (a k) -> p a k", a=2)
        vs = big[:, 7680:8768].bitcast(BF16).rearrange("p (b k) -> p b k", b=17)
        scr = big
        h0 = big[:, 8192:8704].rearrange("p (a q s) -> p a q s", a=2, q=16)
        cvst = big[:, 3264:3968].rearrange("p (c s j) -> p c s j", c=NFC, s=NSEQ)
        def prep_views(o):
            return (scr[:, o:o + 768].rearrange("p (a b) -> p a b", a=3), scr[:, o + 768:o + 2304].rearrange("p (a b) -> p a b", a=6),
                    scr[:, o + 2304:o + 2816].rearrange("p (a q m) -> p a q m", a=2, q=2), scr[:, o + 2816:o + 3328].rearrange("p (a q m) -> p a q m", a=2, q=2))
        Ssx = big[:, 1024:1024 + 17 * 128 + 1]
        idf = big[:, 8960:9088]

        mmA = ps("mmA", [128, 512]); mmB = ps("mmB", [128, 512]); mmC = ps("mmC", [128, 512])
        pT = ps("pT", [128, 8, 128], BF16)
        pT32 = pT[:].rearrange("p c k -> p (c k)").bitcast(F32)
        Sps = ps("Sps", [128, 512]); Ops = ps("Ops", [128, 512])
        acc0 = ps("acc0", [128, 512]); acc1 = ps("acc1", [128, 512])

        op = P.op

        P.dma("sp", lambda h: h.dma_start(out=idf[:], in_=ident_d), writes=["idf"])
        P.dma("sp", lambda h: h.dma_start(out=masks[:], in_=masks_d), writes=["masks"])
        P.dma("pool", lambda h: h.dma_start(out=smask[:], in_=smask_d), writes=["smask"])
        P.dma("sp", lambda h: h.dma_start(out=pcol[:], in_=pcol_d), writes=["pcol"])
        P.dma("sp", lambda h: h.dma_start(out=sink8[:], in_=sinks_d.partition_broadcast(128)), writes=["sink8"])
        P.dma("sp", lambda h: h.dma_start(out=gfb[:], in_=gfin_d.partition_broadcast(128)), writes=["gfb"])
        P.dma("sp", lambda h: h.dma_start(out=lam[:], in_=lam_d), writes=["lam"])
        P.dma("sp", lambda h: h.dma_start(out=h0, in_=h0_d))
        op("pool", lambda h: h.memset(hb[:], 0.0))
        op("pool", lambda h: h.memset(cv[:], 0.0))
        op("pool", lambda h: h.memset(gx[:], 0.0))
        P.dma("pool", lambda h: h.dma_start(out=wi[0][:], in_=w_in.rearrange("(c p) n -> p c n", p=128)))
        P.dma("pool", lambda h: h.dma_start(out=wgl[0][:], in_=w_glu.rearrange("(c p) n -> p c n", p=128)))
        P.dma("pool", lambda h: h.dma_start(out=wo[0][:], in_=w_o.rearrange("(c p) n -> p c n", p=128)))
        P.dma("sp", lambda h: h.dma_start(out=kvs_k[:, 0:124, :], in_=kcache_d[:, 4:128, :]), writes=["kvs_k_a"])
        P.dma("sp", lambda h: h.dma_start(out=kvs_v[:, 0:124, :], in_=vcache_d[:, 4:128, :]), writes=["kvs_v_a"])

        op("dve", lambda h: h.tensor_copy(out=idb[:], in_=idf[:]), ["idf"], ["idb"])
        op("pool", lambda h: h.memset(onesb[:], 1.0), [], ["onesb"])
        op("pool", lambda h: h.memset(epsb[:], EPS), [], ["epsb"])
        op("pool", lambda h: h.memset(kTx[:], 0.0), [], ["kTx"])
        op("pool", lambda h: h.memset(vx[:], 0.0), [], ["vx"])
        op("pool", lambda h: h.memset(car[:], 0.0), [], ["car"])
        op("pool", lambda h: h.memset(gcar[:], 0.0), [], ["gcar"])
        pass
        op("pool", lambda h: h.memset(hfin[:], 0.0))
        op("pool", lambda h: h.memset(sconv[:], 0.0))
        rho = sb("rho", [128, 16]); fre = sb("fre", [128, 16]); fim = sb("fim", [128, 16])
        dtl, thl, c1, s1, tmpa, kfL = (big[:, 9088 + 16 * i:9104 + 16 * i] for i in range(6))
        kiL = big[:, 9184:9200].bitcast(mybir.dt.int32)
        op("act", lambda h: h.activation(out=dtl[:], in_=lam[:, 2, :], func=AF.Exp), ["lam"], ["dtl"])
        op("dve", lambda h: h.tensor_tensor(out=thl[:], in0=lam[:, 1, :], in1=dtl[:], op=ALU.mult), ["lam", "dtl"], ["thl"])
        op("dve", lambda h: h.tensor_tensor(out=tmpa[:], in0=lam[:, 0, :], in1=dtl[:], op=ALU.mult), ["lam", "dtl"], ["tmpa"])
        op("act", lambda h: h.activation(out=rho[:], in_=tmpa[:], func=AF.Exp), ["tmpa"], ["rho"])

        def sincos(eng_tag, th_ap, s_ap, c_ap, tmp_ap, shape_key, ki_ap, kf_ap):
            K = shape_key

            def reduce_(shift, dst_key):
                op("dve", lambda h: h.tensor_scalar(out=tmp_ap, in0=th_ap, scalar1=shift, scalar2=1.0 / (2 * PI), op0=ALU.add, op1=ALU.mult),
                   [K + "th", K + "s", K + "c"], [K + "tmp"])
                op("dve", lambda h: h.tensor_copy(out=ki_ap, in_=tmp_ap), [K + "tmp"], [K + "ki"])
                op("dve", lambda h: h.tensor_copy(out=kf_ap, in_=ki_ap), [K + "ki"], [K + "kf"])
                op("dve", lambda h: h.tensor_scalar(out=tmp_ap, in0=th_ap, scalar1=shift, scalar2=None, op0=ALU.add), [K + "th", K + "ki"], [K + "tmp"])
                op("dve", lambda h: h.scalar_tensor_tensor(out=tmp_ap, in0=kf_ap, scalar=-2 * PI, in1=tmp_ap, op0=ALU.mult, op1=ALU.add),
                   [K + "kf", K + "tmp"], [K + "tmp"])
                op("dve", lambda h: h.tensor_scalar(out=kf_ap, in0=tmp_ap, scalar1=PI, scalar2=None, op0=ALU.is_gt), [K + "tmp"], [K + "kf"])
                op("dve", lambda h: h.scalar_tensor_tensor(out=tmp_ap, in0=kf_ap, scalar=-2 * PI, in1=tmp_ap, op0=ALU.mult, op1=ALU.add),
                   [K + "kf", K + "tmp"], [K + "tmp"])
                op("dve", lambda h: h.tensor_scalar(out=kf_ap, in0=tmp_ap, scalar1=-PI, scalar2=None, op0=ALU.is_lt), [K + "tmp"], [K + "kf"])
                op("dve", lambda h: h.scalar_tensor_tensor(out=tmp_ap, in0=kf_ap, scalar=2 * PI, in1=tmp_ap, op0=ALU.mult, op1=ALU.add),
                   [K + "kf", K + "tmp"], [K + "tmp"])
                op("dve", lambda h: h.tensor_scalar(out=tmp_ap, in0=tmp_ap, scalar1=-PI, scalar2=PI, op0=ALU.max, op1=ALU.min), [K + "tmp"], [K + "tmp"])

            reduce_(0.0, "s")
            op("act", lambda h: h.activation(out=s_ap, in_=tmp_ap, func=AF.Sin), [K + "tmp"], [K + "s"])
            reduce_(0.5 * PI, "c")
            op("act", lambda h: h.activation(out=c_ap, in_=tmp_ap, func=AF.Sin), [K + "tmp"], [K + "c"])

        sincos("L", thl[:], s1[:], c1[:], tmpa[:], "L", kiL[:], kfL[:])
        op("dve", lambda h: h.tensor_tensor(out=are[:], in0=rho[:], in1=c1[:], op=ALU.mult), ["rho", "Lc"], ["are"])
        op("dve", lambda h: h.tensor_tensor(out=aim[:], in0=rho[:], in1=s1[:], op=ALU.mult), ["rho", "Ls"], ["aim"])
        op("pool", lambda h: h.memset(cs[:, :, 0:1], 1.0), [], ["cs"])
        op("pool", lambda h: h.memset(sn[:, :, 0:1], 0.0), [], ["sn"])
        op("dve", lambda h: h.tensor_copy(out=cs[:, :, 1], in_=c1[:]), ["Lc", "cs"], ["cs"])
        op("dve", lambda h: h.tensor_copy(out=sn[:, :, 1], in_=s1[:]), ["Ls", "sn"], ["sn"])
        tA = pp4[:, 0:2].rearrange("p a j (b c) -> p (a j b) c", c=64); tB = big[:, 6656:7680].rearrange("p (a c) -> p a c", c=64)
        m = 1
        while m < 128:
            cm = cs[:, :, m:m + 1].broadcast_to([128, 16, m]); sm = sn[:, :, m:m + 1].broadcast_to([128, 16, m])
            a_c = cs[:, :, 1:m + 1]; a_s = sn[:, :, 1:m + 1]
            o_c = cs[:, :, m + 1:2 * m + 1]; o_s = sn[:, :, m + 1:2 * m + 1]
            ta = tA[:, :, 0:m]; tb = tB[:, :, 0:m]
            op("dve", lambda h, a_c=a_c, cm=cm, ta=ta: h.tensor_tensor(out=ta, in0=a_c, in1=cm, op=ALU.mult), ["cs", "sn"], ["tA"])
            op("dve", lambda h, a_s=a_s, sm=sm, tb=tb: h.tensor_tensor(out=tb, in0=a_s, in1=sm, op=ALU.mult), ["cs", "sn"], ["tB"])
            op("dve", lambda h, o_c=o_c, ta=ta, tb=tb: h.tensor_tensor(out=o_c, in0=ta, in1=tb, op=ALU.subtract), ["tA", "tB", "sn"], ["cs"])
            op("dve", lambda h, a_c=a_c, sm=sm, ta=ta: h.tensor_tensor(out=ta, in0=a_c, in1=sm, op=ALU.mult), ["cs", "sn"], ["tA"])
            op("dve", lambda h, a_s=a_s, cm=cm, tb=tb: h.tensor_tensor(out=tb, in0=a_s, in1=cm, op=ALU.mult), ["cs", "sn"], ["tB"])
            op("dve", lambda h, o_s=o_s, ta=ta, tb=tb: h.tensor_tensor(out=o_s, in0=ta, in1=tb, op=ALU.add), ["tA", "tB", "cs"], ["sn"])
            m *= 2
        op("pool", lambda h: h.memset(mask64[:], 1.0))
        op("pool", lambda h: h.memset(mask64[:].rearrange("p (s t) -> p s t", t=4)[:, :, 0:1], 0.0))
        sm_ = [big[:, 9200 + 16 * i:9216 + 16 * i] for i in range(8)]
        lr_, li_ = lam[:, 0, :], lam[:, 1, :]
        nr, den, t_a, t_b, gr, gi = sm_[0], sm_[1], sm_[2], sm_[3], sm_[4], sm_[5]
        TT = lambda o, x, y, f_: op("dve", lambda h: h.tensor_tensor(out=o, in0=x, in1=y, op=f_))
        op("dve", lambda h: h.tensor_scalar(out=nr, in0=are[:], scalar1=-1.0, scalar2=None, op0=ALU.add))
        TT(den, lr_, lr_, ALU.mult); TT(t_a, li_, li_, ALU.mult); TT(den, den, t_a, ALU.add)
        op("dve", lambda h: h.reciprocal(out=den, in_=den))
        TT(t_a, nr, lr_, ALU.mult); TT(t_b, aim[:], li_, ALU.mult); TT(t_a, t_a, t_b, ALU.add); TT(fre[:], t_a, den, ALU.mult)
        TT(t_a, aim[:], lr_, ALU.mult); TT(t_b, nr, li_, ALU.mult); TT(t_a, t_a, t_b, ALU.subtract); TT(fim[:], t_a, den, ALU.mult)
        TT(den, fre[:], fre[:], ALU.mult); TT(t_a, fim[:], fim[:], ALU.mult); TT(den, den, t_a, ALU.add)
        op("dve", lambda h: h.reciprocal(out=den, in_=den))
        TT(gr, fre[:], den, ALU.mult); TT(gi, fim[:], den, ALU.mult)
        op("dve", lambda h: h.tensor_scalar(out=gi, in0=gi, scalar1=-1.0, scalar2=None, op0=ALU.mult))
        P.dma("pool", lambda h: h.dma_start(out=WBre[:], in_=bblk_d[:, 0]))
        P.dma("pool", lambda h: h.dma_start(out=WBim[:], in_=bblk_d[:, 1]))
        cbf = big[:, 0:4096].rearrange("p (a q m) -> p a q m", a=2, q=16)
        u1 = big[:, 4096:6144].rearrange("p (q m) -> p q m", q=16); u2 = big[:, 6144:8192].rearrange("p (q m) -> p q m", q=16)
        P.dma("sp", lambda h: h.dma_start(out=cbf, in_=cblk_d))
        fre_b = fre[:].unsqueeze(2).broadcast_to([128, 16, 128]); fim_b = fim[:].unsqueeze(2).broadcast_to([128, 16, 128])
        TT(u1, cbf[:, 0], fre_b, ALU.mult); TT(u2, cbf[:, 1], fim_b, ALU.mult); TT(WCre[:], u1, u2, ALU.subtract)
        TT(u1, cbf[:, 0], fim_b, ALU.mult); TT(u2, cbf[:, 1], fre_b, ALU.mult); TT(u1, u1, u2, ALU.add)
        op("dve", lambda h: h.tensor_scalar(out=WCimn[:], in0=u1, scalar1=-1.0, scalar2=None, op0=ALU.mult))

        tD = big[:, 9344:9600].rearrange("p (q s) -> p q s", q=16)
        tE = big[:, 8704:8960].rearrange("p (q s) -> p q s", q=16)
        gr_b = gr.unsqueeze(2).broadcast_to([128, 16, NSEQ]); gi_b = gi.unsqueeze(2).broadcast_to([128, 16, NSEQ])
        TT(tD, h0[:, 0], gr_b, ALU.mult); TT(tE, h0[:, 1], gi_b, ALU.mult); TT(tD, tD, tE, ALU.subtract)
        TT(tE, h0[:, 0], gi_b, ALU.mult); TT(h0[:, 0], tD, tD, ALU.max)
        TT(tD, h0[:, 1], gr_b, ALU.mult); TT(h0[:, 1], tE, tD, ALU.add)
        a_re_b = are[:].unsqueeze(2).broadcast_to([128, 16, NSEQ]); a_im_b = aim[:].unsqueeze(2).broadcast_to([128, 16, NSEQ])
        tC = big[:, 8704:8960].rearrange("p (q s) -> p q s", q=16)
        op("dve", lambda h: h.tensor_tensor(out=ah[:, 0], in0=h0[:, 0], in1=a_re_b, op=ALU.mult), ["h0", "are"], ["ah0"])
        op("dve", lambda h: h.tensor_tensor(out=tC[:], in0=h0[:, 1], in1=a_im_b, op=ALU.mult), ["h0", "aim"], ["tC"])
        op("dve", lambda h: h.tensor_tensor(out=ah[:, 0], in0=ah[:, 0], in1=tC[:], op=ALU.subtract), ["ah0", "tC"], ["ah0"])
        op("dve", lambda h: h.tensor_tensor(out=ah[:, 1], in0=h0[:, 0], in1=a_im_b, op=ALU.mult), ["h0", "aim"], ["ah1"])
        op("dve", lambda h: h.tensor_tensor(out=tC[:], in0=h0[:, 1], in1=a_re_b, op=ALU.mult), ["h0", "are", "ah0"], ["tC"])
        op("dve", lambda h: h.tensor_tensor(out=ah[:, 1], in0=ah[:, 1], in1=tC[:], op=ALU.add), ["ah1", "tC"], ["ah1"])


        def rms(src_ap, key_src, n, slot, scale, pn):
            op("act", lambda h: h.activation(out=junk[0:pn, 0:n], in_=src_ap, func=AF.Square, accum_out=ssq[0:pn, slot:slot + 1]),
               [key_src], ["junk", "ssq%d" % slot])
            op("act", lambda h: h.activation(out=rstd[0:pn, slot:slot + 1], in_=ssq[0:pn, slot:slot + 1], func=AF.Ln, scale=scale, bias=epsb[0:pn, 0:1]))
            op("act", lambda h: h.activation(out=rstd[0:pn, slot:slot + 1], in_=rstd[0:pn, slot:slot + 1], func=AF.Exp, scale=-0.5))

        def partA(ti, tl):
            sample = (ti == NPT)
            xt = xb[:, tl, :]
            n = 128
            ns = NS if sample else 128
            r0 = ti * 128
            par = ti % 2
            if sample:
                op("pool", lambda h: h.memset(kTs[:], 0.0))
                P.dma("pool", lambda h: h.dma_start(out=kTs[0:64, 0, 0:NSEQ * 128], in_=kcT_d[0:64, :]))
                P.dma("pool", lambda h: h.dma_start(out=kTs[64:128, 1, 0:NSEQ * 128], in_=kcT_d[64:128, :]))
                P.dma("pool", lambda h: h.dma_start(out=vs[:, 0:NSEQ, :], in_=vc_d))
                P.dma("sp", lambda h: h.dma_start(out=cvst, in_=cvst_d))
            rms(xt[:], "xt", D, 0, 1.0 / D, 128)
            op("dve", lambda h: h.tensor_scalar(out=xn[:], in0=xt[:], scalar1=rstd[:, 0:1], scalar2=None, op0=ALU.mult))
            for c in range(8):
                op("pe", lambda h, c=c: h.transpose(out=pT[:, c, :], in_=xn[:, c * 128:(c + 1) * 128], identity=idb[:]))
            op("dve", lambda h: h.tensor_tensor(out=xnT[:], in0=pT[:], in1=pcol[:, GMIX:GMIX + 8].unsqueeze(2).broadcast_to([128, 8, 128]), op=ALU.mult))
            W = wi[par]
            yield
            for i in range(4):
                bank = acc0 if i % 2 == 0 else acc1
                for c in range(8):
                    op("pe", lambda h, i=i, c=c, bank=bank: h.matmul(bank[:, 0:128], lhsT=W[:, c, i * 128:(i + 1) * 128], rhs=xnT[:, c, :],
                                                                   start=(c == 0), stop=(c == 7)))
                op("act", lambda h, i=i, bank=bank: h.activation(out=qT[:, i, :], in_=bank[:, 0:128], func=AF.Copy))
            yield
            for c in range(8):
                op("pe", lambda h, c=c: h.matmul(Ops[:, 0:128], lhsT=W[:, c, 512:640], rhs=xnT[:, c, :], start=(c == 0), stop=(c == 7)))
            if sample:
                op("dve", lambda h: h.tensor_copy(out=kTs[0:64, 0, NSEQ * 128:17 * 128], in_=Ops[0:64, 0:128]))
                op("dve", lambda h: h.tensor_copy(out=kTs[64:128, 1, NSEQ * 128:17 * 128], in_=Ops[64:128, 0:128]))
            else:
                op("dve", lambda h: h.tensor_copy(out=kTx[0:64, 0, 128:256], in_=Ops[0:64, 0:128]))
                op("dve", lambda h: h.tensor_copy(out=kTx[64:128, 1, 128:256], in_=Ops[64:128, 0:128]))
            yield
            for i in range(4):
                bank = acc0 if i % 2 == 0 else acc1
                for c in range(8):
                    op("pe", lambda h, i=i, c=c, bank=bank: h.matmul(bank[:, 0:128], lhsT=W[:, c, 768 + i * 128:768 + (i + 1) * 128], rhs=xnT[:, c, :],
                                                                   start=(c == 0), stop=(c == 7)))
                op("act", lambda h, i=i, bank=bank: h.activation(out=uT2[par][:, i, :], in_=bank[:, 0:128], func=AF.Copy))
            yield
            for c in range(8):
                op("pe", lambda h, c=c: h.matmul(Ops[:, 0:256], lhsT=xnT[:, c, :], rhs=W[:, c, 512:768], start=(c == 0), stop=(c == 7)))
            if sample:
                op("dve", lambda h: h.tensor_copy(out=vs[:, 16, :], in_=Ops[:, 128:256]))
            else:
                op("dve", lambda h: h.tensor_copy(out=vx[:, 1, :], in_=Ops[:, 128:256]))
            if sample or ti == NPT - 1:
                op("dve", lambda h: h.tensor_copy(out=kvtok[:], in_=Ops[:, 0:256]))
                if sample:
                    P.dma("sp", lambda h: h.dma_start(out=kvs_k[:, 124:128, :], in_=kvtok[0:NS, 0:128]))
                    P.dma("sp", lambda h: h.dma_start(out=kvs_v[:, 124:128, :], in_=kvtok[0:NS, 128:256]))
                else:
                    P.dma("sp", lambda h: h.dma_start(out=kvp_d, in_=kvtok[:]))

            if not sample:
                mi = 0 if ti == 0 else (1 if ti == 1 else 2)
                for i in range(4):
                    for hh_ in range(2):
                        op("pe", lambda h, i=i, hh_=hh_: h.matmul(Sps[:, hh_ * 256:(hh_ + 1) * 256], lhsT=qT[:, i, :], rhs=kTx[:, hh_, :], start=True, stop=True))
                    for hh_ in range(2):
                        op("dve", lambda h, hh_=hh_, hd=i + 4 * hh_: h.tensor_scalar(out=Sx[:, 0, hh_, 256:257], in0=sink8[:, hd:hd + 1], scalar1=8.0, scalar2=None, op0=ALU.mult))
                    op("dve", lambda h: h.tensor_tensor(out=Sx[:, 0, :, 0:256], in0=Sps[:].rearrange("p (a k) -> p a k", a=2),
                                                        in1=masks[:, mi:mi + 1, :].broadcast_to([128, 2, 256]), op=ALU.add))
                    op("dve", lambda h: h.tensor_reduce(out=mx[:], in_=Sx[:, 0], axis=AX.X, op=ALU.max))
                    op("dve", lambda h: h.tensor_scalar(out=nbias[:], in0=mx[:], scalar1=-0.125, scalar2=None, op0=ALU.mult))
                    for hh_ in range(2):
                        op("act", lambda h, hh_=hh_: h.activation(out=Pb[:, hh_, :], in_=Sx[:, 0, hh_, :], func=AF.Exp, scale=0.125,
                                                                 bias=nbias[:, hh_:hh_ + 1], accum_out=rs[:, hh_:hh_ + 1]))
                    op("dve", lambda h: h.reciprocal(out=rinv[:], in_=rs[:]))
                    for hh_ in range(2):
                        for blk in range(2):
                            op("pe", lambda h, hh_=hh_, blk=blk: h.transpose(out=pT[:, hh_ * 2 + blk, :], in_=Pb[:, hh_, blk * 128:(blk + 1) * 128], identity=idb[:]))
                    op("act", lambda h: h.activation(out=PTs[:], in_=pT[:, 0:4, :], func=AF.Copy))
                    for hh_ in range(2):
                        for blk in range(2):
                            op("pe", lambda h, hh_=hh_, blk=blk: h.matmul(Ops[:, hh_ * 64:(hh_ + 1) * 64], lhsT=PTs[:, hh_ * 2 + blk, :],
                                                                         rhs=vx[:, blk, hh_ * 64:(hh_ + 1) * 64], start=(blk == 0), stop=(blk == 1)))
                    for hh_ in range(2):
                        hd = i + 4 * hh_
                        op("dve", lambda h, hh_=hh_, hd=hd: h.tensor_scalar(out=attn[:, hd * 64:(hd + 1) * 64], in0=Ops[:, hh_ * 64:(hh_ + 1) * 64],
                                                                           scalar1=rinv[:, hh_:hh_ + 1], scalar2=None, op0=ALU.mult))
                    yield
                op("pool", lambda h: h.tensor_copy(out=kTx[:, :, 0:128], in_=kTx[:, :, 128:256]))
                op("pool", lambda h: h.tensor_copy(out=vx[:, 0, :], in_=vx[:, 1, :]))
            else:
                W17 = 17 * 128
                for hd in range(8):
                    i, hh_ = hd % 4, hd // 4
                    for cb in range(5):
                        c0 = cb * 512
                        cw = min(512, W17 - c0)
                        bank = mmA if cb % 2 == 0 else mmB
                        op("pe", lambda h, i=i, hh_=hh_, c0=c0, cw=cw, bank=bank: h.matmul(bank[:, 0:cw], lhsT=qT[:, i, :], rhs=kTs[:, hh_, c0:c0 + cw], start=True, stop=True))
                        op("dve", lambda h, c0=c0, cw=cw, bank=bank: h.tensor_tensor(out=Ssx[:, c0:c0 + cw], in0=bank[:, 0:cw], in1=smask[:, c0:c0 + cw], op=ALU.add))
                    op("dve", lambda h, hd=hd: h.tensor_scalar(out=Ssx[:, W17:W17 + 1], in0=sink8[:, hd:hd + 1], scalar1=8.0, scalar2=None, op0=ALU.mult))
                    op("dve", lambda h: h.tensor_reduce(out=mx[:, 0:1], in_=Ssx[:], axis=AX.X, op=ALU.max))
                    op("dve", lambda h: h.tensor_scalar(out=nbias[:, 0:1], in0=mx[:, 0:1], scalar1=-0.125, scalar2=None, op0=ALU.mult))
                    op("act", lambda h: h.activation(out=Psb[:], in_=Ssx[:], func=AF.Exp, scale=0.125, bias=nbias[:, 0:1], accum_out=rs[:, 0:1]))
                    op("dve", lambda h: h.reciprocal(out=rinv[:, 0:1], in_=rs[:, 0:1]))
                    for g8 in range(3):
                        nb_ = 8 if g8 < 2 else 1
                        for b in range(nb_):
                            blk = g8 * 8 + b
                            op("pe", lambda h, b=b, blk=blk: h.transpose(out=pT[:, b, :], in_=Psb[:, blk * 128:(blk + 1) * 128], identity=idb[:]))
                        op("act", lambda h, g8=g8, nb_=nb_: h.activation(out=PTss[:, g8 * 8:g8 * 8 + nb_, :], in_=pT[:, 0:nb_, :], func=AF.Copy))
                    for blk in range(17):
                        op("pe", lambda h, blk=blk, hh_=hh_: h.matmul(Ops[:, 0:64], lhsT=PTss[:, blk, :], rhs=vs[:, blk, hh_ * 64:(hh_ + 1) * 64],
                                                                     start=(blk == 0), stop=(blk == 16)))
                    op("dve", lambda h, hd=hd: h.tensor_scalar(out=attn[:, hd * 64:(hd + 1) * 64], in0=Ops[:, 0:64], scalar1=rinv[:, 0:1], scalar2=None, op0=ALU.mult))
            rms(attn[0:n, :], "attn", 512, 1, 1.0 / 512, n)
            op("dve", lambda h: h.tensor_scalar(out=anb[0:n, :], in0=attn[0:n, :], scalar1=rstd[0:n, 1:2], scalar2=None, op0=ALU.mult), ["attn", "rstd1"], ["anb"])
            for c in range(4):
                op("pe", lambda h, c=c: h.transpose(out=pT[:, c, 0:n], in_=anb[0:n, c * 128:(c + 1) * 128], identity=idb[0:n, 0:n]), ["anb", "idb"], ["pT"])
            op("dve", lambda h: h.tensor_tensor(out=mixT2[par][:, 0:4, :], in0=pT[:, 0:4, :], in1=pcol[:, GATT:GATT + 4].unsqueeze(2).broadcast_to([128, 4, 128]), op=ALU.mult))

            yield

        def partB(ti, tl):
            sample = (ti == NPT)
            xt = xb[:, tl, :]
            n = 128
            ns = NS if sample else 128
            par = ti % 2
            uT = uT2[par]
            mixT = mixT2[par]
            TT = lambda o, x, y, f_: op("dve", lambda h: h.tensor_tensor(out=o, in0=x, in1=y, op=f_))

            def ssm_mm(c4):
                q0 = 4 * c4
                for ri, (WBx, bank) in enumerate(((WBre, mmA), (WBim, mmC))):
                    for j in range(4):
                        op("pe", lambda h: h.matmul(bank[:, j * 128:(j + 1) * 128], lhsT=WBx[:, q0 + j, :], rhs=uT[:, c4, :], start=True, stop=True))

            def ssm_batch(c4):
                q0 = 4 * c4
                T = [pp4[:, kk] for kk in range(4)]
                Bre = mmA[:, :].rearrange("p (j k) -> p j k", j=4); Bim = mmC[:, :].rearrange("p (j k) -> p j k", j=4)
                if sample:
                    op("act", lambda h: h.activation(out=vv4[:, 0], in_=Bre, func=AF.Copy))
                    op("act", lambda h: h.activation(out=vv4[:, 1], in_=Bim, func=AF.Copy))
                    for ri in range(2):
                        v4 = vv4[:, ri, :, 0:NS].rearrange("p j (s t) -> p j s t", t=4)[:, :, :, 0]
                        TT(v4, v4, ah[:, ri, q0:q0 + 4, :], ALU.add)
                    csq = cs[:, q0:q0 + 4, 1:5].unsqueeze(2).broadcast_to([128, 4, NSEQ, 4])
                    snq = sn[:, q0:q0 + 4, 1:5].unsqueeze(2).broadcast_to([128, 4, NSEQ, 4])
                    v3 = lambda ap: ap[:, :, 0:NS].rearrange("p j (s t) -> p j s t", t=4)
                    Bre, Bim = vv4[:, 0], vv4[:, 1]
                else:
                    csq = cs[:, q0:q0 + 4, 1:129]; snq = sn[:, q0:q0 + 4, 1:129]
                    v3 = lambda ap: ap
                Tv = [v3(t_) for t_ in T]
                RR = [v3(rr4[:, kk]) for kk in range(2)]
                TT(Tv[0], v3(Bre), csq, ALU.mult); TT(Tv[1], v3(Bim), snq, ALU.mult)
                TT(Tv[2], v3(Bim), csq, ALU.mult); TT(Tv[3], v3(Bre), snq, ALU.mult)
                if c4 + 1 < 4:
                    ssm_mm(c4 + 1)
                TT(RR[0], Tv[0], Tv[1], ALU.add); TT(RR[1], Tv[2], Tv[3], ALU.subtract)
                for j in range(4):
                    q = q0 + j
                    if sample:
                        op("dve", lambda h: h.tensor_scalar(out=dtmp[:], in0=mask64[:], scalar1=rho[:, q:q + 1], scalar2=None, op0=ALU.mult))
                    for ri in range(2):
                        if sample:
                            op("dve", lambda h: h.tensor_tensor_scan(out=vv4[:, ri, j, 0:NS], data0=dtmp[:], data1=rr4[:, ri, j, 0:NS], initial=0.0, op0=ALU.mult, op1=ALU.add))
                        else:
                            op("dve", lambda h: h.tensor_tensor_scan(out=vv4[:, ri, j, :], data0=rho[:, q:q + 1].broadcast_to([128, 128]), data1=rr4[:, ri, j, :],
                                                                   initial=car[:, ri, q:q + 1], op0=ALU.mult, op1=ALU.add))
                VV = [v3(vv4[:, kk]) for kk in range(2)]
                TT(Tv[0], VV[0], csq, ALU.mult); TT(Tv[1], VV[1], snq, ALU.mult)
                TT(Tv[2], VV[0], snq, ALU.mult); TT(Tv[3], VV[1], csq, ALU.mult)
                TT(v3(hb[:, :, 0, :]), Tv[0], Tv[1], ALU.subtract); TT(v3(hb[:, :, 1, :]), Tv[2], Tv[3], ALU.add)
                if sample:
                    TT(hfin[:, 0, q0:q0 + 4, 0:NSEQ], Tv[0][:, :, :, 3], Tv[1][:, :, :, 3], ALU.subtract)
                    TT(hfin[:, 1, q0:q0 + 4, 0:NSEQ], Tv[2][:, :, :, 3], Tv[3][:, :, :, 3], ALU.add)
                else:
                    TT(car[:, 0, q0:q0 + 4], Tv[0][:, :, 127], Tv[1][:, :, 127], ALU.subtract)
                    TT(car[:, 1, q0:q0 + 4], Tv[2][:, :, 127], Tv[3][:, :, 127], ALU.add)

            def ssm_y(c4):
                for qq in range(4):
                    q = c4 * 4 + qq
                    op("pe", lambda h: h.matmul(mmB[:, 0:n], lhsT=WCre[:, q, :], rhs=hb[:, qq, 0, 0:n], start=(qq == 0), stop=False))
                    op("pe", lambda h: h.matmul(mmB[:, 0:n], lhsT=WCimn[:, q, :], rhs=hb[:, qq, 1, 0:n], start=False, stop=(qq == 3)))
                op("dve", lambda h: h.scalar_tensor_tensor(out=yv[:, 0:n], in0=uT[:, c4, 0:n], scalar=col(DSK + c4), in1=mmB[:, 0:n], op0=ALU.mult, op1=ALU.add))
                op("act", lambda h: h.activation(out=gl32[:, c4, 0:n], in_=yv[:, 0:n], func=AF.Gelu))
                op("pool", lambda h: h.tensor_copy(out=glb[:, c4, 0:n], in_=gl32[:, c4, 0:n]))

            ssm_mm(0)
            yield
            for c4_ in range(4):
                ssm_batch(c4_)
                yield
                ssm_y(c4_)
                yield
            if ti == NPT - 1:
                op("act", lambda h: h.activation(out=hfin[:, :, :, NSEQ], in_=car[:], func=AF.Copy), ["car"], ["hfin"])
            for oc in range(4):
                for c4 in range(4):
                    op("pe", lambda h, oc=oc, c4=c4: h.matmul(mmC[:, 0:n], lhsT=wgl[0][:, c4, oc * 128:(oc + 1) * 128], rhs=glb[:, c4, 0:n], start=(c4 == 0), stop=(c4 == 3)),
                       ["glb", "wgl"], ["mmC"])
                op("act", lambda h, oc=oc: h.activation(out=sg[:, 0:n], in_=mmC[:, 0:n], func=AF.Sigmoid, bias=col(BGLU + oc)), ["mmC", "pcol"], ["sg"])
                op("dve", lambda h, oc=oc: h.tensor_tensor(out=gl32[:, oc, 0:n], in0=gl32[:, oc, 0:n], in1=sg[:, 0:n], op=ALU.mult), ["gl32", "sg", "glb"], ["gl32"])
                op("act", lambda h, oc=oc: h.activation(out=sqb[:, oc, 0:n], in_=gl32[:, oc, 0:n], func=AF.Square), ["gl32"], ["sqb"])
            for oc in range(4):
                op("pe", lambda h, oc=oc: h.matmul(mmC[:, 0:n], lhsT=onesb[:], rhs=sqb[:, oc, 0:n], start=(oc == 0), stop=(oc == 3)), ["sqb", "onesb"], ["mmC"])
            op("act", lambda h: h.activation(out=rsb[:, 0:n], in_=mmC[:, 0:n], func=AF.Ln, scale=1.0 / 512, bias=epsb[:, 0:1]))
            op("act", lambda h: h.activation(out=rsb[:, 0:n], in_=rsb[:, 0:n], func=AF.Exp, scale=-0.5))
            for oc in range(4):
                op("dve", lambda h, oc=oc: h.scalar_tensor_tensor(out=mixT[:, 4 + oc, 0:n], in0=gl32[:, oc, 0:n], scalar=col(GSSM + oc), in1=rsb[:, 0:n], op0=ALU.mult, op1=ALU.mult),
                   ["gl32", "rsb", "pcol"], ["mixT"])

            for hf in range(2):
                acc, ak = (acc0, "acc0") if hf == 0 else (acc1, "acc1")
                for c in range(8):
                    op("pe", lambda h, hf=hf, c=c, acc=acc: h.matmul(acc[0:n, :], lhsT=mixT[:, c, 0:n], rhs=wo[0][:, c, hf * 512:(hf + 1) * 512], start=(c == 0), stop=(c == 7)),
                       ["mixT", "wo"], [ak])
                op("dve", lambda h, hf=hf, acc=acc: h.tensor_tensor(out=xt[0:n, hf * 512:(hf + 1) * 512], in0=xt[0:n, hf * 512:(hf + 1) * 512], in1=acc[0:n, :], op=ALU.add),
                   ["xt", ak], ["xt"])
            rms(xt[0:n, :], "xt", D, 2, 1.0 / D, n)
            op("dve", lambda h: h.tensor_scalar(out=xn[0:n, :], in0=xt[0:n, :], scalar1=rstd[0:n, 2:3], scalar2=None, op0=ALU.mult), ["xt", "rstd2"], ["xn"])
            for c in range(8):
                op("pe", lambda h, c=c: h.transpose(out=pT[:, c, 0:n], in_=xn[0:n, c * 128:(c + 1) * 128], identity=idb[0:n, 0:n]), ["xn", "idb"], ["pT"])
            op("dve", lambda h: h.tensor_tensor(out=xn2T[:, :, tl * 128:(tl + 1) * 128], in0=pT[:], in1=pcol[:, GFFN:GFFN + 8].unsqueeze(2).broadcast_to([128, 8, 128]), op=ALU.mult))

        def ffn(tiles_):
            sample = (tiles_[0] == NPT)
            ntl = len(tiles_)
            nb = ntl * 128
            hTv = hTsm if (sample and ntl == 1) else hTall
            p0 = 128 if sample else 0
            npc = nb - p0
            for ch in range(NFC):
                b3 = ch % 2
                w3 = ch % 3
                P.dma("sp", lambda h: h.dma_start(out=wg[w3][:], in_=wgb_d[ch]))
                P.dma("sp", lambda h: h.dma_start(out=wu[w3][:], in_=wub_d[ch]))
                gps, ups = (mmA, mmB) if b3 == 0 else (mmC, pT32)
                for c in range(8):
                    op("pe", lambda h, c=c, b3=b3, gps=gps: h.matmul(gps[:, 0:nb], lhsT=wg[w3][:, c, :], rhs=xn2T[:, c, 0:nb], start=(c == 0), stop=(c == 7)))
                for c in range(8):
                    op("pe", lambda h, c=c, b3=b3, ups=ups: h.matmul(ups[:, 0:nb], lhsT=wu[w3][:, c, :], rhs=xn2T[:, c, 0:nb], start=(c == 0), stop=(c == 7)))
                w0, w1, w2, bb = col(CVW + ch * 3), col(CVW + ch * 3 + 1), col(CVW + ch * 3 + 2), col(CVB + ch)
                cvb, slb, gxb = cv[:, 0, :], sl[:, 0, :], gx[:, b3, :]
                def conv3(g0, g1, g2, cvv):
                    op("dve", lambda h: h.tensor_scalar(out=cvv, in0=g2, scalar1=w2, scalar2=bb, op0=ALU.mult, op1=ALU.add))
                    op("dve", lambda h: h.scalar_tensor_tensor(out=cvv, in0=g1, scalar=w1, in1=cvv, op0=ALU.mult, op1=ALU.add))
                    op("dve", lambda h: h.scalar_tensor_tensor(out=cvv, in0=g0, scalar=w0, in1=cvv, op0=ALU.mult, op1=ALU.add))

                if sample:
                    op("pool", lambda h: h.tensor_copy(out=gxs[:, :, 0:2], in_=cvst[:, ch, :, :]))
                    op("act", lambda h: h.activation(out=gxs[:, :, 2:6], in_=gps[:, 0:NS].rearrange("p (s t) -> p s t", t=4), func=AF.Copy))
                    op("pool", lambda h: h.tensor_copy(out=sconv[:, ch, :, :], in_=gxs[:, :, 4:6]))
                    conv3(gxs[:, :, 0:4], gxs[:, :, 1:5], gxs[:, :, 2:6], cvb[:, 0:NS].rearrange("p (s t) -> p s t", t=4))
                if npc:
                    op("pool", lambda h: h.tensor_copy(out=gxb[:, 0:2], in_=gcar[:, ch, :]))
                    op("act", lambda h: h.activation(out=gxb[:, 2:2 + npc], in_=gps[:, p0:nb], func=AF.Copy))
                    op("pool", lambda h: h.tensor_copy(out=gcar[:, ch, :], in_=gxb[:, npc:npc + 2]))
                    conv3(gxb[:, 0:npc], gxb[:, 1:1 + npc], gxb[:, 2:2 + npc], cvb[:, p0:nb])
                op("act", lambda h, cvb=cvb, slb=slb: h.activation(out=slb[:, 0:nb], in_=cvb[:, 0:nb], func=AF.Silu))
                op("dve", lambda h, ch=ch, slb=slb: h.tensor_tensor(out=hTv[:, ch, 0:nb], in0=slb[:, 0:nb], in1=ups[:, 0:nb], op=ALU.mult))
            accs = [acc0, acc1, Sps, Ops]
            for hf in range(2):
                for g in range(NFC // 2):
                    wdb = wd[g % 3]
                    P.dma("sp", lambda h: h.dma_start(out=wdb, in_=wdb_d[hf, g]))
                    for j in range(2):
                        ch = 2 * g + j
                        for tl in range(ntl):
                            op("pe", lambda h: h.matmul(accs[tl][:, :], lhsT=hTv[:, ch, tl * 128:(tl + 1) * 128], rhs=wdb[:, j, :],
                                                        start=(ch == 0), stop=(ch == NFC - 1)))
                for tl in range(ntl):
                    op("dve", lambda h, tl=tl, hf=hf: h.tensor_tensor(out=xb[:, tl, hf * 512:(hf + 1) * 512], in0=xb[:, tl, hf * 512:(hf + 1) * 512], in1=accs[tl][:, :], op=ALU.add))
            for tl, ti in enumerate(tiles_):
                xt = xb[:, tl, :]
                rms(xt, "xt", D, 3, 1.0 / D, 128)
                op("dve", lambda h, xt=xt: h.scalar_tensor_tensor(out=xt, in0=xt, scalar=rstd[:, 3:4], in1=gfb[:], op0=ALU.mult, op1=ALU.mult))
                P.dma("sp", lambda h, xt=xt, ti=ti: h.dma_start(out=y_d[ti * 128:(ti + 1) * 128, :], in_=xt))

        def est_dur(kind, eng, fn):
            rec = _Recorder()
            fn(rec)
            name, args, kwargs = rec.call
            o = kwargs.get("out", args[0] if args else None)
            nfree = 1
            if o is not None and hasattr(o, "shape"):
                for d_ in list(o.shape)[1:]:
                    nfree *= int(d_)
            if kind == "dma":
                return 100.0, 2500.0
            if eng == "dve":
                d = ((2 * nfree if name == "tensor_tensor_scan" else nfree) + 151) / 0.96
            elif eng == "act":
                d = (nfree + 230) / 1.2 + (90 if kwargs.get("accum_out") is not None else 0)
            elif eng == "pool":
                d = (2 * nfree + 250) / 1.2
            else:
                d = max(64, nfree) / 1.9 + 25
            return d, d

        def merge_threads(gens):
            gens = [g for g in gens if g is not None]
            bufs = [[] for _ in gens]
            alive = [True] * len(gens)
            tchain = [sched_now[0]] * len(gens)
            while True:
                for i, g in enumerate(gens):
                    while alive[i] and not bufs[i]:
                        P.capture = bufs[i]
                        if next(g, "done") == "done":
                            alive[i] = False
                        P.capture = None
                cands = [i for i in range(len(gens)) if bufs[i]]
                if not cands:
                    break
                best, best_t = None, None
                for i in cands:
                    kind, eng, fn = bufs[i][0]
                    t = max(eng_free.get(eng, 0.0), tchain[i] + 60.0)
                    if best is None or t < best_t:
                        best, best_t = i, t
                kind, eng, fn = bufs[best].pop(0)
                busy, lat = est_dur(kind, eng, fn)
                eng_free[eng] = best_t + busy
                tchain[best] = best_t + lat
                sched_now[0] = max(sched_now[0], best_t)
                (P.dma if kind == "dma" else P.op)(eng, fn)

        eng_free = {}
        ffn_cast_issued = [False]
        sched_now = [0.0]
        if tiles is None:
            blocks = [[0, 1, 2, 3], [4, 5, 6, 7], [8, 9, 10, 11], [12, 13, 14, 15], [NPT, 16]]
        else:
            blocks = tiles
        for blk_ in blocks:
            def load_x(tl, ti):
                P.dma("sp", lambda h: h.dma_start(out=xb[:, tl, :], in_=xin[ti * 128:(ti + 1) * 128, :]))
            lazy = NPT in blk_
            for tl, ti in enumerate(blk_):
                if tl == 0 or not lazy:
                    load_x(tl, ti)
            for _ in partA(blk_[0], 0):
                pass
            if not ffn_cast_issued[0]:
                ffn_cast_issued[0] = True
                for a_ in range(0, NFC, 11):
                    P.dma("pool", lambda h: h.dma_start(out=wgb_d[a_:a_ + 11], in_=w_gate[a_:a_ + 11]))
                    P.dma("pool", lambda h: h.dma_start(out=wub_d[a_:a_ + 11], in_=w_up[a_:a_ + 11]))
                for hf_ in range(2):
                    P.dma("pool", lambda h: h.dma_start(out=wdb_d[hf_], in_=w_down[hf_]))
            for tl, ti in enumerate(blk_):
                gb = partB(ti, tl)
                ga = partA(blk_[tl + 1], tl + 1) if tl + 1 < len(blk_) else None
                if ga is not None and lazy:
                    load_x(tl + 1, blk_[tl + 1])
                merge_threads([gb, ga])
            ffn(blk_)

        fv = pp4[:].rearrange("p a j k -> p (a j k)")
        v1 = fv[:, 0:272].rearrange("p (q s) -> p q s", q=16); v2 = fv[:, 272:544].rearrange("p (q s) -> p q s", q=16); v3_ = fv[:, 544:816].rearrange("p (q s) -> p q s", q=16)
        fre_c = fre[:].unsqueeze(2).broadcast_to([128, 16, NSEQ + 1]); fim_c = fim[:].unsqueeze(2).broadcast_to([128, 16, NSEQ + 1])
        TT(v1, hfin[:, 0], fre_c, ALU.mult); TT(v2, hfin[:, 1], fim_c, ALU.mult); TT(v3_, v1, v2, ALU.subtract)
        TT(v1, hfin[:, 0], fim_c, ALU.mult); TT(v2, hfin[:, 1], fre_c, ALU.mult); TT(hfin[:, 1], v1, v2, ALU.add)
        op("dve", lambda h: h.tensor_copy(out=hfin[:, 0], in_=v3_))
        P.dma("sp", lambda h: h.dma_start(out=hfin_d, in_=hfin[:]), reads=["hfin"], writes=["hfin_d"])
        P.dma("sp", lambda h: h.dma_start(out=pconv_d, in_=gcar[:]), reads=["gcar"], writes=["pconv_d"])
        P.dma("sp", lambda h: h.dma_start(out=sconv_d, in_=sconv[:]), reads=["sconv"], writes=["sconv_d"])
        P.limit = None
        P.barrier()

        with nc.Block() as block:
            @block.tensor
            def _(h):
                P.run("pe", h, sems)

            @block.scalar
            def _(h):
                P.run("act", h, sems)

            @block.vector
            def _(h):
                P.run("dve", h, sems)

            @block.gpsimd
            def _(h):
                P.run("pool", h, sems)

            @block.sync
            def _(h):
                P.run("sp", h, sems)
    return nc


def _consts():
    ident = np.eye(128, dtype=np.float32)
    i = np.arange(128)[:, None]
    c = np.arange(256)[None, :]
    full = np.where(((c < 128) & (c > i)) | ((c >= 128) & (c - 128 <= i)), 0.0, MASKV)
    m1 = np.where(((c < 128) & (c > i) & (c >= NPAD)) | ((c >= 128) & (c - 128 <= i)), 0.0, MASKV)
    m0 = np.where((c >= 128) & (c - 128 <= i) & (c - 128 >= NPAD), 0.0, MASKV)
    masks = np.stack([m0, m1, full], axis=1).astype(np.float32)
    sm = np.full((128, 17 * 128), MASKV, np.float32)
    for s in range(NSEQ):
        for t in range(4):
            r = s * 4 + t
            sm[r, s * 128 + t + 1:(s + 1) * 128] = 0.0
            sm[r, 2048 + s * 4:2048 + s * 4 + t + 1] = 0.0
    return ident, masks, sm


def prep_inputs(x_prompt, x_sample, cache_k_win, cache_v_win, state_ssm_re, state_ssm_im, state_conv,
           meta_tokens, g_mix, w_in, sinks, lam_re, lam_im, log_dt, b_re, b_im, c_re, c_im, d_skip,
           w_glu, b_glu, g_attn_out, g_ssm_out, w_o, g_ffn, w_gate, w_up, conv_w, conv_b, w_down,
           g_final):
    f32 = np.float32
    A = lambda a: np.ascontiguousarray(np.asarray(a, dtype=f32))
    x_prompt, x_sample = A(x_prompt), A(x_sample)
    ident, masks, smask = _consts()
    w_in0 = A(w_in)[0]
    perm = []
    for i in range(4):
        perm += list(range(i * 64, (i + 1) * 64)) + list(range((4 + i) * 64, (5 + i) * 64))
    w_in_p = np.ascontiguousarray(np.concatenate([w_in0[:, perm], w_in0[:, 512:]], axis=1))
    pcol = np.zeros((128, 128), f32)
    pcol[:, 0:8] = A(g_mix)[0].reshape(8, 128).T
    pcol[:, 8:16] = A(g_ffn)[0].reshape(8, 128).T
    pcol[:, 16:20] = A(g_attn_out)[0].reshape(4, 128).T
    pcol[:, 20:24] = A(g_ssm_out)[0].reshape(4, 128).T
    pcol[:, 24:28] = A(b_glu)[0].reshape(4, 128).T
    pcol[:, 28:32] = A(d_skip)[0].reshape(4, 128).T
    cw = A(conv_w)[0].reshape(3, NFC, 128)
    pcol[:, 32:98] = cw.transpose(2, 1, 0).reshape(128, 66)
    pcol[:, 98:120] = A(conv_b)[0].reshape(NFC, 128).T
    lr, li, ld = A(lam_re)[0], A(lam_im)[0], A(log_dt)[0]
    ldx = np.repeat(ld[:, None], 64, axis=1)

    def pl(a):
        return a.reshape(16, 2, 64).transpose(1, 2, 0).reshape(128, 16)

    lam = np.ascontiguousarray(np.stack([pl(lr), pl(li), pl(ldx)], axis=1))
    lamb = np.ascontiguousarray(np.stack([lr.reshape(-1), li.reshape(-1), ldx.reshape(-1)], axis=0))
    bre, bim, cre, cim = A(b_re)[0], A(b_im)[0], A(c_re)[0], A(c_im)[0]
    bblk = np.zeros((128, 2, 16, 128), f32)
    cblk = np.zeros((128, 2, 16, 128), f32)
    for q in range(16):
        for j2 in range(2):
            g = 2 * q + j2
            g8 = g % 8
            rows = slice(g8 * 16, g8 * 16 + 16)
            cols = slice(j2 * 64, j2 * 64 + 64)
            bblk[rows, 0, q, cols] = bre[g].T
            bblk[rows, 1, q, cols] = bim[g].T
            cblk[cols, 0, q, rows] = cre[g].T
            cblk[cols, 1, q, rows] = cim[g].T
    sre, sim_ = A(state_ssm_re)[0], A(state_ssm_im)[0]
    ck, cvv = A(cache_k_win)[0].reshape(128, 128, 128), A(cache_v_win)[0].reshape(128, 128, 128)
    sc = A(state_conv)[0]
    meta = A(meta_tokens)
    wg_l = np.ascontiguousarray(A(w_gate)[0].reshape(8, 128, NFC, 128).transpose(2, 1, 0, 3))
    wu_l = np.ascontiguousarray(A(w_up)[0].reshape(8, 128, NFC, 128).transpose(2, 1, 0, 3))
    wd_l = np.ascontiguousarray(A(w_down)[0].reshape(NFC // 2, 2, 128, 2, 512).transpose(3, 0, 2, 1, 4))
    in_maps = []
    for c in range(NCORES):
        xin = np.zeros((NPT * 128 + 128, D), f32)
        xin[NPAD:128] = meta
        xin[128:NPT * 128] = x_prompt[c]
        xin[NPT * 128:NPT * 128 + NS] = x_sample[c * NSEQ:(c + 1) * NSEQ].reshape(NS, D)
        sl_ = slice(c * NSEQ, (c + 1) * NSEQ)

        def hl(a):
            return a.reshape(NSEQ, 16, 2, 64).transpose(2, 3, 1, 0).reshape(128, 16, NSEQ)

        h0 = np.ascontiguousarray(np.stack([hl(sre[sl_]), hl(sim_[sl_])], axis=1))
        cvst = np.ascontiguousarray(sc[sl_].reshape(NSEQ, 2, NFC, 128).transpose(3, 2, 0, 1))
        kc, vc = ck[sl_], cvv[sl_]
        kcT = np.ascontiguousarray(kc.transpose(2, 0, 1).reshape(128, NSEQ * 128))
        vcl = np.ascontiguousarray(vc.transpose(1, 0, 2))
        in_maps.append(dict(
            xin=xin, w_in=w_in_p, w_o=A(w_o)[0], w_glu=A(w_glu)[0], w_gate=wg_l, w_up=wu_l,
            w_down=wd_l, ident=ident, masks=masks, smask=smask, pcol=pcol, sinks=A(sinks)[0],
            gfin=A(g_final), lam=lam, lamb=lamb, bblk=bblk, cblk=cblk, h0=h0, cvst=cvst, kcT=kcT, vc=vcl,
            kcache=np.ascontiguousarray(kc), vcache=np.ascontiguousarray(vc)))
    return in_maps


def assemble(R):
    f32 = np.float32
    y_prompt = np.stack([R[c]["y"][128:NPT * 128] for c in range(NCORES)])
    y_sample = np.concatenate([R[c]["y"][NPT * 128:NPT * 128 + NS].reshape(NSEQ, 4, D) for c in range(NCORES)])
    p_k = np.stack([R[c]["kvp"][:, 0:128].reshape(128, 2, 64) for c in range(NCORES)])[None]
    p_v = np.stack([R[c]["kvp"][:, 128:256].reshape(128, 2, 64) for c in range(NCORES)])[None]
    s_k = np.concatenate([R[c]["kvs_k"].reshape(NSEQ, 128, 2, 64) for c in range(NCORES)])[None]
    s_v = np.concatenate([R[c]["kvs_v"].reshape(NSEQ, 128, 2, 64) for c in range(NCORES)])[None]

    def unh(a):
        nn = a.shape[-1]
        return a.reshape(2, 64, 16, nn).transpose(3, 2, 0, 1).reshape(nn, 32, 64)

    p_re = np.stack([unh(R[c]["hfin"][:, 0, :, NSEQ:])[0] for c in range(NCORES)])[None]
    p_im = np.stack([unh(R[c]["hfin"][:, 1, :, NSEQ:])[0] for c in range(NCORES)])[None]
    s_re = np.concatenate([unh(R[c]["hfin"][:, 0, :, :NSEQ]) for c in range(NCORES)])[None]
    s_im = np.concatenate([unh(R[c]["hfin"][:, 1, :, :NSEQ]) for c in range(NCORES)])[None]
    p_conv = np.stack([R[c]["pconv"].transpose(2, 1, 0).reshape(2, FF) for c in range(NCORES)])[None]
    s_conv = np.concatenate([R[c]["sconv"].transpose(2, 3, 1, 0).reshape(NSEQ, 2, FF) for c in range(NCORES)])[None]
    out = (y_prompt, y_sample, p_k, p_v, p_re, p_im, p_conv, s_k, s_v, s_re, s_im, s_conv)
    return tuple(np.ascontiguousarray(o, dtype=f32) for o in out)


def kernel(**inputs):
    in_maps = prep_inputs(**inputs)
    nc = build_nc()
    res = run_bass_kernel_spmd(nc, in_maps, core_ids=list(range(NCORES)))
    return assemble(res.results)
```

```python
import numpy as np
from contextlib import ExitStack
import concourse.bass as bass
import concourse.mybir as mybir
from concourse.bass_utils import run_bass_kernel_spmd

F32 = mybir.dt.float32
BF16 = mybir.dt.bfloat16
ALU = mybir.AluOpType
AF = mybir.ActivationFunctionType
AX = mybir.AxisListType

ENGS = ("pe", "act", "dve", "pool", "sp")
NDMASEM = 12
NCORES = 8
D = 1024
NPT = 17
NPAD = 112
NS = 64
NSEQ = 16
FF = 2816
NFC = 22
EPS = 1e-5
PI = float(np.pi)
MASKV = -30000.0


class _Recorder:
    def __init__(self):
        self.call = None

    def __getattr__(self, name):
        def f(*args, **kwargs):
            self.call = (name, args, kwargs)
            return self
        return f


class Prog:
    def __init__(self):
        self.streams = {e: [] for e in ENGS}
        self.count = {e: 0 for e in ENGS}
        self.waited = {e: {} for e in ENGS}
        self.regs = {}
        self.nrec = 0
        self.limit = None
        self.capture = None
        self.dma_rr = {"sp": 0, "pool": 0}
        self.dma_cnt = {}

    @staticmethod
    def _region(ap):
        dims = [(int(st), int(sz)) for st, sz in ap.ap]
        off = int(ap.offset)
        esz = mybir.dt.size(ap.dtype)
        space = str(ap.space)
        if space == "DRAM":
            ext = sum((sz - 1) * abs(st) for st, sz in dims)
            return (0, 1, off, off + ext)
        pst, npart = dims[0]
        pst = max(pst, 1)
        p0, f0 = off // pst, off % pst
        ext = sum((sz - 1) * abs(st) for st, sz in dims[1:])
        f0, ext = f0 * esz, ext * esz + esz - 1
        if space == "PSUM":
            return (0, 128, 0, 1 << 30)
        return (p0, p0 + npart, f0, f0 + ext)

    @staticmethod
    def _is_ap(v):
        return hasattr(v, "tensor") and hasattr(v, "ap") and hasattr(v, "offset")

    def _record(self, fn):
        rec = _Recorder()
        fn(rec)
        name, args, kwargs = rec.call
        acc = []
        for i, a in enumerate(args):
            if self._is_ap(a):
                acc.append((a, i == 0))
        for k, v in kwargs.items():
            if self._is_ap(v):
                acc.append((v, k in ("out", "accum_out")))
        out = []
        for ap, w in acc:
            if str(ap.space) == "PSUM":
                w = True
            out.append((ap.tensor.name, self._region(ap), w))
        self._last_call = rec.call
        return out

    def _deps(self, eng, acc):
        deps = {}
        for name, R, w in acc:
            for (R2, w2), evs in self.regs.get(name, {}).items():
                if not (w or w2):
                    continue
                if R[0] < R2[1] and R2[0] < R[1] and R[2] <= R2[3] and R2[2] <= R[3]:
                    for s_, v in evs.items():
                        if s_ == eng and eng == "pe":
                            continue
                        if deps.get(s_, 0) < v:
                            deps[s_] = v
        out = []
        wd = self.waited[eng]
        for s_, v in deps.items():
            if wd.get(s_, 0) < v:
                wd[s_] = v
                out.append((s_, v))
        return out

    def _commit(self, ev, acc):
        s_, v = ev
        for name, R, w in acc:
            d = self.regs.setdefault(name, {})
            if w:
                for key in [k for k in d if k[0][0] >= R[0] and k[0][1] <= R[1] and k[0][2] >= R[2] and k[0][3] <= R[3]]:
                    del d[key]
            e = d.setdefault((R, w), {})
            if e.get(s_, 0) < v:
                e[s_] = v

    @staticmethod
    def _freeze(fn):
        rec = _Recorder()
        fn(rec)
        return lambda h, c=rec.call: getattr(h, c[0])(*c[1], **c[2])

    def op(self, eng, fn, reads=(), writes=()):
        if self.capture is not None:
            self.capture.append(("op", eng, self._freeze(fn)))
            return
        self.nrec += 1
        if self.limit is not None and self.nrec > self.limit:
            return
        acc = self._record(fn)
        deps = self._deps(eng, acc)
        self.count[eng] += 1
        ev = (eng, self.count[eng])
        self.streams[eng].append((deps, self._last_call, (eng, 1)))
        self._commit(ev, acc)

    def dma(self, eng, fn, reads=(), writes=()):
        if self.capture is not None:
            self.capture.append(("dma", eng, self._freeze(fn)))
            return
        self.nrec += 1
        if self.limit is not None and self.nrec > self.limit:
            return
        acc = self._record(fn)
        k = self.dma_rr[eng]
        self.dma_rr[eng] = (k + 1) % NDMASEM
        sname = "dma%s%d" % (eng, k)
        deps = self._deps(eng, acc)
        prev = self.dma_cnt.get(sname, 0) * 16
        if prev and self.waited[eng].get(sname, 0) < prev:
            self.waited[eng][sname] = prev
            deps.append((sname, prev))
        self.dma_cnt[sname] = self.dma_cnt.get(sname, 0) + 1
        ev = (sname, self.dma_cnt[sname] * 16)
        self.streams[eng].append((deps, self._last_call, (sname, 16)))
        self._commit(ev, acc)

    def barrier(self):
        evs = [(e, self.count[e]) for e in ENGS if self.count[e]]
        evs += [(k, c * 16) for k, c in self.dma_cnt.items()]
        for e in ENGS:
            deps = []
            for s, v in evs:
                if s == e:
                    continue
                if self.waited[e].get(s, 0) < v:
                    self.waited[e][s] = v
                    deps.append((s, v))
            if deps:
                self.streams[e].append((deps, None, None))

    def run(self, eng, h, sems):
        for deps, fn, inc in self.streams[eng]:
            for s, v in deps:
                h.wait_ge(sems[s], v)
            if fn is not None:
                name, args, kwargs = fn
                getattr(h, name)(*args, **kwargs).then_inc(sems[inc[0]], inc[1])


def build_nc(tiles=None, limit=None):
    nc = bass.Bass("TRN2", target_bir_lowering=False)

    def din(name, shape, dt=F32):
        return nc.dram_tensor(name, list(shape), dt, kind="ExternalInput").ap()

    def dout(name, shape, dt=F32):
        return nc.dram_tensor(name, list(shape), dt, kind="ExternalOutput").ap()

    xin = din("xin", [NPT * 128 + 128, D])
    w_in = din("w_in", [D, 1280])
    w_o = din("w_o", [D, D])
    w_glu = din("w_glu", [512, 512])
    w_gate = din("w_gate", [NFC, 128, 8, 128])
    w_up = din("w_up", [NFC, 128, 8, 128])
    w_down = din("w_down", [2, NFC // 2, 128, 2, 512])
    ident_d = din("ident", [128, 128])
    masks_d = din("masks", [128, 3, 256])
    smask_d = din("smask", [128, 17 * 128])
    pcol_d = din("pcol", [128, 128])
    sinks_d = din("sinks", [8])
    gfin_d = din("gfin", [D])
    lam_d = din("lam", [128, 3, 16])
    lamb_d = din("lamb", [3, 16 * 128])
    bblk_d = din("bblk", [128, 2, 16, 128])
    cblk_d = din("cblk", [128, 2, 16, 128])
    h0_d = din("h0", [128, 2, 16, NSEQ])
    cvst_d = din("cvst", [128, NFC, NSEQ, 2])
    kcT_d = din("kcT", [128, NSEQ * 128])
    vc_d = din("vc", [128, NSEQ, 128])
    kcache_d = din("kcache", [NSEQ, 128, 128])
    vcache_d = din("vcache", [NSEQ, 128, 128])

    wgb_d = nc.dram_tensor("wgb", [NFC, 128, 8, 128], BF16, kind="Internal").ap()
    wub_d = nc.dram_tensor("wub", [NFC, 128, 8, 128], BF16, kind="Internal").ap()
    wdb_d = nc.dram_tensor("wdb", [2, NFC // 2, 128, 2, 512], BF16, kind="Internal").ap()
    y_d = dout("y", [NPT * 128 + 128, D])
    kvp_d = dout("kvp", [128, 256])
    kvs_k = dout("kvs_k", [NSEQ, 128, 128])
    kvs_v = dout("kvs_v", [NSEQ, 128, 128])
    hfin_d = dout("hfin", [128, 2, 16, NSEQ + 1])
    pconv_d = dout("pconv", [128, NFC, 2])
    sconv_d = dout("sconv", [128, NFC, NSEQ, 2])

    P = Prog()
    P.limit = limit
    es = ExitStack()
    with es:
        def sb(name, shape, dt=F32):
            return es.enter_context(nc.sbuf_tensor(name, list(shape), dt))

        def ps(name, shape, dt=F32):
            return es.enter_context(nc.psum_tensor(name, list(shape), dt))

        sems = {e: es.enter_context(nc.semaphore("s_" + e)) for e in ENGS}
        for k in range(NDMASEM):
            for e_ in ("sp", "pool"):
                sems["dma%s%d" % (e_, k)] = es.enter_context(nc.semaphore("s_dma%s%d" % (e_, k)))

        idb = sb("idb", [128, 128], BF16)
        onesb = sb("onesb", [128, 128], BF16)
        masks = sb("masks_s", [128, 3, 256])
        smask = sb("smask_s", [128, 17 * 128], BF16)
        pcol = sb("pcol_s", [128, 128])
        sink8 = sb("sink8", [128, 8])
        gfb = sb("gfb", [128, D])
        epsb = sb("epsb", [128, 1])
        lam = sb("lam_s", [128, 3, 16])
        WBre = sb("WBre", [128, 16, 128], BF16); WBim = sb("WBim", [128, 16, 128], BF16)
        WCre = sb("WCre", [128, 16, 128], BF16); WCimn = sb("WCimn", [128, 16, 128], BF16)
        cs = sb("cs", [128, 16, 129]); sn = sb("sn", [128, 16, 129])
        mask64 = sb("mask64", [128, NS]); dtmp = sb("dtmp", [128, NS])
        are = sb("are", [128, 16]); aim = sb("aim", [128, 16])
        ah = sb("ah", [128, 2, 16, NSEQ])
        car = sb("car", [128, 2, 16])
        hfin = sb("hfin_s", [128, 2, 16, NSEQ + 1])
        gcar = sb("gcar", [128, NFC, 2])
        sconv = sb("sconv_s", [128, NFC, NSEQ, 2])
        kTx = sb("kTx", [128, 2, 256], BF16)
        vx = sb("vx", [128, 2, 128], BF16)
        GMIX, GFFN, GATT, GSSM, BGLU, DSK, CVW, CVB = 0, 8, 16, 20, 24, 28, 32, 98

        def col(c):
            return pcol[:, c:c + 1]

        ssq = sb("ssq", [128, 4]); rstd = sb("rstd", [128, 4])
        xn = sb("xn", [128, D], BF16)
        junk = xn
        xnT = sb("xnT", [128, 8, 128], BF16)
        qT = sb("qT", [128, 4, 128], BF16)
        uT2 = [sb("uT%d" % i, [128, 4, 128], BF16) for i in range(2)]
        Sx = sb("Sx", [128, 1, 2, 257]); mx = sb("mx", [128, 2]); nbias = sb("nbias", [128, 2])
        rs = sb("rs", [128, 2]); rinv = sb("rinv", [128, 2])
        Pb = sb("Pb", [128, 2, 257], BF16); PTs = sb("PTs", [128, 4, 128], BF16)
        mixT2 = [sb("mixT%d" % i, [128, 8, 128], BF16) for i in range(2)]
        mixT = mixT2[0]
        Psb = sb("Psb", [128, 17 * 128 + 1], BF16)
        PTss = sb("PTss", [128, 17, 128], BF16)
        pp4 = sb("pp4", [128, 4, 4, 128]); rr4 = sb("rr4", [128, 2, 4, 128]); vv4 = sb("vv4", [128, 2, 4, 128])
        hb = sb("hb", [128, 4, 2, 128], BF16)
        gl32 = sb("gl32", [128, 4, 128]); glb = sb("glb", [128, 4, 128], BF16)
        xn2T = sb("xn2T", [128, 8, 512], BF16)
        gx = sb("gx", [128, 2, 514]); gxs = sb("gxs", [128, NSEQ, 6])
        cv = sb("cv", [128, 1, 512]); sl = sb("sl", [128, 1, 512])
        kvtok = sl[:, 0, 0:256]
        yv = gx[:, 0, 0:128]; sg = gx[:, 0, 128:256]; rsb = gx[:, 0, 256:384]
        sqb = gx[:, 1, 0:256].bitcast(BF16).rearrange("p (a k) -> p a k", a=4)
        attn = cv[:, 0, :]
        anb = sl[:, 0, 256:512].bitcast(BF16)
        wi = [sb("wi0", [128, 8, 1280], BF16)] * 2
        wo = [sb("wo0", [128, 8, D], BF16)] * 2
        wgl = [sb("wgl0", [128, 4, 512], BF16)] * 2
        wg = [sb("wg%d" % i, [128, 8, 128], BF16) for i in range(2)] + [xnT]
        wu = [sb("wu%d" % i, [128, 8, 128], BF16) for i in range(2)] + [mixT]
        wdA = sb("wdA", [128, 2, 512], BF16)
        wd = [wdA[:], Sx[:].rearrange("p a b c -> p (a b c)").bitcast(BF16)[:, 0:1024].rearrange("p (j n) -> p j n", j=2),
              xn[:].rearrange("p (j n) -> p j n", j=2)]
        big = sb("big", [128, 9728])
        xb = big[:, 0:4096].rearrange("p (t d) -> p t d", t=4)
        hTall = big[:, 4096:9728].bitcast(BF16).rearrange("p (c n) -> p c n", c=NFC)
        hTsm = big[:, 4096:5504].bitcast(BF16).rearrange("p (c n) -> p c n", c=NFC)
        kTs = big[:, 5504:7680].bitcast(BF16).rearrange("p (a k) -> p a k", a=2)
        vs = big[:, 7680:8768].bitcast(BF16).rearrange("p (b k) -> p b k", b=17)
        scr = big
        h0 = big[:, 8192:8704].rearrange("p (a q s) -> p a q s", a=2, q=16)
        cvst = big[:, 3264:3968].rearrange("p (c s j) -> p c s j", c=NFC, s=NSEQ)
        def prep_views(o):
            return (scr[:, o:o + 768].rearrange("p (a b) -> p a b", a=3), scr[:, o + 768:o + 2304].rearrange("p (a b) -> p a b", a=6),
                    scr[:, o + 2304:o + 2816].rearrange("p (a q m) -> p a q m", a=2, q=2), scr[:, o + 2816:o + 3328].rearrange("p (a q m) -> p a q m", a=2, q=2))
        Ssx = big[:, 1024:1024 + 17 * 128 + 1]
        idf = big[:, 8960:9088]

        mmA = ps("mmA", [128, 512]); mmB = ps("mmB", [128, 512]); mmC = ps("mmC", [128, 512])
        pT = ps("pT", [128, 8, 128], BF16)
        pT32 = pT[:].rearrange("p c k -> p (c k)").bitcast(F32)
        Sps = ps("Sps", [128, 512]); Ops = ps("Ops", [128, 512])
        acc0 = ps("acc0", [128, 512]); acc1 = ps("acc1", [128, 512])

        op = P.op

        P.dma("sp", lambda h: h.dma_start(out=idf[:], in_=ident_d), writes=["idf"])
        P.dma("sp", lambda h: h.dma_start(out=masks[:], in_=masks_d), writes=["masks"])
        P.dma("pool", lambda h: h.dma_start(out=smask[:], in_=smask_d), writes=["smask"])
        P.dma("sp", lambda h: h.dma_start(out=pcol[:], in_=pcol_d), writes=["pcol"])
        P.dma("sp", lambda h: h.dma_start(out=sink8[:], in_=sinks_d.partition_broadcast(128)), writes=["sink8"])
        P.dma("sp", lambda h: h.dma_start(out=gfb[:], in_=gfin_d.partition_broadcast(128)), writes=["gfb"])
        P.dma("sp", lambda h: h.dma_start(out=lam[:], in_=lam_d), writes=["lam"])
        P.dma("sp", lambda h: h.dma_start(out=h0, in_=h0_d))
        op("pool", lambda h: h.memset(hb[:], 0.0))
        op("pool", lambda h: h.memset(cv[:], 0.0))
        op("pool", lambda h: h.memset(gx[:], 0.0))
        P.dma("pool", lambda h: h.dma_start(out=wi[0][:], in_=w_in.rearrange("(c p) n -> p c n", p=128)))
        P.dma("pool", lambda h: h.dma_start(out=wgl[0][:], in_=w_glu.rearrange("(c p) n -> p c n", p=128)))
        P.dma("pool", lambda h: h.dma_start(out=wo[0][:], in_=w_o.rearrange("(c p) n -> p c n", p=128)))
        P.dma("sp", lambda h: h.dma_start(out=kvs_k[:, 0:124, :], in_=kcache_d[:, 4:128, :]), writes=["kvs_k_a"])
        P.dma("sp", lambda h: h.dma_start(out=kvs_v[:, 0:124, :], in_=vcache_d[:, 4:128, :]), writes=["kvs_v_a"])

        op("dve", lambda h: h.tensor_copy(out=idb[:], in_=idf[:]), ["idf"], ["idb"])
        op("pool", lambda h: h.memset(onesb[:], 1.0), [], ["onesb"])
        op("pool", lambda h: h.memset(epsb[:], EPS), [], ["epsb"])
        op("pool", lambda h: h.memset(kTx[:], 0.0), [], ["kTx"])
        op("pool", lambda h: h.memset(vx[:], 0.0), [], ["vx"])
        op("pool", lambda h: h.memset(car[:], 0.0), [], ["car"])
        op("pool", lambda h: h.memset(gcar[:], 0.0), [], ["gcar"])
        pass
        op("pool", lambda h: h.memset(hfin[:], 0.0))
        op("pool", lambda h: h.memset(sconv[:], 0.0))
        rho = sb("rho", [128, 16]); fre = sb("fre", [128, 16]); fim = sb("fim", [128, 16])
        dtl, thl, c1, s1, tmpa, kfL = (big[:, 9088 + 16 * i:9104 + 16 * i] for i in range(6))
        kiL = big[:, 9184:9200].bitcast(mybir.dt.int32)
        op("act", lambda h: h.activation(out=dtl[:], in_=lam[:, 2, :], func=AF.Exp), ["lam"], ["dtl"])
        op("dve", lambda h: h.tensor_tensor(out=thl[:], in0=lam[:, 1, :], in1=dtl[:], op=ALU.mult), ["lam", "dtl"], ["thl"])
        op("dve", lambda h: h.tensor_tensor(out=tmpa[:], in0=lam[:, 0, :], in1=dtl[:], op=ALU.mult), ["lam", "dtl"], ["tmpa"])
        op("act", lambda h: h.activation(out=rho[:], in_=tmpa[:], func=AF.Exp), ["tmpa"], ["rho"])

        def sincos(eng_tag, th_ap, s_ap, c_ap, tmp_ap, shape_key, ki_ap, kf_ap):
            K = shape_key

            def reduce_(shift, dst_key):
                op("dve", lambda h: h.tensor_scalar(out=tmp_ap, in0=th_ap, scalar1=shift, scalar2=1.0 / (2 * PI), op0=ALU.add, op1=ALU.mult),
                   [K + "th", K + "s", K + "c"], [K + "tmp"])
                op("dve", lambda h: h.tensor_copy(out=ki_ap, in_=tmp_ap), [K + "tmp"], [K + "ki"])
                op("dve", lambda h: h.tensor_copy(out=kf_ap, in_=ki_ap), [K + "ki"], [K + "kf"])
                op("dve", lambda h: h.tensor_scalar(out=tmp_ap, in0=th_ap, scalar1=shift, scalar2=None, op0=ALU.add), [K + "th", K + "ki"], [K + "tmp"])
                op("dve", lambda h: h.scalar_tensor_tensor(out=tmp_ap, in0=kf_ap, scalar=-2 * PI, in1=tmp_ap, op0=ALU.mult, op1=ALU.add),
                   [K + "kf", K + "tmp"], [K + "tmp"])
                op("dve", lambda h: h.tensor_scalar(out=kf_ap, in0=tmp_ap, scalar1=PI, scalar2=None, op0=ALU.is_gt), [K + "tmp"], [K + "kf"])
                op("dve", lambda h: h.scalar_tensor_tensor(out=tmp_ap, in0=kf_ap, scalar=-2 * PI, in1=tmp_ap, op0=ALU.mult, op1=ALU.add),
                   [K + "kf", K + "tmp"], [K + "tmp"])
                op("dve", lambda h: h.tensor_scalar(out=kf_ap, in0=tmp_ap, scalar1=-PI, scalar2=None, op0=ALU.is_lt), [K + "tmp"], [K + "kf"])
                op("dve", lambda h: h.scalar_tensor_tensor(out=tmp_ap, in0=kf_ap, scalar=2 * PI, in1=tmp_ap, op0=ALU.mult, op1=ALU.add),
                   [K + "kf", K + "tmp"], [K + "tmp"])
                op("dve", lambda h: h.tensor_scalar(out=tmp_ap, in0=tmp_ap, scalar1=-PI, scalar2=PI, op0=ALU.max, op1=ALU.min), [K + "tmp"], [K + "tmp"])

            reduce_(0.0, "s")
            op("act", lambda h: h.activation(out=s_ap, in_=tmp_ap, func=AF.Sin), [K + "tmp"], [K + "s"])
            reduce_(0.5 * PI, "c")
            op("act", lambda h: h.activation(out=c_ap, in_=tmp_ap, func=AF.Sin), [K + "tmp"], [K + "c"])

        sincos("L", thl[:], s1[:], c1[:], tmpa[:], "L", kiL[:], kfL[:])
        op("dve", lambda h: h.tensor_tensor(out=are[:], in0=rho[:], in1=c1[:], op=ALU.mult), ["rho", "Lc"], ["are"])
        op("dve", lambda h: h.tensor_tensor(out=aim[:], in0=rho[:], in1=s1[:], op=ALU.mult), ["rho", "Ls"], ["aim"])
        op("pool", lambda h: h.memset(cs[:, :, 0:1], 1.0), [], ["cs"])
        op("pool", lambda h: h.memset(sn[:, :, 0:1], 0.0), [], ["sn"])
        op("dve", lambda h: h.tensor_copy(out=cs[:, :, 1], in_=c1[:]), ["Lc", "cs"], ["cs"])
        op("dve", lambda h: h.tensor_copy(out=sn[:, :, 1], in_=s1[:]), ["Ls", "sn"], ["sn"])
        tA = pp4[:, 0:2].rearrange("p a j (b c) -> p (a j b) c", c=64); tB = big[:, 6656:7680].rearrange("p (a c) -> p a c", c=64)
        m = 1
        while m < 128:
            cm = cs[:, :, m:m + 1].broadcast_to([128, 16, m]); sm = sn[:, :, m:m + 1].broadcast_to([128, 16, m])
            a_c = cs[:, :, 1:m + 1]; a_s = sn[:, :, 1:m + 1]
            o_c = cs[:, :, m + 1:2 * m + 1]; o_s = sn[:, :, m + 1:2 * m + 1]
            ta = tA[:, :, 0:m]; tb = tB[:, :, 0:m]
            op("dve", lambda h, a_c=a_c, cm=cm, ta=ta: h.tensor_tensor(out=ta, in0=a_c, in1=cm, op=ALU.mult), ["cs", "sn"], ["tA"])
            op("dve", lambda h, a_s=a_s, sm=sm, tb=tb: h.tensor_tensor(out=tb, in0=a_s, in1=sm, op=ALU.mult), ["cs", "sn"], ["tB"])
            op("dve", lambda h, o_c=o_c, ta=ta, tb=tb: h.tensor_tensor(out=o_c, in0=ta, in1=tb, op=ALU.subtract), ["tA", "tB", "sn"], ["cs"])
            op("dve", lambda h, a_c=a_c, sm=sm, ta=ta: h.tensor_tensor(out=ta, in0=a_c, in1=sm, op=ALU.mult), ["cs", "sn"], ["tA"])
            op("dve", lambda h, a_s=a_s, cm=cm, tb=tb: h.tensor_tensor(out=tb, in0=a_s, in1=cm, op=ALU.mult), ["cs", "sn"], ["tB"])
            op("dve", lambda h, o_s=o_s, ta=ta, tb=tb: h.tensor_tensor(out=o_s, in0=ta, in1=tb, op=ALU.add), ["tA", "tB", "cs"], ["sn"])
            m *= 2
        op("pool", lambda h: h.memset(mask64[:], 1.0))
        op("pool", lambda h: h.memset(mask64[:].rearrange("p (s t) -> p s t", t=4)[:, :, 0:1], 0.0))
        sm_ = [big[:, 9200 + 16 * i:9216 + 16 * i] for i in range(8)]
        lr_, li_ = lam[:, 0, :], lam[:, 1, :]
        nr, den, t_a, t_b, gr, gi = sm_[0], sm_[1], sm_[2], sm_[3], sm_[4], sm_[5]
        TT = lambda o, x, y, f_: op("dve", lambda h: h.tensor_tensor(out=o, in0=x, in1=y, op=f_))
        op("dve", lambda h: h.tensor_scalar(out=nr, in0=are[:], scalar1=-1.0, scalar2=None, op0=ALU.add))
        TT(den, lr_, lr_, ALU.mult); TT(t_a, li_, li_, ALU.mult); TT(den, den, t_a, ALU.add)
        op("dve", lambda h: h.reciprocal(out=den, in_=den))
        TT(t_a, nr, lr_, ALU.mult); TT(t_b, aim[:], li_, ALU.mult); TT(t_a, t_a, t_b, ALU.add); TT(fre[:], t_a, den, ALU.mult)
        TT(t_a, aim[:], lr_, ALU.mult); TT(t_b, nr, li_, ALU.mult); TT(t_a, t_a, t_b, ALU.subtract); TT(fim[:], t_a, den, ALU.mult)
        TT(den, fre[:], fre[:], ALU.mult); TT(t_a, fim[:], fim[:], ALU.mult); TT(den, den, t_a, ALU.add)
        op("dve", lambda h: h.reciprocal(out=den, in_=den))
        TT(gr, fre[:], den, ALU.mult); TT(gi, fim[:], den, ALU.mult)
        op("dve", lambda h: h.tensor_scalar(out=gi, in0=gi, scalar1=-1.0, scalar2=None, op0=ALU.mult))
        P.dma("pool", lambda h: h.dma_start(out=WBre[:], in_=bblk_d[:, 0]))
        P.dma("pool", lambda h: h.dma_start(out=WBim[:], in_=bblk_d[:, 1]))
        cbf = big[:, 0:4096].rearrange("p (a q m) -> p a q m", a=2, q=16)
        u1 = big[:, 4096:6144].rearrange("p (q m) -> p q m", q=16); u2 = big[:, 6144:8192].rearrange("p (q m) -> p q m", q=16)
        P.dma("sp", lambda h: h.dma_start(out=cbf, in_=cblk_d))
        fre_b = fre[:].unsqueeze(2).broadcast_to([128, 16, 128]); fim_b = fim[:].unsqueeze(2).broadcast_to([128, 16, 128])
        TT(u1, cbf[:, 0], fre_b, ALU.mult); TT(u2, cbf[:, 1], fim_b, ALU.mult); TT(WCre[:], u1, u2, ALU.subtract)
        TT(u1, cbf[:, 0], fim_b, ALU.mult); TT(u2, cbf[:, 1], fre_b, ALU.mult); TT(u1, u1, u2, ALU.add)
        op("dve", lambda h: h.tensor_scalar(out=WCimn[:], in0=u1, scalar1=-1.0, scalar2=None, op0=ALU.mult))

        tD = big[:, 9344:9600].rearrange("p (q s) -> p q s", q=16)
        tE = big[:, 8704:8960].rearrange("p (q s) -> p q s", q=16)
        gr_b = gr.unsqueeze(2).broadcast_to([128, 16, NSEQ]); gi_b = gi.unsqueeze(2).broadcast_to([128, 16, NSEQ])
        TT(tD, h0[:, 0], gr_b, ALU.mult); TT(tE, h0[:, 1], gi_b, ALU.mult); TT(tD, tD, tE, ALU.subtract)
        TT(tE, h0[:, 0], gi_b, ALU.mult); TT(h0[:, 0], tD, tD, ALU.max)
        TT(tD, h0[:, 1], gr_b, ALU.mult); TT(h0[:, 1], tE, tD, ALU.add)
        a_re_b = are[:].unsqueeze(2).broadcast_to([128, 16, NSEQ]); a_im_b = aim[:].unsqueeze(2).broadcast_to([128, 16, NSEQ])
        tC = big[:, 8704:8960].rearrange("p (q s) -> p q s", q=16)
        op("dve", lambda h: h.tensor_tensor(out=ah[:, 0], in0=h0[:, 0], in1=a_re_b, op=ALU.mult), ["h0", "are"], ["ah0"])
        op("dve", lambda h: h.tensor_tensor(out=tC[:], in0=h0[:, 1], in1=a_im_b, op=ALU.mult), ["h0", "aim"], ["tC"])
        op("dve", lambda h: h.tensor_tensor(out=ah[:, 0], in0=ah[:, 0], in1=tC[:], op=ALU.subtract), ["ah0", "tC"], ["ah0"])
        op("dve", lambda h: h.tensor_tensor(out=ah[:, 1], in0=h0[:, 0], in1=a_im_b, op=ALU.mult), ["h0", "aim"], ["ah1"])
        op("dve", lambda h: h.tensor_tensor(out=tC[:], in0=h0[:, 1], in1=a_re_b, op=ALU.mult), ["h0", "are", "ah0"], ["tC"])
        op("dve", lambda h: h.tensor_tensor(out=ah[:, 1], in0=ah[:, 1], in1=tC[:], op=ALU.add), ["ah1", "tC"], ["ah1"])


        def rms(src_ap, key_src, n, slot, scale, pn):
            op("act", lambda h: h.activation(out=junk[0:pn, 0:n], in_=src_ap, func=AF.Square, accum_out=ssq[0:pn, slot:slot + 1]),
               [key_src], ["junk", "ssq%d" % slot])
            op("act", lambda h: h.activation(out=rstd[0:pn, slot:slot + 1], in_=ssq[0:pn, slot:slot + 1], func=AF.Ln, scale=scale, bias=epsb[0:pn, 0:1]))
            op("act", lambda h: h.activation(out=rstd[0:pn, slot:slot + 1], in_=rstd[0:pn, slot:slot + 1], func=AF.Exp, scale=-0.5))

        def partA(ti, tl):
            sample = (ti == NPT)
            xt = xb[:, tl, :]
            n = 128
            ns = NS if sample else 128
            r0 = ti * 128
            par = ti % 2
            if sample:
                op("pool", lambda h: h.memset(kTs[:], 0.0))
                P.dma("pool", lambda h: h.dma_start(out=kTs[0:64, 0, 0:NSEQ * 128], in_=kcT_d[0:64, :]))
                P.dma("pool", lambda h: h.dma_start(out=kTs[64:128, 1, 0:NSEQ * 128], in_=kcT_d[64:128, :]))
                P.dma("pool", lambda h: h.dma_start(out=vs[:, 0:NSEQ, :], in_=vc_d))
                P.dma("sp", lambda h: h.dma_start(out=cvst, in_=cvst_d))
            rms(xt[:], "xt", D, 0, 1.0 / D, 128)
            op("dve", lambda h: h.tensor_scalar(out=xn[:], in0=xt[:], scalar1=rstd[:, 0:1], scalar2=None, op0=ALU.mult))
            for c in range(8):
                op("pe", lambda h, c=c: h.transpose(out=pT[:, c, :], in_=xn[:, c * 128:(c + 1) * 128], identity=idb[:]))
            op("dve", lambda h: h.tensor_tensor(out=xnT[:], in0=pT[:], in1=pcol[:, GMIX:GMIX + 8].unsqueeze(2).broadcast_to([128, 8, 128]), op=ALU.mult))
            W = wi[par]
            yield
            for i in range(4):
                bank = acc0 if i % 2 == 0 else acc1
                for c in range(8):
                    op("pe", lambda h, i=i, c=c, bank=bank: h.matmul(bank[:, 0:128], lhsT=W[:, c, i * 128:(i + 1) * 128], rhs=xnT[:, c, :],
                                                                   start=(c == 0), stop=(c == 7)))
                op("act", lambda h, i=i, bank=bank: h.activation(out=qT[:, i, :], in_=bank[:, 0:128], func=AF.Copy))
            yield
            for c in range(8):
                op("pe", lambda h, c=c: h.matmul(Ops[:, 0:128], lhsT=W[:, c, 512:640], rhs=xnT[:, c, :], start=(c == 0), stop=(c == 7)))
            if sample:
                op("dve", lambda h: h.tensor_copy(out=kTs[0:64, 0, NSEQ * 128:17 * 128], in_=Ops[0:64, 0:128]))
                op("dve", lambda h: h.tensor_copy(out=kTs[64:128, 1, NSEQ * 128:17 * 128], in_=Ops[64:128, 0:128]))
            else:
                op("dve", lambda h: h.tensor_copy(out=kTx[0:64, 0, 128:256], in_=Ops[0:64, 0:128]))
                op("dve", lambda h: h.tensor_copy(out=kTx[64:128, 1, 128:256], in_=Ops[64:128, 0:128]))
            yield
            for i in range(4):
                bank = acc0 if i % 2 == 0 else acc1
                for c in range(8):
                    op("pe", lambda h, i=i, c=c, bank=bank: h.matmul(bank[:, 0:128], lhsT=W[:, c, 768 + i * 128:768 + (i + 1) * 128], rhs=xnT[:, c, :],
                                                                   start=(c == 0), stop=(c == 7)))
                op("act", lambda h, i=i, bank=bank: h.activation(out=uT2[par][:, i, :], in_=bank[:, 0:128], func=AF.Copy))
            yield
            for c in range(8):
                op("pe", lambda h, c=c: h.matmul(Ops[:, 0:256], lhsT=xnT[:, c, :], rhs=W[:, c, 512:768], start=(c == 0), stop=(c == 7)))
            if sample:
                op("dve", lambda h: h.tensor_copy(out=vs[:, 16, :], in_=Ops[:, 128:256]))
            else:
                op("dve", lambda h: h.tensor_copy(out=vx[:, 1, :], in_=Ops[:, 128:256]))
            if sample or ti == NPT - 1:
                op("dve", lambda h: h.tensor_copy(out=kvtok[:], in_=Ops[:, 0:256]))
                if sample:
                    P.dma("sp", lambda h: h.dma_start(out=kvs_k[:, 124:128, :], in_=kvtok[0:NS, 0:128]))
                    P.dma("sp", lambda h: h.dma_start(out=kvs_v[:, 124:128, :], in_=kvtok[0:NS, 128:256]))
                else:
                    P.dma("sp", lambda h: h.dma_start(out=kvp_d, in_=kvtok[:]))

            if not sample:
                mi = 0 if ti == 0 else (1 if ti == 1 else 2)
                for i in range(4):
                    for hh_ in range(2):
                        op("pe", lambda h, i=i, hh_=hh_: h.matmul(Sps[:, hh_ * 256:(hh_ + 1) * 256], lhsT=qT[:, i, :], rhs=kTx[:, hh_, :], start=True, stop=True))
                    for hh_ in range(2):
                        op("dve", lambda h, hh_=hh_, hd=i + 4 * hh_: h.tensor_scalar(out=Sx[:, 0, hh_, 256:257], in0=sink8[:, hd:hd + 1], scalar1=8.0, scalar2=None, op0=ALU.mult))
                    op("dve", lambda h: h.tensor_tensor(out=Sx[:, 0, :, 0:256], in0=Sps[:].rearrange("p (a k) -> p a k", a=2),
                                                        in1=masks[:, mi:mi + 1, :].broadcast_to([128, 2, 256]), op=ALU.add))
                    op("dve", lambda h: h.tensor_reduce(out=mx[:], in_=Sx[:, 0], axis=AX.X, op=ALU.max))
                    op("dve", lambda h: h.tensor_scalar(out=nbias[:], in0=mx[:], scalar1=-0.125, scalar2=None, op0=ALU.mult))
                    for hh_ in range(2):
                        op("act", lambda h, hh_=hh_: h.activation(out=Pb[:, hh_, :], in_=Sx[:, 0, hh_, :], func=AF.Exp, scale=0.125,
                                                                 bias=nbias[:, hh_:hh_ + 1], accum_out=rs[:, hh_:hh_ + 1]))
                    op("dve", lambda h: h.reciprocal(out=rinv[:], in_=rs[:]))
                    for hh_ in range(2):
                        for blk in range(2):
                            op("pe", lambda h, hh_=hh_, blk=blk: h.transpose(out=pT[:, hh_ * 2 + blk, :], in_=Pb[:, hh_, blk * 128:(blk + 1) * 128], identity=idb[:]))
                    op("act", lambda h: h.activation(out=PTs[:], in_=pT[:, 0:4, :], func=AF.Copy))
                    for hh_ in range(2):
                        for blk in range(2):
                            op("pe", lambda h, hh_=hh_, blk=blk: h.matmul(Ops[:, hh_ * 64:(hh_ + 1) * 64], lhsT=PTs[:, hh_ * 2 + blk, :],
                                                                         rhs=vx[:, blk, hh_ * 64:(hh_ + 1) * 64], start=(blk == 0), stop=(blk == 1)))
                    for hh_ in range(2):
                        hd = i + 4 * hh_
                        op("dve", lambda h, hh_=hh_, hd=hd: h.tensor_scalar(out=attn[:, hd * 64:(hd + 1) * 64], in0=Ops[:, hh_ * 64:(hh_ + 1) * 64],
                                                                           scalar1=rinv[:, hh_:hh_ + 1], scalar2=None, op0=ALU.mult))
                    yield
                op("pool", lambda h: h.tensor_copy(out=kTx[:, :, 0:128], in_=kTx[:, :, 128:256]))
                op("pool", lambda h: h.tensor_copy(out=vx[:, 0, :], in_=vx[:, 1, :]))
            else:
                W17 = 17 * 128
                for hd in range(8):
                    i, hh_ = hd % 4, hd // 4
                    for cb in range(5):
                        c0 = cb * 512
                        cw = min(512, W17 - c0)
                        bank = mmA if cb % 2 == 0 else mmB
                        op("pe", lambda h, i=i, hh_=hh_, c0=c0, cw=cw, bank=bank: h.matmul(bank[:, 0:cw], lhsT=qT[:, i, :], rhs=kTs[:, hh_, c0:c0 + cw], start=True, stop=True))
                        op("dve", lambda h, c0=c0, cw=cw, bank=bank: h.tensor_tensor(out=Ssx[:, c0:c0 + cw], in0=bank[:, 0:cw], in1=smask[:, c0:c0 + cw], op=ALU.add))
                    op("dve", lambda h, hd=hd: h.tensor_scalar(out=Ssx[:, W17:W17 + 1], in0=sink8[:, hd:hd + 1], scalar1=8.0, scalar2=None, op0=ALU.mult))
                    op("dve", lambda h: h.tensor_reduce(out=mx[:, 0:1], in_=Ssx[:], axis=AX.X, op=ALU.max))
                    op("dve", lambda h: h.tensor_scalar(out=nbias[:, 0:1], in0=mx[:, 0:1], scalar1=-0.125, scalar2=None, op0=ALU.mult))
                    op("act", lambda h: h.activation(out=Psb[:], in_=Ssx[:], func=AF.Exp, scale=0.125, bias=nbias[:, 0:1], accum_out=rs[:, 0:1]))
                    op("dve", lambda h: h.reciprocal(out=rinv[:, 0:1], in_=rs[:, 0:1]))
                    for g8 in range(3):
                        nb_ = 8 if g8 < 2 else 1
                        for b in range(nb_):
                            blk = g8 * 8 + b
                            op("pe", lambda h, b=b, blk=blk: h.transpose(out=pT[:, b, :], in_=Psb[:, blk * 128:(blk + 1) * 128], identity=idb[:]))
                        op("act", lambda h, g8=g8, nb_=nb_: h.activation(out=PTss[:, g8 * 8:g8 * 8 + nb_, :], in_=pT[:, 0:nb_, :], func=AF.Copy))
                    for blk in range(17):
                        op("pe", lambda h, blk=blk, hh_=hh_: h.matmul(Ops[:, 0:64], lhsT=PTss[:, blk, :], rhs=vs[:, blk, hh_ * 64:(hh_ + 1) * 64],
                                                                     start=(blk == 0), stop=(blk == 16)))
                    op("dve", lambda h, hd=hd: h.tensor_scalar(out=attn[:, hd * 64:(hd + 1) * 64], in0=Ops[:, 0:64], scalar1=rinv[:, 0:1], scalar2=None, op0=ALU.mult))
            rms(attn[0:n, :], "attn", 512, 1, 1.0 / 512, n)
            op("dve", lambda h: h.tensor_scalar(out=anb[0:n, :], in0=attn[0:n, :], scalar1=rstd[0:n, 1:2], scalar2=None, op0=ALU.mult), ["attn", "rstd1"], ["anb"])
            for c in range(4):
                op("pe", lambda h, c=c: h.transpose(out=pT[:, c, 0:n], in_=anb[0:n, c * 128:(c + 1) * 128], identity=idb[0:n, 0:n]), ["anb", "idb"], ["pT"])
            op("dve", lambda h: h.tensor_tensor(out=mixT2[par][:, 0:4, :], in0=pT[:, 0:4, :], in1=pcol[:, GATT:GATT + 4].unsqueeze(2).broadcast_to([128, 4, 128]), op=ALU.mult))

            yield

        def partB(ti, tl):
            sample = (ti == NPT)
            xt = xb[:, tl, :]
            n = 128
            ns = NS if sample else 128
            par = ti % 2
            uT = uT2[par]
            mixT = mixT2[par]
            TT = lambda o, x, y, f_: op("dve", lambda h: h.tensor_tensor(out=o, in0=x, in1=y, op=f_))

            def ssm_mm(c4):
                q0 = 4 * c4
                for ri, (WBx, bank) in enumerate(((WBre, mmA), (WBim, mmC))):
                    for j in range(4):
                        op("pe", lambda h: h.matmul(bank[:, j * 128:(j + 1) * 128], lhsT=WBx[:, q0 + j, :], rhs=uT[:, c4, :], start=True, stop=True))

            def ssm_batch(c4):
                q0 = 4 * c4
                T = [pp4[:, kk] for kk in range(4)]
                Bre = mmA[:, :].rearrange("p (j k) -> p j k", j=4); Bim = mmC[:, :].rearrange("p (j k) -> p j k", j=4)
                if sample:
                    op("act", lambda h: h.activation(out=vv4[:, 0], in_=Bre, func=AF.Copy))
                    op("act", lambda h: h.activation(out=vv4[:, 1], in_=Bim, func=AF.Copy))
                    for ri in range(2):
                        v4 = vv4[:, ri, :, 0:NS].rearrange("p j (s t) -> p j s t", t=4)[:, :, :, 0]
                        TT(v4, v4, ah[:, ri, q0:q0 + 4, :], ALU.add)
                    csq = cs[:, q0:q0 + 4, 1:5].unsqueeze(2).broadcast_to([128, 4, NSEQ, 4])
                    snq = sn[:, q0:q0 + 4, 1:5].unsqueeze(2).broadcast_to([128, 4, NSEQ, 4])
                    v3 = lambda ap: ap[:, :, 0:NS].rearrange("p j (s t) -> p j s t", t=4)
                    Bre, Bim = vv4[:, 0], vv4[:, 1]
                else:
                    csq = cs[:, q0:q0 + 4, 1:129]; snq = sn[:, q0:q0 + 4, 1:129]
                    v3 = lambda ap: ap
                Tv = [v3(t_) for t_ in T]
                RR = [v3(rr4[:, kk]) for kk in range(2)]
                TT(Tv[0], v3(Bre), csq, ALU.mult); TT(Tv[1], v3(Bim), snq, ALU.mult)
                TT(Tv[2], v3(Bim), csq, ALU.mult); TT(Tv[3], v3(Bre), snq, ALU.mult)
                if c4 + 1 < 4:
                    ssm_mm(c4 + 1)
                TT(RR[0], Tv[0], Tv[1], ALU.add); TT(RR[1], Tv[2], Tv[3], ALU.subtract)
                for j in range(4):
                    q = q0 + j
                    if sample:
                        op("dve", lambda h: h.tensor_scalar(out=dtmp[:], in0=mask64[:], scalar1=rho[:, q:q + 1], scalar2=None, op0=ALU.mult))
                    for ri in range(2):
                        if sample:
                            op("dve", lambda h: h.tensor_tensor_scan(out=vv4[:, ri, j, 0:NS], data0=dtmp[:], data1=rr4[:, ri, j, 0:NS], initial=0.0, op0=ALU.mult, op1=ALU.add))
                        else:
                            op("dve", lambda h: h.tensor_tensor_scan(out=vv4[:, ri, j, :], data0=rho[:, q:q + 1].broadcast_to([128, 128]), data1=rr4[:, ri, j, :],
                                                                   initial=car[:, ri, q:q + 1], op0=ALU.mult, op1=ALU.add))
                VV = [v3(vv4[:, kk]) for kk in range(2)]
                TT(Tv[0], VV[0], csq, ALU.mult); TT(Tv[1], VV[1], snq, ALU.mult)
                TT(Tv[2], VV[0], snq, ALU.mult); TT(Tv[3], VV[1], csq, ALU.mult)
                TT(v3(hb[:, :, 0, :]), Tv[0], Tv[1], ALU.subtract); TT(v3(hb[:, :, 1, :]), Tv[2], Tv[3], ALU.add)
                if sample:
                    TT(hfin[:, 0, q0:q0 + 4, 0:NSEQ], Tv[0][:, :, :, 3], Tv[1][:, :, :, 3], ALU.subtract)
                    TT(hfin[:, 1, q0:q0 + 4, 0:NSEQ], Tv[2][:, :, :, 3], Tv[3][:, :, :, 3], ALU.add)
                else:
                    TT(car[:, 0, q0:q0 + 4], Tv[0][:, :, 127], Tv[1][:, :, 127], ALU.subtract)
                    TT(car[:, 1, q0:q0 + 4], Tv[2][:, :, 127], Tv[3][:, :, 127], ALU.add)

            def ssm_y(c4):
                for qq in range(4):
                    q = c4 * 4 + qq
                    op("pe", lambda h: h.matmul(mmB[:, 0:n], lhsT=WCre[:, q, :], rhs=hb[:, qq, 0, 0:n], start=(qq == 0), stop=False))
                    op("pe", lambda h: h.matmul(mmB[:, 0:n], lhsT=WCimn[:, q, :], rhs=hb[:, qq, 1, 0:n], start=False, stop=(qq == 3)))
                op("dve", lambda h: h.scalar_tensor_tensor(out=yv[:, 0:n], in0=uT[:, c4, 0:n], scalar=col(DSK + c4), in1=mmB[:, 0:n], op0=ALU.mult, op1=ALU.add))
                op("act", lambda h: h.activation(out=gl32[:, c4, 0:n], in_=yv[:, 0:n], func=AF.Gelu))
                op("pool", lambda h: h.tensor_copy(out=glb[:, c4, 0:n], in_=gl32[:, c4, 0:n]))

            ssm_mm(0)
            yield
            for c4_ in range(4):
                ssm_batch(c4_)
                yield
                ssm_y(c4_)
                yield
            if ti == NPT - 1:
                op("act", lambda h: h.activation(out=hfin[:, :, :, NSEQ], in_=car[:], func=AF.Copy), ["car"], ["hfin"])
            for oc in range(4):
                for c4 in range(4):
                    op("pe", lambda h, oc=oc, c4=c4: h.matmul(mmC[:, 0:n], lhsT=wgl[0][:, c4, oc * 128:(oc + 1) * 128], rhs=glb[:, c4, 0:n], start=(c4 == 0), stop=(c4 == 3)),
                       ["glb", "wgl"], ["mmC"])
                op("act", lambda h, oc=oc: h.activation(out=sg[:, 0:n], in_=mmC[:, 0:n], func=AF.Sigmoid, bias=col(BGLU + oc)), ["mmC", "pcol"], ["sg"])
                op("dve", lambda h, oc=oc: h.tensor_tensor(out=gl32[:, oc, 0:n], in0=gl32[:, oc, 0:n], in1=sg[:, 0:n], op=ALU.mult), ["gl32", "sg", "glb"], ["gl32"])
                op("act", lambda h, oc=oc: h.activation(out=sqb[:, oc, 0:n], in_=gl32[:, oc, 0:n], func=AF.Square), ["gl32"], ["sqb"])
            for oc in range(4):
                op("pe", lambda h, oc=oc: h.matmul(mmC[:, 0:n], lhsT=onesb[:], rhs=sqb[:, oc, 0:n], start=(oc == 0), stop=(oc == 3)), ["sqb", "onesb"], ["mmC"])
            op("act", lambda h: h.activation(out=rsb[:, 0:n], in_=mmC[:, 0:n], func=AF.Ln, scale=1.0 / 512, bias=epsb[:, 0:1]))
            op("act", lambda h: h.activation(out=rsb[:, 0:n], in_=rsb[:, 0:n], func=AF.Exp, scale=-0.5))
            for oc in range(4):
                op("dve", lambda h, oc=oc: h.scalar_tensor_tensor(out=mixT[:, 4 + oc, 0:n], in0=gl32[:, oc, 0:n], scalar=col(GSSM + oc), in1=rsb[:, 0:n], op0=ALU.mult, op1=ALU.mult),
                   ["gl32", "rsb", "pcol"], ["mixT"])

            for hf in range(2):
                acc, ak = (acc0, "acc0") if hf == 0 else (acc1, "acc1")
                for c in range(8):
                    op("pe", lambda h, hf=hf, c=c, acc=acc: h.matmul(acc[0:n, :], lhsT=mixT[:, c, 0:n], rhs=wo[0][:, c, hf * 512:(hf + 1) * 512], start=(c == 0), stop=(c == 7)),
                       ["mixT", "wo"], [ak])
                op("dve", lambda h, hf=hf, acc=acc: h.tensor_tensor(out=xt[0:n, hf * 512:(hf + 1) * 512], in0=xt[0:n, hf * 512:(hf + 1) * 512], in1=acc[0:n, :], op=ALU.add),
                   ["xt", ak], ["xt"])
            rms(xt[0:n, :], "xt", D, 2, 1.0 / D, n)
            op("dve", lambda h: h.tensor_scalar(out=xn[0:n, :], in0=xt[0:n, :], scalar1=rstd[0:n, 2:3], scalar2=None, op0=ALU.mult), ["xt", "rstd2"], ["xn"])
            for c in range(8):
                op("pe", lambda h, c=c: h.transpose(out=pT[:, c, 0:n], in_=xn[0:n, c * 128:(c + 1) * 128], identity=idb[0:n, 0:n]), ["xn", "idb"], ["pT"])
            op("dve", lambda h: h.tensor_tensor(out=xn2T[:, :, tl * 128:(tl + 1) * 128], in0=pT[:], in1=pcol[:, GFFN:GFFN + 8].unsqueeze(2).broadcast_to([128, 8, 128]), op=ALU.mult))

        def ffn(tiles_):
            sample = (tiles_[0] == NPT)
            ntl = len(tiles_)
            nb = ntl * 128
            hTv = hTsm if (sample and ntl == 1) else hTall
            p0 = 128 if sample else 0
            npc = nb - p0
            for ch in range(NFC):
                b3 = ch % 2
                w3 = ch % 3
                P.dma("sp", lambda h: h.dma_start(out=wg[w3][:], in_=wgb_d[ch]))
                P.dma("sp", lambda h: h.dma_start(out=wu[w3][:], in_=wub_d[ch]))
                gps, ups = (mmA, mmB) if b3 == 0 else (mmC, pT32)
                for c in range(8):
                    op("pe", lambda h, c=c, b3=b3, gps=gps: h.matmul(gps[:, 0:nb], lhsT=wg[w3][:, c, :], rhs=xn2T[:, c, 0:nb], start=(c == 0), stop=(c == 7)))
                for c in range(8):
                    op("pe", lambda h, c=c, b3=b3, ups=ups: h.matmul(ups[:, 0:nb], lhsT=wu[w3][:, c, :], rhs=xn2T[:, c, 0:nb], start=(c == 0), stop=(c == 7)))
                w0, w1, w2, bb = col(CVW + ch * 3), col(CVW + ch * 3 + 1), col(CVW + ch * 3 + 2), col(CVB + ch)
                cvb, slb, gxb = cv[:, 0, :], sl[:, 0, :], gx[:, b3, :]
                def conv3(g0, g1, g2, cvv):
                    op("dve", lambda h: h.tensor_scalar(out=cvv, in0=g2, scalar1=w2, scalar2=bb, op0=ALU.mult, op1=ALU.add))
                    op("dve", lambda h: h.scalar_tensor_tensor(out=cvv, in0=g1, scalar=w1, in1=cvv, op0=ALU.mult, op1=ALU.add))
                    op("dve", lambda h: h.scalar_tensor_tensor(out=cvv, in0=g0, scalar=w0, in1=cvv, op0=ALU.mult, op1=ALU.add))

                if sample:
                    op("pool", lambda h: h.tensor_copy(out=gxs[:, :, 0:2], in_=cvst[:, ch, :, :]))
                    op("act", lambda h: h.activation(out=gxs[:, :, 2:6], in_=gps[:, 0:NS].rearrange("p (s t) -> p s t", t=4), func=AF.Copy))
                    op("pool", lambda h: h.tensor_copy(out=sconv[:, ch, :, :], in_=gxs[:, :, 4:6]))
                    conv3(gxs[:, :, 0:4], gxs[:, :, 1:5], gxs[:, :, 2:6], cvb[:, 0:NS].rearrange("p (s t) -> p s t", t=4))
                if npc:
                    op("pool", lambda h: h.tensor_copy(out=gxb[:, 0:2], in_=gcar[:, ch, :]))
                    op("act", lambda h: h.activation(out=gxb[:, 2:2 + npc], in_=gps[:, p0:nb], func=AF.Copy))
                    op("pool", lambda h: h.tensor_copy(out=gcar[:, ch, :], in_=gxb[:, npc:npc + 2]))
                    conv3(gxb[:, 0:npc], gxb[:, 1:1 + npc], gxb[:, 2:2 + npc], cvb[:, p0:nb])
                op("act", lambda h, cvb=cvb, slb=slb: h.activation(out=slb[:, 0:nb], in_=cvb[:, 0:nb], func=AF.Silu))
                op("dve", lambda h, ch=ch, slb=slb: h.tensor_tensor(out=hTv[:, ch, 0:nb], in0=slb[:, 0:nb], in1=ups[:, 0:nb], op=ALU.mult))
            accs = [acc0, acc1, Sps, Ops]
            for hf in range(2):
                for g in range(NFC // 2):
                    wdb = wd[g % 3]
                    P.dma("sp", lambda h: h.dma_start(out=wdb, in_=wdb_d[hf, g]))
                    for j in range(2):
                        ch = 2 * g + j
                        for tl in range(ntl):
                            op("pe", lambda h: h.matmul(accs[tl][:, :], lhsT=hTv[:, ch, tl * 128:(tl + 1) * 128], rhs=wdb[:, j, :],
                                                        start=(ch == 0), stop=(ch == NFC - 1)))
                for tl in range(ntl):
                    op("dve", lambda h, tl=tl, hf=hf: h.tensor_tensor(out=xb[:, tl, hf * 512:(hf + 1) * 512], in0=xb[:, tl, hf * 512:(hf + 1) * 512], in1=accs[tl][:, :], op=ALU.add))
            for tl, ti in enumerate(tiles_):
                xt = xb[:, tl, :]
                rms(xt, "xt", D, 3, 1.0 / D, 128)
                op("dve", lambda h, xt=xt: h.scalar_tensor_tensor(out=xt, in0=xt, scalar=rstd[:, 3:4], in1=gfb[:], op0=ALU.mult, op1=ALU.mult))
                P.dma("sp", lambda h, xt=xt, ti=ti: h.dma_start(out=y_d[ti * 128:(ti + 1) * 128, :], in_=xt))

        def est_dur(kind, eng, fn):
            rec = _Recorder()
            fn(rec)
            name, args, kwargs = rec.call
            o = kwargs.get("out", args[0] if args else None)
            nfree = 1
            if o is not None and hasattr(o, "shape"):
                for d_ in list(o.shape)[1:]:
                    nfree *= int(d_)
            if kind == "dma":
                return 100.0, 2500.0
            if eng == "dve":
                d = ((2 * nfree if name == "tensor_tensor_scan" else nfree) + 151) / 0.96
            elif eng == "act":
                d = (nfree + 230) / 1.2 + (90 if kwargs.get("accum_out") is not None else 0)
            elif eng == "pool":
                d = (2 * nfree + 250) / 1.2
            else:
                d = max(64, nfree) / 1.9 + 25
            return d, d

        def merge_threads(gens):
            gens = [g for g in gens if g is not None]
            bufs = [[] for _ in gens]
            alive = [True] * len(gens)
            tchain = [sched_now[0]] * len(gens)
            while True:
                for i, g in enumerate(gens):
                    while alive[i] and not bufs[i]:
                        P.capture = bufs[i]
                        if next(g, "done") == "done":
                            alive[i] = False
                        P.capture = None
                cands = [i for i in range(len(gens)) if bufs[i]]
                if not cands:
                    break
                best, best_t = None, None
                for i in cands:
                    kind, eng, fn = bufs[i][0]
                    t = max(eng_free.get(eng, 0.0), tchain[i] + 250.0)
                    if best is None or t < best_t:
                        best, best_t = i, t
                kind, eng, fn = bufs[best].pop(0)
                busy, lat = est_dur(kind, eng, fn)
                eng_free[eng] = best_t + busy
                tchain[best] = best_t + lat
                sched_now[0] = max(sched_now[0], best_t)
                (P.dma if kind == "dma" else P.op)(eng, fn)

        eng_free = {}
        ffn_cast_issued = [False]
        sched_now = [0.0]
        if tiles is None:
            blocks = [[0, 1, 2, 3], [4, 5, 6, 7], [8, 9, 10, 11], [12, 13, 14, 15], [NPT, 16]]
        else:
            blocks = tiles
        for blk_ in blocks:
            def load_x(tl, ti):
                P.dma("sp", lambda h: h.dma_start(out=xb[:, tl, :], in_=xin[ti * 128:(ti + 1) * 128, :]))
            lazy = NPT in blk_
            for tl, ti in enumerate(blk_):
                if tl == 0 or not lazy:
                    load_x(tl, ti)
            for _ in partA(blk_[0], 0):
                pass
            if not ffn_cast_issued[0]:
                ffn_cast_issued[0] = True
                for a_ in range(0, NFC, 11):
                    P.dma("pool", lambda h: h.dma_start(out=wgb_d[a_:a_ + 11], in_=w_gate[a_:a_ + 11]))
                    P.dma("pool", lambda h: h.dma_start(out=wub_d[a_:a_ + 11], in_=w_up[a_:a_ + 11]))
                for hf_ in range(2):
                    P.dma("pool", lambda h: h.dma_start(out=wdb_d[hf_], in_=w_down[hf_]))
            for tl, ti in enumerate(blk_):
                gb = partB(ti, tl)
                ga = partA(blk_[tl + 1], tl + 1) if tl + 1 < len(blk_) else None
                if ga is not None and lazy:
                    load_x(tl + 1, blk_[tl + 1])
                merge_threads([gb, ga])
            ffn(blk_)

        fv = pp4[:].rearrange("p a j k -> p (a j k)")
        v1 = fv[:, 0:272].rearrange("p (q s) -> p q s", q=16); v2 = fv[:, 272:544].rearrange("p (q s) -> p q s", q=16); v3_ = fv[:, 544:816].rearrange("p (q s) -> p q s", q=16)
        fre_c = fre[:].unsqueeze(2).broadcast_to([128, 16, NSEQ + 1]); fim_c = fim[:].unsqueeze(2).broadcast_to([128, 16, NSEQ + 1])
        TT(v1, hfin[:, 0], fre_c, ALU.mult); TT(v2, hfin[:, 1], fim_c, ALU.mult); TT(v3_, v1, v2, ALU.subtract)
        TT(v1, hfin[:, 0], fim_c, ALU.mult); TT(v2, hfin[:, 1], fre_c, ALU.mult); TT(hfin[:, 1], v1, v2, ALU.add)
        op("dve", lambda h: h.tensor_copy(out=hfin[:, 0], in_=v3_))
        P.dma("sp", lambda h: h.dma_start(out=hfin_d, in_=hfin[:]), reads=["hfin"], writes=["hfin_d"])
        P.dma("sp", lambda h: h.dma_start(out=pconv_d, in_=gcar[:]), reads=["gcar"], writes=["pconv_d"])
        P.dma("sp", lambda h: h.dma_start(out=sconv_d, in_=sconv[:]), reads=["sconv"], writes=["sconv_d"])
        P.limit = None
        P.barrier()

        with nc.Block() as block:
            @block.tensor
            def _(h):
                P.run("pe", h, sems)

            @block.scalar
            def _(h):
                P.run("act", h, sems)

            @block.vector
            def _(h):
                P.run("dve", h, sems)

            @block.gpsimd
            def _(h):
                P.run("pool", h, sems)

            @block.sync
            def _(h):
                P.run("sp", h, sems)
    return nc


def _consts():
    ident = np.eye(128, dtype=np.float32)
    i = np.arange(128)[:, None]
    c = np.arange(256)[None, :]
    full = np.where(((c < 128) & (c > i)) | ((c >= 128) & (c - 128 <= i)), 0.0, MASKV)
    m1 = np.where(((c < 128) & (c > i) & (c >= NPAD)) | ((c >= 128) & (c - 128 <= i)), 0.0, MASKV)
    m0 = np.where((c >= 128) & (c - 128 <= i) & (c - 128 >= NPAD), 0.0, MASKV)
    masks = np.stack([m0, m1, full], axis=1).astype(np.float32)
    sm = np.full((128, 17 * 128), MASKV, np.float32)
    for s in range(NSEQ):
        for t in range(4):
            r = s * 4 + t
            sm[r, s * 128 + t + 1:(s + 1) * 128] = 0.0
            sm[r, 2048 + s * 4:2048 + s * 4 + t + 1] = 0.0
    return ident, masks, sm


def prep_inputs(x_prompt, x_sample, cache_k_win, cache_v_win, state_ssm_re, state_ssm_im, state_conv,
           meta_tokens, g_mix, w_in, sinks, lam_re, lam_im, log_dt, b_re, b_im, c_re, c_im, d_skip,
           w_glu, b_glu, g_attn_out, g_ssm_out, w_o, g_ffn, w_gate, w_up, conv_w, conv_b, w_down,
           g_final):
    f32 = np.float32
    A = lambda a: np.ascontiguousarray(np.asarray(a, dtype=f32))
    x_prompt, x_sample = A(x_prompt), A(x_sample)
    ident, masks, smask = _consts()
    w_in0 = A(w_in)[0]
    perm = []
    for i in range(4):
        perm += list(range(i * 64, (i + 1) * 64)) + list(range((4 + i) * 64, (5 + i) * 64))
    w_in_p = np.ascontiguousarray(np.concatenate([w_in0[:, perm], w_in0[:, 512:]], axis=1))
    pcol = np.zeros((128, 128), f32)
    pcol[:, 0:8] = A(g_mix)[0].reshape(8, 128).T
    pcol[:, 8:16] = A(g_ffn)[0].reshape(8, 128).T
    pcol[:, 16:20] = A(g_attn_out)[0].reshape(4, 128).T
    pcol[:, 20:24] = A(g_ssm_out)[0].reshape(4, 128).T
    pcol[:, 24:28] = A(b_glu)[0].reshape(4, 128).T
    pcol[:, 28:32] = A(d_skip)[0].reshape(4, 128).T
    cw = A(conv_w)[0].reshape(3, NFC, 128)
    pcol[:, 32:98] = cw.transpose(2, 1, 0).reshape(128, 66)
    pcol[:, 98:120] = A(conv_b)[0].reshape(NFC, 128).T
    lr, li, ld = A(lam_re)[0], A(lam_im)[0], A(log_dt)[0]
    ldx = np.repeat(ld[:, None], 64, axis=1)

    def pl(a):
        return a.reshape(16, 2, 64).transpose(1, 2, 0).reshape(128, 16)

    lam = np.ascontiguousarray(np.stack([pl(lr), pl(li), pl(ldx)], axis=1))
    lamb = np.ascontiguousarray(np.stack([lr.reshape(-1), li.reshape(-1), ldx.reshape(-1)], axis=0))
    bre, bim, cre, cim = A(b_re)[0], A(b_im)[0], A(c_re)[0], A(c_im)[0]
    bblk = np.zeros((128, 2, 16, 128), f32)
    cblk = np.zeros((128, 2, 16, 128), f32)
    for q in range(16):
        for j2 in range(2):
            g = 2 * q + j2
            g8 = g % 8
            rows = slice(g8 * 16, g8 * 16 + 16)
            cols = slice(j2 * 64, j2 * 64 + 64)
            bblk[rows, 0, q, cols] = bre[g].T
            bblk[rows, 1, q, cols] = bim[g].T
            cblk[cols, 0, q, rows] = cre[g].T
            cblk[cols, 1, q, rows] = cim[g].T
    sre, sim_ = A(state_ssm_re)[0], A(state_ssm_im)[0]
    ck, cvv = A(cache_k_win)[0].reshape(128, 128, 128), A(cache_v_win)[0].reshape(128, 128, 128)
    sc = A(state_conv)[0]
    meta = A(meta_tokens)
    wg_l = np.ascontiguousarray(A(w_gate)[0].reshape(8, 128, NFC, 128).transpose(2, 1, 0, 3))
    wu_l = np.ascontiguousarray(A(w_up)[0].reshape(8, 128, NFC, 128).transpose(2, 1, 0, 3))
    wd_l = np.ascontiguousarray(A(w_down)[0].reshape(NFC // 2, 2, 128, 2, 512).transpose(3, 0, 2, 1, 4))
    in_maps = []
    for c in range(NCORES):
        xin = np.zeros((NPT * 128 + 128, D), f32)
        xin[NPAD:128] = meta
        xin[128:NPT * 128] = x_prompt[c]
        xin[NPT * 128:NPT * 128 + NS] = x_sample[c * NSEQ:(c + 1) * NSEQ].reshape(NS, D)
        sl_ = slice(c * NSEQ, (c + 1) * NSEQ)

        def hl(a):
            return a.reshape(NSEQ, 16, 2, 64).transpose(2, 3, 1, 0).reshape(128, 16, NSEQ)

        h0 = np.ascontiguousarray(np.stack([hl(sre[sl_]), hl(sim_[sl_])], axis=1))
        cvst = np.ascontiguousarray(sc[sl_].reshape(NSEQ, 2, NFC, 128).transpose(3, 2, 0, 1))
        kc, vc = ck[sl_], cvv[sl_]
        kcT = np.ascontiguousarray(kc.transpose(2, 0, 1).reshape(128, NSEQ * 128))
        vcl = np.ascontiguousarray(vc.transpose(1, 0, 2))
        in_maps.append(dict(
            xin=xin, w_in=w_in_p, w_o=A(w_o)[0], w_glu=A(w_glu)[0], w_gate=wg_l, w_up=wu_l,
            w_down=wd_l, ident=ident, masks=masks, smask=smask, pcol=pcol, sinks=A(sinks)[0],
            gfin=A(g_final), lam=lam, lamb=lamb, bblk=bblk, cblk=cblk, h0=h0, cvst=cvst, kcT=kcT, vc=vcl,
            kcache=np.ascontiguousarray(kc), vcache=np.ascontiguousarray(vc)))
    return in_maps


def assemble(R):
    f32 = np.float32
    y_prompt = np.stack([R[c]["y"][128:NPT * 128] for c in range(NCORES)])
    y_sample = np.concatenate([R[c]["y"][NPT * 128:NPT * 128 + NS].reshape(NSEQ, 4, D) for c in range(NCORES)])
    p_k = np.stack([R[c]["kvp"][:, 0:128].reshape(128, 2, 64) for c in range(NCORES)])[None]
    p_v = np.stack([R[c]["kvp"][:, 128:256].reshape(128, 2, 64) for c in range(NCORES)])[None]
    s_k = np.concatenate([R[c]["kvs_k"].reshape(NSEQ, 128, 2, 64) for c in range(NCORES)])[None]
    s_v = np.concatenate([R[c]["kvs_v"].reshape(NSEQ, 128, 2, 64) for c in range(NCORES)])[None]

    def unh(a):
        nn = a.shape[-1]
        return a.reshape(2, 64, 16, nn).transpose(3, 2, 0, 1).reshape(nn, 32, 64)

    p_re = np.stack([unh(R[c]["hfin"][:, 0, :, NSEQ:])[0] for c in range(NCORES)])[None]
    p_im = np.stack([unh(R[c]["hfin"][:, 1, :, NSEQ:])[0] for c in range(NCORES)])[None]
    s_re = np.concatenate([unh(R[c]["hfin"][:, 0, :, :NSEQ]) for c in range(NCORES)])[None]
    s_im = np.concatenate([unh(R[c]["hfin"][:, 1, :, :NSEQ]) for c in range(NCORES)])[None]
    p_conv = np.stack([R[c]["pconv"].transpose(2, 1, 0).reshape(2, FF) for c in range(NCORES)])[None]
    s_conv = np.concatenate([R[c]["sconv"].transpose(2, 3, 1, 0).reshape(NSEQ, 2, FF) for c in range(NCORES)])[None]
    out = (y_prompt, y_sample, p_k, p_v, p_re, p_im, p_conv, s_k, s_v, s_re, s_im, s_conv)
    return tuple(np.ascontiguousarray(o, dtype=f32) for o in out)


def kernel(**inputs):
    in_maps = prep_inputs(**inputs)
    nc = build_nc()
    res = run_bass_kernel_spmd(nc, in_maps, core_ids=list(range(NCORES)))
    return assemble(res.results)
```

```python
import numpy as np
from contextlib import ExitStack
import concourse.bass as bass
import concourse.mybir as mybir
from concourse.bass_utils import run_bass_kernel_spmd

F32 = mybir.dt.float32
BF16 = mybir.dt.bfloat16
ALU = mybir.AluOpType
AF = mybir.ActivationFunctionType
AX = mybir.AxisListType

ENGS = ("pe", "act", "dve", "pool", "sp")
NDMASEM = 12
NCORES = 8
D = 1024
NPT = 17
NPAD = 112
NS = 64
NSEQ = 16
FF = 2816
NFC = 22
EPS = 1e-5
PI = float(np.pi)
MASKV = -30000.0


class _Recorder:
    def __init__(self):
        self.call = None

    def __getattr__(self, name):
        def f(*args, **kwargs):
            self.call = (name, args, kwargs)
            return self
        return f


class Prog:
    def __init__(self):
        self.streams = {e: [] for e in ENGS}
        self.count = {e: 0 for e in ENGS}
        self.waited = {e: {} for e in ENGS}
        self.regs = {}
        self.nrec = 0
        self.limit = None
        self.capture = None
        self.dma_rr = {"sp": 0, "pool": 0}
        self.dma_cnt = {}

    @staticmethod
    def _region(ap):
        dims = [(int(st), int(sz)) for st, sz in ap.ap]
        off = int(ap.offset)
        esz = mybir.dt.size(ap.dtype)
        space = str(ap.space)
        if space == "DRAM":
            ext = sum((sz - 1) * abs(st) for st, sz in dims)
            return (0, 1, off, off + ext)
        pst, npart = dims[0]
        pst = max(pst, 1)
        p0, f0 = off // pst, off % pst
        ext = sum((sz - 1) * abs(st) for st, sz in dims[1:])
        f0, ext = f0 * esz, ext * esz + esz - 1
        if space == "PSUM":
            return (0, 128, 0, 1 << 30)
        return (p0, p0 + npart, f0, f0 + ext)

    @staticmethod
    def _is_ap(v):
        return hasattr(v, "tensor") and hasattr(v, "ap") and hasattr(v, "offset")

    def _record(self, fn):
        rec = _Recorder()
        fn(rec)
        name, args, kwargs = rec.call
        acc = []
        for i, a in enumerate(args):
            if self._is_ap(a):
                acc.append((a, i == 0))
        for k, v in kwargs.items():
            if self._is_ap(v):
                acc.append((v, k in ("out", "accum_out")))
        out = []
        for ap, w in acc:
            if str(ap.space) == "PSUM":
                w = True
            out.append((ap.tensor.name, self._region(ap), w))
        self._last_call = rec.call
        return out

    def _deps(self, eng, acc):
        deps = {}
        for name, R, w in acc:
            for (R2, w2), evs in self.regs.get(name, {}).items():
                if not (w or w2):
                    continue
                if R[0] < R2[1] and R2[0] < R[1] and R[2] <= R2[3] and R2[2] <= R[3]:
                    for s_, v in evs.items():
                        if s_ == eng and eng == "pe":
                            continue
                        if deps.get(s_, 0) < v:
                            deps[s_] = v
        out = []
        wd = self.waited[eng]
        for s_, v in deps.items():
            if wd.get(s_, 0) < v:
                wd[s_] = v
                out.append((s_, v))
        return out

    def _commit(self, ev, acc):
        s_, v = ev
        for name, R, w in acc:
            d = self.regs.setdefault(name, {})
            if w:
                for key in [k for k in d if k[0][0] >= R[0] and k[0][1] <= R[1] and k[0][2] >= R[2] and k[0][3] <= R[3]]:
                    del d[key]
            e = d.setdefault((R, w), {})
            if e.get(s_, 0) < v:
                e[s_] = v

    @staticmethod
    def _freeze(fn):
        rec = _Recorder()
        fn(rec)
        return lambda h, c=rec.call: getattr(h, c[0])(*c[1], **c[2])

    def op(self, eng, fn, reads=(), writes=()):
        if self.capture is not None:
            self.capture.append(("op", eng, self._freeze(fn)))
            return
        self.nrec += 1
        if self.limit is not None and self.nrec > self.limit:
            return
        acc = self._record(fn)
        deps = self._deps(eng, acc)
        self.count[eng] += 1
        ev = (eng, self.count[eng])
        self.streams[eng].append((deps, self._last_call, (eng, 1)))
        self._commit(ev, acc)

    def dma(self, eng, fn, reads=(), writes=()):
        if self.capture is not None:
            self.capture.append(("dma", eng, self._freeze(fn)))
            return
        self.nrec += 1
        if self.limit is not None and self.nrec > self.limit:
            return
        acc = self._record(fn)
        k = self.dma_rr[eng]
        self.dma_rr[eng] = (k + 1) % NDMASEM
        sname = "dma%s%d" % (eng, k)
        deps = self._deps(eng, acc)
        prev = self.dma_cnt.get(sname, 0) * 16
        if prev and self.waited[eng].get(sname, 0) < prev:
            self.waited[eng][sname] = prev
            deps.append((sname, prev))
        self.dma_cnt[sname] = self.dma_cnt.get(sname, 0) + 1
        ev = (sname, self.dma_cnt[sname] * 16)
        self.streams[eng].append((deps, self._last_call, (sname, 16)))
        self._commit(ev, acc)

    def barrier(self):
        evs = [(e, self.count[e]) for e in ENGS if self.count[e]]
        evs += [(k, c * 16) for k, c in self.dma_cnt.items()]
        for e in ENGS:
            deps = []
            for s, v in evs:
                if s == e:
                    continue
                if self.waited[e].get(s, 0) < v:
                    self.waited[e][s] = v
                    deps.append((s, v))
            if deps:
                self.streams[e].append((deps, None, None))

    def run(self, eng, h, sems):
        for deps, fn, inc in self.streams[eng]:
            for s, v in deps:
                h.wait_ge(sems[s], v)
            if fn is not None:
                name, args, kwargs = fn
                getattr(h, name)(*args, **kwargs).then_inc(sems[inc[0]], inc[1])


def build_nc(tiles=None, limit=None):
    nc = bass.Bass("TRN2", target_bir_lowering=False)

    def din(name, shape, dt=F32):
        return nc.dram_tensor(name, list(shape), dt, kind="ExternalInput").ap()

    def dout(name, shape, dt=F32):
        return nc.dram_tensor(name, list(shape), dt, kind="ExternalOutput").ap()

    xin = din("xin", [NPT * 128 + 128, D])
    w_in = din("w_in", [D, 1280])
    w_o = din("w_o", [D, D])
    w_glu = din("w_glu", [512, 512])
    w_gate = din("w_gate", [NFC, 128, 8, 128])
    w_up = din("w_up", [NFC, 128, 8, 128])
    w_down = din("w_down", [2, NFC // 2, 128, 2, 512])
    ident_d = din("ident", [128, 128])
    masks_d = din("masks", [128, 3, 256])
    smask_d = din("smask", [128, 17 * 128])
    pcol_d = din("pcol", [128, 128])
    sinks_d = din("sinks", [8])
    gfin_d = din("gfin", [D])
    lam_d = din("lam", [128, 3, 16])
    lamb_d = din("lamb", [3, 16 * 128])
    bblk_d = din("bblk", [128, 2, 16, 128])
    cblk_d = din("cblk", [128, 2, 16, 128])
    h0_d = din("h0", [128, 2, 16, NSEQ])
    cvst_d = din("cvst", [128, NFC, NSEQ, 2])
    kcT_d = din("kcT", [128, NSEQ * 128])
    vc_d = din("vc", [128, NSEQ, 128])
    kcache_d = din("kcache", [NSEQ, 128, 128])
    vcache_d = din("vcache", [NSEQ, 128, 128])

    wgb_d = nc.dram_tensor("wgb", [NFC, 128, 8, 128], BF16, kind="Internal").ap()
    wub_d = nc.dram_tensor("wub", [NFC, 128, 8, 128], BF16, kind="Internal").ap()
    wdb_d = nc.dram_tensor("wdb", [2, NFC // 2, 128, 2, 512], BF16, kind="Internal").ap()
    y_d = dout("y", [NPT * 128 + 128, D])
    kvp_d = dout("kvp", [128, 256])
    kvs_k = dout("kvs_k", [NSEQ, 128, 128])
    kvs_v = dout("kvs_v", [NSEQ, 128, 128])
    hfin_d = dout("hfin", [128, 2, 16, NSEQ + 1])
    pconv_d = dout("pconv", [128, NFC, 2])
    sconv_d = dout("sconv", [128, NFC, NSEQ, 2])

    P = Prog()
    P.limit = limit
    es = ExitStack()
    with es:
        def sb(name, shape, dt=F32):
            return es.enter_context(nc.sbuf_tensor(name, list(shape), dt))

        def ps(name, shape, dt=F32):
            return es.enter_context(nc.psum_tensor(name, list(shape), dt))

        sems = {e: es.enter_context(nc.semaphore("s_" + e)) for e in ENGS}
        for k in range(NDMASEM):
            for e_ in ("sp", "pool"):
                sems["dma%s%d" % (e_, k)] = es.enter_context(nc.semaphore("s_dma%s%d" % (e_, k)))

        idb = sb("idb", [128, 128], BF16)
        onesb = sb("onesb", [128, 128], BF16)
        masks = sb("masks_s", [128, 3, 256])
        smask = sb("smask_s", [128, 17 * 128], BF16)
        pcol = sb("pcol_s", [128, 128])
        sink8 = sb("sink8", [128, 8])
        gfb = sb("gfb", [128, D])
        epsb = sb("epsb", [128, 1])
        lam = sb("lam_s", [128, 3, 16])
        WBre = sb("WBre", [128, 16, 128], BF16); WBim = sb("WBim", [128, 16, 128], BF16)
        WCre = sb("WCre", [128, 16, 128], BF16); WCimn = sb("WCimn", [128, 16, 128], BF16)
        cs = sb("cs", [128, 16, 129]); sn = sb("sn", [128, 16, 129])
        mask64 = sb("mask64", [128, NS]); dtmp = sb("dtmp", [128, NS])
        are = sb("are", [128, 16]); aim = sb("aim", [128, 16])
        ah = sb("ah", [128, 2, 16, NSEQ])
        car = sb("car", [128, 2, 16])
        hfin = sb("hfin_s", [128, 2, 16, NSEQ + 1])
        gcar = sb("gcar", [128, NFC, 2])
        sconv = sb("sconv_s", [128, NFC, NSEQ, 2])
        kTx = sb("kTx", [128, 2, 256], BF16)
        vx = sb("vx", [128, 2, 128], BF16)
        GMIX, GFFN, GATT, GSSM, BGLU, DSK, CVW, CVB = 0, 8, 16, 20, 24, 28, 32, 98

        def col(c):
            return pcol[:, c:c + 1]

        ssq = sb("ssq", [128, 4]); rstd = sb("rstd", [128, 4])
        xn = sb("xn", [128, D], BF16)
        junk = xn
        xnT = sb("xnT", [128, 8, 128], BF16)
        qT = sb("qT", [128, 4, 128], BF16)
        uT2 = [sb("uT%d" % i, [128, 4, 128], BF16) for i in range(2)]
        Sx = sb("Sx", [128, 1, 2, 257]); mx = sb("mx", [128, 2]); nbias = sb("nbias", [128, 2])
        rs = sb("rs", [128, 2]); rinv = sb("rinv", [128, 2])
        Pb = sb("Pb", [128, 2, 257], BF16); PTs = sb("PTs", [128, 4, 128], BF16)
        mixT2 = [sb("mixT%d" % i, [128, 8, 128], BF16) for i in range(2)]
        mixT = mixT2[0]
        Psb = sb("Psb", [128, 17 * 128 + 1], BF16)
        PTss = sb("PTss", [128, 17, 128], BF16)
        pp4 = sb("pp4", [128, 4, 4, 128]); rr4 = sb("rr4", [128, 2, 4, 128]); vv4 = sb("vv4", [128, 2, 4, 128])
        hb = sb("hb", [128, 4, 2, 128], BF16)
        gl32 = sb("gl32", [128, 4, 128]); glb = sb("glb", [128, 4, 128], BF16)
        xn2T = sb("xn2T", [128, 8, 512], BF16)
        gx = sb("gx", [128, 2, 514]); gxs = sb("gxs", [128, NSEQ, 6])
        cv = sb("cv", [128, 1, 512]); sl = sb("sl", [128, 1, 512])
        kvtok = sl[:, 0, 0:256]
        yv = gx[:, 0, 0:128]; sg = gx[:, 0, 128:256]; rsb = gx[:, 0, 256:384]
        sqb = gx[:, 1, 0:256].bitcast(BF16).rearrange("p (a k) -> p a k", a=4)
        attn = cv[:, 0, :]
        anb = sl[:, 0, 256:512].bitcast(BF16)
        wi = [sb("wi0", [128, 8, 1280], BF16)] * 2
        wo = [sb("wo0", [128, 8, D], BF16)] * 2
        wgl = [sb("wgl0", [128, 4, 512], BF16)] * 2
        wg = [sb("wg%d" % i, [128, 8, 128], BF16) for i in range(2)] + [xnT]
        wu = [sb("wu%d" % i, [128, 8, 128], BF16) for i in range(2)] + [mixT]
        wdA = sb("wdA", [128, 2, 512], BF16)
        wd = [wdA[:], Sx[:].rearrange("p a b c -> p (a b c)").bitcast(BF16)[:, 0:1024].rearrange("p (j n) -> p j n", j=2),
              xn[:].rearrange("p (j n) -> p j n", j=2)]
        big = sb("big", [128, 9728])
        xb = big[:, 0:4096].rearrange("p (t d) -> p t d", t=4)
        hTall = big[:, 4096:9728].bitcast(BF16).rearrange("p (c n) -> p c n", c=NFC)
        hTsm = big[:, 4096:5504].bitcast(BF16).rearrange("p (c n) -> p c n", c=NFC)
        kTs = big[:, 5504:7680].bitcast(BF16).rearrange("p (a k) -> p a k", a=2)
        vs = big[:, 7680:8768].bitcast(BF16).rearrange("p (b k) -> p b k", b=17)
        scr = big
        h0 = big[:, 8192:8704].rearrange("p (a q s) -> p a q s", a=2, q=16)
        cvst = big[:, 3264:3968].rearrange("p (c s j) -> p c s j", c=NFC, s=NSEQ)
        def prep_views(o):
            return (scr[:, o:o + 768].rearrange("p (a b) -> p a b", a=3), scr[:, o + 768:o + 2304].rearrange("p (a b) -> p a b", a=6),
                    scr[:, o + 2304:o + 2816].rearrange("p (a q m) -> p a q m", a=2, q=2), scr[:, o + 2816:o + 3328].rearrange("p (a q m) -> p a q m", a=2, q=2))
        Ssx = big[:, 1024:1024 + 17 * 128 + 1]
        idf = big[:, 8960:9088]

        mmA = ps("mmA", [128, 512]); mmB = ps("mmB", [128, 512]); mmC = ps("mmC", [128, 512])
        pT = ps("pT", [128, 8, 128], BF16)
        pT32 = pT[:].rearrange("p c k -> p (c k)").bitcast(F32)
        Sps = ps("Sps", [128, 512]); Ops = ps("Ops", [128, 512])
        acc0 = ps("acc0", [128, 512]); acc1 = ps("acc1", [128, 512])

        op = P.op

        P.dma("sp", lambda h: h.dma_start(out=idf[:], in_=ident_d), writes=["idf"])
        P.dma("sp", lambda h: h.dma_start(out=masks[:], in_=masks_d), writes=["masks"])
        P.dma("pool", lambda h: h.dma_start(out=smask[:], in_=smask_d), writes=["smask"])
        P.dma("sp", lambda h: h.dma_start(out=pcol[:], in_=pcol_d), writes=["pcol"])
        P.dma("sp", lambda h: h.dma_start(out=sink8[:], in_=sinks_d.partition_broadcast(128)), writes=["sink8"])
        P.dma("sp", lambda h: h.dma_start(out=gfb[:], in_=gfin_d.partition_broadcast(128)), writes=["gfb"])
        P.dma("sp", lambda h: h.dma_start(out=lam[:], in_=lam_d), writes=["lam"])
        P.dma("sp", lambda h: h.dma_start(out=h0, in_=h0_d))
        op("pool", lambda h: h.memset(hb[:], 0.0))
        op("pool", lambda h: h.memset(cv[:], 0.0))
        op("pool", lambda h: h.memset(gx[:], 0.0))
        P.dma("pool", lambda h: h.dma_start(out=wi[0][:], in_=w_in.rearrange("(c p) n -> p c n", p=128)))
        P.dma("pool", lambda h: h.dma_start(out=wgl[0][:], in_=w_glu.rearrange("(c p) n -> p c n", p=128)))
        P.dma("pool", lambda h: h.dma_start(out=wo[0][:], in_=w_o.rearrange("(c p) n -> p c n", p=128)))
        P.dma("sp", lambda h: h.dma_start(out=kvs_k[:, 0:124, :], in_=kcache_d[:, 4:128, :]), writes=["kvs_k_a"])
        P.dma("sp", lambda h: h.dma_start(out=kvs_v[:, 0:124, :], in_=vcache_d[:, 4:128, :]), writes=["kvs_v_a"])

        op("dve", lambda h: h.tensor_copy(out=idb[:], in_=idf[:]), ["idf"], ["idb"])
        op("pool", lambda h: h.memset(onesb[:], 1.0), [], ["onesb"])
        op("pool", lambda h: h.memset(epsb[:], EPS), [], ["epsb"])
        op("pool", lambda h: h.memset(kTx[:], 0.0), [], ["kTx"])
        op("pool", lambda h: h.memset(vx[:], 0.0), [], ["vx"])
        op("pool", lambda h: h.memset(car[:], 0.0), [], ["car"])
        op("pool", lambda h: h.memset(gcar[:], 0.0), [], ["gcar"])
        pass
        op("pool", lambda h: h.memset(hfin[:], 0.0))
        op("pool", lambda h: h.memset(sconv[:], 0.0))
        rho = sb("rho", [128, 16]); fre = sb("fre", [128, 16]); fim = sb("fim", [128, 16])
        dtl, thl, c1, s1, tmpa, kfL = (big[:, 9088 + 16 * i:9104 + 16 * i] for i in range(6))
        kiL = big[:, 9184:9200].bitcast(mybir.dt.int32)
        op("act", lambda h: h.activation(out=dtl[:], in_=lam[:, 2, :], func=AF.Exp), ["lam"], ["dtl"])
        op("dve", lambda h: h.tensor_tensor(out=thl[:], in0=lam[:, 1, :], in1=dtl[:], op=ALU.mult), ["lam", "dtl"], ["thl"])
        op("dve", lambda h: h.tensor_tensor(out=tmpa[:], in0=lam[:, 0, :], in1=dtl[:], op=ALU.mult), ["lam", "dtl"], ["tmpa"])
        op("act", lambda h: h.activation(out=rho[:], in_=tmpa[:], func=AF.Exp), ["tmpa"], ["rho"])

        def sincos(eng_tag, th_ap, s_ap, c_ap, tmp_ap, shape_key, ki_ap, kf_ap):
            K = shape_key

            def reduce_(shift, dst_key):
                op("dve", lambda h: h.tensor_scalar(out=tmp_ap, in0=th_ap, scalar1=shift, scalar2=1.0 / (2 * PI), op0=ALU.add, op1=ALU.mult),
                   [K + "th", K + "s", K + "c"], [K + "tmp"])
                op("dve", lambda h: h.tensor_copy(out=ki_ap, in_=tmp_ap), [K + "tmp"], [K + "ki"])
                op("dve", lambda h: h.tensor_copy(out=kf_ap, in_=ki_ap), [K + "ki"], [K + "kf"])
                op("dve", lambda h: h.tensor_scalar(out=tmp_ap, in0=th_ap, scalar1=shift, scalar2=None, op0=ALU.add), [K + "th", K + "ki"], [K + "tmp"])
                op("dve", lambda h: h.scalar_tensor_tensor(out=tmp_ap, in0=kf_ap, scalar=-2 * PI, in1=tmp_ap, op0=ALU.mult, op1=ALU.add),
                   [K + "kf", K + "tmp"], [K + "tmp"])
                op("dve", lambda h: h.tensor_scalar(out=kf_ap, in0=tmp_ap, scalar1=PI, scalar2=None, op0=ALU.is_gt), [K + "tmp"], [K + "kf"])
                op("dve", lambda h: h.scalar_tensor_tensor(out=tmp_ap, in0=kf_ap, scalar=-2 * PI, in1=tmp_ap, op0=ALU.mult, op1=ALU.add),
                   [K + "kf", K + "tmp"], [K + "tmp"])
                op("dve", lambda h: h.tensor_scalar(out=kf_ap, in0=tmp_ap, scalar1=-PI, scalar2=None, op0=ALU.is_lt), [K + "tmp"], [K + "kf"])
                op("dve", lambda h: h.scalar_tensor_tensor(out=tmp_ap, in0=kf_ap, scalar=2 * PI, in1=tmp_ap, op0=ALU.mult, op1=ALU.add),
                   [K + "kf", K + "tmp"], [K + "tmp"])
                op("dve", lambda h: h.tensor_scalar(out=tmp_ap, in0=tmp_ap, scalar1=-PI, scalar2=PI, op0=ALU.max, op1=ALU.min), [K + "tmp"], [K + "tmp"])

            reduce_(0.0, "s")
            op("act", lambda h: h.activation(out=s_ap, in_=tmp_ap, func=AF.Sin), [K + "tmp"], [K + "s"])
            reduce_(0.5 * PI, "c")
            op("act", lambda h: h.activation(out=c_ap, in_=tmp_ap, func=AF.Sin), [K + "tmp"], [K + "c"])

        sincos("L", thl[:], s1[:], c1[:], tmpa[:], "L", kiL[:], kfL[:])
        op("dve", lambda h: h.tensor_tensor(out=are[:], in0=rho[:], in1=c1[:], op=ALU.mult), ["rho", "Lc"], ["are"])
        op("dve", lambda h: h.tensor_tensor(out=aim[:], in0=rho[:], in1=s1[:], op=ALU.mult), ["rho", "Ls"], ["aim"])
        op("pool", lambda h: h.memset(cs[:, :, 0:1], 1.0), [], ["cs"])
        op("pool", lambda h: h.memset(sn[:, :, 0:1], 0.0), [], ["sn"])
        op("dve", lambda h: h.tensor_copy(out=cs[:, :, 1], in_=c1[:]), ["Lc", "cs"], ["cs"])
        op("dve", lambda h: h.tensor_copy(out=sn[:, :, 1], in_=s1[:]), ["Ls", "sn"], ["sn"])
        tA = pp4[:, 0:2].rearrange("p a j (b c) -> p (a j b) c", c=64); tB = big[:, 6656:7680].rearrange("p (a c) -> p a c", c=64)
        m = 1
        while m < 128:
            cm = cs[:, :, m:m + 1].broadcast_to([128, 16, m]); sm = sn[:, :, m:m + 1].broadcast_to([128, 16, m])
            a_c = cs[:, :, 1:m + 1]; a_s = sn[:, :, 1:m + 1]
            o_c = cs[:, :, m + 1:2 * m + 1]; o_s = sn[:, :, m + 1:2 * m + 1]
            ta = tA[:, :, 0:m]; tb = tB[:, :, 0:m]
            op("dve", lambda h, a_c=a_c, cm=cm, ta=ta: h.tensor_tensor(out=ta, in0=a_c, in1=cm, op=ALU.mult), ["cs", "sn"], ["tA"])
            op("dve", lambda h, a_s=a_s, sm=sm, tb=tb: h.tensor_tensor(out=tb, in0=a_s, in1=sm, op=ALU.mult), ["cs", "sn"], ["tB"])
            op("dve", lambda h, o_c=o_c, ta=ta, tb=tb: h.tensor_tensor(out=o_c, in0=ta, in1=tb, op=ALU.subtract), ["tA", "tB", "sn"], ["cs"])
            op("dve", lambda h, a_c=a_c, sm=sm, ta=ta: h.tensor_tensor(out=ta, in0=a_c, in1=sm, op=ALU.mult), ["cs", "sn"], ["tA"])
            op("dve", lambda h, a_s=a_s, cm=cm, tb=tb: h.tensor_tensor(out=tb, in0=a_s, in1=cm, op=ALU.mult), ["cs", "sn"], ["tB"])
            op("dve", lambda h, o_s=o_s, ta=ta, tb=tb: h.tensor_tensor(out=o_s, in0=ta, in1=tb, op=ALU.add), ["tA", "tB", "cs"], ["sn"])
            m *= 2
        op("pool", lambda h: h.memset(mask64[:], 1.0))
        op("pool", lambda h: h.memset(mask64[:].rearrange("p (s t) -> p s t", t=4)[:, :, 0:1], 0.0))
        sm_ = [big[:, 9200 + 16 * i:9216 + 16 * i] for i in range(8)]
        lr_, li_ = lam[:, 0, :], lam[:, 1, :]
        nr, den, t_a, t_b, gr, gi = sm_[0], sm_[1], sm_[2], sm_[3], sm_[4], sm_[5]
        TT = lambda o, x, y, f_: op("dve", lambda h: h.tensor_tensor(out=o, in0=x, in1=y, op=f_))
        op("dve", lambda h: h.tensor_scalar(out=nr, in0=are[:], scalar1=-1.0, scalar2=None, op0=ALU.add))
        TT(den, lr_, lr_, ALU.mult); TT(t_a, li_, li_, ALU.mult); TT(den, den, t_a, ALU.add)
        op("dve", lambda h: h.reciprocal(out=den, in_=den))
        TT(t_a, nr, lr_, ALU.mult); TT(t_b, aim[:], li_, ALU.mult); TT(t_a, t_a, t_b, ALU.add); TT(fre[:], t_a, den, ALU.mult)
        TT(t_a, aim[:], lr_, ALU.mult); TT(t_b, nr, li_, ALU.mult); TT(t_a, t_a, t_b, ALU.subtract); TT(fim[:], t_a, den, ALU.mult)
        TT(den, fre[:], fre[:], ALU.mult); TT(t_a, fim[:], fim[:], ALU.mult); TT(den, den, t_a, ALU.add)
        op("dve", lambda h: h.reciprocal(out=den, in_=den))
        TT(gr, fre[:], den, ALU.mult); TT(gi, fim[:], den, ALU.mult)
        op("dve", lambda h: h.tensor_scalar(out=gi, in0=gi, scalar1=-1.0, scalar2=None, op0=ALU.mult))
        P.dma("pool", lambda h: h.dma_start(out=WBre[:], in_=bblk_d[:, 0]))
        P.dma("pool", lambda h: h.dma_start(out=WBim[:], in_=bblk_d[:, 1]))
        cbf = big[:, 0:4096].rearrange("p (a q m) -> p a q m", a=2, q=16)
        u1 = big[:, 4096:6144].rearrange("p (q m) -> p q m", q=16); u2 = big[:, 6144:8192].rearrange("p (q m) -> p q m", q=16)
        P.dma("sp", lambda h: h.dma_start(out=cbf, in_=cblk_d))
        fre_b = fre[:].unsqueeze(2).broadcast_to([128, 16, 128]); fim_b = fim[:].unsqueeze(2).broadcast_to([128, 16, 128])
        TT(u1, cbf[:, 0], fre_b, ALU.mult); TT(u2, cbf[:, 1], fim_b, ALU.mult); TT(WCre[:], u1, u2, ALU.subtract)
        TT(u1, cbf[:, 0], fim_b, ALU.mult); TT(u2, cbf[:, 1], fre_b, ALU.mult); TT(u1, u1, u2, ALU.add)
        op("dve", lambda h: h.tensor_scalar(out=WCimn[:], in0=u1, scalar1=-1.0, scalar2=None, op0=ALU.mult))

        tD = big[:, 9344:9600].rearrange("p (q s) -> p q s", q=16)
        tE = big[:, 8704:8960].rearrange("p (q s) -> p q s", q=16)
        gr_b = gr.unsqueeze(2).broadcast_to([128, 16, NSEQ]); gi_b = gi.unsqueeze(2).broadcast_to([128, 16, NSEQ])
        TT(tD, h0[:, 0], gr_b, ALU.mult); TT(tE, h0[:, 1], gi_b, ALU.mult); TT(tD, tD, tE, ALU.subtract)
        TT(tE, h0[:, 0], gi_b, ALU.mult); TT(h0[:, 0], tD, tD, ALU.max)
        TT(tD, h0[:, 1], gr_b, ALU.mult); TT(h0[:, 1], tE, tD, ALU.add)
        a_re_b = are[:].unsqueeze(2).broadcast_to([128, 16, NSEQ]); a_im_b = aim[:].unsqueeze(2).broadcast_to([128, 16, NSEQ])
        tC = big[:, 8704:8960].rearrange("p (q s) -> p q s", q=16)
        op("dve", lambda h: h.tensor_tensor(out=ah[:, 0], in0=h0[:, 0], in1=a_re_b, op=ALU.mult), ["h0", "are"], ["ah0"])
        op("dve", lambda h: h.tensor_tensor(out=tC[:], in0=h0[:, 1], in1=a_im_b, op=ALU.mult), ["h0", "aim"], ["tC"])
        op("dve", lambda h: h.tensor_tensor(out=ah[:, 0], in0=ah[:, 0], in1=tC[:], op=ALU.subtract), ["ah0", "tC"], ["ah0"])
        op("dve", lambda h: h.tensor_tensor(out=ah[:, 1], in0=h0[:, 0], in1=a_im_b, op=ALU.mult), ["h0", "aim"], ["ah1"])
        op("dve", lambda h: h.tensor_tensor(out=tC[:], in0=h0[:, 1], in1=a_re_b, op=ALU.mult), ["h0", "are", "ah0"], ["tC"])
        op("dve", lambda h: h.tensor_tensor(out=ah[:, 1], in0=ah[:, 1], in1=tC[:], op=ALU.add), ["ah1", "tC"], ["ah1"])


        def rms(src_ap, key_src, n, slot, scale, pn, junk=None):
            junk = xn if junk is None else junk
            op("act", lambda h: h.activation(out=junk[0:pn, 0:n], in_=src_ap, func=AF.Square, accum_out=ssq[0:pn, slot:slot + 1]),
               [key_src], ["junk", "ssq%d" % slot])
            op("act", lambda h: h.activation(out=rstd[0:pn, slot:slot + 1], in_=ssq[0:pn, slot:slot + 1], func=AF.Ln, scale=scale, bias=epsb[0:pn, 0:1]))
            op("act", lambda h: h.activation(out=rstd[0:pn, slot:slot + 1], in_=rstd[0:pn, slot:slot + 1], func=AF.Exp, scale=-0.5))

        def partA(ti, tl):
            sample = (ti == NPT)
            xt = xb[:, tl, :]
            n = 128
            ns = NS if sample else 128
            r0 = ti * 128
            par = ti % 2
            if sample:
                op("pool", lambda h: h.memset(kTs[:], 0.0))
                P.dma("pool", lambda h: h.dma_start(out=kTs[0:64, 0, 0:NSEQ * 128], in_=kcT_d[0:64, :]))
                P.dma("pool", lambda h: h.dma_start(out=kTs[64:128, 1, 0:NSEQ * 128], in_=kcT_d[64:128, :]))
                P.dma("pool", lambda h: h.dma_start(out=vs[:, 0:NSEQ, :], in_=vc_d))
                P.dma("sp", lambda h: h.dma_start(out=cvst, in_=cvst_d))
            rms(xt[:], "xt", D, 0, 1.0 / D, 128)
            op("dve", lambda h: h.tensor_scalar(out=xn[:], in0=xt[:], scalar1=rstd[:, 0:1], scalar2=None, op0=ALU.mult))
            for c in range(8):
                op("pe", lambda h, c=c: h.transpose(out=pT[:, c, :], in_=xn[:, c * 128:(c + 1) * 128], identity=idb[:]))
            op("dve", lambda h: h.tensor_tensor(out=xnT[:], in0=pT[:], in1=pcol[:, GMIX:GMIX + 8].unsqueeze(2).broadcast_to([128, 8, 128]), op=ALU.mult))
            W = wi[par]
            yield
            for i in range(4):
                bank = acc0 if i % 2 == 0 else acc1
                for c in range(8):
                    op("pe", lambda h, i=i, c=c, bank=bank: h.matmul(bank[:, 0:128], lhsT=W[:, c, i * 128:(i + 1) * 128], rhs=xnT[:, c, :],
                                                                   start=(c == 0), stop=(c == 7)))
                op("act", lambda h, i=i, bank=bank: h.activation(out=qT[:, i, :], in_=bank[:, 0:128], func=AF.Copy))
            yield
            for c in range(8):
                op("pe", lambda h, c=c: h.matmul(Ops[:, 0:128], lhsT=W[:, c, 512:640], rhs=xnT[:, c, :], start=(c == 0), stop=(c == 7)))
            if sample:
                op("dve", lambda h: h.tensor_copy(out=kTs[0:64, 0, NSEQ * 128:17 * 128], in_=Ops[0:64, 0:128]))
                op("dve", lambda h: h.tensor_copy(out=kTs[64:128, 1, NSEQ * 128:17 * 128], in_=Ops[64:128, 0:128]))
            else:
                op("dve", lambda h: h.tensor_copy(out=kTx[0:64, 0, 128:256], in_=Ops[0:64, 0:128]))
                op("dve", lambda h: h.tensor_copy(out=kTx[64:128, 1, 128:256], in_=Ops[64:128, 0:128]))
            yield
            for i in range(4):
                bank = acc0 if i % 2 == 0 else acc1
                for c in range(8):
                    op("pe", lambda h, i=i, c=c, bank=bank: h.matmul(bank[:, 0:128], lhsT=W[:, c, 768 + i * 128:768 + (i + 1) * 128], rhs=xnT[:, c, :],
                                                                   start=(c == 0), stop=(c == 7)))
                op("act", lambda h, i=i, bank=bank: h.activation(out=uT2[par][:, i, :], in_=bank[:, 0:128], func=AF.Copy))
            yield
            for c in range(8):
                op("pe", lambda h, c=c: h.matmul(Ops[:, 0:256], lhsT=xnT[:, c, :], rhs=W[:, c, 512:768], start=(c == 0), stop=(c == 7)))
            if sample:
                op("dve", lambda h: h.tensor_copy(out=vs[:, 16, :], in_=Ops[:, 128:256]))
            else:
                op("dve", lambda h: h.tensor_copy(out=vx[:, 1, :], in_=Ops[:, 128:256]))
            if sample or ti == NPT - 1:
                op("dve", lambda h: h.tensor_copy(out=kvtok[:], in_=Ops[:, 0:256]))
                if sample:
                    P.dma("sp", lambda h: h.dma_start(out=kvs_k[:, 124:128, :], in_=kvtok[0:NS, 0:128]))
                    P.dma("sp", lambda h: h.dma_start(out=kvs_v[:, 124:128, :], in_=kvtok[0:NS, 128:256]))
                else:
                    P.dma("sp", lambda h: h.dma_start(out=kvp_d, in_=kvtok[:]))

            if not sample:
                mi = 0 if ti == 0 else (1 if ti == 1 else 2)
                for i in range(4):
                    for hh_ in range(2):
                        op("pe", lambda h, i=i, hh_=hh_: h.matmul(Sps[:, hh_ * 256:(hh_ + 1) * 256], lhsT=qT[:, i, :], rhs=kTx[:, hh_, :], start=True, stop=True))
                    for hh_ in range(2):
                        op("dve", lambda h, hh_=hh_, hd=i + 4 * hh_: h.tensor_scalar(out=Sx[:, 0, hh_, 256:257], in0=sink8[:, hd:hd + 1], scalar1=8.0, scalar2=None, op0=ALU.mult))
                    op("dve", lambda h: h.tensor_tensor(out=Sx[:, 0, :, 0:256], in0=Sps[:].rearrange("p (a k) -> p a k", a=2),
                                                        in1=masks[:, mi:mi + 1, :].broadcast_to([128, 2, 256]), op=ALU.add))
                    op("dve", lambda h: h.tensor_reduce(out=mx[:], in_=Sx[:, 0], axis=AX.X, op=ALU.max))
                    op("dve", lambda h: h.tensor_scalar(out=nbias[:], in0=mx[:], scalar1=-0.125, scalar2=None, op0=ALU.mult))
                    for hh_ in range(2):
                        op("act", lambda h, hh_=hh_: h.activation(out=Pb[:, hh_, :], in_=Sx[:, 0, hh_, :], func=AF.Exp, scale=0.125,
                                                                 bias=nbias[:, hh_:hh_ + 1], accum_out=rs[:, hh_:hh_ + 1]))
                    op("dve", lambda h: h.reciprocal(out=rinv[:], in_=rs[:]))
                    for hh_ in range(2):
                        for blk in range(2):
                            op("pe", lambda h, hh_=hh_, blk=blk: h.transpose(out=pT[:, hh_ * 2 + blk, :], in_=Pb[:, hh_, blk * 128:(blk + 1) * 128], identity=idb[:]))
                    op("act", lambda h: h.activation(out=PTs[:], in_=pT[:, 0:4, :], func=AF.Copy))
                    for hh_ in range(2):
                        for blk in range(2):
                            op("pe", lambda h, hh_=hh_, blk=blk: h.matmul(Ops[:, hh_ * 64:(hh_ + 1) * 64], lhsT=PTs[:, hh_ * 2 + blk, :],
                                                                         rhs=vx[:, blk, hh_ * 64:(hh_ + 1) * 64], start=(blk == 0), stop=(blk == 1)))
                    for hh_ in range(2):
                        hd = i + 4 * hh_
                        op("dve", lambda h, hh_=hh_, hd=hd: h.tensor_scalar(out=attn[:, hd * 64:(hd + 1) * 64], in0=Ops[:, hh_ * 64:(hh_ + 1) * 64],
                                                                           scalar1=rinv[:, hh_:hh_ + 1], scalar2=None, op0=ALU.mult))
                    yield
                op("pool", lambda h: h.tensor_copy(out=kTx[:, :, 0:128], in_=kTx[:, :, 128:256]))
                op("pool", lambda h: h.tensor_copy(out=vx[:, 0, :], in_=vx[:, 1, :]))
            else:
                W17 = 17 * 128
                for hd in range(8):
                    i, hh_ = hd % 4, hd // 4
                    for cb in range(5):
                        c0 = cb * 512
                        cw = min(512, W17 - c0)
                        bank = mmA if cb % 2 == 0 else mmB
                        op("pe", lambda h, i=i, hh_=hh_, c0=c0, cw=cw, bank=bank: h.matmul(bank[:, 0:cw], lhsT=qT[:, i, :], rhs=kTs[:, hh_, c0:c0 + cw], start=True, stop=True))
                        op("dve", lambda h, c0=c0, cw=cw, bank=bank: h.tensor_tensor(out=Ssx[:, c0:c0 + cw], in0=bank[:, 0:cw], in1=smask[:, c0:c0 + cw], op=ALU.add))
                    op("dve", lambda h, hd=hd: h.tensor_scalar(out=Ssx[:, W17:W17 + 1], in0=sink8[:, hd:hd + 1], scalar1=8.0, scalar2=None, op0=ALU.mult))
                    op("dve", lambda h: h.tensor_reduce(out=mx[:, 0:1], in_=Ssx[:], axis=AX.X, op=ALU.max))
                    op("dve", lambda h: h.tensor_scalar(out=nbias[:, 0:1], in0=mx[:, 0:1], scalar1=-0.125, scalar2=None, op0=ALU.mult))
                    op("act", lambda h: h.activation(out=Psb[:], in_=Ssx[:], func=AF.Exp, scale=0.125, bias=nbias[:, 0:1], accum_out=rs[:, 0:1]))
                    op("dve", lambda h: h.reciprocal(out=rinv[:, 0:1], in_=rs[:, 0:1]))
                    for g8 in range(3):
                        nb_ = 8 if g8 < 2 else 1
                        for b in range(nb_):
                            blk = g8 * 8 + b
                            op("pe", lambda h, b=b, blk=blk: h.transpose(out=pT[:, b, :], in_=Psb[:, blk * 128:(blk + 1) * 128], identity=idb[:]))
                        op("act", lambda h, g8=g8, nb_=nb_: h.activation(out=PTss[:, g8 * 8:g8 * 8 + nb_, :], in_=pT[:, 0:nb_, :], func=AF.Copy))
                    for blk in range(17):
                        op("pe", lambda h, blk=blk, hh_=hh_: h.matmul(Ops[:, 0:64], lhsT=PTss[:, blk, :], rhs=vs[:, blk, hh_ * 64:(hh_ + 1) * 64],
                                                                     start=(blk == 0), stop=(blk == 16)))
                    op("dve", lambda h, hd=hd: h.tensor_scalar(out=attn[:, hd * 64:(hd + 1) * 64], in0=Ops[:, 0:64], scalar1=rinv[:, 0:1], scalar2=None, op0=ALU.mult))
            rms(attn[0:n, :], "attn", 512, 1, 1.0 / 512, n)
            op("dve", lambda h: h.tensor_scalar(out=anb[0:n, :], in0=attn[0:n, :], scalar1=rstd[0:n, 1:2], scalar2=None, op0=ALU.mult), ["attn", "rstd1"], ["anb"])
            for c in range(4):
                op("pe", lambda h, c=c: h.transpose(out=pT[:, c, 0:n], in_=anb[0:n, c * 128:(c + 1) * 128], identity=idb[0:n, 0:n]), ["anb", "idb"], ["pT"])
            op("dve", lambda h: h.tensor_tensor(out=mixT2[par][:, 0:4, :], in0=pT[:, 0:4, :], in1=pcol[:, GATT:GATT + 4].unsqueeze(2).broadcast_to([128, 4, 128]), op=ALU.mult))

            yield

        def partB(ti, tl):
            sample = (ti == NPT)
            xt = xb[:, tl, :]
            n = 128
            ns = NS if sample else 128
            par = ti % 2
            uT = uT2[par]
            mixT = mixT2[par]
            TT = lambda o, x, y, f_: op("dve", lambda h: h.tensor_tensor(out=o, in0=x, in1=y, op=f_))

            def ssm_mm(c4):
                q0 = 4 * c4
                for ri, (WBx, bank) in enumerate(((WBre, mmA), (WBim, mmC))):
                    for j in range(4):
                        op("pe", lambda h: h.matmul(bank[:, j * 128:(j + 1) * 128], lhsT=WBx[:, q0 + j, :], rhs=uT[:, c4, :], start=True, stop=True))

            def ssm_batch(c4):
                q0 = 4 * c4
                T = [pp4[:, kk] for kk in range(4)]
                Bre = mmA[:, :].rearrange("p (j k) -> p j k", j=4); Bim = mmC[:, :].rearrange("p (j k) -> p j k", j=4)
                if sample:
                    op("act", lambda h: h.activation(out=vv4[:, 0], in_=Bre, func=AF.Copy))
                    op("act", lambda h: h.activation(out=vv4[:, 1], in_=Bim, func=AF.Copy))
                    for ri in range(2):
                        v4 = vv4[:, ri, :, 0:NS].rearrange("p j (s t) -> p j s t", t=4)[:, :, :, 0]
                        TT(v4, v4, ah[:, ri, q0:q0 + 4, :], ALU.add)
                    csq = cs[:, q0:q0 + 4, 1:5].unsqueeze(2).broadcast_to([128, 4, NSEQ, 4])
                    snq = sn[:, q0:q0 + 4, 1:5].unsqueeze(2).broadcast_to([128, 4, NSEQ, 4])
                    v3 = lambda ap: ap[:, :, 0:NS].rearrange("p j (s t) -> p j s t", t=4)
                    Bre, Bim = vv4[:, 0], vv4[:, 1]
                else:
                    csq = cs[:, q0:q0 + 4, 1:129]; snq = sn[:, q0:q0 + 4, 1:129]
                    v3 = lambda ap: ap
                Tv = [v3(t_) for t_ in T]
                RR = [v3(rr4[:, kk]) for kk in range(2)]
                TT(Tv[0], v3(Bre), csq, ALU.mult); TT(Tv[1], v3(Bim), snq, ALU.mult)
                TT(Tv[2], v3(Bim), csq, ALU.mult); TT(Tv[3], v3(Bre), snq, ALU.mult)
                if c4 + 1 < 4:
                    ssm_mm(c4 + 1)
                TT(RR[0], Tv[0], Tv[1], ALU.add); TT(RR[1], Tv[2], Tv[3], ALU.subtract)
                for j in range(4):
                    q = q0 + j
                    if sample:
                        op("dve", lambda h: h.tensor_scalar(out=dtmp[:], in0=mask64[:], scalar1=rho[:, q:q + 1], scalar2=None, op0=ALU.mult))
                    for ri in range(2):
                        if sample:
                            op("dve", lambda h: h.tensor_tensor_scan(out=vv4[:, ri, j, 0:NS], data0=dtmp[:], data1=rr4[:, ri, j, 0:NS], initial=0.0, op0=ALU.mult, op1=ALU.add))
                        else:
                            op("dve", lambda h: h.tensor_tensor_scan(out=vv4[:, ri, j, :], data0=rho[:, q:q + 1].broadcast_to([128, 128]), data1=rr4[:, ri, j, :],
                                                                   initial=car[:, ri, q:q + 1], op0=ALU.mult, op1=ALU.add))
                VV = [v3(vv4[:, kk]) for kk in range(2)]
                TT(Tv[0], VV[0], csq, ALU.mult); TT(Tv[1], VV[1], snq, ALU.mult)
                TT(Tv[2], VV[0], snq, ALU.mult); TT(Tv[3], VV[1], csq, ALU.mult)
                TT(v3(hb[:, :, 0, :]), Tv[0], Tv[1], ALU.subtract); TT(v3(hb[:, :, 1, :]), Tv[2], Tv[3], ALU.add)
                if sample:
                    TT(hfin[:, 0, q0:q0 + 4, 0:NSEQ], Tv[0][:, :, :, 3], Tv[1][:, :, :, 3], ALU.subtract)
                    TT(hfin[:, 1, q0:q0 + 4, 0:NSEQ], Tv[2][:, :, :, 3], Tv[3][:, :, :, 3], ALU.add)
                else:
                    TT(car[:, 0, q0:q0 + 4], Tv[0][:, :, 127], Tv[1][:, :, 127], ALU.subtract)
                    TT(car[:, 1, q0:q0 + 4], Tv[2][:, :, 127], Tv[3][:, :, 127], ALU.add)

            def ssm_y(c4):
                for qq in range(4):
                    q = c4 * 4 + qq
                    op("pe", lambda h: h.matmul(mmB[:, 0:n], lhsT=WCre[:, q, :], rhs=hb[:, qq, 0, 0:n], start=(qq == 0), stop=False))
                    op("pe", lambda h: h.matmul(mmB[:, 0:n], lhsT=WCimn[:, q, :], rhs=hb[:, qq, 1, 0:n], start=False, stop=(qq == 3)))
                op("dve", lambda h: h.scalar_tensor_tensor(out=yv[:, 0:n], in0=uT[:, c4, 0:n], scalar=col(DSK + c4), in1=mmB[:, 0:n], op0=ALU.mult, op1=ALU.add))
                op("act", lambda h: h.activation(out=gl32[:, c4, 0:n], in_=yv[:, 0:n], func=AF.Gelu))
                op("pool", lambda h: h.tensor_copy(out=glb[:, c4, 0:n], in_=gl32[:, c4, 0:n]))

            ssm_mm(0)
            yield
            for c4_ in range(4):
                ssm_batch(c4_)
                yield
                ssm_y(c4_)
                yield
            if ti == NPT - 1:
                op("act", lambda h: h.activation(out=hfin[:, :, :, NSEQ], in_=car[:], func=AF.Copy), ["car"], ["hfin"])
            for oc in range(4):
                for c4 in range(4):
                    op("pe", lambda h, oc=oc, c4=c4: h.matmul(mmC[:, 0:n], lhsT=wgl[0][:, c4, oc * 128:(oc + 1) * 128], rhs=glb[:, c4, 0:n], start=(c4 == 0), stop=(c4 == 3)),
                       ["glb", "wgl"], ["mmC"])
                op("act", lambda h, oc=oc: h.activation(out=sg[:, 0:n], in_=mmC[:, 0:n], func=AF.Sigmoid, bias=col(BGLU + oc)), ["mmC", "pcol"], ["sg"])
                op("dve", lambda h, oc=oc: h.tensor_tensor(out=gl32[:, oc, 0:n], in0=gl32[:, oc, 0:n], in1=sg[:, 0:n], op=ALU.mult), ["gl32", "sg", "glb"], ["gl32"])
                op("act", lambda h, oc=oc: h.activation(out=sqb[:, oc, 0:n], in_=gl32[:, oc, 0:n], func=AF.Square), ["gl32"], ["sqb"])
            for oc in range(4):
                op("pe", lambda h, oc=oc: h.matmul(mmC[:, 0:n], lhsT=onesb[:], rhs=sqb[:, oc, 0:n], start=(oc == 0), stop=(oc == 3)), ["sqb", "onesb"], ["mmC"])
            op("act", lambda h: h.activation(out=rsb[:, 0:n], in_=mmC[:, 0:n], func=AF.Ln, scale=1.0 / 512, bias=epsb[:, 0:1]))
            op("act", lambda h: h.activation(out=rsb[:, 0:n], in_=rsb[:, 0:n], func=AF.Exp, scale=-0.5))
            for oc in range(4):
                op("dve", lambda h, oc=oc: h.scalar_tensor_tensor(out=mixT[:, 4 + oc, 0:n], in0=gl32[:, oc, 0:n], scalar=col(GSSM + oc), in1=rsb[:, 0:n], op0=ALU.mult, op1=ALU.mult),
                   ["gl32", "rsb", "pcol"], ["mixT"])

            yield "join"
            for hf in range(2):
                acc, ak = (acc0, "acc0") if hf == 0 else (acc1, "acc1")
                for c in range(8):
                    op("pe", lambda h, hf=hf, c=c, acc=acc: h.matmul(acc[0:n, :], lhsT=mixT[:, c, 0:n], rhs=wo[0][:, c, hf * 512:(hf + 1) * 512], start=(c == 0), stop=(c == 7)),
                       ["mixT", "wo"], [ak])
                op("dve", lambda h, hf=hf, acc=acc: h.tensor_tensor(out=xt[0:n, hf * 512:(hf + 1) * 512], in0=xt[0:n, hf * 512:(hf + 1) * 512], in1=acc[0:n, :], op=ALU.add),
                   ["xt", ak], ["xt"])
            rms(xt[0:n, :], "xt", D, 2, 1.0 / D, n)
            op("dve", lambda h: h.tensor_scalar(out=xn[0:n, :], in0=xt[0:n, :], scalar1=rstd[0:n, 2:3], scalar2=None, op0=ALU.mult), ["xt", "rstd2"], ["xn"])
            for c in range(8):
                op("pe", lambda h, c=c: h.transpose(out=pT[:, c, 0:n], in_=xn[0:n, c * 128:(c + 1) * 128], identity=idb[0:n, 0:n]), ["xn", "idb"], ["pT"])
            op("dve", lambda h: h.tensor_tensor(out=xn2T[:, :, tl * 128:(tl + 1) * 128], in0=pT[:], in1=pcol[:, GFFN:GFFN + 8].unsqueeze(2).broadcast_to([128, 8, 128]), op=ALU.mult))

        def ffn(tiles_):
            sample = (tiles_[0] == NPT)
            ntl = len(tiles_)
            nb = ntl * 128
            hTv = hTsm if (sample and ntl == 1) else hTall
            p0 = 128 if sample else 0
            npc = nb - p0
            for ch in range(NFC):
                b3 = ch % 2
                w3 = ch % 3
                P.dma("sp", lambda h: h.dma_start(out=wg[w3][:], in_=wgb_d[ch]))
                P.dma("sp", lambda h: h.dma_start(out=wu[w3][:], in_=wub_d[ch]))
                gps, ups = (mmA, mmB) if b3 == 0 else (mmC, pT32)
                for c in range(8):
                    op("pe", lambda h, c=c, b3=b3, gps=gps: h.matmul(gps[:, 0:nb], lhsT=wg[w3][:, c, :], rhs=xn2T[:, c, 0:nb], start=(c == 0), stop=(c == 7)))
                for c in range(8):
                    op("pe", lambda h, c=c, b3=b3, ups=ups: h.matmul(ups[:, 0:nb], lhsT=wu[w3][:, c, :], rhs=xn2T[:, c, 0:nb], start=(c == 0), stop=(c == 7)))
                w0, w1, w2, bb = col(CVW + ch * 3), col(CVW + ch * 3 + 1), col(CVW + ch * 3 + 2), col(CVB + ch)
                cvb, slb, gxb = cv[:, 0, :], sl[:, 0, :], gx[:, b3, :]
                def conv3(g0, g1, g2, cvv):
                    op("dve", lambda h: h.tensor_scalar(out=cvv, in0=g2, scalar1=w2, scalar2=bb, op0=ALU.mult, op1=ALU.add))
                    op("dve", lambda h: h.scalar_tensor_tensor(out=cvv, in0=g1, scalar=w1, in1=cvv, op0=ALU.mult, op1=ALU.add))
                    op("dve", lambda h: h.scalar_tensor_tensor(out=cvv, in0=g0, scalar=w0, in1=cvv, op0=ALU.mult, op1=ALU.add))

                if sample:
                    op("pool", lambda h: h.tensor_copy(out=gxs[:, :, 0:2], in_=cvst[:, ch, :, :]))
                    op("act", lambda h: h.activation(out=gxs[:, :, 2:6], in_=gps[:, 0:NS].rearrange("p (s t) -> p s t", t=4), func=AF.Copy))
                    op("pool", lambda h: h.tensor_copy(out=sconv[:, ch, :, :], in_=gxs[:, :, 4:6]))
                    conv3(gxs[:, :, 0:4], gxs[:, :, 1:5], gxs[:, :, 2:6], cvb[:, 0:NS].rearrange("p (s t) -> p s t", t=4))
                if npc:
                    op("pool", lambda h: h.tensor_copy(out=gxb[:, 0:2], in_=gcar[:, ch, :]))
                    op("act", lambda h: h.activation(out=gxb[:, 2:2 + npc], in_=gps[:, p0:nb], func=AF.Copy))
                    op("pool", lambda h: h.tensor_copy(out=gcar[:, ch, :], in_=gxb[:, npc:npc + 2]))
                    conv3(gxb[:, 0:npc], gxb[:, 1:1 + npc], gxb[:, 2:2 + npc], cvb[:, p0:nb])
                op("act", lambda h, cvb=cvb, slb=slb: h.activation(out=slb[:, 0:nb], in_=cvb[:, 0:nb], func=AF.Silu))
                op("dve", lambda h, ch=ch, slb=slb: h.tensor_tensor(out=hTv[:, ch, 0:nb], in0=slb[:, 0:nb], in1=ups[:, 0:nb], op=ALU.mult))
            accs = [acc0, acc1, Sps, Ops]
            for hf in range(2):
                for g in range(NFC // 2):
                    wdb = wd[g % 3]
                    P.dma("sp", lambda h: h.dma_start(out=wdb, in_=wdb_d[hf, g]))
                    for j in range(2):
                        ch = 2 * g + j
                        for tl in range(ntl):
                            op("pe", lambda h: h.matmul(accs[tl][:, :], lhsT=hTv[:, ch, tl * 128:(tl + 1) * 128], rhs=wdb[:, j, :],
                                                        start=(ch == 0), stop=(ch == NFC - 1)))
                for tl in range(ntl):
                    op("dve", lambda h, tl=tl, hf=hf: h.tensor_tensor(out=xb[:, tl, hf * 512:(hf + 1) * 512], in0=xb[:, tl, hf * 512:(hf + 1) * 512], in1=accs[tl][:, :], op=ALU.add))

        def ffn_tail(tiles_, nxt_, first, last):
            for tl in range(first, last):
                ti = tiles_[tl]
                xt = xb[:, tl, :]
                rms(xt, "xt", D, 3, 1.0 / D, 128, junk=gl32[:].rearrange("p a k -> p (a k)").bitcast(BF16))
                op("dve", lambda h: h.scalar_tensor_tensor(out=xt, in0=xt, scalar=rstd[:, 3:4], in1=gfb[:], op0=ALU.mult, op1=ALU.mult))
                P.dma("sp", lambda h: h.dma_start(out=y_d[ti * 128:(ti + 1) * 128, :], in_=xt))
                if nxt_ is not None and tl < len(nxt_) and (tl == 0 or NPT not in nxt_):
                    load_x(tl, nxt_[tl])
                yield

        def est_dur(kind, eng, fn):
            rec = _Recorder()
            fn(rec)
            name, args, kwargs = rec.call
            o = kwargs.get("out", args[0] if args else None)
            nfree = 1
            if o is not None and hasattr(o, "shape"):
                for d_ in list(o.shape)[1:]:
                    nfree *= int(d_)
            if kind == "dma":
                return 100.0, 2500.0
            if eng == "dve":
                d = ((2 * nfree if name == "tensor_tensor_scan" else nfree) + 151) / 0.96
            elif eng == "act":
                d = (nfree + 230) / 1.2 + (90 if kwargs.get("accum_out") is not None else 0)
            elif eng == "pool":
                d = (2 * nfree + 250) / 1.2
            else:
                d = max(64, nfree) / 1.9 + 25
            return d, d

        def merge_threads(gens):
            gens = [g for g in gens if g is not None]
            bufs = [[] for _ in gens]
            alive = [True] * len(gens)
            tchain = [sched_now[0]] * len(gens)
            joined = [False] * len(gens)
            while True:
                for i, g in enumerate(gens):
                    while alive[i] and not bufs[i] and not joined[i]:
                        P.capture = bufs[i]
                        r_ = next(g, "done")
                        P.capture = None
                        if r_ == "done":
                            alive[i] = False
                        elif r_ == "join":
                            joined[i] = True
                for i in range(len(gens)):
                    if joined[i] and not bufs[i] and not any((alive[j] or bufs[j]) for j in range(len(gens)) if j != i):
                        joined[i] = False
                if not any(bufs) and any(alive):
                    continue
                cands = [i for i in range(len(gens)) if bufs[i]]
                if not cands:
                    break
                best, best_t = None, None
                for i in cands:
                    kind, eng, fn = bufs[i][0]
                    t = max(eng_free.get(eng, 0.0), tchain[i] + 250.0)
                    if best is None or t < best_t:
                        best, best_t = i, t
                kind, eng, fn = bufs[best].pop(0)
                busy, lat = est_dur(kind, eng, fn)
                eng_free[eng] = best_t + busy
                tchain[best] = best_t + lat
                sched_now[0] = max(sched_now[0], best_t)
                (P.dma if kind == "dma" else P.op)(eng, fn)

        eng_free = {}
        ffn_cast_issued = [False]
        sched_now = [0.0]
        if tiles is None:
            blocks = [[0, 1, 2, 3], [4, 5, 6, 7], [8, 9, 10, 11], [12, 13, 14, 15], [NPT, 16]]
        else:
            blocks = tiles
        loaded = set()

        def load_x(tl, ti):
            if ti not in loaded:
                loaded.add(ti)
                P.dma("sp", lambda h: h.dma_start(out=xb[:, tl, :], in_=xin[ti * 128:(ti + 1) * 128, :]))

        pre_done = False
        for bi, blk_ in enumerate(blocks):
            lazy = NPT in blk_
            for tl, ti in enumerate(blk_):
                if tl == 0 or not lazy:
                    load_x(tl, ti)
            if not pre_done:
                for _ in partA(blk_[0], 0):
                    pass
            if not ffn_cast_issued[0]:
                ffn_cast_issued[0] = True
                for a_ in range(0, NFC, 11):
                    P.dma("pool", lambda h: h.dma_start(out=wgb_d[a_:a_ + 11], in_=w_gate[a_:a_ + 11]))
                    P.dma("pool", lambda h: h.dma_start(out=wub_d[a_:a_ + 11], in_=w_up[a_:a_ + 11]))
                for hf_ in range(2):
                    P.dma("pool", lambda h: h.dma_start(out=wdb_d[hf_], in_=w_down[hf_]))
            for tl, ti in enumerate(blk_):
                gb = partB(ti, tl)
                ga = partA(blk_[tl + 1], tl + 1) if tl + 1 < len(blk_) else None
                if ga is not None and lazy:
                    load_x(tl + 1, blk_[tl + 1])
                merge_threads([gb, ga])
            ffn(blk_)
            nxt = blocks[bi + 1] if bi + 1 < len(blocks) else None
            for _ in ffn_tail(blk_, nxt, 0, 1):
                pass
            if nxt is not None and NPT not in nxt:
                merge_threads([ffn_tail(blk_, nxt, 1, len(blk_)), partA(nxt[0], 0)])
                pre_done = True
            else:
                for _ in ffn_tail(blk_, nxt, 1, len(blk_)):
                    pass
                pre_done = False

        fv = pp4[:].rearrange("p a j k -> p (a j k)")
        v1 = fv[:, 0:272].rearrange("p (q s) -> p q s", q=16); v2 = fv[:, 272:544].rearrange("p (q s) -> p q s", q=16); v3_ = fv[:, 544:816].rearrange("p (q s) -> p q s", q=16)
        fre_c = fre[:].unsqueeze(2).broadcast_to([128, 16, NSEQ + 1]); fim_c = fim[:].unsqueeze(2).broadcast_to([128, 16, NSEQ + 1])
        TT(v1, hfin[:, 0], fre_c, ALU.mult); TT(v2, hfin[:, 1], fim_c, ALU.mult); TT(v3_, v1, v2, ALU.subtract)
        TT(v1, hfin[:, 0], fim_c, ALU.mult); TT(v2, hfin[:, 1], fre_c, ALU.mult); TT(hfin[:, 1], v1, v2, ALU.add)
        op("dve", lambda h: h.tensor_copy(out=hfin[:, 0], in_=v3_))
        P.dma("sp", lambda h: h.dma_start(out=hfin_d, in_=hfin[:]), reads=["hfin"], writes=["hfin_d"])
        P.dma("sp", lambda h: h.dma_start(out=pconv_d, in_=gcar[:]), reads=["gcar"], writes=["pconv_d"])
        P.dma("sp", lambda h: h.dma_start(out=sconv_d, in_=sconv[:]), reads=["sconv"], writes=["sconv_d"])
        P.limit = None
        P.barrier()

        with nc.Block() as block:
            @block.tensor
            def _(h):
                P.run("pe", h, sems)

            @block.scalar
            def _(h):
                P.run("act", h, sems)

            @block.vector
            def _(h):
                P.run("dve", h, sems)

            @block.gpsimd
            def _(h):
                P.run("pool", h, sems)

            @block.sync
            def _(h):
                P.run("sp", h, sems)
    return nc


def _consts():
    ident = np.eye(128, dtype=np.float32)
    i = np.arange(128)[:, None]
    c = np.arange(256)[None, :]
    full = np.where(((c < 128) & (c > i)) | ((c >= 128) & (c - 128 <= i)), 0.0, MASKV)
    m1 = np.where(((c < 128) & (c > i) & (c >= NPAD)) | ((c >= 128) & (c - 128 <= i)), 0.0, MASKV)
    m0 = np.where((c >= 128) & (c - 128 <= i) & (c - 128 >= NPAD), 0.0, MASKV)
    masks = np.stack([m0, m1, full], axis=1).astype(np.float32)
    sm = np.full((128, 17 * 128), MASKV, np.float32)
    for s in range(NSEQ):
        for t in range(4):
            r = s * 4 + t
            sm[r, s * 128 + t + 1:(s + 1) * 128] = 0.0
            sm[r, 2048 + s * 4:2048 + s * 4 + t + 1] = 0.0
    return ident, masks, sm


def prep_inputs(x_prompt, x_sample, cache_k_win, cache_v_win, state_ssm_re, state_ssm_im, state_conv,
           meta_tokens, g_mix, w_in, sinks, lam_re, lam_im, log_dt, b_re, b_im, c_re, c_im, d_skip,
           w_glu, b_glu, g_attn_out, g_ssm_out, w_o, g_ffn, w_gate, w_up, conv_w, conv_b, w_down,
           g_final):
    f32 = np.float32
    A = lambda a: np.ascontiguousarray(np.asarray(a, dtype=f32))
    x_prompt, x_sample = A(x_prompt), A(x_sample)
    ident, masks, smask = _consts()
    w_in0 = A(w_in)[0]
    perm = []
    for i in range(4):
        perm += list(range(i * 64, (i + 1) * 64)) + list(range((4 + i) * 64, (5 + i) * 64))
    w_in_p = np.ascontiguousarray(np.concatenate([w_in0[:, perm], w_in0[:, 512:]], axis=1))
    pcol = np.zeros((128, 128), f32)
    pcol[:, 0:8] = A(g_mix)[0].reshape(8, 128).T
    pcol[:, 8:16] = A(g_ffn)[0].reshape(8, 128).T
    pcol[:, 16:20] = A(g_attn_out)[0].reshape(4, 128).T
    pcol[:, 20:24] = A(g_ssm_out)[0].reshape(4, 128).T
    pcol[:, 24:28] = A(b_glu)[0].reshape(4, 128).T
    pcol[:, 28:32] = A(d_skip)[0].reshape(4, 128).T
    cw = A(conv_w)[0].reshape(3, NFC, 128)
    pcol[:, 32:98] = cw.transpose(2, 1, 0).reshape(128, 66)
    pcol[:, 98:120] = A(conv_b)[0].reshape(NFC, 128).T
    lr, li, ld = A(lam_re)[0], A(lam_im)[0], A(log_dt)[0]
    ldx = np.repeat(ld[:, None], 64, axis=1)

    def pl(a):
        return a.reshape(16, 2, 64).transpose(1, 2, 0).reshape(128, 16)

    lam = np.ascontiguousarray(np.stack([pl(lr), pl(li), pl(ldx)], axis=1))
    lamb = np.ascontiguousarray(np.stack([lr.reshape(-1), li.reshape(-1), ldx.reshape(-1)], axis=0))
    bre, bim, cre, cim = A(b_re)[0], A(b_im)[0], A(c_re)[0], A(c_im)[0]
    bblk = np.zeros((128, 2, 16, 128), f32)
    cblk = np.zeros((128, 2, 16, 128), f32)
    for q in range(16):
        for j2 in range(2):
            g = 2 * q + j2
            g8 = g % 8
            rows = slice(g8 * 16, g8 * 16 + 16)
            cols = slice(j2 * 64, j2 * 64 + 64)
            bblk[rows, 0, q, cols] = bre[g].T
            bblk[rows, 1, q, cols] = bim[g].T
            cblk[cols, 0, q, rows] = cre[g].T
            cblk[cols, 1, q, rows] = cim[g].T
    sre, sim_ = A(state_ssm_re)[0], A(state_ssm_im)[0]
    ck, cvv = A(cache_k_win)[0].reshape(128, 128, 128), A(cache_v_win)[0].reshape(128, 128, 128)
    sc = A(state_conv)[0]
    meta = A(meta_tokens)
    wg_l = np.ascontiguousarray(A(w_gate)[0].reshape(8, 128, NFC, 128).transpose(2, 1, 0, 3))
    wu_l = np.ascontiguousarray(A(w_up)[0].reshape(8, 128, NFC, 128).transpose(2, 1, 0, 3))
    wd_l = np.ascontiguousarray(A(w_down)[0].reshape(NFC // 2, 2, 128, 2, 512).transpose(3, 0, 2, 1, 4))
    in_maps = []
    for c in range(NCORES):
        xin = np.zeros((NPT * 128 + 128, D), f32)
        xin[NPAD:128] = meta
        xin[128:NPT * 128] = x_prompt[c]
        xin[NPT * 128:NPT * 128 + NS] = x_sample[c * NSEQ:(c + 1) * NSEQ].reshape(NS, D)
        sl_ = slice(c * NSEQ, (c + 1) * NSEQ)

        def hl(a):
            return a.reshape(NSEQ, 16, 2, 64).transpose(2, 3, 1, 0).reshape(128, 16, NSEQ)

        h0 = np.ascontiguousarray(np.stack([hl(sre[sl_]), hl(sim_[sl_])], axis=1))
        cvst = np.ascontiguousarray(sc[sl_].reshape(NSEQ, 2, NFC, 128).transpose(3, 2, 0, 1))
        kc, vc = ck[sl_], cvv[sl_]
        kcT = np.ascontiguousarray(kc.transpose(2, 0, 1).reshape(128, NSEQ * 128))
        vcl = np.ascontiguousarray(vc.transpose(1, 0, 2))
        in_maps.append(dict(
            xin=xin, w_in=w_in_p, w_o=A(w_o)[0], w_glu=A(w_glu)[0], w_gate=wg_l, w_up=wu_l,
            w_down=wd_l, ident=ident, masks=masks, smask=smask, pcol=pcol, sinks=A(sinks)[0],
            gfin=A(g_final), lam=lam, lamb=lamb, bblk=bblk, cblk=cblk, h0=h0, cvst=cvst, kcT=kcT, vc=vcl,
            kcache=np.ascontiguousarray(kc), vcache=np.ascontiguousarray(vc)))
    return in_maps


def assemble(R):
    f32 = np.float32
    y_prompt = np.stack([R[c]["y"][128:NPT * 128] for c in range(NCORES)])
    y_sample = np.concatenate([R[c]["y"][NPT * 128:NPT * 128 + NS].reshape(NSEQ, 4, D) for c in range(NCORES)])
    p_k = np.stack([R[c]["kvp"][:, 0:128].reshape(128, 2, 64) for c in range(NCORES)])[None]
    p_v = np.stack([R[c]["kvp"][:, 128:256].reshape(128, 2, 64) for c in range(NCORES)])[None]
    s_k = np.concatenate([R[c]["kvs_k"].reshape(NSEQ, 128, 2, 64) for c in range(NCORES)])[None]
    s_v = np.concatenate([R[c]["kvs_v"].reshape(NSEQ, 128, 2, 64) for c in range(NCORES)])[None]

    def unh(a):
        nn = a.shape[-1]
        return a.reshape(2, 64, 16, nn).transpose(3, 2, 0, 1).reshape(nn, 32, 64)

    p_re = np.stack([unh(R[c]["hfin"][:, 0, :, NSEQ:])[0] for c in range(NCORES)])[None]
    p_im = np.stack([unh(R[c]["hfin"][:, 1, :, NSEQ:])[0] for c in range(NCORES)])[None]
    s_re = np.concatenate([unh(R[c]["hfin"][:, 0, :, :NSEQ]) for c in range(NCORES)])[None]
    s_im = np.concatenate([unh(R[c]["hfin"][:, 1, :, :NSEQ]) for c in range(NCORES)])[None]
    p_conv = np.stack([R[c]["pconv"].transpose(2, 1, 0).reshape(2, FF) for c in range(NCORES)])[None]
    s_conv = np.concatenate([R[c]["sconv"].transpose(2, 3, 1, 0).reshape(NSEQ, 2, FF) for c in range(NCORES)])[None]
    out = (y_prompt, y_sample, p_k, p_v, p_re, p_im, p_conv, s_k, s_v, s_re, s_im, s_conv)
    return tuple(np.ascontiguousarray(o, dtype=f32) for o in out)


def kernel(**inputs):
    in_maps = prep_inputs(**inputs)
    nc = build_nc()
    res = run_bass_kernel_spmd(nc, in_maps, core_ids=list(range(NCORES)))
    return assemble(res.results)
```

```python
import numpy as np
from contextlib import ExitStack
import concourse.bass as bass
import concourse.mybir as mybir
from concourse.bass_utils import run_bass_kernel_spmd

F32 = mybir.dt.float32
BF16 = mybir.dt.bfloat16
ALU = mybir.AluOpType
AF = mybir.ActivationFunctionType
AX = mybir.AxisListType

ENGS = ("pe", "act", "dve", "pool", "sp")
NDMASEM = 12
NCORES = 8
D = 1024
NPT = 17
NPAD = 112
NS = 64
NSEQ = 16
FF = 2816
NFC = 22
EPS = 1e-5
PI = float(np.pi)
MASKV = -30000.0


class _Recorder:
    def __init__(self):
        self.call = None

    def __getattr__(self, name):
        def f(*args, **kwargs):
            self.call = (name, args, kwargs)
            return self
        return f


class Prog:
    def __init__(self):
        self.streams = {e: [] for e in ENGS}
        self.count = {e: 0 for e in ENGS}
        self.waited = {e: {} for e in ENGS}
        self.regs = {}
        self.nrec = 0
        self.limit = None
        self.capture = None
        self.dma_rr = {"sp": 0, "pool": 0}
        self.dma_cnt = {}

    @staticmethod
    def _region(ap):
        dims = [(int(st), int(sz)) for st, sz in ap.ap]
        off = int(ap.offset)
        esz = mybir.dt.size(ap.dtype)
        space = str(ap.space)
        if space == "DRAM":
            ext = sum((sz - 1) * abs(st) for st, sz in dims)
            return (0, 1, off, off + ext)
        pst, npart = dims[0]
        pst = max(pst, 1)
        p0, f0 = off // pst, off % pst
        ext = sum((sz - 1) * abs(st) for st, sz in dims[1:])
        f0, ext = f0 * esz, ext * esz + esz - 1
        if space == "PSUM":
            return (0, 128, 0, 1 << 30)
        return (p0, p0 + npart, f0, f0 + ext)

    @staticmethod
    def _is_ap(v):
        return hasattr(v, "tensor") and hasattr(v, "ap") and hasattr(v, "offset")

    def _record(self, fn):
        rec = _Recorder()
        fn(rec)
        name, args, kwargs = rec.call
        acc = []
        for i, a in enumerate(args):
            if self._is_ap(a):
                acc.append((a, i == 0))
        for k, v in kwargs.items():
            if self._is_ap(v):
                acc.append((v, k in ("out", "accum_out")))
        out = []
        for ap, w in acc:
            if str(ap.space) == "PSUM":
                w = True
            out.append((ap.tensor.name, self._region(ap), w))
        self._last_call = rec.call
        return out

    def _deps(self, eng, acc):
        deps = {}
        for name, R, w in acc:
            for (R2, w2), evs in self.regs.get(name, {}).items():
                if not (w or w2):
                    continue
                if R[0] < R2[1] and R2[0] < R[1] and R[2] <= R2[3] and R2[2] <= R[3]:
                    for s_, v in evs.items():
                        if s_ == eng and eng == "pe":
                            continue
                        if deps.get(s_, 0) < v:
                            deps[s_] = v
        out = []
        wd = self.waited[eng]
        for s_, v in deps.items():
            if wd.get(s_, 0) < v:
                wd[s_] = v
                out.append((s_, v))
        return out

    def _commit(self, ev, acc):
        s_, v = ev
        for name, R, w in acc:
            d = self.regs.setdefault(name, {})
            if w:
                for key in [k for k in d if k[0][0] >= R[0] and k[0][1] <= R[1] and k[0][2] >= R[2] and k[0][3] <= R[3]]:
                    del d[key]
            e = d.setdefault((R, w), {})
            if e.get(s_, 0) < v:
                e[s_] = v

    @staticmethod
    def _freeze(fn):
        rec = _Recorder()
        fn(rec)
        return lambda h, c=rec.call: getattr(h, c[0])(*c[1], **c[2])

    def op(self, eng, fn, reads=(), writes=()):
        if self.capture is not None:
            self.capture.append(("op", eng, self._freeze(fn)))
            return
        self.nrec += 1
        if self.limit is not None and self.nrec > self.limit:
            return
        acc = self._record(fn)
        deps = self._deps(eng, acc)
        self.count[eng] += 1
        ev = (eng, self.count[eng])
        self.streams[eng].append((deps, self._last_call, (eng, 1)))
        self._commit(ev, acc)

    def dma(self, eng, fn, reads=(), writes=()):
        if self.capture is not None:
            self.capture.append(("dma", eng, self._freeze(fn)))
            return
        self.nrec += 1
        if self.limit is not None and self.nrec > self.limit:
            return
        acc = self._record(fn)
        k = self.dma_rr[eng]
        self.dma_rr[eng] = (k + 1) % NDMASEM
        sname = "dma%s%d" % (eng, k)
        deps = self._deps(eng, acc)
        prev = self.dma_cnt.get(sname, 0) * 16
        if prev and self.waited[eng].get(sname, 0) < prev:
            self.waited[eng][sname] = prev
            deps.append((sname, prev))
        self.dma_cnt[sname] = self.dma_cnt.get(sname, 0) + 1
        ev = (sname, self.dma_cnt[sname] * 16)
        self.streams[eng].append((deps, self._last_call, (sname, 16)))
        self._commit(ev, acc)

    def barrier(self):
        evs = [(e, self.count[e]) for e in ENGS if self.count[e]]
        evs += [(k, c * 16) for k, c in self.dma_cnt.items()]
        for e in ENGS:
            deps = []
            for s, v in evs:
                if s == e:
                    continue
                if self.waited[e].get(s, 0) < v:
                    self.waited[e][s] = v
                    deps.append((s, v))
            if deps:
                self.streams[e].append((deps, None, None))

    def run(self, eng, h, sems):
        for deps, fn, inc in self.streams[eng]:
            for s, v in deps:
                h.wait_ge(sems[s], v)
            if fn is not None:
                name, args, kwargs = fn
                getattr(h, name)(*args, **kwargs).then_inc(sems[inc[0]], inc[1])


def build_nc(tiles=None, limit=None):
    nc = bass.Bass("TRN2", target_bir_lowering=False)

    def din(name, shape, dt=F32):
        return nc.dram_tensor(name, list(shape), dt, kind="ExternalInput").ap()

    def dout(name, shape, dt=F32):
        return nc.dram_tensor(name, list(shape), dt, kind="ExternalOutput").ap()

    xin = din("xin", [NPT * 128 + 128, D])
    w_in = din("w_in", [D, 1280])
    w_o = din("w_o", [D, D])
    w_glu = din("w_glu", [512, 512])
    w_gate = din("w_gate", [NFC, 128, 8, 128])
    w_up = din("w_up", [NFC, 128, 8, 128])
    w_down = din("w_down", [2, NFC // 2, 128, 2, 512])
    ident_d = din("ident", [128, 128])
    masks_d = din("masks", [128, 3, 256])
    smask_d = din("smask", [128, 17 * 128])
    pcol_d = din("pcol", [128, 128])
    sinks_d = din("sinks", [8])
    gfin_d = din("gfin", [D])
    lam_d = din("lam", [128, 3, 16])
    lamb_d = din("lamb", [3, 16 * 128])
    bblk_d = din("bblk", [128, 2, 16, 128])
    cblk_d = din("cblk", [128, 2, 16, 128])
    h0_d = din("h0", [128, 2, 16, NSEQ])
    cvst_d = din("cvst", [128, NFC, NSEQ, 2])
    kcT_d = din("kcT", [128, NSEQ * 128])
    vc_d = din("vc", [128, NSEQ, 128])
    kcache_d = din("kcache", [NSEQ, 128, 128])
    vcache_d = din("vcache", [NSEQ, 128, 128])

    wgb_d = nc.dram_tensor("wgb", [NFC, 128, 8, 128], BF16, kind="Internal").ap()
    wub_d = nc.dram_tensor("wub", [NFC, 128, 8, 128], BF16, kind="Internal").ap()
    wdb_d = nc.dram_tensor("wdb", [2, NFC // 2, 128, 2, 512], BF16, kind="Internal").ap()
    y_d = dout("y", [NPT * 128 + 128, D])
    kvp_d = dout("kvp", [128, 256])
    kvs_k = dout("kvs_k", [NSEQ, 128, 128])
    kvs_v = dout("kvs_v", [NSEQ, 128, 128])
    hfin_d = dout("hfin", [128, 2, 16, NSEQ + 1])
    pconv_d = dout("pconv", [128, NFC, 2])
    sconv_d = dout("sconv", [128, NFC, NSEQ, 2])

    P = Prog()
    P.limit = limit
    es = ExitStack()
    with es:
        def sb(name, shape, dt=F32):
            return es.enter_context(nc.sbuf_tensor(name, list(shape), dt))

        def ps(name, shape, dt=F32):
            return es.enter_context(nc.psum_tensor(name, list(shape), dt))

        sems = {e: es.enter_context(nc.semaphore("s_" + e)) for e in ENGS}
        for k in range(NDMASEM):
            for e_ in ("sp", "pool"):
                sems["dma%s%d" % (e_, k)] = es.enter_context(nc.semaphore("s_dma%s%d" % (e_, k)))

        idb = sb("idb", [128, 128], BF16)
        onesb = sb("onesb", [128, 128], BF16)
        masks = sb("masks_s", [128, 3, 256])
        smask = sb("smask_s", [128, 17 * 128], BF16)
        pcol = sb("pcol_s", [128, 128])
        sink8 = sb("sink8", [128, 8])
        gfb = sb("gfb", [128, D])
        epsb = sb("epsb", [128, 1])
        lam = sb("lam_s", [128, 3, 16])
        WBre = sb("WBre", [128, 16, 128], BF16); WBim = sb("WBim", [128, 16, 128], BF16)
        WCre = sb("WCre", [128, 16, 128], BF16); WCimn = sb("WCimn", [128, 16, 128], BF16)
        cs = sb("cs", [128, 16, 129]); sn = sb("sn", [128, 16, 129])
        mask64 = sb("mask64", [128, NS]); dtmp = sb("dtmp", [128, NS])
        are = sb("are", [128, 16]); aim = sb("aim", [128, 16])
        ah = sb("ah", [128, 2, 16, NSEQ])
        car = sb("car", [128, 2, 16])
        hfin = sb("hfin_s", [128, 2, 16, NSEQ + 1])
        gcar = sb("gcar", [128, NFC, 2])
        sconv = sb("sconv_s", [128, NFC, NSEQ, 2])
        kTx = sb("kTx", [128, 2, 256], BF16)
        vx = sb("vx", [128, 2, 128], BF16)
        GMIX, GFFN, GATT, GSSM, BGLU, DSK, CVW, CVB = 0, 8, 16, 20, 24, 28, 32, 98

        def col(c):
            return pcol[:, c:c + 1]

        ssq = sb("ssq", [128, 4]); rstd = sb("rstd", [128, 4])
        xn = sb("xn", [128, D], BF16)
        junk = xn
        xnT = sb("xnT", [128, 8, 128], BF16)
        qT = sb("qT", [128, 4, 128], BF16)
        uT2 = [sb("uT%d" % i, [128, 4, 128], BF16) for i in range(2)]
        Sx = sb("Sx", [128, 1, 2, 257]); mx = sb("mx", [128, 2]); nbias = sb("nbias", [128, 2])
        rs = sb("rs", [128, 2]); rinv = sb("rinv", [128, 2]); esink = sb("esink", [128, 2])
        maskb = sb("maskb", [128, 3, 256], BF16); sink8x = sb("sink8x", [128, 4, 2])
        Pb = sb("Pb", [128, 2, 257], BF16); PTs = sb("PTs", [128, 4, 128], BF16)
        mixT2 = [sb("mixT%d" % i, [128, 8, 128], BF16) for i in range(2)]
        mixT = mixT2[0]
        Psb = sb("Psb", [128, 17 * 128 + 1], BF16)
        PTss = sb("PTss", [128, 17, 128], BF16)
        pp4 = sb("pp4", [128, 4, 4, 128]); rr4 = sb("rr4", [128, 2, 4, 128]); vv4 = sb("vv4", [128, 2, 4, 128])
        hb = sb("hb", [128, 4, 2, 128], BF16)
        gl32 = sb("gl32", [128, 4, 128]); glb = sb("glb", [128, 4, 128], BF16)
        xn2T = sb("xn2T", [128, 8, 512], BF16)
        gx = sb("gx", [128, 2, 514]); gxs = sb("gxs", [128, NSEQ, 6])
        cv = sb("cv", [128, 1, 512]); sl = sb("sl", [128, 1, 512])
        kvtok = sl[:, 0, 0:256]
        yv = gx[:, 0, 0:128]; sg = gx[:, 0, 128:256]; rsb = gx[:, 0, 256:384]
        sqb = gx[:, 1, 0:256].bitcast(BF16).rearrange("p (a k) -> p a k", a=4)
        attn = cv[:, 0, :]
        anb = sl[:, 0, 256:512].bitcast(BF16)
        wi = [sb("wi0", [128, 8, 1280], BF16)] * 2
        wo = [sb("wo0", [128, 8, D], BF16)] * 2
        wgl = [sb("wgl0", [128, 4, 512], BF16)] * 2
        wg = [sb("wg%d" % i, [128, 8, 128], BF16) for i in range(2)] + [xnT]
        wu = [sb("wu%d" % i, [128, 8, 128], BF16) for i in range(2)] + [mixT]
        wdA = sb("wdA", [128, 2, 512], BF16)
        wd = [wdA[:], Sx[:].rearrange("p a b c -> p (a b c)").bitcast(BF16)[:, 0:1024].rearrange("p (j n) -> p j n", j=2),
              xn[:].rearrange("p (j n) -> p j n", j=2)]
        big = sb("big", [128, 9728])
        xb = big[:, 0:4096].rearrange("p (t d) -> p t d", t=4)
        hTall = big[:, 4096:9728].bitcast(BF16).rearrange("p (c n) -> p c n", c=NFC)
        hTsm = big[:, 4096:5504].bitcast(BF16).rearrange("p (c n) -> p c n", c=NFC)
        kTs = big[:, 5504:7680].bitcast(BF16).rearrange("p (a k) -> p a k", a=2)
        vs = big[:, 7680:8768].bitcast(BF16).rearrange("p (b k) -> p b k", b=17)
        scr = big
        h0 = big[:, 8192:8704].rearrange("p (a q s) -> p a q s", a=2, q=16)
        cvst = big[:, 3264:3968].rearrange("p (c s j) -> p c s j", c=NFC, s=NSEQ)
        def prep_views(o):
            return (scr[:, o:o + 768].rearrange("p (a b) -> p a b", a=3), scr[:, o + 768:o + 2304].rearrange("p (a b) -> p a b", a=6),
                    scr[:, o + 2304:o + 2816].rearrange("p (a q m) -> p a q m", a=2, q=2), scr[:, o + 2816:o + 3328].rearrange("p (a q m) -> p a q m", a=2, q=2))
        Ssx = big[:, 1024:1024 + 17 * 128 + 1]
        idf = big[:, 8960:9088]

        mmA = ps("mmA", [128, 512]); mmB = ps("mmB", [128, 512]); mmC = ps("mmC", [128, 512])
        pT = ps("pT", [128, 8, 128], BF16)
        pT32 = pT[:].rearrange("p c k -> p (c k)").bitcast(F32)
        Sps = ps("Sps", [128, 512]); Ops = ps("Ops", [128, 512])
        acc0 = ps("acc0", [128, 512]); acc1 = ps("acc1", [128, 512])

        op = P.op

        P.dma("sp", lambda h: h.dma_start(out=idf[:], in_=ident_d), writes=["idf"])
        P.dma("sp", lambda h: h.dma_start(out=masks[:], in_=masks_d), writes=["masks"])
        P.dma("pool", lambda h: h.dma_start(out=smask[:], in_=smask_d), writes=["smask"])
        P.dma("sp", lambda h: h.dma_start(out=pcol[:], in_=pcol_d), writes=["pcol"])
        P.dma("sp", lambda h: h.dma_start(out=sink8[:], in_=sinks_d.partition_broadcast(128)), writes=["sink8"])
        P.dma("sp", lambda h: h.dma_start(out=gfb[:], in_=gfin_d.partition_broadcast(128)), writes=["gfb"])
        P.dma("sp", lambda h: h.dma_start(out=lam[:], in_=lam_d), writes=["lam"])
        P.dma("sp", lambda h: h.dma_start(out=h0, in_=h0_d))
        op("pool", lambda h: h.memset(hb[:], 0.0))
        op("pool", lambda h: h.memset(cv[:], 0.0))
        op("pool", lambda h: h.memset(gx[:], 0.0))
        P.dma("pool", lambda h: h.dma_start(out=wi[0][:], in_=w_in.rearrange("(c p) n -> p c n", p=128)))
        P.dma("pool", lambda h: h.dma_start(out=wgl[0][:], in_=w_glu.rearrange("(c p) n -> p c n", p=128)))
        P.dma("pool", lambda h: h.dma_start(out=wo[0][:], in_=w_o.rearrange("(c p) n -> p c n", p=128)))
        P.dma("sp", lambda h: h.dma_start(out=kvs_k[:, 0:124, :], in_=kcache_d[:, 4:128, :]), writes=["kvs_k_a"])
        P.dma("sp", lambda h: h.dma_start(out=kvs_v[:, 0:124, :], in_=vcache_d[:, 4:128, :]), writes=["kvs_v_a"])

        op("dve", lambda h: h.tensor_copy(out=idb[:], in_=idf[:]), ["idf"], ["idb"])
        P.dma("pool", lambda h: h.dma_start(out=maskb[:], in_=masks_d))
        op("dve", lambda h: h.tensor_scalar(out=sink8x[:].rearrange("p i a -> p a i"), in0=sink8[:].rearrange("p (a i) -> p a i", a=2), scalar1=8.0, scalar2=None, op0=ALU.mult))
        op("pool", lambda h: h.memset(onesb[:], 1.0), [], ["onesb"])
        op("pool", lambda h: h.memset(epsb[:], EPS), [], ["epsb"])
        op("pool", lambda h: h.memset(kTx[:], 0.0), [], ["kTx"])
        op("pool", lambda h: h.memset(vx[:], 0.0), [], ["vx"])
        op("pool", lambda h: h.memset(car[:], 0.0), [], ["car"])
        op("pool", lambda h: h.memset(gcar[:], 0.0), [], ["gcar"])
        pass
        op("pool", lambda h: h.memset(hfin[:], 0.0))
        op("pool", lambda h: h.memset(sconv[:], 0.0))
        rho = sb("rho", [128, 16]); fre = sb("fre", [128, 16]); fim = sb("fim", [128, 16])
        dtl, thl, c1, s1, tmpa, kfL = (big[:, 9088 + 16 * i:9104 + 16 * i] for i in range(6))
        kiL = big[:, 9184:9200].bitcast(mybir.dt.int32)
        op("act", lambda h: h.activation(out=dtl[:], in_=lam[:, 2, :], func=AF.Exp), ["lam"], ["dtl"])
        op("dve", lambda h: h.tensor_tensor(out=thl[:], in0=lam[:, 1, :], in1=dtl[:], op=ALU.mult), ["lam", "dtl"], ["thl"])
        op("dve", lambda h: h.tensor_tensor(out=tmpa[:], in0=lam[:, 0, :], in1=dtl[:], op=ALU.mult), ["lam", "dtl"], ["tmpa"])
        op("act", lambda h: h.activation(out=rho[:], in_=tmpa[:], func=AF.Exp), ["tmpa"], ["rho"])

        def sincos(eng_tag, th_ap, s_ap, c_ap, tmp_ap, shape_key, ki_ap, kf_ap):
            K = shape_key

            def reduce_(shift, dst_key):
                op("dve", lambda h: h.tensor_scalar(out=tmp_ap, in0=th_ap, scalar1=shift, scalar2=1.0 / (2 * PI), op0=ALU.add, op1=ALU.mult),
                   [K + "th", K + "s", K + "c"], [K + "tmp"])
                op("dve", lambda h: h.tensor_copy(out=ki_ap, in_=tmp_ap), [K + "tmp"], [K + "ki"])
                op("dve", lambda h: h.tensor_copy(out=kf_ap, in_=ki_ap), [K + "ki"], [K + "kf"])
                op("dve", lambda h: h.tensor_scalar(out=tmp_ap, in0=th_ap, scalar1=shift, scalar2=None, op0=ALU.add), [K + "th", K + "ki"], [K + "tmp"])
                op("dve", lambda h: h.scalar_tensor_tensor(out=tmp_ap, in0=kf_ap, scalar=-2 * PI, in1=tmp_ap, op0=ALU.mult, op1=ALU.add),
                   [K + "kf", K + "tmp"], [K + "tmp"])
                op("dve", lambda h: h.tensor_scalar(out=kf_ap, in0=tmp_ap, scalar1=PI, scalar2=None, op0=ALU.is_gt), [K + "tmp"], [K + "kf"])
                op("dve", lambda h: h.scalar_tensor_tensor(out=tmp_ap, in0=kf_ap, scalar=-2 * PI, in1=tmp_ap, op0=ALU.mult, op1=ALU.add),
                   [K + "kf", K + "tmp"], [K + "tmp"])
                op("dve", lambda h: h.tensor_scalar(out=kf_ap, in0=tmp_ap, scalar1=-PI, scalar2=None, op0=ALU.is_lt), [K + "tmp"], [K + "kf"])
                op("dve", lambda h: h.scalar_tensor_tensor(out=tmp_ap, in0=kf_ap, scalar=2 * PI, in1=tmp_ap, op0=ALU.mult, op1=ALU.add),
                   [K + "kf", K + "tmp"], [K + "tmp"])
                op("dve", lambda h: h.tensor_scalar(out=tmp_ap, in0=tmp_ap, scalar1=-PI, scalar2=PI, op0=ALU.max, op1=ALU.min), [K + "tmp"], [K + "tmp"])

            reduce_(0.0, "s")
            op("act", lambda h: h.activation(out=s_ap, in_=tmp_ap, func=AF.Sin), [K + "tmp"], [K + "s"])
            reduce_(0.5 * PI, "c")
            op("act", lambda h: h.activation(out=c_ap, in_=tmp_ap, func=AF.Sin), [K + "tmp"], [K + "c"])

        sincos("L", thl[:], s1[:], c1[:], tmpa[:], "L", kiL[:], kfL[:])
        op("dve", lambda h: h.tensor_tensor(out=are[:], in0=rho[:], in1=c1[:], op=ALU.mult), ["rho", "Lc"], ["are"])
        op("dve", lambda h: h.tensor_tensor(out=aim[:], in0=rho[:], in1=s1[:], op=ALU.mult), ["rho", "Ls"], ["aim"])
        op("pool", lambda h: h.memset(cs[:, :, 0:1], 1.0), [], ["cs"])
        op("pool", lambda h: h.memset(sn[:, :, 0:1], 0.0), [], ["sn"])
        op("dve", lambda h: h.tensor_copy(out=cs[:, :, 1], in_=c1[:]), ["Lc", "cs"], ["cs"])
        op("dve", lambda h: h.tensor_copy(out=sn[:, :, 1], in_=s1[:]), ["Ls", "sn"], ["sn"])
        tA = pp4[:, 0:2].rearrange("p a j (b c) -> p (a j b) c", c=64); tB = big[:, 6656:7680].rearrange("p (a c) -> p a c", c=64)
        m = 1
        while m < 128:
            cm = cs[:, :, m:m + 1].broadcast_to([128, 16, m]); sm = sn[:, :, m:m + 1].broadcast_to([128, 16, m])
            a_c = cs[:, :, 1:m + 1]; a_s = sn[:, :, 1:m + 1]
            o_c = cs[:, :, m + 1:2 * m + 1]; o_s = sn[:, :, m + 1:2 * m + 1]
            ta = tA[:, :, 0:m]; tb = tB[:, :, 0:m]
            op("dve", lambda h, a_c=a_c, cm=cm, ta=ta: h.tensor_tensor(out=ta, in0=a_c, in1=cm, op=ALU.mult), ["cs", "sn"], ["tA"])
            op("dve", lambda h, a_s=a_s, sm=sm, tb=tb: h.tensor_tensor(out=tb, in0=a_s, in1=sm, op=ALU.mult), ["cs", "sn"], ["tB"])
            op("dve", lambda h, o_c=o_c, ta=ta, tb=tb: h.tensor_tensor(out=o_c, in0=ta, in1=tb, op=ALU.subtract), ["tA", "tB", "sn"], ["cs"])
            op("dve", lambda h, a_c=a_c, sm=sm, ta=ta: h.tensor_tensor(out=ta, in0=a_c, in1=sm, op=ALU.mult), ["cs", "sn"], ["tA"])
            op("dve", lambda h, a_s=a_s, cm=cm, tb=tb: h.tensor_tensor(out=tb, in0=a_s, in1=cm, op=ALU.mult), ["cs", "sn"], ["tB"])
            op("dve", lambda h, o_s=o_s, ta=ta, tb=tb: h.tensor_tensor(out=o_s, in0=ta, in1=tb, op=ALU.add), ["tA", "tB", "cs"], ["sn"])
            m *= 2
        op("pool", lambda h: h.memset(mask64[:], 1.0))
        op("pool", lambda h: h.memset(mask64[:].rearrange("p (s t) -> p s t", t=4)[:, :, 0:1], 0.0))
        sm_ = [big[:, 9200 + 16 * i:9216 + 16 * i] for i in range(8)]
        lr_, li_ = lam[:, 0, :], lam[:, 1, :]
        nr, den, t_a, t_b, gr, gi = sm_[0], sm_[1], sm_[2], sm_[3], sm_[4], sm_[5]
        TT = lambda o, x, y, f_: op("dve", lambda h: h.tensor_tensor(out=o, in0=x, in1=y, op=f_))
        op("dve", lambda h: h.tensor_scalar(out=nr, in0=are[:], scalar1=-1.0, scalar2=None, op0=ALU.add))
        TT(den, lr_, lr_, ALU.mult); TT(t_a, li_, li_, ALU.mult); TT(den, den, t_a, ALU.add)
        op("dve", lambda h: h.reciprocal(out=den, in_=den))
        TT(t_a, nr, lr_, ALU.mult); TT(t_b, aim[:], li_, ALU.mult); TT(t_a, t_a, t_b, ALU.add); TT(fre[:], t_a, den, ALU.mult)
        TT(t_a, aim[:], lr_, ALU.mult); TT(t_b, nr, li_, ALU.mult); TT(t_a, t_a, t_b, ALU.subtract); TT(fim[:], t_a, den, ALU.mult)
        TT(den, fre[:], fre[:], ALU.mult); TT(t_a, fim[:], fim[:], ALU.mult); TT(den, den, t_a, ALU.add)
        op("dve", lambda h: h.reciprocal(out=den, in_=den))
        TT(gr, fre[:], den, ALU.mult); TT(gi, fim[:], den, ALU.mult)
        op("dve", lambda h: h.tensor_scalar(out=gi, in0=gi, scalar1=-1.0, scalar2=None, op0=ALU.mult))
        P.dma("pool", lambda h: h.dma_start(out=WBre[:], in_=bblk_d[:, 0]))
        P.dma("pool", lambda h: h.dma_start(out=WBim[:], in_=bblk_d[:, 1]))
        cbf = big[:, 0:4096].rearrange("p (a q m) -> p a q m", a=2, q=16)
        u1 = big[:, 4096:6144].rearrange("p (q m) -> p q m", q=16); u2 = big[:, 6144:8192].rearrange("p (q m) -> p q m", q=16)
        P.dma("sp", lambda h: h.dma_start(out=cbf, in_=cblk_d))
        fre_b = fre[:].unsqueeze(2).broadcast_to([128, 16, 128]); fim_b = fim[:].unsqueeze(2).broadcast_to([128, 16, 128])
        TT(u1, cbf[:, 0], fre_b, ALU.mult); TT(u2, cbf[:, 1], fim_b, ALU.mult); TT(WCre[:], u1, u2, ALU.subtract)
        TT(u1, cbf[:, 0], fim_b, ALU.mult); TT(u2, cbf[:, 1], fre_b, ALU.mult); TT(u1, u1, u2, ALU.add)
        op("dve", lambda h: h.tensor_scalar(out=WCimn[:], in0=u1, scalar1=-1.0, scalar2=None, op0=ALU.mult))

        tD = big[:, 9344:9600].rearrange("p (q s) -> p q s", q=16)
        tE = big[:, 8704:8960].rearrange("p (q s) -> p q s", q=16)
        gr_b = gr.unsqueeze(2).broadcast_to([128, 16, NSEQ]); gi_b = gi.unsqueeze(2).broadcast_to([128, 16, NSEQ])
        TT(tD, h0[:, 0], gr_b, ALU.mult); TT(tE, h0[:, 1], gi_b, ALU.mult); TT(tD, tD, tE, ALU.subtract)
        TT(tE, h0[:, 0], gi_b, ALU.mult); TT(h0[:, 0], tD, tD, ALU.max)
        TT(tD, h0[:, 1], gr_b, ALU.mult); TT(h0[:, 1], tE, tD, ALU.add)
        a_re_b = are[:].unsqueeze(2).broadcast_to([128, 16, NSEQ]); a_im_b = aim[:].unsqueeze(2).broadcast_to([128, 16, NSEQ])
        tC = big[:, 8704:8960].rearrange("p (q s) -> p q s", q=16)
        op("dve", lambda h: h.tensor_tensor(out=ah[:, 0], in0=h0[:, 0], in1=a_re_b, op=ALU.mult), ["h0", "are"], ["ah0"])
        op("dve", lambda h: h.tensor_tensor(out=tC[:], in0=h0[:, 1], in1=a_im_b, op=ALU.mult), ["h0", "aim"], ["tC"])
        op("dve", lambda h: h.tensor_tensor(out=ah[:, 0], in0=ah[:, 0], in1=tC[:], op=ALU.subtract), ["ah0", "tC"], ["ah0"])
        op("dve", lambda h: h.tensor_tensor(out=ah[:, 1], in0=h0[:, 0], in1=a_im_b, op=ALU.mult), ["h0", "aim"], ["ah1"])
        op("dve", lambda h: h.tensor_tensor(out=tC[:], in0=h0[:, 1], in1=a_re_b, op=ALU.mult), ["h0", "are", "ah0"], ["tC"])
        op("dve", lambda h: h.tensor_tensor(out=ah[:, 1], in0=ah[:, 1], in1=tC[:], op=ALU.add), ["ah1", "tC"], ["ah1"])


        def rms(src_ap, key_src, n, slot, scale, pn, junk=None):
            junk = xn if junk is None else junk
            op("act", lambda h: h.activation(out=junk[0:pn, 0:n], in_=src_ap, func=AF.Square, accum_out=ssq[0:pn, slot:slot + 1]),
               [key_src], ["junk", "ssq%d" % slot])
            op("act", lambda h: h.activation(out=rstd[0:pn, slot:slot + 1], in_=ssq[0:pn, slot:slot + 1], func=AF.Ln, scale=scale, bias=epsb[0:pn, 0:1]))
            op("act", lambda h: h.activation(out=rstd[0:pn, slot:slot + 1], in_=rstd[0:pn, slot:slot + 1], func=AF.Exp, scale=-0.5))

        def partA(ti, tl):
            sample = (ti == NPT)
            xt = xb[:, tl, :]
            n = 128
            ns = NS if sample else 128
            r0 = ti * 128
            par = ti % 2
            if sample:
                op("pool", lambda h: h.memset(kTs[:], 0.0))
                P.dma("pool", lambda h: h.dma_start(out=kTs[0:64, 0, 0:NSEQ * 128], in_=kcT_d[0:64, :]))
                P.dma("pool", lambda h: h.dma_start(out=kTs[64:128, 1, 0:NSEQ * 128], in_=kcT_d[64:128, :]))
                P.dma("pool", lambda h: h.dma_start(out=vs[:, 0:NSEQ, :], in_=vc_d))
                P.dma("sp", lambda h: h.dma_start(out=cvst, in_=cvst_d))
            rms(xt[:], "xt", D, 0, 1.0 / D, 128)
            op("dve", lambda h: h.tensor_scalar(out=xn[:], in0=xt[:], scalar1=rstd[:, 0:1], scalar2=None, op0=ALU.mult))
            for c in range(8):
                op("pe", lambda h, c=c: h.transpose(out=pT[:, c, :], in_=xn[:, c * 128:(c + 1) * 128], identity=idb[:]))
            op("dve", lambda h: h.tensor_tensor(out=xnT[:], in0=pT[:], in1=pcol[:, GMIX:GMIX + 8].unsqueeze(2).broadcast_to([128, 8, 128]), op=ALU.mult))
            W = wi[par]
            yield
            for i in range(4):
                bank = acc0 if i % 2 == 0 else acc1
                for c in range(8):
                    op("pe", lambda h, i=i, c=c, bank=bank: h.matmul(bank[:, 0:128], lhsT=W[:, c, i * 128:(i + 1) * 128], rhs=xnT[:, c, :],
                                                                   start=(c == 0), stop=(c == 7)))
                op("act", lambda h, i=i, bank=bank: h.activation(out=qT[:, i, :], in_=bank[:, 0:128], func=AF.Copy))
            yield
            for c in range(8):
                op("pe", lambda h, c=c: h.matmul(Ops[:, 0:128], lhsT=W[:, c, 512:640], rhs=xnT[:, c, :], start=(c == 0), stop=(c == 7)))
            if sample:
                op("dve", lambda h: h.tensor_copy(out=kTs[0:64, 0, NSEQ * 128:17 * 128], in_=Ops[0:64, 0:128]))
                op("dve", lambda h: h.tensor_copy(out=kTs[64:128, 1, NSEQ * 128:17 * 128], in_=Ops[64:128, 0:128]))
            else:
                op("dve", lambda h: h.tensor_copy(out=kTx[0:64, 0, 128:256], in_=Ops[0:64, 0:128]))
                op("dve", lambda h: h.tensor_copy(out=kTx[64:128, 1, 128:256], in_=Ops[64:128, 0:128]))
            yield
            for i in range(4):
                bank = acc0 if i % 2 == 0 else acc1
                for c in range(8):
                    op("pe", lambda h, i=i, c=c, bank=bank: h.matmul(bank[:, 0:128], lhsT=W[:, c, 768 + i * 128:768 + (i + 1) * 128], rhs=xnT[:, c, :],
                                                                   start=(c == 0), stop=(c == 7)))
                op("act", lambda h, i=i, bank=bank: h.activation(out=uT2[par][:, i, :], in_=bank[:, 0:128], func=AF.Copy))
            yield
            for c in range(8):
                op("pe", lambda h, c=c: h.matmul(Ops[:, 0:256], lhsT=xnT[:, c, :], rhs=W[:, c, 512:768], start=(c == 0), stop=(c == 7)))
            if sample:
                op("dve", lambda h: h.tensor_copy(out=vs[:, 16, :], in_=Ops[:, 128:256]))
            else:
                op("dve", lambda h: h.tensor_copy(out=vx[:, 1, :], in_=Ops[:, 128:256]))
            if sample or ti == NPT - 1:
                op("dve", lambda h: h.tensor_copy(out=kvtok[:], in_=Ops[:, 0:256]))
                if sample:
                    P.dma("sp", lambda h: h.dma_start(out=kvs_k[:, 124:128, :], in_=kvtok[0:NS, 0:128]))
                    P.dma("sp", lambda h: h.dma_start(out=kvs_v[:, 124:128, :], in_=kvtok[0:NS, 128:256]))
                else:
                    P.dma("sp", lambda h: h.dma_start(out=kvp_d, in_=kvtok[:]))

            if not sample:
                mi = 0 if ti == 0 else (1 if ti == 1 else 2)
                for i in range(4):
                    for hh_ in range(2):
                        op("pe", lambda h: h.matmul(Sps[:, hh_ * 256:(hh_ + 1) * 256], lhsT=qT[:, i, :], rhs=kTx[:, hh_, :], start=True, stop=False))
                        op("pe", lambda h: h.matmul(Sps[:, hh_ * 256:(hh_ + 1) * 256], lhsT=idb[:], rhs=maskb[:, mi, :], start=False, stop=True))
                    op("dve", lambda h: h.tensor_reduce(out=mx[:], in_=Sps[:].rearrange("p (a k) -> p a k", a=2), axis=AX.X, op=ALU.max))
                    op("dve", lambda h: h.tensor_tensor(out=mx[:], in0=mx[:], in1=sink8x[:, i, :], op=ALU.max))
                    op("dve", lambda h: h.tensor_scalar(out=nbias[:], in0=mx[:], scalar1=-0.125, scalar2=None, op0=ALU.mult))
                    for hh_ in range(2):
                        op("act", lambda h: h.activation(out=Pb[:, hh_, 0:256], in_=Sps[:, hh_ * 256:(hh_ + 1) * 256], func=AF.Exp, scale=0.125,
                                                         bias=nbias[:, hh_:hh_ + 1], accum_out=rs[:, hh_:hh_ + 1]))
                        op("act", lambda h: h.activation(out=esink[:, hh_:hh_ + 1], in_=sink8x[:, i, hh_:hh_ + 1], func=AF.Exp, scale=0.125,
                                                         bias=nbias[:, hh_:hh_ + 1]))
                    op("dve", lambda h: h.tensor_tensor(out=rs[:], in0=rs[:], in1=esink[:], op=ALU.add))
                    op("dve", lambda h: h.reciprocal(out=rinv[:], in_=rs[:]))
                    for hh_ in range(2):
                        for blk in range(2):
                            op("pe", lambda h, hh_=hh_, blk=blk: h.transpose(out=pT[:, hh_ * 2 + blk, :], in_=Pb[:, hh_, blk * 128:(blk + 1) * 128], identity=idb[:]))
                    op("act", lambda h: h.activation(out=PTs[:], in_=pT[:, 0:4, :], func=AF.Copy))
                    for hh_ in range(2):
                        for blk in range(2):
                            op("pe", lambda h, hh_=hh_, blk=blk: h.matmul(Ops[:, hh_ * 64:(hh_ + 1) * 64], lhsT=PTs[:, hh_ * 2 + blk, :],
                                                                         rhs=vx[:, blk, hh_ * 64:(hh_ + 1) * 64], start=(blk == 0), stop=(blk == 1)))
                    for hh_ in range(2):
                        hd = i + 4 * hh_
                        op("dve", lambda h, hh_=hh_, hd=hd: h.tensor_scalar(out=attn[:, hd * 64:(hd + 1) * 64], in0=Ops[:, hh_ * 64:(hh_ + 1) * 64],
                                                                           scalar1=rinv[:, hh_:hh_ + 1], scalar2=None, op0=ALU.mult))
                    yield
                op("pool", lambda h: h.tensor_copy(out=kTx[:, :, 0:128], in_=kTx[:, :, 128:256]))
                op("pool", lambda h: h.tensor_copy(out=vx[:, 0, :], in_=vx[:, 1, :]))
            else:
                W17 = 17 * 128
                for hd in range(8):
                    i, hh_ = hd % 4, hd // 4
                    for cb in range(5):
                        c0 = cb * 512
                        cw = min(512, W17 - c0)
                        bank = mmA if cb % 2 == 0 else mmB
                        op("pe", lambda h, i=i, hh_=hh_, c0=c0, cw=cw, bank=bank: h.matmul(bank[:, 0:cw], lhsT=qT[:, i, :], rhs=kTs[:, hh_, c0:c0 + cw], start=True, stop=True))
                        op("dve", lambda h, c0=c0, cw=cw, bank=bank: h.tensor_tensor(out=Ssx[:, c0:c0 + cw], in0=bank[:, 0:cw], in1=smask[:, c0:c0 + cw], op=ALU.add))
                    op("dve", lambda h, hd=hd: h.tensor_scalar(out=Ssx[:, W17:W17 + 1], in0=sink8[:, hd:hd + 1], scalar1=8.0, scalar2=None, op0=ALU.mult))
                    op("dve", lambda h: h.tensor_reduce(out=mx[:, 0:1], in_=Ssx[:], axis=AX.X, op=ALU.max))
                    op("dve", lambda h: h.tensor_scalar(out=nbias[:, 0:1], in0=mx[:, 0:1], scalar1=-0.125, scalar2=None, op0=ALU.mult))
                    op("act", lambda h: h.activation(out=Psb[:], in_=Ssx[:], func=AF.Exp, scale=0.125, bias=nbias[:, 0:1], accum_out=rs[:, 0:1]))
                    op("dve", lambda h: h.reciprocal(out=rinv[:, 0:1], in_=rs[:, 0:1]))
                    for g8 in range(3):
                        nb_ = 8 if g8 < 2 else 1
                        for b in range(nb_):
                            blk = g8 * 8 + b
                            op("pe", lambda h, b=b, blk=blk: h.transpose(out=pT[:, b, :], in_=Psb[:, blk * 128:(blk + 1) * 128], identity=idb[:]))
                        op("act", lambda h, g8=g8, nb_=nb_: h.activation(out=PTss[:, g8 * 8:g8 * 8 + nb_, :], in_=pT[:, 0:nb_, :], func=AF.Copy))
                    for blk in range(17):
                        op("pe", lambda h, blk=blk, hh_=hh_: h.matmul(Ops[:, 0:64], lhsT=PTss[:, blk, :], rhs=vs[:, blk, hh_ * 64:(hh_ + 1) * 64],
                                                                     start=(blk == 0), stop=(blk == 16)))
                    op("dve", lambda h, hd=hd: h.tensor_scalar(out=attn[:, hd * 64:(hd + 1) * 64], in0=Ops[:, 0:64], scalar1=rinv[:, 0:1], scalar2=None, op0=ALU.mult))
            rms(attn[0:n, :], "attn", 512, 1, 1.0 / 512, n)
            op("dve", lambda h: h.tensor_scalar(out=anb[0:n, :], in0=attn[0:n, :], scalar1=rstd[0:n, 1:2], scalar2=None, op0=ALU.mult), ["attn", "rstd1"], ["anb"])
            for c in range(4):
                op("pe", lambda h, c=c: h.transpose(out=pT[:, c, 0:n], in_=anb[0:n, c * 128:(c + 1) * 128], identity=idb[0:n, 0:n]), ["anb", "idb"], ["pT"])
            op("dve", lambda h: h.tensor_tensor(out=mixT2[par][:, 0:4, :], in0=pT[:, 0:4, :], in1=pcol[:, GATT:GATT + 4].unsqueeze(2).broadcast_to([128, 4, 128]), op=ALU.mult))

            yield

        def partB(ti, tl):
            sample = (ti == NPT)
            xt = xb[:, tl, :]
            n = 128
            ns = NS if sample else 128
            par = ti % 2
            uT = uT2[par]
            mixT = mixT2[par]
            TT = lambda o, x, y, f_: op("dve", lambda h: h.tensor_tensor(out=o, in0=x, in1=y, op=f_))

            def ssm_mm(c4):
                q0 = 4 * c4
                for ri, (WBx, bank) in enumerate(((WBre, mmA), (WBim, mmC))):
                    for j in range(4):
                        op("pe", lambda h: h.matmul(bank[:, j * 128:(j + 1) * 128], lhsT=WBx[:, q0 + j, :], rhs=uT[:, c4, :], start=True, stop=True))

            def ssm_batch(c4):
                q0 = 4 * c4
                T = [pp4[:, kk] for kk in range(4)]
                Bre = mmA[:, :].rearrange("p (j k) -> p j k", j=4); Bim = mmC[:, :].rearrange("p (j k) -> p j k", j=4)
                if sample:
                    op("act", lambda h: h.activation(out=vv4[:, 0], in_=Bre, func=AF.Copy))
                    op("act", lambda h: h.activation(out=vv4[:, 1], in_=Bim, func=AF.Copy))
                    for ri in range(2):
                        v4 = vv4[:, ri, :, 0:NS].rearrange("p j (s t) -> p j s t", t=4)[:, :, :, 0]
                        TT(v4, v4, ah[:, ri, q0:q0 + 4, :], ALU.add)
                    csq = cs[:, q0:q0 + 4, 1:5].unsqueeze(2).broadcast_to([128, 4, NSEQ, 4])
                    snq = sn[:, q0:q0 + 4, 1:5].unsqueeze(2).broadcast_to([128, 4, NSEQ, 4])
                    v3 = lambda ap: ap[:, :, 0:NS].rearrange("p j (s t) -> p j s t", t=4)
                    Bre, Bim = vv4[:, 0], vv4[:, 1]
                else:
                    csq = cs[:, q0:q0 + 4, 1:129]; snq = sn[:, q0:q0 + 4, 1:129]
                    v3 = lambda ap: ap
                Tv = [v3(t_) for t_ in T]
                RR = [v3(rr4[:, kk]) for kk in range(2)]
                TT(Tv[0], v3(Bre), csq, ALU.mult); TT(Tv[1], v3(Bim), snq, ALU.mult)
                TT(Tv[2], v3(Bim), csq, ALU.mult); TT(Tv[3], v3(Bre), snq, ALU.mult)
                if c4 + 1 < 4:
                    ssm_mm(c4 + 1)
                TT(RR[0], Tv[0], Tv[1], ALU.add); TT(RR[1], Tv[2], Tv[3], ALU.subtract)
                for j in range(4):
                    q = q0 + j
                    if sample:
                        op("dve", lambda h: h.tensor_scalar(out=dtmp[:], in0=mask64[:], scalar1=rho[:, q:q + 1], scalar2=None, op0=ALU.mult))
                    for ri in range(2):
                        if sample:
                            op("dve", lambda h: h.tensor_tensor_scan(out=vv4[:, ri, j, 0:NS], data0=dtmp[:], data1=rr4[:, ri, j, 0:NS], initial=0.0, op0=ALU.mult, op1=ALU.add))
                        else:
                            op("dve", lambda h: h.tensor_tensor_scan(out=vv4[:, ri, j, :], data0=rho[:, q:q + 1].broadcast_to([128, 128]), data1=rr4[:, ri, j, :],
                                                                   initial=car[:, ri, q:q + 1], op0=ALU.mult, op1=ALU.add))
                VV = [v3(vv4[:, kk]) for kk in range(2)]
                TT(Tv[0], VV[0], csq, ALU.mult); TT(Tv[1], VV[1], snq, ALU.mult)
                TT(Tv[2], VV[0], snq, ALU.mult); TT(Tv[3], VV[1], csq, ALU.mult)
                TT(v3(hb[:, :, 0, :]), Tv[0], Tv[1], ALU.subtract); TT(v3(hb[:, :, 1, :]), Tv[2], Tv[3], ALU.add)
                if sample:
                    TT(hfin[:, 0, q0:q0 + 4, 0:NSEQ], Tv[0][:, :, :, 3], Tv[1][:, :, :, 3], ALU.subtract)
                    TT(hfin[:, 1, q0:q0 + 4, 0:NSEQ], Tv[2][:, :, :, 3], Tv[3][:, :, :, 3], ALU.add)
                else:
                    TT(car[:, 0, q0:q0 + 4], Tv[0][:, :, 127], Tv[1][:, :, 127], ALU.subtract)
                    TT(car[:, 1, q0:q0 + 4], Tv[2][:, :, 127], Tv[3][:, :, 127], ALU.add)

            def ssm_y(c4):
                for qq in range(4):
                    q = c4 * 4 + qq
                    op("pe", lambda h: h.matmul(mmB[:, 0:n], lhsT=WCre[:, q, :], rhs=hb[:, qq, 0, 0:n], start=(qq == 0), stop=False))
                    op("pe", lambda h: h.matmul(mmB[:, 0:n], lhsT=WCimn[:, q, :], rhs=hb[:, qq, 1, 0:n], start=False, stop=(qq == 3)))
                op("dve", lambda h: h.scalar_tensor_tensor(out=yv[:, 0:n], in0=uT[:, c4, 0:n], scalar=col(DSK + c4), in1=mmB[:, 0:n], op0=ALU.mult, op1=ALU.add))
                op("act", lambda h: h.activation(out=gl32[:, c4, 0:n], in_=yv[:, 0:n], func=AF.Gelu))
                op("pool", lambda h: h.tensor_copy(out=glb[:, c4, 0:n], in_=gl32[:, c4, 0:n]))

            ssm_mm(0)
            yield
            for c4_ in range(4):
                ssm_batch(c4_)
                yield
                ssm_y(c4_)
                yield
            if ti == NPT - 1:
                op("act", lambda h: h.activation(out=hfin[:, :, :, NSEQ], in_=car[:], func=AF.Copy), ["car"], ["hfin"])
            for oc in range(4):
                for c4 in range(4):
                    op("pe", lambda h, oc=oc, c4=c4: h.matmul(mmC[:, 0:n], lhsT=wgl[0][:, c4, oc * 128:(oc + 1) * 128], rhs=glb[:, c4, 0:n], start=(c4 == 0), stop=(c4 == 3)),
                       ["glb", "wgl"], ["mmC"])
                op("act", lambda h, oc=oc: h.activation(out=sg[:, 0:n], in_=mmC[:, 0:n], func=AF.Sigmoid, bias=col(BGLU + oc)), ["mmC", "pcol"], ["sg"])
                op("dve", lambda h, oc=oc: h.tensor_tensor(out=gl32[:, oc, 0:n], in0=gl32[:, oc, 0:n], in1=sg[:, 0:n], op=ALU.mult), ["gl32", "sg", "glb"], ["gl32"])
                op("act", lambda h, oc=oc: h.activation(out=sqb[:, oc, 0:n], in_=gl32[:, oc, 0:n], func=AF.Square), ["gl32"], ["sqb"])
            for oc in range(4):
                op("pe", lambda h, oc=oc: h.matmul(mmC[:, 0:n], lhsT=onesb[:], rhs=sqb[:, oc, 0:n], start=(oc == 0), stop=(oc == 3)), ["sqb", "onesb"], ["mmC"])
            op("act", lambda h: h.activation(out=rsb[:, 0:n], in_=mmC[:, 0:n], func=AF.Ln, scale=1.0 / 512, bias=epsb[:, 0:1]))
            op("act", lambda h: h.activation(out=rsb[:, 0:n], in_=rsb[:, 0:n], func=AF.Exp, scale=-0.5))
            for oc in range(4):
                op("dve", lambda h, oc=oc: h.scalar_tensor_tensor(out=mixT[:, 4 + oc, 0:n], in0=gl32[:, oc, 0:n], scalar=col(GSSM + oc), in1=rsb[:, 0:n], op0=ALU.mult, op1=ALU.mult),
                   ["gl32", "rsb", "pcol"], ["mixT"])

            yield "join"
            for hf in range(2):
                acc, ak = (acc0, "acc0") if hf == 0 else (acc1, "acc1")
                for c in range(8):
                    op("pe", lambda h, hf=hf, c=c, acc=acc: h.matmul(acc[0:n, :], lhsT=mixT[:, c, 0:n], rhs=wo[0][:, c, hf * 512:(hf + 1) * 512], start=(c == 0), stop=(c == 7)),
                       ["mixT", "wo"], [ak])
                op("dve", lambda h, hf=hf, acc=acc: h.tensor_tensor(out=xt[0:n, hf * 512:(hf + 1) * 512], in0=xt[0:n, hf * 512:(hf + 1) * 512], in1=acc[0:n, :], op=ALU.add),
                   ["xt", ak], ["xt"])
            rms(xt[0:n, :], "xt", D, 2, 1.0 / D, n)
            op("dve", lambda h: h.tensor_scalar(out=xn[0:n, :], in0=xt[0:n, :], scalar1=rstd[0:n, 2:3], scalar2=None, op0=ALU.mult), ["xt", "rstd2"], ["xn"])
            for c in range(8):
                op("pe", lambda h, c=c: h.transpose(out=pT[:, c, 0:n], in_=xn[0:n, c * 128:(c + 1) * 128], identity=idb[0:n, 0:n]), ["xn", "idb"], ["pT"])
            op("dve", lambda h: h.tensor_tensor(out=xn2T[:, :, tl * 128:(tl + 1) * 128], in0=pT[:], in1=pcol[:, GFFN:GFFN + 8].unsqueeze(2).broadcast_to([128, 8, 128]), op=ALU.mult))

        def ffn(tiles_):
            sample = (tiles_[0] == NPT)
            ntl = len(tiles_)
            nb = ntl * 128
            hTv = hTsm if (sample and ntl == 1) else hTall
            p0 = 128 if sample else 0
            npc = nb - p0
            for ch in range(NFC):
                b3 = ch % 2
                w3 = ch % 3
                P.dma("sp", lambda h: h.dma_start(out=wg[w3][:], in_=wgb_d[ch]))
                P.dma("sp", lambda h: h.dma_start(out=wu[w3][:], in_=wub_d[ch]))
                gps, ups = (mmA, mmB) if b3 == 0 else (mmC, pT32)
                for c in range(8):
                    op("pe", lambda h, c=c, b3=b3, gps=gps: h.matmul(gps[:, 0:nb], lhsT=wg[w3][:, c, :], rhs=xn2T[:, c, 0:nb], start=(c == 0), stop=(c == 7)))
                for c in range(8):
                    op("pe", lambda h, c=c, b3=b3, ups=ups: h.matmul(ups[:, 0:nb], lhsT=wu[w3][:, c, :], rhs=xn2T[:, c, 0:nb], start=(c == 0), stop=(c == 7)))
                w0, w1, w2, bb = col(CVW + ch * 3), col(CVW + ch * 3 + 1), col(CVW + ch * 3 + 2), col(CVB + ch)
                cvb, slb, gxb = cv[:, 0, :], sl[:, 0, :], gx[:, b3, :]
                def conv3(g0, g1, g2, cvv):
                    op("dve", lambda h: h.tensor_scalar(out=cvv, in0=g2, scalar1=w2, scalar2=bb, op0=ALU.mult, op1=ALU.add))
                    op("dve", lambda h: h.scalar_tensor_tensor(out=cvv, in0=g1, scalar=w1, in1=cvv, op0=ALU.mult, op1=ALU.add))
                    op("dve", lambda h: h.scalar_tensor_tensor(out=cvv, in0=g0, scalar=w0, in1=cvv, op0=ALU.mult, op1=ALU.add))

                if sample:
                    op("pool", lambda h: h.tensor_copy(out=gxs[:, :, 0:2], in_=cvst[:, ch, :, :]))
                    op("act", lambda h: h.activation(out=gxs[:, :, 2:6], in_=gps[:, 0:NS].rearrange("p (s t) -> p s t", t=4), func=AF.Copy))
                    op("pool", lambda h: h.tensor_copy(out=sconv[:, ch, :, :], in_=gxs[:, :, 4:6]))
                    conv3(gxs[:, :, 0:4], gxs[:, :, 1:5], gxs[:, :, 2:6], cvb[:, 0:NS].rearrange("p (s t) -> p s t", t=4))
                if npc:
                    op("pool", lambda h: h.tensor_copy(out=gxb[:, 0:2], in_=gcar[:, ch, :]))
                    op("act", lambda h: h.activation(out=gxb[:, 2:2 + npc], in_=gps[:, p0:nb], func=AF.Copy))
                    op("pool", lambda h: h.tensor_copy(out=gcar[:, ch, :], in_=gxb[:, npc:npc + 2]))
                    conv3(gxb[:, 0:npc], gxb[:, 1:1 + npc], gxb[:, 2:2 + npc], cvb[:, p0:nb])
                op("act", lambda h, cvb=cvb, slb=slb: h.activation(out=slb[:, 0:nb], in_=cvb[:, 0:nb], func=AF.Silu))
                op("dve", lambda h, ch=ch, slb=slb: h.tensor_tensor(out=hTv[:, ch, 0:nb], in0=slb[:, 0:nb], in1=ups[:, 0:nb], op=ALU.mult))
            accs = [acc0, acc1, Sps, Ops]
            for hf in range(2):
                for g in range(NFC // 2):
                    wdb = wd[g % 3]
                    P.dma("sp", lambda h: h.dma_start(out=wdb, in_=wdb_d[hf, g]))
                    for j in range(2):
                        ch = 2 * g + j
                        for tl in range(ntl):
                            op("pe", lambda h: h.matmul(accs[tl][:, :], lhsT=hTv[:, ch, tl * 128:(tl + 1) * 128], rhs=wdb[:, j, :],
                                                        start=(ch == 0), stop=(ch == NFC - 1)))
                for tl in range(ntl):
                    op("dve", lambda h, tl=tl, hf=hf: h.tensor_tensor(out=xb[:, tl, hf * 512:(hf + 1) * 512], in0=xb[:, tl, hf * 512:(hf + 1) * 512], in1=accs[tl][:, :], op=ALU.add))

        def ffn_tail(tiles_, nxt_, first, last):
            for tl in range(first, last):
                ti = tiles_[tl]
                xt = xb[:, tl, :]
                rms(xt, "xt", D, 3, 1.0 / D, 128, junk=gl32[:].rearrange("p a k -> p (a k)").bitcast(BF16))
                op("dve", lambda h: h.scalar_tensor_tensor(out=xt, in0=xt, scalar=rstd[:, 3:4], in1=gfb[:], op0=ALU.mult, op1=ALU.mult))
                P.dma("sp", lambda h: h.dma_start(out=y_d[ti * 128:(ti + 1) * 128, :], in_=xt))
                if nxt_ is not None and tl < len(nxt_) and (tl == 0 or NPT not in nxt_):
                    load_x(tl, nxt_[tl])
                yield

        def est_dur(kind, eng, fn):
            rec = _Recorder()
            fn(rec)
            name, args, kwargs = rec.call
            o = kwargs.get("out", args[0] if args else None)
            nfree = 1
            if o is not None and hasattr(o, "shape"):
                for d_ in list(o.shape)[1:]:
                    nfree *= int(d_)
            if kind == "dma":
                return 100.0, 2500.0
            if eng == "dve":
                d = ((2 * nfree if name == "tensor_tensor_scan" else nfree) + 151) / 0.96
            elif eng == "act":
                d = (nfree + 230) / 1.2 + (90 if kwargs.get("accum_out") is not None else 0)
            elif eng == "pool":
                d = (2 * nfree + 250) / 1.2
            else:
                d = max(64, nfree) / 1.9 + 25
            return d, d

        def merge_threads(gens):
            gens = [g for g in gens if g is not None]
            bufs = [[] for _ in gens]
            alive = [True] * len(gens)
            tchain = [sched_now[0]] * len(gens)
            joined = [False] * len(gens)
            while True:
                for i, g in enumerate(gens):
                    while alive[i] and not bufs[i] and not joined[i]:
                        P.capture = bufs[i]
                        r_ = next(g, "done")
                        P.capture = None
                        if r_ == "done":
                            alive[i] = False
                        elif r_ == "join":
                            joined[i] = True
                for i in range(len(gens)):
                    if joined[i] and not bufs[i] and not any((alive[j] or bufs[j]) for j in range(len(gens)) if j != i):
                        joined[i] = False
                if not any(bufs) and any(alive):
                    continue
                cands = [i for i in range(len(gens)) if bufs[i]]
                if not cands:
                    break
                best, best_t = None, None
                for i in cands:
                    kind, eng, fn = bufs[i][0]
                    t = max(eng_free.get(eng, 0.0), tchain[i] + 250.0)
                    if best is None or t < best_t:
                        best, best_t = i, t
                kind, eng, fn = bufs[best].pop(0)
                busy, lat = est_dur(kind, eng, fn)
                eng_free[eng] = best_t + busy
                tchain[best] = best_t + lat
                sched_now[0] = max(sched_now[0], best_t)
                (P.dma if kind == "dma" else P.op)(eng, fn)

        eng_free = {}
        ffn_cast_issued = [False]
        sched_now = [0.0]
        if tiles is None:
            blocks = [[0, 1, 2, 3], [4, 5, 6, 7], [8, 9, 10, 11], [12, 13, 14, 15], [NPT, 16]]
        else:
            blocks = tiles
        loaded = set()

        def load_x(tl, ti):
            if ti not in loaded:
                loaded.add(ti)
                P.dma("sp", lambda h: h.dma_start(out=xb[:, tl, :], in_=xin[ti * 128:(ti + 1) * 128, :]))

        pre_done = False
        for bi, blk_ in enumerate(blocks):
            lazy = NPT in blk_
            for tl, ti in enumerate(blk_):
                if tl == 0 or not lazy:
                    load_x(tl, ti)
            if not pre_done:
                for _ in partA(blk_[0], 0):
                    pass
            if not ffn_cast_issued[0]:
                ffn_cast_issued[0] = True
                for a_ in range(0, NFC, 11):
                    P.dma("pool", lambda h: h.dma_start(out=wgb_d[a_:a_ + 11], in_=w_gate[a_:a_ + 11]))
                    P.dma("pool", lambda h: h.dma_start(out=wub_d[a_:a_ + 11], in_=w_up[a_:a_ + 11]))
                for hf_ in range(2):
                    P.dma("pool", lambda h: h.dma_start(out=wdb_d[hf_], in_=w_down[hf_]))
            for tl, ti in enumerate(blk_):
                gb = partB(ti, tl)
                ga = partA(blk_[tl + 1], tl + 1) if tl + 1 < len(blk_) else None
                if ga is not None and lazy:
                    load_x(tl + 1, blk_[tl + 1])
                merge_threads([gb, ga])
            ffn(blk_)
            nxt = blocks[bi + 1] if bi + 1 < len(blocks) else None
            for _ in ffn_tail(blk_, nxt, 0, 1):
                pass
            if nxt is not None and NPT not in nxt:
                merge_threads([ffn_tail(blk_, nxt, 1, len(blk_)), partA(nxt[0], 0)])
                pre_done = True
            else:
                for _ in ffn_tail(blk_, nxt, 1, len(blk_)):
                    pass
                pre_done = False

        fv = pp4[:].rearrange("p a j k -> p (a j k)")
        v1 = fv[:, 0:272].rearrange("p (q s) -> p q s", q=16); v2 = fv[:, 272:544].rearrange("p (q s) -> p q s", q=16); v3_ = fv[:, 544:816].rearrange("p (q s) -> p q s", q=16)
        fre_c = fre[:].unsqueeze(2).broadcast_to([128, 16, NSEQ + 1]); fim_c = fim[:].unsqueeze(2).broadcast_to([128, 16, NSEQ + 1])
        TT(v1, hfin[:, 0], fre_c, ALU.mult); TT(v2, hfin[:, 1], fim_c, ALU.mult); TT(v3_, v1, v2, ALU.subtract)
        TT(v1, hfin[:, 0], fim_c, ALU.mult); TT(v2, hfin[:, 1], fre_c, ALU.mult); TT(hfin[:, 1], v1, v2, ALU.add)
        op("dve", lambda h: h.tensor_copy(out=hfin[:, 0], in_=v3_))
        P.dma("sp", lambda h: h.dma_start(out=hfin_d, in_=hfin[:]), reads=["hfin"], writes=["hfin_d"])
        P.dma("sp", lambda h: h.dma_start(out=pconv_d, in_=gcar[:]), reads=["gcar"], writes=["pconv_d"])
        P.dma("sp", lambda h: h.dma_start(out=sconv_d, in_=sconv[:]), reads=["sconv"], writes=["sconv_d"])
        P.limit = None
        P.barrier()

        with nc.Block() as block:
            @block.tensor
            def _(h):
                P.run("pe", h, sems)

            @block.scalar
            def _(h):
                P.run("act", h, sems)

            @block.vector
            def _(h):
                P.run("dve", h, sems)

            @block.gpsimd
            def _(h):
                P.run("pool", h, sems)

            @block.sync
            def _(h):
                P.run("sp", h, sems)
    return nc


def _consts():
    ident = np.eye(128, dtype=np.float32)
    i = np.arange(128)[:, None]
    c = np.arange(256)[None, :]
    full = np.where(((c < 128) & (c > i)) | ((c >= 128) & (c - 128 <= i)), 0.0, MASKV)
    m1 = np.where(((c < 128) & (c > i) & (c >= NPAD)) | ((c >= 128) & (c - 128 <= i)), 0.0, MASKV)
    m0 = np.where((c >= 128) & (c - 128 <= i) & (c - 128 >= NPAD), 0.0, MASKV)
    masks = np.stack([m0, m1, full], axis=1).astype(np.float32)
    sm = np.full((128, 17 * 128), MASKV, np.float32)
    for s in range(NSEQ):
        for t in range(4):
            r = s * 4 + t
            sm[r, s * 128 + t + 1:(s + 1) * 128] = 0.0
            sm[r, 2048 + s * 4:2048 + s * 4 + t + 1] = 0.0
    return ident, masks, sm


def prep_inputs(x_prompt, x_sample, cache_k_win, cache_v_win, state_ssm_re, state_ssm_im, state_conv,
           meta_tokens, g_mix, w_in, sinks, lam_re, lam_im, log_dt, b_re, b_im, c_re, c_im, d_skip,
           w_glu, b_glu, g_attn_out, g_ssm_out, w_o, g_ffn, w_gate, w_up, conv_w, conv_b, w_down,
           g_final):
    f32 = np.float32
    A = lambda a: np.ascontiguousarray(np.asarray(a, dtype=f32))
    x_prompt, x_sample = A(x_prompt), A(x_sample)
    ident, masks, smask = _consts()
    w_in0 = A(w_in)[0]
    perm = []
    for i in range(4):
        perm += list(range(i * 64, (i + 1) * 64)) + list(range((4 + i) * 64, (5 + i) * 64))
    w_in_p = np.ascontiguousarray(np.concatenate([w_in0[:, perm], w_in0[:, 512:]], axis=1))
    pcol = np.zeros((128, 128), f32)
    pcol[:, 0:8] = A(g_mix)[0].reshape(8, 128).T
    pcol[:, 8:16] = A(g_ffn)[0].reshape(8, 128).T
    pcol[:, 16:20] = A(g_attn_out)[0].reshape(4, 128).T
    pcol[:, 20:24] = A(g_ssm_out)[0].reshape(4, 128).T
    pcol[:, 24:28] = A(b_glu)[0].reshape(4, 128).T
    pcol[:, 28:32] = A(d_skip)[0].reshape(4, 128).T
    cw = A(conv_w)[0].reshape(3, NFC, 128)
    pcol[:, 32:98] = cw.transpose(2, 1, 0).reshape(128, 66)
    pcol[:, 98:120] = A(conv_b)[0].reshape(NFC, 128).T
    lr, li, ld = A(lam_re)[0], A(lam_im)[0], A(log_dt)[0]
    ldx = np.repeat(ld[:, None], 64, axis=1)

    def pl(a):
        return a.reshape(16, 2, 64).transpose(1, 2, 0).reshape(128, 16)

    lam = np.ascontiguousarray(np.stack([pl(lr), pl(li), pl(ldx)], axis=1))
    lamb = np.ascontiguousarray(np.stack([lr.reshape(-1), li.reshape(-1), ldx.reshape(-1)], axis=0))
    bre, bim, cre, cim = A(b_re)[0], A(b_im)[0], A(c_re)[0], A(c_im)[0]
    bblk = np.zeros((128, 2, 16, 128), f32)
    cblk = np.zeros((128, 2, 16, 128), f32)
    for q in range(16):
        for j2 in range(2):
            g = 2 * q + j2
            g8 = g % 8
            rows = slice(g8 * 16, g8 * 16 + 16)
            cols = slice(j2 * 64, j2 * 64 + 64)
            bblk[rows, 0, q, cols] = bre[g].T
            bblk[rows, 1, q, cols] = bim[g].T
            cblk[cols, 0, q, rows] = cre[g].T
            cblk[cols, 1, q, rows] = cim[g].T
    sre, sim_ = A(state_ssm_re)[0], A(state_ssm_im)[0]
    ck, cvv = A(cache_k_win)[0].reshape(128, 128, 128), A(cache_v_win)[0].reshape(128, 128, 128)
    sc = A(state_conv)[0]
    meta = A(meta_tokens)
    wg_l = np.ascontiguousarray(A(w_gate)[0].reshape(8, 128, NFC, 128).transpose(2, 1, 0, 3))
    wu_l = np.ascontiguousarray(A(w_up)[0].reshape(8, 128, NFC, 128).transpose(2, 1, 0, 3))
    wd_l = np.ascontiguousarray(A(w_down)[0].reshape(NFC // 2, 2, 128, 2, 512).transpose(3, 0, 2, 1, 4))
    in_maps = []
    for c in range(NCORES):
        xin = np.zeros((NPT * 128 + 128, D), f32)
        xin[NPAD:128] = meta
        xin[128:NPT * 128] = x_prompt[c]
        xin[NPT * 128:NPT * 128 + NS] = x_sample[c * NSEQ:(c + 1) * NSEQ].reshape(NS, D)
        sl_ = slice(c * NSEQ, (c + 1) * NSEQ)

        def hl(a):
            return a.reshape(NSEQ, 16, 2, 64).transpose(2, 3, 1, 0).reshape(128, 16, NSEQ)

        h0 = np.ascontiguousarray(np.stack([hl(sre[sl_]), hl(sim_[sl_])], axis=1))
        cvst = np.ascontiguousarray(sc[sl_].reshape(NSEQ, 2, NFC, 128).transpose(3, 2, 0, 1))
        kc, vc = ck[sl_], cvv[sl_]
        kcT = np.ascontiguousarray(kc.transpose(2, 0, 1).reshape(128, NSEQ * 128))
        vcl = np.ascontiguousarray(vc.transpose(1, 0, 2))
        in_maps.append(dict(
            xin=xin, w_in=w_in_p, w_o=A(w_o)[0], w_glu=A(w_glu)[0], w_gate=wg_l, w_up=wu_l,
            w_down=wd_l, ident=ident, masks=masks, smask=smask, pcol=pcol, sinks=A(sinks)[0],
            gfin=A(g_final), lam=lam, lamb=lamb, bblk=bblk, cblk=cblk, h0=h0, cvst=cvst, kcT=kcT, vc=vcl,
            kcache=np.ascontiguousarray(kc), vcache=np.ascontiguousarray(vc)))
    return in_maps


def assemble(R):
    f32 = np.float32
    y_prompt = np.stack([R[c]["y"][128:NPT * 128] for c in range(NCORES)])
    y_sample = np.concatenate([R[c]["y"][NPT * 128:NPT * 128 + NS].reshape(NSEQ, 4, D) for c in range(NCORES)])
    p_k = np.stack([R[c]["kvp"][:, 0:128].reshape(128, 2, 64) for c in range(NCORES)])[None]
    p_v = np.stack([R[c]["kvp"][:, 128:256].reshape(128, 2, 64) for c in range(NCORES)])[None]
    s_k = np.concatenate([R[c]["kvs_k"].reshape(NSEQ, 128, 2, 64) for c in range(NCORES)])[None]
    s_v = np.concatenate([R[c]["kvs_v"].reshape(NSEQ, 128, 2, 64) for c in range(NCORES)])[None]

    def unh(a):
        nn = a.shape[-1]
        return a.reshape(2, 64, 16, nn).transpose(3, 2, 0, 1).reshape(nn, 32, 64)

    p_re = np.stack([unh(R[c]["hfin"][:, 0, :, NSEQ:])[0] for c in range(NCORES)])[None]
    p_im = np.stack([unh(R[c]["hfin"][:, 1, :, NSEQ:])[0] for c in range(NCORES)])[None]
    s_re = np.concatenate([unh(R[c]["hfin"][:, 0, :, :NSEQ]) for c in range(NCORES)])[None]
    s_im = np.concatenate([unh(R[c]["hfin"][:, 1, :, :NSEQ]) for c in range(NCORES)])[None]
    p_conv = np.stack([R[c]["pconv"].transpose(2, 1, 0).reshape(2, FF) for c in range(NCORES)])[None]
    s_conv = np.concatenate([R[c]["sconv"].transpose(2, 3, 1, 0).reshape(NSEQ, 2, FF) for c in range(NCORES)])[None]
    out = (y_prompt, y_sample, p_k, p_v, p_re, p_im, p_conv, s_k, s_v, s_re, s_im, s_conv)
    return tuple(np.ascontiguousarray(o, dtype=f32) for o in out)


def kernel(**inputs):
    in_maps = prep_inputs(**inputs)
    nc = build_nc()
    res = run_bass_kernel_spmd(nc, in_maps, core_ids=list(range(NCORES)))
    return assemble(res.results)
```

```python
import numpy as np
from contextlib import ExitStack
import concourse.bass as bass
import concourse.mybir as mybir
from concourse.bass_utils import run_bass_kernel_spmd

F32 = mybir.dt.float32
BF16 = mybir.dt.bfloat16
ALU = mybir.AluOpType
AF = mybir.ActivationFunctionType
AX = mybir.AxisListType

ENGS = ("pe", "act", "dve", "pool", "sp")
NDMASEM = 12
NCORES = 8
D = 1024
NPT = 17
NPAD = 112
NS = 64
NSEQ = 16
FF = 2816
NFC = 22
EPS = 1e-5
PI = float(np.pi)
MASKV = -30000.0


class _Recorder:
    def __init__(self):
        self.call = None

    def __getattr__(self, name):
        def f(*args, **kwargs):
            self.call = (name, args, kwargs)
            return self
        return f


class Prog:
    def __init__(self):
        self.streams = {e: [] for e in ENGS}
        self.count = {e: 0 for e in ENGS}
        self.waited = {e: {} for e in ENGS}
        self.regs = {}
        self.nrec = 0
        self.limit = None
        self.capture = None
        self.dma_rr = {"sp": 0, "pool": 0}
        self.dma_cnt = {}

    @staticmethod
    def _region(ap):
        dims = [(int(st), int(sz)) for st, sz in ap.ap]
        off = int(ap.offset)
        esz = mybir.dt.size(ap.dtype)
        space = str(ap.space)
        if space == "DRAM":
            ext = sum((sz - 1) * abs(st) for st, sz in dims)
            return (0, 1, off, off + ext)
        pst, npart = dims[0]
        pst = max(pst, 1)
        p0, f0 = off // pst, off % pst
        ext = sum((sz - 1) * abs(st) for st, sz in dims[1:])
        f0, ext = f0 * esz, ext * esz + esz - 1
        if space == "PSUM":
            return (0, 128, 0, 1 << 30)
        return (p0, p0 + npart, f0, f0 + ext)

    @staticmethod
    def _is_ap(v):
        return hasattr(v, "tensor") and hasattr(v, "ap") and hasattr(v, "offset")

    def _record(self, fn):
        rec = _Recorder()
        fn(rec)
        name, args, kwargs = rec.call
        acc = []
        for i, a in enumerate(args):
            if self._is_ap(a):
                acc.append((a, i == 0))
        for k, v in kwargs.items():
            if self._is_ap(v):
                acc.append((v, k in ("out", "accum_out")))
        out = []
        for ap, w in acc:
            if str(ap.space) == "PSUM":
                w = True
            out.append((ap.tensor.name, self._region(ap), w))
        self._last_call = rec.call
        return out

    def _deps(self, eng, acc):
        deps = {}
        for name, R, w in acc:
            for (R2, w2), evs in self.regs.get(name, {}).items():
                if not (w or w2):
                    continue
                if R[0] < R2[1] and R2[0] < R[1] and R[2] <= R2[3] and R2[2] <= R[3]:
                    for s_, v in evs.items():
                        if s_ == eng and eng == "pe":
                            continue
                        if deps.get(s_, 0) < v:
                            deps[s_] = v
        out = []
        wd = self.waited[eng]
        for s_, v in deps.items():
            if wd.get(s_, 0) < v:
                wd[s_] = v
                out.append((s_, v))
        return out

    def _commit(self, ev, acc):
        s_, v = ev
        for name, R, w in acc:
            d = self.regs.setdefault(name, {})
            if w:
                for key in [k for k in d if k[0][0] >= R[0] and k[0][1] <= R[1] and k[0][2] >= R[2] and k[0][3] <= R[3]]:
                    del d[key]
            e = d.setdefault((R, w), {})
            if e.get(s_, 0) < v:
                e[s_] = v

    @staticmethod
    def _freeze(fn):
        rec = _Recorder()
        fn(rec)
        return lambda h, c=rec.call: getattr(h, c[0])(*c[1], **c[2])

    def op(self, eng, fn, reads=(), writes=()):
        if self.capture is not None:
            self.capture.append(("op", eng, self._freeze(fn)))
            return
        self.nrec += 1
        if self.limit is not None and self.nrec > self.limit:
            return
        acc = self._record(fn)
        deps = self._deps(eng, acc)
        self.count[eng] += 1
        ev = (eng, self.count[eng])
        self.streams[eng].append((deps, self._last_call, (eng, 1)))
        self._commit(ev, acc)

    def dma(self, eng, fn, reads=(), writes=()):
        if self.capture is not None:
            self.capture.append(("dma", eng, self._freeze(fn)))
            return
        self.nrec += 1
        if self.limit is not None and self.nrec > self.limit:
            return
        acc = self._record(fn)
        k = self.dma_rr[eng]
        self.dma_rr[eng] = (k + 1) % NDMASEM
        sname = "dma%s%d" % (eng, k)
        deps = self._deps(eng, acc)
        prev = self.dma_cnt.get(sname, 0) * 16
        if prev and self.waited[eng].get(sname, 0) < prev:
            self.waited[eng][sname] = prev
            deps.append((sname, prev))
        self.dma_cnt[sname] = self.dma_cnt.get(sname, 0) + 1
        ev = (sname, self.dma_cnt[sname] * 16)
        self.streams[eng].append((deps, self._last_call, (sname, 16)))
        self._commit(ev, acc)

    def barrier(self):
        evs = [(e, self.count[e]) for e in ENGS if self.count[e]]
        evs += [(k, c * 16) for k, c in self.dma_cnt.items()]
        for e in ENGS:
            deps = []
            for s, v in evs:
                if s == e:
                    continue
                if self.waited[e].get(s, 0) < v:
                    self.waited[e][s] = v
                    deps.append((s, v))
            if deps:
                self.streams[e].append((deps, None, None))

    def run(self, eng, h, sems):
        for deps, fn, inc in self.streams[eng]:
            for s, v in deps:
                h.wait_ge(sems[s], v)
            if fn is not None:
                name, args, kwargs = fn
                getattr(h, name)(*args, **kwargs).then_inc(sems[inc[0]], inc[1])


def build_nc(tiles=None, limit=None):
    nc = bass.Bass("TRN2", target_bir_lowering=False)

    def din(name, shape, dt=F32):
        return nc.dram_tensor(name, list(shape), dt, kind="ExternalInput").ap()

    def dout(name, shape, dt=F32):
        return nc.dram_tensor(name, list(shape), dt, kind="ExternalOutput").ap()

    xin = din("xin", [NPT * 128 + 128, D])
    w_in = din("w_in", [D, 1280])
    w_o = din("w_o", [D, D])
    w_glu = din("w_glu", [512, 512])
    w_gate = din("w_gate", [NFC, 128, 8, 128])
    w_up = din("w_up", [NFC, 128, 8, 128])
    w_down = din("w_down", [2, NFC // 2, 128, 2, 512])
    ident_d = din("ident", [128, 128])
    masks_d = din("masks", [128, 3, 256])
    smask_d = din("smask", [128, 17 * 128])
    pcol_d = din("pcol", [128, 128])
    sinks_d = din("sinks", [8])
    gfin_d = din("gfin", [D])
    lam_d = din("lam", [128, 3, 16])
    lamb_d = din("lamb", [3, 16 * 128])
    bblk_d = din("bblk", [128, 2, 16, 128])
    cblk_d = din("cblk", [128, 2, 16, 128])
    h0_d = din("h0", [128, 2, 16, NSEQ])
    cvst_d = din("cvst", [128, NFC, NSEQ, 2])
    kcT_d = din("kcT", [128, NSEQ * 128])
    vc_d = din("vc", [128, NSEQ, 128])
    kcache_d = din("kcache", [NSEQ, 128, 128])
    vcache_d = din("vcache", [NSEQ, 128, 128])

    wgb_d = nc.dram_tensor("wgb", [NFC, 128, 8, 128], BF16, kind="Internal").ap()
    wub_d = nc.dram_tensor("wub", [NFC, 128, 8, 128], BF16, kind="Internal").ap()
    wdb_d = nc.dram_tensor("wdb", [2, NFC // 2, 128, 2, 512], BF16, kind="Internal").ap()
    y_d = dout("y", [NPT * 128 + 128, D])
    kvp_d = dout("kvp", [128, 256])
    kvs_k = dout("kvs_k", [NSEQ, 128, 128])
    kvs_v = dout("kvs_v", [NSEQ, 128, 128])
    hfin_d = dout("hfin", [128, 2, 16, NSEQ + 1])
    pconv_d = dout("pconv", [128, NFC, 2])
    sconv_d = dout("sconv", [128, NFC, NSEQ, 2])

    P = Prog()
    P.limit = limit
    es = ExitStack()
    with es:
        def sb(name, shape, dt=F32):
            return es.enter_context(nc.sbuf_tensor(name, list(shape), dt))

        def ps(name, shape, dt=F32):
            return es.enter_context(nc.psum_tensor(name, list(shape), dt))

        sems = {e: es.enter_context(nc.semaphore("s_" + e)) for e in ENGS}
        for k in range(NDMASEM):
            for e_ in ("sp", "pool"):
                sems["dma%s%d" % (e_, k)] = es.enter_context(nc.semaphore("s_dma%s%d" % (e_, k)))

        idb = sb("idb", [128, 128], BF16)
        onesb = sb("onesb", [128, 128], BF16)
        masks = sb("masks_s", [128, 3, 256])
        smask = sb("smask_s", [128, 17 * 128], BF16)
        pcol = sb("pcol_s", [128, 128])
        sink8 = sb("sink8", [128, 8])
        gfb = sb("gfb", [128, D])
        epsb = sb("epsb", [128, 1])
        lam = sb("lam_s", [128, 3, 16])
        WBre = sb("WBre", [128, 16, 128], BF16); WBim = sb("WBim", [128, 16, 128], BF16)
        WCre = sb("WCre", [128, 16, 128], BF16); WCimn = sb("WCimn", [128, 16, 128], BF16)
        cs = sb("cs", [128, 16, 129]); sn = sb("sn", [128, 16, 129])
        mask64 = sb("mask64", [128, NS]); dtmp = sb("dtmp", [128, NS])
        are = sb("are", [128, 16]); aim = sb("aim", [128, 16])
        ah = sb("ah", [128, 2, 16, NSEQ])
        car = sb("car", [128, 2, 16])
        hfin = sb("hfin_s", [128, 2, 16, NSEQ + 1])
        gcar = sb("gcar", [128, NFC, 2])
        sconv = sb("sconv_s", [128, NFC, NSEQ, 2])
        kTx = sb("kTx", [128, 2, 256], BF16)
        vx = sb("vx", [128, 2, 128], BF16)
        GMIX, GFFN, GATT, GSSM, BGLU, DSK, CVW, CVB = 0, 8, 16, 20, 24, 28, 32, 98

        def col(c):
            return pcol[:, c:c + 1]

        ssq = sb("ssq", [128, 4]); rstd = sb("rstd", [128, 4])
        xn = sb("xn", [128, D], BF16)
        junk = xn
        xnT = sb("xnT", [128, 8, 128], BF16)
        qT = sb("qT", [128, 4, 128], BF16)
        uT2 = [sb("uT%d" % i, [128, 4, 128], BF16) for i in range(2)]
        Sx = sb("Sx", [128, 1, 2, 257]); mx = sb("mx", [128, 2]); nbias = sb("nbias", [128, 2])
        rs = sb("rs", [128, 2]); rinv = sb("rinv", [128, 2]); esink = sb("esink", [128, 2])
        maskb = sb("maskb", [128, 3, 256], BF16); sink8x = sb("sink8x", [128, 4, 2])
        dDhi = sb("dDhi", [128, 4, 128], BF16); dDlo = sb("dDlo", [128, 4, 128], BF16)
        Pb = sb("Pb", [128, 2, 257], BF16); PTs = sb("PTs", [128, 4, 128], BF16)
        mixT2 = [sb("mixT%d" % i, [128, 8, 128], BF16) for i in range(2)]
        mixT = mixT2[0]
        Psb = sb("Psb", [128, 17 * 128 + 1], BF16)
        PTss = sb("PTss", [128, 17, 128], BF16)
        pp4 = sb("pp4", [128, 4, 4, 128]); rr4 = sb("rr4", [128, 2, 4, 128]); vv4 = sb("vv4", [128, 2, 4, 128])
        hb = sb("hb", [128, 4, 2, 128], BF16)
        gl32 = sb("gl32", [128, 4, 128]); glb = sb("glb", [128, 4, 128], BF16)
        xn2T = sb("xn2T", [128, 8, 512], BF16)
        gx = sb("gx", [128, 2, 514]); gxs = sb("gxs", [128, NSEQ, 6])
        cv = sb("cv", [128, 1, 512]); sl = sb("sl", [128, 1, 512])
        kvtok = sl[:, 0, 0:256]
        yv = gx[:, 0, 0:128]; sg = gx[:, 0, 128:256]; rsb = gx[:, 0, 256:384]
        sqb = gx[:, 1, 0:256].bitcast(BF16).rearrange("p (a k) -> p a k", a=4)
        attn = cv[:, 0, :]
        anb = sl[:, 0, 256:512].bitcast(BF16)
        wi = [sb("wi0", [128, 8, 1280], BF16)] * 2
        wo = [sb("wo0", [128, 8, D], BF16)] * 2
        wgl = [sb("wgl0", [128, 4, 512], BF16)] * 2
        wg = [sb("wg%d" % i, [128, 8, 128], BF16) for i in range(2)] + [xnT]
        wu = [sb("wu%d" % i, [128, 8, 128], BF16) for i in range(2)] + [mixT]
        wdA = sb("wdA", [128, 2, 512], BF16)
        wd = [wdA[:], Sx[:].rearrange("p a b c -> p (a b c)").bitcast(BF16)[:, 0:1024].rearrange("p (j n) -> p j n", j=2),
              xn[:].rearrange("p (j n) -> p j n", j=2)]
        big = sb("big", [128, 9728])
        xb = big[:, 0:4096].rearrange("p (t d) -> p t d", t=4)
        hTall = big[:, 4096:9728].bitcast(BF16).rearrange("p (c n) -> p c n", c=NFC)
        hTsm = big[:, 4096:5504].bitcast(BF16).rearrange("p (c n) -> p c n", c=NFC)
        kTs = big[:, 5504:7680].bitcast(BF16).rearrange("p (a k) -> p a k", a=2)
        vs = big[:, 7680:8768].bitcast(BF16).rearrange("p (b k) -> p b k", b=17)
        scr = big
        h0 = big[:, 8192:8704].rearrange("p (a q s) -> p a q s", a=2, q=16)
        cvst = big[:, 3264:3968].rearrange("p (c s j) -> p c s j", c=NFC, s=NSEQ)
        def prep_views(o):
            return (scr[:, o:o + 768].rearrange("p (a b) -> p a b", a=3), scr[:, o + 768:o + 2304].rearrange("p (a b) -> p a b", a=6),
                    scr[:, o + 2304:o + 2816].rearrange("p (a q m) -> p a q m", a=2, q=2), scr[:, o + 2816:o + 3328].rearrange("p (a q m) -> p a q m", a=2, q=2))
        Ssx = big[:, 1024:1024 + 17 * 128 + 1]
        idf = big[:, 8960:9088]

        mmA = ps("mmA", [128, 512]); mmB = ps("mmB", [128, 512]); mmC = ps("mmC", [128, 512])
        pT = ps("pT", [128, 8, 128], BF16)
        pT32 = pT[:].rearrange("p c k -> p (c k)").bitcast(F32)
        Sps = ps("Sps", [128, 512]); Ops = ps("Ops", [128, 512])
        acc0 = ps("acc0", [128, 512]); acc1 = ps("acc1", [128, 512])

        op = P.op

        P.dma("sp", lambda h: h.dma_start(out=idf[:], in_=ident_d), writes=["idf"])
        P.dma("sp", lambda h: h.dma_start(out=masks[:], in_=masks_d), writes=["masks"])
        P.dma("pool", lambda h: h.dma_start(out=smask[:], in_=smask_d), writes=["smask"])
        P.dma("sp", lambda h: h.dma_start(out=pcol[:], in_=pcol_d), writes=["pcol"])
        P.dma("sp", lambda h: h.dma_start(out=sink8[:], in_=sinks_d.partition_broadcast(128)), writes=["sink8"])
        P.dma("sp", lambda h: h.dma_start(out=gfb[:], in_=gfin_d.partition_broadcast(128)), writes=["gfb"])
        P.dma("sp", lambda h: h.dma_start(out=lam[:], in_=lam_d), writes=["lam"])
        P.dma("sp", lambda h: h.dma_start(out=h0, in_=h0_d))
        op("pool", lambda h: h.memset(hb[:], 0.0))
        op("pool", lambda h: h.memset(cv[:], 0.0))
        op("pool", lambda h: h.memset(gx[:], 0.0))
        P.dma("pool", lambda h: h.dma_start(out=wi[0][:], in_=w_in.rearrange("(c p) n -> p c n", p=128)))
        P.dma("pool", lambda h: h.dma_start(out=wgl[0][:], in_=w_glu.rearrange("(c p) n -> p c n", p=128)))
        P.dma("pool", lambda h: h.dma_start(out=wo[0][:], in_=w_o.rearrange("(c p) n -> p c n", p=128)))
        P.dma("sp", lambda h: h.dma_start(out=kvs_k[:, 0:124, :], in_=kcache_d[:, 4:128, :]), writes=["kvs_k_a"])
        P.dma("sp", lambda h: h.dma_start(out=kvs_v[:, 0:124, :], in_=vcache_d[:, 4:128, :]), writes=["kvs_v_a"])

        op("dve", lambda h: h.tensor_copy(out=idb[:], in_=idf[:]), ["idf"], ["idb"])
        P.dma("pool", lambda h: h.dma_start(out=maskb[:], in_=masks_d))
        op("dve", lambda h: h.tensor_scalar(out=sink8x[:].rearrange("p i a -> p a i"), in0=sink8[:].rearrange("p (a i) -> p a i", a=2), scalar1=8.0, scalar2=None, op0=ALU.mult))
        op("pool", lambda h: h.memset(onesb[:], 1.0), [], ["onesb"])
        op("pool", lambda h: h.memset(epsb[:], EPS), [], ["epsb"])
        op("pool", lambda h: h.memset(kTx[:], 0.0), [], ["kTx"])
        op("pool", lambda h: h.memset(vx[:], 0.0), [], ["vx"])
        op("pool", lambda h: h.memset(car[:], 0.0), [], ["car"])
        op("pool", lambda h: h.memset(gcar[:], 0.0), [], ["gcar"])
        pass
        op("pool", lambda h: h.memset(hfin[:], 0.0))
        op("pool", lambda h: h.memset(sconv[:], 0.0))
        rho = sb("rho", [128, 16]); fre = sb("fre", [128, 16]); fim = sb("fim", [128, 16])
        dtl, thl, c1, s1, tmpa, kfL = (big[:, 9088 + 16 * i:9104 + 16 * i] for i in range(6))
        kiL = big[:, 9184:9200].bitcast(mybir.dt.int32)
        op("act", lambda h: h.activation(out=dtl[:], in_=lam[:, 2, :], func=AF.Exp), ["lam"], ["dtl"])
        op("dve", lambda h: h.tensor_tensor(out=thl[:], in0=lam[:, 1, :], in1=dtl[:], op=ALU.mult), ["lam", "dtl"], ["thl"])
        op("dve", lambda h: h.tensor_tensor(out=tmpa[:], in0=lam[:, 0, :], in1=dtl[:], op=ALU.mult), ["lam", "dtl"], ["tmpa"])
        op("act", lambda h: h.activation(out=rho[:], in_=tmpa[:], func=AF.Exp), ["tmpa"], ["rho"])

        def sincos(eng_tag, th_ap, s_ap, c_ap, tmp_ap, shape_key, ki_ap, kf_ap):
            K = shape_key

            def reduce_(shift, dst_key):
                op("dve", lambda h: h.tensor_scalar(out=tmp_ap, in0=th_ap, scalar1=shift, scalar2=1.0 / (2 * PI), op0=ALU.add, op1=ALU.mult),
                   [K + "th", K + "s", K + "c"], [K + "tmp"])
                op("dve", lambda h: h.tensor_copy(out=ki_ap, in_=tmp_ap), [K + "tmp"], [K + "ki"])
                op("dve", lambda h: h.tensor_copy(out=kf_ap, in_=ki_ap), [K + "ki"], [K + "kf"])
                op("dve", lambda h: h.tensor_scalar(out=tmp_ap, in0=th_ap, scalar1=shift, scalar2=None, op0=ALU.add), [K + "th", K + "ki"], [K + "tmp"])
                op("dve", lambda h: h.scalar_tensor_tensor(out=tmp_ap, in0=kf_ap, scalar=-2 * PI, in1=tmp_ap, op0=ALU.mult, op1=ALU.add),
                   [K + "kf", K + "tmp"], [K + "tmp"])
                op("dve", lambda h: h.tensor_scalar(out=kf_ap, in0=tmp_ap, scalar1=PI, scalar2=None, op0=ALU.is_gt), [K + "tmp"], [K + "kf"])
                op("dve", lambda h: h.scalar_tensor_tensor(out=tmp_ap, in0=kf_ap, scalar=-2 * PI, in1=tmp_ap, op0=ALU.mult, op1=ALU.add),
                   [K + "kf", K + "tmp"], [K + "tmp"])
                op("dve", lambda h: h.tensor_scalar(out=kf_ap, in0=tmp_ap, scalar1=-PI, scalar2=None, op0=ALU.is_lt), [K + "tmp"], [K + "kf"])
                op("dve", lambda h: h.scalar_tensor_tensor(out=tmp_ap, in0=kf_ap, scalar=2 * PI, in1=tmp_ap, op0=ALU.mult, op1=ALU.add),
                   [K + "kf", K + "tmp"], [K + "tmp"])
                op("dve", lambda h: h.tensor_scalar(out=tmp_ap, in0=tmp_ap, scalar1=-PI, scalar2=PI, op0=ALU.max, op1=ALU.min), [K + "tmp"], [K + "tmp"])

            reduce_(0.0, "s")
            op("act", lambda h: h.activation(out=s_ap, in_=tmp_ap, func=AF.Sin), [K + "tmp"], [K + "s"])
            reduce_(0.5 * PI, "c")
            op("act", lambda h: h.activation(out=c_ap, in_=tmp_ap, func=AF.Sin), [K + "tmp"], [K + "c"])

        sincos("L", thl[:], s1[:], c1[:], tmpa[:], "L", kiL[:], kfL[:])
        op("dve", lambda h: h.tensor_tensor(out=are[:], in0=rho[:], in1=c1[:], op=ALU.mult), ["rho", "Lc"], ["are"])
        op("dve", lambda h: h.tensor_tensor(out=aim[:], in0=rho[:], in1=s1[:], op=ALU.mult), ["rho", "Ls"], ["aim"])
        op("pool", lambda h: h.memset(cs[:, :, 0:1], 1.0), [], ["cs"])
        op("pool", lambda h: h.memset(sn[:, :, 0:1], 0.0), [], ["sn"])
        op("dve", lambda h: h.tensor_copy(out=cs[:, :, 1], in_=c1[:]), ["Lc", "cs"], ["cs"])
        op("dve", lambda h: h.tensor_copy(out=sn[:, :, 1], in_=s1[:]), ["Ls", "sn"], ["sn"])
        tA = pp4[:, 0:2].rearrange("p a j (b c) -> p (a j b) c", c=64); tB = big[:, 6656:7680].rearrange("p (a c) -> p a c", c=64)
        m = 1
        while m < 128:
            cm = cs[:, :, m:m + 1].broadcast_to([128, 16, m]); sm = sn[:, :, m:m + 1].broadcast_to([128, 16, m])
            a_c = cs[:, :, 1:m + 1]; a_s = sn[:, :, 1:m + 1]
            o_c = cs[:, :, m + 1:2 * m + 1]; o_s = sn[:, :, m + 1:2 * m + 1]
            ta = tA[:, :, 0:m]; tb = tB[:, :, 0:m]
            op("dve", lambda h, a_c=a_c, cm=cm, ta=ta: h.tensor_tensor(out=ta, in0=a_c, in1=cm, op=ALU.mult), ["cs", "sn"], ["tA"])
            op("dve", lambda h, a_s=a_s, sm=sm, tb=tb: h.tensor_tensor(out=tb, in0=a_s, in1=sm, op=ALU.mult), ["cs", "sn"], ["tB"])
            op("dve", lambda h, o_c=o_c, ta=ta, tb=tb: h.tensor_tensor(out=o_c, in0=ta, in1=tb, op=ALU.subtract), ["tA", "tB", "sn"], ["cs"])
            op("dve", lambda h, a_c=a_c, sm=sm, ta=ta: h.tensor_tensor(out=ta, in0=a_c, in1=sm, op=ALU.mult), ["cs", "sn"], ["tA"])
            op("dve", lambda h, a_s=a_s, cm=cm, tb=tb: h.tensor_tensor(out=tb, in0=a_s, in1=cm, op=ALU.mult), ["cs", "sn"], ["tB"])
            op("dve", lambda h, o_s=o_s, ta=ta, tb=tb: h.tensor_tensor(out=o_s, in0=ta, in1=tb, op=ALU.add), ["tA", "tB", "cs"], ["sn"])
            m *= 2
        op("pool", lambda h: h.memset(mask64[:], 1.0))
        op("pool", lambda h: h.memset(mask64[:].rearrange("p (s t) -> p s t", t=4)[:, :, 0:1], 0.0))
        sm_ = [big[:, 9200 + 16 * i:9216 + 16 * i] for i in range(8)]
        lr_, li_ = lam[:, 0, :], lam[:, 1, :]
        nr, den, t_a, t_b, gr, gi = sm_[0], sm_[1], sm_[2], sm_[3], sm_[4], sm_[5]
        TT = lambda o, x, y, f_: op("dve", lambda h: h.tensor_tensor(out=o, in0=x, in1=y, op=f_))
        op("dve", lambda h: h.tensor_scalar(out=nr, in0=are[:], scalar1=-1.0, scalar2=None, op0=ALU.add))
        TT(den, lr_, lr_, ALU.mult); TT(t_a, li_, li_, ALU.mult); TT(den, den, t_a, ALU.add)
        op("dve", lambda h: h.reciprocal(out=den, in_=den))
        TT(t_a, nr, lr_, ALU.mult); TT(t_b, aim[:], li_, ALU.mult); TT(t_a, t_a, t_b, ALU.add); TT(fre[:], t_a, den, ALU.mult)
        TT(t_a, aim[:], lr_, ALU.mult); TT(t_b, nr, li_, ALU.mult); TT(t_a, t_a, t_b, ALU.subtract); TT(fim[:], t_a, den, ALU.mult)
        TT(den, fre[:], fre[:], ALU.mult); TT(t_a, fim[:], fim[:], ALU.mult); TT(den, den, t_a, ALU.add)
        op("dve", lambda h: h.reciprocal(out=den, in_=den))
        TT(gr, fre[:], den, ALU.mult); TT(gi, fim[:], den, ALU.mult)
        op("dve", lambda h: h.tensor_scalar(out=gi, in0=gi, scalar1=-1.0, scalar2=None, op0=ALU.mult))
        dD32 = big[:, 4096:4608].rearrange("p (a k) -> p a k", a=4); dDb = big[:, 4608:5120].rearrange("p (a k) -> p a k", a=4)
        for c4_ in range(4):
            op("dve", lambda h: h.tensor_scalar(out=dD32[:, c4_, :], in0=idf[:], scalar1=col(DSK + c4_), scalar2=None, op0=ALU.mult))
        op("dve", lambda h: h.tensor_copy(out=dDhi[:], in_=dD32))
        op("dve", lambda h: h.tensor_copy(out=dDb, in_=dDhi[:]))
        TT(dDb, dD32, dDb, ALU.subtract)
        op("dve", lambda h: h.tensor_copy(out=dDlo[:], in_=dDb))
        P.dma("pool", lambda h: h.dma_start(out=WBre[:], in_=bblk_d[:, 0]))
        P.dma("pool", lambda h: h.dma_start(out=WBim[:], in_=bblk_d[:, 1]))
        cbf = big[:, 0:4096].rearrange("p (a q m) -> p a q m", a=2, q=16)
        u1 = big[:, 4096:6144].rearrange("p (q m) -> p q m", q=16); u2 = big[:, 6144:8192].rearrange("p (q m) -> p q m", q=16)
        P.dma("sp", lambda h: h.dma_start(out=cbf, in_=cblk_d))
        fre_b = fre[:].unsqueeze(2).broadcast_to([128, 16, 128]); fim_b = fim[:].unsqueeze(2).broadcast_to([128, 16, 128])
        TT(u1, cbf[:, 0], fre_b, ALU.mult); TT(u2, cbf[:, 1], fim_b, ALU.mult); TT(WCre[:], u1, u2, ALU.subtract)
        TT(u1, cbf[:, 0], fim_b, ALU.mult); TT(u2, cbf[:, 1], fre_b, ALU.mult); TT(u1, u1, u2, ALU.add)
        op("dve", lambda h: h.tensor_scalar(out=WCimn[:], in0=u1, scalar1=-1.0, scalar2=None, op0=ALU.mult))

        tD = big[:, 9344:9600].rearrange("p (q s) -> p q s", q=16)
        tE = big[:, 8704:8960].rearrange("p (q s) -> p q s", q=16)
        gr_b = gr.unsqueeze(2).broadcast_to([128, 16, NSEQ]); gi_b = gi.unsqueeze(2).broadcast_to([128, 16, NSEQ])
        TT(tD, h0[:, 0], gr_b, ALU.mult); TT(tE, h0[:, 1], gi_b, ALU.mult); TT(tD, tD, tE, ALU.subtract)
        TT(tE, h0[:, 0], gi_b, ALU.mult); TT(h0[:, 0], tD, tD, ALU.max)
        TT(tD, h0[:, 1], gr_b, ALU.mult); TT(h0[:, 1], tE, tD, ALU.add)
        a_re_b = are[:].unsqueeze(2).broadcast_to([128, 16, NSEQ]); a_im_b = aim[:].unsqueeze(2).broadcast_to([128, 16, NSEQ])
        tC = big[:, 8704:8960].rearrange("p (q s) -> p q s", q=16)
        op("dve", lambda h: h.tensor_tensor(out=ah[:, 0], in0=h0[:, 0], in1=a_re_b, op=ALU.mult), ["h0", "are"], ["ah0"])
        op("dve", lambda h: h.tensor_tensor(out=tC[:], in0=h0[:, 1], in1=a_im_b, op=ALU.mult), ["h0", "aim"], ["tC"])
        op("dve", lambda h: h.tensor_tensor(out=ah[:, 0], in0=ah[:, 0], in1=tC[:], op=ALU.subtract), ["ah0", "tC"], ["ah0"])
        op("dve", lambda h: h.tensor_tensor(out=ah[:, 1], in0=h0[:, 0], in1=a_im_b, op=ALU.mult), ["h0", "aim"], ["ah1"])
        op("dve", lambda h: h.tensor_tensor(out=tC[:], in0=h0[:, 1], in1=a_re_b, op=ALU.mult), ["h0", "are", "ah0"], ["tC"])
        op("dve", lambda h: h.tensor_tensor(out=ah[:, 1], in0=ah[:, 1], in1=tC[:], op=ALU.add), ["ah1", "tC"], ["ah1"])


        def rms(src_ap, key_src, n, slot, scale, pn, junk=None):
            junk = xn if junk is None else junk
            op("act", lambda h: h.activation(out=junk[0:pn, 0:n], in_=src_ap, func=AF.Square, accum_out=ssq[0:pn, slot:slot + 1]),
               [key_src], ["junk", "ssq%d" % slot])
            op("act", lambda h: h.activation(out=rstd[0:pn, slot:slot + 1], in_=ssq[0:pn, slot:slot + 1], func=AF.Ln, scale=scale, bias=epsb[0:pn, 0:1]))
            op("act", lambda h: h.activation(out=rstd[0:pn, slot:slot + 1], in_=rstd[0:pn, slot:slot + 1], func=AF.Exp, scale=-0.5))

        def partA(ti, tl):
            sample = (ti == NPT)
            xt = xb[:, tl, :]
            n = 128
            ns = NS if sample else 128
            r0 = ti * 128
            par = ti % 2
            if sample:
                op("pool", lambda h: h.memset(kTs[:], 0.0))
                P.dma("pool", lambda h: h.dma_start(out=kTs[0:64, 0, 0:NSEQ * 128], in_=kcT_d[0:64, :]))
                P.dma("pool", lambda h: h.dma_start(out=kTs[64:128, 1, 0:NSEQ * 128], in_=kcT_d[64:128, :]))
                P.dma("pool", lambda h: h.dma_start(out=vs[:, 0:NSEQ, :], in_=vc_d))
                P.dma("sp", lambda h: h.dma_start(out=cvst, in_=cvst_d))
            rms(xt[:], "xt", D, 0, 1.0 / D, 128)
            op("act", lambda h: h.activation(out=xn[:], in_=xt[:], func=AF.Copy, scale=rstd[:, 0:1]))
            for c in range(8):
                op("pe", lambda h, c=c: h.transpose(out=pT[:, c, :], in_=xn[:, c * 128:(c + 1) * 128], identity=idb[:]))
            op("dve", lambda h: h.tensor_tensor(out=xnT[:], in0=pT[:], in1=pcol[:, GMIX:GMIX + 8].unsqueeze(2).broadcast_to([128, 8, 128]), op=ALU.mult))
            W = wi[par]
            yield
            for i in range(4):
                bank = acc0 if i % 2 == 0 else acc1
                for c in range(8):
                    op("pe", lambda h, i=i, c=c, bank=bank: h.matmul(bank[:, 0:128], lhsT=W[:, c, i * 128:(i + 1) * 128], rhs=xnT[:, c, :],
                                                                   start=(c == 0), stop=(c == 7)))
                op("act", lambda h, i=i, bank=bank: h.activation(out=qT[:, i, :], in_=bank[:, 0:128], func=AF.Copy))
            yield
            for c in range(8):
                op("pe", lambda h, c=c: h.matmul(Ops[:, 0:128], lhsT=W[:, c, 512:640], rhs=xnT[:, c, :], start=(c == 0), stop=(c == 7)))
            if sample:
                op("dve", lambda h: h.tensor_copy(out=kTs[0:64, 0, NSEQ * 128:17 * 128], in_=Ops[0:64, 0:128]))
                op("dve", lambda h: h.tensor_copy(out=kTs[64:128, 1, NSEQ * 128:17 * 128], in_=Ops[64:128, 0:128]))
            else:
                op("act", lambda h: h.activation(out=kTx[0:64, 0, 128:256], in_=Ops[0:64, 0:128], func=AF.Copy))
                op("act", lambda h: h.activation(out=kTx[64:128, 1, 128:256], in_=Ops[64:128, 0:128], func=AF.Copy))
            yield
            for i in range(4):
                bank = acc0 if i % 2 == 0 else acc1
                for c in range(8):
                    op("pe", lambda h, i=i, c=c, bank=bank: h.matmul(bank[:, 0:128], lhsT=W[:, c, 768 + i * 128:768 + (i + 1) * 128], rhs=xnT[:, c, :],
                                                                   start=(c == 0), stop=(c == 7)))
                op("act", lambda h, i=i, bank=bank: h.activation(out=uT2[par][:, i, :], in_=bank[:, 0:128], func=AF.Copy))
            yield
            for c in range(8):
                op("pe", lambda h, c=c: h.matmul(Ops[:, 0:256], lhsT=xnT[:, c, :], rhs=W[:, c, 512:768], start=(c == 0), stop=(c == 7)))
            if sample:
                op("dve", lambda h: h.tensor_copy(out=vs[:, 16, :], in_=Ops[:, 128:256]))
            else:
                op("act", lambda h: h.activation(out=vx[:, 1, :], in_=Ops[:, 128:256], func=AF.Copy))
            if sample or ti == NPT - 1:
                op("dve", lambda h: h.tensor_copy(out=kvtok[:], in_=Ops[:, 0:256]))
                if sample:
                    P.dma("sp", lambda h: h.dma_start(out=kvs_k[:, 124:128, :], in_=kvtok[0:NS, 0:128]))
                    P.dma("sp", lambda h: h.dma_start(out=kvs_v[:, 124:128, :], in_=kvtok[0:NS, 128:256]))
                else:
                    P.dma("sp", lambda h: h.dma_start(out=kvp_d, in_=kvtok[:]))

            if not sample:
                mi = 0 if ti == 0 else (1 if ti == 1 else 2)
                for i in range(4):
                    for hh_ in range(2):
                        op("pe", lambda h: h.matmul(Sps[:, hh_ * 256:(hh_ + 1) * 256], lhsT=qT[:, i, :], rhs=kTx[:, hh_, :], start=True, stop=False))
                        op("pe", lambda h: h.matmul(Sps[:, hh_ * 256:(hh_ + 1) * 256], lhsT=idb[:], rhs=maskb[:, mi, :], start=False, stop=True))
                    op("dve", lambda h: h.tensor_reduce(out=mx[:], in_=Sps[:].rearrange("p (a k) -> p a k", a=2), axis=AX.X, op=ALU.max))
                    op("dve", lambda h: h.tensor_tensor(out=mx[:], in0=mx[:], in1=sink8x[:, i, :], op=ALU.max))
                    op("dve", lambda h: h.tensor_scalar(out=nbias[:], in0=mx[:], scalar1=-0.125, scalar2=None, op0=ALU.mult))
                    for hh_ in range(2):
                        op("act", lambda h: h.activation(out=Pb[:, hh_, 0:256], in_=Sps[:, hh_ * 256:(hh_ + 1) * 256], func=AF.Exp, scale=0.125,
                                                         bias=nbias[:, hh_:hh_ + 1], accum_out=rs[:, hh_:hh_ + 1]))
                        op("act", lambda h: h.activation(out=esink[:, hh_:hh_ + 1], in_=sink8x[:, i, hh_:hh_ + 1], func=AF.Exp, scale=0.125,
                                                         bias=nbias[:, hh_:hh_ + 1]))
                    op("dve", lambda h: h.tensor_tensor(out=rs[:], in0=rs[:], in1=esink[:], op=ALU.add))
                    op("dve", lambda h: h.reciprocal(out=rinv[:], in_=rs[:]))
                    for hh_ in range(2):
                        for blk in range(2):
                            op("pe", lambda h, hh_=hh_, blk=blk: h.transpose(out=pT[:, hh_ * 2 + blk, :], in_=Pb[:, hh_, blk * 128:(blk + 1) * 128], identity=idb[:]))
                    op("act", lambda h: h.activation(out=PTs[:], in_=pT[:, 0:4, :], func=AF.Copy))
                    for hh_ in range(2):
                        for blk in range(2):
                            op("pe", lambda h, hh_=hh_, blk=blk: h.matmul(Ops[:, hh_ * 64:(hh_ + 1) * 64], lhsT=PTs[:, hh_ * 2 + blk, :],
                                                                         rhs=vx[:, blk, hh_ * 64:(hh_ + 1) * 64], start=(blk == 0), stop=(blk == 1)))
                    for hh_ in range(2):
                        hd = i + 4 * hh_
                        op("act", lambda h, hh_=hh_, hd=hd: h.activation(out=attn[:, hd * 64:(hd + 1) * 64], in_=Ops[:, hh_ * 64:(hh_ + 1) * 64],
                                                                        func=AF.Copy, scale=rinv[:, hh_:hh_ + 1]))
                    yield
                op("pool", lambda h: h.tensor_copy(out=kTx[:, :, 0:128], in_=kTx[:, :, 128:256]))
                op("pool", lambda h: h.tensor_copy(out=vx[:, 0, :], in_=vx[:, 1, :]))
            else:
                W17 = 17 * 128
                for hd in range(8):
                    i, hh_ = hd % 4, hd // 4
                    for cb in range(5):
                        c0 = cb * 512
                        cw = min(512, W17 - c0)
                        bank = mmA if cb % 2 == 0 else mmB
                        op("pe", lambda h, i=i, hh_=hh_, c0=c0, cw=cw, bank=bank: h.matmul(bank[:, 0:cw], lhsT=qT[:, i, :], rhs=kTs[:, hh_, c0:c0 + cw], start=True, stop=True))
                        op("dve", lambda h, c0=c0, cw=cw, bank=bank: h.tensor_tensor(out=Ssx[:, c0:c0 + cw], in0=bank[:, 0:cw], in1=smask[:, c0:c0 + cw], op=ALU.add))
                    op("dve", lambda h, hd=hd: h.tensor_scalar(out=Ssx[:, W17:W17 + 1], in0=sink8[:, hd:hd + 1], scalar1=8.0, scalar2=None, op0=ALU.mult))
                    op("dve", lambda h: h.tensor_reduce(out=mx[:, 0:1], in_=Ssx[:], axis=AX.X, op=ALU.max))
                    op("dve", lambda h: h.tensor_scalar(out=nbias[:, 0:1], in0=mx[:, 0:1], scalar1=-0.125, scalar2=None, op0=ALU.mult))
                    op("act", lambda h: h.activation(out=Psb[:], in_=Ssx[:], func=AF.Exp, scale=0.125, bias=nbias[:, 0:1], accum_out=rs[:, 0:1]))
                    op("dve", lambda h: h.reciprocal(out=rinv[:, 0:1], in_=rs[:, 0:1]))
                    for g8 in range(3):
                        nb_ = 8 if g8 < 2 else 1
                        for b in range(nb_):
                            blk = g8 * 8 + b
                            op("pe", lambda h, b=b, blk=blk: h.transpose(out=pT[:, b, :], in_=Psb[:, blk * 128:(blk + 1) * 128], identity=idb[:]))
                        op("act", lambda h, g8=g8, nb_=nb_: h.activation(out=PTss[:, g8 * 8:g8 * 8 + nb_, :], in_=pT[:, 0:nb_, :], func=AF.Copy))
                    for blk in range(17):
                        op("pe", lambda h, blk=blk, hh_=hh_: h.matmul(Ops[:, 0:64], lhsT=PTss[:, blk, :], rhs=vs[:, blk, hh_ * 64:(hh_ + 1) * 64],
                                                                     start=(blk == 0), stop=(blk == 16)))
                    op("dve", lambda h, hd=hd: h.tensor_scalar(out=attn[:, hd * 64:(hd + 1) * 64], in0=Ops[:, 0:64], scalar1=rinv[:, 0:1], scalar2=None, op0=ALU.mult))
            rms(attn[0:n, :], "attn", 512, 1, 1.0 / 512, n)
            op("act", lambda h: h.activation(out=anb[0:n, :], in_=attn[0:n, :], func=AF.Copy, scale=rstd[0:n, 1:2]))
            for c in range(4):
                op("pe", lambda h, c=c: h.transpose(out=pT[:, c, 0:n], in_=anb[0:n, c * 128:(c + 1) * 128], identity=idb[0:n, 0:n]), ["anb", "idb"], ["pT"])
            op("dve", lambda h: h.tensor_tensor(out=mixT2[par][:, 0:4, :], in0=pT[:, 0:4, :], in1=pcol[:, GATT:GATT + 4].unsqueeze(2).broadcast_to([128, 4, 128]), op=ALU.mult))

            yield

        def partB(ti, tl):
            sample = (ti == NPT)
            xt = xb[:, tl, :]
            n = 128
            ns = NS if sample else 128
            par = ti % 2
            uT = uT2[par]
            mixT = mixT2[par]
            TT = lambda o, x, y, f_: op("dve", lambda h: h.tensor_tensor(out=o, in0=x, in1=y, op=f_))

            def ssm_mm(c4):
                q0 = 4 * c4
                for ri, (WBx, bank) in enumerate(((WBre, mmA), (WBim, mmC))):
                    for j in range(4):
                        op("pe", lambda h: h.matmul(bank[:, j * 128:(j + 1) * 128], lhsT=WBx[:, q0 + j, :], rhs=uT[:, c4, :], start=True, stop=True))

            def ssm_batch(c4):
                q0 = 4 * c4
                T = [pp4[:, kk] for kk in range(4)]
                Bre = mmA[:, :].rearrange("p (j k) -> p j k", j=4); Bim = mmC[:, :].rearrange("p (j k) -> p j k", j=4)
                if sample:
                    op("act", lambda h: h.activation(out=vv4[:, 0], in_=Bre, func=AF.Copy))
                    op("act", lambda h: h.activation(out=vv4[:, 1], in_=Bim, func=AF.Copy))
                    for ri in range(2):
                        v4 = vv4[:, ri, :, 0:NS].rearrange("p j (s t) -> p j s t", t=4)[:, :, :, 0]
                        TT(v4, v4, ah[:, ri, q0:q0 + 4, :], ALU.add)
                    csq = cs[:, q0:q0 + 4, 1:5].unsqueeze(2).broadcast_to([128, 4, NSEQ, 4])
                    snq = sn[:, q0:q0 + 4, 1:5].unsqueeze(2).broadcast_to([128, 4, NSEQ, 4])
                    v3 = lambda ap: ap[:, :, 0:NS].rearrange("p j (s t) -> p j s t", t=4)
                    Bre, Bim = vv4[:, 0], vv4[:, 1]
                else:
                    csq = cs[:, q0:q0 + 4, 1:129]; snq = sn[:, q0:q0 + 4, 1:129]
                    v3 = lambda ap: ap
                Tv = [v3(t_) for t_ in T]
                RR = [v3(rr4[:, kk]) for kk in range(2)]
                TT(Tv[0], v3(Bre), csq, ALU.mult); TT(Tv[1], v3(Bim), snq, ALU.mult)
                TT(Tv[2], v3(Bim), csq, ALU.mult); TT(Tv[3], v3(Bre), snq, ALU.mult)
                if c4 + 1 < 4:
                    ssm_mm(c4 + 1)
                TT(RR[0], Tv[0], Tv[1], ALU.add); TT(RR[1], Tv[2], Tv[3], ALU.subtract)
                for j in range(4):
                    q = q0 + j
                    if sample:
                        op("dve", lambda h: h.tensor_scalar(out=dtmp[:], in0=mask64[:], scalar1=rho[:, q:q + 1], scalar2=None, op0=ALU.mult))
                    for ri in range(2):
                        if sample:
                            op("dve", lambda h: h.tensor_tensor_scan(out=vv4[:, ri, j, 0:NS], data0=dtmp[:], data1=rr4[:, ri, j, 0:NS], initial=0.0, op0=ALU.mult, op1=ALU.add))
                        else:
                            op("dve", lambda h: h.tensor_tensor_scan(out=vv4[:, ri, j, :], data0=rho[:, q:q + 1].broadcast_to([128, 128]), data1=rr4[:, ri, j, :],
                                                                   initial=car[:, ri, q:q + 1], op0=ALU.mult, op1=ALU.add))
                VV = [v3(vv4[:, kk]) for kk in range(2)]
                TT(Tv[0], VV[0], csq, ALU.mult); TT(Tv[1], VV[1], snq, ALU.mult)
                TT(Tv[2], VV[0], snq, ALU.mult); TT(Tv[3], VV[1], csq, ALU.mult)
                TT(v3(hb[:, :, 0, :]), Tv[0], Tv[1], ALU.subtract); TT(v3(hb[:, :, 1, :]), Tv[2], Tv[3], ALU.add)
                if sample:
                    TT(hfin[:, 0, q0:q0 + 4, 0:NSEQ], Tv[0][:, :, :, 3], Tv[1][:, :, :, 3], ALU.subtract)
                    TT(hfin[:, 1, q0:q0 + 4, 0:NSEQ], Tv[2][:, :, :, 3], Tv[3][:, :, :, 3], ALU.add)
                else:
                    TT(car[:, 0, q0:q0 + 4], Tv[0][:, :, 127], Tv[1][:, :, 127], ALU.subtract)
                    TT(car[:, 1, q0:q0 + 4], Tv[2][:, :, 127], Tv[3][:, :, 127], ALU.add)

            def ssm_y(c4):
                op("pe", lambda h: h.matmul(mmB[:, 0:n], lhsT=dDhi[:, c4, :], rhs=uT[:, c4, 0:n], start=True, stop=False))
                op("pe", lambda h: h.matmul(mmB[:, 0:n], lhsT=dDlo[:, c4, :], rhs=uT[:, c4, 0:n], start=False, stop=False))
                for qq in range(4):
                    q = c4 * 4 + qq
                    op("pe", lambda h: h.matmul(mmB[:, 0:n], lhsT=WCre[:, q, :], rhs=hb[:, qq, 0, 0:n], start=False, stop=False))
                    op("pe", lambda h: h.matmul(mmB[:, 0:n], lhsT=WCimn[:, q, :], rhs=hb[:, qq, 1, 0:n], start=False, stop=(qq == 3)))
                op("act", lambda h: h.activation(out=gl32[:, c4, 0:n], in_=mmB[:, 0:n], func=AF.Gelu))
                op("pool", lambda h: h.tensor_copy(out=glb[:, c4, 0:n], in_=gl32[:, c4, 0:n]))

            ssm_mm(0)
            yield
            for c4_ in range(4):
                ssm_batch(c4_)
                yield
                ssm_y(c4_)
                yield
            if ti == NPT - 1:
                op("act", lambda h: h.activation(out=hfin[:, :, :, NSEQ], in_=car[:], func=AF.Copy), ["car"], ["hfin"])
            for oc in range(4):
                for c4 in range(4):
                    op("pe", lambda h, oc=oc, c4=c4: h.matmul(mmC[:, 0:n], lhsT=wgl[0][:, c4, oc * 128:(oc + 1) * 128], rhs=glb[:, c4, 0:n], start=(c4 == 0), stop=(c4 == 3)),
                       ["glb", "wgl"], ["mmC"])
                op("act", lambda h, oc=oc: h.activation(out=sg[:, 0:n], in_=mmC[:, 0:n], func=AF.Sigmoid, bias=col(BGLU + oc)), ["mmC", "pcol"], ["sg"])
                op("dve", lambda h, oc=oc: h.tensor_tensor(out=gl32[:, oc, 0:n], in0=gl32[:, oc, 0:n], in1=sg[:, 0:n], op=ALU.mult), ["gl32", "sg", "glb"], ["gl32"])
                op("act", lambda h, oc=oc: h.activation(out=sqb[:, oc, 0:n], in_=gl32[:, oc, 0:n], func=AF.Square), ["gl32"], ["sqb"])
            for oc in range(4):
                op("pe", lambda h, oc=oc: h.matmul(mmC[:, 0:n], lhsT=onesb[:], rhs=sqb[:, oc, 0:n], start=(oc == 0), stop=(oc == 3)), ["sqb", "onesb"], ["mmC"])
            op("act", lambda h: h.activation(out=rsb[:, 0:n], in_=mmC[:, 0:n], func=AF.Ln, scale=1.0 / 512, bias=epsb[:, 0:1]))
            op("act", lambda h: h.activation(out=rsb[:, 0:n], in_=rsb[:, 0:n], func=AF.Exp, scale=-0.5))
            for oc in range(4):
                op("dve", lambda h, oc=oc: h.scalar_tensor_tensor(out=mixT[:, 4 + oc, 0:n], in0=gl32[:, oc, 0:n], scalar=col(GSSM + oc), in1=rsb[:, 0:n], op0=ALU.mult, op1=ALU.mult),
                   ["gl32", "rsb", "pcol"], ["mixT"])

            yield "join"
            for hf in range(2):
                acc, ak = (acc0, "acc0") if hf == 0 else (acc1, "acc1")
                for c in range(8):
                    op("pe", lambda h, hf=hf, c=c, acc=acc: h.matmul(acc[0:n, :], lhsT=mixT[:, c, 0:n], rhs=wo[0][:, c, hf * 512:(hf + 1) * 512], start=(c == 0), stop=(c == 7)),
                       ["mixT", "wo"], [ak])
                op("dve", lambda h, hf=hf, acc=acc: h.tensor_tensor(out=xt[0:n, hf * 512:(hf + 1) * 512], in0=xt[0:n, hf * 512:(hf + 1) * 512], in1=acc[0:n, :], op=ALU.add),
                   ["xt", ak], ["xt"])
            rms(xt[0:n, :], "xt", D, 2, 1.0 / D, n)
            op("act", lambda h: h.activation(out=xn[0:n, :], in_=xt[0:n, :], func=AF.Copy, scale=rstd[0:n, 2:3]))
            for c in range(8):
                op("pe", lambda h, c=c: h.transpose(out=pT[:, c, 0:n], in_=xn[0:n, c * 128:(c + 1) * 128], identity=idb[0:n, 0:n]), ["xn", "idb"], ["pT"])
            op("dve", lambda h: h.tensor_tensor(out=xn2T[:, :, tl * 128:(tl + 1) * 128], in0=pT[:], in1=pcol[:, GFFN:GFFN + 8].unsqueeze(2).broadcast_to([128, 8, 128]), op=ALU.mult))

        def ffn(tiles_):
            sample = (tiles_[0] == NPT)
            ntl = len(tiles_)
            nb = ntl * 128
            hTv = hTsm if (sample and ntl == 1) else hTall
            p0 = 128 if sample else 0
            npc = nb - p0
            for ch in range(NFC):
                b3 = ch % 2
                w3 = ch % 3
                P.dma("sp", lambda h: h.dma_start(out=wg[w3][:], in_=wgb_d[ch]))
                P.dma("sp", lambda h: h.dma_start(out=wu[w3][:], in_=wub_d[ch]))
                gps, ups = (mmA, mmB) if b3 == 0 else (mmC, pT32)
                for c in range(8):
                    op("pe", lambda h, c=c, b3=b3, gps=gps: h.matmul(gps[:, 0:nb], lhsT=wg[w3][:, c, :], rhs=xn2T[:, c, 0:nb], start=(c == 0), stop=(c == 7)))
                for c in range(8):
                    op("pe", lambda h, c=c, b3=b3, ups=ups: h.matmul(ups[:, 0:nb], lhsT=wu[w3][:, c, :], rhs=xn2T[:, c, 0:nb], start=(c == 0), stop=(c == 7)))
                w0, w1, w2, bb = col(CVW + ch * 3), col(CVW + ch * 3 + 1), col(CVW + ch * 3 + 2), col(CVB + ch)
                cvb, slb, gxb = cv[:, 0, :], sl[:, 0, :], gx[:, b3, :]
                def conv3(g0, g1, g2, cvv):
                    op("dve", lambda h: h.tensor_scalar(out=cvv, in0=g2, scalar1=w2, scalar2=bb, op0=ALU.mult, op1=ALU.add))
                    op("dve", lambda h: h.scalar_tensor_tensor(out=cvv, in0=g1, scalar=w1, in1=cvv, op0=ALU.mult, op1=ALU.add))
                    op("dve", lambda h: h.scalar_tensor_tensor(out=cvv, in0=g0, scalar=w0, in1=cvv, op0=ALU.mult, op1=ALU.add))

                if sample:
                    op("pool", lambda h: h.tensor_copy(out=gxs[:, :, 0:2], in_=cvst[:, ch, :, :]))
                    op("act", lambda h: h.activation(out=gxs[:, :, 2:6], in_=gps[:, 0:NS].rearrange("p (s t) -> p s t", t=4), func=AF.Copy))
                    op("pool", lambda h: h.tensor_copy(out=sconv[:, ch, :, :], in_=gxs[:, :, 4:6]))
                    conv3(gxs[:, :, 0:4], gxs[:, :, 1:5], gxs[:, :, 2:6], cvb[:, 0:NS].rearrange("p (s t) -> p s t", t=4))
                if npc:
                    op("pool", lambda h: h.tensor_copy(out=gxb[:, 0:2], in_=gcar[:, ch, :]))
                    op("act", lambda h: h.activation(out=gxb[:, 2:2 + npc], in_=gps[:, p0:nb], func=AF.Copy))
                    op("pool", lambda h: h.tensor_copy(out=gcar[:, ch, :], in_=gxb[:, npc:npc + 2]))
                    conv3(gxb[:, 0:npc], gxb[:, 1:1 + npc], gxb[:, 2:2 + npc], cvb[:, p0:nb])
                op("act", lambda h, cvb=cvb, slb=slb: h.activation(out=slb[:, 0:nb], in_=cvb[:, 0:nb], func=AF.Silu))
                op("dve", lambda h, ch=ch, slb=slb: h.tensor_tensor(out=hTv[:, ch, 0:nb], in0=slb[:, 0:nb], in1=ups[:, 0:nb], op=ALU.mult))
            accs = [acc0, acc1, Sps, Ops]
            for hf in range(2):
                for g in range(NFC // 2):
                    wdb = wd[g % 3]
                    P.dma("sp", lambda h: h.dma_start(out=wdb, in_=wdb_d[hf, g]))
                    for j in range(2):
                        ch = 2 * g + j
                        for tl in range(ntl):
                            op("pe", lambda h: h.matmul(accs[tl][:, :], lhsT=hTv[:, ch, tl * 128:(tl + 1) * 128], rhs=wdb[:, j, :],
                                                        start=(ch == 0), stop=(ch == NFC - 1)))
                for tl in range(ntl):
                    op("dve", lambda h, tl=tl, hf=hf: h.tensor_tensor(out=xb[:, tl, hf * 512:(hf + 1) * 512], in0=xb[:, tl, hf * 512:(hf + 1) * 512], in1=accs[tl][:, :], op=ALU.add))

        def ffn_tail(tiles_, nxt_, first, last):
            for tl in range(first, last):
                ti = tiles_[tl]
                xt = xb[:, tl, :]
                rms(xt, "xt", D, 3, 1.0 / D, 128, junk=gl32[:].rearrange("p a k -> p (a k)").bitcast(BF16))
                op("dve", lambda h: h.scalar_tensor_tensor(out=xt, in0=xt, scalar=rstd[:, 3:4], in1=gfb[:], op0=ALU.mult, op1=ALU.mult))
                P.dma("sp", lambda h: h.dma_start(out=y_d[ti * 128:(ti + 1) * 128, :], in_=xt))
                if nxt_ is not None and tl < len(nxt_) and (tl == 0 or NPT not in nxt_):
                    load_x(tl, nxt_[tl])
                yield

        def est_dur(kind, eng, fn):
            rec = _Recorder()
            fn(rec)
            name, args, kwargs = rec.call
            o = kwargs.get("out", args[0] if args else None)
            nfree = 1
            if o is not None and hasattr(o, "shape"):
                for d_ in list(o.shape)[1:]:
                    nfree *= int(d_)
            if kind == "dma":
                return 100.0, 2500.0
            if eng == "dve":
                d = ((2 * nfree if name == "tensor_tensor_scan" else nfree) + 151) / 0.96
            elif eng == "act":
                d = (nfree + 230) / 1.2 + (90 if kwargs.get("accum_out") is not None else 0)
            elif eng == "pool":
                d = (2 * nfree + 250) / 1.2
            else:
                d = max(64, nfree) / 1.9 + 25
            return d, d

        def merge_threads(gens):
            gens = [g for g in gens if g is not None]
            bufs = [[] for _ in gens]
            alive = [True] * len(gens)
            tchain = [sched_now[0]] * len(gens)
            joined = [False] * len(gens)
            while True:
                for i, g in enumerate(gens):
                    while alive[i] and not bufs[i] and not joined[i]:
                        P.capture = bufs[i]
                        r_ = next(g, "done")
                        P.capture = None
                        if r_ == "done":
                            alive[i] = False
                        elif r_ == "join":
                            joined[i] = True
                for i in range(len(gens)):
                    if joined[i] and not bufs[i] and not any((alive[j] or bufs[j]) for j in range(len(gens)) if j != i):
                        joined[i] = False
                if not any(bufs) and any(alive):
                    continue
                cands = [i for i in range(len(gens)) if bufs[i]]
                if not cands:
                    break
                best, best_t = None, None
                for i in cands:
                    kind, eng, fn = bufs[i][0]
                    t = max(eng_free.get(eng, 0.0), tchain[i] + 250.0)
                    if best is None or t < best_t:
                        best, best_t = i, t
                kind, eng, fn = bufs[best].pop(0)
                busy, lat = est_dur(kind, eng, fn)
                eng_free[eng] = best_t + busy
                tchain[best] = best_t + lat
                sched_now[0] = max(sched_now[0], best_t)
                (P.dma if kind == "dma" else P.op)(eng, fn)

        eng_free = {}
        ffn_cast_issued = [False]
        sched_now = [0.0]
        if tiles is None:
            blocks = [[0, 1, 2, 3], [4, 5, 6, 7], [8, 9, 10, 11], [12, 13, 14, 15], [NPT, 16]]
        else:
            blocks = tiles
        loaded = set()

        def load_x(tl, ti):
            if ti not in loaded:
                loaded.add(ti)
                P.dma("sp", lambda h: h.dma_start(out=xb[:, tl, :], in_=xin[ti * 128:(ti + 1) * 128, :]))

        pre_done = False
        for bi, blk_ in enumerate(blocks):
            lazy = NPT in blk_
            for tl, ti in enumerate(blk_):
                if tl == 0 or not lazy:
                    load_x(tl, ti)
            if not pre_done:
                for _ in partA(blk_[0], 0):
                    pass
            if not ffn_cast_issued[0]:
                ffn_cast_issued[0] = True
                for a_ in range(0, NFC, 11):
                    P.dma("pool", lambda h: h.dma_start(out=wgb_d[a_:a_ + 11], in_=w_gate[a_:a_ + 11]))
                    P.dma("pool", lambda h: h.dma_start(out=wub_d[a_:a_ + 11], in_=w_up[a_:a_ + 11]))
                for hf_ in range(2):
                    P.dma("pool", lambda h: h.dma_start(out=wdb_d[hf_], in_=w_down[hf_]))
            for tl, ti in enumerate(blk_):
                gb = partB(ti, tl)
                ga = partA(blk_[tl + 1], tl + 1) if tl + 1 < len(blk_) else None
                if ga is not None and lazy:
                    load_x(tl + 1, blk_[tl + 1])
                merge_threads([gb, ga])
            ffn(blk_)
            nxt = blocks[bi + 1] if bi + 1 < len(blocks) else None
            for _ in ffn_tail(blk_, nxt, 0, 1):
                pass
            if nxt is not None and NPT not in nxt:
                merge_threads([ffn_tail(blk_, nxt, 1, len(blk_)), partA(nxt[0], 0)])
                pre_done = True
            else:
                for _ in ffn_tail(blk_, nxt, 1, len(blk_)):
                    pass
                pre_done = False

        fv = pp4[:].rearrange("p a j k -> p (a j k)")
        v1 = fv[:, 0:272].rearrange("p (q s) -> p q s", q=16); v2 = fv[:, 272:544].rearrange("p (q s) -> p q s", q=16); v3_ = fv[:, 544:816].rearrange("p (q s) -> p q s", q=16)
        fre_c = fre[:].unsqueeze(2).broadcast_to([128, 16, NSEQ + 1]); fim_c = fim[:].unsqueeze(2).broadcast_to([128, 16, NSEQ + 1])
        TT(v1, hfin[:, 0], fre_c, ALU.mult); TT(v2, hfin[:, 1], fim_c, ALU.mult); TT(v3_, v1, v2, ALU.subtract)
        TT(v1, hfin[:, 0], fim_c, ALU.mult); TT(v2, hfin[:, 1], fre_c, ALU.mult); TT(hfin[:, 1], v1, v2, ALU.add)
        op("dve", lambda h: h.tensor_copy(out=hfin[:, 0], in_=v3_))
        P.dma("sp", lambda h: h.dma_start(out=hfin_d, in_=hfin[:]), reads=["hfin"], writes=["hfin_d"])
        P.dma("sp", lambda h: h.dma_start(out=pconv_d, in_=gcar[:]), reads=["gcar"], writes=["pconv_d"])
        P.dma("sp", lambda h: h.dma_start(out=sconv_d, in_=sconv[:]), reads=["sconv"], writes=["sconv_d"])
        P.limit = None
        P.barrier()

        with nc.Block() as block:
            @block.tensor
            def _(h):
                P.run("pe", h, sems)

            @block.scalar
            def _(h):
                P.run("act", h, sems)

            @block.vector
            def _(h):
                P.run("dve", h, sems)

            @block.gpsimd
            def _(h):
                P.run("pool", h, sems)

            @block.sync
            def _(h):
                P.run("sp", h, sems)
    return nc


def _consts():
    ident = np.eye(128, dtype=np.float32)
    i = np.arange(128)[:, None]
    c = np.arange(256)[None, :]
    full = np.where(((c < 128) & (c > i)) | ((c >= 128) & (c - 128 <= i)), 0.0, MASKV)
    m1 = np.where(((c < 128) & (c > i) & (c >= NPAD)) | ((c >= 128) & (c - 128 <= i)), 0.0, MASKV)
    m0 = np.where((c >= 128) & (c - 128 <= i) & (c - 128 >= NPAD), 0.0, MASKV)
    masks = np.stack([m0, m1, full], axis=1).astype(np.float32)
    sm = np.full((128, 17 * 128), MASKV, np.float32)
    for s in range(NSEQ):
        for t in range(4):
            r = s * 4 + t
            sm[r, s * 128 + t + 1:(s + 1) * 128] = 0.0
            sm[r, 2048 + s * 4:2048 + s * 4 + t + 1] = 0.0
    return ident, masks, sm


def prep_inputs(x_prompt, x_sample, cache_k_win, cache_v_win, state_ssm_re, state_ssm_im, state_conv,
           meta_tokens, g_mix, w_in, sinks, lam_re, lam_im, log_dt, b_re, b_im, c_re, c_im, d_skip,
           w_glu, b_glu, g_attn_out, g_ssm_out, w_o, g_ffn, w_gate, w_up, conv_w, conv_b, w_down,
           g_final):
    f32 = np.float32
    A = lambda a: np.ascontiguousarray(np.asarray(a, dtype=f32))
    x_prompt, x_sample = A(x_prompt), A(x_sample)
    ident, masks, smask = _consts()
    w_in0 = A(w_in)[0]
    perm = []
    for i in range(4):
        perm += list(range(i * 64, (i + 1) * 64)) + list(range((4 + i) * 64, (5 + i) * 64))
    w_in_p = np.ascontiguousarray(np.concatenate([w_in0[:, perm], w_in0[:, 512:]], axis=1))
    pcol = np.zeros((128, 128), f32)
    pcol[:, 0:8] = A(g_mix)[0].reshape(8, 128).T
    pcol[:, 8:16] = A(g_ffn)[0].reshape(8, 128).T
    pcol[:, 16:20] = A(g_attn_out)[0].reshape(4, 128).T
    pcol[:, 20:24] = A(g_ssm_out)[0].reshape(4, 128).T
    pcol[:, 24:28] = A(b_glu)[0].reshape(4, 128).T
    pcol[:, 28:32] = A(d_skip)[0].reshape(4, 128).T
    cw = A(conv_w)[0].reshape(3, NFC, 128)
    pcol[:, 32:98] = cw.transpose(2, 1, 0).reshape(128, 66)
    pcol[:, 98:120] = A(conv_b)[0].reshape(NFC, 128).T
    lr, li, ld = A(lam_re)[0], A(lam_im)[0], A(log_dt)[0]
    ldx = np.repeat(ld[:, None], 64, axis=1)

    def pl(a):
        return a.reshape(16, 2, 64).transpose(1, 2, 0).reshape(128, 16)

    lam = np.ascontiguousarray(np.stack([pl(lr), pl(li), pl(ldx)], axis=1))
    lamb = np.ascontiguousarray(np.stack([lr.reshape(-1), li.reshape(-1), ldx.reshape(-1)], axis=0))
    bre, bim, cre, cim = A(b_re)[0], A(b_im)[0], A(c_re)[0], A(c_im)[0]
    bblk = np.zeros((128, 2, 16, 128), f32)
    cblk = np.zeros((128, 2, 16, 128), f32)
    for q in range(16):
        for j2 in range(2):
            g = 2 * q + j2
            g8 = g % 8
            rows = slice(g8 * 16, g8 * 16 + 16)
            cols = slice(j2 * 64, j2 * 64 + 64)
            bblk[rows, 0, q, cols] = bre[g].T
            bblk[rows, 1, q, cols] = bim[g].T
            cblk[cols, 0, q, rows] = cre[g].T
            cblk[cols, 1, q, rows] = cim[g].T
    sre, sim_ = A(state_ssm_re)[0], A(state_ssm_im)[0]
    ck, cvv = A(cache_k_win)[0].reshape(128, 128, 128), A(cache_v_win)[0].reshape(128, 128, 128)
    sc = A(state_conv)[0]
    meta = A(meta_tokens)
    wg_l = np.ascontiguousarray(A(w_gate)[0].reshape(8, 128, NFC, 128).transpose(2, 1, 0, 3))
    wu_l = np.ascontiguousarray(A(w_up)[0].reshape(8, 128, NFC, 128).transpose(2, 1, 0, 3))
    wd_l = np.ascontiguousarray(A(w_down)[0].reshape(NFC // 2, 2, 128, 2, 512).transpose(3, 0, 2, 1, 4))
    in_maps = []
    for c in range(NCORES):
        xin = np.zeros((NPT * 128 + 128, D), f32)
        xin[NPAD:128] = meta
        xin[128:NPT * 128] = x_prompt[c]
        xin[NPT * 128:NPT * 128 + NS] = x_sample[c * NSEQ:(c + 1) * NSEQ].reshape(NS, D)
        sl_ = slice(c * NSEQ, (c + 1) * NSEQ)

        def hl(a):
            return a.reshape(NSEQ, 16, 2, 64).transpose(2, 3, 1, 0).reshape(128, 16, NSEQ)

        h0 = np.ascontiguousarray(np.stack([hl(sre[sl_]), hl(sim_[sl_])], axis=1))
        cvst = np.ascontiguousarray(sc[sl_].reshape(NSEQ, 2, NFC, 128).transpose(3, 2, 0, 1))
        kc, vc = ck[sl_], cvv[sl_]
        kcT = np.ascontiguousarray(kc.transpose(2, 0, 1).reshape(128, NSEQ * 128))
        vcl = np.ascontiguousarray(vc.transpose(1, 0, 2))
        in_maps.append(dict(
            xin=xin, w_in=w_in_p, w_o=A(w_o)[0], w_glu=A(w_glu)[0], w_gate=wg_l, w_up=wu_l,
            w_down=wd_l, ident=ident, masks=masks, smask=smask, pcol=pcol, sinks=A(sinks)[0],
            gfin=A(g_final), lam=lam, lamb=lamb, bblk=bblk, cblk=cblk, h0=h0, cvst=cvst, kcT=kcT, vc=vcl,
            kcache=np.ascontiguousarray(kc), vcache=np.ascontiguousarray(vc)))
    return in_maps


def assemble(R):
    f32 = np.float32
    y_prompt = np.stack([R[c]["y"][128:NPT * 128] for c in range(NCORES)])
    y_sample = np.concatenate([R[c]["y"][NPT * 128:NPT * 128 + NS].reshape(NSEQ, 4, D) for c in range(NCORES)])
    p_k = np.stack([R[c]["kvp"][:, 0:128].reshape(128, 2, 64) for c in range(NCORES)])[None]
    p_v = np.stack([R[c]["kvp"][:, 128:256].reshape(128, 2, 64) for c in range(NCORES)])[None]
    s_k = np.concatenate([R[c]["kvs_k"].reshape(NSEQ, 128, 2, 64) for c in range(NCORES)])[None]
    s_v = np.concatenate([R[c]["kvs_v"].reshape(NSEQ, 128, 2, 64) for c in range(NCORES)])[None]

    def unh(a):
        nn = a.shape[-1]
        return a.reshape(2, 64, 16, nn).transpose(3, 2, 0, 1).reshape(nn, 32, 64)

    p_re = np.stack([unh(R[c]["hfin"][:, 0, :, NSEQ:])[0] for c in range(NCORES)])[None]
    p_im = np.stack([unh(R[c]["hfin"][:, 1, :, NSEQ:])[0] for c in range(NCORES)])[None]
    s_re = np.concatenate([unh(R[c]["hfin"][:, 0, :, :NSEQ]) for c in range(NCORES)])[None]
    s_im = np.concatenate([unh(R[c]["hfin"][:, 1, :, :NSEQ]) for c in range(NCORES)])[None]
    p_conv = np.stack([R[c]["pconv"].transpose(2, 1, 0).reshape(2, FF) for c in range(NCORES)])[None]
    s_conv = np.concatenate([R[c]["sconv"].transpose(2, 3, 1, 0).reshape(NSEQ, 2, FF) for c in range(NCORES)])[None]
    out = (y_prompt, y_sample, p_k, p_v, p_re, p_im, p_conv, s_k, s_v, s_re, s_im, s_conv)
    return tuple(np.ascontiguousarray(o, dtype=f32) for o in out)


def kernel(**inputs):
    in_maps = prep_inputs(**inputs)
    nc = build_nc()
    res = run_bass_kernel_spmd(nc, in_maps, core_ids=list(range(NCORES)))
    return assemble(res.results)
```

```python
import numpy as np
from contextlib import ExitStack
import concourse.bass as bass
import concourse.mybir as mybir
from concourse.bass_utils import run_bass_kernel_spmd

F32 = mybir.dt.float32
BF16 = mybir.dt.bfloat16
ALU = mybir.AluOpType
AF = mybir.ActivationFunctionType
AX = mybir.AxisListType

ENGS = ("pe", "act", "dve", "pool", "sp")
NDMASEM = 12
NCORES = 8
D = 1024
NPT = 17
NPAD = 112
NS = 64
NSEQ = 16
FF = 2816
NFC = 22
EPS = 1e-5
PI = float(np.pi)
MASKV = -30000.0


class _Recorder:
    def __init__(self):
        self.call = None

    def __getattr__(self, name):
        def f(*args, **kwargs):
            self.call = (name, args, kwargs)
            return self
        return f


class Prog:
    def __init__(self):
        self.streams = {e: [] for e in ENGS}
        self.count = {e: 0 for e in ENGS}
        self.waited = {e: {} for e in ENGS}
        self.regs = {}
        self.nrec = 0
        self.limit = None
        self.capture = None
        self.dma_rr = {"sp": 0, "pool": 0}
        self.dma_cnt = {}

    @staticmethod
    def _region(ap):
        dims = [(int(st), int(sz)) for st, sz in ap.ap]
        off = int(ap.offset)
        esz = mybir.dt.size(ap.dtype)
        space = str(ap.space)
        if space == "DRAM":
            ext = sum((sz - 1) * abs(st) for st, sz in dims)
            return (0, 1, off, off + ext)
        pst, npart = dims[0]
        pst = max(pst, 1)
        p0, f0 = off // pst, off % pst
        ext = sum((sz - 1) * abs(st) for st, sz in dims[1:])
        f0, ext = f0 * esz, ext * esz + esz - 1
        if space == "PSUM":
            return (0, 128, 0, 1 << 30)
        return (p0, p0 + npart, f0, f0 + ext)

    @staticmethod
    def _is_ap(v):
        return hasattr(v, "tensor") and hasattr(v, "ap") and hasattr(v, "offset")

    def _record(self, fn):
        rec = _Recorder()
        fn(rec)
        name, args, kwargs = rec.call
        acc = []
        for i, a in enumerate(args):
            if self._is_ap(a):
                acc.append((a, i == 0))
        for k, v in kwargs.items():
            if self._is_ap(v):
                acc.append((v, k in ("out", "accum_out")))
        out = []
        for ap, w in acc:
            if str(ap.space) == "PSUM":
                w = True
            out.append((ap.tensor.name, self._region(ap), w))
        self._last_call = rec.call
        return out

    def _deps(self, eng, acc):
        deps = {}
        for name, R, w in acc:
            for (R2, w2), evs in self.regs.get(name, {}).items():
                if not (w or w2):
                    continue
                if R[0] < R2[1] and R2[0] < R[1] and R[2] <= R2[3] and R2[2] <= R[3]:
                    for s_, v in evs.items():
                        if s_ == eng and eng == "pe":
                            continue
                        if deps.get(s_, 0) < v:
                            deps[s_] = v
        out = []
        wd = self.waited[eng]
        for s_, v in deps.items():
            if wd.get(s_, 0) < v:
                wd[s_] = v
                out.append((s_, v))
        return out

    def _commit(self, ev, acc):
        s_, v = ev
        for name, R, w in acc:
            d = self.regs.setdefault(name, {})
            if w:
                for key in [k for k in d if k[0][0] >= R[0] and k[0][1] <= R[1] and k[0][2] >= R[2] and k[0][3] <= R[3]]:
                    del d[key]
            e = d.setdefault((R, w), {})
            if e.get(s_, 0) < v:
                e[s_] = v

    @staticmethod
    def _freeze(fn):
        rec = _Recorder()
        fn(rec)
        return lambda h, c=rec.call: getattr(h, c[0])(*c[1], **c[2])

    def op(self, eng, fn, reads=(), writes=()):
        if self.capture is not None:
            self.capture.append(("op", eng, self._freeze(fn)))
            return
        self.nrec += 1
        if self.limit is not None and self.nrec > self.limit:
            return
        acc = self._record(fn)
        deps = self._deps(eng, acc)
        self.count[eng] += 1
        ev = (eng, self.count[eng])
        self.streams[eng].append((deps, self._last_call, (eng, 1)))
        self._commit(ev, acc)

    def dma(self, eng, fn, reads=(), writes=()):
        if self.capture is not None:
            self.capture.append(("dma", eng, self._freeze(fn)))
            return
        self.nrec += 1
        if self.limit is not None and self.nrec > self.limit:
            return
        acc = self._record(fn)
        k = self.dma_rr[eng]
        self.dma_rr[eng] = (k + 1) % NDMASEM
        sname = "dma%s%d" % (eng, k)
        deps = self._deps(eng, acc)
        prev = self.dma_cnt.get(sname, 0) * 16
        if prev and self.waited[eng].get(sname, 0) < prev:
            self.waited[eng][sname] = prev
            deps.append((sname, prev))
        self.dma_cnt[sname] = self.dma_cnt.get(sname, 0) + 1
        ev = (sname, self.dma_cnt[sname] * 16)
        self.streams[eng].append((deps, self._last_call, (sname, 16)))
        self._commit(ev, acc)

    def barrier(self):
        evs = [(e, self.count[e]) for e in ENGS if self.count[e]]
        evs += [(k, c * 16) for k, c in self.dma_cnt.items()]
        for e in ENGS:
            deps = []
            for s, v in evs:
                if s == e:
                    continue
                if self.waited[e].get(s, 0) < v:
                    self.waited[e][s] = v
                    deps.append((s, v))
            if deps:
                self.streams[e].append((deps, None, None))

    def run(self, eng, h, sems):
        for deps, fn, inc in self.streams[eng]:
            for s, v in deps:
                h.wait_ge(sems[s], v)
            if fn is not None:
                name, args, kwargs = fn
                getattr(h, name)(*args, **kwargs).then_inc(sems[inc[0]], inc[1])


def build_nc(tiles=None, limit=None):
    nc = bass.Bass("TRN2", target_bir_lowering=False)

    def din(name, shape, dt=F32):
        return nc.dram_tensor(name, list(shape), dt, kind="ExternalInput").ap()

    def dout(name, shape, dt=F32):
        return nc.dram_tensor(name, list(shape), dt, kind="ExternalOutput").ap()

    xin = din("xin", [NPT * 128 + 128, D])
    w_in = din("w_in", [D, 1280])
    w_o = din("w_o", [D, D])
    w_glu = din("w_glu", [512, 512])
    w_gate = din("w_gate", [NFC, 128, 8, 128])
    w_up = din("w_up", [NFC, 128, 8, 128])
    w_down = din("w_down", [2, NFC // 2, 128, 2, 512])
    ident_d = din("ident", [128, 128])
    masks_d = din("masks", [128, 3, 256])
    smask_d = din("smask", [128, 17 * 128])
    pcol_d = din("pcol", [128, 128])
    sinks_d = din("sinks", [8])
    gfin_d = din("gfin", [D])
    lam_d = din("lam", [128, 3, 16])
    lamb_d = din("lamb", [3, 16 * 128])
    bblk_d = din("bblk", [128, 2, 16, 128])
    cblk_d = din("cblk", [128, 2, 16, 128])
    h0_d = din("h0", [128, 2, 16, NSEQ])
    cvst_d = din("cvst", [128, NFC, NSEQ, 2])
    kcT_d = din("kcT", [128, NSEQ * 128])
    vc_d = din("vc", [128, NSEQ, 128])
    kcache_d = din("kcache", [NSEQ, 128, 128])
    vcache_d = din("vcache", [NSEQ, 128, 128])

    wgb_d = nc.dram_tensor("wgb", [NFC, 128, 8, 128], BF16, kind="Internal").ap()
    wub_d = nc.dram_tensor("wub", [NFC, 128, 8, 128], BF16, kind="Internal").ap()
    wdb_d = nc.dram_tensor("wdb", [2, NFC // 2, 128, 2, 512], BF16, kind="Internal").ap()
    y_d = dout("y", [NPT * 128 + 128, D])
    kvp_d = dout("kvp", [128, 256])
    kvs_k = dout("kvs_k", [NSEQ, 128, 128])
    kvs_v = dout("kvs_v", [NSEQ, 128, 128])
    hfin_d = dout("hfin", [128, 2, 16, NSEQ + 1])
    pconv_d = dout("pconv", [128, NFC, 2])
    sconv_d = dout("sconv", [128, NFC, NSEQ, 2])

    P = Prog()
    P.limit = limit
    es = ExitStack()
    with es:
        def sb(name, shape, dt=F32):
            return es.enter_context(nc.sbuf_tensor(name, list(shape), dt))

        def ps(name, shape, dt=F32):
            return es.enter_context(nc.psum_tensor(name, list(shape), dt))

        sems = {e: es.enter_context(nc.semaphore("s_" + e)) for e in ENGS}
        for k in range(NDMASEM):
            for e_ in ("sp", "pool"):
                sems["dma%s%d" % (e_, k)] = es.enter_context(nc.semaphore("s_dma%s%d" % (e_, k)))

        idb = sb("idb", [128, 128], BF16)
        onesb = sb("onesb", [128, 128], BF16)
        masks = sb("masks_s", [128, 3, 256])
        smask = sb("smask_s", [128, 17 * 128], BF16)
        pcol = sb("pcol_s", [128, 128])
        sink8 = sb("sink8", [128, 8])
        gfb = sb("gfb", [128, D])
        epsb = sb("epsb", [128, 1])
        lam = sb("lam_s", [128, 3, 16])
        WBre = sb("WBre", [128, 16, 128], BF16); WBim = sb("WBim", [128, 16, 128], BF16)
        WCre = sb("WCre", [128, 16, 128], BF16); WCimn = sb("WCimn", [128, 16, 128], BF16)
        cs = sb("cs", [128, 16, 129]); sn = sb("sn", [128, 16, 129])
        mask64 = sb("mask64", [128, NS]); dtmp = sb("dtmp", [128, NS])
        are = sb("are", [128, 16]); aim = sb("aim", [128, 16])
        ah = sb("ah", [128, 2, 16, NSEQ])
        car = sb("car", [128, 2, 16])
        hfin = sb("hfin_s", [128, 2, 16, NSEQ + 1])
        gcar = sb("gcar", [128, NFC, 2])
        sconv = sb("sconv_s", [128, NFC, NSEQ, 2])
        kTx = sb("kTx", [128, 2, 256], BF16)
        vx = sb("vx", [128, 2, 128], BF16)
        GMIX, GFFN, GATT, GSSM, BGLU, DSK, CVW, CVB = 0, 8, 16, 20, 24, 28, 32, 98

        def col(c):
            return pcol[:, c:c + 1]

        ssq = sb("ssq", [128, 4]); rstd = sb("rstd", [128, 4])
        xn = sb("xn", [128, D], BF16)
        junk = xn
        xnT = sb("xnT", [128, 8, 128], BF16)
        qT = sb("qT", [128, 4, 128], BF16)
        uT2 = [sb("uT%d" % i, [128, 4, 128], BF16) for i in range(2)]
        Sx = sb("Sx", [128, 1, 2, 257]); mx = sb("mx", [128, 2]); nbias = sb("nbias", [128, 2])
        rs = sb("rs", [128, 2]); rinv = sb("rinv", [128, 2]); esink = sb("esink", [128, 2])
        maskb = sb("maskb", [128, 3, 256], BF16); sink8x = sb("sink8x", [128, 4, 2])
        dDhi = sb("dDhi", [128, 4, 128], BF16); dDlo = sb("dDlo", [128, 4, 128], BF16)
        Pb = sb("Pb", [128, 2, 257], BF16); PTs = sb("PTs", [128, 4, 128], BF16)
        mixT2 = [sb("mixT%d" % i, [128, 8, 128], BF16) for i in range(2)]
        mixT = mixT2[0]
        Psb = sb("Psb", [128, 17 * 128 + 1], BF16)
        PTss = sb("PTss", [128, 17, 128], BF16)
        pp4 = sb("pp4", [128, 4, 4, 128]); rr4 = sb("rr4", [128, 2, 4, 128]); vv4 = sb("vv4", [128, 2, 4, 128])
        hb = sb("hb", [128, 4, 2, 128], BF16)
        gl32 = sb("gl32", [128, 4, 128]); glb = sb("glb", [128, 4, 128], BF16)
        xn2T = sb("xn2T", [128, 8, 512], BF16)
        gx = sb("gx", [128, 2, 514]); gxs = sb("gxs", [128, NSEQ, 6])
        cv = sb("cv", [128, 1, 512]); sl = sb("sl", [128, 1, 512])
        kvtok = sl[:, 0, 0:256]
        sg4 = gx[:, 0, 0:512].rearrange("p (a k) -> p a k", a=4); rsb = gx[:, 1, 256:384]
        sqb = gx[:, 1, 0:256].bitcast(BF16).rearrange("p (a k) -> p a k", a=4)
        attn = cv[:, 0, :]
        anb = sl[:, 0, 256:512].bitcast(BF16)
        wi = [sb("wi0", [128, 8, 1280], BF16)] * 2
        wo = [sb("wo0", [128, 8, D], BF16)] * 2
        wgl = [sb("wgl0", [128, 4, 512], BF16)] * 2
        wg = [sb("wg%d" % i, [128, 8, 128], BF16) for i in range(2)] + [xnT]
        wu = [sb("wu%d" % i, [128, 8, 128], BF16) for i in range(2)] + [mixT]
        wdA = sb("wdA", [128, 2, 512], BF16)
        wd = [wdA[:], Sx[:].rearrange("p a b c -> p (a b c)").bitcast(BF16)[:, 0:1024].rearrange("p (j n) -> p j n", j=2),
              xn[:].rearrange("p (j n) -> p j n", j=2)]
        big = sb("big", [128, 9728])
        xb = big[:, 0:4096].rearrange("p (t d) -> p t d", t=4)
        hTall = big[:, 4096:9728].bitcast(BF16).rearrange("p (c n) -> p c n", c=NFC)
        hTsm = big[:, 4096:5504].bitcast(BF16).rearrange("p (c n) -> p c n", c=NFC)
        kTs = big[:, 5504:7680].bitcast(BF16).rearrange("p (a k) -> p a k", a=2)
        vs = big[:, 7680:8768].bitcast(BF16).rearrange("p (b k) -> p b k", b=17)
        scr = big
        h0 = big[:, 8192:8704].rearrange("p (a q s) -> p a q s", a=2, q=16)
        cvst = big[:, 3264:3968].rearrange("p (c s j) -> p c s j", c=NFC, s=NSEQ)
        def prep_views(o):
            return (scr[:, o:o + 768].rearrange("p (a b) -> p a b", a=3), scr[:, o + 768:o + 2304].rearrange("p (a b) -> p a b", a=6),
                    scr[:, o + 2304:o + 2816].rearrange("p (a q m) -> p a q m", a=2, q=2), scr[:, o + 2816:o + 3328].rearrange("p (a q m) -> p a q m", a=2, q=2))
        Ssx = big[:, 1024:1024 + 17 * 128 + 1]
        idf = big[:, 8960:9088]

        mmA = ps("mmA", [128, 512]); mmB = ps("mmB", [128, 512]); mmC = ps("mmC", [128, 512])
        pT = ps("pT", [128, 8, 128], BF16)
        pT32 = pT[:].rearrange("p c k -> p (c k)").bitcast(F32)
        Sps = ps("Sps", [128, 512]); Ops = ps("Ops", [128, 512])
        acc0 = ps("acc0", [128, 512]); acc1 = ps("acc1", [128, 512])

        op = P.op

        P.dma("sp", lambda h: h.dma_start(out=idf[:], in_=ident_d), writes=["idf"])
        P.dma("sp", lambda h: h.dma_start(out=masks[:], in_=masks_d), writes=["masks"])
        P.dma("pool", lambda h: h.dma_start(out=smask[:], in_=smask_d), writes=["smask"])
        P.dma("sp", lambda h: h.dma_start(out=pcol[:], in_=pcol_d), writes=["pcol"])
        P.dma("sp", lambda h: h.dma_start(out=sink8[:], in_=sinks_d.partition_broadcast(128)), writes=["sink8"])
        P.dma("sp", lambda h: h.dma_start(out=gfb[:], in_=gfin_d.partition_broadcast(128)), writes=["gfb"])
        P.dma("sp", lambda h: h.dma_start(out=lam[:], in_=lam_d), writes=["lam"])
        P.dma("sp", lambda h: h.dma_start(out=h0, in_=h0_d))
        op("pool", lambda h: h.memset(hb[:], 0.0))
        op("pool", lambda h: h.memset(cv[:], 0.0))
        op("pool", lambda h: h.memset(gx[:], 0.0))
        P.dma("pool", lambda h: h.dma_start(out=wi[0][:], in_=w_in.rearrange("(c p) n -> p c n", p=128)))
        P.dma("pool", lambda h: h.dma_start(out=wgl[0][:], in_=w_glu.rearrange("(c p) n -> p c n", p=128)))
        P.dma("pool", lambda h: h.dma_start(out=wo[0][:], in_=w_o.rearrange("(c p) n -> p c n", p=128)))
        P.dma("sp", lambda h: h.dma_start(out=kvs_k[:, 0:124, :], in_=kcache_d[:, 4:128, :]), writes=["kvs_k_a"])
        P.dma("sp", lambda h: h.dma_start(out=kvs_v[:, 0:124, :], in_=vcache_d[:, 4:128, :]), writes=["kvs_v_a"])

        op("dve", lambda h: h.tensor_copy(out=idb[:], in_=idf[:]), ["idf"], ["idb"])
        P.dma("pool", lambda h: h.dma_start(out=maskb[:], in_=masks_d))
        op("dve", lambda h: h.tensor_scalar(out=sink8x[:].rearrange("p i a -> p a i"), in0=sink8[:].rearrange("p (a i) -> p a i", a=2), scalar1=8.0, scalar2=None, op0=ALU.mult))
        op("pool", lambda h: h.memset(onesb[:], 1.0), [], ["onesb"])
        op("pool", lambda h: h.memset(epsb[:], EPS), [], ["epsb"])
        op("pool", lambda h: h.memset(kTx[:], 0.0), [], ["kTx"])
        op("pool", lambda h: h.memset(vx[:], 0.0), [], ["vx"])
        op("pool", lambda h: h.memset(car[:], 0.0), [], ["car"])
        op("pool", lambda h: h.memset(gcar[:], 0.0), [], ["gcar"])
        pass
        op("pool", lambda h: h.memset(hfin[:], 0.0))
        op("pool", lambda h: h.memset(sconv[:], 0.0))
        rho = sb("rho", [128, 16]); fre = sb("fre", [128, 16]); fim = sb("fim", [128, 16])
        dtl, thl, c1, s1, tmpa, kfL = (big[:, 9088 + 16 * i:9104 + 16 * i] for i in range(6))
        kiL = big[:, 9184:9200].bitcast(mybir.dt.int32)
        op("act", lambda h: h.activation(out=dtl[:], in_=lam[:, 2, :], func=AF.Exp), ["lam"], ["dtl"])
        op("dve", lambda h: h.tensor_tensor(out=thl[:], in0=lam[:, 1, :], in1=dtl[:], op=ALU.mult), ["lam", "dtl"], ["thl"])
        op("dve", lambda h: h.tensor_tensor(out=tmpa[:], in0=lam[:, 0, :], in1=dtl[:], op=ALU.mult), ["lam", "dtl"], ["tmpa"])
        op("act", lambda h: h.activation(out=rho[:], in_=tmpa[:], func=AF.Exp), ["tmpa"], ["rho"])

        def sincos(eng_tag, th_ap, s_ap, c_ap, tmp_ap, shape_key, ki_ap, kf_ap):
            K = shape_key

            def reduce_(shift, dst_key):
                op("dve", lambda h: h.tensor_scalar(out=tmp_ap, in0=th_ap, scalar1=shift, scalar2=1.0 / (2 * PI), op0=ALU.add, op1=ALU.mult),
                   [K + "th", K + "s", K + "c"], [K + "tmp"])
                op("dve", lambda h: h.tensor_copy(out=ki_ap, in_=tmp_ap), [K + "tmp"], [K + "ki"])
                op("dve", lambda h: h.tensor_copy(out=kf_ap, in_=ki_ap), [K + "ki"], [K + "kf"])
                op("dve", lambda h: h.tensor_scalar(out=tmp_ap, in0=th_ap, scalar1=shift, scalar2=None, op0=ALU.add), [K + "th", K + "ki"], [K + "tmp"])
                op("dve", lambda h: h.scalar_tensor_tensor(out=tmp_ap, in0=kf_ap, scalar=-2 * PI, in1=tmp_ap, op0=ALU.mult, op1=ALU.add),
                   [K + "kf", K + "tmp"], [K + "tmp"])
                op("dve", lambda h: h.tensor_scalar(out=kf_ap, in0=tmp_ap, scalar1=PI, scalar2=None, op0=ALU.is_gt), [K + "tmp"], [K + "kf"])
                op("dve", lambda h: h.scalar_tensor_tensor(out=tmp_ap, in0=kf_ap, scalar=-2 * PI, in1=tmp_ap, op0=ALU.mult, op1=ALU.add),
                   [K + "kf", K + "tmp"], [K + "tmp"])
                op("dve", lambda h: h.tensor_scalar(out=kf_ap, in0=tmp_ap, scalar1=-PI, scalar2=None, op0=ALU.is_lt), [K + "tmp"], [K + "kf"])
                op("dve", lambda h: h.scalar_tensor_tensor(out=tmp_ap, in0=kf_ap, scalar=2 * PI, in1=tmp_ap, op0=ALU.mult, op1=ALU.add),
                   [K + "kf", K + "tmp"], [K + "tmp"])
                op("dve", lambda h: h.tensor_scalar(out=tmp_ap, in0=tmp_ap, scalar1=-PI, scalar2=PI, op0=ALU.max, op1=ALU.min), [K + "tmp"], [K + "tmp"])

            reduce_(0.0, "s")
            op("act", lambda h: h.activation(out=s_ap, in_=tmp_ap, func=AF.Sin), [K + "tmp"], [K + "s"])
            reduce_(0.5 * PI, "c")
            op("act", lambda h: h.activation(out=c_ap, in_=tmp_ap, func=AF.Sin), [K + "tmp"], [K + "c"])

        sincos("L", thl[:], s1[:], c1[:], tmpa[:], "L", kiL[:], kfL[:])
        op("dve", lambda h: h.tensor_tensor(out=are[:], in0=rho[:], in1=c1[:], op=ALU.mult), ["rho", "Lc"], ["are"])
        op("dve", lambda h: h.tensor_tensor(out=aim[:], in0=rho[:], in1=s1[:], op=ALU.mult), ["rho", "Ls"], ["aim"])
        op("pool", lambda h: h.memset(cs[:, :, 0:1], 1.0), [], ["cs"])
        op("pool", lambda h: h.memset(sn[:, :, 0:1], 0.0), [], ["sn"])
        op("dve", lambda h: h.tensor_copy(out=cs[:, :, 1], in_=c1[:]), ["Lc", "cs"], ["cs"])
        op("dve", lambda h: h.tensor_copy(out=sn[:, :, 1], in_=s1[:]), ["Ls", "sn"], ["sn"])
        tA = pp4[:, 0:2].rearrange("p a j (b c) -> p (a j b) c", c=64); tB = big[:, 6656:7680].rearrange("p (a c) -> p a c", c=64)
        m = 1
        while m < 128:
            cm = cs[:, :, m:m + 1].broadcast_to([128, 16, m]); sm = sn[:, :, m:m + 1].broadcast_to([128, 16, m])
            a_c = cs[:, :, 1:m + 1]; a_s = sn[:, :, 1:m + 1]
            o_c = cs[:, :, m + 1:2 * m + 1]; o_s = sn[:, :, m + 1:2 * m + 1]
            ta = tA[:, :, 0:m]; tb = tB[:, :, 0:m]
            op("dve", lambda h, a_c=a_c, cm=cm, ta=ta: h.tensor_tensor(out=ta, in0=a_c, in1=cm, op=ALU.mult), ["cs", "sn"], ["tA"])
            op("dve", lambda h, a_s=a_s, sm=sm, tb=tb: h.tensor_tensor(out=tb, in0=a_s, in1=sm, op=ALU.mult), ["cs", "sn"], ["tB"])
            op("dve", lambda h, o_c=o_c, ta=ta, tb=tb: h.tensor_tensor(out=o_c, in0=ta, in1=tb, op=ALU.subtract), ["tA", "tB", "sn"], ["cs"])
            op("dve", lambda h, a_c=a_c, sm=sm, ta=ta: h.tensor_tensor(out=ta, in0=a_c, in1=sm, op=ALU.mult), ["cs", "sn"], ["tA"])
            op("dve", lambda h, a_s=a_s, cm=cm, tb=tb: h.tensor_tensor(out=tb, in0=a_s, in1=cm, op=ALU.mult), ["cs", "sn"], ["tB"])
            op("dve", lambda h, o_s=o_s, ta=ta, tb=tb: h.tensor_tensor(out=o_s, in0=ta, in1=tb, op=ALU.add), ["tA", "tB", "cs"], ["sn"])
            m *= 2
        op("pool", lambda h: h.memset(mask64[:], 1.0))
        op("pool", lambda h: h.memset(mask64[:].rearrange("p (s t) -> p s t", t=4)[:, :, 0:1], 0.0))
        sm_ = [big[:, 9200 + 16 * i:9216 + 16 * i] for i in range(8)]
        lr_, li_ = lam[:, 0, :], lam[:, 1, :]
        nr, den, t_a, t_b, gr, gi = sm_[0], sm_[1], sm_[2], sm_[3], sm_[4], sm_[5]
        TT = lambda o, x, y, f_: op("dve", lambda h: h.tensor_tensor(out=o, in0=x, in1=y, op=f_))
        op("dve", lambda h: h.tensor_scalar(out=nr, in0=are[:], scalar1=-1.0, scalar2=None, op0=ALU.add))
        TT(den, lr_, lr_, ALU.mult); TT(t_a, li_, li_, ALU.mult); TT(den, den, t_a, ALU.add)
        op("dve", lambda h: h.reciprocal(out=den, in_=den))
        TT(t_a, nr, lr_, ALU.mult); TT(t_b, aim[:], li_, ALU.mult); TT(t_a, t_a, t_b, ALU.add); TT(fre[:], t_a, den, ALU.mult)
        TT(t_a, aim[:], lr_, ALU.mult); TT(t_b, nr, li_, ALU.mult); TT(t_a, t_a, t_b, ALU.subtract); TT(fim[:], t_a, den, ALU.mult)
        TT(den, fre[:], fre[:], ALU.mult); TT(t_a, fim[:], fim[:], ALU.mult); TT(den, den, t_a, ALU.add)
        op("dve", lambda h: h.reciprocal(out=den, in_=den))
        TT(gr, fre[:], den, ALU.mult); TT(gi, fim[:], den, ALU.mult)
        op("dve", lambda h: h.tensor_scalar(out=gi, in0=gi, scalar1=-1.0, scalar2=None, op0=ALU.mult))
        dD32 = big[:, 4096:4608].rearrange("p (a k) -> p a k", a=4); dDb = big[:, 4608:5120].rearrange("p (a k) -> p a k", a=4)
        for c4_ in range(4):
            op("dve", lambda h: h.tensor_scalar(out=dD32[:, c4_, :], in0=idf[:], scalar1=col(DSK + c4_), scalar2=None, op0=ALU.mult))
        op("dve", lambda h: h.tensor_copy(out=dDhi[:], in_=dD32))
        op("dve", lambda h: h.tensor_copy(out=dDb, in_=dDhi[:]))
        TT(dDb, dD32, dDb, ALU.subtract)
        op("dve", lambda h: h.tensor_copy(out=dDlo[:], in_=dDb))
        P.dma("pool", lambda h: h.dma_start(out=WBre[:], in_=bblk_d[:, 0]))
        P.dma("pool", lambda h: h.dma_start(out=WBim[:], in_=bblk_d[:, 1]))
        cbf = big[:, 0:4096].rearrange("p (a q m) -> p a q m", a=2, q=16)
        u1 = big[:, 4096:6144].rearrange("p (q m) -> p q m", q=16); u2 = big[:, 6144:8192].rearrange("p (q m) -> p q m", q=16)
        P.dma("sp", lambda h: h.dma_start(out=cbf, in_=cblk_d))
        fre_b = fre[:].unsqueeze(2).broadcast_to([128, 16, 128]); fim_b = fim[:].unsqueeze(2).broadcast_to([128, 16, 128])
        TT(u1, cbf[:, 0], fre_b, ALU.mult); TT(u2, cbf[:, 1], fim_b, ALU.mult); TT(WCre[:], u1, u2, ALU.subtract)
        TT(u1, cbf[:, 0], fim_b, ALU.mult); TT(u2, cbf[:, 1], fre_b, ALU.mult); TT(u1, u1, u2, ALU.add)
        op("dve", lambda h: h.tensor_scalar(out=WCimn[:], in0=u1, scalar1=-1.0, scalar2=None, op0=ALU.mult))

        tD = big[:, 9344:9600].rearrange("p (q s) -> p q s", q=16)
        tE = big[:, 8704:8960].rearrange("p (q s) -> p q s", q=16)
        gr_b = gr.unsqueeze(2).broadcast_to([128, 16, NSEQ]); gi_b = gi.unsqueeze(2).broadcast_to([128, 16, NSEQ])
        TT(tD, h0[:, 0], gr_b, ALU.mult); TT(tE, h0[:, 1], gi_b, ALU.mult); TT(tD, tD, tE, ALU.subtract)
        TT(tE, h0[:, 0], gi_b, ALU.mult); TT(h0[:, 0], tD, tD, ALU.max)
        TT(tD, h0[:, 1], gr_b, ALU.mult); TT(h0[:, 1], tE, tD, ALU.add)
        a_re_b = are[:].unsqueeze(2).broadcast_to([128, 16, NSEQ]); a_im_b = aim[:].unsqueeze(2).broadcast_to([128, 16, NSEQ])
        tC = big[:, 8704:8960].rearrange("p (q s) -> p q s", q=16)
        op("dve", lambda h: h.tensor_tensor(out=ah[:, 0], in0=h0[:, 0], in1=a_re_b, op=ALU.mult), ["h0", "are"], ["ah0"])
        op("dve", lambda h: h.tensor_tensor(out=tC[:], in0=h0[:, 1], in1=a_im_b, op=ALU.mult), ["h0", "aim"], ["tC"])
        op("dve", lambda h: h.tensor_tensor(out=ah[:, 0], in0=ah[:, 0], in1=tC[:], op=ALU.subtract), ["ah0", "tC"], ["ah0"])
        op("dve", lambda h: h.tensor_tensor(out=ah[:, 1], in0=h0[:, 0], in1=a_im_b, op=ALU.mult), ["h0", "aim"], ["ah1"])
        op("dve", lambda h: h.tensor_tensor(out=tC[:], in0=h0[:, 1], in1=a_re_b, op=ALU.mult), ["h0", "are", "ah0"], ["tC"])
        op("dve", lambda h: h.tensor_tensor(out=ah[:, 1], in0=ah[:, 1], in1=tC[:], op=ALU.add), ["ah1", "tC"], ["ah1"])


        def rms(src_ap, key_src, n, slot, scale, pn, junk=None):
            junk = xn if junk is None else junk
            op("act", lambda h: h.activation(out=junk[0:pn, 0:n], in_=src_ap, func=AF.Square, accum_out=ssq[0:pn, slot:slot + 1]),
               [key_src], ["junk", "ssq%d" % slot])
            op("act", lambda h: h.activation(out=rstd[0:pn, slot:slot + 1], in_=ssq[0:pn, slot:slot + 1], func=AF.Ln, scale=scale, bias=epsb[0:pn, 0:1]))
            op("act", lambda h: h.activation(out=rstd[0:pn, slot:slot + 1], in_=rstd[0:pn, slot:slot + 1], func=AF.Exp, scale=-0.5))

        def partA(ti, tl):
            sample = (ti == NPT)
            xt = xb[:, tl, :]
            n = 128
            ns = NS if sample else 128
            r0 = ti * 128
            par = ti % 2
            if sample:
                op("pool", lambda h: h.memset(kTs[:], 0.0))
                P.dma("pool", lambda h: h.dma_start(out=kTs[0:64, 0, 0:NSEQ * 128], in_=kcT_d[0:64, :]))
                P.dma("pool", lambda h: h.dma_start(out=kTs[64:128, 1, 0:NSEQ * 128], in_=kcT_d[64:128, :]))
                P.dma("pool", lambda h: h.dma_start(out=vs[:, 0:NSEQ, :], in_=vc_d))
                P.dma("sp", lambda h: h.dma_start(out=cvst, in_=cvst_d))
            rms(xt[:], "xt", D, 0, 1.0 / D, 128)
            op("act", lambda h: h.activation(out=xn[:], in_=xt[:], func=AF.Copy, scale=rstd[:, 0:1]))
            for c in range(8):
                op("pe", lambda h, c=c: h.transpose(out=pT[:, c, :], in_=xn[:, c * 128:(c + 1) * 128], identity=idb[:]))
            op("dve", lambda h: h.tensor_tensor(out=xnT[:], in0=pT[:], in1=pcol[:, GMIX:GMIX + 8].unsqueeze(2).broadcast_to([128, 8, 128]), op=ALU.mult))
            W = wi[par]
            yield
            for i in range(4):
                bank = acc0 if i % 2 == 0 else acc1
                for c in range(8):
                    op("pe", lambda h, i=i, c=c, bank=bank: h.matmul(bank[:, 0:128], lhsT=W[:, c, i * 128:(i + 1) * 128], rhs=xnT[:, c, :],
                                                                   start=(c == 0), stop=(c == 7)))
                op("act", lambda h, i=i, bank=bank: h.activation(out=qT[:, i, :], in_=bank[:, 0:128], func=AF.Copy))
            yield
            for c in range(8):
                op("pe", lambda h, c=c: h.matmul(Ops[:, 0:128], lhsT=W[:, c, 512:640], rhs=xnT[:, c, :], start=(c == 0), stop=(c == 7)))
            if sample:
                op("dve", lambda h: h.tensor_copy(out=kTs[0:64, 0, NSEQ * 128:17 * 128], in_=Ops[0:64, 0:128]))
                op("dve", lambda h: h.tensor_copy(out=kTs[64:128, 1, NSEQ * 128:17 * 128], in_=Ops[64:128, 0:128]))
            else:
                op("act", lambda h: h.activation(out=kTx[0:64, 0, 128:256], in_=Ops[0:64, 0:128], func=AF.Copy))
                op("act", lambda h: h.activation(out=kTx[64:128, 1, 128:256], in_=Ops[64:128, 0:128], func=AF.Copy))
            yield
            for i in range(4):
                bank = acc0 if i % 2 == 0 else acc1
                for c in range(8):
                    op("pe", lambda h, i=i, c=c, bank=bank: h.matmul(bank[:, 0:128], lhsT=W[:, c, 768 + i * 128:768 + (i + 1) * 128], rhs=xnT[:, c, :],
                                                                   start=(c == 0), stop=(c == 7)))
                op("act", lambda h, i=i, bank=bank: h.activation(out=uT2[par][:, i, :], in_=bank[:, 0:128], func=AF.Copy))
            yield
            for c in range(8):
                op("pe", lambda h, c=c: h.matmul(Ops[:, 0:256], lhsT=xnT[:, c, :], rhs=W[:, c, 512:768], start=(c == 0), stop=(c == 7)))
            if sample:
                op("dve", lambda h: h.tensor_copy(out=vs[:, 16, :], in_=Ops[:, 128:256]))
            else:
                op("act", lambda h: h.activation(out=vx[:, 1, :], in_=Ops[:, 128:256], func=AF.Copy))
            if sample or ti == NPT - 1:
                op("dve", lambda h: h.tensor_copy(out=kvtok[:], in_=Ops[:, 0:256]))
                if sample:
                    P.dma("sp", lambda h: h.dma_start(out=kvs_k[:, 124:128, :], in_=kvtok[0:NS, 0:128]))
                    P.dma("sp", lambda h: h.dma_start(out=kvs_v[:, 124:128, :], in_=kvtok[0:NS, 128:256]))
                else:
                    P.dma("sp", lambda h: h.dma_start(out=kvp_d, in_=kvtok[:]))

            if not sample:
                mi = 0 if ti == 0 else (1 if ti == 1 else 2)
                for i in range(4):
                    for hh_ in range(2):
                        op("pe", lambda h: h.matmul(Sps[:, hh_ * 256:(hh_ + 1) * 256], lhsT=qT[:, i, :], rhs=kTx[:, hh_, :], start=True, stop=False))
                        op("pe", lambda h: h.matmul(Sps[:, hh_ * 256:(hh_ + 1) * 256], lhsT=idb[:], rhs=maskb[:, mi, :], start=False, stop=True))
                    op("dve", lambda h: h.tensor_reduce(out=mx[:], in_=Sps[:].rearrange("p (a k) -> p a k", a=2), axis=AX.X, op=ALU.max))
                    op("dve", lambda h: h.tensor_tensor(out=mx[:], in0=mx[:], in1=sink8x[:, i, :], op=ALU.max))
                    op("dve", lambda h: h.tensor_scalar(out=nbias[:], in0=mx[:], scalar1=-0.125, scalar2=None, op0=ALU.mult))
                    for hh_ in range(2):
                        op("act", lambda h: h.activation(out=Pb[:, hh_, 0:256], in_=Sps[:, hh_ * 256:(hh_ + 1) * 256], func=AF.Exp, scale=0.125,
                                                         bias=nbias[:, hh_:hh_ + 1], accum_out=rs[:, hh_:hh_ + 1]))
                        op("act", lambda h: h.activation(out=esink[:, hh_:hh_ + 1], in_=sink8x[:, i, hh_:hh_ + 1], func=AF.Exp, scale=0.125,
                                                         bias=nbias[:, hh_:hh_ + 1]))
                    op("dve", lambda h: h.tensor_tensor(out=rs[:], in0=rs[:], in1=esink[:], op=ALU.add))
                    op("dve", lambda h: h.reciprocal(out=rinv[:], in_=rs[:]))
                    for hh_ in range(2):
                        for blk in range(2):
                            op("pe", lambda h, hh_=hh_, blk=blk: h.transpose(out=pT[:, hh_ * 2 + blk, :], in_=Pb[:, hh_, blk * 128:(blk + 1) * 128], identity=idb[:]))
                    op("act", lambda h: h.activation(out=PTs[:], in_=pT[:, 0:4, :], func=AF.Copy))
                    for hh_ in range(2):
                        for blk in range(2):
                            op("pe", lambda h, hh_=hh_, blk=blk: h.matmul(Ops[:, hh_ * 64:(hh_ + 1) * 64], lhsT=PTs[:, hh_ * 2 + blk, :],
                                                                         rhs=vx[:, blk, hh_ * 64:(hh_ + 1) * 64], start=(blk == 0), stop=(blk == 1)))
                    for hh_ in range(2):
                        hd = i + 4 * hh_
                        op("act", lambda h, hh_=hh_, hd=hd: h.activation(out=attn[:, hd * 64:(hd + 1) * 64], in_=Ops[:, hh_ * 64:(hh_ + 1) * 64],
                                                                        func=AF.Copy, scale=rinv[:, hh_:hh_ + 1]))
                    yield
                op("pool", lambda h: h.tensor_copy(out=kTx[:, :, 0:128], in_=kTx[:, :, 128:256]))
                op("pool", lambda h: h.tensor_copy(out=vx[:, 0, :], in_=vx[:, 1, :]))
            else:
                W17 = 17 * 128
                for hd in range(8):
                    i, hh_ = hd % 4, hd // 4
                    for cb in range(5):
                        c0 = cb * 512
                        cw = min(512, W17 - c0)
                        bank = mmA if cb % 2 == 0 else mmB
                        op("pe", lambda h, i=i, hh_=hh_, c0=c0, cw=cw, bank=bank: h.matmul(bank[:, 0:cw], lhsT=qT[:, i, :], rhs=kTs[:, hh_, c0:c0 + cw], start=True, stop=True))
                        op("dve", lambda h, c0=c0, cw=cw, bank=bank: h.tensor_tensor(out=Ssx[:, c0:c0 + cw], in0=bank[:, 0:cw], in1=smask[:, c0:c0 + cw], op=ALU.add))
                    op("dve", lambda h, hd=hd: h.tensor_scalar(out=Ssx[:, W17:W17 + 1], in0=sink8[:, hd:hd + 1], scalar1=8.0, scalar2=None, op0=ALU.mult))
                    op("dve", lambda h: h.tensor_reduce(out=mx[:, 0:1], in_=Ssx[:], axis=AX.X, op=ALU.max))
                    op("dve", lambda h: h.tensor_scalar(out=nbias[:, 0:1], in0=mx[:, 0:1], scalar1=-0.125, scalar2=None, op0=ALU.mult))
                    op("act", lambda h: h.activation(out=Psb[:], in_=Ssx[:], func=AF.Exp, scale=0.125, bias=nbias[:, 0:1], accum_out=rs[:, 0:1]))
                    op("dve", lambda h: h.reciprocal(out=rinv[:, 0:1], in_=rs[:, 0:1]))
                    for g8 in range(3):
                        nb_ = 8 if g8 < 2 else 1
                        for b in range(nb_):
                            blk = g8 * 8 + b
                            op("pe", lambda h, b=b, blk=blk: h.transpose(out=pT[:, b, :], in_=Psb[:, blk * 128:(blk + 1) * 128], identity=idb[:]))
                        op("act", lambda h, g8=g8, nb_=nb_: h.activation(out=PTss[:, g8 * 8:g8 * 8 + nb_, :], in_=pT[:, 0:nb_, :], func=AF.Copy))
                    for blk in range(17):
                        op("pe", lambda h, blk=blk, hh_=hh_: h.matmul(Ops[:, 0:64], lhsT=PTss[:, blk, :], rhs=vs[:, blk, hh_ * 64:(hh_ + 1) * 64],
                                                                     start=(blk == 0), stop=(blk == 16)))
                    op("dve", lambda h, hd=hd: h.tensor_scalar(out=attn[:, hd * 64:(hd + 1) * 64], in0=Ops[:, 0:64], scalar1=rinv[:, 0:1], scalar2=None, op0=ALU.mult))
            rms(attn[0:n, :], "attn", 512, 1, 1.0 / 512, n)
            op("act", lambda h: h.activation(out=anb[0:n, :], in_=attn[0:n, :], func=AF.Copy, scale=rstd[0:n, 1:2]))
            for c in range(4):
                op("pe", lambda h, c=c: h.transpose(out=pT[:, c, 0:n], in_=anb[0:n, c * 128:(c + 1) * 128], identity=idb[0:n, 0:n]), ["anb", "idb"], ["pT"])
            op("dve", lambda h: h.tensor_tensor(out=mixT2[par][:, 0:4, :], in0=pT[:, 0:4, :], in1=pcol[:, GATT:GATT + 4].unsqueeze(2).broadcast_to([128, 4, 128]), op=ALU.mult))

            yield

        def partB(ti, tl):
            sample = (ti == NPT)
            xt = xb[:, tl, :]
            n = 128
            ns = NS if sample else 128
            par = ti % 2
            uT = uT2[par]
            mixT = mixT2[par]
            TT = lambda o, x, y, f_: op("dve", lambda h: h.tensor_tensor(out=o, in0=x, in1=y, op=f_))

            def ssm_mm(c4):
                q0 = 4 * c4
                for ri, (WBx, bank) in enumerate(((WBre, mmA), (WBim, mmC))):
                    for j in range(4):
                        op("pe", lambda h: h.matmul(bank[:, j * 128:(j + 1) * 128], lhsT=WBx[:, q0 + j, :], rhs=uT[:, c4, :], start=True, stop=True))

            def ssm_batch(c4):
                q0 = 4 * c4
                T = [pp4[:, kk] for kk in range(4)]
                Bre = mmA[:, :].rearrange("p (j k) -> p j k", j=4); Bim = mmC[:, :].rearrange("p (j k) -> p j k", j=4)
                if sample:
                    op("act", lambda h: h.activation(out=vv4[:, 0], in_=Bre, func=AF.Copy))
                    op("act", lambda h: h.activation(out=vv4[:, 1], in_=Bim, func=AF.Copy))
                    for ri in range(2):
                        v4 = vv4[:, ri, :, 0:NS].rearrange("p j (s t) -> p j s t", t=4)[:, :, :, 0]
                        TT(v4, v4, ah[:, ri, q0:q0 + 4, :], ALU.add)
                    csq = cs[:, q0:q0 + 4, 1:5].unsqueeze(2).broadcast_to([128, 4, NSEQ, 4])
                    snq = sn[:, q0:q0 + 4, 1:5].unsqueeze(2).broadcast_to([128, 4, NSEQ, 4])
                    v3 = lambda ap: ap[:, :, 0:NS].rearrange("p j (s t) -> p j s t", t=4)
                    Bre, Bim = vv4[:, 0], vv4[:, 1]
                else:
                    csq = cs[:, q0:q0 + 4, 1:129]; snq = sn[:, q0:q0 + 4, 1:129]
                    v3 = lambda ap: ap
                Tv = [v3(t_) for t_ in T]
                RR = [v3(rr4[:, kk]) for kk in range(2)]
                TT(Tv[0], v3(Bre), csq, ALU.mult); TT(Tv[1], v3(Bim), snq, ALU.mult)
                TT(Tv[2], v3(Bim), csq, ALU.mult); TT(Tv[3], v3(Bre), snq, ALU.mult)
                if c4 + 1 < 4:
                    ssm_mm(c4 + 1)
                TT(RR[0], Tv[0], Tv[1], ALU.add); TT(RR[1], Tv[2], Tv[3], ALU.subtract)
                for j in range(4):
                    q = q0 + j
                    if sample:
                        op("dve", lambda h: h.tensor_scalar(out=dtmp[:], in0=mask64[:], scalar1=rho[:, q:q + 1], scalar2=None, op0=ALU.mult))
                    for ri in range(2):
                        if sample:
                            op("dve", lambda h: h.tensor_tensor_scan(out=vv4[:, ri, j, 0:NS], data0=dtmp[:], data1=rr4[:, ri, j, 0:NS], initial=0.0, op0=ALU.mult, op1=ALU.add))
                        else:
                            op("dve", lambda h: h.tensor_tensor_scan(out=vv4[:, ri, j, :], data0=rho[:, q:q + 1].broadcast_to([128, 128]), data1=rr4[:, ri, j, :],
                                                                   initial=car[:, ri, q:q + 1], op0=ALU.mult, op1=ALU.add))
                VV = [v3(vv4[:, kk]) for kk in range(2)]
                TT(Tv[0], VV[0], csq, ALU.mult); TT(Tv[1], VV[1], snq, ALU.mult)
                TT(Tv[2], VV[0], snq, ALU.mult); TT(Tv[3], VV[1], csq, ALU.mult)
                TT(v3(hb[:, :, 0, :]), Tv[0], Tv[1], ALU.subtract); TT(v3(hb[:, :, 1, :]), Tv[2], Tv[3], ALU.add)
                if sample:
                    TT(hfin[:, 0, q0:q0 + 4, 0:NSEQ], Tv[0][:, :, :, 3], Tv[1][:, :, :, 3], ALU.subtract)
                    TT(hfin[:, 1, q0:q0 + 4, 0:NSEQ], Tv[2][:, :, :, 3], Tv[3][:, :, :, 3], ALU.add)
                else:
                    TT(car[:, 0, q0:q0 + 4], Tv[0][:, :, 127], Tv[1][:, :, 127], ALU.subtract)
                    TT(car[:, 1, q0:q0 + 4], Tv[2][:, :, 127], Tv[3][:, :, 127], ALU.add)

            def ssm_y(c4):
                op("pe", lambda h: h.matmul(mmB[:, 0:n], lhsT=dDhi[:, c4, :], rhs=uT[:, c4, 0:n], start=True, stop=False))
                op("pe", lambda h: h.matmul(mmB[:, 0:n], lhsT=dDlo[:, c4, :], rhs=uT[:, c4, 0:n], start=False, stop=False))
                for qq in range(4):
                    q = c4 * 4 + qq
                    op("pe", lambda h: h.matmul(mmB[:, 0:n], lhsT=WCre[:, q, :], rhs=hb[:, qq, 0, 0:n], start=False, stop=False))
                    op("pe", lambda h: h.matmul(mmB[:, 0:n], lhsT=WCimn[:, q, :], rhs=hb[:, qq, 1, 0:n], start=False, stop=(qq == 3)))
                op("act", lambda h: h.activation(out=gl32[:, c4, 0:n], in_=mmB[:, 0:n], func=AF.Gelu))
                op("pool", lambda h: h.tensor_copy(out=glb[:, c4, 0:n], in_=gl32[:, c4, 0:n]))

            ssm_mm(0)
            yield
            for c4_ in range(4):
                ssm_batch(c4_)
                yield
                ssm_y(c4_)
                yield
            if ti == NPT - 1:
                op("act", lambda h: h.activation(out=hfin[:, :, :, NSEQ], in_=car[:], func=AF.Copy), ["car"], ["hfin"])
            for oc in range(4):
                for c4 in range(4):
                    op("pe", lambda h: h.matmul(mmC[:, oc * 128:(oc + 1) * 128], lhsT=wgl[0][:, c4, oc * 128:(oc + 1) * 128], rhs=glb[:, c4, :],
                                                start=(c4 == 0), stop=(c4 == 3)))
            for oc in range(4):
                op("act", lambda h: h.activation(out=sg4[:, oc, :], in_=mmC[:, oc * 128:(oc + 1) * 128], func=AF.Sigmoid, bias=col(BGLU + oc)))
            op("dve", lambda h: h.tensor_tensor(out=gl32[:], in0=gl32[:], in1=sg4, op=ALU.mult))
            op("act", lambda h: h.activation(out=sqb[:], in_=gl32[:], func=AF.Square))
            for oc in range(4):
                op("pe", lambda h, oc=oc: h.matmul(mmC[:, 0:n], lhsT=onesb[:], rhs=sqb[:, oc, 0:n], start=(oc == 0), stop=(oc == 3)), ["sqb", "onesb"], ["mmC"])
            op("act", lambda h: h.activation(out=rsb[:, 0:n], in_=mmC[:, 0:n], func=AF.Ln, scale=1.0 / 512, bias=epsb[:, 0:1]))
            op("act", lambda h: h.activation(out=rsb[:, 0:n], in_=rsb[:, 0:n], func=AF.Exp, scale=-0.5))
            for oc in range(4):
                op("dve", lambda h, oc=oc: h.scalar_tensor_tensor(out=mixT[:, 4 + oc, 0:n], in0=gl32[:, oc, 0:n], scalar=col(GSSM + oc), in1=rsb[:, 0:n], op0=ALU.mult, op1=ALU.mult),
                   ["gl32", "rsb", "pcol"], ["mixT"])

            yield "join"
            for hf in range(2):
                acc, ak = (acc0, "acc0") if hf == 0 else (acc1, "acc1")
                for c in range(8):
                    op("pe", lambda h, hf=hf, c=c, acc=acc: h.matmul(acc[0:n, :], lhsT=mixT[:, c, 0:n], rhs=wo[0][:, c, hf * 512:(hf + 1) * 512], start=(c == 0), stop=(c == 7)),
                       ["mixT", "wo"], [ak])
                op("dve", lambda h, hf=hf, acc=acc: h.tensor_tensor(out=xt[0:n, hf * 512:(hf + 1) * 512], in0=xt[0:n, hf * 512:(hf + 1) * 512], in1=acc[0:n, :], op=ALU.add),
                   ["xt", ak], ["xt"])
            rms(xt[0:n, :], "xt", D, 2, 1.0 / D, n)
            op("act", lambda h: h.activation(out=xn[0:n, :], in_=xt[0:n, :], func=AF.Copy, scale=rstd[0:n, 2:3]))
            for c in range(8):
                op("pe", lambda h, c=c: h.transpose(out=pT[:, c, 0:n], in_=xn[0:n, c * 128:(c + 1) * 128], identity=idb[0:n, 0:n]), ["xn", "idb"], ["pT"])
            op("dve", lambda h: h.tensor_tensor(out=xn2T[:, :, tl * 128:(tl + 1) * 128], in0=pT[:], in1=pcol[:, GFFN:GFFN + 8].unsqueeze(2).broadcast_to([128, 8, 128]), op=ALU.mult))

        def ffn(tiles_):
            sample = (tiles_[0] == NPT)
            ntl = len(tiles_)
            nb = ntl * 128
            hTv = hTsm if (sample and ntl == 1) else hTall
            p0 = 128 if sample else 0
            npc = nb - p0
            for ch in range(NFC):
                b3 = ch % 2
                w3 = ch % 3
                P.dma("sp", lambda h: h.dma_start(out=wg[w3][:], in_=wgb_d[ch]))
                P.dma("sp", lambda h: h.dma_start(out=wu[w3][:], in_=wub_d[ch]))
                gps, ups = (mmA, mmB) if b3 == 0 else (mmC, pT32)
                for c in range(8):
                    op("pe", lambda h, c=c, b3=b3, gps=gps: h.matmul(gps[:, 0:nb], lhsT=wg[w3][:, c, :], rhs=xn2T[:, c, 0:nb], start=(c == 0), stop=(c == 7)))
                for c in range(8):
                    op("pe", lambda h, c=c, b3=b3, ups=ups: h.matmul(ups[:, 0:nb], lhsT=wu[w3][:, c, :], rhs=xn2T[:, c, 0:nb], start=(c == 0), stop=(c == 7)))
                w0, w1, w2, bb = col(CVW + ch * 3), col(CVW + ch * 3 + 1), col(CVW + ch * 3 + 2), col(CVB + ch)
                cvb, slb, gxb = cv[:, 0, :], sl[:, 0, :], gx[:, b3, :]
                def conv3(g0, g1, g2, cvv):
                    op("dve", lambda h: h.tensor_scalar(out=cvv, in0=g2, scalar1=w2, scalar2=bb, op0=ALU.mult, op1=ALU.add))
                    op("dve", lambda h: h.scalar_tensor_tensor(out=cvv, in0=g1, scalar=w1, in1=cvv, op0=ALU.mult, op1=ALU.add))
                    op("dve", lambda h: h.scalar_tensor_tensor(out=cvv, in0=g0, scalar=w0, in1=cvv, op0=ALU.mult, op1=ALU.add))

                if sample:
                    op("pool", lambda h: h.tensor_copy(out=gxs[:, :, 0:2], in_=cvst[:, ch, :, :]))
                    op("act", lambda h: h.activation(out=gxs[:, :, 2:6], in_=gps[:, 0:NS].rearrange("p (s t) -> p s t", t=4), func=AF.Copy))
                    op("pool", lambda h: h.tensor_copy(out=sconv[:, ch, :, :], in_=gxs[:, :, 4:6]))
                    conv3(gxs[:, :, 0:4], gxs[:, :, 1:5], gxs[:, :, 2:6], cvb[:, 0:NS].rearrange("p (s t) -> p s t", t=4))
                if npc:
                    op("pool", lambda h: h.tensor_copy(out=gxb[:, 0:2], in_=gcar[:, ch, :]))
                    op("act", lambda h: h.activation(out=gxb[:, 2:2 + npc], in_=gps[:, p0:nb], func=AF.Copy))
                    op("pool", lambda h: h.tensor_copy(out=gcar[:, ch, :], in_=gxb[:, npc:npc + 2]))
                    conv3(gxb[:, 0:npc], gxb[:, 1:1 + npc], gxb[:, 2:2 + npc], cvb[:, p0:nb])
                op("act", lambda h, cvb=cvb, slb=slb: h.activation(out=slb[:, 0:nb], in_=cvb[:, 0:nb], func=AF.Silu))
                op("dve", lambda h, ch=ch, slb=slb: h.tensor_tensor(out=hTv[:, ch, 0:nb], in0=slb[:, 0:nb], in1=ups[:, 0:nb], op=ALU.mult))
            accs = [acc0, acc1, Sps, Ops]
            for hf in range(2):
                for g in range(NFC // 2):
                    wdb = wd[g % 3]
                    P.dma("sp", lambda h: h.dma_start(out=wdb, in_=wdb_d[hf, g]))
                    for j in range(2):
                        ch = 2 * g + j
                        for tl in range(ntl):
                            op("pe", lambda h: h.matmul(accs[tl][:, :], lhsT=hTv[:, ch, tl * 128:(tl + 1) * 128], rhs=wdb[:, j, :],
                                                        start=(ch == 0), stop=(ch == NFC - 1)))
                for tl in range(ntl):
                    op("dve", lambda h, tl=tl, hf=hf: h.tensor_tensor(out=xb[:, tl, hf * 512:(hf + 1) * 512], in0=xb[:, tl, hf * 512:(hf + 1) * 512], in1=accs[tl][:, :], op=ALU.add))

        def ffn_tail(tiles_, nxt_, first, last):
            for tl in range(first, last):
                ti = tiles_[tl]
                xt = xb[:, tl, :]
                rms(xt, "xt", D, 3, 1.0 / D, 128, junk=gl32[:].rearrange("p a k -> p (a k)").bitcast(BF16))
                op("dve", lambda h: h.scalar_tensor_tensor(out=xt, in0=xt, scalar=rstd[:, 3:4], in1=gfb[:], op0=ALU.mult, op1=ALU.mult))
                P.dma("sp", lambda h: h.dma_start(out=y_d[ti * 128:(ti + 1) * 128, :], in_=xt))
                if nxt_ is not None and tl < len(nxt_) and (tl == 0 or NPT not in nxt_):
                    load_x(tl, nxt_[tl])
                yield

        def est_dur(kind, eng, fn):
            rec = _Recorder()
            fn(rec)
            name, args, kwargs = rec.call
            o = kwargs.get("out", args[0] if args else None)
            nfree = 1
            if o is not None and hasattr(o, "shape"):
                for d_ in list(o.shape)[1:]:
                    nfree *= int(d_)
            if kind == "dma":
                return 100.0, 2500.0
            if eng == "dve":
                d = ((2 * nfree if name == "tensor_tensor_scan" else nfree) + 151) / 0.96
            elif eng == "act":
                d = (nfree + 230) / 1.2 + (90 if kwargs.get("accum_out") is not None else 0)
            elif eng == "pool":
                d = (2 * nfree + 250) / 1.2
            else:
                d = max(64, nfree) / 1.9 + 25
            return d, d

        def merge_threads(gens):
            gens = [g for g in gens if g is not None]
            bufs = [[] for _ in gens]
            alive = [True] * len(gens)
            tchain = [sched_now[0]] * len(gens)
            joined = [False] * len(gens)
            while True:
                for i, g in enumerate(gens):
                    while alive[i] and not bufs[i] and not joined[i]:
                        P.capture = bufs[i]
                        r_ = next(g, "done")
                        P.capture = None
                        if r_ == "done":
                            alive[i] = False
                        elif r_ == "join":
                            joined[i] = True
                for i in range(len(gens)):
                    if joined[i] and not bufs[i] and not any((alive[j] or bufs[j]) for j in range(len(gens)) if j != i):
                        joined[i] = False
                if not any(bufs) and any(alive):
                    continue
                cands = [i for i in range(len(gens)) if bufs[i]]
                if not cands:
                    break
                best, best_t = None, None
                for i in cands:
                    kind, eng, fn = bufs[i][0]
                    t = max(eng_free.get(eng, 0.0), tchain[i] + 250.0)
                    if best is None or t < best_t:
                        best, best_t = i, t
                kind, eng, fn = bufs[best].pop(0)
                busy, lat = est_dur(kind, eng, fn)
                eng_free[eng] = best_t + busy
                tchain[best] = best_t + lat
                sched_now[0] = max(sched_now[0], best_t)
                (P.dma if kind == "dma" else P.op)(eng, fn)

        eng_free = {}
        ffn_cast_issued = [False]
        sched_now = [0.0]
        if tiles is None:
            blocks = [[0, 1, 2, 3], [4, 5, 6, 7], [8, 9, 10, 11], [12, 13, 14, 15], [NPT, 16]]
        else:
            blocks = tiles
        loaded = set()

        def load_x(tl, ti):
            if ti not in loaded:
                loaded.add(ti)
                P.dma("sp", lambda h: h.dma_start(out=xb[:, tl, :], in_=xin[ti * 128:(ti + 1) * 128, :]))

        pre_done = False
        for bi, blk_ in enumerate(blocks):
            lazy = NPT in blk_
            for tl, ti in enumerate(blk_):
                if tl == 0 or not lazy:
                    load_x(tl, ti)
            if not pre_done:
                for _ in partA(blk_[0], 0):
                    pass
            if not ffn_cast_issued[0]:
                ffn_cast_issued[0] = True
                for a_ in range(0, NFC, 11):
                    P.dma("pool", lambda h: h.dma_start(out=wgb_d[a_:a_ + 11], in_=w_gate[a_:a_ + 11]))
                    P.dma("pool", lambda h: h.dma_start(out=wub_d[a_:a_ + 11], in_=w_up[a_:a_ + 11]))
                for hf_ in range(2):
                    P.dma("pool", lambda h: h.dma_start(out=wdb_d[hf_], in_=w_down[hf_]))
            for tl, ti in enumerate(blk_):
                gb = partB(ti, tl)
                ga = partA(blk_[tl + 1], tl + 1) if tl + 1 < len(blk_) else None
                if ga is not None and lazy:
                    load_x(tl + 1, blk_[tl + 1])
                merge_threads([gb, ga])
            ffn(blk_)
            nxt = blocks[bi + 1] if bi + 1 < len(blocks) else None
            for _ in ffn_tail(blk_, nxt, 0, 1):
                pass
            if nxt is not None and NPT not in nxt:
                merge_threads([ffn_tail(blk_, nxt, 1, len(blk_)), partA(nxt[0], 0)])
                pre_done = True
            else:
                for _ in ffn_tail(blk_, nxt, 1, len(blk_)):
                    pass
                pre_done = False

        fv = pp4[:].rearrange("p a j k -> p (a j k)")
        v1 = fv[:, 0:272].rearrange("p (q s) -> p q s", q=16); v2 = fv[:, 272:544].rearrange("p (q s) -> p q s", q=16); v3_ = fv[:, 544:816].rearrange("p (q s) -> p q s", q=16)
        fre_c = fre[:].unsqueeze(2).broadcast_to([128, 16, NSEQ + 1]); fim_c = fim[:].unsqueeze(2).broadcast_to([128, 16, NSEQ + 1])
        TT(v1, hfin[:, 0], fre_c, ALU.mult); TT(v2, hfin[:, 1], fim_c, ALU.mult); TT(v3_, v1, v2, ALU.subtract)
        TT(v1, hfin[:, 0], fim_c, ALU.mult); TT(v2, hfin[:, 1], fre_c, ALU.mult); TT(hfin[:, 1], v1, v2, ALU.add)
        op("dve", lambda h: h.tensor_copy(out=hfin[:, 0], in_=v3_))
        P.dma("sp", lambda h: h.dma_start(out=hfin_d, in_=hfin[:]), reads=["hfin"], writes=["hfin_d"])
        P.dma("sp", lambda h: h.dma_start(out=pconv_d, in_=gcar[:]), reads=["gcar"], writes=["pconv_d"])
        P.dma("sp", lambda h: h.dma_start(out=sconv_d, in_=sconv[:]), reads=["sconv"], writes=["sconv_d"])
        P.limit = None
        P.barrier()

        with nc.Block() as block:
            @block.tensor
            def _(h):
                P.run("pe", h, sems)

            @block.scalar
            def _(h):
                P.run("act", h, sems)

            @block.vector
            def _(h):
                P.run("dve", h, sems)

            @block.gpsimd
            def _(h):
                P.run("pool", h, sems)

            @block.sync
            def _(h):
                P.run("sp", h, sems)
    return nc


def _consts():
    ident = np.eye(128, dtype=np.float32)
    i = np.arange(128)[:, None]
    c = np.arange(256)[None, :]
    full = np.where(((c < 128) & (c > i)) | ((c >= 128) & (c - 128 <= i)), 0.0, MASKV)
    m1 = np.where(((c < 128) & (c > i) & (c >= NPAD)) | ((c >= 128) & (c - 128 <= i)), 0.0, MASKV)
    m0 = np.where((c >= 128) & (c - 128 <= i) & (c - 128 >= NPAD), 0.0, MASKV)
    masks = np.stack([m0, m1, full], axis=1).astype(np.float32)
    sm = np.full((128, 17 * 128), MASKV, np.float32)
    for s in range(NSEQ):
        for t in range(4):
            r = s * 4 + t
            sm[r, s * 128 + t + 1:(s + 1) * 128] = 0.0
            sm[r, 2048 + s * 4:2048 + s * 4 + t + 1] = 0.0
    return ident, masks, sm


def prep_inputs(x_prompt, x_sample, cache_k_win, cache_v_win, state_ssm_re, state_ssm_im, state_conv,
           meta_tokens, g_mix, w_in, sinks, lam_re, lam_im, log_dt, b_re, b_im, c_re, c_im, d_skip,
           w_glu, b_glu, g_attn_out, g_ssm_out, w_o, g_ffn, w_gate, w_up, conv_w, conv_b, w_down,
           g_final):
    f32 = np.float32
    A = lambda a: np.ascontiguousarray(np.asarray(a, dtype=f32))
    x_prompt, x_sample = A(x_prompt), A(x_sample)
    ident, masks, smask = _consts()
    w_in0 = A(w_in)[0]
    perm = []
    for i in range(4):
        perm += list(range(i * 64, (i + 1) * 64)) + list(range((4 + i) * 64, (5 + i) * 64))
    w_in_p = np.ascontiguousarray(np.concatenate([w_in0[:, perm], w_in0[:, 512:]], axis=1))
    pcol = np.zeros((128, 128), f32)
    pcol[:, 0:8] = A(g_mix)[0].reshape(8, 128).T
    pcol[:, 8:16] = A(g_ffn)[0].reshape(8, 128).T
    pcol[:, 16:20] = A(g_attn_out)[0].reshape(4, 128).T
    pcol[:, 20:24] = A(g_ssm_out)[0].reshape(4, 128).T
    pcol[:, 24:28] = A(b_glu)[0].reshape(4, 128).T
    pcol[:, 28:32] = A(d_skip)[0].reshape(4, 128).T
    cw = A(conv_w)[0].reshape(3, NFC, 128)
    pcol[:, 32:98] = cw.transpose(2, 1, 0).reshape(128, 66)
    pcol[:, 98:120] = A(conv_b)[0].reshape(NFC, 128).T
    lr, li, ld = A(lam_re)[0], A(lam_im)[0], A(log_dt)[0]
    ldx = np.repeat(ld[:, None], 64, axis=1)

    def pl(a):
        return a.reshape(16, 2, 64).transpose(1, 2, 0).reshape(128, 16)

    lam = np.ascontiguousarray(np.stack([pl(lr), pl(li), pl(ldx)], axis=1))
    lamb = np.ascontiguousarray(np.stack([lr.reshape(-1), li.reshape(-1), ldx.reshape(-1)], axis=0))
    bre, bim, cre, cim = A(b_re)[0], A(b_im)[0], A(c_re)[0], A(c_im)[0]
    bblk = np.zeros((128, 2, 16, 128), f32)
    cblk = np.zeros((128, 2, 16, 128), f32)
    for q in range(16):
        for j2 in range(2):
            g = 2 * q + j2
            g8 = g % 8
            rows = slice(g8 * 16, g8 * 16 + 16)
            cols = slice(j2 * 64, j2 * 64 + 64)
            bblk[rows, 0, q, cols] = bre[g].T
            bblk[rows, 1, q, cols] = bim[g].T
            cblk[cols, 0, q, rows] = cre[g].T
            cblk[cols, 1, q, rows] = cim[g].T
    sre, sim_ = A(state_ssm_re)[0], A(state_ssm_im)[0]
    ck, cvv = A(cache_k_win)[0].reshape(128, 128, 128), A(cache_v_win)[0].reshape(128, 128, 128)
    sc = A(state_conv)[0]
    meta = A(meta_tokens)
    wg_l = np.ascontiguousarray(A(w_gate)[0].reshape(8, 128, NFC, 128).transpose(2, 1, 0, 3))
    wu_l = np.ascontiguousarray(A(w_up)[0].reshape(8, 128, NFC, 128).transpose(2, 1, 0, 3))
    wd_l = np.ascontiguousarray(A(w_down)[0].reshape(NFC // 2, 2, 128, 2, 512).transpose(3, 0, 2, 1, 4))
    in_maps = []
    for c in range(NCORES):
        xin = np.zeros((NPT * 128 + 128, D), f32)
        xin[NPAD:128] = meta
        xin[128:NPT * 128] = x_prompt[c]
        xin[NPT * 128:NPT * 128 + NS] = x_sample[c * NSEQ:(c + 1) * NSEQ].reshape(NS, D)
        sl_ = slice(c * NSEQ, (c + 1) * NSEQ)

        def hl(a):
            return a.reshape(NSEQ, 16, 2, 64).transpose(2, 3, 1, 0).reshape(128, 16, NSEQ)

        h0 = np.ascontiguousarray(np.stack([hl(sre[sl_]), hl(sim_[sl_])], axis=1))
        cvst = np.ascontiguousarray(sc[sl_].reshape(NSEQ, 2, NFC, 128).transpose(3, 2, 0, 1))
        kc, vc = ck[sl_], cvv[sl_]
        kcT = np.ascontiguousarray(kc.transpose(2, 0, 1).reshape(128, NSEQ * 128))
        vcl = np.ascontiguousarray(vc.transpose(1, 0, 2))
        in_maps.append(dict(
            xin=xin, w_in=w_in_p, w_o=A(w_o)[0], w_glu=A(w_glu)[0], w_gate=wg_l, w_up=wu_l,
            w_down=wd_l, ident=ident, masks=masks, smask=smask, pcol=pcol, sinks=A(sinks)[0],
            gfin=A(g_final), lam=lam, lamb=lamb, bblk=bblk, cblk=cblk, h0=h0, cvst=cvst, kcT=kcT, vc=vcl,
            kcache=np.ascontiguousarray(kc), vcache=np.ascontiguousarray(vc)))
    return in_maps


def assemble(R):
    f32 = np.float32
    y_prompt = np.stack([R[c]["y"][128:NPT * 128] for c in range(NCORES)])
    y_sample = np.concatenate([R[c]["y"][NPT * 128:NPT * 128 + NS].reshape(NSEQ, 4, D) for c in range(NCORES)])
    p_k = np.stack([R[c]["kvp"][:, 0:128].reshape(128, 2, 64) for c in range(NCORES)])[None]
    p_v = np.stack([R[c]["kvp"][:, 128:256].reshape(128, 2, 64) for c in range(NCORES)])[None]
    s_k = np.concatenate([R[c]["kvs_k"].reshape(NSEQ, 128, 2, 64) for c in range(NCORES)])[None]
    s_v = np.concatenate([R[c]["kvs_v"].reshape(NSEQ, 128, 2, 64) for c in range(NCORES)])[None]

    def unh(a):
        nn = a.shape[-1]
        return a.reshape(2, 64, 16, nn).transpose(3, 2, 0, 1).reshape(nn, 32, 64)

    p_re = np.stack([unh(R[c]["hfin"][:, 0, :, NSEQ:])[0] for c in range(NCORES)])[None]
    p_im = np.stack([unh(R[c]["hfin"][:, 1, :, NSEQ:])[0] for c in range(NCORES)])[None]
    s_re = np.concatenate([unh(R[c]["hfin"][:, 0, :, :NSEQ]) for c in range(NCORES)])[None]
    s_im = np.concatenate([unh(R[c]["hfin"][:, 1, :, :NSEQ]) for c in range(NCORES)])[None]
    p_conv = np.stack([R[c]["pconv"].transpose(2, 1, 0).reshape(2, FF) for c in range(NCORES)])[None]
    s_conv = np.concatenate([R[c]["sconv"].transpose(2, 3, 1, 0).reshape(NSEQ, 2, FF) for c in range(NCORES)])[None]
    out = (y_prompt, y_sample, p_k, p_v, p_re, p_im, p_conv, s_k, s_v, s_re, s_im, s_conv)
    return tuple(np.ascontiguousarray(o, dtype=f32) for o in out)


def kernel(**inputs):
    in_maps = prep_inputs(**inputs)
    nc = build_nc()
    res = run_bass_kernel_spmd(nc, in_maps, core_ids=list(range(NCORES)))
    return assemble(res.results)
```

```python
import numpy as np
from contextlib import ExitStack
import concourse.bass as bass
import concourse.mybir as mybir
from concourse.bass_utils import run_bass_kernel_spmd

F32 = mybir.dt.float32
BF16 = mybir.dt.bfloat16
ALU = mybir.AluOpType
AF = mybir.ActivationFunctionType
AX = mybir.AxisListType

ENGS = ("pe", "act", "dve", "pool", "sp")
NDMASEM = 12
NCORES = 8
D = 1024
NPT = 17
NPAD = 112
NS = 64
NSEQ = 16
FF = 2816
NFC = 22
EPS = 1e-5
PI = float(np.pi)
MASKV = -30000.0


class _Recorder:
    def __init__(self):
        self.call = None

    def __getattr__(self, name):
        def f(*args, **kwargs):
            self.call = (name, args, kwargs)
            return self
        return f


class Prog:
    def __init__(self):
        self.streams = {e: [] for e in ENGS}
        self.count = {e: 0 for e in ENGS}
        self.waited = {e: {} for e in ENGS}
        self.regs = {}
        self.nrec = 0
        self.limit = None
        self.capture = None
        self.dma_rr = {"sp": 0, "pool": 0}
        self.dma_cnt = {}

    @staticmethod
    def _region(ap):
        dims = [(int(st), int(sz)) for st, sz in ap.ap]
        off = int(ap.offset)
        esz = mybir.dt.size(ap.dtype)
        space = str(ap.space)
        if space == "DRAM":
            ext = sum((sz - 1) * abs(st) for st, sz in dims)
            return (0, 1, off, off + ext)
        pst, npart = dims[0]
        pst = max(pst, 1)
        p0, f0 = off // pst, off % pst
        ext = sum((sz - 1) * abs(st) for st, sz in dims[1:])
        f0, ext = f0 * esz, ext * esz + esz - 1
        if space == "PSUM":
            return (0, 128, 0, 1 << 30)
        return (p0, p0 + npart, f0, f0 + ext)

    @staticmethod
    def _is_ap(v):
        return hasattr(v, "tensor") and hasattr(v, "ap") and hasattr(v, "offset")

    def _record(self, fn):
        rec = _Recorder()
        fn(rec)
        name, args, kwargs = rec.call
        acc = []
        for i, a in enumerate(args):
            if self._is_ap(a):
                acc.append((a, i == 0))
        for k, v in kwargs.items():
            if self._is_ap(v):
                acc.append((v, k in ("out", "accum_out")))
        out = []
        for ap, w in acc:
            if str(ap.space) == "PSUM":
                w = True
            out.append((ap.tensor.name, self._region(ap), w))
        self._last_call = rec.call
        return out

    def _deps(self, eng, acc):
        deps = {}
        for name, R, w in acc:
            for (R2, w2), evs in self.regs.get(name, {}).items():
                if not (w or w2):
                    continue
                if R[0] < R2[1] and R2[0] < R[1] and R[2] <= R2[3] and R2[2] <= R[3]:
                    for s_, v in evs.items():
                        if s_ == eng and eng == "pe":
                            continue
                        if deps.get(s_, 0) < v:
                            deps[s_] = v
        out = []
        wd = self.waited[eng]
        for s_, v in deps.items():
            if wd.get(s_, 0) < v:
                wd[s_] = v
                out.append((s_, v))
        return out

    def _commit(self, ev, acc):
        s_, v = ev
        for name, R, w in acc:
            d = self.regs.setdefault(name, {})
            if w:
                for key in [k for k in d if k[0][0] >= R[0] and k[0][1] <= R[1] and k[0][2] >= R[2] and k[0][3] <= R[3]]:
                    del d[key]
            e = d.setdefault((R, w), {})
            if e.get(s_, 0) < v:
                e[s_] = v

    @staticmethod
    def _freeze(fn):
        rec = _Recorder()
        fn(rec)
        return lambda h, c=rec.call: getattr(h, c[0])(*c[1], **c[2])

    def op(self, eng, fn, reads=(), writes=()):
        if self.capture is not None:
            self.capture.append(("op", eng, self._freeze(fn)))
            return
        self.nrec += 1
        if self.limit is not None and self.nrec > self.limit:
            return
        acc = self._record(fn)
        deps = self._deps(eng, acc)
        self.count[eng] += 1
        ev = (eng, self.count[eng])
        self.streams[eng].append((deps, self._last_call, (eng, 1)))
        self._commit(ev, acc)

    def dma(self, eng, fn, reads=(), writes=()):
        if self.capture is not None:
            self.capture.append(("dma", eng, self._freeze(fn)))
            return
        self.nrec += 1
        if self.limit is not None and self.nrec > self.limit:
            return
        acc = self._record(fn)
        k = self.dma_rr[eng]
        self.dma_rr[eng] = (k + 1) % NDMASEM
        sname = "dma%s%d" % (eng, k)
        deps = self._deps(eng, acc)
        prev = self.dma_cnt.get(sname, 0) * 16
        if prev and self.waited[eng].get(sname, 0) < prev:
            self.waited[eng][sname] = prev
            deps.append((sname, prev))
        self.dma_cnt[sname] = self.dma_cnt.get(sname, 0) + 1
        ev = (sname, self.dma_cnt[sname] * 16)
        self.streams[eng].append((deps, self._last_call, (sname, 16)))
        self._commit(ev, acc)

    def barrier(self):
        evs = [(e, self.count[e]) for e in ENGS if self.count[e]]
        evs += [(k, c * 16) for k, c in self.dma_cnt.items()]
        for e in ENGS:
            deps = []
            for s, v in evs:
                if s == e:
                    continue
                if self.waited[e].get(s, 0) < v:
                    self.waited[e][s] = v
                    deps.append((s, v))
            if deps:
                self.streams[e].append((deps, None, None))

    def run(self, eng, h, sems):
        for deps, fn, inc in self.streams[eng]:
            for s, v in deps:
                h.wait_ge(sems[s], v)
            if fn is not None:
                name, args, kwargs = fn
                getattr(h, name)(*args, **kwargs).then_inc(sems[inc[0]], inc[1])


def build_nc(tiles=None, limit=None):
    nc = bass.Bass("TRN2", target_bir_lowering=False)

    def din(name, shape, dt=F32):
        return nc.dram_tensor(name, list(shape), dt, kind="ExternalInput").ap()

    def dout(name, shape, dt=F32):
        return nc.dram_tensor(name, list(shape), dt, kind="ExternalOutput").ap()

    xin = din("xin", [NPT * 128 + 128, D])
    w_in = din("w_in", [D, 1280])
    w_o = din("w_o", [D, D])
    w_glu = din("w_glu", [512, 512])
    w_gate = din("w_gate", [NFC, 128, 8, 128])
    w_up = din("w_up", [NFC, 128, 8, 128])
    w_down = din("w_down", [2, NFC // 2, 128, 2, 512])
    ident_d = din("ident", [128, 128])
    masks_d = din("masks", [128, 3, 256])
    smask_d = din("smask", [128, 17 * 128])
    pcol_d = din("pcol", [128, 128])
    sinks_d = din("sinks", [8])
    gfin_d = din("gfin", [D])
    lam_d = din("lam", [128, 3, 16])
    lamb_d = din("lamb", [3, 16 * 128])
    bblk_d = din("bblk", [128, 2, 16, 128])
    cblk_d = din("cblk", [128, 2, 16, 128])
    h0_d = din("h0", [128, 2, 16, NSEQ])
    cvst_d = din("cvst", [128, NFC, NSEQ, 2])
    kcT_d = din("kcT", [128, NSEQ * 128])
    vc_d = din("vc", [128, NSEQ, 128])
    kcache_d = din("kcache", [NSEQ, 128, 128])
    vcache_d = din("vcache", [NSEQ, 128, 128])

    wgb_d = nc.dram_tensor("wgb", [NFC, 128, 8, 128], BF16, kind="Internal").ap()
    wub_d = nc.dram_tensor("wub", [NFC, 128, 8, 128], BF16, kind="Internal").ap()
    wdb_d = nc.dram_tensor("wdb", [2, NFC // 2, 128, 2, 512], BF16, kind="Internal").ap()
    y_d = dout("y", [NPT * 128 + 128, D])
    kvp_d = dout("kvp", [128, 256])
    kvs_k = dout("kvs_k", [NSEQ, 128, 128])
    kvs_v = dout("kvs_v", [NSEQ, 128, 128])
    hfin_d = dout("hfin", [128, 2, 16, NSEQ + 1])
    pconv_d = dout("pconv", [128, NFC, 2])
    sconv_d = dout("sconv", [128, NFC, NSEQ, 2])

    P = Prog()
    P.limit = limit
    es = ExitStack()
    with es:
        def sb(name, shape, dt=F32):
            return es.enter_context(nc.sbuf_tensor(name, list(shape), dt))

        def ps(name, shape, dt=F32):
            return es.enter_context(nc.psum_tensor(name, list(shape), dt))

        sems = {e: es.enter_context(nc.semaphore("s_" + e)) for e in ENGS}
        for k in range(NDMASEM):
            for e_ in ("sp", "pool"):
                sems["dma%s%d" % (e_, k)] = es.enter_context(nc.semaphore("s_dma%s%d" % (e_, k)))

        idb = sb("idb", [128, 128], BF16)
        onesb = sb("onesb", [128, 128], BF16)
        masks = sb("masks_s", [128, 3, 256])
        smask = sb("smask_s", [128, 17 * 128], BF16)
        pcol = sb("pcol_s", [128, 128])
        sink8 = sb("sink8", [128, 8])
        gfb = sb("gfb", [128, D])
        epsb = sb("epsb", [128, 1])
        lam = sb("lam_s", [128, 3, 16])
        WBre = sb("WBre", [128, 16, 128], BF16); WBim = sb("WBim", [128, 16, 128], BF16)
        WCre = sb("WCre", [128, 16, 128], BF16); WCimn = sb("WCimn", [128, 16, 128], BF16)
        cs = sb("cs", [128, 16, 129]); sn = sb("sn", [128, 16, 129])
        mask64 = sb("mask64", [128, NS]); dtmp = sb("dtmp", [128, NS])
        are = sb("are", [128, 16]); aim = sb("aim", [128, 16])
        ah = sb("ah", [128, 2, 16, NSEQ])
        car = sb("car", [128, 2, 16])
        hfin = sb("hfin_s", [128, 2, 16, NSEQ + 1])
        gcar = sb("gcar", [128, NFC, 2])
        sconv = sb("sconv_s", [128, NFC, NSEQ, 2])
        kTx = sb("kTx", [128, 2, 256], BF16)
        vx = sb("vx", [128, 2, 128], BF16)
        GMIX, GFFN, GATT, GSSM, BGLU, DSK, CVW, CVB = 0, 8, 16, 20, 24, 28, 32, 98

        def col(c):
            return pcol[:, c:c + 1]

        ssq = sb("ssq", [128, 4]); rstd = sb("rstd", [128, 4])
        xn = sb("xn", [128, D], BF16)
        junk = xn
        xnT = sb("xnT", [128, 8, 128], BF16)
        qT = sb("qT", [128, 4, 128], BF16)
        uT2 = [sb("uT%d" % i, [128, 4, 128], BF16) for i in range(2)]
        Sx = sb("Sx", [128, 1, 2, 257]); mx = sb("mx", [128, 2]); nbias = sb("nbias", [128, 2])
        rs = sb("rs", [128, 2]); rinv = sb("rinv", [128, 2]); esink = sb("esink", [128, 2])
        maskb = sb("maskb", [128, 3, 256], BF16); sink8x = sb("sink8x", [128, 4, 2])
        dDhi = sb("dDhi", [128, 4, 128], BF16); dDlo = sb("dDlo", [128, 4, 128], BF16)
        Pb = sb("Pb", [128, 2, 257], BF16); PTs = sb("PTs", [128, 4, 128], BF16)
        mixT2 = [sb("mixT%d" % i, [128, 8, 128], BF16) for i in range(2)]
        mixT = mixT2[0]
        Psb = sb("Psb", [128, 17 * 128 + 1], BF16)
        PTss = sb("PTss", [128, 17, 128], BF16)
        pp4 = sb("pp4", [128, 4, 4, 128]); rr4 = sb("rr4", [128, 2, 4, 128]); vv4 = sb("vv4", [128, 2, 4, 128])
        hb = sb("hb", [128, 4, 2, 128], BF16)
        gl32 = sb("gl32", [128, 4, 128]); glb = sb("glb", [128, 4, 128], BF16)
        xn2T = sb("xn2T", [128, 8, 512], BF16)
        gx = sb("gx", [128, 2, 514]); gxs = sb("gxs", [128, NSEQ, 6])
        cv = sb("cv", [128, 1, 512]); sl = sb("sl", [128, 1, 512])
        kvtok = sl[:, 0, 0:256]
        yv = gx[:, 0, 0:128]; sg = gx[:, 0, 128:256]; rsb = gx[:, 0, 256:384]
        sqb = gx[:, 1, 0:256].bitcast(BF16).rearrange("p (a k) -> p a k", a=4)
        attn = cv[:, 0, :]
        anb = sl[:, 0, 256:512].bitcast(BF16)
        wi = [sb("wi0", [128, 8, 1280], BF16)] * 2
        wo = [sb("wo0", [128, 8, D], BF16)] * 2
        wgl = [sb("wgl0", [128, 4, 512], BF16)] * 2
        wg = [sb("wg%d" % i, [128, 8, 128], BF16) for i in range(2)] + [xnT]
        wu = [sb("wu%d" % i, [128, 8, 128], BF16) for i in range(2)] + [mixT]
        wdA = sb("wdA", [128, 2, 512], BF16)
        wd = [wdA[:], Sx[:].rearrange("p a b c -> p (a b c)").bitcast(BF16)[:, 0:1024].rearrange("p (j n) -> p j n", j=2),
              xn[:].rearrange("p (j n) -> p j n", j=2)]
        big = sb("big", [128, 9728])
        xb = big[:, 0:4096].rearrange("p (t d) -> p t d", t=4)
        hTall = big[:, 4096:9728].bitcast(BF16).rearrange("p (c n) -> p c n", c=NFC)
        hTsm = big[:, 4096:5504].bitcast(BF16).rearrange("p (c n) -> p c n", c=NFC)
        kTs = big[:, 5504:7680].bitcast(BF16).rearrange("p (a k) -> p a k", a=2)
        vs = big[:, 7680:8768].bitcast(BF16).rearrange("p (b k) -> p b k", b=17)
        scr = big
        h0 = big[:, 8192:8704].rearrange("p (a q s) -> p a q s", a=2, q=16)
        cvst = big[:, 3264:3968].rearrange("p (c s j) -> p c s j", c=NFC, s=NSEQ)
        def prep_views(o):
            return (scr[:, o:o + 768].rearrange("p (a b) -> p a b", a=3), scr[:, o + 768:o + 2304].rearrange("p (a b) -> p a b", a=6),
                    scr[:, o + 2304:o + 2816].rearrange("p (a q m) -> p a q m", a=2, q=2), scr[:, o + 2816:o + 3328].rearrange("p (a q m) -> p a q m", a=2, q=2))
        Ssx = big[:, 1024:1024 + 17 * 128 + 1]
        idf = big[:, 8960:9088]

        mmA = ps("mmA", [128, 512]); mmB = ps("mmB", [128, 512]); mmC = ps("mmC", [128, 512])
        pT = ps("pT", [128, 8, 128], BF16)
        pT32 = pT[:].rearrange("p c k -> p (c k)").bitcast(F32)
        Sps = ps("Sps", [128, 512]); Ops = ps("Ops", [128, 512])
        acc0 = ps("acc0", [128, 512]); acc1 = ps("acc1", [128, 512])

        op = P.op

        P.dma("sp", lambda h: h.dma_start(out=idf[:], in_=ident_d), writes=["idf"])
        P.dma("sp", lambda h: h.dma_start(out=masks[:], in_=masks_d), writes=["masks"])
        P.dma("pool", lambda h: h.dma_start(out=smask[:], in_=smask_d), writes=["smask"])
        P.dma("sp", lambda h: h.dma_start(out=pcol[:], in_=pcol_d), writes=["pcol"])
        P.dma("sp", lambda h: h.dma_start(out=sink8[:], in_=sinks_d.partition_broadcast(128)), writes=["sink8"])
        P.dma("sp", lambda h: h.dma_start(out=gfb[:], in_=gfin_d.partition_broadcast(128)), writes=["gfb"])
        P.dma("sp", lambda h: h.dma_start(out=lam[:], in_=lam_d), writes=["lam"])
        P.dma("sp", lambda h: h.dma_start(out=h0, in_=h0_d))
        op("pool", lambda h: h.memset(hb[:], 0.0))
        op("pool", lambda h: h.memset(cv[:], 0.0))
        op("pool", lambda h: h.memset(gx[:], 0.0))
        P.dma("pool", lambda h: h.dma_start(out=wi[0][:], in_=w_in.rearrange("(c p) n -> p c n", p=128)))
        P.dma("pool", lambda h: h.dma_start(out=wgl[0][:], in_=w_glu.rearrange("(c p) n -> p c n", p=128)))
        P.dma("pool", lambda h: h.dma_start(out=wo[0][:], in_=w_o.rearrange("(c p) n -> p c n", p=128)))
        P.dma("sp", lambda h: h.dma_start(out=kvs_k[:, 0:124, :], in_=kcache_d[:, 4:128, :]), writes=["kvs_k_a"])
        P.dma("sp", lambda h: h.dma_start(out=kvs_v[:, 0:124, :], in_=vcache_d[:, 4:128, :]), writes=["kvs_v_a"])

        op("dve", lambda h: h.tensor_copy(out=idb[:], in_=idf[:]), ["idf"], ["idb"])
        P.dma("pool", lambda h: h.dma_start(out=maskb[:], in_=masks_d))
        op("dve", lambda h: h.tensor_scalar(out=sink8x[:].rearrange("p i a -> p a i"), in0=sink8[:].rearrange("p (a i) -> p a i", a=2), scalar1=8.0, scalar2=None, op0=ALU.mult))
        op("pool", lambda h: h.memset(onesb[:], 1.0), [], ["onesb"])
        op("pool", lambda h: h.memset(epsb[:], EPS), [], ["epsb"])
        op("pool", lambda h: h.memset(kTx[:], 0.0), [], ["kTx"])
        op("pool", lambda h: h.memset(vx[:], 0.0), [], ["vx"])
        op("pool", lambda h: h.memset(car[:], 0.0), [], ["car"])
        op("pool", lambda h: h.memset(gcar[:], 0.0), [], ["gcar"])
        pass
        op("pool", lambda h: h.memset(hfin[:], 0.0))
        op("pool", lambda h: h.memset(sconv[:], 0.0))
        prep_buf = []
        P.capture = prep_buf
        rho = sb("rho", [128, 16]); fre = sb("fre", [128, 16]); fim = sb("fim", [128, 16])
        dtl, thl, c1, s1, tmpa, kfL = (big[:, 9088 + 16 * i:9104 + 16 * i] for i in range(6))
        kiL = big[:, 9184:9200].bitcast(mybir.dt.int32)
        op("act", lambda h: h.activation(out=dtl[:], in_=lam[:, 2, :], func=AF.Exp), ["lam"], ["dtl"])
        op("dve", lambda h: h.tensor_tensor(out=thl[:], in0=lam[:, 1, :], in1=dtl[:], op=ALU.mult), ["lam", "dtl"], ["thl"])
        op("dve", lambda h: h.tensor_tensor(out=tmpa[:], in0=lam[:, 0, :], in1=dtl[:], op=ALU.mult), ["lam", "dtl"], ["tmpa"])
        op("act", lambda h: h.activation(out=rho[:], in_=tmpa[:], func=AF.Exp), ["tmpa"], ["rho"])

        def sincos(eng_tag, th_ap, s_ap, c_ap, tmp_ap, shape_key, ki_ap, kf_ap):
            K = shape_key

            def reduce_(shift, dst_key):
                op("dve", lambda h: h.tensor_scalar(out=tmp_ap, in0=th_ap, scalar1=shift, scalar2=1.0 / (2 * PI), op0=ALU.add, op1=ALU.mult),
                   [K + "th", K + "s", K + "c"], [K + "tmp"])
                op("dve", lambda h: h.tensor_copy(out=ki_ap, in_=tmp_ap), [K + "tmp"], [K + "ki"])
                op("dve", lambda h: h.tensor_copy(out=kf_ap, in_=ki_ap), [K + "ki"], [K + "kf"])
                op("dve", lambda h: h.tensor_scalar(out=tmp_ap, in0=th_ap, scalar1=shift, scalar2=None, op0=ALU.add), [K + "th", K + "ki"], [K + "tmp"])
                op("dve", lambda h: h.scalar_tensor_tensor(out=tmp_ap, in0=kf_ap, scalar=-2 * PI, in1=tmp_ap, op0=ALU.mult, op1=ALU.add),
                   [K + "kf", K + "tmp"], [K + "tmp"])
                op("dve", lambda h: h.tensor_scalar(out=kf_ap, in0=tmp_ap, scalar1=PI, scalar2=None, op0=ALU.is_gt), [K + "tmp"], [K + "kf"])
                op("dve", lambda h: h.scalar_tensor_tensor(out=tmp_ap, in0=kf_ap, scalar=-2 * PI, in1=tmp_ap, op0=ALU.mult, op1=ALU.add),
                   [K + "kf", K + "tmp"], [K + "tmp"])
                op("dve", lambda h: h.tensor_scalar(out=kf_ap, in0=tmp_ap, scalar1=-PI, scalar2=None, op0=ALU.is_lt), [K + "tmp"], [K + "kf"])
                op("dve", lambda h: h.scalar_tensor_tensor(out=tmp_ap, in0=kf_ap, scalar=2 * PI, in1=tmp_ap, op0=ALU.mult, op1=ALU.add),
                   [K + "kf", K + "tmp"], [K + "tmp"])
                op("dve", lambda h: h.tensor_scalar(out=tmp_ap, in0=tmp_ap, scalar1=-PI, scalar2=PI, op0=ALU.max, op1=ALU.min), [K + "tmp"], [K + "tmp"])

            reduce_(0.0, "s")
            op("act", lambda h: h.activation(out=s_ap, in_=tmp_ap, func=AF.Sin), [K + "tmp"], [K + "s"])
            reduce_(0.5 * PI, "c")
            op("act", lambda h: h.activation(out=c_ap, in_=tmp_ap, func=AF.Sin), [K + "tmp"], [K + "c"])

        sincos("L", thl[:], s1[:], c1[:], tmpa[:], "L", kiL[:], kfL[:])
        op("dve", lambda h: h.tensor_tensor(out=are[:], in0=rho[:], in1=c1[:], op=ALU.mult), ["rho", "Lc"], ["are"])
        op("dve", lambda h: h.tensor_tensor(out=aim[:], in0=rho[:], in1=s1[:], op=ALU.mult), ["rho", "Ls"], ["aim"])
        op("pool", lambda h: h.memset(cs[:, :, 0:1], 1.0), [], ["cs"])
        op("pool", lambda h: h.memset(sn[:, :, 0:1], 0.0), [], ["sn"])
        op("dve", lambda h: h.tensor_copy(out=cs[:, :, 1], in_=c1[:]), ["Lc", "cs"], ["cs"])
        op("dve", lambda h: h.tensor_copy(out=sn[:, :, 1], in_=s1[:]), ["Ls", "sn"], ["sn"])
        tA = pp4[:, 0:2].rearrange("p a j (b c) -> p (a j b) c", c=64); tB = big[:, 6656:7680].rearrange("p (a c) -> p a c", c=64)
        m = 1
        while m < 128:
            cm = cs[:, :, m:m + 1].broadcast_to([128, 16, m]); sm = sn[:, :, m:m + 1].broadcast_to([128, 16, m])
            a_c = cs[:, :, 1:m + 1]; a_s = sn[:, :, 1:m + 1]
            o_c = cs[:, :, m + 1:2 * m + 1]; o_s = sn[:, :, m + 1:2 * m + 1]
            ta = tA[:, :, 0:m]; tb = tB[:, :, 0:m]
            op("dve", lambda h, a_c=a_c, cm=cm, ta=ta: h.tensor_tensor(out=ta, in0=a_c, in1=cm, op=ALU.mult), ["cs", "sn"], ["tA"])
            op("dve", lambda h, a_s=a_s, sm=sm, tb=tb: h.tensor_tensor(out=tb, in0=a_s, in1=sm, op=ALU.mult), ["cs", "sn"], ["tB"])
            op("dve", lambda h, o_c=o_c, ta=ta, tb=tb: h.tensor_tensor(out=o_c, in0=ta, in1=tb, op=ALU.subtract), ["tA", "tB", "sn"], ["cs"])
            op("dve", lambda h, a_c=a_c, sm=sm, ta=ta: h.tensor_tensor(out=ta, in0=a_c, in1=sm, op=ALU.mult), ["cs", "sn"], ["tA"])
            op("dve", lambda h, a_s=a_s, cm=cm, tb=tb: h.tensor_tensor(out=tb, in0=a_s, in1=cm, op=ALU.mult), ["cs", "sn"], ["tB"])
            op("dve", lambda h, o_s=o_s, ta=ta, tb=tb: h.tensor_tensor(out=o_s, in0=ta, in1=tb, op=ALU.add), ["tA", "tB", "cs"], ["sn"])
            m *= 2
        op("pool", lambda h: h.memset(mask64[:], 1.0))
        op("pool", lambda h: h.memset(mask64[:].rearrange("p (s t) -> p s t", t=4)[:, :, 0:1], 0.0))
        sm_ = [big[:, 9200 + 16 * i:9216 + 16 * i] for i in range(8)]
        lr_, li_ = lam[:, 0, :], lam[:, 1, :]
        nr, den, t_a, t_b, gr, gi = sm_[0], sm_[1], sm_[2], sm_[3], sm_[4], sm_[5]
        TT = lambda o, x, y, f_: op("dve", lambda h: h.tensor_tensor(out=o, in0=x, in1=y, op=f_))
        op("dve", lambda h: h.tensor_scalar(out=nr, in0=are[:], scalar1=-1.0, scalar2=None, op0=ALU.add))
        TT(den, lr_, lr_, ALU.mult); TT(t_a, li_, li_, ALU.mult); TT(den, den, t_a, ALU.add)
        op("dve", lambda h: h.reciprocal(out=den, in_=den))
        TT(t_a, nr, lr_, ALU.mult); TT(t_b, aim[:], li_, ALU.mult); TT(t_a, t_a, t_b, ALU.add); TT(fre[:], t_a, den, ALU.mult)
        TT(t_a, aim[:], lr_, ALU.mult); TT(t_b, nr, li_, ALU.mult); TT(t_a, t_a, t_b, ALU.subtract); TT(fim[:], t_a, den, ALU.mult)
        TT(den, fre[:], fre[:], ALU.mult); TT(t_a, fim[:], fim[:], ALU.mult); TT(den, den, t_a, ALU.add)
        op("dve", lambda h: h.reciprocal(out=den, in_=den))
        TT(gr, fre[:], den, ALU.mult); TT(gi, fim[:], den, ALU.mult)
        op("dve", lambda h: h.tensor_scalar(out=gi, in0=gi, scalar1=-1.0, scalar2=None, op0=ALU.mult))
        dD32 = big[:, 4096:4608].rearrange("p (a k) -> p a k", a=4); dDb = big[:, 4608:5120].rearrange("p (a k) -> p a k", a=4)
        for c4_ in range(4):
            op("dve", lambda h: h.tensor_scalar(out=dD32[:, c4_, :], in0=idf[:], scalar1=col(DSK + c4_), scalar2=None, op0=ALU.mult))
        op("dve", lambda h: h.tensor_copy(out=dDhi[:], in_=dD32))
        op("dve", lambda h: h.tensor_copy(out=dDb, in_=dDhi[:]))
        TT(dDb, dD32, dDb, ALU.subtract)
        op("dve", lambda h: h.tensor_copy(out=dDlo[:], in_=dDb))
        P.dma("pool", lambda h: h.dma_start(out=WBre[:], in_=bblk_d[:, 0]))
        P.dma("pool", lambda h: h.dma_start(out=WBim[:], in_=bblk_d[:, 1]))
        for hcf in range(2):
            cbf = big[:, 4096:6144].rearrange("p (a q m) -> p a q m", a=2, q=8)
            u1 = big[:, 6144:7168].rearrange("p (q m) -> p q m", q=8); u2 = big[:, 7168:8192].rearrange("p (q m) -> p q m", q=8)
            qs = slice(hcf * 8, hcf * 8 + 8)
            P.dma("sp", lambda h: h.dma_start(out=cbf, in_=cblk_d[:, :, qs, :]))
            fre_b = fre[:, qs].unsqueeze(2).broadcast_to([128, 8, 128]); fim_b = fim[:, qs].unsqueeze(2).broadcast_to([128, 8, 128])
            TT(u1, cbf[:, 0], fre_b, ALU.mult); TT(u2, cbf[:, 1], fim_b, ALU.mult); TT(WCre[:, qs, :], u1, u2, ALU.subtract)
            TT(u1, cbf[:, 0], fim_b, ALU.mult); TT(u2, cbf[:, 1], fre_b, ALU.mult); TT(u1, u1, u2, ALU.add)
            op("dve", lambda h: h.tensor_scalar(out=WCimn[:, qs, :], in0=u1, scalar1=-1.0, scalar2=None, op0=ALU.mult))

        tD = big[:, 9344:9600].rearrange("p (q s) -> p q s", q=16)
        tE = big[:, 8704:8960].rearrange("p (q s) -> p q s", q=16)
        gr_b = gr.unsqueeze(2).broadcast_to([128, 16, NSEQ]); gi_b = gi.unsqueeze(2).broadcast_to([128, 16, NSEQ])
        TT(tD, h0[:, 0], gr_b, ALU.mult); TT(tE, h0[:, 1], gi_b, ALU.mult); TT(tD, tD, tE, ALU.subtract)
        TT(tE, h0[:, 0], gi_b, ALU.mult); TT(h0[:, 0], tD, tD, ALU.max)
        TT(tD, h0[:, 1], gr_b, ALU.mult); TT(h0[:, 1], tE, tD, ALU.add)
        a_re_b = are[:].unsqueeze(2).broadcast_to([128, 16, NSEQ]); a_im_b = aim[:].unsqueeze(2).broadcast_to([128, 16, NSEQ])
        tC = big[:, 8704:8960].rearrange("p (q s) -> p q s", q=16)
        op("dve", lambda h: h.tensor_tensor(out=ah[:, 0], in0=h0[:, 0], in1=a_re_b, op=ALU.mult), ["h0", "are"], ["ah0"])
        op("dve", lambda h: h.tensor_tensor(out=tC[:], in0=h0[:, 1], in1=a_im_b, op=ALU.mult), ["h0", "aim"], ["tC"])
        op("dve", lambda h: h.tensor_tensor(out=ah[:, 0], in0=ah[:, 0], in1=tC[:], op=ALU.subtract), ["ah0", "tC"], ["ah0"])
        op("dve", lambda h: h.tensor_tensor(out=ah[:, 1], in0=h0[:, 0], in1=a_im_b, op=ALU.mult), ["h0", "aim"], ["ah1"])
        op("dve", lambda h: h.tensor_tensor(out=tC[:], in0=h0[:, 1], in1=a_re_b, op=ALU.mult), ["h0", "are", "ah0"], ["tC"])
        op("dve", lambda h: h.tensor_tensor(out=ah[:, 1], in0=ah[:, 1], in1=tC[:], op=ALU.add), ["ah1", "tC"], ["ah1"])


        P.capture = None
        def rms(src_ap, key_src, n, slot, scale, pn, junk=None):
            junk = xn if junk is None else junk
            op("act", lambda h: h.activation(out=junk[0:pn, 0:n], in_=src_ap, func=AF.Square, accum_out=ssq[0:pn, slot:slot + 1]),
               [key_src], ["junk", "ssq%d" % slot])
            op("act", lambda h: h.activation(out=rstd[0:pn, slot:slot + 1], in_=ssq[0:pn, slot:slot + 1], func=AF.Ln, scale=scale, bias=epsb[0:pn, 0:1]))
            op("act", lambda h: h.activation(out=rstd[0:pn, slot:slot + 1], in_=rstd[0:pn, slot:slot + 1], func=AF.Exp, scale=-0.5))

        def partA(ti, tl):
            sample = (ti == NPT)
            xt = xb[:, tl, :]
            n = 128
            ns = NS if sample else 128
            r0 = ti * 128
            par = ti % 2
            if sample:
                op("pool", lambda h: h.memset(kTs[:], 0.0))
                P.dma("pool", lambda h: h.dma_start(out=kTs[0:64, 0, 0:NSEQ * 128], in_=kcT_d[0:64, :]))
                P.dma("pool", lambda h: h.dma_start(out=kTs[64:128, 1, 0:NSEQ * 128], in_=kcT_d[64:128, :]))
                P.dma("pool", lambda h: h.dma_start(out=vs[:, 0:NSEQ, :], in_=vc_d))
                P.dma("sp", lambda h: h.dma_start(out=cvst, in_=cvst_d))
            rms(xt[:], "xt", D, 0, 1.0 / D, 128)
            op("act", lambda h: h.activation(out=xn[:], in_=xt[:], func=AF.Copy, scale=rstd[:, 0:1]))
            for c in range(8):
                op("pe", lambda h, c=c: h.transpose(out=pT[:, c, :], in_=xn[:, c * 128:(c + 1) * 128], identity=idb[:]))
            op("dve", lambda h: h.tensor_tensor(out=xnT[:], in0=pT[:], in1=pcol[:, GMIX:GMIX + 8].unsqueeze(2).broadcast_to([128, 8, 128]), op=ALU.mult))
            W = wi[par]
            yield
            for i in range(4):
                bank = acc0 if i % 2 == 0 else acc1
                for c in range(8):
                    op("pe", lambda h, i=i, c=c, bank=bank: h.matmul(bank[:, 0:128], lhsT=W[:, c, i * 128:(i + 1) * 128], rhs=xnT[:, c, :],
                                                                   start=(c == 0), stop=(c == 7)))
                op("act", lambda h, i=i, bank=bank: h.activation(out=qT[:, i, :], in_=bank[:, 0:128], func=AF.Copy))
            yield
            for c in range(8):
                op("pe", lambda h, c=c: h.matmul(Ops[:, 0:128], lhsT=W[:, c, 512:640], rhs=xnT[:, c, :], start=(c == 0), stop=(c == 7)))
            if sample:
                op("dve", lambda h: h.tensor_copy(out=kTs[0:64, 0, NSEQ * 128:17 * 128], in_=Ops[0:64, 0:128]))
                op("dve", lambda h: h.tensor_copy(out=kTs[64:128, 1, NSEQ * 128:17 * 128], in_=Ops[64:128, 0:128]))
            else:
                op("act", lambda h: h.activation(out=kTx[0:64, 0, 128:256], in_=Ops[0:64, 0:128], func=AF.Copy))
                op("act", lambda h: h.activation(out=kTx[64:128, 1, 128:256], in_=Ops[64:128, 0:128], func=AF.Copy))
            yield
            for i in range(4):
                bank = acc0 if i % 2 == 0 else acc1
                for c in range(8):
                    op("pe", lambda h, i=i, c=c, bank=bank: h.matmul(bank[:, 0:128], lhsT=W[:, c, 768 + i * 128:768 + (i + 1) * 128], rhs=xnT[:, c, :],
                                                                   start=(c == 0), stop=(c == 7)))
                op("act", lambda h, i=i, bank=bank: h.activation(out=uT2[par][:, i, :], in_=bank[:, 0:128], func=AF.Copy))
            yield
            for c in range(8):
                op("pe", lambda h, c=c: h.matmul(Ops[:, 0:256], lhsT=xnT[:, c, :], rhs=W[:, c, 512:768], start=(c == 0), stop=(c == 7)))
            if sample:
                op("dve", lambda h: h.tensor_copy(out=vs[:, 16, :], in_=Ops[:, 128:256]))
            else:
                op("act", lambda h: h.activation(out=vx[:, 1, :], in_=Ops[:, 128:256], func=AF.Copy))
            if sample or ti == NPT - 1:
                op("dve", lambda h: h.tensor_copy(out=kvtok[:], in_=Ops[:, 0:256]))
                if sample:
                    P.dma("sp", lambda h: h.dma_start(out=kvs_k[:, 124:128, :], in_=kvtok[0:NS, 0:128]))
                    P.dma("sp", lambda h: h.dma_start(out=kvs_v[:, 124:128, :], in_=kvtok[0:NS, 128:256]))
                else:
                    P.dma("sp", lambda h: h.dma_start(out=kvp_d, in_=kvtok[:]))

            if not sample:
                mi = 0 if ti == 0 else (1 if ti == 1 else 2)
                for i in range(4):
                    for hh_ in range(2):
                        op("pe", lambda h: h.matmul(Sps[:, hh_ * 256:(hh_ + 1) * 256], lhsT=qT[:, i, :], rhs=kTx[:, hh_, :], start=True, stop=False))
                        op("pe", lambda h: h.matmul(Sps[:, hh_ * 256:(hh_ + 1) * 256], lhsT=idb[:], rhs=maskb[:, mi, :], start=False, stop=True))
                    op("dve", lambda h: h.tensor_reduce(out=mx[:], in_=Sps[:].rearrange("p (a k) -> p a k", a=2), axis=AX.X, op=ALU.max))
                    op("dve", lambda h: h.tensor_tensor(out=mx[:], in0=mx[:], in1=sink8x[:, i, :], op=ALU.max))
                    op("dve", lambda h: h.tensor_scalar(out=nbias[:], in0=mx[:], scalar1=-0.125, scalar2=None, op0=ALU.mult))
                    for hh_ in range(2):
                        op("act", lambda h: h.activation(out=Pb[:, hh_, 0:256], in_=Sps[:, hh_ * 256:(hh_ + 1) * 256], func=AF.Exp, scale=0.125,
                                                         bias=nbias[:, hh_:hh_ + 1], accum_out=rs[:, hh_:hh_ + 1]))
                        op("act", lambda h: h.activation(out=esink[:, hh_:hh_ + 1], in_=sink8x[:, i, hh_:hh_ + 1], func=AF.Exp, scale=0.125,
                                                         bias=nbias[:, hh_:hh_ + 1]))
                    op("dve", lambda h: h.tensor_tensor(out=rs[:], in0=rs[:], in1=esink[:], op=ALU.add))
                    op("dve", lambda h: h.reciprocal(out=rinv[:], in_=rs[:]))
                    for hh_ in range(2):
                        for blk in range(2):
                            op("pe", lambda h, hh_=hh_, blk=blk: h.transpose(out=pT[:, hh_ * 2 + blk, :], in_=Pb[:, hh_, blk * 128:(blk + 1) * 128], identity=idb[:]))
                    op("act", lambda h: h.activation(out=PTs[:], in_=pT[:, 0:4, :], func=AF.Copy))
                    for hh_ in range(2):
                        for blk in range(2):
                            op("pe", lambda h, hh_=hh_, blk=blk: h.matmul(Ops[:, hh_ * 64:(hh_ + 1) * 64], lhsT=PTs[:, hh_ * 2 + blk, :],
                                                                         rhs=vx[:, blk, hh_ * 64:(hh_ + 1) * 64], start=(blk == 0), stop=(blk == 1)))
                    for hh_ in range(2):
                        hd = i + 4 * hh_
                        op("act", lambda h, hh_=hh_, hd=hd: h.activation(out=attn[:, hd * 64:(hd + 1) * 64], in_=Ops[:, hh_ * 64:(hh_ + 1) * 64],
                                                                        func=AF.Copy, scale=rinv[:, hh_:hh_ + 1]))
                    yield
                op("pool", lambda h: h.tensor_copy(out=kTx[:, :, 0:128], in_=kTx[:, :, 128:256]))
                op("pool", lambda h: h.tensor_copy(out=vx[:, 0, :], in_=vx[:, 1, :]))
            else:
                W17 = 17 * 128
                for hd in range(8):
                    i, hh_ = hd % 4, hd // 4
                    for cb in range(5):
                        c0 = cb * 512
                        cw = min(512, W17 - c0)
                        bank = mmA if cb % 2 == 0 else mmB
                        op("pe", lambda h, i=i, hh_=hh_, c0=c0, cw=cw, bank=bank: h.matmul(bank[:, 0:cw], lhsT=qT[:, i, :], rhs=kTs[:, hh_, c0:c0 + cw], start=True, stop=True))
                        op("dve", lambda h, c0=c0, cw=cw, bank=bank: h.tensor_tensor(out=Ssx[:, c0:c0 + cw], in0=bank[:, 0:cw], in1=smask[:, c0:c0 + cw], op=ALU.add))
                    op("dve", lambda h, hd=hd: h.tensor_scalar(out=Ssx[:, W17:W17 + 1], in0=sink8[:, hd:hd + 1], scalar1=8.0, scalar2=None, op0=ALU.mult))
                    op("dve", lambda h: h.tensor_reduce(out=mx[:, 0:1], in_=Ssx[:], axis=AX.X, op=ALU.max))
                    op("dve", lambda h: h.tensor_scalar(out=nbias[:, 0:1], in0=mx[:, 0:1], scalar1=-0.125, scalar2=None, op0=ALU.mult))
                    op("act", lambda h: h.activation(out=Psb[:], in_=Ssx[:], func=AF.Exp, scale=0.125, bias=nbias[:, 0:1], accum_out=rs[:, 0:1]))
                    op("dve", lambda h: h.reciprocal(out=rinv[:, 0:1], in_=rs[:, 0:1]))
                    for g8 in range(3):
                        nb_ = 8 if g8 < 2 else 1
                        for b in range(nb_):
                            blk = g8 * 8 + b
                            op("pe", lambda h, b=b, blk=blk: h.transpose(out=pT[:, b, :], in_=Psb[:, blk * 128:(blk + 1) * 128], identity=idb[:]))
                        op("act", lambda h, g8=g8, nb_=nb_: h.activation(out=PTss[:, g8 * 8:g8 * 8 + nb_, :], in_=pT[:, 0:nb_, :], func=AF.Copy))
                    for blk in range(17):
                        op("pe", lambda h, blk=blk, hh_=hh_: h.matmul(Ops[:, 0:64], lhsT=PTss[:, blk, :], rhs=vs[:, blk, hh_ * 64:(hh_ + 1) * 64],
                                                                     start=(blk == 0), stop=(blk == 16)))
                    op("dve", lambda h, hd=hd: h.tensor_scalar(out=attn[:, hd * 64:(hd + 1) * 64], in0=Ops[:, 0:64], scalar1=rinv[:, 0:1], scalar2=None, op0=ALU.mult))
            rms(attn[0:n, :], "attn", 512, 1, 1.0 / 512, n)
            op("act", lambda h: h.activation(out=anb[0:n, :], in_=attn[0:n, :], func=AF.Copy, scale=rstd[0:n, 1:2]))
            for c in range(4):
                op("pe", lambda h, c=c: h.transpose(out=pT[:, c, 0:n], in_=anb[0:n, c * 128:(c + 1) * 128], identity=idb[0:n, 0:n]), ["anb", "idb"], ["pT"])
            op("dve", lambda h: h.tensor_tensor(out=mixT2[par][:, 0:4, :], in0=pT[:, 0:4, :], in1=pcol[:, GATT:GATT + 4].unsqueeze(2).broadcast_to([128, 4, 128]), op=ALU.mult))

            yield

        def partB(ti, tl):
            sample = (ti == NPT)
            xt = xb[:, tl, :]
            n = 128
            ns = NS if sample else 128
            par = ti % 2
            uT = uT2[par]
            mixT = mixT2[par]
            TT = lambda o, x, y, f_: op("dve", lambda h: h.tensor_tensor(out=o, in0=x, in1=y, op=f_))

            def ssm_mm(c4):
                q0 = 4 * c4
                for ri, (WBx, bank) in enumerate(((WBre, mmA), (WBim, mmC))):
                    for j in range(4):
                        op("pe", lambda h: h.matmul(bank[:, j * 128:(j + 1) * 128], lhsT=WBx[:, q0 + j, :], rhs=uT[:, c4, :], start=True, stop=True))

            def ssm_batch(c4):
                q0 = 4 * c4
                T = [pp4[:, kk] for kk in range(4)]
                Bre = mmA[:, :].rearrange("p (j k) -> p j k", j=4); Bim = mmC[:, :].rearrange("p (j k) -> p j k", j=4)
                if sample:
                    op("act", lambda h: h.activation(out=vv4[:, 0], in_=Bre, func=AF.Copy))
                    op("act", lambda h: h.activation(out=vv4[:, 1], in_=Bim, func=AF.Copy))
                    for ri in range(2):
                        v4 = vv4[:, ri, :, 0:NS].rearrange("p j (s t) -> p j s t", t=4)[:, :, :, 0]
                        TT(v4, v4, ah[:, ri, q0:q0 + 4, :], ALU.add)
                    csq = cs[:, q0:q0 + 4, 1:5].unsqueeze(2).broadcast_to([128, 4, NSEQ, 4])
                    snq = sn[:, q0:q0 + 4, 1:5].unsqueeze(2).broadcast_to([128, 4, NSEQ, 4])
                    v3 = lambda ap: ap[:, :, 0:NS].rearrange("p j (s t) -> p j s t", t=4)
                    Bre, Bim = vv4[:, 0], vv4[:, 1]
                else:
                    csq = cs[:, q0:q0 + 4, 1:129]; snq = sn[:, q0:q0 + 4, 1:129]
                    v3 = lambda ap: ap
                Tv = [v3(t_) for t_ in T]
                RR = [v3(rr4[:, kk]) for kk in range(2)]
                TT(Tv[0], v3(Bre), csq, ALU.mult); TT(Tv[1], v3(Bim), snq, ALU.mult)
                TT(Tv[2], v3(Bim), csq, ALU.mult); TT(Tv[3], v3(Bre), snq, ALU.mult)
                if c4 + 1 < 4:
                    ssm_mm(c4 + 1)
                TT(RR[0], Tv[0], Tv[1], ALU.add); TT(RR[1], Tv[2], Tv[3], ALU.subtract)
                for j in range(4):
                    q = q0 + j
                    if sample:
                        op("dve", lambda h: h.tensor_scalar(out=dtmp[:], in0=mask64[:], scalar1=rho[:, q:q + 1], scalar2=None, op0=ALU.mult))
                    for ri in range(2):
                        if sample:
                            op("dve", lambda h: h.tensor_tensor_scan(out=vv4[:, ri, j, 0:NS], data0=dtmp[:], data1=rr4[:, ri, j, 0:NS], initial=0.0, op0=ALU.mult, op1=ALU.add))
                        else:
                            op("dve", lambda h: h.tensor_tensor_scan(out=vv4[:, ri, j, :], data0=rho[:, q:q + 1].broadcast_to([128, 128]), data1=rr4[:, ri, j, :],
                                                                   initial=car[:, ri, q:q + 1], op0=ALU.mult, op1=ALU.add))
                VV = [v3(vv4[:, kk]) for kk in range(2)]
                TT(Tv[0], VV[0], csq, ALU.mult); TT(Tv[1], VV[1], snq, ALU.mult)
                TT(Tv[2], VV[0], snq, ALU.mult); TT(Tv[3], VV[1], csq, ALU.mult)
                TT(v3(hb[:, :, 0, :]), Tv[0], Tv[1], ALU.subtract); TT(v3(hb[:, :, 1, :]), Tv[2], Tv[3], ALU.add)
                if sample:
                    TT(hfin[:, 0, q0:q0 + 4, 0:NSEQ], Tv[0][:, :, :, 3], Tv[1][:, :, :, 3], ALU.subtract)
                    TT(hfin[:, 1, q0:q0 + 4, 0:NSEQ], Tv[2][:, :, :, 3], Tv[3][:, :, :, 3], ALU.add)
                else:
                    TT(car[:, 0, q0:q0 + 4], Tv[0][:, :, 127], Tv[1][:, :, 127], ALU.subtract)
                    TT(car[:, 1, q0:q0 + 4], Tv[2][:, :, 127], Tv[3][:, :, 127], ALU.add)

            def ssm_y(c4):
                op("pe", lambda h: h.matmul(mmB[:, 0:n], lhsT=dDhi[:, c4, :], rhs=uT[:, c4, 0:n], start=True, stop=False))
                op("pe", lambda h: h.matmul(mmB[:, 0:n], lhsT=dDlo[:, c4, :], rhs=uT[:, c4, 0:n], start=False, stop=False))
                for qq in range(4):
                    q = c4 * 4 + qq
                    op("pe", lambda h: h.matmul(mmB[:, 0:n], lhsT=WCre[:, q, :], rhs=hb[:, qq, 0, 0:n], start=False, stop=False))
                    op("pe", lambda h: h.matmul(mmB[:, 0:n], lhsT=WCimn[:, q, :], rhs=hb[:, qq, 1, 0:n], start=False, stop=(qq == 3)))
                op("act", lambda h: h.activation(out=gl32[:, c4, 0:n], in_=mmB[:, 0:n], func=AF.Gelu))
                op("pool", lambda h: h.tensor_copy(out=glb[:, c4, 0:n], in_=gl32[:, c4, 0:n]))

            ssm_mm(0)
            yield
            for c4_ in range(4):
                ssm_batch(c4_)
                yield
                ssm_y(c4_)
                yield
            if ti == NPT - 1:
                op("act", lambda h: h.activation(out=hfin[:, :, :, NSEQ], in_=car[:], func=AF.Copy), ["car"], ["hfin"])
            for oc in range(4):
                for c4 in range(4):
                    op("pe", lambda h, oc=oc, c4=c4: h.matmul(mmC[:, 0:n], lhsT=wgl[0][:, c4, oc * 128:(oc + 1) * 128], rhs=glb[:, c4, 0:n], start=(c4 == 0), stop=(c4 == 3)),
                       ["glb", "wgl"], ["mmC"])
                op("act", lambda h, oc=oc: h.activation(out=sg[:, 0:n], in_=mmC[:, 0:n], func=AF.Sigmoid, bias=col(BGLU + oc)), ["mmC", "pcol"], ["sg"])
                op("dve", lambda h, oc=oc: h.tensor_tensor(out=gl32[:, oc, 0:n], in0=gl32[:, oc, 0:n], in1=sg[:, 0:n], op=ALU.mult), ["gl32", "sg", "glb"], ["gl32"])
                op("act", lambda h, oc=oc: h.activation(out=sqb[:, oc, 0:n], in_=gl32[:, oc, 0:n], func=AF.Square), ["gl32"], ["sqb"])
            for oc in range(4):
                op("pe", lambda h, oc=oc: h.matmul(mmC[:, 0:n], lhsT=onesb[:], rhs=sqb[:, oc, 0:n], start=(oc == 0), stop=(oc == 3)), ["sqb", "onesb"], ["mmC"])
            op("act", lambda h: h.activation(out=rsb[:, 0:n], in_=mmC[:, 0:n], func=AF.Ln, scale=1.0 / 512, bias=epsb[:, 0:1]))
            op("act", lambda h: h.activation(out=rsb[:, 0:n], in_=rsb[:, 0:n], func=AF.Exp, scale=-0.5))
            for oc in range(4):
                op("dve", lambda h, oc=oc: h.scalar_tensor_tensor(out=mixT[:, 4 + oc, 0:n], in0=gl32[:, oc, 0:n], scalar=col(GSSM + oc), in1=rsb[:, 0:n], op0=ALU.mult, op1=ALU.mult),
                   ["gl32", "rsb", "pcol"], ["mixT"])

            yield "join"
            for hf in range(2):
                acc, ak = (acc0, "acc0") if hf == 0 else (acc1, "acc1")
                for c in range(8):
                    op("pe", lambda h, hf=hf, c=c, acc=acc: h.matmul(acc[0:n, :], lhsT=mixT[:, c, 0:n], rhs=wo[0][:, c, hf * 512:(hf + 1) * 512], start=(c == 0), stop=(c == 7)),
                       ["mixT", "wo"], [ak])
                op("dve", lambda h, hf=hf, acc=acc: h.tensor_tensor(out=xt[0:n, hf * 512:(hf + 1) * 512], in0=xt[0:n, hf * 512:(hf + 1) * 512], in1=acc[0:n, :], op=ALU.add),
                   ["xt", ak], ["xt"])
            rms(xt[0:n, :], "xt", D, 2, 1.0 / D, n)
            op("act", lambda h: h.activation(out=xn[0:n, :], in_=xt[0:n, :], func=AF.Copy, scale=rstd[0:n, 2:3]))
            for c in range(8):
                op("pe", lambda h, c=c: h.transpose(out=pT[:, c, 0:n], in_=xn[0:n, c * 128:(c + 1) * 128], identity=idb[0:n, 0:n]), ["xn", "idb"], ["pT"])
            op("dve", lambda h: h.tensor_tensor(out=xn2T[:, :, tl * 128:(tl + 1) * 128], in0=pT[:], in1=pcol[:, GFFN:GFFN + 8].unsqueeze(2).broadcast_to([128, 8, 128]), op=ALU.mult))

        def ffn(tiles_):
            sample = (tiles_[0] == NPT)
            ntl = len(tiles_)
            nb = ntl * 128
            hTv = hTsm if (sample and ntl == 1) else hTall
            p0 = 128 if sample else 0
            npc = nb - p0
            for ch in range(NFC):
                b3 = ch % 2
                w3 = ch % 3
                P.dma("sp", lambda h: h.dma_start(out=wg[w3][:], in_=wgb_d[ch]))
                P.dma("sp", lambda h: h.dma_start(out=wu[w3][:], in_=wub_d[ch]))
                gps, ups = (mmA, mmB) if b3 == 0 else (mmC, pT32)
                for c in range(8):
                    op("pe", lambda h, c=c, b3=b3, gps=gps: h.matmul(gps[:, 0:nb], lhsT=wg[w3][:, c, :], rhs=xn2T[:, c, 0:nb], start=(c == 0), stop=(c == 7)))
                for c in range(8):
                    op("pe", lambda h, c=c, b3=b3, ups=ups: h.matmul(ups[:, 0:nb], lhsT=wu[w3][:, c, :], rhs=xn2T[:, c, 0:nb], start=(c == 0), stop=(c == 7)))
                w0, w1, w2, bb = col(CVW + ch * 3), col(CVW + ch * 3 + 1), col(CVW + ch * 3 + 2), col(CVB + ch)
                cvb, slb, gxb = cv[:, 0, :], sl[:, 0, :], gx[:, b3, :]
                def conv3(g0, g1, g2, cvv):
                    op("dve", lambda h: h.tensor_scalar(out=cvv, in0=g2, scalar1=w2, scalar2=bb, op0=ALU.mult, op1=ALU.add))
                    op("dve", lambda h: h.scalar_tensor_tensor(out=cvv, in0=g1, scalar=w1, in1=cvv, op0=ALU.mult, op1=ALU.add))
                    op("dve", lambda h: h.scalar_tensor_tensor(out=cvv, in0=g0, scalar=w0, in1=cvv, op0=ALU.mult, op1=ALU.add))

                if sample:
                    op("pool", lambda h: h.tensor_copy(out=gxs[:, :, 0:2], in_=cvst[:, ch, :, :]))
                    op("act", lambda h: h.activation(out=gxs[:, :, 2:6], in_=gps[:, 0:NS].rearrange("p (s t) -> p s t", t=4), func=AF.Copy))
                    op("pool", lambda h: h.tensor_copy(out=sconv[:, ch, :, :], in_=gxs[:, :, 4:6]))
                    conv3(gxs[:, :, 0:4], gxs[:, :, 1:5], gxs[:, :, 2:6], cvb[:, 0:NS].rearrange("p (s t) -> p s t", t=4))
                if npc:
                    op("pool", lambda h: h.tensor_copy(out=gxb[:, 0:2], in_=gcar[:, ch, :]))
                    op("act", lambda h: h.activation(out=gxb[:, 2:2 + npc], in_=gps[:, p0:nb], func=AF.Copy))
                    op("pool", lambda h: h.tensor_copy(out=gcar[:, ch, :], in_=gxb[:, npc:npc + 2]))
                    conv3(gxb[:, 0:npc], gxb[:, 1:1 + npc], gxb[:, 2:2 + npc], cvb[:, p0:nb])
                op("act", lambda h, cvb=cvb, slb=slb: h.activation(out=slb[:, 0:nb], in_=cvb[:, 0:nb], func=AF.Silu))
                op("dve", lambda h, ch=ch, slb=slb: h.tensor_tensor(out=hTv[:, ch, 0:nb], in0=slb[:, 0:nb], in1=ups[:, 0:nb], op=ALU.mult))
            accs = [acc0, acc1, Sps, Ops]
            for hf in range(2):
                for g in range(NFC // 2):
                    wdb = wd[g % 3]
                    P.dma("sp", lambda h: h.dma_start(out=wdb, in_=wdb_d[hf, g]))
                    for j in range(2):
                        ch = 2 * g + j
                        for tl in range(ntl):
                            op("pe", lambda h: h.matmul(accs[tl][:, :], lhsT=hTv[:, ch, tl * 128:(tl + 1) * 128], rhs=wdb[:, j, :],
                                                        start=(ch == 0), stop=(ch == NFC - 1)))
                for tl in range(ntl):
                    op("dve", lambda h, tl=tl, hf=hf: h.tensor_tensor(out=xb[:, tl, hf * 512:(hf + 1) * 512], in0=xb[:, tl, hf * 512:(hf + 1) * 512], in1=accs[tl][:, :], op=ALU.add))

        def ffn_tail(tiles_, nxt_, first, last):
            for tl in range(first, last):
                ti = tiles_[tl]
                xt = xb[:, tl, :]
                rms(xt, "xt", D, 3, 1.0 / D, 128, junk=gl32[:].rearrange("p a k -> p (a k)").bitcast(BF16))
                op("dve", lambda h: h.scalar_tensor_tensor(out=xt, in0=xt, scalar=rstd[:, 3:4], in1=gfb[:], op0=ALU.mult, op1=ALU.mult))
                P.dma("sp", lambda h: h.dma_start(out=y_d[ti * 128:(ti + 1) * 128, :], in_=xt))
                if nxt_ is not None and tl < len(nxt_) and (tl == 0 or NPT not in nxt_):
                    load_x(tl, nxt_[tl])
                yield

        def est_dur(kind, eng, fn):
            rec = _Recorder()
            fn(rec)
            name, args, kwargs = rec.call
            o = kwargs.get("out", args[0] if args else None)
            nfree = 1
            if o is not None and hasattr(o, "shape"):
                for d_ in list(o.shape)[1:]:
                    nfree *= int(d_)
            if kind == "dma":
                return 100.0, 2500.0
            if eng == "dve":
                d = ((2 * nfree if name == "tensor_tensor_scan" else nfree) + 151) / 0.96
            elif eng == "act":
                d = (nfree + 230) / 1.2 + (90 if kwargs.get("accum_out") is not None else 0)
            elif eng == "pool":
                d = (2 * nfree + 250) / 1.2
            else:
                d = max(64, nfree) / 1.9 + 25
            return d, d

        def merge_threads(gens):
            gens = [g for g in gens if g is not None]
            bufs = [[] for _ in gens]
            alive = [True] * len(gens)
            tchain = [sched_now[0]] * len(gens)
            joined = [False] * len(gens)
            while True:
                for i, g in enumerate(gens):
                    while alive[i] and not bufs[i] and not joined[i]:
                        P.capture = bufs[i]
                        r_ = next(g, "done")
                        P.capture = None
                        if r_ == "done":
                            alive[i] = False
                        elif r_ == "join":
                            joined[i] = True
                for i in range(len(gens)):
                    if joined[i] and not bufs[i] and not any((alive[j] or bufs[j]) for j in range(len(gens)) if j != i):
                        joined[i] = False
                if not any(bufs) and any(alive):
                    continue
                cands = [i for i in range(len(gens)) if bufs[i]]
                if not cands:
                    break
                best, best_t = None, None
                for i in cands:
                    kind, eng, fn = bufs[i][0]
                    t = max(eng_free.get(eng, 0.0), tchain[i] + 250.0)
                    if best is None or t < best_t:
                        best, best_t = i, t
                kind, eng, fn = bufs[best].pop(0)
                busy, lat = est_dur(kind, eng, fn)
                eng_free[eng] = best_t + busy
                tchain[best] = best_t + lat
                sched_now[0] = max(sched_now[0], best_t)
                (P.dma if kind == "dma" else P.op)(eng, fn)

        eng_free = {}
        ffn_cast_issued = [False]
        sched_now = [0.0]
        if tiles is None:
            blocks = [[0, 1, 2, 3], [4, 5, 6, 7], [8, 9, 10, 11], [12, 13, 14, 15], [NPT, 16]]
        else:
            blocks = tiles
        loaded = set()

        def load_x(tl, ti):
            if ti not in loaded:
                loaded.add(ti)
                P.dma("sp", lambda h: h.dma_start(out=xb[:, tl, :], in_=xin[ti * 128:(ti + 1) * 128, :]))

        pre_done = False
        for bi, blk_ in enumerate(blocks):
            lazy = NPT in blk_
            for tl, ti in enumerate(blk_):
                if tl == 0 or not lazy:
                    load_x(tl, ti)
            if not pre_done:
                if prep_buf:
                    def replay_prep():
                        pend = list(prep_buf)
                        del prep_buf[:]
                        for k_ in range(0, len(pend), 6):
                            P.capture.extend(pend[k_:k_ + 6])
                            yield
                    merge_threads([replay_prep(), partA(blk_[0], 0)])
                else:
                    for _ in partA(blk_[0], 0):
                        pass
            if not ffn_cast_issued[0]:
                ffn_cast_issued[0] = True
                for a_ in range(0, NFC, 11):
                    P.dma("pool", lambda h: h.dma_start(out=wgb_d[a_:a_ + 11], in_=w_gate[a_:a_ + 11]))
                    P.dma("pool", lambda h: h.dma_start(out=wub_d[a_:a_ + 11], in_=w_up[a_:a_ + 11]))
                for hf_ in range(2):
                    P.dma("pool", lambda h: h.dma_start(out=wdb_d[hf_], in_=w_down[hf_]))
            for tl, ti in enumerate(blk_):
                gb = partB(ti, tl)
                ga = partA(blk_[tl + 1], tl + 1) if tl + 1 < len(blk_) else None
                if ga is not None and lazy:
                    load_x(tl + 1, blk_[tl + 1])
                merge_threads([gb, ga])
            ffn(blk_)
            nxt = blocks[bi + 1] if bi + 1 < len(blocks) else None
            for _ in ffn_tail(blk_, nxt, 0, 1):
                pass
            if nxt is not None and NPT not in nxt:
                merge_threads([ffn_tail(blk_, nxt, 1, len(blk_)), partA(nxt[0], 0)])
                pre_done = True
            else:
                for _ in ffn_tail(blk_, nxt, 1, len(blk_)):
                    pass
                pre_done = False

        fv = pp4[:].rearrange("p a j k -> p (a j k)")
        v1 = fv[:, 0:272].rearrange("p (q s) -> p q s", q=16); v2 = fv[:, 272:544].rearrange("p (q s) -> p q s", q=16); v3_ = fv[:, 544:816].rearrange("p (q s) -> p q s", q=16)
        fre_c = fre[:].unsqueeze(2).broadcast_to([128, 16, NSEQ + 1]); fim_c = fim[:].unsqueeze(2).broadcast_to([128, 16, NSEQ + 1])
        TT(v1, hfin[:, 0], fre_c, ALU.mult); TT(v2, hfin[:, 1], fim_c, ALU.mult); TT(v3_, v1, v2, ALU.subtract)
        TT(v1, hfin[:, 0], fim_c, ALU.mult); TT(v2, hfin[:, 1], fre_c, ALU.mult); TT(hfin[:, 1], v1, v2, ALU.add)
        op("dve", lambda h: h.tensor_copy(out=hfin[:, 0], in_=v3_))
        P.dma("sp", lambda h: h.dma_start(out=hfin_d, in_=hfin[:]), reads=["hfin"], writes=["hfin_d"])
        P.dma("sp", lambda h: h.dma_start(out=pconv_d, in_=gcar[:]), reads=["gcar"], writes=["pconv_d"])
        P.dma("sp", lambda h: h.dma_start(out=sconv_d, in_=sconv[:]), reads=["sconv"], writes=["sconv_d"])
        P.limit = None
        P.barrier()

        with nc.Block() as block:
            @block.tensor
            def _(h):
                P.run("pe", h, sems)

            @block.scalar
            def _(h):
                P.run("act", h, sems)

            @block.vector
            def _(h):
                P.run("dve", h, sems)

            @block.gpsimd
            def _(h):
                P.run("pool", h, sems)

            @block.sync
            def _(h):
                P.run("sp", h, sems)
    return nc


def _consts():
    ident = np.eye(128, dtype=np.float32)
    i = np.arange(128)[:, None]
    c = np.arange(256)[None, :]
    full = np.where(((c < 128) & (c > i)) | ((c >= 128) & (c - 128 <= i)), 0.0, MASKV)
    m1 = np.where(((c < 128) & (c > i) & (c >= NPAD)) | ((c >= 128) & (c - 128 <= i)), 0.0, MASKV)
    m0 = np.where((c >= 128) & (c - 128 <= i) & (c - 128 >= NPAD), 0.0, MASKV)
    masks = np.stack([m0, m1, full], axis=1).astype(np.float32)
    sm = np.full((128, 17 * 128), MASKV, np.float32)
    for s in range(NSEQ):
        for t in range(4):
            r = s * 4 + t
            sm[r, s * 128 + t + 1:(s + 1) * 128] = 0.0
            sm[r, 2048 + s * 4:2048 + s * 4 + t + 1] = 0.0
    return ident, masks, sm


def prep_inputs(x_prompt, x_sample, cache_k_win, cache_v_win, state_ssm_re, state_ssm_im, state_conv,
           meta_tokens, g_mix, w_in, sinks, lam_re, lam_im, log_dt, b_re, b_im, c_re, c_im, d_skip,
           w_glu, b_glu, g_attn_out, g_ssm_out, w_o, g_ffn, w_gate, w_up, conv_w, conv_b, w_down,
           g_final):
    f32 = np.float32
    A = lambda a: np.ascontiguousarray(np.asarray(a, dtype=f32))
    x_prompt, x_sample = A(x_prompt), A(x_sample)
    ident, masks, smask = _consts()
    w_in0 = A(w_in)[0]
    perm = []
    for i in range(4):
        perm += list(range(i * 64, (i + 1) * 64)) + list(range((4 + i) * 64, (5 + i) * 64))
    w_in_p = np.ascontiguousarray(np.concatenate([w_in0[:, perm], w_in0[:, 512:]], axis=1))
    pcol = np.zeros((128, 128), f32)
    pcol[:, 0:8] = A(g_mix)[0].reshape(8, 128).T
    pcol[:, 8:16] = A(g_ffn)[0].reshape(8, 128).T
    pcol[:, 16:20] = A(g_attn_out)[0].reshape(4, 128).T
    pcol[:, 20:24] = A(g_ssm_out)[0].reshape(4, 128).T
    pcol[:, 24:28] = A(b_glu)[0].reshape(4, 128).T
    pcol[:, 28:32] = A(d_skip)[0].reshape(4, 128).T
    cw = A(conv_w)[0].reshape(3, NFC, 128)
    pcol[:, 32:98] = cw.transpose(2, 1, 0).reshape(128, 66)
    pcol[:, 98:120] = A(conv_b)[0].reshape(NFC, 128).T
    lr, li, ld = A(lam_re)[0], A(lam_im)[0], A(log_dt)[0]
    ldx = np.repeat(ld[:, None], 64, axis=1)

    def pl(a):
        return a.reshape(16, 2, 64).transpose(1, 2, 0).reshape(128, 16)

    lam = np.ascontiguousarray(np.stack([pl(lr), pl(li), pl(ldx)], axis=1))
    lamb = np.ascontiguousarray(np.stack([lr.reshape(-1), li.reshape(-1), ldx.reshape(-1)], axis=0))
    bre, bim, cre, cim = A(b_re)[0], A(b_im)[0], A(c_re)[0], A(c_im)[0]
    bblk = np.zeros((128, 2, 16, 128), f32)
    cblk = np.zeros((128, 2, 16, 128), f32)
    for q in range(16):
        for j2 in range(2):
            g = 2 * q + j2
            g8 = g % 8
            rows = slice(g8 * 16, g8 * 16 + 16)
            cols = slice(j2 * 64, j2 * 64 + 64)
            bblk[rows, 0, q, cols] = bre[g].T
            bblk[rows, 1, q, cols] = bim[g].T
            cblk[cols, 0, q, rows] = cre[g].T
            cblk[cols, 1, q, rows] = cim[g].T
    sre, sim_ = A(state_ssm_re)[0], A(state_ssm_im)[0]
    ck, cvv = A(cache_k_win)[0].reshape(128, 128, 128), A(cache_v_win)[0].reshape(128, 128, 128)
    sc = A(state_conv)[0]
    meta = A(meta_tokens)
    wg_l = np.ascontiguousarray(A(w_gate)[0].reshape(8, 128, NFC, 128).transpose(2, 1, 0, 3))
    wu_l = np.ascontiguousarray(A(w_up)[0].reshape(8, 128, NFC, 128).transpose(2, 1, 0, 3))
    wd_l = np.ascontiguousarray(A(w_down)[0].reshape(NFC // 2, 2, 128, 2, 512).transpose(3, 0, 2, 1, 4))
    in_maps = []
    for c in range(NCORES):
        xin = np.zeros((NPT * 128 + 128, D), f32)
        xin[NPAD:128] = meta
        xin[128:NPT * 128] = x_prompt[c]
        xin[NPT * 128:NPT * 128 + NS] = x_sample[c * NSEQ:(c + 1) * NSEQ].reshape(NS, D)
        sl_ = slice(c * NSEQ, (c + 1) * NSEQ)

        def hl(a):
            return a.reshape(NSEQ, 16, 2, 64).transpose(2, 3, 1, 0).reshape(128, 16, NSEQ)

        h0 = np.ascontiguousarray(np.stack([hl(sre[sl_]), hl(sim_[sl_])], axis=1))
        cvst = np.ascontiguousarray(sc[sl_].reshape(NSEQ, 2, NFC, 128).transpose(3, 2, 0, 1))
        kc, vc = ck[sl_], cvv[sl_]
        kcT = np.ascontiguousarray(kc.transpose(2, 0, 1).reshape(128, NSEQ * 128))
        vcl = np.ascontiguousarray(vc.transpose(1, 0, 2))
        in_maps.append(dict(
            xin=xin, w_in=w_in_p, w_o=A(w_o)[0], w_glu=A(w_glu)[0], w_gate=wg_l, w_up=wu_l,
            w_down=wd_l, ident=ident, masks=masks, smask=smask, pcol=pcol, sinks=A(sinks)[0],
            gfin=A(g_final), lam=lam, lamb=lamb, bblk=bblk, cblk=cblk, h0=h0, cvst=cvst, kcT=kcT, vc=vcl,
            kcache=np.ascontiguousarray(kc), vcache=np.ascontiguousarray(vc)))
    return in_maps


def assemble(R):
    f32 = np.float32
    y_prompt = np.stack([R[c]["y"][128:NPT * 128] for c in range(NCORES)])
    y_sample = np.concatenate([R[c]["y"][NPT * 128:NPT * 128 + NS].reshape(NSEQ, 4, D) for c in range(NCORES)])
    p_k = np.stack([R[c]["kvp"][:, 0:128].reshape(128, 2, 64) for c in range(NCORES)])[None]
    p_v = np.stack([R[c]["kvp"][:, 128:256].reshape(128, 2, 64) for c in range(NCORES)])[None]
    s_k = np.concatenate([R[c]["kvs_k"].reshape(NSEQ, 128, 2, 64) for c in range(NCORES)])[None]
    s_v = np.concatenate([R[c]["kvs_v"].reshape(NSEQ, 128, 2, 64) for c in range(NCORES)])[None]

    def unh(a):
        nn = a.shape[-1]
        return a.reshape(2, 64, 16, nn).transpose(3, 2, 0, 1).reshape(nn, 32, 64)

    p_re = np.stack([unh(R[c]["hfin"][:, 0, :, NSEQ:])[0] for c in range(NCORES)])[None]
    p_im = np.stack([unh(R[c]["hfin"][:, 1, :, NSEQ:])[0] for c in range(NCORES)])[None]
    s_re = np.concatenate([unh(R[c]["hfin"][:, 0, :, :NSEQ]) for c in range(NCORES)])[None]
    s_im = np.concatenate([unh(R[c]["hfin"][:, 1, :, :NSEQ]) for c in range(NCORES)])[None]
    p_conv = np.stack([R[c]["pconv"].transpose(2, 1, 0).reshape(2, FF) for c in range(NCORES)])[None]
    s_conv = np.concatenate([R[c]["sconv"].transpose(2, 3, 1, 0).reshape(NSEQ, 2, FF) for c in range(NCORES)])[None]
    out = (y_prompt, y_sample, p_k, p_v, p_re, p_im, p_conv, s_k, s_v, s_re, s_im, s_conv)
    return tuple(np.ascontiguousarray(o, dtype=f32) for o in out)


def kernel(**inputs):
    in_maps = prep_inputs(**inputs)
    nc = build_nc()
    res = run_bass_kernel_spmd(nc, in_maps, core_ids=list(range(NCORES)))
    return assemble(res.results)
```
